# Optimizing a Trainium2 kernel written in Bass

```python
import math
import jax
import jax.numpy as jnp
from jax import lax
import numpy as np

D_MODEL = 1024
BATCH = 8
SEQ = 4096
DEPTH = 4

HEAD_DIM = 64
D_MIX = D_MODEL
H_RWKV = D_MIX // (4 * HEAD_DIM)
C_RWKV = H_RWKV * HEAD_DIM
H_FOX = (D_MIX - C_RWKV) // (2 * HEAD_DIM)
C_FOX = H_FOX * HEAD_DIM
H_NSA = (D_MIX - C_RWKV - C_FOX) // HEAD_DIM
C_NSA = H_NSA * HEAD_DIM
G_NSA = 2
HPG = H_NSA // G_NSA
C_KV_NSA = G_NSA * HEAD_DIM
R_DECAY = 32
R_AAA = 32
R_GATE = 64
L_CMP = 32
D_CMP = 16
CMP_HID = 128
L_SLC = 64
N_SLC = 16
N_LOCAL = 2
WINDOW = 512
Q_BLOCK = 128
SLC_Q_CHUNK = 64
NUM_BUCKETS = 32
MAX_DISTANCE = 128
D_FF = 2816
CONV_W = 3
RMS_EPS = 1e-6
GN_EPS = 64e-5
NEG_INF = -1e30
FORCE_SCORE = 1e4

RWKV_COLS = (C_RWKV, C_RWKV, C_RWKV, R_DECAY, R_AAA, R_GATE)
FOX_COLS = (C_FOX, C_FOX, C_FOX, H_FOX)
NSA_COLS = (C_NSA,) + (C_KV_NSA,) * 6 + (3 * H_NSA,)
N_RWKV_IN = sum(RWKV_COLS)
N_FOX_IN = sum(FOX_COLS)
N_NSA_IN = sum(NSA_COLS)
N_IN = N_RWKV_IN + N_FOX_IN + N_NSA_IN

kernel_name = "hybrid_rwkv7_fox_nsa_convffn"


def rmsnorm(x, g):
    xf = x.astype(jnp.float32)
    y = xf * lax.rsqrt(jnp.mean(xf * xf, axis=-1, keepdims=True) + RMS_EPS)
    return (y * g.astype(jnp.float32)).astype(x.dtype)


def split_cols(p, sizes):
    idx = np.cumsum(np.array(sizes))[:-1].tolist()
    return jnp.split(p, idx, axis=-1)


def shift_right(x):
    return jnp.pad(x, ((0, 0), (1, 0), (0, 0)))[:, : x.shape[1]]


def masked_softmax(logits, mask):
    p = jax.nn.softmax(jnp.where(mask, logits, NEG_INF), axis=-1)
    return jnp.where(mask, p, 0.0)


def t5_bucket(dist):
    n = jnp.maximum(dist, 0)
    max_exact = NUM_BUCKETS // 2
    nf = jnp.maximum(n, 1).astype(jnp.float32)
    large = max_exact + (jnp.log(nf / max_exact) / math.log(MAX_DISTANCE / max_exact)
                         * (NUM_BUCKETS - max_exact)).astype(jnp.int32)
    large = jnp.minimum(large, NUM_BUCKETS - 1)
    return jnp.where(n < max_exact, n, large)


def wkv7_scan(r, w, k, v, a, b):
    B, S, H, N = r.shape

    def step(state, inp):
        r_t, w_t, k_t, v_t, a_t, b_t = inp
        sa = jnp.einsum('bhvk,bhk->bhv', state, a_t)
        state = (state * w_t[:, :, None, :] + sa[..., None] * b_t[:, :, None, :]
                 + v_t[..., None] * k_t[:, :, None, :])
        return state, jnp.einsum('bhvk,bhk->bhv', state, r_t)

    xs = tuple(jnp.moveaxis(t, 1, 0) for t in (r, w, k, v, a, b))
    state0 = jnp.zeros((B, H, N, N), jnp.float32)
    _, ys = lax.scan(step, state0, xs)
    return jnp.moveaxis(ys, 0, 1)


def rwkv7_mix(p, mu, w0, w_up, a0, a_up, g_up, k_k, k_a, r_k, ln_w, ln_b):
    B, S, _ = p.shape
    dt = p.dtype
    p = p + (shift_right(p) - p) * mu
    r, k, v, xw, xa, xg = split_cols(p.astype(jnp.float32), RWKV_COLS)
    w = -jax.nn.softplus(-(w0 + jnp.tanh(xw) @ w_up)) - 0.5
    decay = jnp.exp(-jnp.exp(w))
    a = jax.nn.sigmoid(a0 + xa @ a_up)
    g = jax.nn.sigmoid(xg) @ g_up
    heads = lambda t: t.reshape(B, S, H_RWKV, HEAD_DIM)
    kk = heads(k * k_k)
    kk = kk / jnp.maximum(jnp.linalg.norm(kk, axis=-1, keepdims=True), 1e-12)
    k = k * (1.0 + (a - 1.0) * k_a)
    r_h, k_h, v_h, a_h = heads(r), heads(k), heads(v), heads(a)
    y = wkv7_scan(r_h, heads(decay), k_h, v_h, -kk, kk * a_h)
    mean = jnp.mean(y, axis=-1, keepdims=True)
    var = jnp.mean(jnp.square(y - mean), axis=-1, keepdims=True)
    yn = ((y - mean) * lax.rsqrt(var + GN_EPS)).reshape(B, S, C_RWKV) * ln_w + ln_b
    bonus = jnp.sum(r_h * k_h * r_k, axis=-1, keepdims=True) * v_h
    return ((yn + bonus.reshape(B, S, C_RWKV)) * g).astype(dt)


def fox_mix(p, b_f):
    B, S, _ = p.shape
    q, k, v, f_logit = split_cols(p, FOX_COLS)
    to_heads = lambda t: t.reshape(B, S, H_FOX, HEAD_DIM).transpose(0, 2, 1, 3)
    q, k, v = to_heads(q), to_heads(k), to_heads(v)
    log_f = jax.nn.log_sigmoid((f_logit + b_f).astype(jnp.float32))
    cum = jnp.cumsum(log_f, axis=1).transpose(0, 2, 1)
    scale = HEAD_DIM ** -0.5
    kpos = jnp.arange(S)

    def block(i):
        q0 = i * Q_BLOCK
        qb = lax.dynamic_slice_in_dim(q, q0, Q_BLOCK, axis=2)
        cb = lax.dynamic_slice_in_dim(cum, q0, Q_BLOCK, axis=2)
        logits = (jnp.einsum('bhqd,bhkd->bhqk', qb, k).astype(jnp.float32) * scale
                  + cb[..., None] - cum[:, :, None, :])
        qpos = q0 + jnp.arange(Q_BLOCK)
        probs = masked_softmax(logits, kpos[None, :] <= qpos[:, None])
        return jnp.einsum('bhqk,bhkd->bhqd', probs.astype(v.dtype), v)

    out = lax.map(block, jnp.arange(S // Q_BLOCK))
    return out.transpose(1, 0, 3, 2, 4).reshape(B, S, C_FOX)


def nsa_mix(p, pe_k, pe_v, ck_w1, ck_w2, cv_w1, cv_w2, rel_bias):
    B, S, _ = p.shape
    dt = p.dtype
    q, kc, vc, ks, vs, kw, vw, gate_logit = split_cols(p, NSA_COLS)
    q = q.reshape(B, S, G_NSA, HPG, HEAD_DIM)
    kvh = lambda t: t.reshape(B, S, G_NSA, HEAD_DIM)
    kc, vc, ks, vs, kw, vw = (kvh(t) for t in (kc, vc, ks, vs, kw, vw))
    scale = HEAD_DIM ** -0.5
    tpos = jnp.arange(S)

    n_ch = S // D_CMP
    ov = L_CMP // D_CMP
    n_cmp = n_ch - ov + 1

    def compress(t, pe, w1, w2):
        ch = t.reshape(B, n_ch, D_CMP, G_NSA, HEAD_DIM)
        blocks = jnp.concatenate([ch[:, j:j + n_cmp] for j in range(ov)], axis=2)
        blocks = blocks + pe[:, None, :]
        flat = blocks.transpose(0, 1, 3, 2, 4).reshape(B, n_cmp, G_NSA, L_CMP * HEAD_DIM)
        return jax.nn.gelu(flat @ w1) @ w2

    k_cmp = compress(kc, pe_k, ck_w1, ck_w2)
    v_cmp = compress(vc, pe_v, cv_w1, cv_w2)
    cmp_start = jnp.arange(n_cmp) * D_CMP
    cmp_end = cmp_start + L_CMP - 1
    dist_c = tpos[:, None] - cmp_end[None, :]
    bias_c = rel_bias[t5_bucket(dist_c)].reshape(S, n_cmp, G_NSA, HPG).transpose(2, 3, 0, 1)
    logits_c = jnp.einsum('bsghd,bcgd->bghsc', q, k_cmp).astype(jnp.float32) * scale + bias_c
    p_cmp = masked_softmax(logits_c, dist_c >= 0)
    o_cmp = jnp.einsum('bghsc,bcgd->bsghd', p_cmp.astype(dt), v_cmp)

    nsb = S // L_SLC
    slc_start = jnp.arange(nsb) * L_SLC
    overlap = ((cmp_start[:, None] <= slc_start[None, :] + L_SLC - 1)
               & (cmp_end[:, None] >= slc_start[None, :])).astype(jnp.float32)
    imp = jnp.einsum('bghsc,cj->bgsj', p_cmp, overlap)
    cur = tpos // L_SLC
    jb = jnp.arange(nsb)
    back = cur[:, None] - jb[None, :]
    valid = back >= 0
    forced = (jb[None, :] == 0) | (valid & (back < N_LOCAL))
    score = jnp.where(valid, jnp.where(forced, FORCE_SCORE, imp), -1.0)
    n_sel = min(N_SLC, nsb)
    top_val, top_idx = lax.top_k(score, n_sel)
    sel_ok = top_val >= 0.0

    kb = ks.reshape(B, nsb, L_SLC, G_NSA, HEAD_DIM).transpose(0, 3, 1, 2, 4)
    vb = vs.reshape(B, nsb, L_SLC, G_NSA, HEAD_DIM).transpose(0, 3, 1, 2, 4)
    gather = jax.vmap(jax.vmap(lambda blocks, idx: blocks[idx]))
    qg = q.transpose(0, 2, 1, 3, 4)
    tbl_g = rel_bias.reshape(NUM_BUCKETS, G_NSA, HPG).transpose(1, 0, 2)
    g_idx = jnp.arange(G_NSA)[None, :, None, None, None]

    def slc_chunk(i):
        q0 = i * SLC_Q_CHUNK
        qc = lax.dynamic_slice_in_dim(qg, q0, SLC_Q_CHUNK, axis=2)
        ic = lax.dynamic_slice_in_dim(top_idx, q0, SLC_Q_CHUNK, axis=2)
        okc = lax.dynamic_slice_in_dim(sel_ok, q0, SLC_Q_CHUNK, axis=2)
        kg = gather(kb, ic)
        vg = gather(vb, ic)
        qpos = q0 + jnp.arange(SLC_Q_CHUNK)
        kpos = ic[..., None] * L_SLC + jnp.arange(L_SLC)
        dist = qpos[None, None, :, None, None] - kpos
        mask = (okc[..., None] & (dist >= 0))[:, :, :, None]
        bias = tbl_g[g_idx, t5_bucket(dist)].transpose(0, 1, 2, 5, 3, 4)
        logits = jnp.einsum('bgqhd,bgqnld->bgqhnl', qc, kg).astype(jnp.float32) * scale + bias
        shp = logits.shape
        flat_mask = jnp.broadcast_to(mask, shp).reshape(shp[:4] + (n_sel * L_SLC,))
        probs = masked_softmax(logits.reshape(shp[:4] + (n_sel * L_SLC,)), flat_mask).reshape(shp)
        return jnp.einsum('bgqhnl,bgqnld->bgqhd', probs.astype(dt), vg)

    o_slc = lax.map(slc_chunk, jnp.arange(S // SLC_Q_CHUNK))
    o_slc = o_slc.transpose(1, 0, 3, 2, 4, 5).reshape(B, S, G_NSA, HPG, HEAD_DIM)

    kw_pad = jnp.pad(kw, ((0, 0), (WINDOW, 0), (0, 0), (0, 0)))
    vw_pad = jnp.pad(vw, ((0, 0), (WINDOW, 0), (0, 0), (0, 0)))
    n_keys = Q_BLOCK + WINDOW

    def win_block(i):
        q0 = i * Q_BLOCK
        qb = lax.dynamic_slice_in_dim(q, q0, Q_BLOCK, axis=1)
        kblk = lax.dynamic_slice_in_dim(kw_pad, q0, n_keys, axis=1)
        vblk = lax.dynamic_slice_in_dim(vw_pad, q0, n_keys, axis=1)
        qpos = q0 + jnp.arange(Q_BLOCK)
        kpos = q0 - WINDOW + jnp.arange(n_keys)
        dist = qpos[:, None] - kpos[None, :]
        mask = (dist >= 0) & (dist < WINDOW) & (kpos[None, :] >= 0)
        bias = rel_bias[t5_bucket(dist)].reshape(Q_BLOCK, n_keys, G_NSA, HPG).transpose(2, 3, 0, 1)
        logits = jnp.einsum('bqghd,bkgd->bghqk', qb, kblk).astype(jnp.float32) * scale + bias
        probs = masked_softmax(logits, mask)
        return jnp.einsum('bghqk,bkgd->bqghd', probs.astype(dt), vblk)

    o_win = lax.map(win_block, jnp.arange(S // Q_BLOCK))
    o_win = o_win.transpose(1, 0, 2, 3, 4, 5).reshape(B, S, G_NSA, HPG, HEAD_DIM)

    gates = jax.nn.sigmoid(gate_logit).reshape(B, S, G_NSA, HPG, 3)
    o = gates[..., 0:1] * o_cmp + gates[..., 1:2] * o_slc + gates[..., 2:3] * o_win
    return o.reshape(B, S, C_NSA)


def conv_ffn(h, w_up, conv_w, conv_b, w_down):
    S = h.shape[1]
    u = h @ w_up
    u_pad = jnp.pad(u, ((0, 0), (CONV_W - 1, 0), (0, 0)))
    u = conv_b + sum(conv_w[j] * u_pad[:, j:j + S] for j in range(CONV_W))
    gate, val = jnp.split(u, 2, axis=-1)
    return (jax.nn.silu(gate) * val) @ w_down


def setup_inputs(seed: int = 0) -> dict:
    key = jax.random.key(seed)
    ks = iter(jax.random.split(key, 40))

    def nrm(shape, scale):
        return scale * jax.random.normal(next(ks), shape, jnp.float32)

    L, D = DEPTH, D_MODEL
    return {
        "x": nrm((BATCH, SEQ, D), 1.0),
        "c": nrm((BATCH, D), 1.0),
        "ada_w": nrm((L, D, 6 * D), 0.5 * D ** -0.5),
        "ada_b": nrm((L, 6 * D), 0.1),
        "norm_g": 1.0 + nrm((L, 4, D), 0.02),
        "w_in": nrm((L, D, N_IN), D ** -0.5),
        "rwkv_mu": 0.5 + nrm((L, N_RWKV_IN), 0.1),
        "rwkv_w0": nrm((L, C_RWKV), 0.5),
        "rwkv_w_up": nrm((L, R_DECAY, C_RWKV), 0.5 * R_DECAY ** -0.5),
        "rwkv_a0": nrm((L, C_RWKV), 0.3),
        "rwkv_a_up": nrm((L, R_AAA, C_RWKV), 0.5 * R_AAA ** -0.5),
        "rwkv_g_up": nrm((L, R_GATE, C_RWKV), R_GATE ** -0.5),
        "rwkv_k_k": 0.85 + nrm((L, C_RWKV), 0.05),
        "rwkv_k_a": 1.0 + nrm((L, C_RWKV), 0.05),
        "rwkv_r_k": nrm((L, H_RWKV, HEAD_DIM), 0.1),
        "rwkv_ln_w": 1.0 + nrm((L, C_RWKV), 0.02),
        "rwkv_ln_b": nrm((L, C_RWKV), 0.02),
        "fox_b_f": 3.0 + nrm((L, H_FOX), 0.5),
        "nsa_pe_k": nrm((L, L_CMP, HEAD_DIM), 0.1),
        "nsa_pe_v": nrm((L, L_CMP, HEAD_DIM), 0.1),
        "nsa_ck_w1": nrm((L, L_CMP * HEAD_DIM, CMP_HID), (L_CMP * HEAD_DIM) ** -0.5),
        "nsa_ck_w2": nrm((L, CMP_HID, HEAD_DIM), CMP_HID ** -0.5),
        "nsa_cv_w1": nrm((L, L_CMP * HEAD_DIM, CMP_HID), (L_CMP * HEAD_DIM) ** -0.5),
        "nsa_cv_w2": nrm((L, CMP_HID, HEAD_DIM), CMP_HID ** -0.5),
        "rel_bias": nrm((NUM_BUCKETS, H_NSA), 0.5),
        "w_out": nrm((L, D_MIX, D), D_MIX ** -0.5),
        "ffn_up": nrm((L, D, 2 * D_FF), D ** -0.5),
        "ffn_conv_w": nrm((L, CONV_W, 2 * D_FF), 0.3).at[:, -1].add(1.0),
        "ffn_conv_b": nrm((L, 2 * D_FF), 0.02),
        "ffn_down": nrm((L, D_FF, D), D_FF ** -0.5),
    }


def reference(x, c, ada_w, ada_b, norm_g, w_in, rwkv_mu, rwkv_w0, rwkv_w_up, rwkv_a0,
              rwkv_a_up, rwkv_g_up, rwkv_k_k, rwkv_k_a, rwkv_r_k, rwkv_ln_w, rwkv_ln_b,
              fox_b_f, nsa_pe_k, nsa_pe_v, nsa_ck_w1, nsa_ck_w2, nsa_cv_w1, nsa_cv_w2,
              rel_bias, w_out, ffn_up, ffn_conv_w, ffn_conv_b, ffn_down):
    for l in range(DEPTH):
        mod = jax.nn.silu(c) @ ada_w[l] + ada_b[l]
        sh_m, sc_m, g_m, sh_f, sc_f, g_f = jnp.split(mod[:, None, :], 6, axis=-1)

        h = rmsnorm(x, norm_g[l, 0]) * (1.0 + sc_m) + sh_m
        p = h @ w_in[l]
        p_a, p_b, p_c = split_cols(p, (N_RWKV_IN, N_FOX_IN, N_NSA_IN))
        y_a = rwkv7_mix(p_a, rwkv_mu[l], rwkv_w0[l], rwkv_w_up[l], rwkv_a0[l], rwkv_a_up[l],
                        rwkv_g_up[l], rwkv_k_k[l], rwkv_k_a[l], rwkv_r_k[l],
                        rwkv_ln_w[l], rwkv_ln_b[l])
        y_b = fox_mix(p_b, fox_b_f[l])
        y_c = nsa_mix(p_c, nsa_pe_k[l], nsa_pe_v[l], nsa_ck_w1[l], nsa_ck_w2[l],
                      nsa_cv_w1[l], nsa_cv_w2[l], rel_bias)
        y = jnp.concatenate([y_a, y_b, y_c], axis=-1) @ w_out[l]
        x = x + g_m * rmsnorm(y, norm_g[l, 1])

        h = rmsnorm(x, norm_g[l, 2]) * (1.0 + sc_f) + sh_f
        f = conv_ffn(h, ffn_up[l], ffn_conv_w[l], ffn_conv_b[l], ffn_down[l])
        x = x + g_f * rmsnorm(f, norm_g[l, 3])
    return x
```

```python
import numpy as np
import ml_dtypes
from contextlib import ExitStack
import concourse.bass as bass
import concourse.mybir as mybir
from concourse.bass_utils import run_bass_kernel_spmd

F32 = mybir.dt.float32
BF16 = mybir.dt.bfloat16
AF = mybir.ActivationFunctionType
ALU = mybir.AluOpType
AX = mybir.AxisListType
NPBF = ml_dtypes.bfloat16

S_LEN = 4096
D = 1024
DEPTH = 4
NTB = 8
N_IN = 3224
D_FF = 2816
NEG = -30000.0
RMS_EPS = 1e-6
GN_EPS = 64e-5


class Sched:
    ENG = ('pe', 'act', 'dve', 'pool')
    LIMIT = 30000

    def __init__(self, nc):
        self.nc = nc
        self.e = {'pe': nc.tensor, 'act': nc.scalar, 'dve': nc.vector, 'pool': nc.gpsimd, 'sp': nc.sync}
        self.epoch = {k: 0 for k in self.ENG}
        self.sem = {k: nc.alloc_semaphore("c_%s_0" % k) for k in self.ENG}
        self.cnt = {k: 0 for k in self.ENG}
        self.seen = {k: {} for k in self.e}
        self.lastw = {}
        self.reads = {}
        self.dma_sems = {'hw': [[nc.alloc_semaphore("d%d" % i), 0, "dma%d" % i] for i in range(24)],
                         'sw': [[nc.alloc_semaphore("ds%d" % i), 0, "dmas%d" % i] for i in range(8)]}
        self.ndma = {'hw': 0, 'sw': 0}
        self.n_inst = 0
        self.n_wait = 0
        self.per = {}

    def _wait(self, eng, tok):
        key, sem, val = tok
        if self.seen[eng].get(key, 0) >= val:
            return
        self.e[eng].wait_ge(sem, val)
        self.n_wait += 1
        self.per[eng] = self.per.get(eng, 0) + 1
        self.seen[eng][key] = val

    def _deps(self, eng, reads, writes):
        for b in reads:
            t = self.lastw.get(b)
            if t is not None:
                self._wait(eng, t)
        for b in writes:
            t = self.lastw.get(b)
            if t is not None:
                self._wait(eng, t)
            for t in self.reads.get(b, ()):
                self._wait(eng, t)

    def _commit(self, tok, reads, writes):
        for b in reads:
            self.reads.setdefault(b, []).append(tok)
        for b in writes:
            self.lastw[b] = tok
            self.reads[b] = []

    def _bump(self, eng, ins):
        if self.cnt[eng] >= self.LIMIT:
            self.epoch[eng] += 1
            self.sem[eng] = self.nc.alloc_semaphore("c_%s_%d" % (eng, self.epoch[eng]))
            self.cnt[eng] = 0
        self.cnt[eng] += 1
        ins.then_inc(self.sem[eng], 1)
        return ("%s_%d" % (eng, self.epoch[eng]), self.sem[eng], self.cnt[eng])

    @staticmethod
    def _norm(reads, writes):
        rd = [getattr(b, 'n', b) for b in reads]
        wr = [getattr(b, 'n', b) for b in writes]
        ps = [b for b in rd if b.startswith("ps")]
        rd = [b for b in rd if not b.startswith("ps")]
        return rd, wr + [b for b in ps if b not in wr]

    def op(self, eng, inst_fn, reads=(), writes=()):
        reads, writes = self._norm(reads, writes)
        self._deps(eng, reads, writes)
        ins = inst_fn()
        self.per[eng] = self.per.get(eng, 0) + 1
        tok = self._bump(eng, ins)
        self._commit(tok, reads, writes)
        self.n_inst += 1
        return tok

    def pe_group(self, fns, reads=(), writes=()):
        reads, writes = self._norm(reads, writes)
        self._deps('pe', reads, writes)
        ins = None
        for f in fns:
            ins = f()
            self.n_inst += 1
            self.per['pe'] = self.per.get('pe', 0) + 1
        tok = self._bump('pe', ins)
        self._commit(tok, reads, writes)
        return tok

    def dma(self, q, out, in_, reads=(), writes=(), **kw):
        reads, writes = self._norm(reads, writes)
        self._deps(q, reads, writes)
        cls = 'sw' if q == 'pool' else 'hw'
        pool_ = self.dma_sems[cls]
        slot = pool_[self.ndma[cls] % len(pool_)]
        self.ndma[cls] += 1
        if slot[1] > 0:
            self._wait(q, (slot[2], slot[0], slot[1]))
        if slot[1] >= self.LIMIT:
            slot[0] = self.nc.alloc_semaphore("%s_e%d" % (slot[2], self.ndma[cls]))
            slot[1] = 0
            slot[2] = slot[2] + "x"
        slot[1] += 16
        ins = self.e[q].dma_start(out=out, in_=in_, **kw)
        self.per[q] = self.per.get(q, 0) + 1
        ins.then_inc(slot[0], 16)
        tok = (slot[2], slot[0], slot[1])
        self._commit(tok, reads, writes)
        self.n_inst += 1
        return tok

    def barrier(self, engines=('pe', 'act', 'dve', 'pool', 'sp')):
        toks = [("%s_%d" % (k, self.epoch[k]), self.sem[k], self.cnt[k]) for k in self.ENG if self.cnt[k] > 0]
        toks += [(s[2], s[0], s[1]) for p_ in self.dma_sems.values() for s in p_ if s[1] > 0]
        for e in engines:
            for t in toks:
                self._wait(e, t)
        self.lastw = {}
        self.reads = {}


class Tile:
    def __init__(self, h, name):
        self.h = h
        self.n = name

    def __getitem__(self, idx):
        return self.h[idx]


class Scope:
    cnt = 0

    def __init__(self, k):
        self.k = k
        self.es = ExitStack()

    def __enter__(self):
        self.es.__enter__()
        Scope.cnt += 1
        self.id = Scope.cnt
        return self

    def sb(self, name, shape, dt):
        nm = "%s_%d" % (name, self.id)
        h = self.es.enter_context(self.k.nc.sbuf_tensor(nm, list(shape), dt))
        return Tile(h, nm)

    def ps(self, name, shape, dt=F32):
        nm = "%s_%d" % (name, self.id)
        h = self.es.enter_context(self.k.nc.psum_tensor(nm, list(shape), dt))
        return Tile(h, nm)

    def __exit__(self, *a):
        self.k.S.barrier()
        return self.es.__exit__(*a)


class K:
    def __init__(self, nlayers, taps=()):
        self.nc = bass.Bass("TRN2", target_bir_lowering=False)
        self.S = Sched(self.nc)
        self.nl = nlayers
        self.taps = set(taps)
        self.ins = {}
        self.dr = {}

    def inp(self, name, shape, dt=F32):
        t = self.nc.dram_tensor(name, list(shape), dt, kind="ExternalInput").ap()
        self.ins[name] = t
        return t

    def scratch(self, name, shape, dt=F32, out=False):
        kind = "ExternalOutput" if (out or name in self.taps) else "Internal"
        t = self.nc.dram_tensor(name, list(shape), dt, kind=kind).ap()
        self.dr[name] = t
        return t

    def act(self, out, in_, func, r, w, bias=0.0, scale=1.0, accum=None):
        nc = self.nc
        if accum is None:
            return self.S.op('act', lambda: nc.scalar.activation(out=out, in_=in_, func=func, bias=bias, scale=scale), r, w)
        return self.S.op('act', lambda: nc.scalar.activation(out=out, in_=in_, func=func, bias=bias, scale=scale, accum_out=accum), r, w)

    def ts(self, eng, out, in0, s1, s2, op0, op1, r, w):
        e = self.S.e[eng]
        if op1 is None:
            return self.S.op(eng, lambda: e.tensor_scalar(out=out, in0=in0, scalar1=s1, scalar2=None, op0=op0), r, w)
        return self.S.op(eng, lambda: e.tensor_scalar(out=out, in0=in0, scalar1=s1, scalar2=s2, op0=op0, op1=op1), r, w)

    def tt(self, eng, out, in0, in1, op, r, w):
        e = self.S.e[eng]
        return self.S.op(eng, lambda: e.tensor_tensor(out=out, in0=in0, in1=in1, op=op), r, w)

    def stt(self, eng, out, in0, scalar, in1, op0, op1, r, w):
        e = self.S.e[eng]
        return self.S.op(eng, lambda: e.scalar_tensor_tensor(out=out, in0=in0, scalar=scalar, in1=in1, op0=op0, op1=op1), r, w)

    def copy(self, eng, out, in_, r, w):
        if eng == 'act':
            return self.S.op('act', lambda: self.nc.scalar.copy(out=out, in_=in_), r, w)
        e = self.S.e[eng]
        return self.S.op(eng, lambda: e.tensor_copy(out=out, in_=in_), r, w)

    def mm(self, out, pairs, r, w, start=True, stop=True, sgc=False):
        nc = self.nc
        n = len(pairs)
        fns = []
        for i, (l, rh) in enumerate(pairs):
            fns.append(lambda l=l, rh=rh, i=i: nc.tensor.matmul(out, lhsT=l, rhs=rh, start=(start and i == 0), stop=(stop and i == n - 1),
                                                               skip_group_check=(sgc or not start)))
        return self.S.pe_group(fns, r, w)

    def transpose(self, out, in_, ident, r, w):
        nc = self.nc
        return self.S.op('pe', lambda: nc.tensor.transpose(out=out, in_=in_, identity=ident), r, w)

    def dma(self, q, out, in_, r=(), w=(), **kw):
        return self.S.dma(q, out, in_, r, w, **kw)


def w_in_perm_index():
    idx = list(range(0, 896))
    idx += list(range(896, 1664))
    for c in range(3):
        idx += list(range(2054 + c * 64, 2054 + c * 64 + 64))
        idx += list(range(2054 + (c + 3) * 64, 2054 + (c + 3) * 64 + 64))
    idx += list(range(2438, 2566))
    idx += list(range(2566, 2694))
    idx += list(range(2694, 2822))
    idx += list(range(2950, 3078))
    idx += list(range(2048, 2054))
    idx += list(range(1664, 2048))
    idx += list(range(2822, 2950))
    idx += list(range(3078, 3206))
    idx += list(range(3206, 3224))
    assert len(idx) == N_IN and len(set(idx)) == N_IN
    return np.array(idx)


QKT_ROWS = 1664


def setup_globals(k):
    nc = k.nc
    k.x_in = k.inp("x", [S_LEN, D])
    k.cT = k.inp("cT", [128, 8])
    k.ada_w = k.inp("ada_w", [DEPTH, D, 6 * D])
    k.ada_b_fm = k.inp("ada_b_fm", [DEPTH, 128, 48])
    k.ada_b_row = k.inp("ada_b_row", [DEPTH, 6 * D])
    k.normg_fm = k.inp("normg_fm", [DEPTH, 4, 128, 8])
    k.normg_row = k.inp("normg_row", [DEPTH, 4, D])
    k.w_in = k.inp("w_in_p", [DEPTH, D, N_IN])
    k.ident_bf_d = k.inp("ident_bf", [128, 128], BF16)
    k.ident_f_d = k.inp("ident_f", [128, 128], F32)

    k.PT = k.scratch("PT", [896, S_LEN], F32)
    k.QKT = k.scratch("QKT", [QKT_ROWS, S_LEN], BF16)
    k.FL = k.scratch("FL", [6, S_LEN], F32)
    k.VT = k.scratch("VT", [S_LEN, 640], BF16)
    k.GT = k.scratch("GT", [S_LEN, 18], F32)
    k.Y = k.scratch("Y", [S_LEN, D], BF16)
    k.XR = k.scratch("XR", [S_LEN, D], F32)
    k.XR1 = k.scratch("XR1", [S_LEN, D], F32)
    k.OUT = k.scratch("out", [S_LEN, D], F32, out=True)

    def pers(name, shape, dt):
        return Tile(nc.alloc_sbuf_tensor(name, list(shape), dt), name)
    k.ident_bf = pers("ident_bf_sb", [128, 128], BF16)
    k.ident_f = pers("ident_f_sb", [128, 128], F32)
    k.sc = pers("sc", [128, 8], F32)
    k.sc_rep = pers("sc_rep", [128, 8, 128], F32)
    k.modAB = pers("modAB", [128, 32], F32)
    k.gm_row = pers("gm_row", [128, D], F32)
    k.gf_row = pers("gf_row", [128, D], F32)
    k.dma('sp', k.ident_bf[:], k.ident_bf_d, w=[k.ident_bf])
    k.dma('sp', k.ident_f[:], k.ident_f_d, w=[k.ident_f])
    k.dma('sp', k.sc[:], k.cT, w=[k.sc])
    k.act(k.sc[:], k.sc[:], AF.Silu, r=[k.sc], w=[k.sc])
    for kc in range(8):
        k.copy('dve', k.sc_rep[:, kc, :], k.sc[:, kc:kc + 1].to_broadcast([128, 128]), r=[k.sc], w=[k.sc_rep])


def stage_mod(k, l):
    with Scope(k) as sc:
        slab = [sc.sb("adaslab%d" % i, [128, 6 * D], F32) for i in range(2)]
        psA = sc.ps("psA", [128, 32])
        psR = [sc.ps("psR%d" % i, [128, 512]) for i in range(4)]
        bfm = sc.sb("bfm", [128, 48], F32)
        gfm = sc.sb("gfm", [128, 4, 8], F32)
        brow = sc.sb("brow", [128, 2, D], F32)
        grow = sc.sb("grow", [128, 2, D], F32)
        mfm = sc.sb("mfm", [128, 32], F32)
        k.dma('sp', bfm[:], k.ada_b_fm[l], w=[bfm])
        k.dma('sp', gfm[:], k.normg_fm[l].rearrange("g p c -> p g c"), w=[gfm])
        k.dma('sp', brow[:, 0, :], k.ada_b_row[l:l + 1, 2 * D:3 * D].broadcast_to([128, D]), w=[brow])
        k.dma('sp', brow[:, 1, :], k.ada_b_row[l:l + 1, 5 * D:6 * D].broadcast_to([128, D]), w=[brow])
        k.dma('sp', grow[:, 0, :], k.normg_row[l, 1:2, :].broadcast_to([128, D]), w=[grow])
        k.dma('sp', grow[:, 1, :], k.normg_row[l, 3:4, :].broadcast_to([128, D]), w=[grow])
        fm_chunks = list(range(0, 16)) + list(range(24, 40))
        row_cols = [2 * D, 2 * D + 512, 5 * D, 5 * D + 512]
        for kc in range(8):
            sl = slab[kc % 2]
            k.dma('sp', sl[:], k.ada_w[l, kc * 128:(kc + 1) * 128, :], w=[sl])
            for i, j in enumerate(fm_chunks):
                k.mm(psA[:, i:i + 1], [(sl[:, j * 128:(j + 1) * 128], k.sc[:, kc:kc + 1])], r=[sl, k.sc], w=[psA],
                     start=(kc == 0 and i == 0), stop=(kc == 7), sgc=True)
            for i, c0 in enumerate(row_cols):
                k.mm(psR[i][:], [(k.sc_rep[:, kc, :], sl[:, c0:c0 + 512])], r=[sl, k.sc_rep], w=[psR[i]],
                     start=(kc == 0), stop=(kc == 7), sgc=True)
        k.tt('dve', mfm[:, 0:16], psA[:, 0:16], bfm[:, 0:16], ALU.add, r=[psA, bfm], w=[mfm])
        k.tt('dve', mfm[:, 16:32], psA[:, 16:32], bfm[:, 24:40], ALU.add, r=[psA, bfm], w=[mfm])
        k.stt('dve', k.modAB[:, 0:8], mfm[:, 8:16], 1.0, gfm[:, 0, :], ALU.add, ALU.mult, r=[mfm, gfm], w=[k.modAB])
        k.copy('dve', k.modAB[:, 8:16], mfm[:, 0:8], r=[mfm], w=[k.modAB])
        k.stt('dve', k.modAB[:, 16:24], mfm[:, 24:32], 1.0, gfm[:, 2, :], ALU.add, ALU.mult, r=[mfm, gfm], w=[k.modAB])
        k.copy('dve', k.modAB[:, 24:32], mfm[:, 16:24], r=[mfm], w=[k.modAB])
        for i in range(4):
            dst = (k.gm_row if i < 2 else k.gf_row)
            cs = slice((i % 2) * 512, (i % 2) * 512 + 512)
            k.tt('dve', dst[:, cs], psR[i][:], brow[:, i // 2, cs], ALU.add, r=[psR[i], brow], w=[dst])
            k.tt('pool', dst[:, cs], dst[:, cs], grow[:, i // 2, cs], ALU.mult, r=[dst, grow], w=[dst])


def stage_proj(k, l, xsrc):
    nc = k.nc
    with Scope(k) as sc:
        wsb = sc.sb("wsb", [128, 8, N_IN], BF16)
        wst = [sc.sb("wst%d" % i, [128, N_IN], F32) for i in range(2)]
        xt = [sc.sb("xt%d" % i, [128, D], F32) for i in range(2)]
        junk = sc.sb("junk", [128, D], BF16)
        xn = [sc.sb("xn%d" % i, [128, D], BF16) for i in range(2)]
        st = [sc.sb("st%d" % i, [128, 4], F32) for i in range(2)]
        HT = [sc.sb("HT%d" % i, [128, 8, 512], BF16) for i in range(2)]
        psT = [sc.ps("psT%d" % i, [128, D], BF16) for i in range(2)]
        psM = [sc.ps("psM%d" % i, [128, 512]) for i in range(4)]
        evf = [sc.sb("evf%d" % i, [128, 512], F32) for i in range(3)]
        evb = [sc.sb("evb%d" % i, [128, 512], BF16) for i in range(3)]
        evt = [sc.sb("evt%d" % i, [128, 640], BF16) for i in range(2)]
        evg = [sc.sb("evg%d" % i, [128, 18], F32) for i in range(2)]
        for kc in range(8):
            s = wst[kc % 2]
            k.dma('sp', s[:], k.w_in[l, kc * 128:(kc + 1) * 128, :], w=[s])
            k.copy('pool', wsb[:, kc, :], s[:], r=[s], w=[wsb])
        nev = 0
        npm = 0
        for tb in range(NTB):
            ht = HT[tb % 2]
            for sub in range(4):
                ti = tb * 4 + sub
                x_ = xt[ti % 2]; xn_ = xn[ti % 2]; st_ = st[ti % 2]; pt_ = psT[ti % 2]
                k.dma('sp', x_[:], xsrc[ti * 128:(ti + 1) * 128, :], w=[x_])
                k.act(junk[:], x_[:], AF.Square, r=[x_], w=[junk, st_], scale=1.0 / 32.0, accum=st_[:, 0:1])
                k.act(st_[:, 1:2], st_[:, 0:1], AF.Ln, r=[st_], w=[st_], bias=RMS_EPS)
                k.act(st_[:, 2:3], st_[:, 1:2], AF.Exp, r=[st_], w=[st_], scale=-0.5)
                k.ts('dve', xn_[:], x_[:], st_[:, 2:3], None, ALU.mult, None, r=[x_, st_], w=[xn_])
                for kc in range(8):
                    k.transpose(pt_[:, kc * 128:(kc + 1) * 128], xn_[:, kc * 128:(kc + 1) * 128], k.ident_bf[:],
                                r=[xn_, k.ident_bf], w=[pt_])
                for kc in range(8):
                    o = ht[:, kc, sub * 128:(sub + 1) * 128]
                    i_ = pt_[:, kc * 128:(kc + 1) * 128]
                    if kc % 2 == 0:
                        k.ts('dve', o, i_, k.modAB[:, kc:kc + 1], k.modAB[:, 8 + kc:9 + kc], ALU.mult, ALU.add,
                             r=[pt_, k.modAB], w=[ht])
                    else:
                        k.act(o, i_, AF.Identity, r=[pt_, k.modAB], w=[ht], scale=k.modAB[:, kc:kc + 1],
                              bias=k.modAB[:, 8 + kc:9 + kc])
            tsl = slice(tb * 512, (tb + 1) * 512)
            fm = [(c * 128, 128, 'PT', c * 128) for c in range(7)]
            fm += [(896 + c * 128, 128, 'QKT', c * 128) for c in range(13)]
            fm += [(2560, 6, 'FL', 0)]
            for (c0, m, dst, r0) in fm:
                ps = psM[npm % 4]; npm += 1
                k.mm(ps[0:m, :], [(wsb[:, kc, c0:c0 + m], ht[:, kc, :]) for kc in range(8)], r=[wsb, ht], w=[ps])
                eng = 'act' if nev % 2 == 0 else 'dve'
                if dst == 'QKT':
                    ev = evb[nev % 3]
                    dd = k.QKT[r0:r0 + m, tsl]
                else:
                    ev = evf[nev % 3]
                    dd = (k.PT if dst == 'PT' else k.FL)[r0:r0 + m, tsl]
                nev += 1
                k.copy(eng, ev[0:m, :], ps[0:m, :], r=[ps], w=[ev])
                k.dma('sp', dd, ev[0:m, :], r=[ev])
            for sub in range(4):
                ti = tb * 4 + sub
                tok = slice(ti * 128, (ti + 1) * 128)
                ps0 = psM[npm % 4]; npm += 1
                ps1 = psM[npm % 4]; npm += 1
                lhs = lambda kc: ht[:, kc, sub * 128:(sub + 1) * 128]
                k.mm(ps0[:, 0:384], [(lhs(kc), wsb[:, kc, 2566:2950]) for kc in range(8)], r=[wsb, ht], w=[ps0])
                k.mm(ps1[:, 0:274], [(lhs(kc), wsb[:, kc, 2950:3224]) for kc in range(8)], r=[wsb, ht], w=[ps1])
                et = evt[ti % 2]; eg = evg[ti % 2]
                k.copy('act', et[:, 0:384], ps0[:, 0:384], r=[ps0], w=[et])
                k.copy('dve', et[:, 384:640], ps1[:, 0:256], r=[ps1], w=[et])
                k.copy('dve', eg[:], ps1[:, 256:274], r=[ps1], w=[eg])
                k.dma('sp', k.VT[tok, :], et[:], r=[et])
                k.dma('sp', k.GT[tok, :], eg[:], r=[eg])


def prep_shared(inp):
    f = lambda a: np.ascontiguousarray(np.asarray(a, dtype=np.float32))
    sh = {}
    sh["ada_w"] = f(inp["ada_w"])
    sh["ada_b_fm"] = f(np.asarray(inp["ada_b"]).reshape(DEPTH, 48, 128).transpose(0, 2, 1))
    sh["ada_b_row"] = f(inp["ada_b"])
    sh["normg_fm"] = f(np.asarray(inp["norm_g"]).reshape(DEPTH, 4, 8, 128).transpose(0, 1, 3, 2))
    sh["normg_row"] = f(inp["norm_g"])
    sh["w_in_p"] = f(np.asarray(inp["w_in"])[:, :, w_in_perm_index()])
    sh["ident_bf"] = np.eye(128, dtype=np.float32).astype(NPBF)
    sh["ident_f"] = np.eye(128, dtype=np.float32)
    sh["w_out"] = f(inp["w_out"]); sh["ffn_up"] = f(inp["ffn_up"]); sh["ffn_down"] = f(inp["ffn_down"])
    sh["conv_w_fm"] = f(np.asarray(inp["ffn_conv_w"]).reshape(DEPTH, 3, 44, 128).transpose(0, 3, 1, 2))
    sh["conv_b_fm"] = f(np.asarray(inp["ffn_conv_b"]).reshape(DEPTH, 44, 128).transpose(0, 2, 1))
    sh.update(nsa_host_consts())
    sh["rel_bias"] = f(inp["rel_bias"])
    sh["nsa_pe_kT"] = f(np.asarray(inp["nsa_pe_k"]).transpose(0, 2, 1))
    sh["nsa_pe_vT"] = f(np.asarray(inp["nsa_pe_v"]).transpose(0, 2, 1))
    for n in ("nsa_ck_w1", "nsa_cv_w1", "nsa_ck_w2", "nsa_cv_w2"):
        sh[n] = f(inp[n])
    sh["fox_b_f"] = f(np.asarray(inp["fox_b_f"]).reshape(DEPTH, 6, 1))
    sh.update(rwkv_host(inp))
    return sh


def prep_core(inp, b):
    d = {}
    d["x"] = np.ascontiguousarray(np.asarray(inp["x"][b], dtype=np.float32))
    d["cT"] = np.ascontiguousarray(np.asarray(inp["c"][b], dtype=np.float32).reshape(8, 128).T)
    return d


def setup_fox(k):
    k.fox_bf = k.inp("fox_b_f", [DEPTH, 6, 1])
    k.CUMA = k.scratch("CUMA", [6, 3, S_LEN], BF16)


def stage_fox(k, l):
    nc = k.nc
    with Scope(k) as sc:
        nb = sc.sb("nb", [128, 32, 6], F32)
        with Scope(k) as s2:
            fl = s2.sb("fl", [6, S_LEN], F32)
            t1 = s2.sb("t1", [6, S_LEN], F32)
            ones = s2.sb("ones", [6, S_LEN], F32)
            cum = s2.sb("cum", [6, S_LEN], F32)
            parts = s2.sb("parts", [6, 3, S_LEN], BF16)
            bfv = s2.sb("bfv", [6, 2], F32)
            psn = s2.ps("psn", [128, 512])
            k.dma('sp', fl[:], k.FL, w=[fl])
            k.dma('sp', bfv[:, 0:1], k.fox_bf[l], w=[bfv])
            k.ts('dve', bfv[:, 1:2], bfv[:, 0:1], -1.0, None, ALU.mult, None, r=[bfv], w=[bfv])
            k.S.op('pool', lambda: nc.gpsimd.memset(ones[:], 1.0), [], [ones])
            k.act(t1[:], fl[:], AF.Exp, r=[fl, bfv], w=[t1], bias=bfv[:, 1:2], scale=-1.0)
            k.act(t1[:], t1[:], AF.Ln, r=[t1], w=[t1], bias=1.0, scale=1.0)
            k.ts('dve', t1[:], t1[:], -1.0, None, ALU.mult, None, r=[t1], w=[t1])
            k.S.op('dve', lambda: nc.vector.tensor_tensor_scan(out=cum[:], data0=ones[:], data1=t1[:], initial=0.0,
                                                               op0=ALU.mult, op1=ALU.add), [ones, t1], [cum])
            for t in range(32):
                k.transpose(psn[:, t * 6:(t + 1) * 6], cum[:, t * 128:(t + 1) * 128], k.ident_f[0:6, 0:6],
                            r=[cum, k.ident_f], w=[psn])
            k.ts('dve', nb[:].rearrange("p t h -> p (t h)"), psn[:, 0:192], -1.0, None, ALU.mult, None, r=[psn], w=[nb])
            k.ts('dve', t1[:], cum[:], 8.0, None, ALU.mult, None, r=[cum], w=[t1])
            k.copy('dve', parts[:, 0, :], t1[:], r=[t1], w=[parts])
            k.tt('dve', t1[:], t1[:], parts[:, 0, :], ALU.subtract, r=[t1, parts], w=[t1])
            k.copy('dve', parts[:, 1, :], t1[:], r=[t1], w=[parts])
            k.tt('dve', t1[:], t1[:], parts[:, 1, :], ALU.subtract, r=[t1, parts], w=[t1])
            k.copy('dve', parts[:, 2, :], t1[:], r=[t1], w=[parts])
            k.dma('sp', k.CUMA, parts[:], r=[parts], w=["CUMA"])
        QA = [sc.sb("QA%d" % i, [128, S_LEN], BF16) for i in range(2)]
        KA = [sc.sb("KA%d" % i, [128, S_LEN], BF16) for i in range(2)]
        VA = sc.sb("VA", [128, 32, 6, 65], BF16)
        yb = sc.sb("yb", [128, 32, 384], BF16)
        PTl = [sc.sb("PTl%d" % i, [128, 512], BF16) for i in range(4)]
        rc = [sc.sb("rc%d" % i, [128, 4], F32) for i in range(2)]
        psS = [sc.ps("psS%d" % i, [128, 512]) for i in range(3)]
        psO = [sc.ps("psO%d" % i, [128, 512]) for i in range(2)]
        k.dma('sp', yb[:], k.VT[:, 0:384].rearrange("(t p) c -> p t c", p=128), w=[yb])
        k.S.op('pool', lambda: nc.gpsimd.memset(VA[:, :, :, 64:65], 1.0), [], [VA])
        k.copy('pool', VA[:, :, :, 0:64], yb[:].rearrange("p t (h d) -> p t h d", h=6), r=[yb], w=[VA])
        for i in range(2):
            k.S.op('dve', lambda i=i: nc.vector.memset(KA[i][64:67, :], 1.0), [], [KA[i]])
        nS = 0
        nO = 0
        nP = 0
        for h in range(6):
            qa = QA[h % 2]; ka = KA[h % 2]
            k.dma('sp', qa[0:64, :], k.QKT[h * 64:(h + 1) * 64, :], w=[qa])
            k.dma('sp', qa[64:67, :], k.CUMA[h], w=[qa])
            k.dma('sp', ka[0:64, :], k.QKT[384 + h * 64:384 + (h + 1) * 64, :], w=[ka])
            for qb in range(NTB):
                po = psO[nO % 2]; nO += 1
                nkt = 4 * qb + 4
                for kt in range(nkt):
                    j = kt - 4 * qb
                    c0 = max(j, 0) * 128
                    ps = psS[nS % 3]; nS += 1
                    pt = PTl[nP % 4]; nP += 1
                    k.mm(ps[:, c0:512], [(ka[0:67, kt * 128:(kt + 1) * 128], qa[0:67, qb * 512 + c0:(qb + 1) * 512])],
                         r=[ka, qa], w=[ps])
                    k.act(pt[:, c0:512], ps[:, c0:512], AF.Exp, r=[ps, nb], w=[pt], bias=nb[:, kt, h:h + 1], scale=0.125)
                    if j >= 0:
                        k.S.op('pool', lambda pt=pt, c0=c0: nc.gpsimd.affine_select(
                            out=pt[:, c0:c0 + 128], in_=pt[:, c0:c0 + 128], pattern=[[1, 128]], compare_op=ALU.is_ge,
                            fill=0.0, base=0, channel_multiplier=-1), [pt], [pt])
                    fns = []
                    for qs in range(max(j, 0), 4):
                        fns.append(lambda qs=qs, pt=pt, kt=kt, po=po: nc.tensor.matmul(
                            po[:, qs * 65:(qs + 1) * 65], lhsT=pt[:, qs * 128:(qs + 1) * 128], rhs=VA[:, kt, h, :],
                            start=(kt == 0 and qs == 0), stop=(kt == 4 * qb + qs), skip_group_check=True))
                    k.S.pe_group(fns, [pt, VA], [po])
                r_ = rc[qb % 2]
                pov = po[:, 0:260].rearrange("p (q c) -> p q c", c=65)
                k.S.op('dve', lambda r_=r_, pov=pov: nc.vector.reciprocal(out=r_[:], in_=pov[:, :, 64]), [po], [r_])
                for qs in range(4):
                    k.ts('dve', yb[:, qb * 4 + qs, h * 64:(h + 1) * 64], po[:, qs * 65:qs * 65 + 64], r_[:, qs:qs + 1], None,
                         ALU.mult, None, r=[po, r_], w=[yb])
        k.dma('sp', k.Y[:, 256:640].rearrange("(t p) c -> p t c", p=128), yb[:], r=[yb], w=["Y"])


LW = 1536
LC = 4608
NEG8 = -240000.0


def t5_bucket_np(n):
    n = np.maximum(n, 0)
    nf = np.maximum(n, 1).astype(np.float32)
    large = 16 + (np.log(nf / np.float32(16)) / np.float32(np.log(128 / 16)) * np.float32(16)).astype(np.int32)
    large = np.minimum(large, 31)
    return np.where(n < 16, n, large)


def nsa_host_consts():
    c = {}
    i = np.arange(LW); n = i - 511
    oh = np.zeros((33, LW), np.float32)
    ok = (n >= 0) & (n < 512)
    oh[t5_bucket_np(n)[ok], i[ok]] = 1.0
    oh[32, ~ok] = NEG8
    c["oh_w"] = oh
    i = np.arange(LC); n = i - 2063
    oh = np.zeros((33, LC), np.float32)
    ok = n >= 0
    oh[t5_bucket_np(n)[ok], i[ok]] = 1.0
    oh[32, ~ok] = NEG8
    c["oh_c"] = oh
    E = np.zeros((128, 32, 128), np.float32)
    for kt in range(32):
        E[2 * kt, kt, 0:64] = 1.0
        E[2 * kt + 1, kt, 64:128] = 1.0
    c["E_blk"] = E.astype(NPBF)
    cs = np.arange(256) * 16
    ce = cs + 31
    ss = np.arange(64) * 64
    ov = ((cs[:, None] <= ss[None, :] + 63) & (ce[:, None] >= ss[None, :])).astype(np.float32)
    ov[255] = 0.0
    c["ovl"] = np.ascontiguousarray(ov.reshape(2, 128, 64).transpose(1, 0, 2)).astype(NPBF)
    t = np.arange(S_LEN)
    cur = t // 64
    jb = np.arange(64)
    back = cur[:, None] - jb[None, :]
    valid = back >= 0
    forced = (jb[None, :] == 0) | (valid & (back < 2))
    tkm = (valid & ~forced).astype(np.float32)
    tka = np.where(valid, np.where(forced, 1e4, 0.0), -1.0).astype(np.float32)
    c["tkm"] = np.ascontiguousarray(tkm.reshape(32, 128, 64).transpose(1, 0, 2)).astype(NPBF)
    c["tka"] = np.ascontiguousarray(tka.reshape(32, 128, 64).transpose(1, 0, 2)).astype(NPBF)
    return c


def setup_nsa(k):
    nc = k.nc
    k.rel_bias = k.inp("rel_bias", [32, 6])
    k.oh_w = k.inp("oh_w", [33, LW])
    k.oh_c = k.inp("oh_c", [33, LC])
    k.E_d = k.inp("E_blk", [128, 32, 128], BF16)
    k.ovl_d = k.inp("ovl", [128, 2, 64], BF16)
    k.tkm_d = k.inp("tkm", [128, 32, 64], BF16)
    k.tka_d = k.inp("tka", [128, 32, 64], BF16)
    k.pe_kT = k.inp("nsa_pe_kT", [DEPTH, 64, 32])
    k.pe_vT = k.inp("nsa_pe_vT", [DEPTH, 64, 32])
    k.ck_w1 = k.inp("nsa_ck_w1", [DEPTH, 2048, 128])
    k.cv_w1 = k.inp("nsa_cv_w1", [DEPTH, 2048, 128])
    k.ck_w2 = k.inp("nsa_ck_w2", [DEPTH, 128, 64])
    k.cv_w2 = k.inp("nsa_cv_w2", [DEPTH, 128, 64])
    k.WVW = k.scratch("WVW", [6, 128, LW], BF16)
    k.WVC = k.scratch("WVC", [6, 128, LC], BF16)
    with Scope(k) as sc:
        rb = sc.sb("rb", [33, 6], F32)
        rb31 = sc.sb("rb31", [32, 6], F32)
        rrep = sc.sb("rrep", [33, 6, 128], F32)
        ohw = sc.sb("ohw", [33, LW], F32)
        ohc = sc.sb("ohc", [33, LC], F32)
        ps = [sc.ps("psb%d" % i, [128, 512]) for i in range(2)]
        ev = [sc.sb("evb%d" % i, [128, 512], BF16) for i in range(2)]
        k.dma('sp', rb[0:32, :], k.rel_bias, w=[rb])
        k.dma('sp', rb31[:], k.rel_bias[31:32, :].broadcast_to([32, 6]), w=[rb31])
        k.dma('sp', ohw[:], k.oh_w, w=[ohw])
        k.dma('sp', ohc[:], k.oh_c, w=[ohc])
        k.S.op('dve', lambda: nc.vector.memset(rb[32:33, :], 1.0), [], [rb])
        k.tt('dve', rb[0:32, :], rb[0:32, :], rb31[:], ALU.subtract, r=[rb, rb31], w=[rb])
        k.ts('dve', rb[0:32, :], rb[0:32, :], 8.0, None, ALU.mult, None, r=[rb], w=[rb])
        for h in range(6):
            k.copy('dve', rrep[:, h, :], rb[:, h:h + 1].to_broadcast([33, 128]), r=[rb], w=[rrep])
        n = 0
        for h in range(6):
            for (oh, L, dst) in ((ohw, LW, k.WVW), (ohc, LC, k.WVC)):
                for c0 in range(0, L, 512):
                    p_ = ps[n % 2]; e_ = ev[n % 2]; n += 1
                    k.mm(p_[:], [(rrep[:, h, :], oh[:, c0:c0 + 512])], r=[rrep, oh], w=[p_])
                    k.copy('act' if n % 2 else 'dve', e_[:], p_[:], r=[p_], w=[e_])
                    k.dma('sp', dst[h, :, c0:c0 + 512], e_[:], r=[e_])


class DbgStop(Exception):
    pass


def dbg(k, lvl):
    if getattr(k, 'dbg_stop', None) == lvl:
        raise DbgStop()


def stage_nsa(k, l):
    nc = k.nc
    with Scope(k) as sc:
        Gw = sc.sb("Gw", [128, 6, 1408], BF16)
        Gc = sc.sb("Gc", [128, 6, 2560], BF16)
        E = sc.sb("E", [128, 32, 128], BF16)
        tkm = sc.sb("tkm", [128, 32, 64], BF16)
        tka = sc.sb("tka", [128, 32, 64], BF16)
        QC = [sc.sb("QC%d" % c, [128, S_LEN], BF16) for c in range(3)]
        KS = sc.sb("KS", [128, S_LEN], BF16)
        KW = sc.sb("KW", [128, S_LEN], BF16)
        VS = sc.sb("VS", [128, 32, 2, 65], BF16)
        VW = sc.sb("VW", [128, 32, 2, 65], BF16)
        KCMP = sc.sb("KCMP", [128, 256], BF16)
        VE = sc.sb("VE", [128, 2, 2, 129], BF16)
        sg = sc.sb("sg", [128, 32, 18], F32)
        for h in range(6):
            k.dma('sp', Gw[:, h, :], bass.AP(k.WVW.tensor, h * 128 * LW + 127, [[LW - 1, 128], [1, 1408]]), w=[Gw])
            k.dma('sp', Gc[:, h, :], bass.AP(k.WVC.tensor, h * 128 * LC + 2032, [[LC - 16, 128], [1, 2560]]), w=[Gc])
        k.dma('sp', E[:], k.E_d, w=[E])
        k.dma('sp', tkm[:], k.tkm_d, w=[tkm])
        k.dma('sp', tka[:], k.tka_d, w=[tka])
        for c in range(3):
            k.dma('sp', QC[c][:], k.QKT[768 + c * 128:768 + (c + 1) * 128, :], w=[QC[c]])
        k.dma('sp', KS[:], k.QKT[1408:1536, :], w=[KS])
        k.dma('sp', KW[:], k.QKT[1536:1664, :], w=[KW])
        k.dma('sp', sg[:], k.GT.rearrange("(t p) c -> p t c", p=128), w=[sg])
        k.act(sg[:], sg[:], AF.Exp, r=[sg], w=[sg], scale=-1.0)
        k.ts('dve', sg[:], sg[:], 1.0, None, ALU.add, None, r=[sg], w=[sg])
        k.S.op('dve', lambda: nc.vector.reciprocal(out=sg[:], in_=sg[:]), [sg], [sg])
        k.dma('sp', VE[:, 0, :, 65:129], k.ovl_d, w=[VE])
        k.dma('sp', VE[:, 1, :, 65:129], k.ovl_d, w=[VE])
        k.S.op('pool', lambda: nc.gpsimd.memset(VE[:, :, :, 64:65], 1.0), [], [VE])
        k.S.op('pool', lambda: nc.gpsimd.memset(VE[:, :, :, 0:64], 0.0), [], [VE])
        k.S.op('pool', lambda: nc.gpsimd.memset(KCMP[:], 0.0), [], [KCMP])
        dbg(k, 1)
        with Scope(k) as s2:
            vst = s2.sb("vst", [128, 32, 256], BF16)
            k.dma('sp', vst[:], k.VT[:, 384:640].rearrange("(t p) c -> p t c", p=128), w=[vst])
            k.S.op('pool', lambda: nc.gpsimd.memset(VS[:, :, :, 64:65], 1.0), [], [VS])
            k.S.op('pool', lambda: nc.gpsimd.memset(VW[:, :, :, 64:65], 1.0), [], [VW])
            k.copy('pool', VS[:, :, :, 0:64], vst[:, :, 0:128].rearrange("p t (g d) -> p t g d", g=2), r=[vst], w=[VS])
            k.copy('pool', VW[:, :, :, 0:64], vst[:, :, 128:256].rearrange("p t (g d) -> p t g d", g=2), r=[vst], w=[VW])
        dbg(k, 2)
        with Scope(k) as s2:
            KC = s2.sb("KC", [128, S_LEN], BF16)
            VC = s2.sb("VC", [128, S_LEN], BF16)
            k.dma('sp', KC[:], k.QKT[1152:1280, :], w=[KC])
            k.dma('sp', VC[:], k.QKT[1280:1408, :], w=[VC])
            w1s = s2.sb("w1s", [128, 16, 128], F32)
            w1b = [s2.sb("w1b%d" % i, [128, 32, 128], BF16) for i in range(2)]
            w2s = s2.sb("w2s", [128, 2, 64], F32)
            w2b = s2.sb("w2b", [128, 2, 64], BF16)
            pes = s2.sb("pes", [128, 2, 32], F32)
            peb = s2.sb("peb", [128, 2, 32], BF16)
            hb = s2.sb("hb", [128, 2], F32)
            gx = s2.sb("gx", [128, 256], F32)
            gu = s2.sb("gu", [128, 256], F32)
            gg = s2.sb("gg", [128, 256], BF16)
            psh = s2.ps("psh", [128, 512])
            psb_ = s2.ps("pshb", [128, 512])
            pso = s2.ps("pso", [128, 512])
            for kv, (w1d, w2d, ped) in enumerate(((k.ck_w1, k.ck_w2, k.pe_kT), (k.cv_w1, k.cv_w2, k.pe_vT))):
                for lh in range(2):
                    for half in range(2):
                        k.dma('sp', w1s[half * 64:(half + 1) * 64, :, :],
                              w1d[l, lh * 1024:(lh + 1) * 1024, :].rearrange("(l d) h -> d l h", d=64), w=[w1s])
                    k.copy('pool', w1b[kv][:, lh * 16:(lh + 1) * 16, :], w1s[:], r=[w1s], w=[w1b[kv]])
                for half in range(2):
                    k.dma('sp', pes[half * 64:(half + 1) * 64, kv, :], ped[l], w=[pes])
                k.dma('sp', w2s[:, kv, :], w2d[l], w=[w2s])
            k.copy('dve', w2b[:], w2s[:], r=[w2s], w=[w2b])
            w2kd = s2.sb("w2kd", [128, 2, 64], BF16)
            for a_ in range(2):
                k.copy('dve', w2kd[:, a_, :], w2s[:, 0, :], r=[w2s], w=[w2kd])
            k.copy('dve', peb[:], pes[:], r=[pes], w=[peb])
            for kv in range(2):
                src = KC if kv == 0 else VC
                k.mm(psb_[:, kv:kv + 1], [(w1b[kv][0:64, li, :], peb[0:64, kv, li:li + 1]) for li in range(32)],
                     r=[w1b[kv], peb], w=[psb_], start=True)
                k.copy('dve', hb[:, kv:kv + 1], psb_[:, kv:kv + 1], r=[psb_], w=[hb])
                for g in range(2):
                    pr = slice(g * 64, (g + 1) * 64)
                    k.mm(psh[:, 0:255], [(w1b[kv][pr, li, :], src[pr, li:li + 16 * 254 + 1:16]) for li in range(32)],
                         r=[w1b[kv], src], w=[psh])
                    k.ts('dve', gx[:, 0:255], psh[:, 0:255], hb[:, kv:kv + 1], None, ALU.add, None, r=[psh, hb], w=[gx])
                    k.tt('dve', gu[:, 0:255], gx[:, 0:255], gx[:, 0:255], ALU.mult, r=[gx], w=[gu])
                    k.ts('dve', gu[:, 0:255], gu[:, 0:255], 0.044715, 1.0, ALU.mult, ALU.add, r=[gu], w=[gu])
                    k.tt('dve', gu[:, 0:255], gu[:, 0:255], gx[:, 0:255], ALU.mult, r=[gu, gx], w=[gu])
                    k.act(gu[:, 0:255], gu[:, 0:255], AF.Exp, r=[gu], w=[gu], scale=-2.0 * 0.7978845608028654)
                    k.ts('dve', gu[:, 0:255], gu[:, 0:255], 1.0, None, ALU.add, None, r=[gu], w=[gu])
                    k.S.op('dve', lambda: nc.vector.reciprocal(out=gu[:, 0:255], in_=gu[:, 0:255]), [gu], [gu])
                    k.S.op('dve', lambda: nc.vector.memset(gg[:, 255:256], 0.0), [], [gg])
                    k.tt('dve', gg[:, 0:255], gu[:, 0:255], gx[:, 0:255], ALU.mult, r=[gu, gx], w=[gg])
                    if kv == 0:
                        k.mm(pso[:, 0:256], [(w2kd[:].rearrange("p a d -> p (a d)"), gg[:, 0:256])], r=[w2kd, gg], w=[pso])
                        k.copy('dve', KCMP[pr, :], pso[pr, 0:256], r=[pso], w=[KCMP])
                    else:
                        for ct in range(2):
                            k.mm(pso[:, ct * 64:(ct + 1) * 64], [(gg[:, ct * 128:(ct + 1) * 128], w2b[:, 1, :])],
                                 r=[w2b, gg], w=[pso], start=(ct == 0))
                        k.copy('dve', VE[:, g, :, 0:64], pso[:, 0:128].rearrange("p (c d) -> p c d", c=2), r=[pso], w=[VE])
        dbg(k, 3)
        NM = [sc.sb("NM%d" % g, [128, 512], BF16) for g in range(2)]
        for g in range(2):
            k.S.op('pool', lambda g=g: nc.gpsimd.memset(NM[g][:], 0.0), [], [NM[g]])
        PTl = [sc.sb("PTn%d" % i, [128, 512], BF16) for i in range(4)]
        yacc = [sc.sb("yacc%d" % i, [128, 4, 384], F32) for i in range(2)]
        ybf = [sc.sb("ybf%d" % i, [128, 4, 384], BF16) for i in range(2)]
        impt = [sc.sb("impt%d" % g, [128, 4, 64], F32) for g in range(2)]
        scr = sc.sb("scr", [128, 4, 64], F32)
        wk = sc.sb("wk", [128, 4, 64], F32)
        m8 = sc.sb("m8", [128, 4, 16], F32)
        nmq = sc.sb("nmq", [128, 4, 64], BF16)
        rcs = [sc.sb("rcs%d" % i, [128, 8], F32) for i in range(3)]
        psS = [sc.ps("psS%d" % i, [128, 512]) for i in range(3)]
        psO = [sc.ps("psO%d" % i, [128, 512]) for i in range(3)]
        psT = sc.ps("psTn", [128, 1024], BF16)
        st = {"S": 0, "O": 0, "P": 0, "R": 0}

        def q_ap(h, c0, c1):
            g, hp = h // 3, h % 3
            return QC[hp][g * 64:(g + 1) * 64, c0:c1]

        def evac(views, h, branch, qb, ya, first):
            r_ = rcs[st["R"] % 3]; st["R"] += 1
            for qs, (po, cb) in enumerate(views):
                if branch == 0:
                    k.ts('dve', r_[:, qs:qs + 1], po[:, cb + 64:cb + 65], 1e-30, None, ALU.max, None, r=[po], w=[r_])
                    k.S.op('dve', lambda r_=r_, qs=qs: nc.vector.reciprocal(out=r_[:, qs:qs + 1], in_=r_[:, qs:qs + 1]), [r_], [r_])
                else:
                    k.S.op('dve', lambda r_=r_, po=po, cb=cb, qs=qs: nc.vector.reciprocal(out=r_[:, qs:qs + 1], in_=po[:, cb + 64:cb + 65]), [po], [r_])
            k.tt('dve', r_[:, 4:8], r_[:, 0:4], sg[:, qb * 4:(qb + 1) * 4, h * 3 + branch], ALU.mult, r=[r_, sg], w=[r_])
            for qs, (po, cb) in enumerate(views):
                o = ya[:, qs, h * 64:(h + 1) * 64]
                if first:
                    k.ts('dve', o, po[:, cb:cb + 64], r_[:, 4 + qs:5 + qs], None, ALU.mult, None, r=[po, r_], w=[ya])
                else:
                    k.stt('dve', o, po[:, cb:cb + 64], r_[:, 4 + qs:5 + qs], o, ALU.mult, ALU.add, r=[po, r_, ya], w=[ya])
            return r_

        def attend(h, qb, tiles, kmat, vmat, po, g):
            nt = len(tiles)
            first_o = True
            for idx, (kt, c0, c1, extra) in enumerate(tiles):
                ps = psS[st["S"] % 3]; st["S"] += 1
                pt = PTl[st["P"] % 4]; st["P"] += 1
                pairs = [(kmat[g * 64:(g + 1) * 64, kt * 128:(kt + 1) * 128], q_ap(h, qb * 512 + c0, qb * 512 + c1))]
                rd = [kmat, QC[h % 3]]
                for (lt, rt, lap, rap) in extra:
                    pairs.append((lap, rap)); rd += [lt, rt]
                k.mm(ps[:, c0:c1], pairs, r=rd, w=[ps])
                k.act(pt[:, c0:c1], ps[:, c0:c1], AF.Exp, r=[ps], w=[pt], scale=0.125)
                fns = []
                for qs in range(c0 // 128, (c1 + 127) // 128):
                    last = all(not (t2[1] <= qs * 128 < t2[2]) for t2 in tiles[idx + 1:])
                    fns.append(lambda qs=qs, pt=pt, kt=kt, fo=first_o, last=last: nc.tensor.matmul(
                        po[:, qs * 65:(qs + 1) * 65], lhsT=pt[:, qs * 128:(qs + 1) * 128], rhs=vmat[:, kt, g, :],
                        start=fo, stop=last, skip_group_check=True))
                    first_o = False
                k.S.pe_group(fns, [pt, vmat], [po])

        for qb in getattr(k, 'dbg_qbs', range(NTB)):
            ya = yacc[qb % 2]
            for h in range(6):
                g = h // 3
                poA = psO[st["O"] % 3]; st["O"] += 1
                poB = psO[st["O"] % 3]; st["O"] += 1
                cts = [0] + ([1] if qb >= 4 else [])
                firstA = True; firstB = True
                for ct in cts:
                    delta = 512 * qb - 2048 * ct
                    ps = psS[st["S"] % 3]; st["S"] += 1
                    pt = PTl[st["P"] % 4]; st["P"] += 1
                    pairs = [(KCMP[g * 64:(g + 1) * 64, ct * 128:(ct + 1) * 128], q_ap(h, qb * 512, (qb + 1) * 512))]
                    rd = [KCMP, QC[h % 3]]
                    if delta < 2560:
                        pairs.append((k.ident_bf[:], Gc[:, h, delta:delta + 512])); rd += [k.ident_bf, Gc]
                    k.mm(ps[:], pairs, r=rd, w=[ps])
                    k.act(pt[:], ps[:], AF.Exp, r=[ps], w=[pt], scale=0.125)
                    fns = []
                    for qs in range(4):
                        po, cb = (poA, qs * 129) if qs < 3 else (poB, 0)
                        stt_ = (firstA if qs < 3 else firstB)
                        if qs < 3: firstA = False
                        else: firstB = False
                        fns.append(lambda qs=qs, pt=pt, ct=ct, po=po, cb=cb, stt_=stt_: nc.tensor.matmul(
                            po[:, cb:cb + 129], lhsT=pt[:, qs * 128:(qs + 1) * 128], rhs=VE[:, g, ct, :],
                            start=stt_, stop=(ct == cts[-1]), skip_group_check=True))
                    k.S.pe_group(fns, [pt, VE], [poA, poB])
                views = [(poA, 0), (poA, 129), (poA, 258), (poB, 0)]
                r_ = evac(views, h, 0, qb, ya, True)
                for qs, (po, cb) in enumerate(views):
                    o = impt[g][:, qs, :]
                    if h % 3 == 0:
                        k.ts('dve', o, po[:, cb + 65:cb + 129], r_[:, qs:qs + 1], None, ALU.mult, None, r=[po, r_], w=[impt[g]])
                    else:
                        k.stt('dve', o, po[:, cb + 65:cb + 129], r_[:, qs:qs + 1], o, ALU.mult, ALU.add, r=[po, r_, impt[g]], w=[impt[g]])
            dbg(k, 4)
            for g in range(2):
                k.tt('dve', scr[:], impt[g][:], tkm[:, qb * 4:(qb + 1) * 4, :], ALU.mult, r=[impt[g], tkm], w=[scr])
                k.tt('dve', scr[:], scr[:], tka[:, qb * 4:(qb + 1) * 4, :], ALU.add, r=[scr, tka], w=[scr])
                for qs in range(4):
                    k.S.op('dve', lambda qs=qs: nc.vector.max(out=m8[:, qs, 0:8], in_=scr[:, qs, :]), [scr], [m8])
                    k.S.op('dve', lambda qs=qs: nc.vector.match_replace(out=wk[:, qs, :], in_to_replace=m8[:, qs, 0:8],
                                                                        in_values=scr[:, qs, :], imm_value=-1e9), [scr, m8], [wk])
                    k.S.op('dve', lambda qs=qs: nc.vector.max(out=m8[:, qs, 8:16], in_=wk[:, qs, :]), [wk], [m8])
                    k.ts('dve', wk[:, qs, :], scr[:, qs, :], m8[:, qs, 15:16], 1.0, ALU.is_ge, ALU.subtract, r=[scr, m8, wk], w=[wk])
                k.ts('dve', nmq[:], wk[:], -NEG8, None, ALU.mult, None, r=[wk], w=[nmq])
                for qs in range(4):
                    k.transpose(psT[0:64, qs * 128:(qs + 1) * 128], nmq[:, qs, :], k.ident_bf[:], r=[nmq, k.ident_bf], w=[psT])
                k.copy('dve', NM[g][0:64, :], psT[0:64, 0:512], r=[psT], w=[NM[g]])
            k._dbg = dict(NM=NM, scr=scr, m8=m8, impt=impt, nmq=nmq, wk=wk)
            dbg(k, 5)
            for h in (range(6) if not getattr(k, 'dbg_nowin', False) else []):
                g = h // 3
                po = psO[st["O"] % 3]; st["O"] += 1
                tiles = []
                for kt in range(max(0, 4 * qb - 4), 4 * qb + 4):
                    delta = 512 * qb - 128 * kt
                    c0 = max(-delta, 0)
                    c1 = min(512, 640 - delta) if delta > 0 else 512
                    tiles.append((kt, c0, c1, [(k.ident_bf, Gw, k.ident_bf[:], Gw[:, h, delta + 384 + c0:delta + 384 + c1])]))
                attend(h, qb, tiles, KW, VW, po, g)
                evac([(po, qs * 65) for qs in range(4)], h, 2, qb, ya, False)
            dbg(k, 6)
            for h in range(6):
                g = h // 3
                po = psO[st["O"] % 3]; st["O"] += 1
                tiles = []
                for kt in range(0, 4 * qb + 4):
                    delta = 512 * qb - 128 * kt
                    c0 = max(-delta, 0)
                    ex = [(E, NM[g], E[:, kt, :], NM[g][:, c0:512])] if not getattr(k, 'dbg_noE', False) else []
                    if delta <= 128:
                        c1b = 256 if delta == 128 else 512
                        ex.append((k.ident_bf, Gw, k.ident_bf[:], Gw[:, h, delta + 384 + c0:delta + 384 + c1b]))
                    tiles.append((kt, c0, 512, ex))
                attend_slc(k, nc, st, psS, PTl, QC, KS, VS, po, h, qb, tiles, g)
                evac([(po, qs * 65) for qs in range(4)], h, 1, qb, ya, False)
            dbg(k, 7)
            yb_ = ybf[qb % 2]
            k.copy('pool', yb_[:], ya[:], r=[ya], w=[yb_])
            k.dma('sp', k.Y[qb * 512:(qb + 1) * 512, 640:1024].rearrange("(q p) c -> p q c", p=128), yb_[:], r=[yb_])
            dbg(k, 100 + qb)


def attend_slc(k, nc, st, psS, PTl, QC, kmat, vmat, po, h, qb, tiles, g):
    hp = h % 3
    first_o = True
    for idx, (kt, c0, c1, extra) in enumerate(tiles):
        ps = psS[st["S"] % 3]; st["S"] += 1
        pt = PTl[st["P"] % 4]; st["P"] += 1
        fns = [lambda: nc.tensor.matmul(ps[:, c0:c1], lhsT=kmat[g * 64:(g + 1) * 64, kt * 128:(kt + 1) * 128],
                                        rhs=QC[hp][g * 64:(g + 1) * 64, qb * 512 + c0:qb * 512 + c1], start=True, stop=False,
                                        skip_group_check=True)]
        rd = [kmat, QC[hp]]
        for ei, (lt, rt, lap, rap) in enumerate(extra):
            w_ = rap.shape[-1]
            fns.append(lambda lap=lap, rap=rap, w_=w_, ei=ei: nc.tensor.matmul(
                ps[:, c0:c0 + w_], lhsT=lap, rhs=rap, start=False, stop=(ei == len(extra) - 1), skip_group_check=True))
            rd += [lt, rt]
        k.S.pe_group(fns, rd, [ps])
        k.act(pt[:, c0:c1], ps[:, c0:c1], AF.Exp, r=[ps], w=[pt], scale=0.125)
        fns = []
        for qs in range(c0 // 128, 4):
            last = (kt == 4 * qb + qs)
            fns.append(lambda qs=qs, fo=first_o, last=last: nc.tensor.matmul(
                po[:, qs * 65:(qs + 1) * 65], lhsT=pt[:, qs * 128:(qs + 1) * 128], rhs=vmat[:, kt, g, :],
                start=fo, stop=last, skip_group_check=True))
            first_o = False
        k.S.pe_group(fns, [pt, vmat], [po])


def setup_ffn(k):
    k.w_out = k.inp("w_out", [DEPTH, D, D])
    k.ffn_up = k.inp("ffn_up", [DEPTH, D, 2 * D_FF])
    k.ffn_down = k.inp("ffn_down", [DEPTH, D_FF, D])
    k.conv_w = k.inp("conv_w_fm", [DEPTH, 128, 3, 44])
    k.conv_b = k.inp("conv_b_fm", [DEPTH, 128, 44])


def load_cast(k, sc, dst, src_rows, ncols, nchunks, name, col_split=1):
    w = ncols // col_split
    stg = [sc.sb("%s_stg%d" % (name, i), [128, w], F32) for i in range(2)]
    n = 0
    for c in range(nchunks):
        for cs in range(col_split):
            s = stg[n % 2]
            k.dma('sp', s[:], src_rows(c)[:, cs * w:(cs + 1) * w], w=[s])
            k.copy('pool' if n % 2 == 0 else 'dve', dst[:, c, cs * w:(cs + 1) * w], s[:], r=[s], w=[dst])
            n += 1


def rms_scale(k, ss, st):
    k.act(st[:, 0:1], ss, AF.Ln, r=[st], w=[st], bias=RMS_EPS)
    k.act(st[:, 1:2], st[:, 0:1], AF.Exp, r=[st], w=[st], scale=-0.5)


def stage_out(k, l, xsrc, xdst):
    nc = k.nc
    with Scope(k) as sc:
        wo = sc.sb("wo", [128, 8, D], BF16)
        with Scope(k) as s2:
            load_cast(k, s2, wo, lambda c: k.w_out[l, c * 128:(c + 1) * 128, :], D, 8, "wo")
        yt = [sc.sb("yt%d" % i, [128, D], BF16) for i in range(2)]
        yT = [sc.sb("yT%d" % i, [128, 8, 128], BF16) for i in range(2)]
        xt = [sc.sb("xo%d" % i, [128, D], F32) for i in range(2)]
        tt_ = [sc.sb("to%d" % i, [128, D], F32) for i in range(2)]
        junk = sc.sb("junko", [128, 512], BF16)
        st = [sc.sb("sto%d" % i, [128, 4], F32) for i in range(2)]
        psT = [sc.ps("psTo%d" % i, [128, D], BF16) for i in range(2)]
        psY = [sc.ps("psYo%d" % i, [128, 512]) for i in range(4)]
        for ti in range(32):
            tok = slice(ti * 128, (ti + 1) * 128)
            y_ = yt[ti % 2]; yT_ = yT[ti % 2]; x_ = xt[ti % 2]; t_ = tt_[ti % 2]; st_ = st[ti % 2]; pT = psT[ti % 2]
            p0 = psY[(ti % 2) * 2]; p1 = psY[(ti % 2) * 2 + 1]
            k.dma('sp', y_[:], k.Y[tok, :], w=[y_])
            k.dma('sp', x_[:], xsrc[tok, :], w=[x_])
            for kc in range(8):
                k.transpose(pT[:, kc * 128:(kc + 1) * 128], y_[:, kc * 128:(kc + 1) * 128], k.ident_bf[:], r=[y_, k.ident_bf], w=[pT])
            k.copy('act' if ti % 2 else 'dve', yT_[:].rearrange("p a b -> p (a b)"), pT[:], r=[pT], w=[yT_])
            for half, ps in enumerate((p0, p1)):
                k.mm(ps[:], [(yT_[:, kc, :], wo[:, kc, half * 512:(half + 1) * 512]) for kc in range(8)], r=[yT_, wo], w=[ps])
                k.act(junk[:], ps[:], AF.Square, r=[ps], w=[junk, st_], scale=1.0 / 32.0, accum=st_[:, 2 + half:3 + half])
            k.tt('dve', st_[:, 2:3], st_[:, 2:3], st_[:, 3:4], ALU.add, r=[st_], w=[st_])
            rms_scale(k, st_[:, 2:3], st_)
            for half, ps in enumerate((p0, p1)):
                cs = slice(half * 512, (half + 1) * 512)
                k.stt('dve', t_[:, cs], ps[:], st_[:, 1:2], k.gm_row[:, cs], ALU.mult, ALU.mult, r=[ps, st_, k.gm_row], w=[t_])
            k.tt('pool', t_[:], t_[:], x_[:], ALU.add, r=[t_, x_], w=[t_])
            k.dma('sp', xdst[tok, :], t_[:], r=[t_])


def stage_ffn(k, l, xsrc, xdst):
    nc = k.nc
    NCH = 22
    with Scope(k) as sc:
        wu = sc.sb("wu", [128, 8, 2 * D_FF], BF16)
        wd = sc.sb("wd", [128, NCH, D], BF16)
        with Scope(k) as s2:
            load_cast(k, s2, wu, lambda c: k.ffn_up[l, c * 128:(c + 1) * 128, :], 2 * D_FF, 8, "wu", col_split=2)
            load_cast(k, s2, wd, lambda c: k.ffn_down[l, c * 128:(c + 1) * 128, :], D, NCH, "wd")
        cw = sc.sb("cw", [128, 3, 44], F32)
        cb = sc.sb("cb", [128, 44], F32)
        hal = [sc.sb("hal%d" % i, [128, 44, 2], F32) for i in range(2)]
        k.dma('sp', cw[:], k.conv_w[l], w=[cw])
        k.dma('sp', cb[:], k.conv_b[l], w=[cb])
        k.S.op('pool', lambda: nc.gpsimd.memset(hal[1][:], 0.0), [], [hal[1]])
        actT = sc.sb("actT", [128, NCH, 512], BF16)
        HT = sc.sb("H2T", [128, 8, 512], BF16)
        xt = [sc.sb("xf%d" % i, [128, D], F32) for i in range(2)]
        xn = sc.sb("xnf", [128, D], BF16)
        junk = sc.sb("junkf", [128, D], BF16)
        st = [sc.sb("stf%d" % i, [128, 4], F32) for i in range(2)]
        Tg = [sc.sb("Tg%d" % i, [128, 512], F32) for i in range(2)]
        Tv = [sc.sb("Tv%d" % i, [128, 512], F32) for i in range(2)]
        psT = sc.ps("psTf", [128, D], BF16)
        psU = [sc.ps("psU%d" % i, [128, 512]) for i in range(4)]
        psF = [sc.ps("psF%d" % i, [128, 512]) for i in range(2)]
        nx = 0
        for tb in range(NTB):
            hin = hal[(tb + 1) % 2]; hout = hal[tb % 2]
            for sub in range(4):
                ti = tb * 4 + sub
                x_ = xt[nx % 2]; st_ = st[nx % 2]; nx += 1
                k.dma('sp', x_[:], xsrc[ti * 128:(ti + 1) * 128, :], w=[x_])
                k.act(junk[:], x_[:], AF.Square, r=[x_], w=[junk, st_], scale=1.0 / 32.0, accum=st_[:, 2:3])
                rms_scale(k, st_[:, 2:3], st_)
                k.ts('dve', xn[:], x_[:], st_[:, 1:2], None, ALU.mult, None, r=[x_, st_], w=[xn])
                for kc in range(8):
                    k.transpose(psT[:, kc * 128:(kc + 1) * 128], xn[:, kc * 128:(kc + 1) * 128], k.ident_bf[:], r=[xn, k.ident_bf], w=[psT])
                for kc in range(8):
                    o = HT[:, kc, sub * 128:(sub + 1) * 128]
                    i_ = psT[:, kc * 128:(kc + 1) * 128]
                    if kc % 2 == 0:
                        k.ts('dve', o, i_, k.modAB[:, 16 + kc:17 + kc], k.modAB[:, 24 + kc:25 + kc], ALU.mult, ALU.add, r=[psT, k.modAB], w=[HT])
                    else:
                        k.act(o, i_, AF.Identity, r=[psT, k.modAB], w=[HT], scale=k.modAB[:, 16 + kc:17 + kc], bias=k.modAB[:, 24 + kc:25 + kc])
            for cp in range(NCH):
                tg = Tg[cp % 2]; tv = Tv[cp % 2]
                for which, (T_, c_) in enumerate(((tg, cp), (tv, NCH + cp))):
                    ps = psU[(cp * 2 + which) % 4]
                    k.mm(ps[:], [(wu[:, kc, c_ * 128:(c_ + 1) * 128], HT[:, kc, :]) for kc in range(8)], r=[wu, HT], w=[ps])
                    k.act(T_[:], ps[:], AF.Identity, r=[ps, cw, cb], w=[T_], scale=cw[:, 2, c_:c_ + 1], bias=cb[:, c_:c_ + 1])
                    k.stt('dve', T_[:, 1:512], ps[:, 0:511], cw[:, 1, c_:c_ + 1], T_[:, 1:512], ALU.mult, ALU.add, r=[ps, cw, T_], w=[T_])
                    k.stt('dve', T_[:, 2:512], ps[:, 0:510], cw[:, 0, c_:c_ + 1], T_[:, 2:512], ALU.mult, ALU.add, r=[ps, cw, T_], w=[T_])
                    k.copy('act', hout[:, c_, :], ps[:, 510:512], r=[ps], w=[hout])
                    k.stt('dve', T_[:, 0:1], hin[:, c_, 1:2], cw[:, 1, c_:c_ + 1], T_[:, 0:1], ALU.mult, ALU.add, r=[hin, cw, T_], w=[T_])
                    k.stt('dve', T_[:, 0:2], hin[:, c_, 0:2], cw[:, 0, c_:c_ + 1], T_[:, 0:2], ALU.mult, ALU.add, r=[hin, cw, T_], w=[T_])
                k.act(tg[:], tg[:], AF.Silu, r=[tg], w=[tg])
                k.tt('pool', actT[:, cp, :], tg[:], tv[:], ALU.mult, r=[tg, tv], w=[actT])
            for sub in range(4):
                ti = tb * 4 + sub
                tok = slice(ti * 128, (ti + 1) * 128)
                x_ = xt[nx % 2]; st_ = st[nx % 2]; nx += 1
                k.dma('sp', x_[:], xsrc[tok, :], w=[x_])
                for half in range(2):
                    ps = psF[half]
                    k.mm(ps[:], [(actT[:, cp, sub * 128:(sub + 1) * 128], wd[:, cp, half * 512:(half + 1) * 512]) for cp in range(NCH)],
                         r=[actT, wd], w=[ps])
                    k.act(junk[:, 0:512], ps[:], AF.Square, r=[ps], w=[junk, st_], scale=1.0 / 32.0, accum=st_[:, 2 + half:3 + half])
                k.tt('dve', st_[:, 2:3], st_[:, 2:3], st_[:, 3:4], ALU.add, r=[st_], w=[st_])
                rms_scale(k, st_[:, 2:3], st_)
                t_ = Tg[sub % 2] if False else None
                for half in range(2):
                    cs = slice(half * 512, (half + 1) * 512)
                    T_ = (Tg if half == 0 else Tv)[sub % 2]
                    k.stt('dve', T_[:], psF[half][:], st_[:, 1:2], k.gf_row[:, cs], ALU.mult, ALU.mult, r=[psF[half], st_, k.gf_row], w=[T_])
                    k.tt('pool', x_[:, cs], x_[:, cs], T_[:], ALU.add, r=[x_, T_], w=[x_])
                k.dma('sp', xdst[tok, :], x_[:], r=[x_])


def rwkv_host(inp):
    f = lambda a: np.ascontiguousarray(np.asarray(a, dtype=np.float32))
    mu = np.asarray(inp["rwkv_mu"])
    hd = lambda v: np.asarray(v).reshape(DEPTH, 4, 64).transpose(0, 2, 1)
    pp = np.stack([hd(mu[:, 0:256]), hd(mu[:, 256:512]), hd(mu[:, 512:768]), hd(inp["rwkv_w0"]), hd(inp["rwkv_a0"]),
                   hd(inp["rwkv_k_k"]), hd(inp["rwkv_k_a"]), hd(np.asarray(inp["rwkv_r_k"]).reshape(DEPTH, 256))], axis=2)
    lr = np.zeros((DEPTH, 64, 3), np.float32)
    lr[:, 0:32, 0] = mu[:, 768:800]; lr[:, 0:32, 1] = mu[:, 800:832]; lr[:, :, 2] = mu[:, 832:896]
    i = np.arange(64)
    mk = np.stack([(i[:, None] < i[None, :]), (i[:, None] > i[None, :]), (i[:, None] <= i[None, :]), np.eye(64, dtype=bool)]).astype(np.float32)
    cm = np.ones((64, 512), np.float32); cm[:, ::64] = 0.0
    return {"rwkv_pp": f(pp), "rwkv_lr": f(lr), "rwkv_w_up": f(inp["rwkv_w_up"]), "rwkv_a_up": f(inp["rwkv_a_up"]),
            "rwkv_g_up": f(inp["rwkv_g_up"]), "rwkv_ln": f(np.stack([np.asarray(inp["rwkv_ln_w"]), np.asarray(inp["rwkv_ln_b"])], axis=1)),
            "rwkv_masks": f(mk.transpose(1, 0, 2)), "rwkv_cmask": cm}


def setup_rwkv(k):
    k.rw_pp = k.inp("rwkv_pp", [DEPTH, 64, 8, 4])
    k.rw_lr = k.inp("rwkv_lr", [DEPTH, 64, 3])
    k.rw_wup = k.inp("rwkv_w_up", [DEPTH, 32, 256])
    k.rw_aup = k.inp("rwkv_a_up", [DEPTH, 32, 256])
    k.rw_gup = k.inp("rwkv_g_up", [DEPTH, 64, 256])
    k.rw_ln = k.inp("rwkv_ln", [DEPTH, 2, 256])
    k.rw_masks = k.inp("rwkv_masks", [64, 4, 64])
    k.rw_cmask = k.inp("rwkv_cmask", [64, 512])


def stage_rwkv(k, l):
    nc = k.nc
    H4 = [64, 4, 512]
    with Scope(k) as sc:
        pp = sc.sb("pp", [64, 8, 4], F32)
        lr = sc.sb("lr", [64, 3], F32)
        wup = sc.sb("wup", [32, 256], F32); aup = sc.sb("aup", [32, 256], F32); gup = sc.sb("gup", [64, 256], F32)
        lnr = sc.sb("lnr", [64, 2, 256], F32)
        mk = sc.sb("mk", [64, 4, 64], F32)
        cmask = sc.sb("cmask", [64, 512], F32)
        ones = sc.sb("ones64", [64, 64], F32)
        prm = sc.sb("prm", [64, 4, 4], F32)
        k.dma('sp', pp[:], k.rw_pp[l], w=[pp]); k.dma('sp', lr[:], k.rw_lr[l], w=[lr])
        k.dma('sp', wup[:], k.rw_wup[l], w=[wup]); k.dma('sp', aup[:], k.rw_aup[l], w=[aup]); k.dma('sp', gup[:], k.rw_gup[l], w=[gup])
        for i in range(2):
            k.dma('sp', lnr[:, i, :], k.rw_ln[l, i:i + 1, :].broadcast_to([64, 256]), w=[lnr])
        k.dma('sp', mk[:], k.rw_masks, w=[mk]); k.dma('sp', cmask[:], k.rw_cmask, w=[cmask])
        k.S.op('pool', lambda: nc.gpsimd.memset(ones[:], 1.0), [], [ones])
        k.ts('dve', prm[:, 0, :], pp[:, 3, :], -1.0, None, ALU.mult, None, r=[pp], w=[prm])
        k.ts('dve', prm[:, 1, :], pp[:, 6, :], -1.0, 1.0, ALU.mult, ALU.add, r=[pp], w=[prm])
        P3 = sc.sb("P3", [64, 3, 4, 512], F32)
        halo = sc.sb("halo", [64, 3, 4], F32)
        LR = sc.sb("LR", [64, 3, 512], F32)
        halo2 = sc.sb("halo2", [64, 3], F32)
        ELW = sc.sb("ELW", H4, F32); SC_ = sc.sb("SCAN", H4, F32); AA = sc.sb("AA", H4, F32); KKN = sc.sb("KKN", H4, F32)
        T1 = sc.sb("T1", H4, F32); T2 = sc.sb("T2", H4, F32)
        AT = sc.sb("AT", H4, F32); BT = sc.sb("BT", H4, F32); KT = sc.sb("KT", H4, F32); RT = sc.sb("RT", H4, F32)
        RK = sc.sb("RK", H4, F32); GAM = sc.sb("GAM", H4, F32)
        SG = sc.sb("SG", [64, 512], F32)
        XY = [sc.sb("XY%d" % i, [64, 2, 4, 64], F32) for i in range(2)]
        PP = [sc.sb("PPi%d" % i, [64, 4, 64], F32) for i in range(2)]
        AKRK = sc.sb("AKRK", [64, 2, 4, 64], F32)
        RBT = sc.sb("RBT", [64, 4, 64], F32)
        TOK = sc.sb("TOK", [64, 3, 4, 64], F32)
        Wsb = sc.sb("Wsb", [64, 4, 64], F32); Usb = sc.sb("Usb", [64, 4, 64], F32)
        Hs = [sc.sb("Hs%d" % i, [64, 4, 64], F32) for i in range(2)]
        yc = sc.sb("yc", [64, 4, 64], F32); ysq = sc.sb("ysq", [64, 4, 64], F32)
        sm = sc.sb("sm", [64, 6, 4], F32)
        yab = sc.sb("yab", [64, 8, 256], BF16)
        psM = sc.ps("psMr", [64, 512]); psK = sc.ps("psKr", [64, 512]); psB = sc.ps("psBr", [64, 512])
        psX = sc.ps("psXr", [64, 512]); psC = sc.ps("psCr", [64, 512]); psH = sc.ps("psHr", [64, 512])
        psY = sc.ps("psYr", [64, 512]); psP = sc.ps("psPr", [64, 512])
        k.S.op('pool', lambda: nc.gpsimd.memset(Hs[1][:], 0.0), [], [Hs[1]])
        k.S.op('pool', lambda: nc.gpsimd.memset(halo[:], 0.0), [], [halo])
        k.S.op('pool', lambda: nc.gpsimd.memset(halo2[:], 0.0), [], [halo2])
        bc = lambda ap, shape: ap.to_broadcast(shape)
        nchunk = 0
        for tb in range(NTB):
            t0 = tb * 512
            for q in range(3):
                k.dma('sp', P3[:, q, :, :], k.PT[q * 256:(q + 1) * 256, t0:t0 + 512].rearrange("(h d) t -> d h t", d=64), w=[P3])
            k.dma('sp', LR[0:32, 0, :], k.PT[768:800, t0:t0 + 512], w=[LR])
            k.dma('sp', LR[0:32, 1, :], k.PT[800:832, t0:t0 + 512], w=[LR])
            k.dma('sp', LR[:, 2, :], k.PT[832:896, t0:t0 + 512], w=[LR])
            for q in range(3):
                p_ = P3[:, q, :, :]
                k.tt('dve', T1[:, :, 1:512], p_[:, :, 0:511], p_[:, :, 1:512], ALU.subtract, r=[P3], w=[T1])
                k.tt('dve', T1[:, :, 0:1], halo[:, q, :].unsqueeze(2), p_[:, :, 0:1], ALU.subtract, r=[P3, halo], w=[T1])
                k.copy('pool', halo[:, q, :].unsqueeze(2), p_[:, :, 511:512], r=[P3, T1], w=[halo])
                k.tt('pool', T1[:], T1[:], bc(pp[:, q, :].unsqueeze(2), H4), ALU.mult, r=[T1, pp], w=[T1])
                k.tt('pool', p_, p_, T1[:], ALU.add, r=[P3, T1, halo], w=[P3])
            for q, rows in ((0, 32), (1, 32), (2, 64)):
                x_ = LR[0:rows, q, :]
                t_ = T2[0:rows, 0, :]
                k.tt('dve', t_[:, 1:512], x_[:, 0:511], x_[:, 1:512], ALU.subtract, r=[LR], w=[T2])
                k.tt('dve', t_[:, 0:1], halo2[0:rows, q:q + 1], x_[:, 0:1], ALU.subtract, r=[LR, halo2], w=[T2])
                k.copy('dve', halo2[0:rows, q:q + 1], x_[:, 511:512], r=[LR, T2], w=[halo2])
                k.stt('dve', x_, t_, lr[0:rows, q:q + 1], x_, ALU.mult, ALU.add, r=[T2, lr, LR, halo2], w=[LR])
            R_ = P3[:, 0, :, :]; Kp = P3[:, 1, :, :]; V_ = P3[:, 2, :, :]
            k.act(LR[0:32, 0, :], LR[0:32, 0, :], AF.Tanh, r=[LR], w=[LR])
            k.act(SG[:], LR[:, 2, :], AF.Sigmoid, r=[LR], w=[SG])
            for h in range(4):
                k.mm(psX[:, :], [(wup[:, h * 64:(h + 1) * 64], LR[0:32, 0, :])], r=[wup, LR], w=[psX])
                k.act(T1[:, h, :], psX[:, :], AF.Exp, r=[psX, prm], w=[T1], scale=-1.0, bias=prm[:, 0, h:h + 1])
                k.mm(psC[:, :], [(aup[:, h * 64:(h + 1) * 64], LR[0:32, 1, :])], r=[aup, LR], w=[psC])
                k.act(AA[:, h, :], psC[:, :], AF.Sigmoid, r=[psC, pp], w=[AA], bias=pp[:, 4, h:h + 1])
            k.act(T1[:], T1[:], AF.Ln, r=[T1], w=[T1], bias=1.0)
            k.act(ELW[:], T1[:], AF.Exp, r=[T1], w=[ELW], scale=-1.0, bias=-0.5)
            k.copy('dve', T2[:], bc(cmask[:].unsqueeze(1), H4), r=[cmask], w=[T2])
            k.S.op('dve', lambda: nc.vector.tensor_tensor_scan(
                out=SC_[:].rearrange("p h t -> p (h t)"), data0=T2[:].rearrange("p h t -> p (h t)"),
                data1=ELW[:].rearrange("p h t -> p (h t)"), initial=0.0, op0=ALU.mult, op1=ALU.add), [T2, ELW], [SC_])
            k.tt('pool', KKN[:], Kp, bc(pp[:, 5, :].unsqueeze(2), H4), ALU.mult, r=[P3, pp], w=[KKN])
            k.tt('pool', T1[:], KKN[:], KKN[:], ALU.mult, r=[KKN], w=[T1])
            for h in range(4):
                k.mm(psX[:, :], [(ones[:], T1[:, h, :])], r=[ones, T1], w=[psX])
                k.act(T2[:, h, :], psX[:, :], AF.Ln, r=[psX], w=[T2], bias=1e-24)
            k.act(T2[:], T2[:], AF.Exp, r=[T2], w=[T2], scale=-0.5)
            k.tt('dve', KKN[:], KKN[:], T2[:], ALU.mult, r=[KKN, T2], w=[KKN])
            k.tt('pool', T1[:], SC_[:], ELW[:], ALU.subtract, r=[SC_, ELW], w=[T1])
            k.act(T1[:], T1[:], AF.Exp, r=[T1], w=[T1], scale=-1.0)
            k.stt('dve', AT[:], KKN[:], -1.0, T1[:], ALU.mult, ALU.mult, r=[KKN, T1], w=[AT])
            k.act(T2[:], SC_[:], AF.Exp, r=[SC_], w=[T2])
            k.tt('pool', T1[:], KKN[:], AA[:], ALU.mult, r=[KKN, AA], w=[T1])
            k.tt('dve', BT[:], T1[:], T2[:], ALU.mult, r=[T1, T2], w=[BT])
            k.tt('pool', T1[:], AA[:], bc(pp[:, 6, :].unsqueeze(2), H4), ALU.mult, r=[AA, pp], w=[T1])
            k.tt('pool', T1[:], T1[:], bc(prm[:, 1, :].unsqueeze(2), H4), ALU.add, r=[T1, prm], w=[T1])
            k.tt('dve', Kp, Kp, T1[:], ALU.mult, r=[P3, T1, KKN], w=[P3])
            k.tt('dve', KT[:], Kp, T2[:], ALU.mult, r=[P3, T2], w=[KT])
            k.tt('pool', RK[:], R_, Kp, ALU.mult, r=[P3], w=[RK])
            k.act(GAM[:], SC_[:], AF.Exp, r=[SC_], w=[GAM], scale=-1.0)
            k.tt('dve', RT[:], R_, GAM[:], ALU.mult, r=[P3, GAM], w=[RT])
            for n in range(8):
                c_ = slice(n * 64, (n + 1) * 64)
                Hold = Hs[(nchunk + 1) % 2]; Hnew = Hs[nchunk % 2]
                xy = XY[0]
                fns = []
                for h in range(4):
                    fns.append(lambda h=h: nc.tensor.matmul(psM[:, h * 64:(h + 1) * 64], lhsT=BT[:, h, c_], rhs=AT[:, h, c_], start=True, stop=True, skip_group_check=True))
                    fns.append(lambda h=h: nc.tensor.matmul(psM[:, 256 + h * 64:256 + (h + 1) * 64], lhsT=AT[:, h, c_], rhs=BT[:, h, c_], start=False, stop=True, skip_group_check=True))
                k.S.pe_group(fns, [AT, BT], [psM])
                fns = []
                for h in range(4):
                    fns.append(lambda h=h: nc.tensor.matmul(psK[:, h * 64:(h + 1) * 64], lhsT=KT[:, h, c_], rhs=AT[:, h, c_], start=(h == 0), stop=True, skip_group_check=True))
                    fns.append(lambda h=h: nc.tensor.matmul(psK[:, 256 + h * 64:256 + (h + 1) * 64], lhsT=KT[:, h, c_], rhs=RT[:, h, c_], start=False, stop=True, skip_group_check=True))
                k.S.pe_group(fns, [AT, KT, RT], [psK])
                fns = []
                for h in range(4):
                    fns.append(lambda h=h: nc.tensor.matmul(psB[:, h * 64:(h + 1) * 64], lhsT=BT[:, h, c_], rhs=RT[:, h, c_], start=(h == 0), stop=True, skip_group_check=True))
                k.S.pe_group(fns, [BT, RT], [psB])
                psM2 = psM[:, :].rearrange("p (a h f) -> p a h f", a=2, h=4)
                k.tt('dve', xy[:, 0, :, :], psM2[:, 0, :, :], bc(mk[:, 0, :].unsqueeze(1), [64, 4, 64]), ALU.mult, r=[psM, mk], w=[xy])
                k.tt('dve', xy[:, 1, :, :], psM2[:, 1, :, :], bc(mk[:, 1, :].unsqueeze(1), [64, 4, 64]), ALU.mult, r=[psM, mk], w=[xy])
                psK2 = psK[:, :].rearrange("p (a h f) -> p a h f", a=2, h=4)
                k.tt('dve', AKRK[:, 0, :, :], psK2[:, 0, :, :], bc(mk[:, 0, :].unsqueeze(1), [64, 4, 64]), ALU.mult, r=[psK, mk], w=[AKRK])
                k.tt('dve', AKRK[:, 1, :, :], psK2[:, 1, :, :], bc(mk[:, 2, :].unsqueeze(1), [64, 4, 64]), ALU.mult, r=[psK, mk], w=[AKRK])
                k.tt('dve', RBT[:], psB[:, 0:256].rearrange("p (h f) -> p h f", h=4), bc(mk[:, 2, :].unsqueeze(1), [64, 4, 64]), ALU.mult, r=[psB, mk], w=[RBT])
                fns = []
                for qi, src in enumerate((V_, BT, KT)):
                    for h in range(4):
                        sl_ = src[:, h, c_]
                        fns.append(lambda qi=qi, h=h, sl_=sl_: nc.tensor.transpose(out=psC[:, (qi % 2) * 256 + h * 64:(qi % 2) * 256 + (h + 1) * 64] if qi < 2 else psP[:, h * 64:(h + 1) * 64],
                                                                                  in_=sl_, identity=k.ident_f[0:64, 0:64]))
                k.S.pe_group(fns, [P3, BT, KT, k.ident_f], [psC, psP])
                k.copy('act', TOK[:, 0:2, :, :].rearrange("p a h f -> p (a h f)"), psC[:, :], r=[psC], w=[TOK])
                k.copy('act', TOK[:, 2, :, :].rearrange("p h f -> p (h f)"), psP[:, 0:256], r=[psP], w=[TOK])
                P_ = PP[0]
                k.tt('dve', P_[:], xy[:, 0, :, :], bc(mk[:, 3, :].unsqueeze(1), [64, 4, 64]), ALU.add, r=[xy, mk], w=[P_])
                for lev in range(5):
                    xyn = XY[(lev + 1) % 2]
                    fns = []
                    for h in range(4):
                        fns.append(lambda h=h, xy=xy: nc.tensor.matmul(psX[:, 256 + h * 64:256 + (h + 1) * 64], lhsT=xy[:, 0, h, :], rhs=xy[:, 1, h, :], start=(h == 0), stop=True, skip_group_check=True))
                        if lev < 4:
                            fns.append(lambda h=h, xy=xy: nc.tensor.matmul(psX[:, h * 64:(h + 1) * 64], lhsT=xy[:, 1, h, :], rhs=xy[:, 0, h, :], start=False, stop=True, skip_group_check=True))
                    k.S.pe_group(fns, [xy], [psX])
                    if lev < 4:
                        k.copy('act', xyn[:].rearrange("p a h f -> p (a h f)"), psX[:, :], r=[psX], w=[xyn])
                    else:
                        k.copy('act', xyn[:, 1, :, :].rearrange("p h f -> p (h f)"), psX[:, 256:512], r=[psX], w=[xyn])
                    Pn = PP[(lev + 1) % 2]
                    k.S.pe_group([lambda h=h, xyn=xyn, P_=P_: nc.tensor.matmul(psP[:, h * 64:(h + 1) * 64], lhsT=xyn[:, 1, h, :], rhs=P_[:, h, :], start=(h == 0), stop=True, skip_group_check=True)
                                  for h in range(4)], [xyn, P_], [psP])
                    k.tt('dve', Pn[:], P_[:], psP[:, 0:256].rearrange("p (h f) -> p h f", h=4), ALU.add, r=[P_, psP], w=[Pn])
                    P_ = Pn; xy = xyn
                TT = P_
                fns = []
                for h in range(4):
                    fns.append(lambda h=h: nc.tensor.matmul(psH[:, h * 64:(h + 1) * 64], lhsT=AT[:, h, c_], rhs=Hold[:, h, :], start=(h == 0), stop=False, skip_group_check=True))
                    fns.append(lambda h=h: nc.tensor.matmul(psH[:, h * 64:(h + 1) * 64], lhsT=AKRK[:, 0, h, :], rhs=TOK[:, 0, h, :], start=False, stop=True, skip_group_check=True))
                k.S.pe_group(fns, [AT, Hold, AKRK, TOK], [psH])
                k.copy('act', Wsb[:].rearrange("p h f -> p (h f)"), psH[:, 0:256], r=[psH], w=[Wsb])
                k.S.pe_group([lambda h=h: nc.tensor.matmul(psH[:, 256 + h * 64:256 + (h + 1) * 64], lhsT=TT[:, h, :], rhs=Wsb[:, h, :], start=False, stop=True, skip_group_check=True)
                              for h in range(4)], [TT, Wsb], [psH])
                k.copy('act', Usb[:].rearrange("p h f -> p (h f)"), psH[:, 256:512], r=[psH], w=[Usb])
                fns = []
                for h in range(4):
                    fns.append(lambda h=h: nc.tensor.matmul(psY[:, h * 64:(h + 1) * 64], lhsT=RT[:, h, c_], rhs=Hold[:, h, :], start=(h == 0), stop=False, skip_group_check=True))
                    fns.append(lambda h=h: nc.tensor.matmul(psY[:, h * 64:(h + 1) * 64], lhsT=RBT[:, h, :], rhs=Usb[:, h, :], start=False, stop=False, skip_group_check=True))
                    fns.append(lambda h=h: nc.tensor.matmul(psY[:, h * 64:(h + 1) * 64], lhsT=AKRK[:, 1, h, :], rhs=TOK[:, 0, h, :], start=False, stop=True, skip_group_check=True))
                    fns.append(lambda h=h: nc.tensor.matmul(psY[:, 256 + h:256 + h + 1], lhsT=RK[:, h, c_], rhs=pp[:, 7, h:h + 1], start=False, stop=True, skip_group_check=True))
                fns.append(lambda: nc.tensor.matmul(psB[:, 256:512], lhsT=SG[:, c_], rhs=gup[:, :], start=False, stop=True, skip_group_check=True))
                k.S.pe_group(fns, [RT, Hold, RBT, Usb, AKRK, TOK, RK, pp, SG, gup], [psY, psB])
                fns = []
                for h in range(4):
                    fns.append(lambda h=h: nc.tensor.matmul(psC[:, h * 64:(h + 1) * 64], lhsT=TOK[:, 1, h, :], rhs=Usb[:, h, :], start=(h == 0), stop=False, skip_group_check=True))
                    fns.append(lambda h=h: nc.tensor.matmul(psC[:, h * 64:(h + 1) * 64], lhsT=TOK[:, 2, h, :], rhs=TOK[:, 0, h, :], start=False, stop=True, skip_group_check=True))
                k.S.pe_group(fns, [TOK, Usb], [psC])
                k.tt('dve', Hnew[:], psC[:, 0:256].rearrange("p (h f) -> p h f", h=4), Hold[:], ALU.add, r=[psC, Hold], w=[Hnew])
                k.tt('dve', Hnew[:], Hnew[:], bc(GAM[:, :, n * 64 + 63:n * 64 + 64], [64, 4, 64]), ALU.mult, r=[Hnew, GAM], w=[Hnew])
                y3 = psY[:, 0:256].rearrange("p (h f) -> p h f", h=4)
                k.S.op('dve', lambda: nc.vector.reduce_sum(out=sm[:, 0, :], in_=y3, axis=AX.X), [psY], [sm])
                k.ts('dve', sm[:, 1, :], sm[:, 0, :], 1.0 / 64.0, None, ALU.mult, None, r=[sm], w=[sm])
                k.tt('dve', yc[:], y3, bc(sm[:, 1, :].unsqueeze(2), [64, 4, 64]), ALU.subtract, r=[psY, sm], w=[yc])
                k.tt('pool', ysq[:], yc[:], yc[:], ALU.mult, r=[yc], w=[ysq])
                k.S.op('dve', lambda: nc.vector.reduce_sum(out=sm[:, 2, :], in_=ysq[:], axis=AX.X), [ysq], [sm])
                k.act(sm[:, 3, :], sm[:, 2, :], AF.Ln, r=[sm], w=[sm], scale=1.0 / 64.0, bias=GN_EPS)
                k.act(sm[:, 4, :], sm[:, 3, :], AF.Exp, r=[sm], w=[sm], scale=-0.5)
                k.copy('dve', sm[:, 5, :], psY[:, 256:260], r=[psY], w=[sm])
                k.tt('dve', yc[:], yc[:], bc(sm[:, 4, :].unsqueeze(2), [64, 4, 64]), ALU.mult, r=[yc, sm], w=[yc])
                k.tt('pool', yc[:], yc[:], lnr[:, 0, :].rearrange("p (h f) -> p h f", h=4), ALU.mult, r=[yc, lnr], w=[yc])
                k.tt('pool', yc[:], yc[:], lnr[:, 1, :].rearrange("p (h f) -> p h f", h=4), ALU.add, r=[yc, lnr], w=[yc])
                k.tt('dve', ysq[:], TOK[:, 0, :, :], bc(sm[:, 5, :].unsqueeze(2), [64, 4, 64]), ALU.mult, r=[TOK, sm], w=[ysq])
                k.tt('pool', yc[:], yc[:], ysq[:], ALU.add, r=[yc, ysq], w=[yc])
                k.tt('dve', yab[:, n, :], yc[:].rearrange("p h f -> p (h f)"), psB[:, 256:512], ALU.mult, r=[yc, psB], w=[yab])
                nchunk += 1
            k.dma('sp', k.Y[t0:t0 + 512, 0:256].rearrange("(n p) c -> p n c", p=64), yab[:], r=[yab])


def build(nlayers=DEPTH, taps=()):
    k = K(nlayers, taps=taps)
    setup_globals(k)
    setup_fox(k)
    setup_rwkv(k)
    setup_ffn(k)
    setup_nsa(k)
    for l in range(nlayers):
        xin = k.x_in if l == 0 else k.XR
        xout = k.OUT if l == nlayers - 1 else k.XR
        stage_mod(k, l)
        stage_proj(k, l, xin)
        stage_rwkv(k, l)
        stage_fox(k, l)
        stage_nsa(k, l)
        stage_out(k, l, xin, k.XR1)
        stage_ffn(k, l, k.XR1, xout)
    k.S.barrier()
    return k


_CACHE = {}


def kernel(**inputs):
    if "k" not in _CACHE:
        _CACHE["k"] = build(DEPTH)
    k = _CACHE["k"]
    sh = prep_shared(inputs)
    in_maps = []
    for b in range(8):
        d = dict(sh)
        d.update(prep_core(inputs, b))
        in_maps.append({n: v for n, v in d.items() if n in k.ins})
    res = run_bass_kernel_spmd(k.nc, in_maps, core_ids=list(range(8)))
    out = np.stack([np.asarray(res.results[b]["out"], dtype=np.float32) for b in range(8)], axis=0)
    return out
```

```python
import numpy as np
import ml_dtypes
from contextlib import ExitStack
import concourse.bass as bass
import concourse.mybir as mybir
from concourse.bass_utils import run_bass_kernel_spmd

F32 = mybir.dt.float32
BF16 = mybir.dt.bfloat16
AF = mybir.ActivationFunctionType
ALU = mybir.AluOpType
AX = mybir.AxisListType
NPBF = ml_dtypes.bfloat16

S_LEN = 4096
D = 1024
DEPTH = 4
NTB = 8
N_IN = 3224
D_FF = 2816
NEG = -30000.0
RMS_EPS = 1e-6
GN_EPS = 64e-5


class Sched:
    ENG = ('pe', 'act', 'dve', 'pool')
    LIMIT = 30000

    def __init__(self, nc):
        self.nc = nc
        self.e = {'pe': nc.tensor, 'act': nc.scalar, 'dve': nc.vector, 'pool': nc.gpsimd, 'sp': nc.sync}
        self.epoch = {k: 0 for k in self.ENG}
        self.sem = {k: nc.alloc_semaphore("c_%s_0" % k) for k in self.ENG}
        self.cnt = {k: 0 for k in self.ENG}
        self.seen = {k: {} for k in self.e}
        self.lastw = {}
        self.reads = {}
        self.dma_sems = {'hw': [[nc.alloc_semaphore("d%d" % i), 0, "dma%d" % i] for i in range(24)],
                         'sw': [[nc.alloc_semaphore("ds%d" % i), 0, "dmas%d" % i] for i in range(8)]}
        self.ndma = {'hw': 0, 'sw': 0}
        self.n_inst = 0
        self.n_wait = 0
        self.per = {}

    def _wait(self, eng, tok):
        key, sem, val = tok
        if self.seen[eng].get(key, 0) >= val:
            return
        self.e[eng].wait_ge(sem, val)
        self.n_wait += 1
        self.per[eng] = self.per.get(eng, 0) + 1
        self.seen[eng][key] = val

    def _deps(self, eng, reads, writes):
        for b in reads:
            t = self.lastw.get(b)
            if t is not None:
                self._wait(eng, t)
        for b in writes:
            t = self.lastw.get(b)
            if t is not None:
                self._wait(eng, t)
            for t in self.reads.get(b, ()):
                self._wait(eng, t)

    def _commit(self, tok, reads, writes):
        for b in reads:
            self.reads.setdefault(b, []).append(tok)
        for b in writes:
            self.lastw[b] = tok
            self.reads[b] = []

    def _bump(self, eng, ins):
        if self.cnt[eng] >= self.LIMIT:
            self.epoch[eng] += 1
            self.sem[eng] = self.nc.alloc_semaphore("c_%s_%d" % (eng, self.epoch[eng]))
            self.cnt[eng] = 0
        self.cnt[eng] += 1
        ins.then_inc(self.sem[eng], 1)
        return ("%s_%d" % (eng, self.epoch[eng]), self.sem[eng], self.cnt[eng])

    @staticmethod
    def _norm(reads, writes):
        rd = [getattr(b, 'n', b) for b in reads]
        wr = [getattr(b, 'n', b) for b in writes]
        ps = [b for b in rd if b.startswith("ps")]
        rd = [b for b in rd if not b.startswith("ps")]
        return rd, wr + [b for b in ps if b not in wr]

    def op(self, eng, inst_fn, reads=(), writes=()):
        reads, writes = self._norm(reads, writes)
        self._deps(eng, reads, writes)
        ins = inst_fn()
        self.per[eng] = self.per.get(eng, 0) + 1
        tok = self._bump(eng, ins)
        self._commit(tok, reads, writes)
        self.n_inst += 1
        return tok

    def pe_group(self, fns, reads=(), writes=()):
        reads, writes = self._norm(reads, writes)
        self._deps('pe', reads, writes)
        ins = None
        for f in fns:
            ins = f()
            self.n_inst += 1
            self.per['pe'] = self.per.get('pe', 0) + 1
        tok = self._bump('pe', ins)
        self._commit(tok, reads, writes)
        return tok

    def dma(self, q, out, in_, reads=(), writes=(), **kw):
        reads, writes = self._norm(reads, writes)
        self._deps(q, reads, writes)
        cls = 'sw' if q == 'pool' else 'hw'
        pool_ = self.dma_sems[cls]
        slot = pool_[self.ndma[cls] % len(pool_)]
        self.ndma[cls] += 1
        if slot[1] > 0:
            self._wait(q, (slot[2], slot[0], slot[1]))
        if slot[1] >= self.LIMIT:
            slot[0] = self.nc.alloc_semaphore("%s_e%d" % (slot[2], self.ndma[cls]))
            slot[1] = 0
            slot[2] = slot[2] + "x"
        slot[1] += 16
        ins = self.e[q].dma_start(out=out, in_=in_, **kw)
        self.per[q] = self.per.get(q, 0) + 1
        ins.then_inc(slot[0], 16)
        tok = (slot[2], slot[0], slot[1])
        self._commit(tok, reads, writes)
        self.n_inst += 1
        return tok

    def barrier(self, engines=('pe', 'act', 'dve', 'pool', 'sp')):
        toks = [("%s_%d" % (k, self.epoch[k]), self.sem[k], self.cnt[k]) for k in self.ENG if self.cnt[k] > 0]
        toks += [(s[2], s[0], s[1]) for p_ in self.dma_sems.values() for s in p_ if s[1] > 0]
        for e in engines:
            for t in toks:
                self._wait(e, t)
        self.lastw = {}
        self.reads = {}


class Pipe:
    def __init__(self, lag=2):
        self.q = []
        self.lag = lag

    def push(self, first, second):
        first()
        self.q.append(second)
        while len(self.q) > self.lag:
            self.q.pop(0)()

    def flush(self):
        while self.q:
            self.q.pop(0)()


class Tile:
    def __init__(self, h, name):
        self.h = h
        self.n = name

    def __getitem__(self, idx):
        return self.h[idx]


class Scope:
    cnt = 0

    def __init__(self, k):
        self.k = k
        self.es = ExitStack()

    def __enter__(self):
        self.es.__enter__()
        Scope.cnt += 1
        self.id = Scope.cnt
        return self

    def sb(self, name, shape, dt):
        nm = "%s_%d" % (name, self.id)
        h = self.es.enter_context(self.k.nc.sbuf_tensor(nm, list(shape), dt))
        return Tile(h, nm)

    def ps(self, name, shape, dt=F32):
        nm = "%s_%d" % (name, self.id)
        h = self.es.enter_context(self.k.nc.psum_tensor(nm, list(shape), dt))
        return Tile(h, nm)

    def __exit__(self, *a):
        self.k.S.barrier()
        return self.es.__exit__(*a)


class K:
    def __init__(self, nlayers, taps=()):
        self.nc = bass.Bass("TRN2", target_bir_lowering=False)
        self.S = Sched(self.nc)
        self.nl = nlayers
        self.taps = set(taps)
        self.ins = {}
        self.dr = {}

    def inp(self, name, shape, dt=F32):
        t = self.nc.dram_tensor(name, list(shape), dt, kind="ExternalInput").ap()
        self.ins[name] = t
        return t

    def scratch(self, name, shape, dt=F32, out=False):
        kind = "ExternalOutput" if (out or name in self.taps) else "Internal"
        t = self.nc.dram_tensor(name, list(shape), dt, kind=kind).ap()
        self.dr[name] = t
        return t

    def act(self, out, in_, func, r, w, bias=0.0, scale=1.0, accum=None):
        nc = self.nc
        if accum is None:
            return self.S.op('act', lambda: nc.scalar.activation(out=out, in_=in_, func=func, bias=bias, scale=scale), r, w)
        return self.S.op('act', lambda: nc.scalar.activation(out=out, in_=in_, func=func, bias=bias, scale=scale, accum_out=accum), r, w)

    def ts(self, eng, out, in0, s1, s2, op0, op1, r, w):
        e = self.S.e[eng]
        if op1 is None:
            return self.S.op(eng, lambda: e.tensor_scalar(out=out, in0=in0, scalar1=s1, scalar2=None, op0=op0), r, w)
        return self.S.op(eng, lambda: e.tensor_scalar(out=out, in0=in0, scalar1=s1, scalar2=s2, op0=op0, op1=op1), r, w)

    def tt(self, eng, out, in0, in1, op, r, w):
        e = self.S.e[eng]
        return self.S.op(eng, lambda: e.tensor_tensor(out=out, in0=in0, in1=in1, op=op), r, w)

    def stt(self, eng, out, in0, scalar, in1, op0, op1, r, w):
        e = self.S.e[eng]
        return self.S.op(eng, lambda: e.scalar_tensor_tensor(out=out, in0=in0, scalar=scalar, in1=in1, op0=op0, op1=op1), r, w)

    def copy(self, eng, out, in_, r, w):
        if eng == 'act':
            return self.S.op('act', lambda: self.nc.scalar.copy(out=out, in_=in_), r, w)
        e = self.S.e[eng]
        return self.S.op(eng, lambda: e.tensor_copy(out=out, in_=in_), r, w)

    def mm(self, out, pairs, r, w, start=True, stop=True, sgc=False):
        nc = self.nc
        n = len(pairs)
        fns = []
        for i, (l, rh) in enumerate(pairs):
            fns.append(lambda l=l, rh=rh, i=i: nc.tensor.matmul(out, lhsT=l, rhs=rh, start=(start and i == 0), stop=(stop and i == n - 1),
                                                               skip_group_check=(sgc or not start)))
        return self.S.pe_group(fns, r, w)

    def transpose(self, out, in_, ident, r, w):
        nc = self.nc
        return self.S.op('pe', lambda: nc.tensor.transpose(out=out, in_=in_, identity=ident), r, w)

    def dma(self, q, out, in_, r=(), w=(), **kw):
        return self.S.dma(q, out, in_, r, w, **kw)


def w_in_perm_index():
    idx = list(range(0, 896))
    idx += list(range(896, 1664))
    for c in range(3):
        idx += list(range(2054 + c * 64, 2054 + c * 64 + 64))
        idx += list(range(2054 + (c + 3) * 64, 2054 + (c + 3) * 64 + 64))
    idx += list(range(2438, 2566))
    idx += list(range(2566, 2694))
    idx += list(range(2694, 2822))
    idx += list(range(2950, 3078))
    idx += list(range(2048, 2054))
    idx += list(range(1664, 2048))
    idx += list(range(2822, 2950))
    idx += list(range(3078, 3206))
    idx += list(range(3206, 3224))
    assert len(idx) == N_IN and len(set(idx)) == N_IN
    return np.array(idx)


QKT_ROWS = 1664


def setup_globals(k):
    nc = k.nc
    k.x_in = k.inp("x", [S_LEN, D])
    k.cT = k.inp("cT", [128, 8])
    k.ada_w = k.inp("ada_w", [DEPTH, D, 6 * D])
    k.ada_b_fm = k.inp("ada_b_fm", [DEPTH, 128, 48])
    k.ada_b_row = k.inp("ada_b_row", [DEPTH, 6 * D])
    k.normg_fm = k.inp("normg_fm", [DEPTH, 4, 128, 8])
    k.normg_row = k.inp("normg_row", [DEPTH, 4, D])
    k.w_in = k.inp("w_in_p", [DEPTH, D, N_IN])
    k.ident_bf_d = k.inp("ident_bf", [128, 128], BF16)
    k.ident_f_d = k.inp("ident_f", [128, 128], F32)

    k.PT = k.scratch("PT", [896, S_LEN], F32)
    k.QKT = k.scratch("QKT", [QKT_ROWS, S_LEN], BF16)
    k.FL = k.scratch("FL", [6, S_LEN], F32)
    k.VT = k.scratch("VT", [S_LEN, 640], BF16)
    k.GT = k.scratch("GT", [S_LEN, 18], F32)
    k.Y = k.scratch("Y", [S_LEN, D], BF16)
    k.XR = k.scratch("XR", [S_LEN, D], F32)
    k.XR1 = k.scratch("XR1", [S_LEN, D], F32)
    k.OUT = k.scratch("out", [S_LEN, D], F32, out=True)

    def pers(name, shape, dt):
        return Tile(nc.alloc_sbuf_tensor(name, list(shape), dt), name)
    k.ident_bf = pers("ident_bf_sb", [128, 128], BF16)
    k.ident_f = pers("ident_f_sb", [128, 128], F32)
    k.sc = pers("sc", [128, 8], F32)
    k.sc_rep = pers("sc_rep", [128, 8, 128], F32)
    k.modAB = pers("modAB", [128, 32], F32)
    k.gm_row = pers("gm_row", [128, D], F32)
    k.gf_row = pers("gf_row", [128, D], F32)
    k.dma('sp', k.ident_bf[:], k.ident_bf_d, w=[k.ident_bf])
    k.dma('sp', k.ident_f[:], k.ident_f_d, w=[k.ident_f])
    k.dma('sp', k.sc[:], k.cT, w=[k.sc])
    k.act(k.sc[:], k.sc[:], AF.Silu, r=[k.sc], w=[k.sc])
    for kc in range(8):
        k.copy('dve', k.sc_rep[:, kc, :], k.sc[:, kc:kc + 1].to_broadcast([128, 128]), r=[k.sc], w=[k.sc_rep])


def stage_mod(k, l):
    with Scope(k) as sc:
        slab = [sc.sb("adaslab%d" % i, [128, 6 * D], F32) for i in range(2)]
        psA = sc.ps("psA", [128, 32])
        psR = [sc.ps("psR%d" % i, [128, 512]) for i in range(4)]
        bfm = sc.sb("bfm", [128, 48], F32)
        gfm = sc.sb("gfm", [128, 4, 8], F32)
        brow = sc.sb("brow", [128, 2, D], F32)
        grow = sc.sb("grow", [128, 2, D], F32)
        mfm = sc.sb("mfm", [128, 32], F32)
        k.dma('sp', bfm[:], k.ada_b_fm[l], w=[bfm])
        k.dma('sp', gfm[:], k.normg_fm[l].rearrange("g p c -> p g c"), w=[gfm])
        k.dma('sp', brow[:, 0, :], k.ada_b_row[l:l + 1, 2 * D:3 * D].broadcast_to([128, D]), w=[brow])
        k.dma('sp', brow[:, 1, :], k.ada_b_row[l:l + 1, 5 * D:6 * D].broadcast_to([128, D]), w=[brow])
        k.dma('sp', grow[:, 0, :], k.normg_row[l, 1:2, :].broadcast_to([128, D]), w=[grow])
        k.dma('sp', grow[:, 1, :], k.normg_row[l, 3:4, :].broadcast_to([128, D]), w=[grow])
        fm_chunks = list(range(0, 16)) + list(range(24, 40))
        row_cols = [2 * D, 2 * D + 512, 5 * D, 5 * D + 512]
        for kc in range(8):
            sl = slab[kc % 2]
            k.dma('sp', sl[:], k.ada_w[l, kc * 128:(kc + 1) * 128, :], w=[sl])
            for i, j in enumerate(fm_chunks):
                k.mm(psA[:, i:i + 1], [(sl[:, j * 128:(j + 1) * 128], k.sc[:, kc:kc + 1])], r=[sl, k.sc], w=[psA],
                     start=(kc == 0 and i == 0), stop=(kc == 7), sgc=True)
            for i, c0 in enumerate(row_cols):
                k.mm(psR[i][:], [(k.sc_rep[:, kc, :], sl[:, c0:c0 + 512])], r=[sl, k.sc_rep], w=[psR[i]],
                     start=(kc == 0), stop=(kc == 7), sgc=True)
        k.tt('dve', mfm[:, 0:16], psA[:, 0:16], bfm[:, 0:16], ALU.add, r=[psA, bfm], w=[mfm])
        k.tt('dve', mfm[:, 16:32], psA[:, 16:32], bfm[:, 24:40], ALU.add, r=[psA, bfm], w=[mfm])
        k.stt('dve', k.modAB[:, 0:8], mfm[:, 8:16], 1.0, gfm[:, 0, :], ALU.add, ALU.mult, r=[mfm, gfm], w=[k.modAB])
        k.copy('dve', k.modAB[:, 8:16], mfm[:, 0:8], r=[mfm], w=[k.modAB])
        k.stt('dve', k.modAB[:, 16:24], mfm[:, 24:32], 1.0, gfm[:, 2, :], ALU.add, ALU.mult, r=[mfm, gfm], w=[k.modAB])
        k.copy('dve', k.modAB[:, 24:32], mfm[:, 16:24], r=[mfm], w=[k.modAB])
        for i in range(4):
            dst = (k.gm_row if i < 2 else k.gf_row)
            cs = slice((i % 2) * 512, (i % 2) * 512 + 512)
            k.tt('dve', dst[:, cs], psR[i][:], brow[:, i // 2, cs], ALU.add, r=[psR[i], brow], w=[dst])
            k.tt('pool', dst[:, cs], dst[:, cs], grow[:, i // 2, cs], ALU.mult, r=[dst, grow], w=[dst])


def stage_proj(k, l, xsrc):
    nc = k.nc
    with Scope(k) as sc:
        wsb = sc.sb("wsb", [128, 8, N_IN], BF16)
        wst = [sc.sb("wst%d" % i, [128, N_IN], F32) for i in range(2)]
        xt = [sc.sb("xt%d" % i, [128, D], F32) for i in range(2)]
        junk = sc.sb("junk", [128, D], BF16)
        xn = [sc.sb("xn%d" % i, [128, D], BF16) for i in range(2)]
        st = [sc.sb("st%d" % i, [128, 4], F32) for i in range(2)]
        HT = [sc.sb("HT%d" % i, [128, 8, 512], BF16) for i in range(2)]
        psT = [sc.ps("psT%d" % i, [128, D], BF16) for i in range(2)]
        psM = [sc.ps("psM%d" % i, [128, 512]) for i in range(4)]
        evf = [sc.sb("evf%d" % i, [128, 512], F32) for i in range(3)]
        evb = [sc.sb("evb%d" % i, [128, 512], BF16) for i in range(3)]
        evt = [sc.sb("evt%d" % i, [128, 640], BF16) for i in range(2)]
        evg = [sc.sb("evg%d" % i, [128, 18], F32) for i in range(2)]
        for kc in range(8):
            s = wst[kc % 2]
            k.dma('sp', s[:], k.w_in[l, kc * 128:(kc + 1) * 128, :], w=[s])
            k.copy('pool', wsb[:, kc, :], s[:], r=[s], w=[wsb])
        nev = 0
        npm = 0
        for tb in range(NTB):
            ht = HT[tb % 2]
            for sub in range(4):
                ti = tb * 4 + sub
                x_ = xt[ti % 2]; xn_ = xn[ti % 2]; st_ = st[ti % 2]; pt_ = psT[ti % 2]
                k.dma('sp', x_[:], xsrc[ti * 128:(ti + 1) * 128, :], w=[x_])
                k.act(junk[:], x_[:], AF.Square, r=[x_], w=[junk, st_], scale=1.0 / 32.0, accum=st_[:, 0:1])
                k.act(st_[:, 1:2], st_[:, 0:1], AF.Ln, r=[st_], w=[st_], bias=RMS_EPS)
                k.act(st_[:, 2:3], st_[:, 1:2], AF.Exp, r=[st_], w=[st_], scale=-0.5)
                k.ts('dve', xn_[:], x_[:], st_[:, 2:3], None, ALU.mult, None, r=[x_, st_], w=[xn_])
                for kc in range(8):
                    k.transpose(pt_[:, kc * 128:(kc + 1) * 128], xn_[:, kc * 128:(kc + 1) * 128], k.ident_bf[:],
                                r=[xn_, k.ident_bf], w=[pt_])
                for kc in range(8):
                    o = ht[:, kc, sub * 128:(sub + 1) * 128]
                    i_ = pt_[:, kc * 128:(kc + 1) * 128]
                    if kc % 2 == 0:
                        k.ts('dve', o, i_, k.modAB[:, kc:kc + 1], k.modAB[:, 8 + kc:9 + kc], ALU.mult, ALU.add,
                             r=[pt_, k.modAB], w=[ht])
                    else:
                        k.act(o, i_, AF.Identity, r=[pt_, k.modAB], w=[ht], scale=k.modAB[:, kc:kc + 1],
                              bias=k.modAB[:, 8 + kc:9 + kc])
            tsl = slice(tb * 512, (tb + 1) * 512)
            fm = [(c * 128, 128, 'PT', c * 128) for c in range(7)]
            fm += [(896 + c * 128, 128, 'QKT', c * 128) for c in range(13)]
            fm += [(2560, 6, 'FL', 0)]
            for (c0, m, dst, r0) in fm:
                ps = psM[npm % 4]; npm += 1
                k.mm(ps[0:m, :], [(wsb[:, kc, c0:c0 + m], ht[:, kc, :]) for kc in range(8)], r=[wsb, ht], w=[ps])
                eng = 'act' if nev % 2 == 0 else 'dve'
                if dst == 'QKT':
                    ev = evb[nev % 3]
                    dd = k.QKT[r0:r0 + m, tsl]
                else:
                    ev = evf[nev % 3]
                    dd = (k.PT if dst == 'PT' else k.FL)[r0:r0 + m, tsl]
                nev += 1
                k.copy(eng, ev[0:m, :], ps[0:m, :], r=[ps], w=[ev])
                k.dma('sp', dd, ev[0:m, :], r=[ev])
            for sub in range(4):
                ti = tb * 4 + sub
                tok = slice(ti * 128, (ti + 1) * 128)
                ps0 = psM[npm % 4]; npm += 1
                ps1 = psM[npm % 4]; npm += 1
                lhs = lambda kc: ht[:, kc, sub * 128:(sub + 1) * 128]
                k.mm(ps0[:, 0:384], [(lhs(kc), wsb[:, kc, 2566:2950]) for kc in range(8)], r=[wsb, ht], w=[ps0])
                k.mm(ps1[:, 0:274], [(lhs(kc), wsb[:, kc, 2950:3224]) for kc in range(8)], r=[wsb, ht], w=[ps1])
                et = evt[ti % 2]; eg = evg[ti % 2]
                k.copy('act', et[:, 0:384], ps0[:, 0:384], r=[ps0], w=[et])
                k.copy('dve', et[:, 384:640], ps1[:, 0:256], r=[ps1], w=[et])
                k.copy('dve', eg[:], ps1[:, 256:274], r=[ps1], w=[eg])
                k.dma('sp', k.VT[tok, :], et[:], r=[et])
                k.dma('sp', k.GT[tok, :], eg[:], r=[eg])


def prep_shared(inp):
    f = lambda a: np.ascontiguousarray(np.asarray(a, dtype=np.float32))
    sh = {}
    sh["ada_w"] = f(inp["ada_w"])
    sh["ada_b_fm"] = f(np.asarray(inp["ada_b"]).reshape(DEPTH, 48, 128).transpose(0, 2, 1))
    sh["ada_b_row"] = f(inp["ada_b"])
    sh["normg_fm"] = f(np.asarray(inp["norm_g"]).reshape(DEPTH, 4, 8, 128).transpose(0, 1, 3, 2))
    sh["normg_row"] = f(inp["norm_g"])
    sh["w_in_p"] = f(np.asarray(inp["w_in"])[:, :, w_in_perm_index()])
    sh["ident_bf"] = np.eye(128, dtype=np.float32).astype(NPBF)
    sh["ident_f"] = np.eye(128, dtype=np.float32)
    sh["w_out"] = f(inp["w_out"]); sh["ffn_up"] = f(inp["ffn_up"]); sh["ffn_down"] = f(inp["ffn_down"])
    sh["conv_w_fm"] = f(np.asarray(inp["ffn_conv_w"]).reshape(DEPTH, 3, 44, 128).transpose(0, 3, 1, 2))
    sh["conv_b_fm"] = f(np.asarray(inp["ffn_conv_b"]).reshape(DEPTH, 44, 128).transpose(0, 2, 1))
    sh.update(nsa_host_consts())
    sh["rel_bias"] = f(inp["rel_bias"])
    sh["nsa_pe_kT"] = f(np.asarray(inp["nsa_pe_k"]).transpose(0, 2, 1))
    sh["nsa_pe_vT"] = f(np.asarray(inp["nsa_pe_v"]).transpose(0, 2, 1))
    for n in ("nsa_ck_w1", "nsa_cv_w1", "nsa_ck_w2", "nsa_cv_w2"):
        sh[n] = f(inp[n])
    sh["fox_b_f"] = f(np.asarray(inp["fox_b_f"]).reshape(DEPTH, 6, 1))
    sh.update(rwkv_host(inp))
    return sh


def prep_core(inp, b):
    d = {}
    d["x"] = np.ascontiguousarray(np.asarray(inp["x"][b], dtype=np.float32))
    d["cT"] = np.ascontiguousarray(np.asarray(inp["c"][b], dtype=np.float32).reshape(8, 128).T)
    return d


def setup_fox(k):
    k.fox_bf = k.inp("fox_b_f", [DEPTH, 6, 1])
    k.CUMA = k.scratch("CUMA", [6, 3, S_LEN], BF16)


def stage_fox(k, l):
    nc = k.nc
    with Scope(k) as sc:
        nb = sc.sb("nb", [128, 32, 6], F32)
        with Scope(k) as s2:
            fl = s2.sb("fl", [6, S_LEN], F32)
            t1 = s2.sb("t1", [6, S_LEN], F32)
            ones = s2.sb("ones", [6, S_LEN], F32)
            cum = s2.sb("cum", [6, S_LEN], F32)
            parts = s2.sb("parts", [6, 3, S_LEN], BF16)
            bfv = s2.sb("bfv", [6, 2], F32)
            psn = s2.ps("psn", [128, 512])
            k.dma('sp', fl[:], k.FL, w=[fl])
            k.dma('sp', bfv[:, 0:1], k.fox_bf[l], w=[bfv])
            k.ts('dve', bfv[:, 1:2], bfv[:, 0:1], -1.0, None, ALU.mult, None, r=[bfv], w=[bfv])
            k.S.op('pool', lambda: nc.gpsimd.memset(ones[:], 1.0), [], [ones])
            k.act(t1[:], fl[:], AF.Exp, r=[fl, bfv], w=[t1], bias=bfv[:, 1:2], scale=-1.0)
            k.act(t1[:], t1[:], AF.Ln, r=[t1], w=[t1], bias=1.0, scale=1.0)
            k.ts('dve', t1[:], t1[:], -1.0, None, ALU.mult, None, r=[t1], w=[t1])
            k.S.op('dve', lambda: nc.vector.tensor_tensor_scan(out=cum[:], data0=ones[:], data1=t1[:], initial=0.0,
                                                               op0=ALU.mult, op1=ALU.add), [ones, t1], [cum])
            for t in range(32):
                k.transpose(psn[:, t * 6:(t + 1) * 6], cum[:, t * 128:(t + 1) * 128], k.ident_f[0:6, 0:6],
                            r=[cum, k.ident_f], w=[psn])
            k.ts('dve', nb[:].rearrange("p t h -> p (t h)"), psn[:, 0:192], -1.0, None, ALU.mult, None, r=[psn], w=[nb])
            k.ts('dve', t1[:], cum[:], 8.0, None, ALU.mult, None, r=[cum], w=[t1])
            k.copy('dve', parts[:, 0, :], t1[:], r=[t1], w=[parts])
            k.tt('dve', t1[:], t1[:], parts[:, 0, :], ALU.subtract, r=[t1, parts], w=[t1])
            k.copy('dve', parts[:, 1, :], t1[:], r=[t1], w=[parts])
            k.tt('dve', t1[:], t1[:], parts[:, 1, :], ALU.subtract, r=[t1, parts], w=[t1])
            k.copy('dve', parts[:, 2, :], t1[:], r=[t1], w=[parts])
            k.dma('sp', k.CUMA, parts[:], r=[parts], w=["CUMA"])
        QA = [sc.sb("QA%d" % i, [128, S_LEN], BF16) for i in range(2)]
        KA = [sc.sb("KA%d" % i, [128, S_LEN], BF16) for i in range(2)]
        VA = sc.sb("VA", [128, 32, 6, 65], BF16)
        yb = sc.sb("yb", [128, 32, 384], BF16)
        PTl = [sc.sb("PTl%d" % i, [128, 512], BF16) for i in range(6)]
        rc = [sc.sb("rc%d" % i, [128, 4], F32) for i in range(2)]
        psS = [sc.ps("psS%d" % i, [128, 512]) for i in range(4)]
        psO = [sc.ps("psO%d" % i, [128, 512]) for i in range(2)]
        k.dma('sp', yb[:], k.VT[:, 0:384].rearrange("(t p) c -> p t c", p=128), w=[yb])
        k.S.op('pool', lambda: nc.gpsimd.memset(VA[:, :, :, 64:65], 1.0), [], [VA])
        k.copy('pool', VA[:, :, :, 0:64], yb[:].rearrange("p t (h d) -> p t h d", h=6), r=[yb], w=[VA])
        for i in range(2):
            k.S.op('dve', lambda i=i: nc.vector.memset(KA[i][64:67, :], 1.0), [], [KA[i]])
        nS = 0
        nO = 0
        nP = 0
        pipe = Pipe(2)
        for h in range(6):
            qa = QA[h % 2]; ka = KA[h % 2]
            k.dma('sp', qa[0:64, :], k.QKT[h * 64:(h + 1) * 64, :], w=[qa])
            k.dma('sp', qa[64:67, :], k.CUMA[h], w=[qa])
            k.dma('sp', ka[0:64, :], k.QKT[384 + h * 64:384 + (h + 1) * 64, :], w=[ka])
            for qb in range(NTB):
                po = psO[nO % 2]; nO += 1
                nkt = 4 * qb + 4
                for kt in range(nkt):
                    j = kt - 4 * qb
                    c0 = max(j, 0) * 128
                    ps = psS[nS % len(psS)]; nS += 1
                    pt = PTl[nP % len(PTl)]; nP += 1

                    def first(ps=ps, pt=pt, kt=kt, c0=c0, j=j, qa=qa, ka=ka, qb=qb, h=h):
                        k.mm(ps[:, c0:512], [(ka[0:67, kt * 128:(kt + 1) * 128], qa[0:67, qb * 512 + c0:(qb + 1) * 512])],
                             r=[ka, qa], w=[ps])
                        k.act(pt[:, c0:512], ps[:, c0:512], AF.Exp, r=[ps, nb], w=[pt], bias=nb[:, kt, h:h + 1], scale=0.125)
                        if j >= 0:
                            k.S.op('pool', lambda: nc.gpsimd.affine_select(
                                out=pt[:, c0:c0 + 128], in_=pt[:, c0:c0 + 128], pattern=[[1, 128]], compare_op=ALU.is_ge,
                                fill=0.0, base=0, channel_multiplier=-1), [pt], [pt])

                    def second(pt=pt, kt=kt, j=j, po=po, qb=qb, h=h, last=(kt == nkt - 1)):
                        fns = []
                        for qs in range(max(j, 0), 4):
                            fns.append(lambda qs=qs: nc.tensor.matmul(
                                po[:, qs * 65:(qs + 1) * 65], lhsT=pt[:, qs * 128:(qs + 1) * 128], rhs=VA[:, kt, h, :],
                                start=(kt == 0 and qs == 0), stop=(kt == 4 * qb + qs), skip_group_check=True))
                        k.S.pe_group(fns, [pt, VA], [po])
                        if last:
                            r_ = rc[qb % 2]
                            pov = po[:, 0:260].rearrange("p (q c) -> p q c", c=65)
                            k.S.op('dve', lambda: nc.vector.reciprocal(out=r_[:], in_=pov[:, :, 64]), [po], [r_])
                            for qs in range(4):
                                k.ts('dve', yb[:, qb * 4 + qs, h * 64:(h + 1) * 64], po[:, qs * 65:qs * 65 + 64], r_[:, qs:qs + 1], None,
                                     ALU.mult, None, r=[po, r_], w=[yb])
                    pipe.push(first, second)
        pipe.flush()
        k.dma('sp', k.Y[:, 256:640].rearrange("(t p) c -> p t c", p=128), yb[:], r=[yb], w=["Y"])


LW = 1536
LC = 4608
NEG8 = -240000.0


def t5_bucket_np(n):
    n = np.maximum(n, 0)
    nf = np.maximum(n, 1).astype(np.float32)
    large = 16 + (np.log(nf / np.float32(16)) / np.float32(np.log(128 / 16)) * np.float32(16)).astype(np.int32)
    large = np.minimum(large, 31)
    return np.where(n < 16, n, large)


def nsa_host_consts():
    c = {}
    i = np.arange(LW); n = i - 511
    oh = np.zeros((33, LW), np.float32)
    ok = (n >= 0) & (n < 512)
    oh[t5_bucket_np(n)[ok], i[ok]] = 1.0
    oh[32, ~ok] = NEG8
    c["oh_w"] = oh
    i = np.arange(LC); n = i - 2063
    oh = np.zeros((33, LC), np.float32)
    ok = n >= 0
    oh[t5_bucket_np(n)[ok], i[ok]] = 1.0
    oh[32, ~ok] = NEG8
    c["oh_c"] = oh
    E = np.zeros((128, 32, 128), np.float32)
    for kt in range(32):
        E[2 * kt, kt, 0:64] = 1.0
        E[2 * kt + 1, kt, 64:128] = 1.0
    c["E_blk"] = E.astype(NPBF)
    cs = np.arange(256) * 16
    ce = cs + 31
    ss = np.arange(64) * 64
    ov = ((cs[:, None] <= ss[None, :] + 63) & (ce[:, None] >= ss[None, :])).astype(np.float32)
    ov[255] = 0.0
    c["ovl"] = np.ascontiguousarray(ov.reshape(2, 128, 64).transpose(1, 0, 2)).astype(NPBF)
    t = np.arange(S_LEN)
    cur = t // 64
    jb = np.arange(64)
    back = cur[:, None] - jb[None, :]
    valid = back >= 0
    forced = (jb[None, :] == 0) | (valid & (back < 2))
    tkm = (valid & ~forced).astype(np.float32)
    tka = np.where(valid, np.where(forced, 1e4, 0.0), -1.0).astype(np.float32)
    c["tkm"] = np.ascontiguousarray(tkm.reshape(32, 128, 64).transpose(1, 0, 2)).astype(NPBF)
    c["tka"] = np.ascontiguousarray(tka.reshape(32, 128, 64).transpose(1, 0, 2)).astype(NPBF)
    return c


def setup_nsa(k):
    nc = k.nc
    k.rel_bias = k.inp("rel_bias", [32, 6])
    k.oh_w = k.inp("oh_w", [33, LW])
    k.oh_c = k.inp("oh_c", [33, LC])
    k.E_d = k.inp("E_blk", [128, 32, 128], BF16)
    k.ovl_d = k.inp("ovl", [128, 2, 64], BF16)
    k.tkm_d = k.inp("tkm", [128, 32, 64], BF16)
    k.tka_d = k.inp("tka", [128, 32, 64], BF16)
    k.pe_kT = k.inp("nsa_pe_kT", [DEPTH, 64, 32])
    k.pe_vT = k.inp("nsa_pe_vT", [DEPTH, 64, 32])
    k.ck_w1 = k.inp("nsa_ck_w1", [DEPTH, 2048, 128])
    k.cv_w1 = k.inp("nsa_cv_w1", [DEPTH, 2048, 128])
    k.ck_w2 = k.inp("nsa_ck_w2", [DEPTH, 128, 64])
    k.cv_w2 = k.inp("nsa_cv_w2", [DEPTH, 128, 64])
    k.WVW = k.scratch("WVW", [6, 128, LW], BF16)
    k.WVC = k.scratch("WVC", [6, 128, LC], BF16)
    with Scope(k) as sc:
        rb = sc.sb("rb", [33, 6], F32)
        rb31 = sc.sb("rb31", [32, 6], F32)
        rrep = sc.sb("rrep", [33, 6, 128], F32)
        ohw = sc.sb("ohw", [33, LW], F32)
        ohc = sc.sb("ohc", [33, LC], F32)
        ps = [sc.ps("psb%d" % i, [128, 512]) for i in range(2)]
        ev = [sc.sb("evb%d" % i, [128, 512], BF16) for i in range(2)]
        k.dma('sp', rb[0:32, :], k.rel_bias, w=[rb])
        k.dma('sp', rb31[:], k.rel_bias[31:32, :].broadcast_to([32, 6]), w=[rb31])
        k.dma('sp', ohw[:], k.oh_w, w=[ohw])
        k.dma('sp', ohc[:], k.oh_c, w=[ohc])
        k.S.op('dve', lambda: nc.vector.memset(rb[32:33, :], 1.0), [], [rb])
        k.tt('dve', rb[0:32, :], rb[0:32, :], rb31[:], ALU.subtract, r=[rb, rb31], w=[rb])
        k.ts('dve', rb[0:32, :], rb[0:32, :], 8.0, None, ALU.mult, None, r=[rb], w=[rb])
        for h in range(6):
            k.copy('dve', rrep[:, h, :], rb[:, h:h + 1].to_broadcast([33, 128]), r=[rb], w=[rrep])
        n = 0
        for h in range(6):
            for (oh, L, dst) in ((ohw, LW, k.WVW), (ohc, LC, k.WVC)):
                for c0 in range(0, L, 512):
                    p_ = ps[n % 2]; e_ = ev[n % 2]; n += 1
                    k.mm(p_[:], [(rrep[:, h, :], oh[:, c0:c0 + 512])], r=[rrep, oh], w=[p_])
                    k.copy('act' if n % 2 else 'dve', e_[:], p_[:], r=[p_], w=[e_])
                    k.dma('sp', dst[h, :, c0:c0 + 512], e_[:], r=[e_])


class DbgStop(Exception):
    pass


def dbg(k, lvl):
    if getattr(k, 'dbg_stop', None) == lvl:
        raise DbgStop()


def stage_nsa(k, l):
    nc = k.nc
    with Scope(k) as sc:
        Gw = sc.sb("Gw", [128, 6, 1408], BF16)
        Gc = sc.sb("Gc", [128, 6, 2560], BF16)
        E = sc.sb("E", [128, 32, 128], BF16)
        tkm = sc.sb("tkm", [128, 32, 64], BF16)
        tka = sc.sb("tka", [128, 32, 64], BF16)
        QC = [sc.sb("QC%d" % c, [128, S_LEN], BF16) for c in range(3)]
        KS = sc.sb("KS", [128, S_LEN], BF16)
        KW = sc.sb("KW", [128, S_LEN], BF16)
        VS = sc.sb("VS", [128, 32, 2, 65], BF16)
        VW = sc.sb("VW", [128, 32, 2, 65], BF16)
        KCMP = sc.sb("KCMP", [128, 256], BF16)
        VE = sc.sb("VE", [128, 2, 2, 129], BF16)
        sg = sc.sb("sg", [128, 32, 18], F32)
        for h in range(6):
            k.dma('sp', Gw[:, h, :], bass.AP(k.WVW.tensor, h * 128 * LW + 127, [[LW - 1, 128], [1, 1408]]), w=[Gw])
            k.dma('sp', Gc[:, h, :], bass.AP(k.WVC.tensor, h * 128 * LC + 2032, [[LC - 16, 128], [1, 2560]]), w=[Gc])
        k.dma('sp', E[:], k.E_d, w=[E])
        k.dma('sp', tkm[:], k.tkm_d, w=[tkm])
        k.dma('sp', tka[:], k.tka_d, w=[tka])
        for c in range(3):
            k.dma('sp', QC[c][:], k.QKT[768 + c * 128:768 + (c + 1) * 128, :], w=[QC[c]])
        k.dma('sp', KS[:], k.QKT[1408:1536, :], w=[KS])
        k.dma('sp', KW[:], k.QKT[1536:1664, :], w=[KW])
        k.dma('sp', sg[:], k.GT.rearrange("(t p) c -> p t c", p=128), w=[sg])
        k.act(sg[:], sg[:], AF.Exp, r=[sg], w=[sg], scale=-1.0)
        k.ts('dve', sg[:], sg[:], 1.0, None, ALU.add, None, r=[sg], w=[sg])
        k.S.op('dve', lambda: nc.vector.reciprocal(out=sg[:], in_=sg[:]), [sg], [sg])
        k.dma('sp', VE[:, 0, :, 65:129], k.ovl_d, w=[VE])
        k.dma('sp', VE[:, 1, :, 65:129], k.ovl_d, w=[VE])
        k.S.op('pool', lambda: nc.gpsimd.memset(VE[:, :, :, 64:65], 1.0), [], [VE])
        k.S.op('pool', lambda: nc.gpsimd.memset(VE[:, :, :, 0:64], 0.0), [], [VE])
        k.S.op('pool', lambda: nc.gpsimd.memset(KCMP[:], 0.0), [], [KCMP])
        dbg(k, 1)
        with Scope(k) as s2:
            vst = s2.sb("vst", [128, 32, 256], BF16)
            k.dma('sp', vst[:], k.VT[:, 384:640].rearrange("(t p) c -> p t c", p=128), w=[vst])
            k.S.op('pool', lambda: nc.gpsimd.memset(VS[:, :, :, 64:65], 1.0), [], [VS])
            k.S.op('pool', lambda: nc.gpsimd.memset(VW[:, :, :, 64:65], 1.0), [], [VW])
            k.copy('pool', VS[:, :, :, 0:64], vst[:, :, 0:128].rearrange("p t (g d) -> p t g d", g=2), r=[vst], w=[VS])
            k.copy('pool', VW[:, :, :, 0:64], vst[:, :, 128:256].rearrange("p t (g d) -> p t g d", g=2), r=[vst], w=[VW])
        dbg(k, 2)
        with Scope(k) as s2:
            KC = s2.sb("KC", [128, S_LEN], BF16)
            VC = s2.sb("VC", [128, S_LEN], BF16)
            k.dma('sp', KC[:], k.QKT[1152:1280, :], w=[KC])
            k.dma('sp', VC[:], k.QKT[1280:1408, :], w=[VC])
            w1s = s2.sb("w1s", [128, 16, 128], F32)
            w1b = [s2.sb("w1b%d" % i, [128, 32, 128], BF16) for i in range(2)]
            w2s = s2.sb("w2s", [128, 2, 64], F32)
            w2b = s2.sb("w2b", [128, 2, 64], BF16)
            pes = s2.sb("pes", [128, 2, 32], F32)
            peb = s2.sb("peb", [128, 2, 32], BF16)
            hb = s2.sb("hb", [128, 2], F32)
            gx = s2.sb("gx", [128, 256], F32)
            gu = s2.sb("gu", [128, 256], F32)
            gg = s2.sb("gg", [128, 256], BF16)
            psh = s2.ps("psh", [128, 512])
            psb_ = s2.ps("pshb", [128, 512])
            pso = s2.ps("pso", [128, 512])
            for kv, (w1d, w2d, ped) in enumerate(((k.ck_w1, k.ck_w2, k.pe_kT), (k.cv_w1, k.cv_w2, k.pe_vT))):
                for lh in range(2):
                    for half in range(2):
                        k.dma('sp', w1s[half * 64:(half + 1) * 64, :, :],
                              w1d[l, lh * 1024:(lh + 1) * 1024, :].rearrange("(l d) h -> d l h", d=64), w=[w1s])
                    k.copy('pool', w1b[kv][:, lh * 16:(lh + 1) * 16, :], w1s[:], r=[w1s], w=[w1b[kv]])
                for half in range(2):
                    k.dma('sp', pes[half * 64:(half + 1) * 64, kv, :], ped[l], w=[pes])
                k.dma('sp', w2s[:, kv, :], w2d[l], w=[w2s])
            k.copy('dve', w2b[:], w2s[:], r=[w2s], w=[w2b])
            w2kd = s2.sb("w2kd", [128, 2, 64], BF16)
            for a_ in range(2):
                k.copy('dve', w2kd[:, a_, :], w2s[:, 0, :], r=[w2s], w=[w2kd])
            k.copy('dve', peb[:], pes[:], r=[pes], w=[peb])
            for kv in range(2):
                src = KC if kv == 0 else VC
                k.mm(psb_[:, kv:kv + 1], [(w1b[kv][0:64, li, :], peb[0:64, kv, li:li + 1]) for li in range(32)],
                     r=[w1b[kv], peb], w=[psb_], start=True)
                k.copy('dve', hb[:, kv:kv + 1], psb_[:, kv:kv + 1], r=[psb_], w=[hb])
                for g in range(2):
                    pr = slice(g * 64, (g + 1) * 64)
                    k.mm(psh[:, 0:255], [(w1b[kv][pr, li, :], src[pr, li:li + 16 * 254 + 1:16]) for li in range(32)],
                         r=[w1b[kv], src], w=[psh])
                    k.ts('dve', gx[:, 0:255], psh[:, 0:255], hb[:, kv:kv + 1], None, ALU.add, None, r=[psh, hb], w=[gx])
                    k.tt('dve', gu[:, 0:255], gx[:, 0:255], gx[:, 0:255], ALU.mult, r=[gx], w=[gu])
                    k.ts('dve', gu[:, 0:255], gu[:, 0:255], 0.044715, 1.0, ALU.mult, ALU.add, r=[gu], w=[gu])
                    k.tt('dve', gu[:, 0:255], gu[:, 0:255], gx[:, 0:255], ALU.mult, r=[gu, gx], w=[gu])
                    k.act(gu[:, 0:255], gu[:, 0:255], AF.Exp, r=[gu], w=[gu], scale=-2.0 * 0.7978845608028654)
                    k.ts('dve', gu[:, 0:255], gu[:, 0:255], 1.0, None, ALU.add, None, r=[gu], w=[gu])
                    k.S.op('dve', lambda: nc.vector.reciprocal(out=gu[:, 0:255], in_=gu[:, 0:255]), [gu], [gu])
                    k.S.op('dve', lambda: nc.vector.memset(gg[:, 255:256], 0.0), [], [gg])
                    k.tt('dve', gg[:, 0:255], gu[:, 0:255], gx[:, 0:255], ALU.mult, r=[gu, gx], w=[gg])
                    if kv == 0:
                        k.mm(pso[:, 0:256], [(w2kd[:].rearrange("p a d -> p (a d)"), gg[:, 0:256])], r=[w2kd, gg], w=[pso])
                        k.copy('dve', KCMP[pr, :], pso[pr, 0:256], r=[pso], w=[KCMP])
                    else:
                        for ct in range(2):
                            k.mm(pso[:, ct * 64:(ct + 1) * 64], [(gg[:, ct * 128:(ct + 1) * 128], w2b[:, 1, :])],
                                 r=[w2b, gg], w=[pso], start=(ct == 0))
                        k.copy('dve', VE[:, g, :, 0:64], pso[:, 0:128].rearrange("p (c d) -> p c d", c=2), r=[pso], w=[VE])
        dbg(k, 3)
        NM = [sc.sb("NM%d" % g, [128, 512], BF16) for g in range(2)]
        for g in range(2):
            k.S.op('pool', lambda g=g: nc.gpsimd.memset(NM[g][:], 0.0), [], [NM[g]])
        PTl = [sc.sb("PTn%d" % i, [128, 512], BF16) for i in range(6)]
        yacc = [sc.sb("yacc%d" % i, [128, 4, 384], F32) for i in range(2)]
        ybf = [sc.sb("ybf%d" % i, [128, 4, 384], BF16) for i in range(2)]
        impt = [sc.sb("impt%d" % g, [128, 4, 64], F32) for g in range(2)]
        scr = sc.sb("scr", [128, 4, 64], F32)
        wk = sc.sb("wk", [128, 4, 64], F32)
        m8 = sc.sb("m8", [128, 4, 16], F32)
        nmq = sc.sb("nmq", [128, 4, 64], BF16)
        rcs = [sc.sb("rcs%d" % i, [128, 8], F32) for i in range(3)]
        psS = [sc.ps("psS%d" % i, [128, 512]) for i in range(4)]
        psO = [sc.ps("psO%d" % i, [128, 512]) for i in range(3)]
        psT = sc.ps("psTn", [128, 1024], BF16)
        st = {"S": 0, "O": 0, "P": 0, "R": 0}

        def q_ap(h, c0, c1):
            g, hp = h // 3, h % 3
            return QC[hp][g * 64:(g + 1) * 64, c0:c1]

        def evac(views, h, branch, qb, ya, first):
            r_ = rcs[st["R"] % 3]; st["R"] += 1
            for qs, (po, cb) in enumerate(views):
                if branch == 0:
                    k.ts('dve', r_[:, qs:qs + 1], po[:, cb + 64:cb + 65], 1e-30, None, ALU.max, None, r=[po], w=[r_])
                    k.S.op('dve', lambda r_=r_, qs=qs: nc.vector.reciprocal(out=r_[:, qs:qs + 1], in_=r_[:, qs:qs + 1]), [r_], [r_])
                else:
                    k.S.op('dve', lambda r_=r_, po=po, cb=cb, qs=qs: nc.vector.reciprocal(out=r_[:, qs:qs + 1], in_=po[:, cb + 64:cb + 65]), [po], [r_])
            k.tt('dve', r_[:, 4:8], r_[:, 0:4], sg[:, qb * 4:(qb + 1) * 4, h * 3 + branch], ALU.mult, r=[r_, sg], w=[r_])
            for qs, (po, cb) in enumerate(views):
                o = ya[:, qs, h * 64:(h + 1) * 64]
                if first:
                    k.ts('dve', o, po[:, cb:cb + 64], r_[:, 4 + qs:5 + qs], None, ALU.mult, None, r=[po, r_], w=[ya])
                else:
                    k.stt('dve', o, po[:, cb:cb + 64], r_[:, 4 + qs:5 + qs], o, ALU.mult, ALU.add, r=[po, r_, ya], w=[ya])
            return r_

        pipe = Pipe(2)

        def attend(h, qb, tiles, kmat, vmat, po, g, branch, ya):
            hp = h % 3
            nt = len(tiles)
            state = {"first": True}
            for idx, (kt, c0, c1, extra) in enumerate(tiles):
                ps = psS[st["S"] % len(psS)]; st["S"] += 1
                pt = PTl[st["P"] % len(PTl)]; st["P"] += 1

                def first(ps=ps, pt=pt, kt=kt, c0=c0, c1=c1, extra=extra):
                    fns = [lambda: nc.tensor.matmul(ps[:, c0:c1], lhsT=kmat[g * 64:(g + 1) * 64, kt * 128:(kt + 1) * 128],
                                                    rhs=QC[hp][g * 64:(g + 1) * 64, qb * 512 + c0:qb * 512 + c1],
                                                    start=True, stop=(len(extra) == 0), skip_group_check=True)]
                    rd = [kmat, QC[hp]]
                    for ei, (lt, rt, lap, rap) in enumerate(extra):
                        w_ = rap.shape[-1]
                        fns.append(lambda lap=lap, rap=rap, w_=w_, ei=ei: nc.tensor.matmul(
                            ps[:, c0:c0 + w_], lhsT=lap, rhs=rap, start=False, stop=(ei == len(extra) - 1), skip_group_check=True))
                        rd += [lt, rt]
                    k.S.pe_group(fns, rd, [ps])
                    k.act(pt[:, c0:c1], ps[:, c0:c1], AF.Exp, r=[ps], w=[pt], scale=0.125)

                def second(pt=pt, kt=kt, c0=c0, c1=c1, idx=idx):
                    fns = []
                    for qs in range(c0 // 128, (c1 + 127) // 128):
                        last = all(not (t2[1] <= qs * 128 < t2[2]) for t2 in tiles[idx + 1:])
                        fo = state["first"]
                        state["first"] = False
                        fns.append(lambda qs=qs, fo=fo, last=last: nc.tensor.matmul(
                            po[:, qs * 65:(qs + 1) * 65], lhsT=pt[:, qs * 128:(qs + 1) * 128], rhs=vmat[:, kt, g, :],
                            start=fo, stop=last, skip_group_check=True))
                    k.S.pe_group(fns, [pt, vmat], [po])
                    if idx == nt - 1:
                        evac([(po, qs * 65) for qs in range(4)], h, branch, qb, ya, False)
                pipe.push(first, second)

        for qb in getattr(k, 'dbg_qbs', range(NTB)):
            ya = yacc[qb % 2]
            for h in range(6):
                g = h // 3
                poA = psO[st["O"] % 3]; st["O"] += 1
                poB = psO[st["O"] % 3]; st["O"] += 1
                cts = [0] + ([1] if qb >= 4 else [])
                state = {"A": True, "B": True}
                for ct in cts:
                    delta = 512 * qb - 2048 * ct
                    ps = psS[st["S"] % len(psS)]; st["S"] += 1
                    pt = PTl[st["P"] % len(PTl)]; st["P"] += 1

                    def first(ps=ps, pt=pt, ct=ct, delta=delta, g=g, h=h):
                        pairs = [(KCMP[g * 64:(g + 1) * 64, ct * 128:(ct + 1) * 128], q_ap(h, qb * 512, (qb + 1) * 512))]
                        rd = [KCMP, QC[h % 3]]
                        if delta < 2560:
                            pairs.append((k.ident_bf[:], Gc[:, h, delta:delta + 512])); rd += [k.ident_bf, Gc]
                        k.mm(ps[:], pairs, r=rd, w=[ps])
                        k.act(pt[:], ps[:], AF.Exp, r=[ps], w=[pt], scale=0.125)

                    def second(pt=pt, ct=ct, g=g, h=h, poA=poA, poB=poB, state=state, lastct=(ct == cts[-1])):
                        fns = []
                        for qs in range(4):
                            po, cb = (poA, qs * 129) if qs < 3 else (poB, 0)
                            key = "A" if qs < 3 else "B"
                            stt_ = state[key]
                            state[key] = False
                            fns.append(lambda qs=qs, po=po, cb=cb, stt_=stt_: nc.tensor.matmul(
                                po[:, cb:cb + 129], lhsT=pt[:, qs * 128:(qs + 1) * 128], rhs=VE[:, g, ct, :],
                                start=stt_, stop=lastct, skip_group_check=True))
                        k.S.pe_group(fns, [pt, VE], [poA, poB])
                        if lastct:
                            views = [(poA, 0), (poA, 129), (poA, 258), (poB, 0)]
                            r_ = evac(views, h, 0, qb, ya, True)
                            for qs, (po, cb) in enumerate(views):
                                o = impt[g][:, qs, :]
                                if h % 3 == 0:
                                    k.ts('dve', o, po[:, cb + 65:cb + 129], r_[:, qs:qs + 1], None, ALU.mult, None, r=[po, r_], w=[impt[g]])
                                else:
                                    k.stt('dve', o, po[:, cb + 65:cb + 129], r_[:, qs:qs + 1], o, ALU.mult, ALU.add, r=[po, r_, impt[g]], w=[impt[g]])
                    pipe.push(first, second)
            pipe.flush()
            dbg(k, 4)
            for g in range(2):
                k.tt('dve', scr[:], impt[g][:], tkm[:, qb * 4:(qb + 1) * 4, :], ALU.mult, r=[impt[g], tkm], w=[scr])
                k.tt('dve', scr[:], scr[:], tka[:, qb * 4:(qb + 1) * 4, :], ALU.add, r=[scr, tka], w=[scr])
                for qs in range(4):
                    k.S.op('dve', lambda qs=qs: nc.vector.max(out=m8[:, qs, 0:8], in_=scr[:, qs, :]), [scr], [m8])
                    k.S.op('dve', lambda qs=qs: nc.vector.match_replace(out=wk[:, qs, :], in_to_replace=m8[:, qs, 0:8],
                                                                        in_values=scr[:, qs, :], imm_value=-1e9), [scr, m8], [wk])
                    k.S.op('dve', lambda qs=qs: nc.vector.max(out=m8[:, qs, 8:16], in_=wk[:, qs, :]), [wk], [m8])
                    k.ts('dve', wk[:, qs, :], scr[:, qs, :], m8[:, qs, 15:16], 1.0, ALU.is_ge, ALU.subtract, r=[scr, m8, wk], w=[wk])
                k.ts('dve', nmq[:], wk[:], -NEG8, None, ALU.mult, None, r=[wk], w=[nmq])
                for qs in range(4):
                    k.transpose(psT[0:64, qs * 128:(qs + 1) * 128], nmq[:, qs, :], k.ident_bf[:], r=[nmq, k.ident_bf], w=[psT])
                k.copy('dve', NM[g][0:64, :], psT[0:64, 0:512], r=[psT], w=[NM[g]])
            for h in range(6):
                g = h // 3
                po = psO[st["O"] % 3]; st["O"] += 1
                tiles = []
                for kt in range(max(0, 4 * qb - 4), 4 * qb + 4):
                    delta = 512 * qb - 128 * kt
                    c0 = max(-delta, 0)
                    c1 = min(512, 640 - delta) if delta > 0 else 512
                    tiles.append((kt, c0, c1, [(k.ident_bf, Gw, k.ident_bf[:], Gw[:, h, delta + 384 + c0:delta + 384 + c1])]))
                attend(h, qb, tiles, KW, VW, po, g, 2, ya)
            for h in range(6):
                g = h // 3
                po = psO[st["O"] % 3]; st["O"] += 1
                tiles = []
                for kt in range(0, 4 * qb + 4):
                    delta = 512 * qb - 128 * kt
                    c0 = max(-delta, 0)
                    ex = [(E, NM[g], E[:, kt, :], NM[g][:, c0:512])]
                    if delta <= 128:
                        c1b = 256 if delta == 128 else 512
                        ex.append((k.ident_bf, Gw, k.ident_bf[:], Gw[:, h, delta + 384 + c0:delta + 384 + c1b]))
                    tiles.append((kt, c0, 512, ex))
                attend(h, qb, tiles, KS, VS, po, g, 1, ya)
            pipe.flush()
            yb_ = ybf[qb % 2]
            k.copy('pool', yb_[:], ya[:], r=[ya], w=[yb_])
            k.dma('sp', k.Y[qb * 512:(qb + 1) * 512, 640:1024].rearrange("(q p) c -> p q c", p=128), yb_[:], r=[yb_])
            dbg(k, 100 + qb)


def setup_ffn(k):
    k.w_out = k.inp("w_out", [DEPTH, D, D])
    k.ffn_up = k.inp("ffn_up", [DEPTH, D, 2 * D_FF])
    k.ffn_down = k.inp("ffn_down", [DEPTH, D_FF, D])
    k.conv_w = k.inp("conv_w_fm", [DEPTH, 128, 3, 44])
    k.conv_b = k.inp("conv_b_fm", [DEPTH, 128, 44])


def load_cast(k, sc, dst, src_rows, ncols, nchunks, name, col_split=1):
    w = ncols // col_split
    stg = [sc.sb("%s_stg%d" % (name, i), [128, w], F32) for i in range(2)]
    n = 0
    for c in range(nchunks):
        for cs in range(col_split):
            s = stg[n % 2]
            k.dma('sp', s[:], src_rows(c)[:, cs * w:(cs + 1) * w], w=[s])
            k.copy('pool' if n % 2 == 0 else 'dve', dst[:, c, cs * w:(cs + 1) * w], s[:], r=[s], w=[dst])
            n += 1


def rms_scale(k, ss, st):
    k.act(st[:, 0:1], ss, AF.Ln, r=[st], w=[st], bias=RMS_EPS)
    k.act(st[:, 1:2], st[:, 0:1], AF.Exp, r=[st], w=[st], scale=-0.5)


def stage_out(k, l, xsrc, xdst):
    nc = k.nc
    with Scope(k) as sc:
        wo = sc.sb("wo", [128, 8, D], BF16)
        with Scope(k) as s2:
            load_cast(k, s2, wo, lambda c: k.w_out[l, c * 128:(c + 1) * 128, :], D, 8, "wo")
        yt = [sc.sb("yt%d" % i, [128, D], BF16) for i in range(2)]
        yT = [sc.sb("yT%d" % i, [128, 8, 128], BF16) for i in range(2)]
        xt = [sc.sb("xo%d" % i, [128, D], F32) for i in range(2)]
        tt_ = [sc.sb("to%d" % i, [128, D], F32) for i in range(2)]
        junk = sc.sb("junko", [128, 512], BF16)
        st = [sc.sb("sto%d" % i, [128, 4], F32) for i in range(2)]
        psT = [sc.ps("psTo%d" % i, [128, D], BF16) for i in range(2)]
        psY = [sc.ps("psYo%d" % i, [128, 512]) for i in range(4)]
        for ti in range(32):
            tok = slice(ti * 128, (ti + 1) * 128)
            y_ = yt[ti % 2]; yT_ = yT[ti % 2]; x_ = xt[ti % 2]; t_ = tt_[ti % 2]; st_ = st[ti % 2]; pT = psT[ti % 2]
            p0 = psY[(ti % 2) * 2]; p1 = psY[(ti % 2) * 2 + 1]
            k.dma('sp', y_[:], k.Y[tok, :], w=[y_])
            k.dma('sp', x_[:], xsrc[tok, :], w=[x_])
            for kc in range(8):
                k.transpose(pT[:, kc * 128:(kc + 1) * 128], y_[:, kc * 128:(kc + 1) * 128], k.ident_bf[:], r=[y_, k.ident_bf], w=[pT])
            k.copy('act' if ti % 2 else 'dve', yT_[:].rearrange("p a b -> p (a b)"), pT[:], r=[pT], w=[yT_])
            for half, ps in enumerate((p0, p1)):
                k.mm(ps[:], [(yT_[:, kc, :], wo[:, kc, half * 512:(half + 1) * 512]) for kc in range(8)], r=[yT_, wo], w=[ps])
                k.act(junk[:], ps[:], AF.Square, r=[ps], w=[junk, st_], scale=1.0 / 32.0, accum=st_[:, 2 + half:3 + half])
            k.tt('dve', st_[:, 2:3], st_[:, 2:3], st_[:, 3:4], ALU.add, r=[st_], w=[st_])
            rms_scale(k, st_[:, 2:3], st_)
            for half, ps in enumerate((p0, p1)):
                cs = slice(half * 512, (half + 1) * 512)
                k.stt('dve', t_[:, cs], ps[:], st_[:, 1:2], k.gm_row[:, cs], ALU.mult, ALU.mult, r=[ps, st_, k.gm_row], w=[t_])
            k.tt('pool', t_[:], t_[:], x_[:], ALU.add, r=[t_, x_], w=[t_])
            k.dma('sp', xdst[tok, :], t_[:], r=[t_])


def stage_ffn(k, l, xsrc, xdst):
    nc = k.nc
    NCH = 22
    with Scope(k) as sc:
        wu = sc.sb("wu", [128, 8, 2 * D_FF], BF16)
        wd = sc.sb("wd", [128, NCH, D], BF16)
        with Scope(k) as s2:
            load_cast(k, s2, wu, lambda c: k.ffn_up[l, c * 128:(c + 1) * 128, :], 2 * D_FF, 8, "wu", col_split=2)
            load_cast(k, s2, wd, lambda c: k.ffn_down[l, c * 128:(c + 1) * 128, :], D, NCH, "wd")
        cw = sc.sb("cw", [128, 3, 44], F32)
        cb = sc.sb("cb", [128, 44], F32)
        hal = [sc.sb("hal%d" % i, [128, 44, 2], F32) for i in range(2)]
        k.dma('sp', cw[:], k.conv_w[l], w=[cw])
        k.dma('sp', cb[:], k.conv_b[l], w=[cb])
        k.S.op('pool', lambda: nc.gpsimd.memset(hal[1][:], 0.0), [], [hal[1]])
        actT = sc.sb("actT", [128, NCH, 512], BF16)
        HT = sc.sb("H2T", [128, 8, 512], BF16)
        xt = [sc.sb("xf%d" % i, [128, D], F32) for i in range(2)]
        xn = sc.sb("xnf", [128, D], BF16)
        junk = sc.sb("junkf", [128, D], BF16)
        st = [sc.sb("stf%d" % i, [128, 4], F32) for i in range(2)]
        Tg = [sc.sb("Tg%d" % i, [128, 512], F32) for i in range(2)]
        Tv = [sc.sb("Tv%d" % i, [128, 512], F32) for i in range(2)]
        psT = sc.ps("psTf", [128, D], BF16)
        psU = [sc.ps("psU%d" % i, [128, 512]) for i in range(4)]
        psF = [sc.ps("psF%d" % i, [128, 512]) for i in range(2)]
        nx = 0
        for tb in range(NTB):
            hin = hal[(tb + 1) % 2]; hout = hal[tb % 2]
            for sub in range(4):
                ti = tb * 4 + sub
                x_ = xt[nx % 2]; st_ = st[nx % 2]; nx += 1
                k.dma('sp', x_[:], xsrc[ti * 128:(ti + 1) * 128, :], w=[x_])
                k.act(junk[:], x_[:], AF.Square, r=[x_], w=[junk, st_], scale=1.0 / 32.0, accum=st_[:, 2:3])
                rms_scale(k, st_[:, 2:3], st_)
                k.ts('dve', xn[:], x_[:], st_[:, 1:2], None, ALU.mult, None, r=[x_, st_], w=[xn])
                for kc in range(8):
                    k.transpose(psT[:, kc * 128:(kc + 1) * 128], xn[:, kc * 128:(kc + 1) * 128], k.ident_bf[:], r=[xn, k.ident_bf], w=[psT])
                for kc in range(8):
                    o = HT[:, kc, sub * 128:(sub + 1) * 128]
                    i_ = psT[:, kc * 128:(kc + 1) * 128]
                    if kc % 2 == 0:
                        k.ts('dve', o, i_, k.modAB[:, 16 + kc:17 + kc], k.modAB[:, 24 + kc:25 + kc], ALU.mult, ALU.add, r=[psT, k.modAB], w=[HT])
                    else:
                        k.act(o, i_, AF.Identity, r=[psT, k.modAB], w=[HT], scale=k.modAB[:, 16 + kc:17 + kc], bias=k.modAB[:, 24 + kc:25 + kc])
            for cp in range(NCH):
                tg = Tg[cp % 2]; tv = Tv[cp % 2]
                for which, (T_, c_) in enumerate(((tg, cp), (tv, NCH + cp))):
                    ps = psU[(cp * 2 + which) % 4]
                    k.mm(ps[:], [(wu[:, kc, c_ * 128:(c_ + 1) * 128], HT[:, kc, :]) for kc in range(8)], r=[wu, HT], w=[ps])
                    k.act(T_[:], ps[:], AF.Identity, r=[ps, cw, cb], w=[T_], scale=cw[:, 2, c_:c_ + 1], bias=cb[:, c_:c_ + 1])
                    k.stt('dve', T_[:, 1:512], ps[:, 0:511], cw[:, 1, c_:c_ + 1], T_[:, 1:512], ALU.mult, ALU.add, r=[ps, cw, T_], w=[T_])
                    k.stt('dve', T_[:, 2:512], ps[:, 0:510], cw[:, 0, c_:c_ + 1], T_[:, 2:512], ALU.mult, ALU.add, r=[ps, cw, T_], w=[T_])
                    k.copy('act', hout[:, c_, :], ps[:, 510:512], r=[ps], w=[hout])
                    k.stt('dve', T_[:, 0:1], hin[:, c_, 1:2], cw[:, 1, c_:c_ + 1], T_[:, 0:1], ALU.mult, ALU.add, r=[hin, cw, T_], w=[T_])
                    k.stt('dve', T_[:, 0:2], hin[:, c_, 0:2], cw[:, 0, c_:c_ + 1], T_[:, 0:2], ALU.mult, ALU.add, r=[hin, cw, T_], w=[T_])
                k.act(tg[:], tg[:], AF.Silu, r=[tg], w=[tg])
                k.tt('pool', actT[:, cp, :], tg[:], tv[:], ALU.mult, r=[tg, tv], w=[actT])
            for sub in range(4):
                ti = tb * 4 + sub
                tok = slice(ti * 128, (ti + 1) * 128)
                x_ = xt[nx % 2]; st_ = st[nx % 2]; nx += 1
                k.dma('sp', x_[:], xsrc[tok, :], w=[x_])
                for half in range(2):
                    ps = psF[half]
                    k.mm(ps[:], [(actT[:, cp, sub * 128:(sub + 1) * 128], wd[:, cp, half * 512:(half + 1) * 512]) for cp in range(NCH)],
                         r=[actT, wd], w=[ps])
                    k.act(junk[:, 0:512], ps[:], AF.Square, r=[ps], w=[junk, st_], scale=1.0 / 32.0, accum=st_[:, 2 + half:3 + half])
                k.tt('dve', st_[:, 2:3], st_[:, 2:3], st_[:, 3:4], ALU.add, r=[st_], w=[st_])
                rms_scale(k, st_[:, 2:3], st_)
                t_ = Tg[sub % 2] if False else None
                for half in range(2):
                    cs = slice(half * 512, (half + 1) * 512)
                    T_ = (Tg if half == 0 else Tv)[sub % 2]
                    k.stt('dve', T_[:], psF[half][:], st_[:, 1:2], k.gf_row[:, cs], ALU.mult, ALU.mult, r=[psF[half], st_, k.gf_row], w=[T_])
                    k.tt('pool', x_[:, cs], x_[:, cs], T_[:], ALU.add, r=[x_, T_], w=[x_])
                k.dma('sp', xdst[tok, :], x_[:], r=[x_])


def rwkv_host(inp):
    f = lambda a: np.ascontiguousarray(np.asarray(a, dtype=np.float32))
    mu = np.asarray(inp["rwkv_mu"])
    hd = lambda v: np.asarray(v).reshape(DEPTH, 4, 64).transpose(0, 2, 1)
    pp = np.stack([hd(mu[:, 0:256]), hd(mu[:, 256:512]), hd(mu[:, 512:768]), hd(inp["rwkv_w0"]), hd(inp["rwkv_a0"]),
                   hd(inp["rwkv_k_k"]), hd(inp["rwkv_k_a"]), hd(np.asarray(inp["rwkv_r_k"]).reshape(DEPTH, 256))], axis=2)
    lr = np.zeros((DEPTH, 64, 3), np.float32)
    lr[:, 0:32, 0] = mu[:, 768:800]; lr[:, 0:32, 1] = mu[:, 800:832]; lr[:, :, 2] = mu[:, 832:896]
    i = np.arange(64)
    mk = np.stack([(i[:, None] < i[None, :]), (i[:, None] > i[None, :]), (i[:, None] <= i[None, :]), np.eye(64, dtype=bool)]).astype(np.float32)
    cm = np.ones((64, 512), np.float32); cm[:, ::64] = 0.0
    return {"rwkv_pp": f(pp), "rwkv_lr": f(lr), "rwkv_w_up": f(inp["rwkv_w_up"]), "rwkv_a_up": f(inp["rwkv_a_up"]),
            "rwkv_g_up": f(inp["rwkv_g_up"]), "rwkv_ln": f(np.stack([np.asarray(inp["rwkv_ln_w"]), np.asarray(inp["rwkv_ln_b"])], axis=1)),
            "rwkv_masks": f(mk.transpose(1, 0, 2)), "rwkv_cmask": cm}


def setup_rwkv(k):
    k.rw_pp = k.inp("rwkv_pp", [DEPTH, 64, 8, 4])
    k.rw_lr = k.inp("rwkv_lr", [DEPTH, 64, 3])
    k.rw_wup = k.inp("rwkv_w_up", [DEPTH, 32, 256])
    k.rw_aup = k.inp("rwkv_a_up", [DEPTH, 32, 256])
    k.rw_gup = k.inp("rwkv_g_up", [DEPTH, 64, 256])
    k.rw_ln = k.inp("rwkv_ln", [DEPTH, 2, 256])
    k.rw_masks = k.inp("rwkv_masks", [64, 4, 64])
    k.rw_cmask = k.inp("rwkv_cmask", [64, 512])


def stage_rwkv(k, l):
    nc = k.nc
    H4 = [64, 4, 512]
    with Scope(k) as sc:
        pp = sc.sb("pp", [64, 8, 4], F32)
        lr = sc.sb("lr", [64, 3], F32)
        wup = sc.sb("wup", [32, 256], F32); aup = sc.sb("aup", [32, 256], F32); gup = sc.sb("gup", [64, 256], F32)
        lnr = sc.sb("lnr", [64, 2, 256], F32)
        mk = sc.sb("mk", [64, 4, 64], F32)
        cmask = sc.sb("cmask", [64, 512], F32)
        ones = sc.sb("ones64", [64, 64], F32)
        prm = sc.sb("prm", [64, 4, 4], F32)
        k.dma('sp', pp[:], k.rw_pp[l], w=[pp]); k.dma('sp', lr[:], k.rw_lr[l], w=[lr])
        k.dma('sp', wup[:], k.rw_wup[l], w=[wup]); k.dma('sp', aup[:], k.rw_aup[l], w=[aup]); k.dma('sp', gup[:], k.rw_gup[l], w=[gup])
        for i in range(2):
            k.dma('sp', lnr[:, i, :], k.rw_ln[l, i:i + 1, :].broadcast_to([64, 256]), w=[lnr])
        k.dma('sp', mk[:], k.rw_masks, w=[mk]); k.dma('sp', cmask[:], k.rw_cmask, w=[cmask])
        k.S.op('pool', lambda: nc.gpsimd.memset(ones[:], 1.0), [], [ones])
        k.ts('dve', prm[:, 0, :], pp[:, 3, :], -1.0, None, ALU.mult, None, r=[pp], w=[prm])
        k.ts('dve', prm[:, 1, :], pp[:, 6, :], -1.0, 1.0, ALU.mult, ALU.add, r=[pp], w=[prm])
        P3 = sc.sb("P3", [64, 3, 4, 512], F32)
        halo = sc.sb("halo", [64, 3, 4], F32)
        LR = sc.sb("LR", [64, 3, 512], F32)
        halo2 = sc.sb("halo2", [64, 3], F32)
        ELW = sc.sb("ELW", H4, F32); SC_ = sc.sb("SCAN", H4, F32); AA = sc.sb("AA", H4, F32); KKN = sc.sb("KKN", H4, F32)
        T1 = sc.sb("T1", H4, F32); T2 = sc.sb("T2", H4, F32)
        AT = sc.sb("AT", H4, F32); BT = sc.sb("BT", H4, F32); KT = sc.sb("KT", H4, F32); RT = sc.sb("RT", H4, F32)
        RK = sc.sb("RK", H4, F32); GAM = sc.sb("GAM", H4, F32)
        SG = sc.sb("SG", [64, 512], F32)
        XY = [sc.sb("XY%d" % i, [64, 2, 4, 64], F32) for i in range(2)]
        PP = [sc.sb("PPi%d" % i, [64, 4, 64], F32) for i in range(2)]
        AKRK = sc.sb("AKRK", [64, 2, 4, 64], F32)
        RBT = sc.sb("RBT", [64, 4, 64], F32)
        TOK = sc.sb("TOK", [64, 3, 4, 64], F32)
        Wsb = sc.sb("Wsb", [64, 4, 64], F32); Usb = sc.sb("Usb", [64, 4, 64], F32)
        Hs = [sc.sb("Hs%d" % i, [64, 4, 64], F32) for i in range(2)]
        yc = sc.sb("yc", [64, 4, 64], F32); ysq = sc.sb("ysq", [64, 4, 64], F32)
        sm = sc.sb("sm", [64, 6, 4], F32)
        yab = sc.sb("yab", [64, 8, 256], BF16)
        psM = sc.ps("psMr", [64, 512]); psK = sc.ps("psKr", [64, 512]); psB = sc.ps("psBr", [64, 512])
        psX = sc.ps("psXr", [64, 512]); psC = sc.ps("psCr", [64, 512]); psH = sc.ps("psHr", [64, 512])
        psY = sc.ps("psYr", [64, 512]); psP = sc.ps("psPr", [64, 512])
        k.S.op('pool', lambda: nc.gpsimd.memset(Hs[1][:], 0.0), [], [Hs[1]])
        k.S.op('pool', lambda: nc.gpsimd.memset(halo[:], 0.0), [], [halo])
        k.S.op('pool', lambda: nc.gpsimd.memset(halo2[:], 0.0), [], [halo2])
        bc = lambda ap, shape: ap.to_broadcast(shape)
        nchunk = 0
        for tb in range(NTB):
            t0 = tb * 512
            for q in range(3):
                k.dma('sp', P3[:, q, :, :], k.PT[q * 256:(q + 1) * 256, t0:t0 + 512].rearrange("(h d) t -> d h t", d=64), w=[P3])
            k.dma('sp', LR[0:32, 0, :], k.PT[768:800, t0:t0 + 512], w=[LR])
            k.dma('sp', LR[0:32, 1, :], k.PT[800:832, t0:t0 + 512], w=[LR])
            k.dma('sp', LR[:, 2, :], k.PT[832:896, t0:t0 + 512], w=[LR])
            for q in range(3):
                p_ = P3[:, q, :, :]
                k.tt('dve', T1[:, :, 1:512], p_[:, :, 0:511], p_[:, :, 1:512], ALU.subtract, r=[P3], w=[T1])
                k.tt('dve', T1[:, :, 0:1], halo[:, q, :].unsqueeze(2), p_[:, :, 0:1], ALU.subtract, r=[P3, halo], w=[T1])
                k.copy('pool', halo[:, q, :].unsqueeze(2), p_[:, :, 511:512], r=[P3, T1], w=[halo])
                k.tt('pool', T1[:], T1[:], bc(pp[:, q, :].unsqueeze(2), H4), ALU.mult, r=[T1, pp], w=[T1])
                k.tt('pool', p_, p_, T1[:], ALU.add, r=[P3, T1, halo], w=[P3])
            for q, rows in ((0, 32), (1, 32), (2, 64)):
                x_ = LR[0:rows, q, :]
                t_ = T2[0:rows, 0, :]
                k.tt('dve', t_[:, 1:512], x_[:, 0:511], x_[:, 1:512], ALU.subtract, r=[LR], w=[T2])
                k.tt('dve', t_[:, 0:1], halo2[0:rows, q:q + 1], x_[:, 0:1], ALU.subtract, r=[LR, halo2], w=[T2])
                k.copy('dve', halo2[0:rows, q:q + 1], x_[:, 511:512], r=[LR, T2], w=[halo2])
                k.stt('dve', x_, t_, lr[0:rows, q:q + 1], x_, ALU.mult, ALU.add, r=[T2, lr, LR, halo2], w=[LR])
            R_ = P3[:, 0, :, :]; Kp = P3[:, 1, :, :]; V_ = P3[:, 2, :, :]
            k.act(LR[0:32, 0, :], LR[0:32, 0, :], AF.Tanh, r=[LR], w=[LR])
            k.act(SG[:], LR[:, 2, :], AF.Sigmoid, r=[LR], w=[SG])
            for h in range(4):
                k.mm(psX[:, :], [(wup[:, h * 64:(h + 1) * 64], LR[0:32, 0, :])], r=[wup, LR], w=[psX])
                k.act(T1[:, h, :], psX[:, :], AF.Exp, r=[psX, prm], w=[T1], scale=-1.0, bias=prm[:, 0, h:h + 1])
                k.mm(psC[:, :], [(aup[:, h * 64:(h + 1) * 64], LR[0:32, 1, :])], r=[aup, LR], w=[psC])
                k.act(AA[:, h, :], psC[:, :], AF.Sigmoid, r=[psC, pp], w=[AA], bias=pp[:, 4, h:h + 1])
            k.act(T1[:], T1[:], AF.Ln, r=[T1], w=[T1], bias=1.0)
            k.act(ELW[:], T1[:], AF.Exp, r=[T1], w=[ELW], scale=-1.0, bias=-0.5)
            k.copy('dve', T2[:], bc(cmask[:].unsqueeze(1), H4), r=[cmask], w=[T2])
            k.S.op('dve', lambda: nc.vector.tensor_tensor_scan(
                out=SC_[:].rearrange("p h t -> p (h t)"), data0=T2[:].rearrange("p h t -> p (h t)"),
                data1=ELW[:].rearrange("p h t -> p (h t)"), initial=0.0, op0=ALU.mult, op1=ALU.add), [T2, ELW], [SC_])
            k.tt('pool', KKN[:], Kp, bc(pp[:, 5, :].unsqueeze(2), H4), ALU.mult, r=[P3, pp], w=[KKN])
            k.tt('pool', T1[:], KKN[:], KKN[:], ALU.mult, r=[KKN], w=[T1])
            for h in range(4):
                k.mm(psX[:, :], [(ones[:], T1[:, h, :])], r=[ones, T1], w=[psX])
                k.act(T2[:, h, :], psX[:, :], AF.Ln, r=[psX], w=[T2], bias=1e-24)
            k.act(T2[:], T2[:], AF.Exp, r=[T2], w=[T2], scale=-0.5)
            k.tt('dve', KKN[:], KKN[:], T2[:], ALU.mult, r=[KKN, T2], w=[KKN])
            k.tt('pool', T1[:], SC_[:], ELW[:], ALU.subtract, r=[SC_, ELW], w=[T1])
            k.act(T1[:], T1[:], AF.Exp, r=[T1], w=[T1], scale=-1.0)
            k.stt('dve', AT[:], KKN[:], -1.0, T1[:], ALU.mult, ALU.mult, r=[KKN, T1], w=[AT])
            k.act(T2[:], SC_[:], AF.Exp, r=[SC_], w=[T2])
            k.tt('pool', T1[:], KKN[:], AA[:], ALU.mult, r=[KKN, AA], w=[T1])
            k.tt('dve', BT[:], T1[:], T2[:], ALU.mult, r=[T1, T2], w=[BT])
            k.tt('pool', T1[:], AA[:], bc(pp[:, 6, :].unsqueeze(2), H4), ALU.mult, r=[AA, pp], w=[T1])
            k.tt('pool', T1[:], T1[:], bc(prm[:, 1, :].unsqueeze(2), H4), ALU.add, r=[T1, prm], w=[T1])
            k.tt('dve', Kp, Kp, T1[:], ALU.mult, r=[P3, T1, KKN], w=[P3])
            k.tt('dve', KT[:], Kp, T2[:], ALU.mult, r=[P3, T2], w=[KT])
            k.tt('pool', RK[:], R_, Kp, ALU.mult, r=[P3], w=[RK])
            k.act(GAM[:], SC_[:], AF.Exp, r=[SC_], w=[GAM], scale=-1.0)
            k.tt('dve', RT[:], R_, GAM[:], ALU.mult, r=[P3, GAM], w=[RT])
            for n in range(8):
                c_ = slice(n * 64, (n + 1) * 64)
                Hold = Hs[(nchunk + 1) % 2]; Hnew = Hs[nchunk % 2]
                xy = XY[0]
                fns = []
                for h in range(4):
                    fns.append(lambda h=h: nc.tensor.matmul(psM[:, h * 64:(h + 1) * 64], lhsT=BT[:, h, c_], rhs=AT[:, h, c_], start=True, stop=True, skip_group_check=True))
                    fns.append(lambda h=h: nc.tensor.matmul(psM[:, 256 + h * 64:256 + (h + 1) * 64], lhsT=AT[:, h, c_], rhs=BT[:, h, c_], start=False, stop=True, skip_group_check=True))
                k.S.pe_group(fns, [AT, BT], [psM])
                fns = []
                for h in range(4):
                    fns.append(lambda h=h: nc.tensor.matmul(psK[:, h * 64:(h + 1) * 64], lhsT=KT[:, h, c_], rhs=AT[:, h, c_], start=(h == 0), stop=True, skip_group_check=True))
                    fns.append(lambda h=h: nc.tensor.matmul(psK[:, 256 + h * 64:256 + (h + 1) * 64], lhsT=KT[:, h, c_], rhs=RT[:, h, c_], start=False, stop=True, skip_group_check=True))
                k.S.pe_group(fns, [AT, KT, RT], [psK])
                fns = []
                for h in range(4):
                    fns.append(lambda h=h: nc.tensor.matmul(psB[:, h * 64:(h + 1) * 64], lhsT=BT[:, h, c_], rhs=RT[:, h, c_], start=(h == 0), stop=True, skip_group_check=True))
                k.S.pe_group(fns, [BT, RT], [psB])
                psM2 = psM[:, :].rearrange("p (a h f) -> p a h f", a=2, h=4)
                k.tt('dve', xy[:, 0, :, :], psM2[:, 0, :, :], bc(mk[:, 0, :].unsqueeze(1), [64, 4, 64]), ALU.mult, r=[psM, mk], w=[xy])
                k.tt('dve', xy[:, 1, :, :], psM2[:, 1, :, :], bc(mk[:, 1, :].unsqueeze(1), [64, 4, 64]), ALU.mult, r=[psM, mk], w=[xy])
                psK2 = psK[:, :].rearrange("p (a h f) -> p a h f", a=2, h=4)
                k.tt('dve', AKRK[:, 0, :, :], psK2[:, 0, :, :], bc(mk[:, 0, :].unsqueeze(1), [64, 4, 64]), ALU.mult, r=[psK, mk], w=[AKRK])
                k.tt('dve', AKRK[:, 1, :, :], psK2[:, 1, :, :], bc(mk[:, 2, :].unsqueeze(1), [64, 4, 64]), ALU.mult, r=[psK, mk], w=[AKRK])
                k.tt('dve', RBT[:], psB[:, 0:256].rearrange("p (h f) -> p h f", h=4), bc(mk[:, 2, :].unsqueeze(1), [64, 4, 64]), ALU.mult, r=[psB, mk], w=[RBT])
                fns = []
                for qi, src in enumerate((V_, BT, KT)):
                    for h in range(4):
                        sl_ = src[:, h, c_]
                        fns.append(lambda qi=qi, h=h, sl_=sl_: nc.tensor.transpose(out=psC[:, (qi % 2) * 256 + h * 64:(qi % 2) * 256 + (h + 1) * 64] if qi < 2 else psP[:, h * 64:(h + 1) * 64],
                                                                                  in_=sl_, identity=k.ident_f[0:64, 0:64]))
                k.S.pe_group(fns, [P3, BT, KT, k.ident_f], [psC, psP])
                k.copy('act', TOK[:, 0:2, :, :].rearrange("p a h f -> p (a h f)"), psC[:, :], r=[psC], w=[TOK])
                k.copy('act', TOK[:, 2, :, :].rearrange("p h f -> p (h f)"), psP[:, 0:256], r=[psP], w=[TOK])
                P_ = PP[0]
                k.tt('dve', P_[:], xy[:, 0, :, :], bc(mk[:, 3, :].unsqueeze(1), [64, 4, 64]), ALU.add, r=[xy, mk], w=[P_])
                for lev in range(5):
                    xyn = XY[(lev + 1) % 2]
                    fns = []
                    for h in range(4):
                        fns.append(lambda h=h, xy=xy: nc.tensor.matmul(psX[:, 256 + h * 64:256 + (h + 1) * 64], lhsT=xy[:, 0, h, :], rhs=xy[:, 1, h, :], start=(h == 0), stop=True, skip_group_check=True))
                        if lev < 4:
                            fns.append(lambda h=h, xy=xy: nc.tensor.matmul(psX[:, h * 64:(h + 1) * 64], lhsT=xy[:, 1, h, :], rhs=xy[:, 0, h, :], start=False, stop=True, skip_group_check=True))
                    k.S.pe_group(fns, [xy], [psX])
                    if lev < 4:
                        k.copy('act', xyn[:].rearrange("p a h f -> p (a h f)"), psX[:, :], r=[psX], w=[xyn])
                    else:
                        k.copy('act', xyn[:, 1, :, :].rearrange("p h f -> p (h f)"), psX[:, 256:512], r=[psX], w=[xyn])
                    Pn = PP[(lev + 1) % 2]
                    k.S.pe_group([lambda h=h, xyn=xyn, P_=P_: nc.tensor.matmul(psP[:, h * 64:(h + 1) * 64], lhsT=xyn[:, 1, h, :], rhs=P_[:, h, :], start=(h == 0), stop=True, skip_group_check=True)
                                  for h in range(4)], [xyn, P_], [psP])
                    k.tt('dve', Pn[:], P_[:], psP[:, 0:256].rearrange("p (h f) -> p h f", h=4), ALU.add, r=[P_, psP], w=[Pn])
                    P_ = Pn; xy = xyn
                TT = P_
                fns = []
                for h in range(4):
                    fns.append(lambda h=h: nc.tensor.matmul(psH[:, h * 64:(h + 1) * 64], lhsT=AT[:, h, c_], rhs=Hold[:, h, :], start=(h == 0), stop=False, skip_group_check=True))
                    fns.append(lambda h=h: nc.tensor.matmul(psH[:, h * 64:(h + 1) * 64], lhsT=AKRK[:, 0, h, :], rhs=TOK[:, 0, h, :], start=False, stop=True, skip_group_check=True))
                k.S.pe_group(fns, [AT, Hold, AKRK, TOK], [psH])
                k.copy('act', Wsb[:].rearrange("p h f -> p (h f)"), psH[:, 0:256], r=[psH], w=[Wsb])
                k.S.pe_group([lambda h=h: nc.tensor.matmul(psH[:, 256 + h * 64:256 + (h + 1) * 64], lhsT=TT[:, h, :], rhs=Wsb[:, h, :], start=False, stop=True, skip_group_check=True)
                              for h in range(4)], [TT, Wsb], [psH])
                k.copy('act', Usb[:].rearrange("p h f -> p (h f)"), psH[:, 256:512], r=[psH], w=[Usb])
                fns = []
                for h in range(4):
                    fns.append(lambda h=h: nc.tensor.matmul(psY[:, h * 64:(h + 1) * 64], lhsT=RT[:, h, c_], rhs=Hold[:, h, :], start=(h == 0), stop=False, skip_group_check=True))
                    fns.append(lambda h=h: nc.tensor.matmul(psY[:, h * 64:(h + 1) * 64], lhsT=RBT[:, h, :], rhs=Usb[:, h, :], start=False, stop=False, skip_group_check=True))
                    fns.append(lambda h=h: nc.tensor.matmul(psY[:, h * 64:(h + 1) * 64], lhsT=AKRK[:, 1, h, :], rhs=TOK[:, 0, h, :], start=False, stop=True, skip_group_check=True))
                    fns.append(lambda h=h: nc.tensor.matmul(psY[:, 256 + h:256 + h + 1], lhsT=RK[:, h, c_], rhs=pp[:, 7, h:h + 1], start=False, stop=True, skip_group_check=True))
                fns.append(lambda: nc.tensor.matmul(psB[:, 256:512], lhsT=SG[:, c_], rhs=gup[:, :], start=False, stop=True, skip_group_check=True))
                k.S.pe_group(fns, [RT, Hold, RBT, Usb, AKRK, TOK, RK, pp, SG, gup], [psY, psB])
                fns = []
                for h in range(4):
                    fns.append(lambda h=h: nc.tensor.matmul(psC[:, h * 64:(h + 1) * 64], lhsT=TOK[:, 1, h, :], rhs=Usb[:, h, :], start=(h == 0), stop=False, skip_group_check=True))
                    fns.append(lambda h=h: nc.tensor.matmul(psC[:, h * 64:(h + 1) * 64], lhsT=TOK[:, 2, h, :], rhs=TOK[:, 0, h, :], start=False, stop=True, skip_group_check=True))
                k.S.pe_group(fns, [TOK, Usb], [psC])
                k.tt('dve', Hnew[:], psC[:, 0:256].rearrange("p (h f) -> p h f", h=4), Hold[:], ALU.add, r=[psC, Hold], w=[Hnew])
                k.tt('dve', Hnew[:], Hnew[:], bc(GAM[:, :, n * 64 + 63:n * 64 + 64], [64, 4, 64]), ALU.mult, r=[Hnew, GAM], w=[Hnew])
                y3 = psY[:, 0:256].rearrange("p (h f) -> p h f", h=4)
                k.S.op('dve', lambda: nc.vector.reduce_sum(out=sm[:, 0, :], in_=y3, axis=AX.X), [psY], [sm])
                k.ts('dve', sm[:, 1, :], sm[:, 0, :], 1.0 / 64.0, None, ALU.mult, None, r=[sm], w=[sm])
                k.tt('dve', yc[:], y3, bc(sm[:, 1, :].unsqueeze(2), [64, 4, 64]), ALU.subtract, r=[psY, sm], w=[yc])
                k.tt('pool', ysq[:], yc[:], yc[:], ALU.mult, r=[yc], w=[ysq])
                k.S.op('dve', lambda: nc.vector.reduce_sum(out=sm[:, 2, :], in_=ysq[:], axis=AX.X), [ysq], [sm])
                k.act(sm[:, 3, :], sm[:, 2, :], AF.Ln, r=[sm], w=[sm], scale=1.0 / 64.0, bias=GN_EPS)
                k.act(sm[:, 4, :], sm[:, 3, :], AF.Exp, r=[sm], w=[sm], scale=-0.5)
                k.copy('dve', sm[:, 5, :], psY[:, 256:260], r=[psY], w=[sm])
                k.tt('dve', yc[:], yc[:], bc(sm[:, 4, :].unsqueeze(2), [64, 4, 64]), ALU.mult, r=[yc, sm], w=[yc])
                k.tt('pool', yc[:], yc[:], lnr[:, 0, :].rearrange("p (h f) -> p h f", h=4), ALU.mult, r=[yc, lnr], w=[yc])
                k.tt('pool', yc[:], yc[:], lnr[:, 1, :].rearrange("p (h f) -> p h f", h=4), ALU.add, r=[yc, lnr], w=[yc])
                k.tt('dve', ysq[:], TOK[:, 0, :, :], bc(sm[:, 5, :].unsqueeze(2), [64, 4, 64]), ALU.mult, r=[TOK, sm], w=[ysq])
                k.tt('pool', yc[:], yc[:], ysq[:], ALU.add, r=[yc, ysq], w=[yc])
                k.tt('dve', yab[:, n, :], yc[:].rearrange("p h f -> p (h f)"), psB[:, 256:512], ALU.mult, r=[yc, psB], w=[yab])
                nchunk += 1
            k.dma('sp', k.Y[t0:t0 + 512, 0:256].rearrange("(n p) c -> p n c", p=64), yab[:], r=[yab])


def build(nlayers=DEPTH, taps=()):
    k = K(nlayers, taps=taps)
    setup_globals(k)
    setup_fox(k)
    setup_rwkv(k)
    setup_ffn(k)
    setup_nsa(k)
    for l in range(nlayers):
        xin = k.x_in if l == 0 else k.XR
        xout = k.OUT if l == nlayers - 1 else k.XR
        stage_mod(k, l)
        stage_proj(k, l, xin)
        stage_rwkv(k, l)
        stage_fox(k, l)
        stage_nsa(k, l)
        stage_out(k, l, xin, k.XR1)
        stage_ffn(k, l, k.XR1, xout)
    k.S.barrier()
    return k


_CACHE = {}


def kernel(**inputs):
    if "k" not in _CACHE:
        _CACHE["k"] = build(DEPTH)
    k = _CACHE["k"]
    sh = prep_shared(inputs)
    in_maps = []
    for b in range(8):
        d = dict(sh)
        d.update(prep_core(inputs, b))
        in_maps.append({n: v for n, v in d.items() if n in k.ins})
    res = run_bass_kernel_spmd(k.nc, in_maps, core_ids=list(range(8)))
    out = np.stack([np.asarray(res.results[b]["out"], dtype=np.float32) for b in range(8)], axis=0)
    return out
```

```python
import numpy as np
import ml_dtypes
from contextlib import ExitStack
import concourse.bass as bass
import concourse.mybir as mybir
from concourse.bass_utils import run_bass_kernel_spmd

F32 = mybir.dt.float32
BF16 = mybir.dt.bfloat16
AF = mybir.ActivationFunctionType
ALU = mybir.AluOpType
AX = mybir.AxisListType
NPBF = ml_dtypes.bfloat16

S_LEN = 4096
D = 1024
DEPTH = 4
NTB = 8
N_IN = 3224
D_FF = 2816
NEG = -30000.0
RMS_EPS = 1e-6
GN_EPS = 64e-5


class Sched:
    ENG = ('pe', 'act', 'dve', 'pool')
    LIMIT = 30000

    def __init__(self, nc):
        self.nc = nc
        self.e = {'pe': nc.tensor, 'act': nc.scalar, 'dve': nc.vector, 'pool': nc.gpsimd, 'sp': nc.sync}
        self.epoch = {k: 0 for k in self.ENG}
        self.sem = {k: nc.alloc_semaphore("c_%s_0" % k) for k in self.ENG}
        self.cnt = {k: 0 for k in self.ENG}
        self.seen = {k: {} for k in self.e}
        self.lastw = {}
        self.reads = {}
        self.dma_sems = {'hw': [[nc.alloc_semaphore("d%d" % i), 0, "dma%d" % i] for i in range(24)],
                         'sw': [[nc.alloc_semaphore("ds%d" % i), 0, "dmas%d" % i] for i in range(8)]}
        self.ndma = {'hw': 0, 'sw': 0}
        self.n_inst = 0
        self.n_wait = 0
        self.per = {}

    def _wait(self, eng, tok):
        key, sem, val = tok
        if self.seen[eng].get(key, 0) >= val:
            return
        self.e[eng].wait_ge(sem, val)
        self.n_wait += 1
        self.per[eng] = self.per.get(eng, 0) + 1
        self.seen[eng][key] = val

    def _deps(self, eng, reads, writes):
        for b in reads:
            t = self.lastw.get(b)
            if t is not None:
                self._wait(eng, t)
        for b in writes:
            t = self.lastw.get(b)
            if t is not None:
                self._wait(eng, t)
            for t in self.reads.get(b, ()):
                self._wait(eng, t)

    def _commit(self, tok, reads, writes):
        for b in reads:
            self.reads.setdefault(b, []).append(tok)
        for b in writes:
            self.lastw[b] = tok
            self.reads[b] = []

    def _bump(self, eng, ins):
        if self.cnt[eng] >= self.LIMIT:
            self.epoch[eng] += 1
            self.sem[eng] = self.nc.alloc_semaphore("c_%s_%d" % (eng, self.epoch[eng]))
            self.cnt[eng] = 0
        self.cnt[eng] += 1
        ins.then_inc(self.sem[eng], 1)
        return ("%s_%d" % (eng, self.epoch[eng]), self.sem[eng], self.cnt[eng])

    @staticmethod
    def _norm(reads, writes):
        rd = [getattr(b, 'n', b) for b in reads]
        wr = [getattr(b, 'n', b) for b in writes]
        ps = [b for b in rd if b.startswith("ps")]
        rd = [b for b in rd if not b.startswith("ps")]
        return rd, wr + [b for b in ps if b not in wr]

    def op(self, eng, inst_fn, reads=(), writes=()):
        reads, writes = self._norm(reads, writes)
        self._deps(eng, reads, writes)
        ins = inst_fn()
        self.per[eng] = self.per.get(eng, 0) + 1
        tok = self._bump(eng, ins)
        self._commit(tok, reads, writes)
        self.n_inst += 1
        return tok

    def pe_group(self, fns, reads=(), writes=()):
        reads, writes = self._norm(reads, writes)
        self._deps('pe', reads, writes)
        ins = None
        for f in fns:
            ins = f()
            self.n_inst += 1
            self.per['pe'] = self.per.get('pe', 0) + 1
        tok = self._bump('pe', ins)
        self._commit(tok, reads, writes)
        return tok

    def dma(self, q, out, in_, reads=(), writes=(), **kw):
        reads, writes = self._norm(reads, writes)
        self._deps(q, reads, writes)
        cls = 'sw' if q == 'pool' else 'hw'
        pool_ = self.dma_sems[cls]
        slot = pool_[self.ndma[cls] % len(pool_)]
        self.ndma[cls] += 1
        if slot[1] > 0:
            self._wait(q, (slot[2], slot[0], slot[1]))
        if slot[1] >= self.LIMIT:
            slot[0] = self.nc.alloc_semaphore("%s_e%d" % (slot[2], self.ndma[cls]))
            slot[1] = 0
            slot[2] = slot[2] + "x"
        slot[1] += 16
        ins = self.e[q].dma_start(out=out, in_=in_, **kw)
        self.per[q] = self.per.get(q, 0) + 1
        ins.then_inc(slot[0], 16)
        tok = (slot[2], slot[0], slot[1])
        self._commit(tok, reads, writes)
        self.n_inst += 1
        return tok

    def barrier(self, engines=('pe', 'act', 'dve', 'pool', 'sp')):
        toks = [("%s_%d" % (k, self.epoch[k]), self.sem[k], self.cnt[k]) for k in self.ENG if self.cnt[k] > 0]
        toks += [(s[2], s[0], s[1]) for p_ in self.dma_sems.values() for s in p_ if s[1] > 0]
        for e in engines:
            for t in toks:
                self._wait(e, t)
        self.lastw = {}
        self.reads = {}


class Pipe:
    def __init__(self, lag=2):
        self.q = []
        self.lag = lag

    def push(self, first, second):
        first()
        self.q.append(second)
        while len(self.q) > self.lag:
            self.q.pop(0)()

    def flush(self):
        while self.q:
            self.q.pop(0)()


class Tile:
    def __init__(self, h, name):
        self.h = h
        self.n = name

    def __getitem__(self, idx):
        return self.h[idx]


class Scope:
    cnt = 0

    def __init__(self, k):
        self.k = k
        self.es = ExitStack()

    def __enter__(self):
        self.es.__enter__()
        Scope.cnt += 1
        self.id = Scope.cnt
        return self

    def sb(self, name, shape, dt):
        nm = "%s_%d" % (name, self.id)
        h = self.es.enter_context(self.k.nc.sbuf_tensor(nm, list(shape), dt))
        return Tile(h, nm)

    def ps(self, name, shape, dt=F32):
        nm = "%s_%d" % (name, self.id)
        h = self.es.enter_context(self.k.nc.psum_tensor(nm, list(shape), dt))
        return Tile(h, nm)

    def __exit__(self, *a):
        self.k.S.barrier()
        return self.es.__exit__(*a)


class K:
    def __init__(self, nlayers, taps=()):
        self.nc = bass.Bass("TRN2", target_bir_lowering=False)
        self.S = Sched(self.nc)
        self.nl = nlayers
        self.taps = set(taps)
        self.ins = {}
        self.dr = {}

    def inp(self, name, shape, dt=F32):
        t = self.nc.dram_tensor(name, list(shape), dt, kind="ExternalInput").ap()
        self.ins[name] = t
        return t

    def scratch(self, name, shape, dt=F32, out=False):
        kind = "ExternalOutput" if (out or name in self.taps) else "Internal"
        t = self.nc.dram_tensor(name, list(shape), dt, kind=kind).ap()
        self.dr[name] = t
        return t

    def act(self, out, in_, func, r, w, bias=0.0, scale=1.0, accum=None):
        nc = self.nc
        if accum is None:
            return self.S.op('act', lambda: nc.scalar.activation(out=out, in_=in_, func=func, bias=bias, scale=scale), r, w)
        return self.S.op('act', lambda: nc.scalar.activation(out=out, in_=in_, func=func, bias=bias, scale=scale, accum_out=accum), r, w)

    def ts(self, eng, out, in0, s1, s2, op0, op1, r, w):
        e = self.S.e[eng]
        if op1 is None:
            return self.S.op(eng, lambda: e.tensor_scalar(out=out, in0=in0, scalar1=s1, scalar2=None, op0=op0), r, w)
        return self.S.op(eng, lambda: e.tensor_scalar(out=out, in0=in0, scalar1=s1, scalar2=s2, op0=op0, op1=op1), r, w)

    def tt(self, eng, out, in0, in1, op, r, w):
        e = self.S.e[eng]
        return self.S.op(eng, lambda: e.tensor_tensor(out=out, in0=in0, in1=in1, op=op), r, w)

    def stt(self, eng, out, in0, scalar, in1, op0, op1, r, w):
        e = self.S.e[eng]
        return self.S.op(eng, lambda: e.scalar_tensor_tensor(out=out, in0=in0, scalar=scalar, in1=in1, op0=op0, op1=op1), r, w)

    def copy(self, eng, out, in_, r, w):
        if eng == 'act':
            return self.S.op('act', lambda: self.nc.scalar.copy(out=out, in_=in_), r, w)
        e = self.S.e[eng]
        return self.S.op(eng, lambda: e.tensor_copy(out=out, in_=in_), r, w)

    def mm(self, out, pairs, r, w, start=True, stop=True, sgc=False):
        nc = self.nc
        n = len(pairs)
        fns = []
        for i, (l, rh) in enumerate(pairs):
            fns.append(lambda l=l, rh=rh, i=i: nc.tensor.matmul(out, lhsT=l, rhs=rh, start=(start and i == 0), stop=(stop and i == n - 1),
                                                               skip_group_check=(sgc or not start)))
        return self.S.pe_group(fns, r, w)

    def transpose(self, out, in_, ident, r, w):
        nc = self.nc
        return self.S.op('pe', lambda: nc.tensor.transpose(out=out, in_=in_, identity=ident), r, w)

    def dma(self, q, out, in_, r=(), w=(), **kw):
        return self.S.dma(q, out, in_, r, w, **kw)


def w_in_perm_index():
    idx = list(range(0, 896))
    idx += list(range(896, 1664))
    for c in range(3):
        idx += list(range(2054 + c * 64, 2054 + c * 64 + 64))
        idx += list(range(2054 + (c + 3) * 64, 2054 + (c + 3) * 64 + 64))
    idx += list(range(2438, 2566))
    idx += list(range(2566, 2694))
    idx += list(range(2694, 2822))
    idx += list(range(2950, 3078))
    idx += list(range(2048, 2054))
    idx += list(range(1664, 2048))
    idx += list(range(2822, 2950))
    idx += list(range(3078, 3206))
    idx += list(range(3206, 3224))
    assert len(idx) == N_IN and len(set(idx)) == N_IN
    return np.array(idx)


QKT_ROWS = 1664


def setup_globals(k):
    nc = k.nc
    k.x_in = k.inp("x", [S_LEN, D])
    k.cT = k.inp("cT", [128, 8])
    k.ada_w = k.inp("ada_w", [DEPTH, D, 6 * D])
    k.ada_b_fm = k.inp("ada_b_fm", [DEPTH, 128, 48])
    k.ada_b_row = k.inp("ada_b_row", [DEPTH, 6 * D])
    k.normg_fm = k.inp("normg_fm", [DEPTH, 4, 128, 8])
    k.normg_row = k.inp("normg_row", [DEPTH, 4, D])
    k.w_in = k.inp("w_in_p", [DEPTH, D, N_IN])
    k.ident_bf_d = k.inp("ident_bf", [128, 128], BF16)
    k.ident_f_d = k.inp("ident_f", [128, 128], F32)

    k.PT = k.scratch("PT", [896, S_LEN], F32)
    k.QKT = k.scratch("QKT", [QKT_ROWS, S_LEN], BF16)
    k.FL = k.scratch("FL", [6, S_LEN], F32)
    k.VT = k.scratch("VT", [S_LEN, 640], BF16)
    k.GT = k.scratch("GT", [S_LEN, 18], F32)
    k.Y = k.scratch("Y", [S_LEN, D], BF16)
    k.XR = k.scratch("XR", [S_LEN, D], F32)
    k.XR1 = k.scratch("XR1", [S_LEN, D], F32)
    k.OUT = k.scratch("out", [S_LEN, D], F32, out=True)

    def pers(name, shape, dt):
        return Tile(nc.alloc_sbuf_tensor(name, list(shape), dt), name)
    k.ident_bf = pers("ident_bf_sb", [128, 128], BF16)
    k.ident_f = pers("ident_f_sb", [128, 128], F32)
    k.sc = pers("sc", [128, 8], F32)
    k.sc_rep = pers("sc_rep", [128, 8, 128], F32)
    k.modAB = pers("modAB", [128, 32], F32)
    k.gm_row = pers("gm_row", [128, D], F32)
    k.gf_row = pers("gf_row", [128, D], F32)
    k.dma('sp', k.ident_bf[:], k.ident_bf_d, w=[k.ident_bf])
    k.dma('sp', k.ident_f[:], k.ident_f_d, w=[k.ident_f])
    k.dma('sp', k.sc[:], k.cT, w=[k.sc])
    k.act(k.sc[:], k.sc[:], AF.Silu, r=[k.sc], w=[k.sc])
    for kc in range(8):
        k.copy('dve', k.sc_rep[:, kc, :], k.sc[:, kc:kc + 1].to_broadcast([128, 128]), r=[k.sc], w=[k.sc_rep])


def stage_mod(k, l):
    with Scope(k) as sc:
        slab = [sc.sb("adaslab%d" % i, [128, 6 * D], F32) for i in range(2)]
        psA = sc.ps("psA", [128, 32])
        psR = [sc.ps("psR%d" % i, [128, 512]) for i in range(4)]
        bfm = sc.sb("bfm", [128, 48], F32)
        gfm = sc.sb("gfm", [128, 4, 8], F32)
        brow = sc.sb("brow", [128, 2, D], F32)
        grow = sc.sb("grow", [128, 2, D], F32)
        mfm = sc.sb("mfm", [128, 32], F32)
        k.dma('sp', bfm[:], k.ada_b_fm[l], w=[bfm])
        k.dma('sp', gfm[:], k.normg_fm[l].rearrange("g p c -> p g c"), w=[gfm])
        k.dma('sp', brow[:, 0, :], k.ada_b_row[l:l + 1, 2 * D:3 * D].broadcast_to([128, D]), w=[brow])
        k.dma('sp', brow[:, 1, :], k.ada_b_row[l:l + 1, 5 * D:6 * D].broadcast_to([128, D]), w=[brow])
        k.dma('sp', grow[:, 0, :], k.normg_row[l, 1:2, :].broadcast_to([128, D]), w=[grow])
        k.dma('sp', grow[:, 1, :], k.normg_row[l, 3:4, :].broadcast_to([128, D]), w=[grow])
        fm_chunks = list(range(0, 16)) + list(range(24, 40))
        row_cols = [2 * D, 2 * D + 512, 5 * D, 5 * D + 512]
        for kc in range(8):
            sl = slab[kc % 2]
            k.dma('sp', sl[:], k.ada_w[l, kc * 128:(kc + 1) * 128, :], w=[sl])
            for i, j in enumerate(fm_chunks):
                k.mm(psA[:, i:i + 1], [(sl[:, j * 128:(j + 1) * 128], k.sc[:, kc:kc + 1])], r=[sl, k.sc], w=[psA],
                     start=(kc == 0 and i == 0), stop=(kc == 7), sgc=True)
            for i, c0 in enumerate(row_cols):
                k.mm(psR[i][:], [(k.sc_rep[:, kc, :], sl[:, c0:c0 + 512])], r=[sl, k.sc_rep], w=[psR[i]],
                     start=(kc == 0), stop=(kc == 7), sgc=True)
        k.tt('dve', mfm[:, 0:16], psA[:, 0:16], bfm[:, 0:16], ALU.add, r=[psA, bfm], w=[mfm])
        k.tt('dve', mfm[:, 16:32], psA[:, 16:32], bfm[:, 24:40], ALU.add, r=[psA, bfm], w=[mfm])
        k.stt('dve', k.modAB[:, 0:8], mfm[:, 8:16], 1.0, gfm[:, 0, :], ALU.add, ALU.mult, r=[mfm, gfm], w=[k.modAB])
        k.copy('dve', k.modAB[:, 8:16], mfm[:, 0:8], r=[mfm], w=[k.modAB])
        k.stt('dve', k.modAB[:, 16:24], mfm[:, 24:32], 1.0, gfm[:, 2, :], ALU.add, ALU.mult, r=[mfm, gfm], w=[k.modAB])
        k.copy('dve', k.modAB[:, 24:32], mfm[:, 16:24], r=[mfm], w=[k.modAB])
        for i in range(4):
            dst = (k.gm_row if i < 2 else k.gf_row)
            cs = slice((i % 2) * 512, (i % 2) * 512 + 512)
            k.tt('dve', dst[:, cs], psR[i][:], brow[:, i // 2, cs], ALU.add, r=[psR[i], brow], w=[dst])
            k.tt('pool', dst[:, cs], dst[:, cs], grow[:, i // 2, cs], ALU.mult, r=[dst, grow], w=[dst])


def stage_proj(k, l, xsrc):
    nc = k.nc
    with Scope(k) as sc:
        wsb = sc.sb("wsb", [128, 8, N_IN], BF16)
        wst = [sc.sb("wst%d" % i, [128, N_IN], F32) for i in range(2)]
        xt = [sc.sb("xt%d" % i, [128, D], F32) for i in range(2)]
        junk = sc.sb("junk", [128, D], BF16)
        xn = [sc.sb("xn%d" % i, [128, D], BF16) for i in range(2)]
        st = [sc.sb("st%d" % i, [128, 4], F32) for i in range(2)]
        HT = [sc.sb("HT%d" % i, [128, 8, 512], BF16) for i in range(2)]
        psT = [sc.ps("psT%d" % i, [128, D], BF16) for i in range(2)]
        psM = [sc.ps("psM%d" % i, [128, 512]) for i in range(4)]
        evf = [sc.sb("evf%d" % i, [128, 512], F32) for i in range(3)]
        evb = [sc.sb("evb%d" % i, [128, 512], BF16) for i in range(3)]
        evt = [sc.sb("evt%d" % i, [128, 640], BF16) for i in range(2)]
        evg = [sc.sb("evg%d" % i, [128, 18], F32) for i in range(2)]
        for kc in range(8):
            s = wst[kc % 2]
            k.dma('sp', s[:], k.w_in[l, kc * 128:(kc + 1) * 128, :], w=[s])
            k.copy('pool', wsb[:, kc, :], s[:], r=[s], w=[wsb])
        nev = 0
        npm = 0
        for tb in range(NTB):
            ht = HT[tb % 2]
            for sub in range(4):
                ti = tb * 4 + sub
                x_ = xt[ti % 2]; xn_ = xn[ti % 2]; st_ = st[ti % 2]; pt_ = psT[ti % 2]
                k.dma('act', x_[:], xsrc[ti * 128:(ti + 1) * 128, :], w=[x_])
                k.act(junk[:], x_[:], AF.Square, r=[x_], w=[junk, st_], scale=1.0 / 32.0, accum=st_[:, 0:1])
                k.act(st_[:, 1:2], st_[:, 0:1], AF.Ln, r=[st_], w=[st_], bias=RMS_EPS)
                k.act(st_[:, 2:3], st_[:, 1:2], AF.Exp, r=[st_], w=[st_], scale=-0.5)
                k.ts('dve', xn_[:], x_[:], st_[:, 2:3], None, ALU.mult, None, r=[x_, st_], w=[xn_])
                for kc in range(8):
                    k.transpose(pt_[:, kc * 128:(kc + 1) * 128], xn_[:, kc * 128:(kc + 1) * 128], k.ident_bf[:],
                                r=[xn_, k.ident_bf], w=[pt_])
                for kc in range(8):
                    o = ht[:, kc, sub * 128:(sub + 1) * 128]
                    i_ = pt_[:, kc * 128:(kc + 1) * 128]
                    if kc % 2 == 0:
                        k.ts('dve', o, i_, k.modAB[:, kc:kc + 1], k.modAB[:, 8 + kc:9 + kc], ALU.mult, ALU.add,
                             r=[pt_, k.modAB], w=[ht])
                    else:
                        k.act(o, i_, AF.Identity, r=[pt_, k.modAB], w=[ht], scale=k.modAB[:, kc:kc + 1],
                              bias=k.modAB[:, 8 + kc:9 + kc])
            tsl = slice(tb * 512, (tb + 1) * 512)
            fm = [(c * 128, 128, 'PT', c * 128) for c in range(7)]
            fm += [(896 + c * 128, 128, 'QKT', c * 128) for c in range(13)]
            fm += [(2560, 6, 'FL', 0)]
            for (c0, m, dst, r0) in fm:
                ps = psM[npm % 4]; npm += 1
                k.mm(ps[0:m, :], [(wsb[:, kc, c0:c0 + m], ht[:, kc, :]) for kc in range(8)], r=[wsb, ht], w=[ps])
                eng = 'act' if nev % 2 == 0 else 'dve'
                if dst == 'QKT':
                    ev = evb[nev % 3]
                    dd = k.QKT[r0:r0 + m, tsl]
                else:
                    ev = evf[nev % 3]
                    dd = (k.PT if dst == 'PT' else k.FL)[r0:r0 + m, tsl]
                nev += 1
                k.copy(eng, ev[0:m, :], ps[0:m, :], r=[ps], w=[ev])
                k.dma('sp', dd, ev[0:m, :], r=[ev])
            for sub in range(4):
                ti = tb * 4 + sub
                tok = slice(ti * 128, (ti + 1) * 128)
                ps0 = psM[npm % 4]; npm += 1
                ps1 = psM[npm % 4]; npm += 1
                lhs = lambda kc: ht[:, kc, sub * 128:(sub + 1) * 128]
                k.mm(ps0[:, 0:384], [(lhs(kc), wsb[:, kc, 2566:2950]) for kc in range(8)], r=[wsb, ht], w=[ps0])
                k.mm(ps1[:, 0:274], [(lhs(kc), wsb[:, kc, 2950:3224]) for kc in range(8)], r=[wsb, ht], w=[ps1])
                et = evt[ti % 2]; eg = evg[ti % 2]
                k.copy('act', et[:, 0:384], ps0[:, 0:384], r=[ps0], w=[et])
                k.copy('dve', et[:, 384:640], ps1[:, 0:256], r=[ps1], w=[et])
                k.copy('dve', eg[:], ps1[:, 256:274], r=[ps1], w=[eg])
                k.dma('sp', k.VT[tok, :], et[:], r=[et])
                k.dma('sp', k.GT[tok, :], eg[:], r=[eg])


def prep_shared(inp):
    f = lambda a: np.ascontiguousarray(np.asarray(a, dtype=np.float32))
    sh = {}
    sh["ada_w"] = f(inp["ada_w"])
    sh["ada_b_fm"] = f(np.asarray(inp["ada_b"]).reshape(DEPTH, 48, 128).transpose(0, 2, 1))
    sh["ada_b_row"] = f(inp["ada_b"])
    sh["normg_fm"] = f(np.asarray(inp["norm_g"]).reshape(DEPTH, 4, 8, 128).transpose(0, 1, 3, 2))
    sh["normg_row"] = f(inp["norm_g"])
    sh["w_in_p"] = f(np.asarray(inp["w_in"])[:, :, w_in_perm_index()])
    sh["ident_bf"] = np.eye(128, dtype=np.float32).astype(NPBF)
    sh["ident_f"] = np.eye(128, dtype=np.float32)
    sh["w_out"] = f(inp["w_out"]); sh["ffn_up"] = f(inp["ffn_up"]); sh["ffn_down"] = f(inp["ffn_down"])
    sh["conv_w_fm"] = f(np.asarray(inp["ffn_conv_w"]).reshape(DEPTH, 3, 44, 128).transpose(0, 3, 1, 2))
    sh["conv_b_fm"] = f(np.asarray(inp["ffn_conv_b"]).reshape(DEPTH, 44, 128).transpose(0, 2, 1))
    sh.update(nsa_host_consts())
    sh["rel_bias"] = f(inp["rel_bias"])
    sh["nsa_pe_kT"] = f(np.asarray(inp["nsa_pe_k"]).transpose(0, 2, 1))
    sh["nsa_pe_vT"] = f(np.asarray(inp["nsa_pe_v"]).transpose(0, 2, 1))
    for n in ("nsa_ck_w1", "nsa_cv_w1", "nsa_ck_w2", "nsa_cv_w2"):
        sh[n] = f(inp[n])
    sh["fox_b_f"] = f(np.asarray(inp["fox_b_f"]).reshape(DEPTH, 6, 1))
    sh.update(rwkv_host(inp))
    return sh


def prep_core(inp, b):
    d = {}
    d["x"] = np.ascontiguousarray(np.asarray(inp["x"][b], dtype=np.float32))
    d["cT"] = np.ascontiguousarray(np.asarray(inp["c"][b], dtype=np.float32).reshape(8, 128).T)
    return d


def setup_fox(k):
    k.fox_bf = k.inp("fox_b_f", [DEPTH, 6, 1])
    k.CUMA = k.scratch("CUMA", [6, 3, S_LEN], BF16)


def stage_fox(k, l):
    nc = k.nc
    with Scope(k) as sc:
        nb = sc.sb("nb", [128, 32, 6], F32)
        with Scope(k) as s2:
            fl = s2.sb("fl", [6, S_LEN], F32)
            t1 = s2.sb("t1", [6, S_LEN], F32)
            ones = s2.sb("ones", [6, S_LEN], F32)
            cum = s2.sb("cum", [6, S_LEN], F32)
            parts = s2.sb("parts", [6, 3, S_LEN], BF16)
            bfv = s2.sb("bfv", [6, 2], F32)
            psn = s2.ps("psn", [128, 512])
            k.dma('sp', fl[:], k.FL, w=[fl])
            k.dma('sp', bfv[:, 0:1], k.fox_bf[l], w=[bfv])
            k.ts('dve', bfv[:, 1:2], bfv[:, 0:1], -1.0, None, ALU.mult, None, r=[bfv], w=[bfv])
            k.S.op('pool', lambda: nc.gpsimd.memset(ones[:], 1.0), [], [ones])
            k.act(t1[:], fl[:], AF.Exp, r=[fl, bfv], w=[t1], bias=bfv[:, 1:2], scale=-1.0)
            k.act(t1[:], t1[:], AF.Ln, r=[t1], w=[t1], bias=1.0, scale=1.0)
            k.ts('dve', t1[:], t1[:], -1.0, None, ALU.mult, None, r=[t1], w=[t1])
            k.S.op('dve', lambda: nc.vector.tensor_tensor_scan(out=cum[:], data0=ones[:], data1=t1[:], initial=0.0,
                                                               op0=ALU.mult, op1=ALU.add), [ones, t1], [cum])
            for t in range(32):
                k.transpose(psn[:, t * 6:(t + 1) * 6], cum[:, t * 128:(t + 1) * 128], k.ident_f[0:6, 0:6],
                            r=[cum, k.ident_f], w=[psn])
            k.ts('dve', nb[:].rearrange("p t h -> p (t h)"), psn[:, 0:192], -1.0, None, ALU.mult, None, r=[psn], w=[nb])
            k.ts('dve', t1[:], cum[:], 8.0, None, ALU.mult, None, r=[cum], w=[t1])
            k.copy('dve', parts[:, 0, :], t1[:], r=[t1], w=[parts])
            k.tt('dve', t1[:], t1[:], parts[:, 0, :], ALU.subtract, r=[t1, parts], w=[t1])
            k.copy('dve', parts[:, 1, :], t1[:], r=[t1], w=[parts])
            k.tt('dve', t1[:], t1[:], parts[:, 1, :], ALU.subtract, r=[t1, parts], w=[t1])
            k.copy('dve', parts[:, 2, :], t1[:], r=[t1], w=[parts])
            k.dma('sp', k.CUMA, parts[:], r=[parts], w=["CUMA"])
        QA = [sc.sb("QA%d" % i, [128, S_LEN], BF16) for i in range(2)]
        KA = [sc.sb("KA%d" % i, [128, S_LEN], BF16) for i in range(2)]
        VA = sc.sb("VA", [128, 32, 6, 65], BF16)
        yb = sc.sb("yb", [128, 32, 384], BF16)
        PTl = [sc.sb("PTl%d" % i, [128, 512], BF16) for i in range(6)]
        rc = [sc.sb("rc%d" % i, [128, 4], F32) for i in range(2)]
        psS = [sc.ps("psS%d" % i, [128, 512]) for i in range(4)]
        psO = [sc.ps("psO%d" % i, [128, 512]) for i in range(2)]
        k.dma('sp', yb[:], k.VT[:, 0:384].rearrange("(t p) c -> p t c", p=128), w=[yb])
        k.S.op('pool', lambda: nc.gpsimd.memset(VA[:, :, :, 64:65], 1.0), [], [VA])
        k.copy('pool', VA[:, :, :, 0:64], yb[:].rearrange("p t (h d) -> p t h d", h=6), r=[yb], w=[VA])
        for i in range(2):
            k.S.op('dve', lambda i=i: nc.vector.memset(KA[i][64:67, :], 1.0), [], [KA[i]])
        nS = 0
        nO = 0
        nP = 0
        pipe = Pipe(2)
        for h in range(6):
            qa = QA[h % 2]; ka = KA[h % 2]
            k.dma('sp', qa[0:64, :], k.QKT[h * 64:(h + 1) * 64, :], w=[qa])
            k.dma('sp', qa[64:67, :], k.CUMA[h], w=[qa])
            k.dma('sp', ka[0:64, :], k.QKT[384 + h * 64:384 + (h + 1) * 64, :], w=[ka])
            for qb in range(NTB):
                po = psO[nO % 2]; nO += 1
                nkt = 4 * qb + 4
                for kt in range(nkt):
                    j = kt - 4 * qb
                    c0 = max(j, 0) * 128
                    ps = psS[nS % len(psS)]; nS += 1
                    pt = PTl[nP % len(PTl)]; nP += 1

                    def first(ps=ps, pt=pt, kt=kt, c0=c0, j=j, qa=qa, ka=ka, qb=qb, h=h):
                        k.mm(ps[:, c0:512], [(ka[0:67, kt * 128:(kt + 1) * 128], qa[0:67, qb * 512 + c0:(qb + 1) * 512])],
                             r=[ka, qa], w=[ps])
                        k.act(pt[:, c0:512], ps[:, c0:512], AF.Exp, r=[ps, nb], w=[pt], bias=nb[:, kt, h:h + 1], scale=0.125)
                        if j >= 0:
                            k.S.op('pool', lambda: nc.gpsimd.affine_select(
                                out=pt[:, c0:c0 + 128], in_=pt[:, c0:c0 + 128], pattern=[[1, 128]], compare_op=ALU.is_ge,
                                fill=0.0, base=0, channel_multiplier=-1), [pt], [pt])

                    def second(pt=pt, kt=kt, j=j, po=po, qb=qb, h=h, last=(kt == nkt - 1)):
                        fns = []
                        for qs in range(max(j, 0), 4):
                            fns.append(lambda qs=qs: nc.tensor.matmul(
                                po[:, qs * 65:(qs + 1) * 65], lhsT=pt[:, qs * 128:(qs + 1) * 128], rhs=VA[:, kt, h, :],
                                start=(kt == 0 and qs == 0), stop=(kt == 4 * qb + qs), skip_group_check=True))
                        k.S.pe_group(fns, [pt, VA], [po])
                        if last:
                            r_ = rc[qb % 2]
                            pov = po[:, 0:260].rearrange("p (q c) -> p q c", c=65)
                            k.S.op('dve', lambda: nc.vector.reciprocal(out=r_[:], in_=pov[:, :, 64]), [po], [r_])
                            for qs in range(4):
                                k.ts('dve', yb[:, qb * 4 + qs, h * 64:(h + 1) * 64], po[:, qs * 65:qs * 65 + 64], r_[:, qs:qs + 1], None,
                                     ALU.mult, None, r=[po, r_], w=[yb])
                    pipe.push(first, second)
        pipe.flush()
        k.dma('sp', k.Y[:, 256:640].rearrange("(t p) c -> p t c", p=128), yb[:], r=[yb], w=["Y"])


LW = 1536
LC = 4608
NEG8 = -240000.0


def t5_bucket_np(n):
    n = np.maximum(n, 0)
    nf = np.maximum(n, 1).astype(np.float32)
    large = 16 + (np.log(nf / np.float32(16)) / np.float32(np.log(128 / 16)) * np.float32(16)).astype(np.int32)
    large = np.minimum(large, 31)
    return np.where(n < 16, n, large)


def nsa_host_consts():
    c = {}
    i = np.arange(LW); n = i - 511
    oh = np.zeros((33, LW), np.float32)
    ok = (n >= 0) & (n < 512)
    oh[t5_bucket_np(n)[ok], i[ok]] = 1.0
    oh[32, ~ok] = NEG8
    c["oh_w"] = oh
    i = np.arange(LC); n = i - 2063
    oh = np.zeros((33, LC), np.float32)
    ok = n >= 0
    oh[t5_bucket_np(n)[ok], i[ok]] = 1.0
    oh[32, ~ok] = NEG8
    c["oh_c"] = oh
    E = np.zeros((128, 32, 128), np.float32)
    for kt in range(32):
        E[2 * kt, kt, 0:64] = 1.0
        E[2 * kt + 1, kt, 64:128] = 1.0
    c["E_blk"] = E.astype(NPBF)
    cs = np.arange(256) * 16
    ce = cs + 31
    ss = np.arange(64) * 64
    ov = ((cs[:, None] <= ss[None, :] + 63) & (ce[:, None] >= ss[None, :])).astype(np.float32)
    ov[255] = 0.0
    c["ovl"] = np.ascontiguousarray(ov.reshape(2, 128, 64).transpose(1, 0, 2)).astype(NPBF)
    t = np.arange(S_LEN)
    cur = t // 64
    jb = np.arange(64)
    back = cur[:, None] - jb[None, :]
    valid = back >= 0
    forced = (jb[None, :] == 0) | (valid & (back < 2))
    tkm = (valid & ~forced).astype(np.float32)
    tka = np.where(valid, np.where(forced, 1e4, 0.0), -1.0).astype(np.float32)
    c["tkm"] = np.ascontiguousarray(tkm.reshape(32, 128, 64).transpose(1, 0, 2)).astype(NPBF)
    c["tka"] = np.ascontiguousarray(tka.reshape(32, 128, 64).transpose(1, 0, 2)).astype(NPBF)
    return c


def setup_nsa(k):
    nc = k.nc
    k.rel_bias = k.inp("rel_bias", [32, 6])
    k.oh_w = k.inp("oh_w", [33, LW])
    k.oh_c = k.inp("oh_c", [33, LC])
    k.E_d = k.inp("E_blk", [128, 32, 128], BF16)
    k.ovl_d = k.inp("ovl", [128, 2, 64], BF16)
    k.tkm_d = k.inp("tkm", [128, 32, 64], BF16)
    k.tka_d = k.inp("tka", [128, 32, 64], BF16)
    k.pe_kT = k.inp("nsa_pe_kT", [DEPTH, 64, 32])
    k.pe_vT = k.inp("nsa_pe_vT", [DEPTH, 64, 32])
    k.ck_w1 = k.inp("nsa_ck_w1", [DEPTH, 2048, 128])
    k.cv_w1 = k.inp("nsa_cv_w1", [DEPTH, 2048, 128])
    k.ck_w2 = k.inp("nsa_ck_w2", [DEPTH, 128, 64])
    k.cv_w2 = k.inp("nsa_cv_w2", [DEPTH, 128, 64])
    k.WVW = k.scratch("WVW", [6, 128, LW], BF16)
    k.WVC = k.scratch("WVC", [6, 128, LC], BF16)
    with Scope(k) as sc:
        rb = sc.sb("rb", [33, 6], F32)
        rb31 = sc.sb("rb31", [32, 6], F32)
        rrep = sc.sb("rrep", [33, 6, 128], F32)
        ohw = sc.sb("ohw", [33, LW], F32)
        ohc = sc.sb("ohc", [33, LC], F32)
        ps = [sc.ps("psb%d" % i, [128, 512]) for i in range(2)]
        ev = [sc.sb("evb%d" % i, [128, 512], BF16) for i in range(2)]
        k.dma('sp', rb[0:32, :], k.rel_bias, w=[rb])
        k.dma('sp', rb31[:], k.rel_bias[31:32, :].broadcast_to([32, 6]), w=[rb31])
        k.dma('sp', ohw[:], k.oh_w, w=[ohw])
        k.dma('sp', ohc[:], k.oh_c, w=[ohc])
        k.S.op('dve', lambda: nc.vector.memset(rb[32:33, :], 1.0), [], [rb])
        k.tt('dve', rb[0:32, :], rb[0:32, :], rb31[:], ALU.subtract, r=[rb, rb31], w=[rb])
        k.ts('dve', rb[0:32, :], rb[0:32, :], 8.0, None, ALU.mult, None, r=[rb], w=[rb])
        for h in range(6):
            k.copy('dve', rrep[:, h, :], rb[:, h:h + 1].to_broadcast([33, 128]), r=[rb], w=[rrep])
        n = 0
        for h in range(6):
            for (oh, L, dst) in ((ohw, LW, k.WVW), (ohc, LC, k.WVC)):
                for c0 in range(0, L, 512):
                    p_ = ps[n % 2]; e_ = ev[n % 2]; n += 1
                    k.mm(p_[:], [(rrep[:, h, :], oh[:, c0:c0 + 512])], r=[rrep, oh], w=[p_])
                    k.copy('act' if n % 2 else 'dve', e_[:], p_[:], r=[p_], w=[e_])
                    k.dma('sp', dst[h, :, c0:c0 + 512], e_[:], r=[e_])


class DbgStop(Exception):
    pass


def dbg(k, lvl):
    if getattr(k, 'dbg_stop', None) == lvl:
        raise DbgStop()


def stage_nsa(k, l):
    nc = k.nc
    with Scope(k) as sc:
        Gw = sc.sb("Gw", [128, 6, 1408], BF16)
        Gc = sc.sb("Gc", [128, 6, 2560], BF16)
        E = sc.sb("E", [128, 32, 128], BF16)
        tkm = sc.sb("tkm", [128, 32, 64], BF16)
        tka = sc.sb("tka", [128, 32, 64], BF16)
        QC = [sc.sb("QC%d" % c, [128, S_LEN], BF16) for c in range(3)]
        KS = sc.sb("KS", [128, S_LEN], BF16)
        KW = sc.sb("KW", [128, S_LEN], BF16)
        VS = sc.sb("VS", [128, 32, 2, 65], BF16)
        VW = sc.sb("VW", [128, 32, 2, 65], BF16)
        KCMP = sc.sb("KCMP", [128, 256], BF16)
        VE = sc.sb("VE", [128, 2, 2, 129], BF16)
        sg = sc.sb("sg", [128, 32, 18], F32)
        for h in range(6):
            k.dma('sp', Gw[:, h, :], bass.AP(k.WVW.tensor, h * 128 * LW + 127, [[LW - 1, 128], [1, 1408]]), w=[Gw])
            k.dma('sp', Gc[:, h, :], bass.AP(k.WVC.tensor, h * 128 * LC + 2032, [[LC - 16, 128], [1, 2560]]), w=[Gc])
        k.dma('sp', E[:], k.E_d, w=[E])
        k.dma('sp', tkm[:], k.tkm_d, w=[tkm])
        k.dma('sp', tka[:], k.tka_d, w=[tka])
        for c in range(3):
            k.dma('sp', QC[c][:], k.QKT[768 + c * 128:768 + (c + 1) * 128, :], w=[QC[c]])
        k.dma('sp', KS[:], k.QKT[1408:1536, :], w=[KS])
        k.dma('sp', KW[:], k.QKT[1536:1664, :], w=[KW])
        k.dma('sp', sg[:], k.GT.rearrange("(t p) c -> p t c", p=128), w=[sg])
        k.act(sg[:], sg[:], AF.Exp, r=[sg], w=[sg], scale=-1.0)
        k.ts('dve', sg[:], sg[:], 1.0, None, ALU.add, None, r=[sg], w=[sg])
        k.S.op('dve', lambda: nc.vector.reciprocal(out=sg[:], in_=sg[:]), [sg], [sg])
        k.dma('sp', VE[:, 0, :, 65:129], k.ovl_d, w=[VE])
        k.dma('sp', VE[:, 1, :, 65:129], k.ovl_d, w=[VE])
        k.S.op('pool', lambda: nc.gpsimd.memset(VE[:, :, :, 64:65], 1.0), [], [VE])
        k.S.op('pool', lambda: nc.gpsimd.memset(VE[:, :, :, 0:64], 0.0), [], [VE])
        k.S.op('pool', lambda: nc.gpsimd.memset(KCMP[:], 0.0), [], [KCMP])
        dbg(k, 1)
        with Scope(k) as s2:
            vst = s2.sb("vst", [128, 32, 256], BF16)
            k.dma('sp', vst[:], k.VT[:, 384:640].rearrange("(t p) c -> p t c", p=128), w=[vst])
            k.S.op('pool', lambda: nc.gpsimd.memset(VS[:, :, :, 64:65], 1.0), [], [VS])
            k.S.op('pool', lambda: nc.gpsimd.memset(VW[:, :, :, 64:65], 1.0), [], [VW])
            k.copy('pool', VS[:, :, :, 0:64], vst[:, :, 0:128].rearrange("p t (g d) -> p t g d", g=2), r=[vst], w=[VS])
            k.copy('pool', VW[:, :, :, 0:64], vst[:, :, 128:256].rearrange("p t (g d) -> p t g d", g=2), r=[vst], w=[VW])
        dbg(k, 2)
        with Scope(k) as s2:
            KC = s2.sb("KC", [128, S_LEN], BF16)
            VC = s2.sb("VC", [128, S_LEN], BF16)
            k.dma('sp', KC[:], k.QKT[1152:1280, :], w=[KC])
            k.dma('sp', VC[:], k.QKT[1280:1408, :], w=[VC])
            w1s = s2.sb("w1s", [128, 16, 128], F32)
            w1b = [s2.sb("w1b%d" % i, [128, 32, 128], BF16) for i in range(2)]
            w2s = s2.sb("w2s", [128, 2, 64], F32)
            w2b = s2.sb("w2b", [128, 2, 64], BF16)
            pes = s2.sb("pes", [128, 2, 32], F32)
            peb = s2.sb("peb", [128, 2, 32], BF16)
            hb = s2.sb("hb", [128, 2], F32)
            gx = s2.sb("gx", [128, 256], F32)
            gu = s2.sb("gu", [128, 256], F32)
            gg = s2.sb("gg", [128, 256], BF16)
            psh = s2.ps("psh", [128, 512])
            psb_ = s2.ps("pshb", [128, 512])
            pso = s2.ps("pso", [128, 512])
            for kv, (w1d, w2d, ped) in enumerate(((k.ck_w1, k.ck_w2, k.pe_kT), (k.cv_w1, k.cv_w2, k.pe_vT))):
                for lh in range(2):
                    for half in range(2):
                        k.dma('sp', w1s[half * 64:(half + 1) * 64, :, :],
                              w1d[l, lh * 1024:(lh + 1) * 1024, :].rearrange("(l d) h -> d l h", d=64), w=[w1s])
                    k.copy('pool', w1b[kv][:, lh * 16:(lh + 1) * 16, :], w1s[:], r=[w1s], w=[w1b[kv]])
                for half in range(2):
                    k.dma('sp', pes[half * 64:(half + 1) * 64, kv, :], ped[l], w=[pes])
                k.dma('sp', w2s[:, kv, :], w2d[l], w=[w2s])
            k.copy('dve', w2b[:], w2s[:], r=[w2s], w=[w2b])
            w2kd = s2.sb("w2kd", [128, 2, 64], BF16)
            for a_ in range(2):
                k.copy('dve', w2kd[:, a_, :], w2s[:, 0, :], r=[w2s], w=[w2kd])
            k.copy('dve', peb[:], pes[:], r=[pes], w=[peb])
            for kv in range(2):
                src = KC if kv == 0 else VC
                k.mm(psb_[:, kv:kv + 1], [(w1b[kv][0:64, li, :], peb[0:64, kv, li:li + 1]) for li in range(32)],
                     r=[w1b[kv], peb], w=[psb_], start=True)
                k.copy('dve', hb[:, kv:kv + 1], psb_[:, kv:kv + 1], r=[psb_], w=[hb])
                for g in range(2):
                    pr = slice(g * 64, (g + 1) * 64)
                    k.mm(psh[:, 0:255], [(w1b[kv][pr, li, :], src[pr, li:li + 16 * 254 + 1:16]) for li in range(32)],
                         r=[w1b[kv], src], w=[psh])
                    k.ts('dve', gx[:, 0:255], psh[:, 0:255], hb[:, kv:kv + 1], None, ALU.add, None, r=[psh, hb], w=[gx])
                    k.tt('dve', gu[:, 0:255], gx[:, 0:255], gx[:, 0:255], ALU.mult, r=[gx], w=[gu])
                    k.ts('dve', gu[:, 0:255], gu[:, 0:255], 0.044715, 1.0, ALU.mult, ALU.add, r=[gu], w=[gu])
                    k.tt('dve', gu[:, 0:255], gu[:, 0:255], gx[:, 0:255], ALU.mult, r=[gu, gx], w=[gu])
                    k.act(gu[:, 0:255], gu[:, 0:255], AF.Exp, r=[gu], w=[gu], scale=-2.0 * 0.7978845608028654)
                    k.ts('dve', gu[:, 0:255], gu[:, 0:255], 1.0, None, ALU.add, None, r=[gu], w=[gu])
                    k.S.op('dve', lambda: nc.vector.reciprocal(out=gu[:, 0:255], in_=gu[:, 0:255]), [gu], [gu])
                    k.S.op('dve', lambda: nc.vector.memset(gg[:, 255:256], 0.0), [], [gg])
                    k.tt('dve', gg[:, 0:255], gu[:, 0:255], gx[:, 0:255], ALU.mult, r=[gu, gx], w=[gg])
                    if kv == 0:
                        k.mm(pso[:, 0:256], [(w2kd[:].rearrange("p a d -> p (a d)"), gg[:, 0:256])], r=[w2kd, gg], w=[pso])
                        k.copy('dve', KCMP[pr, :], pso[pr, 0:256], r=[pso], w=[KCMP])
                    else:
                        for ct in range(2):
                            k.mm(pso[:, ct * 64:(ct + 1) * 64], [(gg[:, ct * 128:(ct + 1) * 128], w2b[:, 1, :])],
                                 r=[w2b, gg], w=[pso], start=(ct == 0))
                        k.copy('dve', VE[:, g, :, 0:64], pso[:, 0:128].rearrange("p (c d) -> p c d", c=2), r=[pso], w=[VE])
        dbg(k, 3)
        NM = [sc.sb("NM%d" % g, [128, 512], BF16) for g in range(2)]
        for g in range(2):
            k.S.op('pool', lambda g=g: nc.gpsimd.memset(NM[g][:], 0.0), [], [NM[g]])
        PTl = [sc.sb("PTn%d" % i, [128, 512], BF16) for i in range(6)]
        yacc = [sc.sb("yacc%d" % i, [128, 4, 384], F32) for i in range(2)]
        ybf = [sc.sb("ybf%d" % i, [128, 4, 384], BF16) for i in range(2)]
        impt = [sc.sb("impt%d" % g, [128, 4, 64], F32) for g in range(2)]
        scr = sc.sb("scr", [128, 4, 64], F32)
        wk = sc.sb("wk", [128, 4, 64], F32)
        m8 = sc.sb("m8", [128, 4, 16], F32)
        nmq = sc.sb("nmq", [128, 4, 64], BF16)
        rcs = [sc.sb("rcs%d" % i, [128, 8], F32) for i in range(3)]
        psS = [sc.ps("psS%d" % i, [128, 512]) for i in range(4)]
        psO = [sc.ps("psO%d" % i, [128, 512]) for i in range(3)]
        psT = sc.ps("psTn", [128, 1024], BF16)
        st = {"S": 0, "O": 0, "P": 0, "R": 0}

        def q_ap(h, c0, c1):
            g, hp = h // 3, h % 3
            return QC[hp][g * 64:(g + 1) * 64, c0:c1]

        def evac(views, h, branch, qb, ya, first):
            r_ = rcs[st["R"] % 3]; st["R"] += 1
            for qs, (po, cb) in enumerate(views):
                if branch == 0:
                    k.ts('dve', r_[:, qs:qs + 1], po[:, cb + 64:cb + 65], 1e-30, None, ALU.max, None, r=[po], w=[r_])
                    k.S.op('dve', lambda r_=r_, qs=qs: nc.vector.reciprocal(out=r_[:, qs:qs + 1], in_=r_[:, qs:qs + 1]), [r_], [r_])
                else:
                    k.S.op('dve', lambda r_=r_, po=po, cb=cb, qs=qs: nc.vector.reciprocal(out=r_[:, qs:qs + 1], in_=po[:, cb + 64:cb + 65]), [po], [r_])
            k.tt('dve', r_[:, 4:8], r_[:, 0:4], sg[:, qb * 4:(qb + 1) * 4, h * 3 + branch], ALU.mult, r=[r_, sg], w=[r_])
            for qs, (po, cb) in enumerate(views):
                o = ya[:, qs, h * 64:(h + 1) * 64]
                if first:
                    k.ts('dve', o, po[:, cb:cb + 64], r_[:, 4 + qs:5 + qs], None, ALU.mult, None, r=[po, r_], w=[ya])
                else:
                    k.stt('dve', o, po[:, cb:cb + 64], r_[:, 4 + qs:5 + qs], o, ALU.mult, ALU.add, r=[po, r_, ya], w=[ya])
            return r_

        pipe = Pipe(2)

        def attend(h, qb, tiles, kmat, vmat, po, g, branch, ya):
            hp = h % 3
            nt = len(tiles)
            state = {"first": True}
            for idx, (kt, c0, c1, extra) in enumerate(tiles):
                ps = psS[st["S"] % len(psS)]; st["S"] += 1
                pt = PTl[st["P"] % len(PTl)]; st["P"] += 1

                def first(ps=ps, pt=pt, kt=kt, c0=c0, c1=c1, extra=extra):
                    fns = [lambda: nc.tensor.matmul(ps[:, c0:c1], lhsT=kmat[g * 64:(g + 1) * 64, kt * 128:(kt + 1) * 128],
                                                    rhs=QC[hp][g * 64:(g + 1) * 64, qb * 512 + c0:qb * 512 + c1],
                                                    start=True, stop=(len(extra) == 0), skip_group_check=True)]
                    rd = [kmat, QC[hp]]
                    for ei, (lt, rt, lap, rap) in enumerate(extra):
                        w_ = rap.shape[-1]
                        fns.append(lambda lap=lap, rap=rap, w_=w_, ei=ei: nc.tensor.matmul(
                            ps[:, c0:c0 + w_], lhsT=lap, rhs=rap, start=False, stop=(ei == len(extra) - 1), skip_group_check=True))
                        rd += [lt, rt]
                    k.S.pe_group(fns, rd, [ps])
                    k.act(pt[:, c0:c1], ps[:, c0:c1], AF.Exp, r=[ps], w=[pt], scale=0.125)

                def second(pt=pt, kt=kt, c0=c0, c1=c1, idx=idx):
                    fns = []
                    for qs in range(c0 // 128, (c1 + 127) // 128):
                        last = all(not (t2[1] <= qs * 128 < t2[2]) for t2 in tiles[idx + 1:])
                        fo = state["first"]
                        state["first"] = False
                        fns.append(lambda qs=qs, fo=fo, last=last: nc.tensor.matmul(
                            po[:, qs * 65:(qs + 1) * 65], lhsT=pt[:, qs * 128:(qs + 1) * 128], rhs=vmat[:, kt, g, :],
                            start=fo, stop=last, skip_group_check=True))
                    k.S.pe_group(fns, [pt, vmat], [po])
                    if idx == nt - 1:
                        evac([(po, qs * 65) for qs in range(4)], h, branch, qb, ya, False)
                pipe.push(first, second)

        for qb in getattr(k, 'dbg_qbs', range(NTB)):
            ya = yacc[qb % 2]
            for h in range(6):
                g = h // 3
                poA = psO[st["O"] % 3]; st["O"] += 1
                poB = psO[st["O"] % 3]; st["O"] += 1
                cts = [0] + ([1] if qb >= 4 else [])
                state = {"A": True, "B": True}
                for ct in cts:
                    delta = 512 * qb - 2048 * ct
                    ps = psS[st["S"] % len(psS)]; st["S"] += 1
                    pt = PTl[st["P"] % len(PTl)]; st["P"] += 1

                    def first(ps=ps, pt=pt, ct=ct, delta=delta, g=g, h=h):
                        pairs = [(KCMP[g * 64:(g + 1) * 64, ct * 128:(ct + 1) * 128], q_ap(h, qb * 512, (qb + 1) * 512))]
                        rd = [KCMP, QC[h % 3]]
                        if delta < 2560:
                            pairs.append((k.ident_bf[:], Gc[:, h, delta:delta + 512])); rd += [k.ident_bf, Gc]
                        k.mm(ps[:], pairs, r=rd, w=[ps])
                        k.act(pt[:], ps[:], AF.Exp, r=[ps], w=[pt], scale=0.125)

                    def second(pt=pt, ct=ct, g=g, h=h, poA=poA, poB=poB, state=state, lastct=(ct == cts[-1])):
                        fns = []
                        for qs in range(4):
                            po, cb = (poA, qs * 129) if qs < 3 else (poB, 0)
                            key = "A" if qs < 3 else "B"
                            stt_ = state[key]
                            state[key] = False
                            fns.append(lambda qs=qs, po=po, cb=cb, stt_=stt_: nc.tensor.matmul(
                                po[:, cb:cb + 129], lhsT=pt[:, qs * 128:(qs + 1) * 128], rhs=VE[:, g, ct, :],
                                start=stt_, stop=lastct, skip_group_check=True))
                        k.S.pe_group(fns, [pt, VE], [poA, poB])
                        if lastct:
                            views = [(poA, 0), (poA, 129), (poA, 258), (poB, 0)]
                            r_ = evac(views, h, 0, qb, ya, True)
                            for qs, (po, cb) in enumerate(views):
                                o = impt[g][:, qs, :]
                                if h % 3 == 0:
                                    k.ts('dve', o, po[:, cb + 65:cb + 129], r_[:, qs:qs + 1], None, ALU.mult, None, r=[po, r_], w=[impt[g]])
                                else:
                                    k.stt('dve', o, po[:, cb + 65:cb + 129], r_[:, qs:qs + 1], o, ALU.mult, ALU.add, r=[po, r_, impt[g]], w=[impt[g]])
                    pipe.push(first, second)
            pipe.flush()
            dbg(k, 4)
            for g in range(2):
                k.tt('dve', scr[:], impt[g][:], tkm[:, qb * 4:(qb + 1) * 4, :], ALU.mult, r=[impt[g], tkm], w=[scr])
                k.tt('dve', scr[:], scr[:], tka[:, qb * 4:(qb + 1) * 4, :], ALU.add, r=[scr, tka], w=[scr])
                for qs in range(4):
                    k.S.op('dve', lambda qs=qs: nc.vector.max(out=m8[:, qs, 0:8], in_=scr[:, qs, :]), [scr], [m8])
                    k.S.op('dve', lambda qs=qs: nc.vector.match_replace(out=wk[:, qs, :], in_to_replace=m8[:, qs, 0:8],
                                                                        in_values=scr[:, qs, :], imm_value=-1e9), [scr, m8], [wk])
                    k.S.op('dve', lambda qs=qs: nc.vector.max(out=m8[:, qs, 8:16], in_=wk[:, qs, :]), [wk], [m8])
                    k.ts('dve', wk[:, qs, :], scr[:, qs, :], m8[:, qs, 15:16], 1.0, ALU.is_ge, ALU.subtract, r=[scr, m8, wk], w=[wk])
                k.ts('dve', nmq[:], wk[:], -NEG8, None, ALU.mult, None, r=[wk], w=[nmq])
                for qs in range(4):
                    k.transpose(psT[0:64, qs * 128:(qs + 1) * 128], nmq[:, qs, :], k.ident_bf[:], r=[nmq, k.ident_bf], w=[psT])
                k.copy('dve', NM[g][0:64, :], psT[0:64, 0:512], r=[psT], w=[NM[g]])
            for h in range(6):
                g = h // 3
                po = psO[st["O"] % 3]; st["O"] += 1
                tiles = []
                for kt in range(max(0, 4 * qb - 4), 4 * qb + 4):
                    delta = 512 * qb - 128 * kt
                    c0 = max(-delta, 0)
                    c1 = min(512, 640 - delta) if delta > 0 else 512
                    tiles.append((kt, c0, c1, [(k.ident_bf, Gw, k.ident_bf[:], Gw[:, h, delta + 384 + c0:delta + 384 + c1])]))
                attend(h, qb, tiles, KW, VW, po, g, 2, ya)
            for h in range(6):
                g = h // 3
                po = psO[st["O"] % 3]; st["O"] += 1
                tiles = []
                for kt in range(0, 4 * qb + 4):
                    delta = 512 * qb - 128 * kt
                    c0 = max(-delta, 0)
                    ex = [(E, NM[g], E[:, kt, :], NM[g][:, c0:512])]
                    if delta <= 128:
                        c1b = 256 if delta == 128 else 512
                        ex.append((k.ident_bf, Gw, k.ident_bf[:], Gw[:, h, delta + 384 + c0:delta + 384 + c1b]))
                    tiles.append((kt, c0, 512, ex))
                attend(h, qb, tiles, KS, VS, po, g, 1, ya)
            pipe.flush()
            yb_ = ybf[qb % 2]
            k.copy('pool', yb_[:], ya[:], r=[ya], w=[yb_])
            k.dma('sp', k.Y[qb * 512:(qb + 1) * 512, 640:1024].rearrange("(q p) c -> p q c", p=128), yb_[:], r=[yb_])
            dbg(k, 100 + qb)


def setup_ffn(k):
    k.w_out = k.inp("w_out", [DEPTH, D, D])
    k.ffn_up = k.inp("ffn_up", [DEPTH, D, 2 * D_FF])
    k.ffn_down = k.inp("ffn_down", [DEPTH, D_FF, D])
    k.conv_w = k.inp("conv_w_fm", [DEPTH, 128, 3, 44])
    k.conv_b = k.inp("conv_b_fm", [DEPTH, 128, 44])


def load_cast(k, sc, dst, src_rows, ncols, nchunks, name, col_split=1):
    w = ncols // col_split
    stg = [sc.sb("%s_stg%d" % (name, i), [128, w], F32) for i in range(2)]
    n = 0
    for c in range(nchunks):
        for cs in range(col_split):
            s = stg[n % 2]
            k.dma('sp', s[:], src_rows(c)[:, cs * w:(cs + 1) * w], w=[s])
            k.copy('pool' if n % 2 == 0 else 'dve', dst[:, c, cs * w:(cs + 1) * w], s[:], r=[s], w=[dst])
            n += 1


def rms_scale(k, ss, st):
    k.act(st[:, 0:1], ss, AF.Ln, r=[st], w=[st], bias=RMS_EPS)
    k.act(st[:, 1:2], st[:, 0:1], AF.Exp, r=[st], w=[st], scale=-0.5)


def stage_out(k, l, xsrc, xdst):
    nc = k.nc
    with Scope(k) as sc:
        wo = sc.sb("wo", [128, 8, D], BF16)
        with Scope(k) as s2:
            load_cast(k, s2, wo, lambda c: k.w_out[l, c * 128:(c + 1) * 128, :], D, 8, "wo")
        yt = [sc.sb("yt%d" % i, [128, D], BF16) for i in range(2)]
        yT = [sc.sb("yT%d" % i, [128, 8, 128], BF16) for i in range(2)]
        xt = [sc.sb("xo%d" % i, [128, D], F32) for i in range(2)]
        tt_ = [sc.sb("to%d" % i, [128, D], F32) for i in range(2)]
        junk = sc.sb("junko", [128, 512], BF16)
        st = [sc.sb("sto%d" % i, [128, 4], F32) for i in range(2)]
        psT = [sc.ps("psTo%d" % i, [128, D], BF16) for i in range(2)]
        psY = [sc.ps("psYo%d" % i, [128, 512]) for i in range(4)]
        for ti in range(32):
            tok = slice(ti * 128, (ti + 1) * 128)
            y_ = yt[ti % 2]; yT_ = yT[ti % 2]; x_ = xt[ti % 2]; t_ = tt_[ti % 2]; st_ = st[ti % 2]; pT = psT[ti % 2]
            p0 = psY[(ti % 2) * 2]; p1 = psY[(ti % 2) * 2 + 1]
            k.dma('act', y_[:], k.Y[tok, :], w=[y_])
            k.dma('act', x_[:], xsrc[tok, :], w=[x_])
            for kc in range(8):
                k.transpose(pT[:, kc * 128:(kc + 1) * 128], y_[:, kc * 128:(kc + 1) * 128], k.ident_bf[:], r=[y_, k.ident_bf], w=[pT])
            k.copy('act' if ti % 2 else 'dve', yT_[:].rearrange("p a b -> p (a b)"), pT[:], r=[pT], w=[yT_])
            for half, ps in enumerate((p0, p1)):
                k.mm(ps[:], [(yT_[:, kc, :], wo[:, kc, half * 512:(half + 1) * 512]) for kc in range(8)], r=[yT_, wo], w=[ps])
                k.act(junk[:], ps[:], AF.Square, r=[ps], w=[junk, st_], scale=1.0 / 32.0, accum=st_[:, 2 + half:3 + half])
            k.tt('dve', st_[:, 2:3], st_[:, 2:3], st_[:, 3:4], ALU.add, r=[st_], w=[st_])
            rms_scale(k, st_[:, 2:3], st_)
            for half, ps in enumerate((p0, p1)):
                cs = slice(half * 512, (half + 1) * 512)
                k.stt('dve', t_[:, cs], ps[:], st_[:, 1:2], k.gm_row[:, cs], ALU.mult, ALU.mult, r=[ps, st_, k.gm_row], w=[t_])
            k.tt('pool', t_[:], t_[:], x_[:], ALU.add, r=[t_, x_], w=[t_])
            k.dma('sp', xdst[tok, :], t_[:], r=[t_])


def stage_ffn(k, l, xsrc, xdst):
    nc = k.nc
    NCH = 22
    with Scope(k) as sc:
        wu = sc.sb("wu", [128, 8, 2 * D_FF], BF16)
        wd = sc.sb("wd", [128, NCH, D], BF16)
        with Scope(k) as s2:
            load_cast(k, s2, wu, lambda c: k.ffn_up[l, c * 128:(c + 1) * 128, :], 2 * D_FF, 8, "wu", col_split=2)
            load_cast(k, s2, wd, lambda c: k.ffn_down[l, c * 128:(c + 1) * 128, :], D, NCH, "wd")
        cw = sc.sb("cw", [128, 3, 44], F32)
        cb = sc.sb("cb", [128, 44], F32)
        hal = [sc.sb("hal%d" % i, [128, 44, 2], F32) for i in range(2)]
        k.dma('sp', cw[:], k.conv_w[l], w=[cw])
        k.dma('sp', cb[:], k.conv_b[l], w=[cb])
        k.S.op('pool', lambda: nc.gpsimd.memset(hal[1][:], 0.0), [], [hal[1]])
        actT = sc.sb("actT", [128, NCH, 512], BF16)
        HT = sc.sb("H2T", [128, 8, 512], BF16)
        xt = [sc.sb("xf%d" % i, [128, D], F32) for i in range(2)]
        xn = sc.sb("xnf", [128, D], BF16)
        junk = sc.sb("junkf", [128, D], BF16)
        st = [sc.sb("stf%d" % i, [128, 4], F32) for i in range(2)]
        Tg = [sc.sb("Tg%d" % i, [128, 512], F32) for i in range(2)]
        Tv = [sc.sb("Tv%d" % i, [128, 512], F32) for i in range(2)]
        psT = sc.ps("psTf", [128, D], BF16)
        psU = [sc.ps("psU%d" % i, [128, 512]) for i in range(4)]
        psF = [sc.ps("psF%d" % i, [128, 512]) for i in range(2)]
        nx = 0
        for tb in range(NTB):
            hin = hal[(tb + 1) % 2]; hout = hal[tb % 2]
            for sub in range(4):
                ti = tb * 4 + sub
                x_ = xt[nx % 2]; st_ = st[nx % 2]; nx += 1
                k.dma('act', x_[:], xsrc[ti * 128:(ti + 1) * 128, :], w=[x_])
                k.act(junk[:], x_[:], AF.Square, r=[x_], w=[junk, st_], scale=1.0 / 32.0, accum=st_[:, 2:3])
                rms_scale(k, st_[:, 2:3], st_)
                k.ts('dve', xn[:], x_[:], st_[:, 1:2], None, ALU.mult, None, r=[x_, st_], w=[xn])
                for kc in range(8):
                    k.transpose(psT[:, kc * 128:(kc + 1) * 128], xn[:, kc * 128:(kc + 1) * 128], k.ident_bf[:], r=[xn, k.ident_bf], w=[psT])
                for kc in range(8):
                    o = HT[:, kc, sub * 128:(sub + 1) * 128]
                    i_ = psT[:, kc * 128:(kc + 1) * 128]
                    if kc % 2 == 0:
                        k.ts('dve', o, i_, k.modAB[:, 16 + kc:17 + kc], k.modAB[:, 24 + kc:25 + kc], ALU.mult, ALU.add, r=[psT, k.modAB], w=[HT])
                    else:
                        k.act(o, i_, AF.Identity, r=[psT, k.modAB], w=[HT], scale=k.modAB[:, 16 + kc:17 + kc], bias=k.modAB[:, 24 + kc:25 + kc])
            for cp in range(NCH):
                tg = Tg[cp % 2]; tv = Tv[cp % 2]
                for which, (T_, c_) in enumerate(((tg, cp), (tv, NCH + cp))):
                    ps = psU[(cp * 2 + which) % 4]
                    k.mm(ps[:], [(wu[:, kc, c_ * 128:(c_ + 1) * 128], HT[:, kc, :]) for kc in range(8)], r=[wu, HT], w=[ps])
                    k.act(T_[:], ps[:], AF.Identity, r=[ps, cw, cb], w=[T_], scale=cw[:, 2, c_:c_ + 1], bias=cb[:, c_:c_ + 1])
                    k.stt('dve', T_[:, 1:512], ps[:, 0:511], cw[:, 1, c_:c_ + 1], T_[:, 1:512], ALU.mult, ALU.add, r=[ps, cw, T_], w=[T_])
                    k.stt('dve', T_[:, 2:512], ps[:, 0:510], cw[:, 0, c_:c_ + 1], T_[:, 2:512], ALU.mult, ALU.add, r=[ps, cw, T_], w=[T_])
                    k.copy('act', hout[:, c_, :], ps[:, 510:512], r=[ps], w=[hout])
                    k.stt('dve', T_[:, 0:1], hin[:, c_, 1:2], cw[:, 1, c_:c_ + 1], T_[:, 0:1], ALU.mult, ALU.add, r=[hin, cw, T_], w=[T_])
                    k.stt('dve', T_[:, 0:2], hin[:, c_, 0:2], cw[:, 0, c_:c_ + 1], T_[:, 0:2], ALU.mult, ALU.add, r=[hin, cw, T_], w=[T_])
                k.act(tg[:], tg[:], AF.Silu, r=[tg], w=[tg])
                k.tt('pool', actT[:, cp, :], tg[:], tv[:], ALU.mult, r=[tg, tv], w=[actT])
            for sub in range(4):
                ti = tb * 4 + sub
                tok = slice(ti * 128, (ti + 1) * 128)
                x_ = xt[nx % 2]; st_ = st[nx % 2]; nx += 1
                k.dma('act', x_[:], xsrc[tok, :], w=[x_])
                for half in range(2):
                    ps = psF[half]
                    k.mm(ps[:], [(actT[:, cp, sub * 128:(sub + 1) * 128], wd[:, cp, half * 512:(half + 1) * 512]) for cp in range(NCH)],
                         r=[actT, wd], w=[ps])
                    k.act(junk[:, 0:512], ps[:], AF.Square, r=[ps], w=[junk, st_], scale=1.0 / 32.0, accum=st_[:, 2 + half:3 + half])
                k.tt('dve', st_[:, 2:3], st_[:, 2:3], st_[:, 3:4], ALU.add, r=[st_], w=[st_])
                rms_scale(k, st_[:, 2:3], st_)
                t_ = Tg[sub % 2] if False else None
                for half in range(2):
                    cs = slice(half * 512, (half + 1) * 512)
                    T_ = (Tg if half == 0 else Tv)[sub % 2]
                    k.stt('dve', T_[:], psF[half][:], st_[:, 1:2], k.gf_row[:, cs], ALU.mult, ALU.mult, r=[psF[half], st_, k.gf_row], w=[T_])
                    k.tt('pool', x_[:, cs], x_[:, cs], T_[:], ALU.add, r=[x_, T_], w=[x_])
                k.dma('sp', xdst[tok, :], x_[:], r=[x_])


def rwkv_host(inp):
    f = lambda a: np.ascontiguousarray(np.asarray(a, dtype=np.float32))
    mu = np.asarray(inp["rwkv_mu"])
    hd = lambda v: np.asarray(v).reshape(DEPTH, 4, 64).transpose(0, 2, 1)
    pp = np.stack([hd(mu[:, 0:256]), hd(mu[:, 256:512]), hd(mu[:, 512:768]), hd(inp["rwkv_w0"]), hd(inp["rwkv_a0"]),
                   hd(inp["rwkv_k_k"]), hd(inp["rwkv_k_a"]), hd(np.asarray(inp["rwkv_r_k"]).reshape(DEPTH, 256))], axis=2)
    lr = np.zeros((DEPTH, 64, 3), np.float32)
    lr[:, 0:32, 0] = mu[:, 768:800]; lr[:, 0:32, 1] = mu[:, 800:832]; lr[:, :, 2] = mu[:, 832:896]
    i = np.arange(64)
    mk = np.stack([(i[:, None] < i[None, :]), (i[:, None] > i[None, :]), (i[:, None] <= i[None, :]), np.eye(64, dtype=bool)]).astype(np.float32)
    cm = np.ones((64, 512), np.float32); cm[:, ::64] = 0.0
    return {"rwkv_pp": f(pp), "rwkv_lr": f(lr), "rwkv_w_up": f(inp["rwkv_w_up"]), "rwkv_a_up": f(inp["rwkv_a_up"]),
            "rwkv_g_up": f(inp["rwkv_g_up"]), "rwkv_ln": f(np.stack([np.asarray(inp["rwkv_ln_w"]), np.asarray(inp["rwkv_ln_b"])], axis=1)),
            "rwkv_masks": f(mk.transpose(1, 0, 2)), "rwkv_cmask": cm}


def setup_rwkv(k):
    k.rw_pp = k.inp("rwkv_pp", [DEPTH, 64, 8, 4])
    k.rw_lr = k.inp("rwkv_lr", [DEPTH, 64, 3])
    k.rw_wup = k.inp("rwkv_w_up", [DEPTH, 32, 256])
    k.rw_aup = k.inp("rwkv_a_up", [DEPTH, 32, 256])
    k.rw_gup = k.inp("rwkv_g_up", [DEPTH, 64, 256])
    k.rw_ln = k.inp("rwkv_ln", [DEPTH, 2, 256])
    k.rw_masks = k.inp("rwkv_masks", [64, 4, 64])
    k.rw_cmask = k.inp("rwkv_cmask", [64, 512])


def stage_rwkv(k, l):
    nc = k.nc
    BL = 256
    NB = S_LEN // BL
    CPB = BL // 64
    H4 = [64, 4, BL]
    bc = lambda ap, shape: ap.to_broadcast(shape)
    with Scope(k) as sc:
        pp = sc.sb("pp", [64, 8, 4], F32)
        lr = sc.sb("lr", [64, 3], F32)
        wup = sc.sb("wup", [32, 256], F32); aup = sc.sb("aup", [32, 256], F32); gup = sc.sb("gup", [64, 256], F32)
        lnr = sc.sb("lnr", [64, 2, 256], F32)
        mk = sc.sb("mk", [64, 4, 64], F32)
        cmask = sc.sb("cmask", [64, BL], F32)
        ones = sc.sb("ones64", [64, 64], F32)
        prm = sc.sb("prm", [64, 4, 4], F32)
        k.dma('sp', pp[:], k.rw_pp[l], w=[pp]); k.dma('sp', lr[:], k.rw_lr[l], w=[lr])
        k.dma('sp', wup[:], k.rw_wup[l], w=[wup]); k.dma('sp', aup[:], k.rw_aup[l], w=[aup]); k.dma('sp', gup[:], k.rw_gup[l], w=[gup])
        for i in range(2):
            k.dma('sp', lnr[:, i, :], k.rw_ln[l, i:i + 1, :].broadcast_to([64, 256]), w=[lnr])
        k.dma('sp', mk[:], k.rw_masks, w=[mk]); k.dma('sp', cmask[:], k.rw_cmask[:, 0:BL], w=[cmask])
        k.S.op('pool', lambda: nc.gpsimd.memset(ones[:], 1.0), [], [ones])
        k.ts('dve', prm[:, 0, :], pp[:, 3, :], -1.0, None, ALU.mult, None, r=[pp], w=[prm])
        k.ts('dve', prm[:, 1, :], pp[:, 6, :], -1.0, 1.0, ALU.mult, ALU.add, r=[pp], w=[prm])
        P3 = sc.sb("P3", [64, 3, 4, BL], F32)
        halo = sc.sb("halo", [64, 3, 4], F32)
        LR = sc.sb("LR", [64, 3, BL], F32)
        halo2 = sc.sb("halo2", [64, 3], F32)
        ELW = sc.sb("ELW", H4, F32); SC_ = sc.sb("SCAN", H4, F32); AA = sc.sb("AA", H4, F32); KKN = sc.sb("KKN", H4, F32)
        T1 = sc.sb("T1", H4, F32); T2 = sc.sb("T2", H4, F32); CM4 = sc.sb("CM4", H4, F32)
        OUT = [{nm: sc.sb("%s%d" % (nm, i), H4, F32 if nm == "GAM" else BF16) for nm in ("AT", "BT", "KT", "RT", "RK", "GAM", "V")} for i in range(2)]
        SGs = [sc.sb("SG%d" % i, [64, BL], BF16) for i in range(2)]
        gupb = sc.sb("gupb", [64, 256], BF16)
        ppb = sc.sb("ppb", [64, 4], BF16)
        identb64 = k.ident_bf
        XY = [[sc.sb("XY%d_%d" % (i, j), [64, 2, 4, 64], BF16) for j in range(2)] for i in range(2)]
        PP = [[sc.sb("PPi%d_%d" % (i, j), [64, 4, 64], BF16) for j in range(2)] for i in range(2)]
        AKRK = [sc.sb("AKRK%d" % i, [64, 2, 4, 64], BF16) for i in range(2)]
        RBT = [sc.sb("RBT%d" % i, [64, 4, 64], BF16) for i in range(2)]
        TOK = [sc.sb("TOK%d" % i, [64, 3, 4, 64], BF16) for i in range(2)]
        Wsb = sc.sb("Wsb", [64, 4, 64], BF16); Usb = sc.sb("Usb", [64, 4, 64], BF16)
        Hs = [sc.sb("Hs%d" % i, [64, 4, 64], F32) for i in range(2)]
        Hb = [sc.sb("Hb%d" % i, [64, 4, 64], BF16) for i in range(2)]
        yc = sc.sb("yc", [64, 4, 64], F32); ysq = sc.sb("ysq", [64, 4, 64], F32)
        sm = sc.sb("sm", [64, 6, 4], F32)
        yab = [sc.sb("yab%d" % i, [64, CPB, 256], BF16) for i in range(2)]
        psA1 = sc.ps("psA1", [64, 512]); psA2 = sc.ps("psA2", [64, 512]); psA3 = sc.ps("psA3", [64, 512]); psA4 = sc.ps("psA4", [64, 512])
        psH = sc.ps("psHr", [64, 512]); psY = sc.ps("psYr", [64, 512]); psC = sc.ps("psCr", [64, 512]); psQ = sc.ps("psQr", [64, 512])
        k.S.op('pool', lambda: nc.gpsimd.memset(Hs[1][:], 0.0), [], [Hs[1]])
        k.S.op('pool', lambda: nc.gpsimd.memset(Hb[1][:], 0.0), [], [Hb[1]])
        k.copy('dve', gupb[:], gup[:], r=[gup], w=[gupb])
        k.copy('dve', ppb[:], pp[:, 7, :], r=[pp], w=[ppb])
        k.S.op('pool', lambda: nc.gpsimd.memset(halo[:], 0.0), [], [halo])
        k.S.op('pool', lambda: nc.gpsimd.memset(halo2[:], 0.0), [], [halo2])
        k.copy('dve', CM4[:], bc(cmask[:].unsqueeze(1), H4), r=[cmask], w=[CM4])
        E_ = BL - 1

        def prep(tb):
            O = OUT[tb % 2]; SG = SGs[tb % 2]
            AT, BT, KT, RT, RK, GAM, V_ = O["AT"], O["BT"], O["KT"], O["RT"], O["RK"], O["GAM"], O["V"]
            t0 = tb * BL
            for q in range(3):
                k.dma('act', P3[:, q, :, :], k.PT[q * 256:(q + 1) * 256, t0:t0 + BL].rearrange("(h d) t -> d h t", d=64), w=[P3])
            k.dma('act', LR[0:32, 0, :], k.PT[768:800, t0:t0 + BL], w=[LR])
            k.dma('act', LR[0:32, 1, :], k.PT[800:832, t0:t0 + BL], w=[LR])
            k.dma('act', LR[:, 2, :], k.PT[832:896, t0:t0 + BL], w=[LR])
            yield
            for q in range(3):
                p_ = P3[:, q, :, :]
                k.tt('dve', T1[:, :, 1:BL], p_[:, :, 0:E_], p_[:, :, 1:BL], ALU.subtract, r=[P3], w=[T1])
                k.tt('dve', T1[:, :, 0:1], halo[:, q, :].unsqueeze(2), p_[:, :, 0:1], ALU.subtract, r=[P3, halo], w=[T1])
                k.copy('pool', halo[:, q, :].unsqueeze(2), p_[:, :, E_:BL], r=[P3, T1], w=[halo])
                k.tt('pool', T1[:], T1[:], bc(pp[:, q, :].unsqueeze(2), H4), ALU.mult, r=[T1, pp], w=[T1])
                if q < 2:
                    k.tt('pool', p_, p_, T1[:], ALU.add, r=[P3, T1, halo], w=[P3])
                else:
                    k.tt('pool', V_[:], p_, T1[:], ALU.add, r=[P3, T1, halo], w=[V_])
                yield
            for q, rows in ((0, 32), (1, 32), (2, 64)):
                x_ = LR[0:rows, q, :]
                t_ = T2[0:rows, 0, :]
                k.tt('dve', t_[:, 1:BL], x_[:, 0:E_], x_[:, 1:BL], ALU.subtract, r=[LR], w=[T2])
                k.tt('dve', t_[:, 0:1], halo2[0:rows, q:q + 1], x_[:, 0:1], ALU.subtract, r=[LR, halo2], w=[T2])
                k.copy('dve', halo2[0:rows, q:q + 1], x_[:, E_:BL], r=[LR, T2], w=[halo2])
                k.stt('dve', x_, t_, lr[0:rows, q:q + 1], x_, ALU.mult, ALU.add, r=[T2, lr, LR, halo2], w=[LR])
            yield
            R_ = P3[:, 0, :, :]; Kp = P3[:, 1, :, :]
            k.act(LR[0:32, 0, :], LR[0:32, 0, :], AF.Tanh, r=[LR], w=[LR])
            k.act(SG[:], LR[:, 2, :], AF.Sigmoid, r=[LR], w=[SG])
            for h in range(4):
                k.mm(psQ[:, 0:BL], [(wup[:, h * 64:(h + 1) * 64], LR[0:32, 0, :])], r=[wup, LR], w=[psQ])
                k.act(T1[:, h, :], psQ[:, 0:BL], AF.Exp, r=[psQ, prm], w=[T1], scale=-1.0, bias=prm[:, 0, h:h + 1])
                k.mm(psQ[:, BL:2 * BL], [(aup[:, h * 64:(h + 1) * 64], LR[0:32, 1, :])], r=[aup, LR], w=[psQ], start=False)
                k.act(AA[:, h, :], psQ[:, BL:2 * BL], AF.Sigmoid, r=[psQ, pp], w=[AA], bias=pp[:, 4, h:h + 1])
                yield
            k.act(T1[:], T1[:], AF.Ln, r=[T1], w=[T1], bias=1.0)
            k.act(ELW[:], T1[:], AF.Exp, r=[T1], w=[ELW], scale=-1.0, bias=-0.5)
            k.S.op('dve', lambda: nc.vector.tensor_tensor_scan(
                out=SC_[:].rearrange("p h t -> p (h t)"), data0=CM4[:].rearrange("p h t -> p (h t)"),
                data1=ELW[:].rearrange("p h t -> p (h t)"), initial=0.0, op0=ALU.mult, op1=ALU.add), [CM4, ELW], [SC_])
            yield
            k.tt('pool', KKN[:], Kp, bc(pp[:, 5, :].unsqueeze(2), H4), ALU.mult, r=[P3, pp], w=[KKN])
            k.tt('pool', T1[:], KKN[:], KKN[:], ALU.mult, r=[KKN], w=[T1])
            for h in range(4):
                k.mm(psQ[:, 0:BL], [(ones[:], T1[:, h, :])], r=[ones, T1], w=[psQ])
                k.act(T2[:, h, :], psQ[:, 0:BL], AF.Ln, r=[psQ], w=[T2], bias=1e-24)
                yield
            k.act(T2[:], T2[:], AF.Exp, r=[T2], w=[T2], scale=-0.5)
            k.tt('dve', KKN[:], KKN[:], T2[:], ALU.mult, r=[KKN, T2], w=[KKN])
            yield
            k.tt('pool', T1[:], SC_[:], ELW[:], ALU.subtract, r=[SC_, ELW], w=[T1])
            k.act(T1[:], T1[:], AF.Exp, r=[T1], w=[T1], scale=-1.0)
            k.stt('dve', AT[:], KKN[:], -1.0, T1[:], ALU.mult, ALU.mult, r=[KKN, T1], w=[AT])
            yield
            k.act(T2[:], SC_[:], AF.Exp, r=[SC_], w=[T2])
            k.tt('pool', T1[:], KKN[:], AA[:], ALU.mult, r=[KKN, AA], w=[T1])
            k.tt('dve', BT[:], T1[:], T2[:], ALU.mult, r=[T1, T2], w=[BT])
            yield
            k.tt('pool', T1[:], AA[:], bc(pp[:, 6, :].unsqueeze(2), H4), ALU.mult, r=[AA, pp], w=[T1])
            k.tt('pool', T1[:], T1[:], bc(prm[:, 1, :].unsqueeze(2), H4), ALU.add, r=[T1, prm], w=[T1])
            k.tt('dve', Kp, Kp, T1[:], ALU.mult, r=[P3, T1, KKN], w=[P3])
            yield
            k.tt('dve', KT[:], Kp, T2[:], ALU.mult, r=[P3, T2], w=[KT])
            k.tt('pool', RK[:], R_, Kp, ALU.mult, r=[P3], w=[RK])
            k.act(GAM[:], SC_[:], AF.Exp, r=[SC_], w=[GAM], scale=-1.0)
            k.tt('dve', RT[:], R_, GAM[:], ALU.mult, r=[P3, GAM], w=[RT])
            yield

        def phaseA(nch):
            tb, n = divmod(nch, CPB)
            O = OUT[tb % 2]
            AT, BT, KT, RT, V_ = O["AT"], O["BT"], O["KT"], O["RT"], O["V"]
            c_ = slice(n * 64, (n + 1) * 64)
            par = nch % 2
            xy = XY[par][0]; akrk = AKRK[par]; rbt = RBT[par]; tok = TOK[par]
            fns = []
            for h in range(4):
                fns.append(lambda h=h: nc.tensor.matmul(psA1[:, h * 64:(h + 1) * 64], lhsT=BT[:, h, c_], rhs=AT[:, h, c_], start=True, stop=True, skip_group_check=True))
                fns.append(lambda h=h: nc.tensor.matmul(psA1[:, 256 + h * 64:256 + (h + 1) * 64], lhsT=AT[:, h, c_], rhs=BT[:, h, c_], start=True, stop=True, skip_group_check=True))
            k.S.pe_group(fns, [AT, BT], [psA1])
            fns = []
            for h in range(4):
                fns.append(lambda h=h: nc.tensor.matmul(psA2[:, h * 64:(h + 1) * 64], lhsT=KT[:, h, c_], rhs=AT[:, h, c_], start=True, stop=True, skip_group_check=True))
                fns.append(lambda h=h: nc.tensor.matmul(psA2[:, 256 + h * 64:256 + (h + 1) * 64], lhsT=KT[:, h, c_], rhs=RT[:, h, c_], start=True, stop=True, skip_group_check=True))
            k.S.pe_group(fns, [AT, KT, RT], [psA2])
            k.S.pe_group([lambda h=h: nc.tensor.matmul(psA3[:, h * 64:(h + 1) * 64], lhsT=BT[:, h, c_], rhs=RT[:, h, c_], start=True, stop=True, skip_group_check=True)
                          for h in range(4)], [BT, RT], [psA3])
            yield
            v4 = lambda ps, a: ps[:, a * 256:(a + 1) * 256].rearrange("p (h f) -> p h f", h=4)
            mb = lambda i: bc(mk[:, i, :].unsqueeze(1), [64, 4, 64])
            k.tt('dve', xy[:, 0, :, :], v4(psA1, 0), mb(0), ALU.mult, r=[psA1, mk], w=[xy])
            k.tt('dve', xy[:, 1, :, :], v4(psA1, 1), mb(1), ALU.mult, r=[psA1, mk], w=[xy])
            k.tt('dve', akrk[:, 0, :, :], v4(psA2, 0), mb(0), ALU.mult, r=[psA2, mk], w=[akrk])
            k.tt('dve', akrk[:, 1, :, :], v4(psA2, 1), mb(2), ALU.mult, r=[psA2, mk], w=[akrk])
            k.tt('dve', rbt[:], v4(psA3, 0), mb(2), ALU.mult, r=[psA3, mk], w=[rbt])
            yield
            fns = []
            psA1b = psA1[:, :].bitcast(BF16)
            for qi, src_ in enumerate((V_, BT, KT)):
                for h in range(4):
                    dst = psA1b[:, qi * 256 + h * 64:qi * 256 + (h + 1) * 64]
                    fns.append(lambda dst=dst, s_=src_[:, h, c_]: nc.tensor.transpose(out=dst, in_=s_, identity=k.ident_bf[0:64, 0:64]))
            k.S.pe_group(fns, [V_, BT, KT, k.ident_bf], [psA1])
            yield
            k.copy('act', tok[:].rearrange("p a h f -> p (a h f)"), psA1b[:, 0:768], r=[psA1], w=[tok])
            P_ = PP[par][0]
            k.tt('dve', P_[:], xy[:, 0, :, :], mb(3), ALU.add, r=[xy, mk], w=[P_])
            yield
            for lev in range(5):
                xyn = XY[par][(lev + 1) % 2]
                fns = []
                for h in range(4):
                    fns.append(lambda h=h, xy=xy: nc.tensor.matmul(psA4[:, 256 + h * 64:256 + (h + 1) * 64], lhsT=xy[:, 0, h, :], rhs=xy[:, 1, h, :], start=True, stop=True, skip_group_check=True))
                    if lev < 4:
                        fns.append(lambda h=h, xy=xy: nc.tensor.matmul(psA4[:, h * 64:(h + 1) * 64], lhsT=xy[:, 1, h, :], rhs=xy[:, 0, h, :], start=True, stop=True, skip_group_check=True))
                k.S.pe_group(fns, [xy], [psA4])
                yield
                if lev < 4:
                    k.copy('act', xyn[:].rearrange("p a h f -> p (a h f)"), psA4[:, :], r=[psA4], w=[xyn])
                else:
                    k.copy('act', xyn[:, 1, :, :].rearrange("p h f -> p (h f)"), psA4[:, 256:512], r=[psA4], w=[xyn])
                yield
                Pn = PP[par][(lev + 1) % 2]
                k.S.pe_group([lambda h=h, xyn=xyn, P_=P_: nc.tensor.matmul(psA3[:, 256 + h * 64:256 + (h + 1) * 64], lhsT=xyn[:, 1, h, :], rhs=P_[:, h, :], start=True, stop=True, skip_group_check=True)
                              for h in range(4)], [xyn, P_], [psA3])
                yield
                k.tt('dve', Pn[:], P_[:], v4(psA3, 1), ALU.add, r=[P_, psA3], w=[Pn])
                yield
                P_ = Pn; xy = xyn

        def phaseB(nch):
            tb, n = divmod(nch, CPB)
            O = OUT[tb % 2]; SG = SGs[tb % 2]
            AT, RT, RK, GAM = O["AT"], O["RT"], O["RK"], O["GAM"]
            c_ = slice(n * 64, (n + 1) * 64)
            par = nch % 2
            akrk = AKRK[par]; rbt = RBT[par]; tok = TOK[par]; TT = PP[par][1]
            Hold = Hs[(nch + 1) % 2]; Hnew = Hs[nch % 2]
            Hbo = Hb[(nch + 1) % 2]; Hbn = Hb[nch % 2]
            yab_ = yab[tb % 2]
            fns = []
            for h in range(4):
                fns.append(lambda h=h: nc.tensor.matmul(psH[:, h * 64:(h + 1) * 64], lhsT=AT[:, h, c_], rhs=Hbo[:, h, :], start=(h == 0), stop=False, skip_group_check=True))
                fns.append(lambda h=h: nc.tensor.matmul(psH[:, h * 64:(h + 1) * 64], lhsT=akrk[:, 0, h, :], rhs=tok[:, 0, h, :], start=False, stop=True, skip_group_check=True))
            k.S.pe_group(fns, [AT, Hbo, akrk, tok], [psH])
            yield
            k.copy('act', Wsb[:].rearrange("p h f -> p (h f)"), psH[:, 0:256], r=[psH], w=[Wsb])
            yield
            k.S.pe_group([lambda h=h: nc.tensor.matmul(psH[:, 256 + h * 64:256 + (h + 1) * 64], lhsT=TT[:, h, :], rhs=Wsb[:, h, :], start=False, stop=True, skip_group_check=True)
                          for h in range(4)], [TT, Wsb], [psH])
            yield
            k.copy('act', Usb[:].rearrange("p h f -> p (h f)"), psH[:, 256:512], r=[psH], w=[Usb])
            yield
            fns = []
            for h in range(4):
                fns.append(lambda h=h: nc.tensor.matmul(psC[:, h * 64:(h + 1) * 64], lhsT=tok[:, 1, h, :], rhs=Usb[:, h, :], start=(h == 0), stop=False, skip_group_check=True))
                fns.append(lambda h=h: nc.tensor.matmul(psC[:, h * 64:(h + 1) * 64], lhsT=tok[:, 2, h, :], rhs=tok[:, 0, h, :], start=False, stop=True, skip_group_check=True))
                fns.append(lambda h=h: nc.tensor.matmul(psC[:, 256 + h:256 + h + 1], lhsT=RK[:, h, c_], rhs=ppb[:, h:h + 1], start=False, stop=True, skip_group_check=True))
            k.S.pe_group(fns, [tok, Usb, RK, ppb], [psC])
            fns = []
            for h in range(4):
                fns.append(lambda h=h: nc.tensor.matmul(psY[:, h * 64:(h + 1) * 64], lhsT=RT[:, h, c_], rhs=Hbo[:, h, :], start=(h == 0), stop=False, skip_group_check=True))
                fns.append(lambda h=h: nc.tensor.matmul(psY[:, h * 64:(h + 1) * 64], lhsT=rbt[:, h, :], rhs=Usb[:, h, :], start=False, stop=False, skip_group_check=True))
                fns.append(lambda h=h: nc.tensor.matmul(psY[:, h * 64:(h + 1) * 64], lhsT=akrk[:, 1, h, :], rhs=tok[:, 0, h, :], start=False, stop=True, skip_group_check=True))
            fns.append(lambda: nc.tensor.matmul(psY[:, 256:512], lhsT=SG[:, c_], rhs=gupb[:, :], start=False, stop=True, skip_group_check=True))
            k.S.pe_group(fns, [RT, Hbo, rbt, Usb, akrk, tok, SG, gupb], [psY])
            yield
            k.tt('dve', Hnew[:], psC[:, 0:256].rearrange("p (h f) -> p h f", h=4), Hold[:], ALU.add, r=[psC, Hold], w=[Hnew])
            k.copy('dve', sm[:, 5, :], psC[:, 256:260], r=[psC], w=[sm])
            k.tt('dve', Hnew[:], Hnew[:], bc(GAM[:, :, n * 64 + 63:n * 64 + 64], [64, 4, 64]), ALU.mult, r=[Hnew, GAM], w=[Hnew])
            k.copy('pool', Hbn[:], Hnew[:], r=[Hnew], w=[Hbn])
            yield
            y3 = psY[:, 0:256].rearrange("p (h f) -> p h f", h=4)
            k.S.op('dve', lambda: nc.vector.reduce_sum(out=sm[:, 0, :], in_=y3, axis=AX.X), [psY], [sm])
            k.ts('dve', sm[:, 1, :], sm[:, 0, :], 1.0 / 64.0, None, ALU.mult, None, r=[sm], w=[sm])
            k.tt('dve', yc[:], y3, bc(sm[:, 1, :].unsqueeze(2), [64, 4, 64]), ALU.subtract, r=[psY, sm], w=[yc])
            yield
            k.tt('pool', ysq[:], yc[:], yc[:], ALU.mult, r=[yc], w=[ysq])
            k.S.op('dve', lambda: nc.vector.reduce_sum(out=sm[:, 2, :], in_=ysq[:], axis=AX.X), [ysq], [sm])
            k.act(sm[:, 3, :], sm[:, 2, :], AF.Ln, r=[sm], w=[sm], scale=1.0 / 64.0, bias=GN_EPS)
            k.act(sm[:, 4, :], sm[:, 3, :], AF.Exp, r=[sm], w=[sm], scale=-0.5)
            yield
            k.tt('dve', yc[:], yc[:], bc(sm[:, 4, :].unsqueeze(2), [64, 4, 64]), ALU.mult, r=[yc, sm], w=[yc])
            k.tt('pool', yc[:], yc[:], lnr[:, 0, :].rearrange("p (h f) -> p h f", h=4), ALU.mult, r=[yc, lnr], w=[yc])
            k.tt('pool', yc[:], yc[:], lnr[:, 1, :].rearrange("p (h f) -> p h f", h=4), ALU.add, r=[yc, lnr], w=[yc])
            k.tt('dve', ysq[:], tok[:, 0, :, :], bc(sm[:, 5, :].unsqueeze(2), [64, 4, 64]), ALU.mult, r=[tok, sm], w=[ysq])
            yield
            k.tt('pool', yc[:], yc[:], ysq[:], ALU.add, r=[yc, ysq], w=[yc])
            k.tt('dve', yab_[:, n, :], yc[:].rearrange("p h f -> p (h f)"), psY[:, 256:512], ALU.mult, r=[yc, psY], w=[yab_])
            if n == CPB - 1:
                k.dma('sp', k.Y[tb * BL:(tb + 1) * BL, 0:256].rearrange("(n p) c -> p n c", p=64), yab_[:], r=[yab_])
            yield

        def run_all(*gens):
            gens = [g for g in gens if g is not None]
            while gens:
                for g in list(gens):
                    try:
                        next(g)
                    except StopIteration:
                        gens.remove(g)

        NCH = S_LEN // 64
        run_all(prep(0))
        run_all(phaseA(0), prep(1) if NB > 1 else None)
        for nch in range(NCH):
            tb, n = divmod(nch, CPB)
            gens = [phaseB(nch)]
            if nch + 1 < NCH:
                gens.append(phaseA(nch + 1))
            if n == 1 and tb + 2 <= NB - 1 + 0 and False:
                pass
            if n == 0 and tb >= 1 and tb + 1 < NB:
                gens.append(prep(tb + 1))
            run_all(*gens)


def build(nlayers=DEPTH, taps=()):
    k = K(nlayers, taps=taps)
    setup_globals(k)
    setup_fox(k)
    setup_rwkv(k)
    setup_ffn(k)
    setup_nsa(k)
    for l in range(nlayers):
        xin = k.x_in if l == 0 else k.XR
        xout = k.OUT if l == nlayers - 1 else k.XR
        stage_mod(k, l)
        stage_proj(k, l, xin)
        stage_rwkv(k, l)
        stage_fox(k, l)
        stage_nsa(k, l)
        stage_out(k, l, xin, k.XR1)
        stage_ffn(k, l, k.XR1, xout)
    k.S.barrier()
    return k


_CACHE = {}


def kernel(**inputs):
    if "k" not in _CACHE:
        _CACHE["k"] = build(DEPTH)
    k = _CACHE["k"]
    sh = prep_shared(inputs)
    in_maps = []
    for b in range(8):
        d = dict(sh)
        d.update(prep_core(inputs, b))
        in_maps.append({n: v for n, v in d.items() if n in k.ins})
    res = run_bass_kernel_spmd(k.nc, in_maps, core_ids=list(range(8)))
    out = np.stack([np.asarray(res.results[b]["out"], dtype=np.float32) for b in range(8)], axis=0)
    return out
```

```python
import numpy as np
import ml_dtypes
from contextlib import ExitStack
import concourse.bass as bass
import concourse.mybir as mybir
from concourse.bass_utils import run_bass_kernel_spmd

F32 = mybir.dt.float32
BF16 = mybir.dt.bfloat16
AF = mybir.ActivationFunctionType
ALU = mybir.AluOpType
AX = mybir.AxisListType
NPBF = ml_dtypes.bfloat16

S_LEN = 4096
D = 1024
DEPTH = 4
NTB = 8
N_IN = 3224
D_FF = 2816
NEG = -30000.0
RMS_EPS = 1e-6
GN_EPS = 64e-5


class Sched:
    ENG = ('pe', 'act', 'dve', 'pool')
    LIMIT = 30000

    def __init__(self, nc):
        self.nc = nc
        self.e = {'pe': nc.tensor, 'act': nc.scalar, 'dve': nc.vector, 'pool': nc.gpsimd, 'sp': nc.sync}
        self.epoch = {k: 0 for k in self.ENG}
        self.sem = {k: nc.alloc_semaphore("c_%s_0" % k) for k in self.ENG}
        self.cnt = {k: 0 for k in self.ENG}
        self.seen = {k: {} for k in self.e}
        self.lastw = {}
        self.reads = {}
        self.dma_sems = {'hw': [[nc.alloc_semaphore("d%d" % i), 0, "dma%d" % i] for i in range(24)],
                         'sw': [[nc.alloc_semaphore("ds%d" % i), 0, "dmas%d" % i] for i in range(8)]}
        self.ndma = {'hw': 0, 'sw': 0}
        self.n_inst = 0
        self.n_wait = 0
        self.per = {}

    def _wait(self, eng, tok):
        key, sem, val = tok
        if self.seen[eng].get(key, 0) >= val:
            return
        self.e[eng].wait_ge(sem, val)
        self.n_wait += 1
        self.per[eng] = self.per.get(eng, 0) + 1
        self.seen[eng][key] = val

    def _deps(self, eng, reads, writes):
        for b in reads:
            t = self.lastw.get(b)
            if t is not None:
                self._wait(eng, t)
        for b in writes:
            t = self.lastw.get(b)
            if t is not None:
                self._wait(eng, t)
            for t in self.reads.get(b, ()):
                self._wait(eng, t)

    def _commit(self, tok, reads, writes):
        for b in reads:
            self.reads.setdefault(b, []).append(tok)
        for b in writes:
            self.lastw[b] = tok
            self.reads[b] = []

    def _bump(self, eng, ins):
        if self.cnt[eng] >= self.LIMIT:
            self.epoch[eng] += 1
            self.sem[eng] = self.nc.alloc_semaphore("c_%s_%d" % (eng, self.epoch[eng]))
            self.cnt[eng] = 0
        self.cnt[eng] += 1
        ins.then_inc(self.sem[eng], 1)
        return ("%s_%d" % (eng, self.epoch[eng]), self.sem[eng], self.cnt[eng])

    @staticmethod
    def _norm(reads, writes):
        rd = [getattr(b, 'n', b) for b in reads]
        wr = [getattr(b, 'n', b) for b in writes]
        ps = [b for b in rd if b.startswith("ps")]
        rd = [b for b in rd if not b.startswith("ps")]
        return rd, wr + [b for b in ps if b not in wr]

    def op(self, eng, inst_fn, reads=(), writes=()):
        reads, writes = self._norm(reads, writes)
        self._deps(eng, reads, writes)
        ins = inst_fn()
        self.per[eng] = self.per.get(eng, 0) + 1
        tok = self._bump(eng, ins)
        self._commit(tok, reads, writes)
        self.n_inst += 1
        return tok

    def pe_group(self, fns, reads=(), writes=()):
        reads, writes = self._norm(reads, writes)
        self._deps('pe', reads, writes)
        ins = None
        for f in fns:
            ins = f()
            self.n_inst += 1
            self.per['pe'] = self.per.get('pe', 0) + 1
        tok = self._bump('pe', ins)
        self._commit(tok, reads, writes)
        return tok

    def dma(self, q, out, in_, reads=(), writes=(), **kw):
        reads, writes = self._norm(reads, writes)
        self._deps(q, reads, writes)
        cls = 'sw' if q == 'pool' else 'hw'
        pool_ = self.dma_sems[cls]
        slot = pool_[self.ndma[cls] % len(pool_)]
        self.ndma[cls] += 1
        if slot[1] > 0:
            self._wait(q, (slot[2], slot[0], slot[1]))
        if slot[1] >= self.LIMIT:
            slot[0] = self.nc.alloc_semaphore("%s_e%d" % (slot[2], self.ndma[cls]))
            slot[1] = 0
            slot[2] = slot[2] + "x"
        slot[1] += 16
        ins = self.e[q].dma_start(out=out, in_=in_, **kw)
        self.per[q] = self.per.get(q, 0) + 1
        ins.then_inc(slot[0], 16)
        tok = (slot[2], slot[0], slot[1])
        self._commit(tok, reads, writes)
        self.n_inst += 1
        return tok

    def barrier(self, engines=('pe', 'act', 'dve', 'pool', 'sp')):
        toks = [("%s_%d" % (k, self.epoch[k]), self.sem[k], self.cnt[k]) for k in self.ENG if self.cnt[k] > 0]
        toks += [(s[2], s[0], s[1]) for p_ in self.dma_sems.values() for s in p_ if s[1] > 0]
        for e in engines:
            for t in toks:
                self._wait(e, t)
        self.lastw = {}
        self.reads = {}


class Pipe:
    def __init__(self, lag=2):
        self.q = []
        self.lag = lag

    def push(self, first, second):
        first()
        self.q.append(second)
        while len(self.q) > self.lag:
            self.q.pop(0)()

    def flush(self):
        while self.q:
            self.q.pop(0)()


class Tile:
    def __init__(self, h, name):
        self.h = h
        self.n = name

    def __getitem__(self, idx):
        return self.h[idx]


class Scope:
    cnt = 0

    def __init__(self, k):
        self.k = k
        self.es = ExitStack()

    def __enter__(self):
        self.es.__enter__()
        Scope.cnt += 1
        self.id = Scope.cnt
        return self

    def sb(self, name, shape, dt):
        nm = "%s_%d" % (name, self.id)
        h = self.es.enter_context(self.k.nc.sbuf_tensor(nm, list(shape), dt))
        return Tile(h, nm)

    def ps(self, name, shape, dt=F32):
        nm = "%s_%d" % (name, self.id)
        h = self.es.enter_context(self.k.nc.psum_tensor(nm, list(shape), dt))
        return Tile(h, nm)

    def __exit__(self, *a):
        self.k.S.barrier()
        return self.es.__exit__(*a)


class K:
    def __init__(self, nlayers, taps=()):
        self.nc = bass.Bass("TRN2", target_bir_lowering=False)
        self.S = Sched(self.nc)
        self.nl = nlayers
        self.taps = set(taps)
        self.ins = {}
        self.dr = {}

    def inp(self, name, shape, dt=F32):
        t = self.nc.dram_tensor(name, list(shape), dt, kind="ExternalInput").ap()
        self.ins[name] = t
        return t

    def scratch(self, name, shape, dt=F32, out=False):
        kind = "ExternalOutput" if (out or name in self.taps) else "Internal"
        t = self.nc.dram_tensor(name, list(shape), dt, kind=kind).ap()
        self.dr[name] = t
        return t

    def act(self, out, in_, func, r, w, bias=0.0, scale=1.0, accum=None):
        nc = self.nc
        if accum is None:
            return self.S.op('act', lambda: nc.scalar.activation(out=out, in_=in_, func=func, bias=bias, scale=scale), r, w)
        return self.S.op('act', lambda: nc.scalar.activation(out=out, in_=in_, func=func, bias=bias, scale=scale, accum_out=accum), r, w)

    def ts(self, eng, out, in0, s1, s2, op0, op1, r, w):
        e = self.S.e[eng]
        if op1 is None:
            return self.S.op(eng, lambda: e.tensor_scalar(out=out, in0=in0, scalar1=s1, scalar2=None, op0=op0), r, w)
        return self.S.op(eng, lambda: e.tensor_scalar(out=out, in0=in0, scalar1=s1, scalar2=s2, op0=op0, op1=op1), r, w)

    def tt(self, eng, out, in0, in1, op, r, w):
        e = self.S.e[eng]
        return self.S.op(eng, lambda: e.tensor_tensor(out=out, in0=in0, in1=in1, op=op), r, w)

    def stt(self, eng, out, in0, scalar, in1, op0, op1, r, w):
        e = self.S.e[eng]
        return self.S.op(eng, lambda: e.scalar_tensor_tensor(out=out, in0=in0, scalar=scalar, in1=in1, op0=op0, op1=op1), r, w)

    def copy(self, eng, out, in_, r, w):
        if eng == 'act':
            return self.S.op('act', lambda: self.nc.scalar.copy(out=out, in_=in_), r, w)
        e = self.S.e[eng]
        return self.S.op(eng, lambda: e.tensor_copy(out=out, in_=in_), r, w)

    def mm(self, out, pairs, r, w, start=True, stop=True, sgc=False):
        nc = self.nc
        n = len(pairs)
        fns = []
        for i, (l, rh) in enumerate(pairs):
            fns.append(lambda l=l, rh=rh, i=i: nc.tensor.matmul(out, lhsT=l, rhs=rh, start=(start and i == 0), stop=(stop and i == n - 1),
                                                               skip_group_check=(sgc or not start)))
        return self.S.pe_group(fns, r, w)

    def transpose(self, out, in_, ident, r, w):
        nc = self.nc
        return self.S.op('pe', lambda: nc.tensor.transpose(out=out, in_=in_, identity=ident), r, w)

    def dma(self, q, out, in_, r=(), w=(), **kw):
        return self.S.dma(q, out, in_, r, w, **kw)


def w_in_perm_index():
    idx = list(range(0, 896))
    idx += list(range(896, 1664))
    for c in range(3):
        idx += list(range(2054 + c * 64, 2054 + c * 64 + 64))
        idx += list(range(2054 + (c + 3) * 64, 2054 + (c + 3) * 64 + 64))
    idx += list(range(2438, 2566))
    idx += list(range(2566, 2694))
    idx += list(range(2694, 2822))
    idx += list(range(2950, 3078))
    idx += list(range(2048, 2054))
    idx += list(range(1664, 2048))
    idx += list(range(2822, 2950))
    idx += list(range(3078, 3206))
    idx += list(range(3206, 3224))
    assert len(idx) == N_IN and len(set(idx)) == N_IN
    return np.array(idx)


QKT_ROWS = 1664


def setup_globals(k):
    nc = k.nc
    k.x_in = k.inp("x", [S_LEN, D])
    k.cT = k.inp("cT", [128, 8])
    k.ada_w = k.inp("ada_w", [DEPTH, D, 6 * D])
    k.ada_b_fm = k.inp("ada_b_fm", [DEPTH, 128, 48])
    k.ada_b_row = k.inp("ada_b_row", [DEPTH, 6 * D])
    k.normg_fm = k.inp("normg_fm", [DEPTH, 4, 128, 8])
    k.normg_row = k.inp("normg_row", [DEPTH, 4, D])
    k.w_in = k.inp("w_in_p", [DEPTH, D, N_IN])
    k.ident_bf_d = k.inp("ident_bf", [128, 128], BF16)
    k.ident_f_d = k.inp("ident_f", [128, 128], F32)

    k.PT = k.scratch("PT", [896, S_LEN], F32)
    k.QKT = k.scratch("QKT", [QKT_ROWS, S_LEN], BF16)
    k.FL = k.scratch("FL", [6, S_LEN], F32)
    k.VT = k.scratch("VT", [S_LEN, 640], BF16)
    k.GT = k.scratch("GT", [S_LEN, 18], F32)
    k.Y = k.scratch("Y", [S_LEN, D], BF16)
    k.XR = k.scratch("XR", [S_LEN, D], F32)
    k.XR1 = k.scratch("XR1", [S_LEN, D], F32)
    k.OUT = k.scratch("out", [S_LEN, D], F32, out=True)

    def pers(name, shape, dt):
        return Tile(nc.alloc_sbuf_tensor(name, list(shape), dt), name)
    k.ident_bf = pers("ident_bf_sb", [128, 128], BF16)
    k.ident_f = pers("ident_f_sb", [128, 128], F32)
    k.sc = pers("sc", [128, 8], F32)
    k.modAB = pers("modAB", [128, 32], F32)
    k.gm_row = pers("gm_row", [128, D], F32)
    k.gf_row = pers("gf_row", [128, D], F32)
    k.dma('sp', k.ident_bf[:], k.ident_bf_d, w=[k.ident_bf])
    k.dma('sp', k.ident_f[:], k.ident_f_d, w=[k.ident_f])
    k.dma('sp', k.sc[:], k.cT, w=[k.sc])
    k.act(k.sc[:], k.sc[:], AF.Silu, r=[k.sc], w=[k.sc])


def stage_mod(k, l):
    with Scope(k) as sc:
        slab = [sc.sb("adaslab%d" % i, [128, 6 * D], F32) for i in range(2)]
        psA = sc.ps("psA", [128, 32])
        psR = [sc.ps("psR%d" % i, [128, 512]) for i in range(4)]
        bfm = sc.sb("bfm", [128, 48], F32)
        gfm = sc.sb("gfm", [128, 4, 8], F32)
        brow = sc.sb("brow", [128, 2, D], F32)
        grow = sc.sb("grow", [128, 2, D], F32)
        mfm = sc.sb("mfm", [128, 32], F32)
        sc_rep = sc.sb("sc_rep", [128, 8, 128], F32)
        for kc in range(8):
            k.copy('dve', sc_rep[:, kc, :], k.sc[:, kc:kc + 1].to_broadcast([128, 128]), r=[k.sc], w=[sc_rep])
        k.dma('sp', bfm[:], k.ada_b_fm[l], w=[bfm])
        k.dma('sp', gfm[:], k.normg_fm[l].rearrange("g p c -> p g c"), w=[gfm])
        k.dma('sp', brow[:, 0, :], k.ada_b_row[l:l + 1, 2 * D:3 * D].broadcast_to([128, D]), w=[brow])
        k.dma('sp', brow[:, 1, :], k.ada_b_row[l:l + 1, 5 * D:6 * D].broadcast_to([128, D]), w=[brow])
        k.dma('sp', grow[:, 0, :], k.normg_row[l, 1:2, :].broadcast_to([128, D]), w=[grow])
        k.dma('sp', grow[:, 1, :], k.normg_row[l, 3:4, :].broadcast_to([128, D]), w=[grow])
        fm_chunks = list(range(0, 16)) + list(range(24, 40))
        row_cols = [2 * D, 2 * D + 512, 5 * D, 5 * D + 512]
        for kc in range(8):
            sl = slab[kc % 2]
            k.dma('sp', sl[:], k.ada_w[l, kc * 128:(kc + 1) * 128, :], w=[sl])
            for i, j in enumerate(fm_chunks):
                k.mm(psA[:, i:i + 1], [(sl[:, j * 128:(j + 1) * 128], k.sc[:, kc:kc + 1])], r=[sl, k.sc], w=[psA],
                     start=(kc == 0 and i == 0), stop=(kc == 7), sgc=True)
            for i, c0 in enumerate(row_cols):
                k.mm(psR[i][:], [(sc_rep[:, kc, :], sl[:, c0:c0 + 512])], r=[sl, sc_rep], w=[psR[i]],
                     start=(kc == 0), stop=(kc == 7), sgc=True)
        k.tt('dve', mfm[:, 0:16], psA[:, 0:16], bfm[:, 0:16], ALU.add, r=[psA, bfm], w=[mfm])
        k.tt('dve', mfm[:, 16:32], psA[:, 16:32], bfm[:, 24:40], ALU.add, r=[psA, bfm], w=[mfm])
        k.stt('dve', k.modAB[:, 0:8], mfm[:, 8:16], 1.0, gfm[:, 0, :], ALU.add, ALU.mult, r=[mfm, gfm], w=[k.modAB])
        k.copy('dve', k.modAB[:, 8:16], mfm[:, 0:8], r=[mfm], w=[k.modAB])
        k.stt('dve', k.modAB[:, 16:24], mfm[:, 24:32], 1.0, gfm[:, 2, :], ALU.add, ALU.mult, r=[mfm, gfm], w=[k.modAB])
        k.copy('dve', k.modAB[:, 24:32], mfm[:, 16:24], r=[mfm], w=[k.modAB])
        for i in range(4):
            dst = (k.gm_row if i < 2 else k.gf_row)
            cs = slice((i % 2) * 512, (i % 2) * 512 + 512)
            k.tt('dve', dst[:, cs], psR[i][:], brow[:, i // 2, cs], ALU.add, r=[psR[i], brow], w=[dst])
            k.tt('pool', dst[:, cs], dst[:, cs], grow[:, i // 2, cs], ALU.mult, r=[dst, grow], w=[dst])


def stage_proj(k, l, xsrc):
    nc = k.nc
    with Scope(k) as sc:
        wsb = sc.sb("wsb", [128, 8, N_IN], BF16)
        wst = [sc.sb("wst%d" % i, [128, N_IN], F32) for i in range(2)]
        xt = [sc.sb("xt%d" % i, [128, D], F32) for i in range(2)]
        junk = sc.sb("junk", [128, D], BF16)
        xn = [sc.sb("xn%d" % i, [128, D], BF16) for i in range(2)]
        st = [sc.sb("st%d" % i, [128, 4], F32) for i in range(2)]
        HT = [sc.sb("HT%d" % i, [128, 8, 512], BF16) for i in range(2)]
        psT = [sc.ps("psT%d" % i, [128, D], BF16) for i in range(2)]
        psM = [sc.ps("psM%d" % i, [128, 512]) for i in range(4)]
        evf = [sc.sb("evf%d" % i, [128, 512], F32) for i in range(3)]
        evb = [sc.sb("evb%d" % i, [128, 512], BF16) for i in range(3)]
        evt = [sc.sb("evt%d" % i, [128, 640], BF16) for i in range(2)]
        evg = [sc.sb("evg%d" % i, [128, 18], F32) for i in range(2)]
        for kc in range(8):
            s = wst[kc % 2]
            k.dma('sp', s[:], k.w_in[l, kc * 128:(kc + 1) * 128, :], w=[s])
            k.copy('pool', wsb[:, kc, :], s[:], r=[s], w=[wsb])
        cnt = {"ev": 0, "pm": 0}

        def norm_gen(tb):
            ht = HT[tb % 2]
            t0_ = tb * 4
            k.dma('act', xt[t0_ % 2][:], xsrc[t0_ * 128:(t0_ + 1) * 128, :], w=[xt[t0_ % 2]])
            yield
            for sub in range(4):
                ti = tb * 4 + sub
                x_ = xt[ti % 2]; xn_ = xn[ti % 2]; st_ = st[ti % 2]; pt_ = psT[ti % 2]
                k.act(junk[:], x_[:], AF.Square, r=[x_], w=[junk, st_], scale=1.0 / 32.0, accum=st_[:, 0:1])
                k.act(st_[:, 1:2], st_[:, 0:1], AF.Ln, r=[st_], w=[st_], bias=RMS_EPS)
                k.act(st_[:, 2:3], st_[:, 1:2], AF.Exp, r=[st_], w=[st_], scale=-0.5)
                k.ts('dve', xn_[:], x_[:], st_[:, 2:3], None, ALU.mult, None, r=[x_, st_], w=[xn_])
                if sub < 3:
                    k.dma('act', xt[(ti + 1) % 2][:], xsrc[(ti + 1) * 128:(ti + 2) * 128, :], w=[xt[(ti + 1) % 2]])
                yield
                yield
                for kc in range(8):
                    k.transpose(pt_[:, kc * 128:(kc + 1) * 128], xn_[:, kc * 128:(kc + 1) * 128], k.ident_bf[:],
                                r=[xn_, k.ident_bf], w=[pt_])
                    if kc == 3:
                        yield
                yield
                for kc in range(8):
                    o = ht[:, kc, sub * 128:(sub + 1) * 128]
                    i_ = pt_[:, kc * 128:(kc + 1) * 128]
                    if kc % 2 == 0:
                        k.ts('dve', o, i_, k.modAB[:, kc:kc + 1], k.modAB[:, 8 + kc:9 + kc], ALU.mult, ALU.add,
                             r=[pt_, k.modAB], w=[ht])
                    else:
                        k.act(o, i_, AF.Identity, r=[pt_, k.modAB], w=[ht], scale=k.modAB[:, kc:kc + 1],
                              bias=k.modAB[:, 8 + kc:9 + kc])
                yield

        def step(g):
            if g is not None:
                try:
                    next(g)
                except StopIteration:
                    return None
            return g

        for _ in norm_gen(0):
            pass
        for tb in range(NTB):
            ht = HT[tb % 2]
            g = norm_gen(tb + 1) if tb + 1 < NTB else None
            tsl = slice(tb * 512, (tb + 1) * 512)
            fm = [(c * 128, 128, 'PT', c * 128) for c in range(7)]
            fm += [(896 + c * 128, 128, 'QKT', c * 128) for c in range(13)]
            fm += [(2560, 6, 'FL', 0)]
            for (c0, m, dst, r0) in fm:
                ps = psM[cnt["pm"] % 4]; cnt["pm"] += 1
                k.mm(ps[0:m, :], [(wsb[:, kc, c0:c0 + m], ht[:, kc, :]) for kc in range(8)], r=[wsb, ht], w=[ps])
                eng = 'act' if cnt["ev"] % 2 == 0 else 'dve'
                if dst == 'QKT':
                    ev = evb[cnt["ev"] % 3]
                    dd = k.QKT[r0:r0 + m, tsl]
                else:
                    ev = evf[cnt["ev"] % 3]
                    dd = (k.PT if dst == 'PT' else k.FL)[r0:r0 + m, tsl]
                cnt["ev"] += 1
                k.copy(eng, ev[0:m, :], ps[0:m, :], r=[ps], w=[ev])
                k.dma('sp', dd, ev[0:m, :], r=[ev])
                g = step(g)
            for sub in range(4):
                ti = tb * 4 + sub
                tok = slice(ti * 128, (ti + 1) * 128)
                ps0 = psM[cnt["pm"] % 4]; cnt["pm"] += 1
                ps1 = psM[cnt["pm"] % 4]; cnt["pm"] += 1
                lhs = lambda kc: ht[:, kc, sub * 128:(sub + 1) * 128]
                k.mm(ps0[:, 0:384], [(lhs(kc), wsb[:, kc, 2566:2950]) for kc in range(8)], r=[wsb, ht], w=[ps0])
                k.mm(ps1[:, 0:274], [(lhs(kc), wsb[:, kc, 2950:3224]) for kc in range(8)], r=[wsb, ht], w=[ps1])
                et = evt[ti % 2]; eg = evg[ti % 2]
                k.copy('act', et[:, 0:384], ps0[:, 0:384], r=[ps0], w=[et])
                k.copy('dve', et[:, 384:640], ps1[:, 0:256], r=[ps1], w=[et])
                k.copy('dve', eg[:], ps1[:, 256:274], r=[ps1], w=[eg])
                k.dma('sp', k.VT[tok, :], et[:], r=[et])
                k.dma('sp', k.GT[tok, :], eg[:], r=[eg])
                g = step(g)
            while g is not None:
                g = step(g)


def prep_shared(inp):
    f = lambda a: np.ascontiguousarray(np.asarray(a, dtype=np.float32))
    sh = {}
    sh["ada_w"] = f(inp["ada_w"])
    sh["ada_b_fm"] = f(np.asarray(inp["ada_b"]).reshape(DEPTH, 48, 128).transpose(0, 2, 1))
    sh["ada_b_row"] = f(inp["ada_b"])
    sh["normg_fm"] = f(np.asarray(inp["norm_g"]).reshape(DEPTH, 4, 8, 128).transpose(0, 1, 3, 2))
    sh["normg_row"] = f(inp["norm_g"])
    sh["w_in_p"] = f(np.asarray(inp["w_in"])[:, :, w_in_perm_index()])
    sh["ident_bf"] = np.eye(128, dtype=np.float32).astype(NPBF)
    sh["ident_f"] = np.eye(128, dtype=np.float32)
    sh["w_out"] = f(inp["w_out"]); sh["ffn_up"] = f(inp["ffn_up"]); sh["ffn_down"] = f(inp["ffn_down"])
    sh["conv_w_fm"] = f(np.asarray(inp["ffn_conv_w"]).reshape(DEPTH, 3, 44, 128).transpose(0, 3, 1, 2))
    sh["conv_b_fm"] = f(np.asarray(inp["ffn_conv_b"]).reshape(DEPTH, 44, 128).transpose(0, 2, 1))
    sh.update(nsa_host_consts())
    sh["rel_bias"] = f(inp["rel_bias"])
    sh["nsa_pe_kT"] = f(np.asarray(inp["nsa_pe_k"]).transpose(0, 2, 1))
    sh["nsa_pe_vT"] = f(np.asarray(inp["nsa_pe_v"]).transpose(0, 2, 1))
    for n in ("nsa_ck_w1", "nsa_cv_w1", "nsa_ck_w2", "nsa_cv_w2"):
        sh[n] = f(inp[n])
    sh["fox_b_f"] = f(np.asarray(inp["fox_b_f"]).reshape(DEPTH, 6, 1))
    sh.update(rwkv_host(inp))
    return sh


def prep_core(inp, b):
    d = {}
    d["x"] = np.ascontiguousarray(np.asarray(inp["x"][b], dtype=np.float32))
    d["cT"] = np.ascontiguousarray(np.asarray(inp["c"][b], dtype=np.float32).reshape(8, 128).T)
    return d


def setup_fox(k):
    k.fox_bf = k.inp("fox_b_f", [DEPTH, 6, 1])
    k.CUMA = k.scratch("CUMA", [6, 3, S_LEN], BF16)


def stage_fox(k, l):
    nc = k.nc
    with Scope(k) as sc:
        nb = sc.sb("nb", [128, 32, 6], F32)
        with Scope(k) as s2:
            fl = s2.sb("fl", [6, S_LEN], F32)
            t1 = s2.sb("t1", [6, S_LEN], F32)
            ones = s2.sb("ones", [6, S_LEN], F32)
            cum = s2.sb("cum", [6, S_LEN], F32)
            parts = s2.sb("parts", [6, 3, S_LEN], BF16)
            bfv = s2.sb("bfv", [6, 2], F32)
            psn = s2.ps("psn", [128, 512])
            k.dma('sp', fl[:], k.FL, w=[fl])
            k.dma('sp', bfv[:, 0:1], k.fox_bf[l], w=[bfv])
            k.ts('dve', bfv[:, 1:2], bfv[:, 0:1], -1.0, None, ALU.mult, None, r=[bfv], w=[bfv])
            k.S.op('pool', lambda: nc.gpsimd.memset(ones[:], 1.0), [], [ones])
            k.act(t1[:], fl[:], AF.Exp, r=[fl, bfv], w=[t1], bias=bfv[:, 1:2], scale=-1.0)
            k.act(t1[:], t1[:], AF.Ln, r=[t1], w=[t1], bias=1.0, scale=1.0)
            k.ts('dve', t1[:], t1[:], -1.0, None, ALU.mult, None, r=[t1], w=[t1])
            k.S.op('dve', lambda: nc.vector.tensor_tensor_scan(out=cum[:], data0=ones[:], data1=t1[:], initial=0.0,
                                                               op0=ALU.mult, op1=ALU.add), [ones, t1], [cum])
            for t in range(32):
                k.transpose(psn[:, t * 6:(t + 1) * 6], cum[:, t * 128:(t + 1) * 128], k.ident_f[0:6, 0:6],
                            r=[cum, k.ident_f], w=[psn])
            k.ts('dve', nb[:].rearrange("p t h -> p (t h)"), psn[:, 0:192], -1.0, None, ALU.mult, None, r=[psn], w=[nb])
            k.ts('dve', t1[:], cum[:], 8.0, None, ALU.mult, None, r=[cum], w=[t1])
            k.copy('dve', parts[:, 0, :], t1[:], r=[t1], w=[parts])
            k.tt('dve', t1[:], t1[:], parts[:, 0, :], ALU.subtract, r=[t1, parts], w=[t1])
            k.copy('dve', parts[:, 1, :], t1[:], r=[t1], w=[parts])
            k.tt('dve', t1[:], t1[:], parts[:, 1, :], ALU.subtract, r=[t1, parts], w=[t1])
            k.copy('dve', parts[:, 2, :], t1[:], r=[t1], w=[parts])
            k.dma('sp', k.CUMA, parts[:], r=[parts], w=["CUMA"])
        QA = [sc.sb("QA%d" % i, [128, S_LEN], BF16) for i in range(2)]
        KA = [sc.sb("KA%d" % i, [128, S_LEN], BF16) for i in range(2)]
        VA = sc.sb("VA", [128, 32, 6, 65], BF16)
        yb = sc.sb("yb", [128, 32, 384], BF16)
        PTl = [sc.sb("PTl%d" % i, [128, 512], BF16) for i in range(6)]
        rc = [sc.sb("rc%d" % i, [128, 4], F32) for i in range(2)]
        psS = [sc.ps("psS%d" % i, [128, 512]) for i in range(4)]
        psO = [sc.ps("psO%d" % i, [128, 512]) for i in range(2)]
        k.dma('sp', yb[:], k.VT[:, 0:384].rearrange("(t p) c -> p t c", p=128), w=[yb])
        k.S.op('pool', lambda: nc.gpsimd.memset(VA[:, :, :, 64:65], 1.0), [], [VA])
        k.copy('pool', VA[:, :, :, 0:64], yb[:].rearrange("p t (h d) -> p t h d", h=6), r=[yb], w=[VA])
        for i in range(2):
            k.S.op('dve', lambda i=i: nc.vector.memset(KA[i][64:67, :], 1.0), [], [KA[i]])
        nS = 0
        nO = 0
        nP = 0
        pipe = Pipe(2)
        for h in range(6):
            qa = QA[h % 2]; ka = KA[h % 2]
            k.dma('sp', qa[0:64, :], k.QKT[h * 64:(h + 1) * 64, :], w=[qa])
            k.dma('sp', qa[64:67, :], k.CUMA[h], w=[qa])
            k.dma('sp', ka[0:64, :], k.QKT[384 + h * 64:384 + (h + 1) * 64, :], w=[ka])
            for qb in range(NTB):
                po = psO[nO % 2]; nO += 1
                nkt = 4 * qb + 4
                for kt in range(nkt):
                    j = kt - 4 * qb
                    c0 = max(j, 0) * 128
                    ps = psS[nS % len(psS)]; nS += 1
                    pt = PTl[nP % len(PTl)]; nP += 1

                    def first(ps=ps, pt=pt, kt=kt, c0=c0, j=j, qa=qa, ka=ka, qb=qb, h=h):
                        k.mm(ps[:, c0:512], [(ka[0:67, kt * 128:(kt + 1) * 128], qa[0:67, qb * 512 + c0:(qb + 1) * 512])],
                             r=[ka, qa], w=[ps])
                        k.act(pt[:, c0:512], ps[:, c0:512], AF.Exp, r=[ps, nb], w=[pt], bias=nb[:, kt, h:h + 1], scale=0.125)
                        if j >= 0:
                            k.S.op('pool', lambda: nc.gpsimd.affine_select(
                                out=pt[:, c0:c0 + 128], in_=pt[:, c0:c0 + 128], pattern=[[1, 128]], compare_op=ALU.is_ge,
                                fill=0.0, base=0, channel_multiplier=-1), [pt], [pt])

                    def second(pt=pt, kt=kt, j=j, po=po, qb=qb, h=h, last=(kt == nkt - 1)):
                        fns = []
                        for qs in range(max(j, 0), 4):
                            fns.append(lambda qs=qs: nc.tensor.matmul(
                                po[:, qs * 65:(qs + 1) * 65], lhsT=pt[:, qs * 128:(qs + 1) * 128], rhs=VA[:, kt, h, :],
                                start=(kt == 0 and qs == 0), stop=(kt == 4 * qb + qs), skip_group_check=True))
                        k.S.pe_group(fns, [pt, VA], [po])
                        if last:
                            r_ = rc[qb % 2]
                            pov = po[:, 0:260].rearrange("p (q c) -> p q c", c=65)
                            k.S.op('dve', lambda: nc.vector.reciprocal(out=r_[:], in_=pov[:, :, 64]), [po], [r_])
                            for qs in range(4):
                                k.ts('dve', yb[:, qb * 4 + qs, h * 64:(h + 1) * 64], po[:, qs * 65:qs * 65 + 64], r_[:, qs:qs + 1], None,
                                     ALU.mult, None, r=[po, r_], w=[yb])
                    pipe.push(first, second)
        pipe.flush()
        k.dma('sp', k.Y[:, 256:640].rearrange("(t p) c -> p t c", p=128), yb[:], r=[yb], w=["Y"])


LW = 1536
LC = 4608
NEG8 = -240000.0


def t5_bucket_np(n):
    n = np.maximum(n, 0)
    nf = np.maximum(n, 1).astype(np.float32)
    large = 16 + (np.log(nf / np.float32(16)) / np.float32(np.log(128 / 16)) * np.float32(16)).astype(np.int32)
    large = np.minimum(large, 31)
    return np.where(n < 16, n, large)


def nsa_host_consts():
    c = {}
    i = np.arange(LW); n = i - 511
    oh = np.zeros((33, LW), np.float32)
    ok = (n >= 0) & (n < 512)
    oh[t5_bucket_np(n)[ok], i[ok]] = 1.0
    oh[32, ~ok] = NEG8
    c["oh_w"] = oh
    i = np.arange(LC); n = i - 2063
    oh = np.zeros((33, LC), np.float32)
    ok = n >= 0
    oh[t5_bucket_np(n)[ok], i[ok]] = 1.0
    oh[32, ~ok] = NEG8
    c["oh_c"] = oh
    E = np.zeros((128, 32, 128), np.float32)
    for kt in range(32):
        E[2 * kt, kt, 0:64] = 1.0
        E[2 * kt + 1, kt, 64:128] = 1.0
    c["E_blk"] = E.astype(NPBF)
    cs = np.arange(256) * 16
    ce = cs + 31
    ss = np.arange(64) * 64
    ov = ((cs[:, None] <= ss[None, :] + 63) & (ce[:, None] >= ss[None, :])).astype(np.float32)
    ov[255] = 0.0
    c["ovl"] = np.ascontiguousarray(ov.reshape(2, 128, 64).transpose(1, 0, 2)).astype(NPBF)
    t = np.arange(S_LEN)
    cur = t // 64
    jb = np.arange(64)
    back = cur[:, None] - jb[None, :]
    valid = back >= 0
    forced = (jb[None, :] == 0) | (valid & (back < 2))
    tkm = (valid & ~forced).astype(np.float32)
    tka = np.where(valid, np.where(forced, 1e4, 0.0), -1.0).astype(np.float32)
    c["tkm"] = np.ascontiguousarray(tkm.reshape(32, 128, 64).transpose(1, 0, 2)).astype(NPBF)
    c["tka"] = np.ascontiguousarray(tka.reshape(32, 128, 64).transpose(1, 0, 2)).astype(NPBF)
    return c


def setup_nsa(k):
    nc = k.nc
    k.rel_bias = k.inp("rel_bias", [32, 6])
    k.oh_w = k.inp("oh_w", [33, LW])
    k.oh_c = k.inp("oh_c", [33, LC])
    k.E_d = k.inp("E_blk", [128, 32, 128], BF16)
    k.ovl_d = k.inp("ovl", [128, 2, 64], BF16)
    k.tkm_d = k.inp("tkm", [128, 32, 64], BF16)
    k.tka_d = k.inp("tka", [128, 32, 64], BF16)
    k.pe_kT = k.inp("nsa_pe_kT", [DEPTH, 64, 32])
    k.pe_vT = k.inp("nsa_pe_vT", [DEPTH, 64, 32])
    k.ck_w1 = k.inp("nsa_ck_w1", [DEPTH, 2048, 128])
    k.cv_w1 = k.inp("nsa_cv_w1", [DEPTH, 2048, 128])
    k.ck_w2 = k.inp("nsa_ck_w2", [DEPTH, 128, 64])
    k.cv_w2 = k.inp("nsa_cv_w2", [DEPTH, 128, 64])
    k.WVW = k.scratch("WVW", [6, 128, LW], BF16)
    k.WVC = k.scratch("WVC", [6, 128, LC], BF16)
    with Scope(k) as sc:
        rb = sc.sb("rb", [33, 6], F32)
        rb31 = sc.sb("rb31", [32, 6], F32)
        rrep = sc.sb("rrep", [33, 6, 128], F32)
        ohw = sc.sb("ohw", [33, LW], F32)
        ohc = sc.sb("ohc", [33, LC], F32)
        ps = [sc.ps("psb%d" % i, [128, 512]) for i in range(2)]
        ev = [sc.sb("evb%d" % i, [128, 512], BF16) for i in range(2)]
        k.dma('sp', rb[0:32, :], k.rel_bias, w=[rb])
        k.dma('sp', rb31[:], k.rel_bias[31:32, :].broadcast_to([32, 6]), w=[rb31])
        k.dma('sp', ohw[:], k.oh_w, w=[ohw])
        k.dma('sp', ohc[:], k.oh_c, w=[ohc])
        k.S.op('dve', lambda: nc.vector.memset(rb[32:33, :], 1.0), [], [rb])
        k.tt('dve', rb[0:32, :], rb[0:32, :], rb31[:], ALU.subtract, r=[rb, rb31], w=[rb])
        k.ts('dve', rb[0:32, :], rb[0:32, :], 8.0, None, ALU.mult, None, r=[rb], w=[rb])
        for h in range(6):
            k.copy('dve', rrep[:, h, :], rb[:, h:h + 1].to_broadcast([33, 128]), r=[rb], w=[rrep])
        n = 0
        for h in range(6):
            for (oh, L, dst) in ((ohw, LW, k.WVW), (ohc, LC, k.WVC)):
                for c0 in range(0, L, 512):
                    p_ = ps[n % 2]; e_ = ev[n % 2]; n += 1
                    k.mm(p_[:], [(rrep[:, h, :], oh[:, c0:c0 + 512])], r=[rrep, oh], w=[p_])
                    k.copy('act' if n % 2 else 'dve', e_[:], p_[:], r=[p_], w=[e_])
                    k.dma('sp', dst[h, :, c0:c0 + 512], e_[:], r=[e_])


class DbgStop(Exception):
    pass


def dbg(k, lvl):
    if getattr(k, 'dbg_stop', None) == lvl:
        raise DbgStop()


def stage_nsa(k, l):
    nc = k.nc
    with Scope(k) as sc:
        Gw = sc.sb("Gw", [128, 6, 1408], BF16)
        Gc = sc.sb("Gc", [128, 6, 2560], BF16)
        E = sc.sb("E", [128, 32, 128], BF16)
        tkm = sc.sb("tkm", [128, 32, 64], BF16)
        tka = sc.sb("tka", [128, 32, 64], BF16)
        QC = [sc.sb("QC%d" % c, [128, S_LEN], BF16) for c in range(3)]
        KS = sc.sb("KS", [128, S_LEN], BF16)
        KW = sc.sb("KW", [128, S_LEN], BF16)
        VS = sc.sb("VS", [128, 32, 2, 65], BF16)
        VW = sc.sb("VW", [128, 32, 2, 65], BF16)
        KCMP = sc.sb("KCMP", [128, 256], BF16)
        VE = sc.sb("VE", [128, 2, 2, 129], BF16)
        sg = sc.sb("sg", [128, 32, 18], F32)
        for h in range(6):
            k.dma('sp', Gw[:, h, :], bass.AP(k.WVW.tensor, h * 128 * LW + 127, [[LW - 1, 128], [1, 1408]]), w=[Gw])
            k.dma('sp', Gc[:, h, :], bass.AP(k.WVC.tensor, h * 128 * LC + 2032, [[LC - 16, 128], [1, 2560]]), w=[Gc])
        k.dma('sp', E[:], k.E_d, w=[E])
        k.dma('sp', tkm[:], k.tkm_d, w=[tkm])
        k.dma('sp', tka[:], k.tka_d, w=[tka])
        for c in range(3):
            k.dma('sp', QC[c][:], k.QKT[768 + c * 128:768 + (c + 1) * 128, :], w=[QC[c]])
        k.dma('sp', KS[:], k.QKT[1408:1536, :], w=[KS])
        k.dma('sp', KW[:], k.QKT[1536:1664, :], w=[KW])
        k.dma('sp', sg[:], k.GT.rearrange("(t p) c -> p t c", p=128), w=[sg])
        k.act(sg[:], sg[:], AF.Exp, r=[sg], w=[sg], scale=-1.0)
        k.ts('dve', sg[:], sg[:], 1.0, None, ALU.add, None, r=[sg], w=[sg])
        k.S.op('dve', lambda: nc.vector.reciprocal(out=sg[:], in_=sg[:]), [sg], [sg])
        k.dma('sp', VE[:, 0, :, 65:129], k.ovl_d, w=[VE])
        k.dma('sp', VE[:, 1, :, 65:129], k.ovl_d, w=[VE])
        k.S.op('pool', lambda: nc.gpsimd.memset(VE[:, :, :, 64:65], 1.0), [], [VE])
        k.S.op('pool', lambda: nc.gpsimd.memset(VE[:, :, :, 0:64], 0.0), [], [VE])
        k.S.op('pool', lambda: nc.gpsimd.memset(KCMP[:], 0.0), [], [KCMP])
        dbg(k, 1)
        with Scope(k) as s2:
            vst = s2.sb("vst", [128, 32, 256], BF16)
            k.dma('sp', vst[:], k.VT[:, 384:640].rearrange("(t p) c -> p t c", p=128), w=[vst])
            k.S.op('pool', lambda: nc.gpsimd.memset(VS[:, :, :, 64:65], 1.0), [], [VS])
            k.S.op('pool', lambda: nc.gpsimd.memset(VW[:, :, :, 64:65], 1.0), [], [VW])
            k.copy('pool', VS[:, :, :, 0:64], vst[:, :, 0:128].rearrange("p t (g d) -> p t g d", g=2), r=[vst], w=[VS])
            k.copy('pool', VW[:, :, :, 0:64], vst[:, :, 128:256].rearrange("p t (g d) -> p t g d", g=2), r=[vst], w=[VW])
        dbg(k, 2)
        with Scope(k) as s2:
            KC = s2.sb("KC", [128, S_LEN], BF16)
            VC = s2.sb("VC", [128, S_LEN], BF16)
            k.dma('sp', KC[:], k.QKT[1152:1280, :], w=[KC])
            k.dma('sp', VC[:], k.QKT[1280:1408, :], w=[VC])
            w1s = s2.sb("w1s", [128, 16, 128], F32)
            w1b = [s2.sb("w1b%d" % i, [128, 32, 128], BF16) for i in range(2)]
            w2s = s2.sb("w2s", [128, 2, 64], F32)
            w2b = s2.sb("w2b", [128, 2, 64], BF16)
            pes = s2.sb("pes", [128, 2, 32], F32)
            peb = s2.sb("peb", [128, 2, 32], BF16)
            hb = s2.sb("hb", [128, 2], F32)
            gx = s2.sb("gx", [128, 256], F32)
            gu = s2.sb("gu", [128, 256], F32)
            gg = s2.sb("gg", [128, 256], BF16)
            psh = s2.ps("psh", [128, 512])
            psb_ = s2.ps("pshb", [128, 512])
            pso = s2.ps("pso", [128, 512])
            for kv, (w1d, w2d, ped) in enumerate(((k.ck_w1, k.ck_w2, k.pe_kT), (k.cv_w1, k.cv_w2, k.pe_vT))):
                for lh in range(2):
                    for half in range(2):
                        k.dma('sp', w1s[half * 64:(half + 1) * 64, :, :],
                              w1d[l, lh * 1024:(lh + 1) * 1024, :].rearrange("(l d) h -> d l h", d=64), w=[w1s])
                    k.copy('pool', w1b[kv][:, lh * 16:(lh + 1) * 16, :], w1s[:], r=[w1s], w=[w1b[kv]])
                for half in range(2):
                    k.dma('sp', pes[half * 64:(half + 1) * 64, kv, :], ped[l], w=[pes])
                k.dma('sp', w2s[:, kv, :], w2d[l], w=[w2s])
            k.copy('dve', w2b[:], w2s[:], r=[w2s], w=[w2b])
            w2kd = s2.sb("w2kd", [128, 2, 64], BF16)
            for a_ in range(2):
                k.copy('dve', w2kd[:, a_, :], w2s[:, 0, :], r=[w2s], w=[w2kd])
            k.copy('dve', peb[:], pes[:], r=[pes], w=[peb])
            for kv in range(2):
                src = KC if kv == 0 else VC
                k.mm(psb_[:, kv:kv + 1], [(w1b[kv][0:64, li, :], peb[0:64, kv, li:li + 1]) for li in range(32)],
                     r=[w1b[kv], peb], w=[psb_], start=True)
                k.copy('dve', hb[:, kv:kv + 1], psb_[:, kv:kv + 1], r=[psb_], w=[hb])
                for g in range(2):
                    pr = slice(g * 64, (g + 1) * 64)
                    k.mm(psh[:, 0:255], [(w1b[kv][pr, li, :], src[pr, li:li + 16 * 254 + 1:16]) for li in range(32)],
                         r=[w1b[kv], src], w=[psh])
                    k.ts('dve', gx[:, 0:255], psh[:, 0:255], hb[:, kv:kv + 1], None, ALU.add, None, r=[psh, hb], w=[gx])
                    k.tt('dve', gu[:, 0:255], gx[:, 0:255], gx[:, 0:255], ALU.mult, r=[gx], w=[gu])
                    k.ts('dve', gu[:, 0:255], gu[:, 0:255], 0.044715, 1.0, ALU.mult, ALU.add, r=[gu], w=[gu])
                    k.tt('dve', gu[:, 0:255], gu[:, 0:255], gx[:, 0:255], ALU.mult, r=[gu, gx], w=[gu])
                    k.act(gu[:, 0:255], gu[:, 0:255], AF.Exp, r=[gu], w=[gu], scale=-2.0 * 0.7978845608028654)
                    k.ts('dve', gu[:, 0:255], gu[:, 0:255], 1.0, None, ALU.add, None, r=[gu], w=[gu])
                    k.S.op('dve', lambda: nc.vector.reciprocal(out=gu[:, 0:255], in_=gu[:, 0:255]), [gu], [gu])
                    k.S.op('dve', lambda: nc.vector.memset(gg[:, 255:256], 0.0), [], [gg])
                    k.tt('dve', gg[:, 0:255], gu[:, 0:255], gx[:, 0:255], ALU.mult, r=[gu, gx], w=[gg])
                    if kv == 0:
                        k.mm(pso[:, 0:256], [(w2kd[:].rearrange("p a d -> p (a d)"), gg[:, 0:256])], r=[w2kd, gg], w=[pso])
                        k.copy('dve', KCMP[pr, :], pso[pr, 0:256], r=[pso], w=[KCMP])
                    else:
                        for ct in range(2):
                            k.mm(pso[:, ct * 64:(ct + 1) * 64], [(gg[:, ct * 128:(ct + 1) * 128], w2b[:, 1, :])],
                                 r=[w2b, gg], w=[pso], start=(ct == 0))
                        k.copy('dve', VE[:, g, :, 0:64], pso[:, 0:128].rearrange("p (c d) -> p c d", c=2), r=[pso], w=[VE])
        dbg(k, 3)
        NM = [sc.sb("NM%d" % g, [128, 512], BF16) for g in range(2)]
        for g in range(2):
            k.S.op('pool', lambda g=g: nc.gpsimd.memset(NM[g][:], 0.0), [], [NM[g]])
        PTl = [sc.sb("PTn%d" % i, [128, 512], BF16) for i in range(6)]
        yacc = [sc.sb("yacc%d" % i, [128, 4, 384], F32) for i in range(2)]
        ybf = [sc.sb("ybf%d" % i, [128, 4, 384], BF16) for i in range(2)]
        impt = [sc.sb("impt%d" % g, [128, 4, 64], F32) for g in range(2)]
        scr = sc.sb("scr", [128, 4, 64], F32)
        wk = sc.sb("wk", [128, 4, 64], F32)
        m8 = sc.sb("m8", [128, 4, 16], F32)
        nmq = sc.sb("nmq", [128, 4, 64], BF16)
        rcs = [sc.sb("rcs%d" % i, [128, 8], F32) for i in range(3)]
        psS = [sc.ps("psS%d" % i, [128, 512]) for i in range(4)]
        psO = [sc.ps("psO%d" % i, [128, 512]) for i in range(3)]
        psT = sc.ps("psTn", [128, 1024], BF16)
        st = {"S": 0, "O": 0, "P": 0, "R": 0}

        def q_ap(h, c0, c1):
            g, hp = h // 3, h % 3
            return QC[hp][g * 64:(g + 1) * 64, c0:c1]

        def evac(views, h, branch, qb, ya, first):
            r_ = rcs[st["R"] % 3]; st["R"] += 1
            for qs, (po, cb) in enumerate(views):
                if branch == 0:
                    k.ts('dve', r_[:, qs:qs + 1], po[:, cb + 64:cb + 65], 1e-30, None, ALU.max, None, r=[po], w=[r_])
                    k.S.op('dve', lambda r_=r_, qs=qs: nc.vector.reciprocal(out=r_[:, qs:qs + 1], in_=r_[:, qs:qs + 1]), [r_], [r_])
                else:
                    k.S.op('dve', lambda r_=r_, po=po, cb=cb, qs=qs: nc.vector.reciprocal(out=r_[:, qs:qs + 1], in_=po[:, cb + 64:cb + 65]), [po], [r_])
            k.tt('dve', r_[:, 4:8], r_[:, 0:4], sg[:, qb * 4:(qb + 1) * 4, h * 3 + branch], ALU.mult, r=[r_, sg], w=[r_])
            for qs, (po, cb) in enumerate(views):
                o = ya[:, qs, h * 64:(h + 1) * 64]
                if first:
                    k.ts('dve', o, po[:, cb:cb + 64], r_[:, 4 + qs:5 + qs], None, ALU.mult, None, r=[po, r_], w=[ya])
                else:
                    k.stt('dve', o, po[:, cb:cb + 64], r_[:, 4 + qs:5 + qs], o, ALU.mult, ALU.add, r=[po, r_, ya], w=[ya])
            return r_

        pipe = Pipe(2)

        def attend(h, qb, tiles, kmat, vmat, po, g, branch, ya):
            hp = h % 3
            nt = len(tiles)
            state = {"first": True}
            for idx, (kt, c0, c1, extra) in enumerate(tiles):
                ps = psS[st["S"] % len(psS)]; st["S"] += 1
                pt = PTl[st["P"] % len(PTl)]; st["P"] += 1

                def first(ps=ps, pt=pt, kt=kt, c0=c0, c1=c1, extra=extra):
                    fns = [lambda: nc.tensor.matmul(ps[:, c0:c1], lhsT=kmat[g * 64:(g + 1) * 64, kt * 128:(kt + 1) * 128],
                                                    rhs=QC[hp][g * 64:(g + 1) * 64, qb * 512 + c0:qb * 512 + c1],
                                                    start=True, stop=(len(extra) == 0), skip_group_check=True)]
                    rd = [kmat, QC[hp]]
                    for ei, (lt, rt, lap, rap) in enumerate(extra):
                        w_ = rap.shape[-1]
                        fns.append(lambda lap=lap, rap=rap, w_=w_, ei=ei: nc.tensor.matmul(
                            ps[:, c0:c0 + w_], lhsT=lap, rhs=rap, start=False, stop=(ei == len(extra) - 1), skip_group_check=True))
                        rd += [lt, rt]
                    k.S.pe_group(fns, rd, [ps])
                    k.act(pt[:, c0:c1], ps[:, c0:c1], AF.Exp, r=[ps], w=[pt], scale=0.125)

                def second(pt=pt, kt=kt, c0=c0, c1=c1, idx=idx):
                    fns = []
                    for qs in range(c0 // 128, (c1 + 127) // 128):
                        last = all(not (t2[1] <= qs * 128 < t2[2]) for t2 in tiles[idx + 1:])
                        fo = state["first"]
                        state["first"] = False
                        fns.append(lambda qs=qs, fo=fo, last=last: nc.tensor.matmul(
                            po[:, qs * 65:(qs + 1) * 65], lhsT=pt[:, qs * 128:(qs + 1) * 128], rhs=vmat[:, kt, g, :],
                            start=fo, stop=last, skip_group_check=True))
                    k.S.pe_group(fns, [pt, vmat], [po])
                    if idx == nt - 1:
                        evac([(po, qs * 65) for qs in range(4)], h, branch, qb, ya, False)
                pipe.push(first, second)

        for qb in getattr(k, 'dbg_qbs', range(NTB)):
            ya = yacc[qb % 2]
            for h in range(6):
                g = h // 3
                poA = psO[st["O"] % 3]; st["O"] += 1
                poB = psO[st["O"] % 3]; st["O"] += 1
                cts = [0] + ([1] if qb >= 4 else [])
                state = {"A": True, "B": True}
                for ct in cts:
                    delta = 512 * qb - 2048 * ct
                    ps = psS[st["S"] % len(psS)]; st["S"] += 1
                    pt = PTl[st["P"] % len(PTl)]; st["P"] += 1

                    def first(ps=ps, pt=pt, ct=ct, delta=delta, g=g, h=h):
                        pairs = [(KCMP[g * 64:(g + 1) * 64, ct * 128:(ct + 1) * 128], q_ap(h, qb * 512, (qb + 1) * 512))]
                        rd = [KCMP, QC[h % 3]]
                        if delta < 2560:
                            pairs.append((k.ident_bf[:], Gc[:, h, delta:delta + 512])); rd += [k.ident_bf, Gc]
                        k.mm(ps[:], pairs, r=rd, w=[ps])
                        k.act(pt[:], ps[:], AF.Exp, r=[ps], w=[pt], scale=0.125)

                    def second(pt=pt, ct=ct, g=g, h=h, poA=poA, poB=poB, state=state, lastct=(ct == cts[-1])):
                        fns = []
                        for qs in range(4):
                            po, cb = (poA, qs * 129) if qs < 3 else (poB, 0)
                            key = "A" if qs < 3 else "B"
                            stt_ = state[key]
                            state[key] = False
                            fns.append(lambda qs=qs, po=po, cb=cb, stt_=stt_: nc.tensor.matmul(
                                po[:, cb:cb + 129], lhsT=pt[:, qs * 128:(qs + 1) * 128], rhs=VE[:, g, ct, :],
                                start=stt_, stop=lastct, skip_group_check=True))
                        k.S.pe_group(fns, [pt, VE], [poA, poB])
                        if lastct:
                            views = [(poA, 0), (poA, 129), (poA, 258), (poB, 0)]
                            r_ = evac(views, h, 0, qb, ya, True)
                            for qs, (po, cb) in enumerate(views):
                                o = impt[g][:, qs, :]
                                if h % 3 == 0:
                                    k.ts('dve', o, po[:, cb + 65:cb + 129], r_[:, qs:qs + 1], None, ALU.mult, None, r=[po, r_], w=[impt[g]])
                                else:
                                    k.stt('dve', o, po[:, cb + 65:cb + 129], r_[:, qs:qs + 1], o, ALU.mult, ALU.add, r=[po, r_, impt[g]], w=[impt[g]])
                    pipe.push(first, second)
            pipe.flush()
            dbg(k, 4)
            for g in range(2):
                k.tt('dve', scr[:], impt[g][:], tkm[:, qb * 4:(qb + 1) * 4, :], ALU.mult, r=[impt[g], tkm], w=[scr])
                k.tt('dve', scr[:], scr[:], tka[:, qb * 4:(qb + 1) * 4, :], ALU.add, r=[scr, tka], w=[scr])
                for qs in range(4):
                    k.S.op('dve', lambda qs=qs: nc.vector.max(out=m8[:, qs, 0:8], in_=scr[:, qs, :]), [scr], [m8])
                    k.S.op('dve', lambda qs=qs: nc.vector.match_replace(out=wk[:, qs, :], in_to_replace=m8[:, qs, 0:8],
                                                                        in_values=scr[:, qs, :], imm_value=-1e9), [scr, m8], [wk])
                    k.S.op('dve', lambda qs=qs: nc.vector.max(out=m8[:, qs, 8:16], in_=wk[:, qs, :]), [wk], [m8])
                    k.ts('dve', wk[:, qs, :], scr[:, qs, :], m8[:, qs, 15:16], 1.0, ALU.is_ge, ALU.subtract, r=[scr, m8, wk], w=[wk])
                k.ts('dve', nmq[:], wk[:], -NEG8, None, ALU.mult, None, r=[wk], w=[nmq])
                for qs in range(4):
                    k.transpose(psT[0:64, qs * 128:(qs + 1) * 128], nmq[:, qs, :], k.ident_bf[:], r=[nmq, k.ident_bf], w=[psT])
                k.copy('dve', NM[g][0:64, :], psT[0:64, 0:512], r=[psT], w=[NM[g]])
            for h in range(6):
                g = h // 3
                po = psO[st["O"] % 3]; st["O"] += 1
                tiles = []
                for kt in range(max(0, 4 * qb - 4), 4 * qb + 4):
                    delta = 512 * qb - 128 * kt
                    c0 = max(-delta, 0)
                    c1 = min(512, 640 - delta) if delta > 0 else 512
                    tiles.append((kt, c0, c1, [(k.ident_bf, Gw, k.ident_bf[:], Gw[:, h, delta + 384 + c0:delta + 384 + c1])]))
                attend(h, qb, tiles, KW, VW, po, g, 2, ya)
            for h in range(6):
                g = h // 3
                po = psO[st["O"] % 3]; st["O"] += 1
                tiles = []
                for kt in range(0, 4 * qb + 4):
                    delta = 512 * qb - 128 * kt
                    c0 = max(-delta, 0)
                    ex = [(E, NM[g], E[:, kt, :], NM[g][:, c0:512])]
                    if delta <= 128:
                        c1b = 256 if delta == 128 else 512
                        ex.append((k.ident_bf, Gw, k.ident_bf[:], Gw[:, h, delta + 384 + c0:delta + 384 + c1b]))
                    tiles.append((kt, c0, 512, ex))
                attend(h, qb, tiles, KS, VS, po, g, 1, ya)
            pipe.flush()
            yb_ = ybf[qb % 2]
            k.copy('pool', yb_[:], ya[:], r=[ya], w=[yb_])
            k.dma('sp', k.Y[qb * 512:(qb + 1) * 512, 640:1024].rearrange("(q p) c -> p q c", p=128), yb_[:], r=[yb_])
            dbg(k, 100 + qb)


def setup_ffn(k):
    k.w_out = k.inp("w_out", [DEPTH, D, D])
    k.ffn_up = k.inp("ffn_up", [DEPTH, D, 2 * D_FF])
    k.ffn_down = k.inp("ffn_down", [DEPTH, D_FF, D])
    k.conv_w = k.inp("conv_w_fm", [DEPTH, 128, 3, 44])
    k.conv_b = k.inp("conv_b_fm", [DEPTH, 128, 44])


def load_cast(k, sc, dst, src_rows, ncols, nchunks, name, col_split=1):
    w = ncols // col_split
    stg = [sc.sb("%s_stg%d" % (name, i), [128, w], F32) for i in range(2)]
    n = 0
    for c in range(nchunks):
        for cs in range(col_split):
            s = stg[n % 2]
            k.dma('sp', s[:], src_rows(c)[:, cs * w:(cs + 1) * w], w=[s])
            k.copy('pool' if n % 2 == 0 else 'dve', dst[:, c, cs * w:(cs + 1) * w], s[:], r=[s], w=[dst])
            n += 1


def rms_scale(k, ss, st):
    k.act(st[:, 0:1], ss, AF.Ln, r=[st], w=[st], bias=RMS_EPS)
    k.act(st[:, 1:2], st[:, 0:1], AF.Exp, r=[st], w=[st], scale=-0.5)


def stage_out(k, l, xsrc, xdst):
    nc = k.nc
    with Scope(k) as sc:
        wo = sc.sb("wo", [128, 8, D], BF16)
        with Scope(k) as s2:
            load_cast(k, s2, wo, lambda c: k.w_out[l, c * 128:(c + 1) * 128, :], D, 8, "wo")
        yt = [sc.sb("yt%d" % i, [128, D], BF16) for i in range(2)]
        yT = [sc.sb("yT%d" % i, [128, 8, 128], BF16) for i in range(2)]
        xt = [sc.sb("xo%d" % i, [128, D], F32) for i in range(2)]
        tt_ = [sc.sb("to%d" % i, [128, D], F32) for i in range(2)]
        junk = sc.sb("junko", [128, 512], BF16)
        st = [sc.sb("sto%d" % i, [128, 4], F32) for i in range(2)]
        psT = [sc.ps("psTo%d" % i, [128, D], BF16) for i in range(2)]
        psY = [sc.ps("psYo%d" % i, [128, 512]) for i in range(4)]
        def T(ti):
            tok = slice(ti * 128, (ti + 1) * 128)
            y_ = yt[ti % 2]; yT_ = yT[ti % 2]; x_ = xt[ti % 2]; pT = psT[ti % 2]
            k.dma('act', y_[:], k.Y[tok, :], w=[y_])
            k.dma('act', x_[:], xsrc[tok, :], w=[x_])
            for kc in range(8):
                k.transpose(pT[:, kc * 128:(kc + 1) * 128], y_[:, kc * 128:(kc + 1) * 128], k.ident_bf[:], r=[y_, k.ident_bf], w=[pT])
            k.copy('act' if ti % 2 else 'dve', yT_[:].rearrange("p a b -> p (a b)"), pT[:], r=[pT], w=[yT_])

        def M(ti):
            tok = slice(ti * 128, (ti + 1) * 128)
            yT_ = yT[ti % 2]; x_ = xt[ti % 2]; t_ = tt_[ti % 2]; st_ = st[ti % 2]
            p0 = psY[(ti % 2) * 2]; p1 = psY[(ti % 2) * 2 + 1]
            for half, ps in enumerate((p0, p1)):
                k.mm(ps[:], [(yT_[:, kc, :], wo[:, kc, half * 512:(half + 1) * 512]) for kc in range(8)], r=[yT_, wo], w=[ps])
                k.act(junk[:], ps[:], AF.Square, r=[ps], w=[junk, st_], scale=1.0 / 32.0, accum=st_[:, 2 + half:3 + half])
            k.tt('dve', st_[:, 2:3], st_[:, 2:3], st_[:, 3:4], ALU.add, r=[st_], w=[st_])
            rms_scale(k, st_[:, 2:3], st_)
            for half, ps in enumerate((p0, p1)):
                cs = slice(half * 512, (half + 1) * 512)
                k.stt('dve', t_[:, cs], ps[:], st_[:, 1:2], k.gm_row[:, cs], ALU.mult, ALU.mult, r=[ps, st_, k.gm_row], w=[t_])
            k.tt('pool', t_[:], t_[:], x_[:], ALU.add, r=[t_, x_], w=[t_])
            k.dma('sp', xdst[tok, :], t_[:], r=[t_])

        T(0)
        for ti in range(32):
            if ti + 1 < 32:
                T(ti + 1)
            M(ti)


def stage_ffn(k, l, xsrc, xdst):
    nc = k.nc
    NCH = 22
    with Scope(k) as sc:
        wu = sc.sb("wu", [128, 8, 2 * D_FF], BF16)
        wd = sc.sb("wd", [128, NCH, D], BF16)
        with Scope(k) as s2:
            load_cast(k, s2, wu, lambda c: k.ffn_up[l, c * 128:(c + 1) * 128, :], 2 * D_FF, 8, "wu", col_split=2)
            load_cast(k, s2, wd, lambda c: k.ffn_down[l, c * 128:(c + 1) * 128, :], D, NCH, "wd")
        cw = sc.sb("cw", [128, 3, 44], F32)
        cb = sc.sb("cb", [128, 44], F32)
        hal = [sc.sb("hal%d" % i, [128, 44, 2], F32) for i in range(2)]
        k.dma('sp', cw[:], k.conv_w[l], w=[cw])
        k.dma('sp', cb[:], k.conv_b[l], w=[cb])
        k.S.op('pool', lambda: nc.gpsimd.memset(hal[1][:], 0.0), [], [hal[1]])
        actT = sc.sb("actT", [128, NCH, 512], BF16)
        HTs = [sc.sb("H2T%d" % i, [128, 8, 512], BF16) for i in range(2)]
        xt = [sc.sb("xf%d" % i, [128, D], F32) for i in range(2)]
        xn = sc.sb("xnf", [128, D], BF16)
        junk = sc.sb("junkf", [128, 512], BF16)
        st = [sc.sb("stf%d" % i, [128, 4], F32) for i in range(2)]
        Tg = [sc.sb("Tg%d" % i, [128, 512], F32) for i in range(2)]
        Tv = [sc.sb("Tv%d" % i, [128, 512], F32) for i in range(2)]
        psT = sc.ps("psTf", [128, D], BF16)
        psU = [sc.ps("psU%d" % i, [128, 512]) for i in range(4)]
        psF = [sc.ps("psF%d" % i, [128, 512]) for i in range(2)]
        nxc = {"n": 0}

        def norm_gen(tb):
            HT = HTs[tb % 2]
            t0_ = tb * 4
            k.dma('act', xt[t0_ % 2][:], xsrc[t0_ * 128:(t0_ + 1) * 128, :], w=[xt[t0_ % 2]])
            yield
            for sub in range(4):
                ti = tb * 4 + sub
                x_ = xt[ti % 2]; st_ = st[ti % 2]
                k.act(xn[:], x_[:], AF.Square, r=[x_], w=[xn, st_], scale=1.0 / 32.0, accum=st_[:, 2:3])
                rms_scale(k, st_[:, 2:3], st_)
                k.ts('dve', xn[:], x_[:], st_[:, 1:2], None, ALU.mult, None, r=[x_, st_], w=[xn])
                if sub < 3:
                    k.dma('act', xt[(ti + 1) % 2][:], xsrc[(ti + 1) * 128:(ti + 2) * 128, :], w=[xt[(ti + 1) % 2]])
                yield
                yield
                for kc in range(8):
                    k.transpose(psT[:, kc * 128:(kc + 1) * 128], xn[:, kc * 128:(kc + 1) * 128], k.ident_bf[:], r=[xn, k.ident_bf], w=[psT])
                    if kc == 3:
                        yield
                yield
                for kc in range(8):
                    o = HT[:, kc, sub * 128:(sub + 1) * 128]
                    i_ = psT[:, kc * 128:(kc + 1) * 128]
                    if kc % 2 == 0:
                        k.ts('dve', o, i_, k.modAB[:, 16 + kc:17 + kc], k.modAB[:, 24 + kc:25 + kc], ALU.mult, ALU.add, r=[psT, k.modAB], w=[HT])
                    else:
                        k.act(o, i_, AF.Identity, r=[psT, k.modAB], w=[HT], scale=k.modAB[:, 16 + kc:17 + kc], bias=k.modAB[:, 24 + kc:25 + kc])
                yield

        def step(g):
            if g is not None:
                try:
                    next(g)
                except StopIteration:
                    return None
            return g

        xe = [sc.sb("xe%d" % i, [128, D], F32) for i in range(2)]
        ste = [sc.sb("ste%d" % i, [128, 4], F32) for i in range(2)]
        for _ in norm_gen(0):
            pass
        for tb in range(NTB):
            hin = hal[(tb + 1) % 2]; hout = hal[tb % 2]
            HT = HTs[tb % 2]
            g = norm_gen(tb + 1) if tb + 1 < NTB else None
            for cp in range(NCH):
                tg = Tg[cp % 2]; tv = Tv[cp % 2]
                for which, (T_, c_) in enumerate(((tg, cp), (tv, NCH + cp))):
                    ps = psU[(cp * 2 + which) % 4]
                    k.mm(ps[:], [(wu[:, kc, c_ * 128:(c_ + 1) * 128], HT[:, kc, :]) for kc in range(8)], r=[wu, HT], w=[ps])
                    k.act(T_[:], ps[:], AF.Identity, r=[ps, cw, cb], w=[T_], scale=cw[:, 2, c_:c_ + 1], bias=cb[:, c_:c_ + 1])
                    k.stt('dve', T_[:, 1:512], ps[:, 0:511], cw[:, 1, c_:c_ + 1], T_[:, 1:512], ALU.mult, ALU.add, r=[ps, cw, T_], w=[T_])
                    k.stt('dve', T_[:, 2:512], ps[:, 0:510], cw[:, 0, c_:c_ + 1], T_[:, 2:512], ALU.mult, ALU.add, r=[ps, cw, T_], w=[T_])
                    k.copy('act', hout[:, c_, :], ps[:, 510:512], r=[ps], w=[hout])
                    k.stt('dve', T_[:, 0:1], hin[:, c_, 1:2], cw[:, 1, c_:c_ + 1], T_[:, 0:1], ALU.mult, ALU.add, r=[hin, cw, T_], w=[T_])
                    k.stt('dve', T_[:, 0:2], hin[:, c_, 0:2], cw[:, 0, c_:c_ + 1], T_[:, 0:2], ALU.mult, ALU.add, r=[hin, cw, T_], w=[T_])
                k.act(tg[:], tg[:], AF.Silu, r=[tg], w=[tg])
                k.tt('pool', actT[:, cp, :], tg[:], tv[:], ALU.mult, r=[tg, tv], w=[actT])
                if cp >= 2:
                    g = step(g)
            for sub in range(4):
                ti = tb * 4 + sub
                tok = slice(ti * 128, (ti + 1) * 128)
                x_ = xe[sub % 2]; st_ = ste[sub % 2]
                k.dma('act', x_[:], xsrc[tok, :], w=[x_])
                for half in range(2):
                    ps = psF[half]
                    k.mm(ps[:], [(actT[:, cp, sub * 128:(sub + 1) * 128], wd[:, cp, half * 512:(half + 1) * 512]) for cp in range(NCH)],
                         r=[actT, wd], w=[ps])
                    k.act(junk[:, 0:512], ps[:], AF.Square, r=[ps], w=[junk, st_], scale=1.0 / 32.0, accum=st_[:, 2 + half:3 + half])
                k.tt('dve', st_[:, 2:3], st_[:, 2:3], st_[:, 3:4], ALU.add, r=[st_], w=[st_])
                rms_scale(k, st_[:, 2:3], st_)
                t_ = Tg[sub % 2] if False else None
                for half in range(2):
                    cs = slice(half * 512, (half + 1) * 512)
                    T_ = (Tg if half == 0 else Tv)[sub % 2]
                    k.stt('dve', T_[:], psF[half][:], st_[:, 1:2], k.gf_row[:, cs], ALU.mult, ALU.mult, r=[psF[half], st_, k.gf_row], w=[T_])
                    k.tt('pool', x_[:, cs], x_[:, cs], T_[:], ALU.add, r=[x_, T_], w=[x_])
                k.dma('sp', xdst[tok, :], x_[:], r=[x_])
                g = step(g)
            while g is not None:
                g = step(g)


def rwkv_host(inp):
    f = lambda a: np.ascontiguousarray(np.asarray(a, dtype=np.float32))
    mu = np.asarray(inp["rwkv_mu"])
    hd = lambda v: np.asarray(v).reshape(DEPTH, 4, 64).transpose(0, 2, 1)
    pp = np.stack([hd(mu[:, 0:256]), hd(mu[:, 256:512]), hd(mu[:, 512:768]), hd(inp["rwkv_w0"]), hd(inp["rwkv_a0"]),
                   hd(inp["rwkv_k_k"]), hd(inp["rwkv_k_a"]), hd(np.asarray(inp["rwkv_r_k"]).reshape(DEPTH, 256))], axis=2)
    lr = np.zeros((DEPTH, 64, 3), np.float32)
    lr[:, 0:32, 0] = mu[:, 768:800]; lr[:, 0:32, 1] = mu[:, 800:832]; lr[:, :, 2] = mu[:, 832:896]
    i = np.arange(64)
    mk = np.stack([(i[:, None] < i[None, :]), (i[:, None] > i[None, :]), (i[:, None] <= i[None, :]), np.eye(64, dtype=bool)]).astype(np.float32)
    cm = np.ones((64, 512), np.float32); cm[:, ::64] = 0.0
    return {"rwkv_pp": f(pp), "rwkv_lr": f(lr), "rwkv_w_up": f(inp["rwkv_w_up"]), "rwkv_a_up": f(inp["rwkv_a_up"]),
            "rwkv_g_up": f(inp["rwkv_g_up"]), "rwkv_ln": f(np.stack([np.asarray(inp["rwkv_ln_w"]), np.asarray(inp["rwkv_ln_b"])], axis=1)),
            "rwkv_masks": f(mk.transpose(1, 0, 2)), "rwkv_cmask": cm}


def setup_rwkv(k):
    k.rw_pp = k.inp("rwkv_pp", [DEPTH, 64, 8, 4])
    k.rw_lr = k.inp("rwkv_lr", [DEPTH, 64, 3])
    k.rw_wup = k.inp("rwkv_w_up", [DEPTH, 32, 256])
    k.rw_aup = k.inp("rwkv_a_up", [DEPTH, 32, 256])
    k.rw_gup = k.inp("rwkv_g_up", [DEPTH, 64, 256])
    k.rw_ln = k.inp("rwkv_ln", [DEPTH, 2, 256])
    k.rw_masks = k.inp("rwkv_masks", [64, 4, 64])
    k.rw_cmask = k.inp("rwkv_cmask", [64, 512])


def stage_rwkv(k, l):
    nc = k.nc
    BL = 256
    NB = S_LEN // BL
    CPB = BL // 64
    H4 = [64, 4, BL]
    bc = lambda ap, shape: ap.to_broadcast(shape)
    with Scope(k) as sc:
        pp = sc.sb("pp", [64, 8, 4], F32)
        lr = sc.sb("lr", [64, 3], F32)
        wup = sc.sb("wup", [32, 256], F32); aup = sc.sb("aup", [32, 256], F32); gup = sc.sb("gup", [64, 256], F32)
        lnr = sc.sb("lnr", [64, 2, 256], F32)
        mk = sc.sb("mk", [64, 4, 64], F32)
        cmask = sc.sb("cmask", [64, BL], F32)
        ones = sc.sb("ones64", [64, 64], F32)
        prm = sc.sb("prm", [64, 4, 4], F32)
        k.dma('sp', pp[:], k.rw_pp[l], w=[pp]); k.dma('sp', lr[:], k.rw_lr[l], w=[lr])
        k.dma('sp', wup[:], k.rw_wup[l], w=[wup]); k.dma('sp', aup[:], k.rw_aup[l], w=[aup]); k.dma('sp', gup[:], k.rw_gup[l], w=[gup])
        for i in range(2):
            k.dma('sp', lnr[:, i, :], k.rw_ln[l, i:i + 1, :].broadcast_to([64, 256]), w=[lnr])
        k.dma('sp', mk[:], k.rw_masks, w=[mk]); k.dma('sp', cmask[:], k.rw_cmask[:, 0:BL], w=[cmask])
        k.S.op('pool', lambda: nc.gpsimd.memset(ones[:], 1.0), [], [ones])
        k.ts('dve', prm[:, 0, :], pp[:, 3, :], -1.0, None, ALU.mult, None, r=[pp], w=[prm])
        k.ts('dve', prm[:, 1, :], pp[:, 6, :], -1.0, 1.0, ALU.mult, ALU.add, r=[pp], w=[prm])
        P3 = sc.sb("P3", [64, 3, 4, BL], F32)
        halo = sc.sb("halo", [64, 3, 4], F32)
        LR = sc.sb("LR", [64, 3, BL], F32)
        halo2 = sc.sb("halo2", [64, 3], F32)
        ELW = sc.sb("ELW", H4, F32); SC_ = sc.sb("SCAN", H4, F32); AA = sc.sb("AA", H4, F32); KKN = sc.sb("KKN", H4, F32)
        T1 = sc.sb("T1", H4, F32); T2 = sc.sb("T2", H4, F32); CM4 = sc.sb("CM4", H4, F32)
        OUT = [{nm: sc.sb("%s%d" % (nm, i), H4, F32 if nm == "GAM" else BF16) for nm in ("AT", "BT", "KT", "RT", "RK", "GAM", "V")} for i in range(2)]
        SGs = [sc.sb("SG%d" % i, [64, BL], BF16) for i in range(2)]
        gupb = sc.sb("gupb", [64, 256], BF16)
        ppb = sc.sb("ppb", [64, 4], BF16)
        identb64 = k.ident_bf
        XY = [[sc.sb("XY%d_%d" % (i, j), [64, 2, 4, 64], BF16) for j in range(2)] for i in range(2)]
        PP = [[sc.sb("PPi%d_%d" % (i, j), [64, 4, 64], BF16) for j in range(2)] for i in range(2)]
        AKRK = [sc.sb("AKRK%d" % i, [64, 2, 4, 64], BF16) for i in range(2)]
        RBT = [sc.sb("RBT%d" % i, [64, 4, 64], BF16) for i in range(2)]
        TOK = [sc.sb("TOK%d" % i, [64, 3, 4, 64], BF16) for i in range(2)]
        Wsb = sc.sb("Wsb", [64, 4, 64], BF16); Usb = sc.sb("Usb", [64, 4, 64], BF16)
        Hs = [sc.sb("Hs%d" % i, [64, 4, 64], F32) for i in range(2)]
        Hb = [sc.sb("Hb%d" % i, [64, 4, 64], BF16) for i in range(2)]
        yc = sc.sb("yc", [64, 4, 64], F32); ysq = sc.sb("ysq", [64, 4, 64], F32)
        sm = sc.sb("sm", [64, 6, 4], F32)
        yab = [sc.sb("yab%d" % i, [64, CPB, 256], BF16) for i in range(2)]
        psA1 = sc.ps("psA1", [64, 512]); psA2 = sc.ps("psA2", [64, 512]); psA3 = sc.ps("psA3", [64, 512]); psA4 = sc.ps("psA4", [64, 512])
        psH = sc.ps("psHr", [64, 512]); psY = sc.ps("psYr", [64, 512]); psC = sc.ps("psCr", [64, 512]); psQ = sc.ps("psQr", [64, 512])
        k.S.op('pool', lambda: nc.gpsimd.memset(Hs[1][:], 0.0), [], [Hs[1]])
        k.S.op('pool', lambda: nc.gpsimd.memset(Hb[1][:], 0.0), [], [Hb[1]])
        k.copy('dve', gupb[:], gup[:], r=[gup], w=[gupb])
        k.copy('dve', ppb[:], pp[:, 7, :], r=[pp], w=[ppb])
        k.S.op('pool', lambda: nc.gpsimd.memset(halo[:], 0.0), [], [halo])
        k.S.op('pool', lambda: nc.gpsimd.memset(halo2[:], 0.0), [], [halo2])
        k.copy('dve', CM4[:], bc(cmask[:].unsqueeze(1), H4), r=[cmask], w=[CM4])
        E_ = BL - 1

        def prep(tb):
            O = OUT[tb % 2]; SG = SGs[tb % 2]
            AT, BT, KT, RT, RK, GAM, V_ = O["AT"], O["BT"], O["KT"], O["RT"], O["RK"], O["GAM"], O["V"]
            t0 = tb * BL
            for q in range(3):
                k.dma('act', P3[:, q, :, :], k.PT[q * 256:(q + 1) * 256, t0:t0 + BL].rearrange("(h d) t -> d h t", d=64), w=[P3])
            k.dma('act', LR[0:32, 0, :], k.PT[768:800, t0:t0 + BL], w=[LR])
            k.dma('act', LR[0:32, 1, :], k.PT[800:832, t0:t0 + BL], w=[LR])
            k.dma('act', LR[:, 2, :], k.PT[832:896, t0:t0 + BL], w=[LR])
            yield
            for q in range(3):
                p_ = P3[:, q, :, :]
                k.tt('dve', T1[:, :, 1:BL], p_[:, :, 0:E_], p_[:, :, 1:BL], ALU.subtract, r=[P3], w=[T1])
                k.tt('dve', T1[:, :, 0:1], halo[:, q, :].unsqueeze(2), p_[:, :, 0:1], ALU.subtract, r=[P3, halo], w=[T1])
                k.copy('pool', halo[:, q, :].unsqueeze(2), p_[:, :, E_:BL], r=[P3, T1], w=[halo])
                k.tt('pool', T1[:], T1[:], bc(pp[:, q, :].unsqueeze(2), H4), ALU.mult, r=[T1, pp], w=[T1])
                if q < 2:
                    k.tt('pool', p_, p_, T1[:], ALU.add, r=[P3, T1, halo], w=[P3])
                else:
                    k.tt('pool', V_[:], p_, T1[:], ALU.add, r=[P3, T1, halo], w=[V_])
                yield
            for q, rows in ((0, 32), (1, 32), (2, 64)):
                x_ = LR[0:rows, q, :]
                t_ = T2[0:rows, 0, :]
                k.tt('dve', t_[:, 1:BL], x_[:, 0:E_], x_[:, 1:BL], ALU.subtract, r=[LR], w=[T2])
                k.tt('dve', t_[:, 0:1], halo2[0:rows, q:q + 1], x_[:, 0:1], ALU.subtract, r=[LR, halo2], w=[T2])
                k.copy('dve', halo2[0:rows, q:q + 1], x_[:, E_:BL], r=[LR, T2], w=[halo2])
                k.stt('dve', x_, t_, lr[0:rows, q:q + 1], x_, ALU.mult, ALU.add, r=[T2, lr, LR, halo2], w=[LR])
            yield
            R_ = P3[:, 0, :, :]; Kp = P3[:, 1, :, :]
            k.act(LR[0:32, 0, :], LR[0:32, 0, :], AF.Tanh, r=[LR], w=[LR])
            k.act(SG[:], LR[:, 2, :], AF.Sigmoid, r=[LR], w=[SG])
            for h in range(4):
                k.mm(psQ[:, 0:BL], [(wup[:, h * 64:(h + 1) * 64], LR[0:32, 0, :])], r=[wup, LR], w=[psQ])
                k.act(T1[:, h, :], psQ[:, 0:BL], AF.Exp, r=[psQ, prm], w=[T1], scale=-1.0, bias=prm[:, 0, h:h + 1])
                k.mm(psQ[:, BL:2 * BL], [(aup[:, h * 64:(h + 1) * 64], LR[0:32, 1, :])], r=[aup, LR], w=[psQ], start=False)
                k.act(AA[:, h, :], psQ[:, BL:2 * BL], AF.Sigmoid, r=[psQ, pp], w=[AA], bias=pp[:, 4, h:h + 1])
                yield
            k.act(T1[:], T1[:], AF.Ln, r=[T1], w=[T1], bias=1.0)
            k.act(ELW[:], T1[:], AF.Exp, r=[T1], w=[ELW], scale=-1.0, bias=-0.5)
            k.S.op('dve', lambda: nc.vector.tensor_tensor_scan(
                out=SC_[:].rearrange("p h t -> p (h t)"), data0=CM4[:].rearrange("p h t -> p (h t)"),
                data1=ELW[:].rearrange("p h t -> p (h t)"), initial=0.0, op0=ALU.mult, op1=ALU.add), [CM4, ELW], [SC_])
            yield
            k.tt('pool', KKN[:], Kp, bc(pp[:, 5, :].unsqueeze(2), H4), ALU.mult, r=[P3, pp], w=[KKN])
            k.tt('pool', T1[:], KKN[:], KKN[:], ALU.mult, r=[KKN], w=[T1])
            for h in range(4):
                k.mm(psQ[:, 0:BL], [(ones[:], T1[:, h, :])], r=[ones, T1], w=[psQ])
                k.act(T2[:, h, :], psQ[:, 0:BL], AF.Ln, r=[psQ], w=[T2], bias=1e-24)
                yield
            k.act(T2[:], T2[:], AF.Exp, r=[T2], w=[T2], scale=-0.5)
            k.tt('dve', KKN[:], KKN[:], T2[:], ALU.mult, r=[KKN, T2], w=[KKN])
            yield
            k.tt('pool', T1[:], SC_[:], ELW[:], ALU.subtract, r=[SC_, ELW], w=[T1])
            k.act(T1[:], T1[:], AF.Exp, r=[T1], w=[T1], scale=-1.0)
            k.stt('dve', AT[:], KKN[:], -1.0, T1[:], ALU.mult, ALU.mult, r=[KKN, T1], w=[AT])
            yield
            k.act(T2[:], SC_[:], AF.Exp, r=[SC_], w=[T2])
            k.tt('pool', T1[:], KKN[:], AA[:], ALU.mult, r=[KKN, AA], w=[T1])
            k.tt('dve', BT[:], T1[:], T2[:], ALU.mult, r=[T1, T2], w=[BT])
            yield
            k.tt('pool', T1[:], AA[:], bc(pp[:, 6, :].unsqueeze(2), H4), ALU.mult, r=[AA, pp], w=[T1])
            k.tt('pool', T1[:], T1[:], bc(prm[:, 1, :].unsqueeze(2), H4), ALU.add, r=[T1, prm], w=[T1])
            k.tt('dve', Kp, Kp, T1[:], ALU.mult, r=[P3, T1, KKN], w=[P3])
            yield
            k.tt('dve', KT[:], Kp, T2[:], ALU.mult, r=[P3, T2], w=[KT])
            k.tt('pool', RK[:], R_, Kp, ALU.mult, r=[P3], w=[RK])
            k.act(GAM[:], SC_[:], AF.Exp, r=[SC_], w=[GAM], scale=-1.0)
            k.tt('dve', RT[:], R_, GAM[:], ALU.mult, r=[P3, GAM], w=[RT])
            yield

        def phaseA(nch):
            tb, n = divmod(nch, CPB)
            O = OUT[tb % 2]
            AT, BT, KT, RT, V_ = O["AT"], O["BT"], O["KT"], O["RT"], O["V"]
            c_ = slice(n * 64, (n + 1) * 64)
            par = nch % 2
            xy = XY[par][0]; akrk = AKRK[par]; rbt = RBT[par]; tok = TOK[par]
            fns = []
            for h in range(4):
                fns.append(lambda h=h: nc.tensor.matmul(psA1[:, h * 64:(h + 1) * 64], lhsT=BT[:, h, c_], rhs=AT[:, h, c_], start=True, stop=True, skip_group_check=True))
                fns.append(lambda h=h: nc.tensor.matmul(psA1[:, 256 + h * 64:256 + (h + 1) * 64], lhsT=AT[:, h, c_], rhs=BT[:, h, c_], start=True, stop=True, skip_group_check=True))
            k.S.pe_group(fns, [AT, BT], [psA1])
            fns = []
            for h in range(4):
                fns.append(lambda h=h: nc.tensor.matmul(psA2[:, h * 64:(h + 1) * 64], lhsT=KT[:, h, c_], rhs=AT[:, h, c_], start=True, stop=True, skip_group_check=True))
                fns.append(lambda h=h: nc.tensor.matmul(psA2[:, 256 + h * 64:256 + (h + 1) * 64], lhsT=KT[:, h, c_], rhs=RT[:, h, c_], start=True, stop=True, skip_group_check=True))
            k.S.pe_group(fns, [AT, KT, RT], [psA2])
            k.S.pe_group([lambda h=h: nc.tensor.matmul(psA3[:, h * 64:(h + 1) * 64], lhsT=BT[:, h, c_], rhs=RT[:, h, c_], start=True, stop=True, skip_group_check=True)
                          for h in range(4)], [BT, RT], [psA3])
            yield
            v4 = lambda ps, a: ps[:, a * 256:(a + 1) * 256].rearrange("p (h f) -> p h f", h=4)
            mb = lambda i: bc(mk[:, i, :].unsqueeze(1), [64, 4, 64])
            k.tt('dve', xy[:, 0, :, :], v4(psA1, 0), mb(0), ALU.mult, r=[psA1, mk], w=[xy])
            k.tt('dve', xy[:, 1, :, :], v4(psA1, 1), mb(1), ALU.mult, r=[psA1, mk], w=[xy])
            k.tt('dve', akrk[:, 0, :, :], v4(psA2, 0), mb(0), ALU.mult, r=[psA2, mk], w=[akrk])
            k.tt('dve', akrk[:, 1, :, :], v4(psA2, 1), mb(2), ALU.mult, r=[psA2, mk], w=[akrk])
            k.tt('dve', rbt[:], v4(psA3, 0), mb(2), ALU.mult, r=[psA3, mk], w=[rbt])
            yield
            fns = []
            psA1b = psA1[:, :].bitcast(BF16)
            for qi, src_ in enumerate((V_, BT, KT)):
                for h in range(4):
                    dst = psA1b[:, qi * 256 + h * 64:qi * 256 + (h + 1) * 64]
                    fns.append(lambda dst=dst, s_=src_[:, h, c_]: nc.tensor.transpose(out=dst, in_=s_, identity=k.ident_bf[0:64, 0:64]))
            k.S.pe_group(fns, [V_, BT, KT, k.ident_bf], [psA1])
            yield
            k.copy('act', tok[:].rearrange("p a h f -> p (a h f)"), psA1b[:, 0:768], r=[psA1], w=[tok])
            P_ = PP[par][0]
            k.tt('dve', P_[:], xy[:, 0, :, :], mb(3), ALU.add, r=[xy, mk], w=[P_])
            yield
            for lev in range(5):
                xyn = XY[par][(lev + 1) % 2]
                fns = []
                for h in range(4):
                    fns.append(lambda h=h, xy=xy: nc.tensor.matmul(psA4[:, 256 + h * 64:256 + (h + 1) * 64], lhsT=xy[:, 0, h, :], rhs=xy[:, 1, h, :], start=True, stop=True, skip_group_check=True))
                    if lev < 4:
                        fns.append(lambda h=h, xy=xy: nc.tensor.matmul(psA4[:, h * 64:(h + 1) * 64], lhsT=xy[:, 1, h, :], rhs=xy[:, 0, h, :], start=True, stop=True, skip_group_check=True))
                k.S.pe_group(fns, [xy], [psA4])
                yield
                if lev < 4:
                    k.copy('act', xyn[:].rearrange("p a h f -> p (a h f)"), psA4[:, :], r=[psA4], w=[xyn])
                else:
                    k.copy('act', xyn[:, 1, :, :].rearrange("p h f -> p (h f)"), psA4[:, 256:512], r=[psA4], w=[xyn])
                yield
                Pn = PP[par][(lev + 1) % 2]
                k.S.pe_group([lambda h=h, xyn=xyn, P_=P_: nc.tensor.matmul(psA3[:, 256 + h * 64:256 + (h + 1) * 64], lhsT=xyn[:, 1, h, :], rhs=P_[:, h, :], start=True, stop=True, skip_group_check=True)
                              for h in range(4)], [xyn, P_], [psA3])
                yield
                k.tt('dve', Pn[:], P_[:], v4(psA3, 1), ALU.add, r=[P_, psA3], w=[Pn])
                yield
                P_ = Pn; xy = xyn

        def phaseB(nch):
            tb, n = divmod(nch, CPB)
            O = OUT[tb % 2]; SG = SGs[tb % 2]
            AT, RT, RK, GAM = O["AT"], O["RT"], O["RK"], O["GAM"]
            c_ = slice(n * 64, (n + 1) * 64)
            par = nch % 2
            akrk = AKRK[par]; rbt = RBT[par]; tok = TOK[par]; TT = PP[par][1]
            Hold = Hs[(nch + 1) % 2]; Hnew = Hs[nch % 2]
            Hbo = Hb[(nch + 1) % 2]; Hbn = Hb[nch % 2]
            yab_ = yab[tb % 2]
            fns = []
            for h in range(4):
                fns.append(lambda h=h: nc.tensor.matmul(psH[:, h * 64:(h + 1) * 64], lhsT=AT[:, h, c_], rhs=Hbo[:, h, :], start=(h == 0), stop=False, skip_group_check=True))
                fns.append(lambda h=h: nc.tensor.matmul(psH[:, h * 64:(h + 1) * 64], lhsT=akrk[:, 0, h, :], rhs=tok[:, 0, h, :], start=False, stop=True, skip_group_check=True))
            k.S.pe_group(fns, [AT, Hbo, akrk, tok], [psH])
            yield
            k.copy('act', Wsb[:].rearrange("p h f -> p (h f)"), psH[:, 0:256], r=[psH], w=[Wsb])
            yield
            k.S.pe_group([lambda h=h: nc.tensor.matmul(psH[:, 256 + h * 64:256 + (h + 1) * 64], lhsT=TT[:, h, :], rhs=Wsb[:, h, :], start=False, stop=True, skip_group_check=True)
                          for h in range(4)], [TT, Wsb], [psH])
            yield
            k.copy('act', Usb[:].rearrange("p h f -> p (h f)"), psH[:, 256:512], r=[psH], w=[Usb])
            yield
            fns = []
            for h in range(4):
                fns.append(lambda h=h: nc.tensor.matmul(psC[:, h * 64:(h + 1) * 64], lhsT=tok[:, 1, h, :], rhs=Usb[:, h, :], start=(h == 0), stop=False, skip_group_check=True))
                fns.append(lambda h=h: nc.tensor.matmul(psC[:, h * 64:(h + 1) * 64], lhsT=tok[:, 2, h, :], rhs=tok[:, 0, h, :], start=False, stop=True, skip_group_check=True))
                fns.append(lambda h=h: nc.tensor.matmul(psC[:, 256 + h:256 + h + 1], lhsT=RK[:, h, c_], rhs=ppb[:, h:h + 1], start=False, stop=True, skip_group_check=True))
            k.S.pe_group(fns, [tok, Usb, RK, ppb], [psC])
            fns = []
            for h in range(4):
                fns.append(lambda h=h: nc.tensor.matmul(psY[:, h * 64:(h + 1) * 64], lhsT=RT[:, h, c_], rhs=Hbo[:, h, :], start=(h == 0), stop=False, skip_group_check=True))
                fns.append(lambda h=h: nc.tensor.matmul(psY[:, h * 64:(h + 1) * 64], lhsT=rbt[:, h, :], rhs=Usb[:, h, :], start=False, stop=False, skip_group_check=True))
                fns.append(lambda h=h: nc.tensor.matmul(psY[:, h * 64:(h + 1) * 64], lhsT=akrk[:, 1, h, :], rhs=tok[:, 0, h, :], start=False, stop=True, skip_group_check=True))
            fns.append(lambda: nc.tensor.matmul(psY[:, 256:512], lhsT=SG[:, c_], rhs=gupb[:, :], start=False, stop=True, skip_group_check=True))
            k.S.pe_group(fns, [RT, Hbo, rbt, Usb, akrk, tok, SG, gupb], [psY])
            yield
            k.tt('dve', Hnew[:], psC[:, 0:256].rearrange("p (h f) -> p h f", h=4), Hold[:], ALU.add, r=[psC, Hold], w=[Hnew])
            k.copy('dve', sm[:, 5, :], psC[:, 256:260], r=[psC], w=[sm])
            k.tt('dve', Hnew[:], Hnew[:], bc(GAM[:, :, n * 64 + 63:n * 64 + 64], [64, 4, 64]), ALU.mult, r=[Hnew, GAM], w=[Hnew])
            k.copy('pool', Hbn[:], Hnew[:], r=[Hnew], w=[Hbn])
            yield
            y3 = psY[:, 0:256].rearrange("p (h f) -> p h f", h=4)
            k.S.op('dve', lambda: nc.vector.reduce_sum(out=sm[:, 0, :], in_=y3, axis=AX.X), [psY], [sm])
            k.ts('dve', sm[:, 1, :], sm[:, 0, :], 1.0 / 64.0, None, ALU.mult, None, r=[sm], w=[sm])
            k.tt('dve', yc[:], y3, bc(sm[:, 1, :].unsqueeze(2), [64, 4, 64]), ALU.subtract, r=[psY, sm], w=[yc])
            yield
            k.tt('pool', ysq[:], yc[:], yc[:], ALU.mult, r=[yc], w=[ysq])
            k.S.op('dve', lambda: nc.vector.reduce_sum(out=sm[:, 2, :], in_=ysq[:], axis=AX.X), [ysq], [sm])
            k.act(sm[:, 3, :], sm[:, 2, :], AF.Ln, r=[sm], w=[sm], scale=1.0 / 64.0, bias=GN_EPS)
            k.act(sm[:, 4, :], sm[:, 3, :], AF.Exp, r=[sm], w=[sm], scale=-0.5)
            yield
            k.tt('dve', yc[:], yc[:], bc(sm[:, 4, :].unsqueeze(2), [64, 4, 64]), ALU.mult, r=[yc, sm], w=[yc])
            k.tt('pool', yc[:], yc[:], lnr[:, 0, :].rearrange("p (h f) -> p h f", h=4), ALU.mult, r=[yc, lnr], w=[yc])
            k.tt('pool', yc[:], yc[:], lnr[:, 1, :].rearrange("p (h f) -> p h f", h=4), ALU.add, r=[yc, lnr], w=[yc])
            k.tt('dve', ysq[:], tok[:, 0, :, :], bc(sm[:, 5, :].unsqueeze(2), [64, 4, 64]), ALU.mult, r=[tok, sm], w=[ysq])
            yield
            k.tt('pool', yc[:], yc[:], ysq[:], ALU.add, r=[yc, ysq], w=[yc])
            k.tt('dve', yab_[:, n, :], yc[:].rearrange("p h f -> p (h f)"), psY[:, 256:512], ALU.mult, r=[yc, psY], w=[yab_])
            if n == CPB - 1:
                k.dma('sp', k.Y[tb * BL:(tb + 1) * BL, 0:256].rearrange("(n p) c -> p n c", p=64), yab_[:], r=[yab_])
            yield

        def run_all(*gens):
            gens = [g for g in gens if g is not None]
            while gens:
                for g in list(gens):
                    try:
                        next(g)
                    except StopIteration:
                        gens.remove(g)

        NCH = S_LEN // 64
        run_all(prep(0))
        run_all(phaseA(0), prep(1) if NB > 1 else None)
        for nch in range(NCH):
            tb, n = divmod(nch, CPB)
            gens = [phaseB(nch)]
            if nch + 1 < NCH:
                gens.append(phaseA(nch + 1))
            if n == 1 and tb + 2 <= NB - 1 + 0 and False:
                pass
            if n == 0 and tb >= 1 and tb + 1 < NB:
                gens.append(prep(tb + 1))
            run_all(*gens)


def build(nlayers=DEPTH, taps=()):
    k = K(nlayers, taps=taps)
    setup_globals(k)
    setup_fox(k)
    setup_rwkv(k)
    setup_ffn(k)
    setup_nsa(k)
    for l in range(nlayers):
        xin = k.x_in if l == 0 else k.XR
        xout = k.OUT if l == nlayers - 1 else k.XR
        stage_mod(k, l)
        stage_proj(k, l, xin)
        stage_rwkv(k, l)
        stage_fox(k, l)
        stage_nsa(k, l)
        stage_out(k, l, xin, k.XR1)
        stage_ffn(k, l, k.XR1, xout)
    k.S.barrier()
    return k


_CACHE = {}


def kernel(**inputs):
    if "k" not in _CACHE:
        _CACHE["k"] = build(DEPTH)
    k = _CACHE["k"]
    sh = prep_shared(inputs)
    in_maps = []
    for b in range(8):
        d = dict(sh)
        d.update(prep_core(inputs, b))
        in_maps.append({n: v for n, v in d.items() if n in k.ins})
    res = run_bass_kernel_spmd(k.nc, in_maps, core_ids=list(range(8)))
    out = np.stack([np.asarray(res.results[b]["out"], dtype=np.float32) for b in range(8)], axis=0)
    return out
```

```python
import numpy as np
import ml_dtypes
from contextlib import ExitStack
import concourse.bass as bass
import concourse.mybir as mybir
from concourse.bass_utils import run_bass_kernel_spmd

F32 = mybir.dt.float32
BF16 = mybir.dt.bfloat16
AF = mybir.ActivationFunctionType
ALU = mybir.AluOpType
AX = mybir.AxisListType
NPBF = ml_dtypes.bfloat16

S_LEN = 4096
D = 1024
DEPTH = 4
NTB = 8
N_IN = 3224
D_FF = 2816
NEG = -30000.0
RMS_EPS = 1e-6
GN_EPS = 64e-5


class Sched:
    ENG = ('pe', 'act', 'dve', 'pool')
    LIMIT = 30000

    def __init__(self, nc):
        self.nc = nc
        self.e = {'pe': nc.tensor, 'act': nc.scalar, 'dve': nc.vector, 'pool': nc.gpsimd, 'sp': nc.sync}
        self.epoch = {k: 0 for k in self.ENG}
        self.sem = {k: nc.alloc_semaphore("c_%s_0" % k) for k in self.ENG}
        self.cnt = {k: 0 for k in self.ENG}
        self.seen = {k: {} for k in self.e}
        self.lastw = {}
        self.reads = {}
        self.dma_sems = {'hw': [[nc.alloc_semaphore("d%d" % i), 0, "dma%d" % i] for i in range(24)],
                         'sw': [[nc.alloc_semaphore("ds%d" % i), 0, "dmas%d" % i] for i in range(8)]}
        self.ndma = {'hw': 0, 'sw': 0}
        self.n_inst = 0
        self.n_wait = 0
        self.per = {}

    def _wait(self, eng, tok):
        key, sem, val = tok
        if self.seen[eng].get(key, 0) >= val:
            return
        self.e[eng].wait_ge(sem, val)
        self.n_wait += 1
        self.per[eng] = self.per.get(eng, 0) + 1
        self.seen[eng][key] = val

    def _deps(self, eng, reads, writes):
        for b in reads:
            t = self.lastw.get(b)
            if t is not None:
                self._wait(eng, t)
        for b in writes:
            t = self.lastw.get(b)
            if t is not None:
                self._wait(eng, t)
            for t in self.reads.get(b, ()):
                self._wait(eng, t)

    def _commit(self, tok, reads, writes):
        for b in reads:
            self.reads.setdefault(b, []).append(tok)
        for b in writes:
            self.lastw[b] = tok
            self.reads[b] = []

    def _bump(self, eng, ins):
        if self.cnt[eng] >= self.LIMIT:
            self.epoch[eng] += 1
            self.sem[eng] = self.nc.alloc_semaphore("c_%s_%d" % (eng, self.epoch[eng]))
            self.cnt[eng] = 0
        self.cnt[eng] += 1
        ins.then_inc(self.sem[eng], 1)
        return ("%s_%d" % (eng, self.epoch[eng]), self.sem[eng], self.cnt[eng])

    @staticmethod
    def _norm(reads, writes):
        rd = [getattr(b, 'n', b) for b in reads]
        wr = [getattr(b, 'n', b) for b in writes]
        ps = [b for b in rd if b.startswith("ps")]
        rd = [b for b in rd if not b.startswith("ps")]
        return rd, wr + [b for b in ps if b not in wr]

    def op(self, eng, inst_fn, reads=(), writes=()):
        reads, writes = self._norm(reads, writes)
        self._deps(eng, reads, writes)
        ins = inst_fn()
        self.per[eng] = self.per.get(eng, 0) + 1
        tok = self._bump(eng, ins)
        self._commit(tok, reads, writes)
        self.n_inst += 1
        return tok

    def pe_group(self, fns, reads=(), writes=()):
        reads, writes = self._norm(reads, writes)
        self._deps('pe', reads, writes)
        ins = None
        for f in fns:
            ins = f()
            self.n_inst += 1
            self.per['pe'] = self.per.get('pe', 0) + 1
        tok = self._bump('pe', ins)
        self._commit(tok, reads, writes)
        return tok

    def dma(self, q, out, in_, reads=(), writes=(), **kw):
        reads, writes = self._norm(reads, writes)
        self._deps(q, reads, writes)
        cls = 'sw' if q == 'pool' else 'hw'
        pool_ = self.dma_sems[cls]
        slot = pool_[self.ndma[cls] % len(pool_)]
        self.ndma[cls] += 1
        if slot[1] > 0:
            self._wait(q, (slot[2], slot[0], slot[1]))
        if slot[1] >= self.LIMIT:
            slot[0] = self.nc.alloc_semaphore("%s_e%d" % (slot[2], self.ndma[cls]))
            slot[1] = 0
            slot[2] = slot[2] + "x"
        slot[1] += 16
        ins = self.e[q].dma_start(out=out, in_=in_, **kw)
        self.per[q] = self.per.get(q, 0) + 1
        ins.then_inc(slot[0], 16)
        tok = (slot[2], slot[0], slot[1])
        self._commit(tok, reads, writes)
        self.n_inst += 1
        return tok

    def barrier(self, engines=('pe', 'act', 'dve', 'pool', 'sp')):
        toks = [("%s_%d" % (k, self.epoch[k]), self.sem[k], self.cnt[k]) for k in self.ENG if self.cnt[k] > 0]
        toks += [(s[2], s[0], s[1]) for p_ in self.dma_sems.values() for s in p_ if s[1] > 0]
        for e in engines:
            for t in toks:
                self._wait(e, t)
        self.lastw = {}
        self.reads = {}


class Pipe:
    def __init__(self, lag=2):
        self.q = []
        self.lag = lag

    def push(self, first, second):
        first()
        self.q.append(second)
        while len(self.q) > self.lag:
            self.q.pop(0)()

    def flush(self):
        while self.q:
            self.q.pop(0)()


class Tile:
    def __init__(self, h, name):
        self.h = h
        self.n = name

    def __getitem__(self, idx):
        return self.h[idx]


class Scope:
    cnt = 0

    def __init__(self, k):
        self.k = k
        self.es = ExitStack()

    def __enter__(self):
        self.es.__enter__()
        Scope.cnt += 1
        self.id = Scope.cnt
        return self

    def sb(self, name, shape, dt):
        nm = "%s_%d" % (name, self.id)
        h = self.es.enter_context(self.k.nc.sbuf_tensor(nm, list(shape), dt))
        return Tile(h, nm)

    def ps(self, name, shape, dt=F32):
        nm = "%s_%d" % (name, self.id)
        h = self.es.enter_context(self.k.nc.psum_tensor(nm, list(shape), dt))
        return Tile(h, nm)

    def __exit__(self, *a):
        self.k.S.barrier()
        return self.es.__exit__(*a)


class K:
    def __init__(self, nlayers, taps=()):
        self.nc = bass.Bass("TRN2", target_bir_lowering=False)
        self.S = Sched(self.nc)
        self.nl = nlayers
        self.taps = set(taps)
        self.ins = {}
        self.dr = {}

    def inp(self, name, shape, dt=F32):
        t = self.nc.dram_tensor(name, list(shape), dt, kind="ExternalInput").ap()
        self.ins[name] = t
        return t

    def scratch(self, name, shape, dt=F32, out=False):
        kind = "ExternalOutput" if (out or name in self.taps) else "Internal"
        t = self.nc.dram_tensor(name, list(shape), dt, kind=kind).ap()
        self.dr[name] = t
        return t

    def act(self, out, in_, func, r, w, bias=0.0, scale=1.0, accum=None):
        nc = self.nc
        if accum is None:
            return self.S.op('act', lambda: nc.scalar.activation(out=out, in_=in_, func=func, bias=bias, scale=scale), r, w)
        return self.S.op('act', lambda: nc.scalar.activation(out=out, in_=in_, func=func, bias=bias, scale=scale, accum_out=accum), r, w)

    def ts(self, eng, out, in0, s1, s2, op0, op1, r, w):
        e = self.S.e[eng]
        if op1 is None:
            return self.S.op(eng, lambda: e.tensor_scalar(out=out, in0=in0, scalar1=s1, scalar2=None, op0=op0), r, w)
        return self.S.op(eng, lambda: e.tensor_scalar(out=out, in0=in0, scalar1=s1, scalar2=s2, op0=op0, op1=op1), r, w)

    def tt(self, eng, out, in0, in1, op, r, w):
        e = self.S.e[eng]
        return self.S.op(eng, lambda: e.tensor_tensor(out=out, in0=in0, in1=in1, op=op), r, w)

    def stt(self, eng, out, in0, scalar, in1, op0, op1, r, w):
        e = self.S.e[eng]
        return self.S.op(eng, lambda: e.scalar_tensor_tensor(out=out, in0=in0, scalar=scalar, in1=in1, op0=op0, op1=op1), r, w)

    def copy(self, eng, out, in_, r, w):
        if eng == 'act':
            return self.S.op('act', lambda: self.nc.scalar.copy(out=out, in_=in_), r, w)
        e = self.S.e[eng]
        return self.S.op(eng, lambda: e.tensor_copy(out=out, in_=in_), r, w)

    def mm(self, out, pairs, r, w, start=True, stop=True, sgc=False):
        nc = self.nc
        n = len(pairs)
        fns = []
        for i, (l, rh) in enumerate(pairs):
            fns.append(lambda l=l, rh=rh, i=i: nc.tensor.matmul(out, lhsT=l, rhs=rh, start=(start and i == 0), stop=(stop and i == n - 1),
                                                               skip_group_check=(sgc or not start)))
        return self.S.pe_group(fns, r, w)

    def transpose(self, out, in_, ident, r, w):
        nc = self.nc
        return self.S.op('pe', lambda: nc.tensor.transpose(out=out, in_=in_, identity=ident), r, w)

    def dma(self, q, out, in_, r=(), w=(), **kw):
        return self.S.dma(q, out, in_, r, w, **kw)


def w_in_perm_index():
    idx = list(range(0, 896))
    idx += list(range(896, 1664))
    for c in range(3):
        idx += list(range(2054 + c * 64, 2054 + c * 64 + 64))
        idx += list(range(2054 + (c + 3) * 64, 2054 + (c + 3) * 64 + 64))
    idx += list(range(2438, 2566))
    idx += list(range(2566, 2694))
    idx += list(range(2694, 2822))
    idx += list(range(2950, 3078))
    idx += list(range(2048, 2054))
    idx += list(range(1664, 2048))
    idx += list(range(2822, 2950))
    idx += list(range(3078, 3206))
    idx += list(range(3206, 3224))
    assert len(idx) == N_IN and len(set(idx)) == N_IN
    return np.array(idx)


QKT_ROWS = 1664


def setup_globals(k):
    nc = k.nc
    k.x_in = k.inp("x", [S_LEN, D])
    k.cT = k.inp("cT", [128, 8])
    k.ada_w = k.inp("ada_w", [DEPTH, D, 6 * D])
    k.ada_b_fm = k.inp("ada_b_fm", [DEPTH, 128, 48])
    k.ada_b_row = k.inp("ada_b_row", [DEPTH, 6 * D])
    k.normg_fm = k.inp("normg_fm", [DEPTH, 4, 128, 8])
    k.normg_row = k.inp("normg_row", [DEPTH, 4, D])
    k.w_in = k.inp("w_in_p", [DEPTH, D, N_IN])
    k.ident_bf_d = k.inp("ident_bf", [128, 128], BF16)
    k.ident_f_d = k.inp("ident_f", [128, 128], F32)

    k.PT = k.scratch("PT", [896, S_LEN], F32)
    k.QKT = k.scratch("QKT", [QKT_ROWS, S_LEN], BF16)
    k.FL = k.scratch("FL", [6, S_LEN], F32)
    k.VT = k.scratch("VT", [S_LEN, 640], BF16)
    k.GT = k.scratch("GT", [S_LEN, 18], F32)
    k.Y = k.scratch("Y", [S_LEN, D], BF16)
    k.XR = k.scratch("XR", [S_LEN, D], F32)
    k.XR1 = k.scratch("XR1", [S_LEN, D], F32)
    k.OUT = k.scratch("out", [S_LEN, D], F32, out=True)

    def pers(name, shape, dt):
        return Tile(nc.alloc_sbuf_tensor(name, list(shape), dt), name)
    k.ident_bf = pers("ident_bf_sb", [128, 128], BF16)
    k.ident_f = pers("ident_f_sb", [128, 128], F32)
    k.sc = pers("sc", [128, 8], F32)
    k.modAB = pers("modAB", [128, 32], F32)
    k.gm_row = pers("gm_row", [128, D], F32)
    k.gf_row = pers("gf_row", [128, D], F32)
    k.dma('sp', k.ident_bf[:], k.ident_bf_d, w=[k.ident_bf])
    k.dma('sp', k.ident_f[:], k.ident_f_d, w=[k.ident_f])
    k.dma('sp', k.sc[:], k.cT, w=[k.sc])
    k.act(k.sc[:], k.sc[:], AF.Silu, r=[k.sc], w=[k.sc])


def stage_mod(k, l):
    with Scope(k) as sc:
        slab = [sc.sb("adaslab%d" % i, [128, 6 * D], F32) for i in range(4)]
        psA = sc.ps("psA", [128, 32])
        psR = [sc.ps("psR%d" % i, [128, 512]) for i in range(4)]
        bfm = sc.sb("bfm", [128, 48], F32)
        gfm = sc.sb("gfm", [128, 4, 8], F32)
        brow = sc.sb("brow", [128, 2, D], F32)
        grow = sc.sb("grow", [128, 2, D], F32)
        mfm = sc.sb("mfm", [128, 32], F32)
        sc_rep = sc.sb("sc_rep", [128, 8, 128], F32)
        for kc in range(8):
            k.copy('dve', sc_rep[:, kc, :], k.sc[:, kc:kc + 1].to_broadcast([128, 128]), r=[k.sc], w=[sc_rep])
        k.dma('sp', bfm[:], k.ada_b_fm[l], w=[bfm])
        k.dma('sp', gfm[:], k.normg_fm[l].rearrange("g p c -> p g c"), w=[gfm])
        k.dma('sp', brow[:, 0, :], k.ada_b_row[l:l + 1, 2 * D:3 * D].broadcast_to([128, D]), w=[brow])
        k.dma('sp', brow[:, 1, :], k.ada_b_row[l:l + 1, 5 * D:6 * D].broadcast_to([128, D]), w=[brow])
        k.dma('sp', grow[:, 0, :], k.normg_row[l, 1:2, :].broadcast_to([128, D]), w=[grow])
        k.dma('sp', grow[:, 1, :], k.normg_row[l, 3:4, :].broadcast_to([128, D]), w=[grow])
        fm_chunks = list(range(0, 16)) + list(range(24, 40))
        row_cols = [2 * D, 2 * D + 512, 5 * D, 5 * D + 512]
        for kc in range(8):
            sl = slab[kc % 4]
            k.dma('sp' if kc % 2 == 0 else 'act', sl[:], k.ada_w[l, kc * 128:(kc + 1) * 128, :], w=[sl])
            for i, j in enumerate(fm_chunks):
                k.mm(psA[:, i:i + 1], [(sl[:, j * 128:(j + 1) * 128], k.sc[:, kc:kc + 1])], r=[sl, k.sc], w=[psA],
                     start=(kc == 0 and i == 0), stop=(kc == 7), sgc=True)
            for i, c0 in enumerate(row_cols):
                k.mm(psR[i][:], [(sc_rep[:, kc, :], sl[:, c0:c0 + 512])], r=[sl, sc_rep], w=[psR[i]],
                     start=(kc == 0), stop=(kc == 7), sgc=True)
        k.tt('dve', mfm[:, 0:16], psA[:, 0:16], bfm[:, 0:16], ALU.add, r=[psA, bfm], w=[mfm])
        k.tt('dve', mfm[:, 16:32], psA[:, 16:32], bfm[:, 24:40], ALU.add, r=[psA, bfm], w=[mfm])
        k.stt('dve', k.modAB[:, 0:8], mfm[:, 8:16], 1.0, gfm[:, 0, :], ALU.add, ALU.mult, r=[mfm, gfm], w=[k.modAB])
        k.copy('dve', k.modAB[:, 8:16], mfm[:, 0:8], r=[mfm], w=[k.modAB])
        k.stt('dve', k.modAB[:, 16:24], mfm[:, 24:32], 1.0, gfm[:, 2, :], ALU.add, ALU.mult, r=[mfm, gfm], w=[k.modAB])
        k.copy('dve', k.modAB[:, 24:32], mfm[:, 16:24], r=[mfm], w=[k.modAB])
        for i in range(4):
            dst = (k.gm_row if i < 2 else k.gf_row)
            cs = slice((i % 2) * 512, (i % 2) * 512 + 512)
            k.tt('dve', dst[:, cs], psR[i][:], brow[:, i // 2, cs], ALU.add, r=[psR[i], brow], w=[dst])
            k.tt('pool', dst[:, cs], dst[:, cs], grow[:, i // 2, cs], ALU.mult, r=[dst, grow], w=[dst])


def stage_proj(k, l, xsrc):
    nc = k.nc
    with Scope(k) as sc:
        wsb = sc.sb("wsb", [128, 8, N_IN], BF16)
        wst = [sc.sb("wst%d" % i, [128, N_IN], F32) for i in range(4)]
        xt = [sc.sb("xt%d" % i, [128, D], F32) for i in range(2)]
        junk = sc.sb("junk", [128, D], BF16)
        xn = [sc.sb("xn%d" % i, [128, D], BF16) for i in range(2)]
        st = [sc.sb("st%d" % i, [128, 4], F32) for i in range(2)]
        HT = [sc.sb("HT%d" % i, [128, 8, 512], BF16) for i in range(2)]
        psT = [sc.ps("psT%d" % i, [128, D], BF16) for i in range(2)]
        psM = [sc.ps("psM%d" % i, [128, 512]) for i in range(4)]
        evf = [sc.sb("evf%d" % i, [128, 512], F32) for i in range(3)]
        evb = [sc.sb("evb%d" % i, [128, 512], BF16) for i in range(3)]
        evt = [sc.sb("evt%d" % i, [128, 640], BF16) for i in range(2)]
        evg = [sc.sb("evg%d" % i, [128, 18], F32) for i in range(2)]
        for kc in range(8):
            s = wst[kc % 4]
            k.dma('sp' if kc % 2 == 0 else 'act', s[:], k.w_in[l, kc * 128:(kc + 1) * 128, :], w=[s])
            k.copy('pool' if kc % 2 == 0 else 'dve', wsb[:, kc, :], s[:], r=[s], w=[wsb])
        cnt = {"ev": 0, "pm": 0}

        def norm_gen(tb):
            ht = HT[tb % 2]
            t0_ = tb * 4
            k.dma('act', xt[t0_ % 2][:], xsrc[t0_ * 128:(t0_ + 1) * 128, :], w=[xt[t0_ % 2]])
            yield
            for sub in range(4):
                ti = tb * 4 + sub
                x_ = xt[ti % 2]; xn_ = xn[ti % 2]; st_ = st[ti % 2]; pt_ = psT[ti % 2]
                k.act(junk[:], x_[:], AF.Square, r=[x_], w=[junk, st_], scale=1.0 / 32.0, accum=st_[:, 0:1])
                k.act(st_[:, 1:2], st_[:, 0:1], AF.Ln, r=[st_], w=[st_], bias=RMS_EPS)
                k.act(st_[:, 2:3], st_[:, 1:2], AF.Exp, r=[st_], w=[st_], scale=-0.5)
                k.ts('dve', xn_[:], x_[:], st_[:, 2:3], None, ALU.mult, None, r=[x_, st_], w=[xn_])
                if sub < 3:
                    k.dma('act', xt[(ti + 1) % 2][:], xsrc[(ti + 1) * 128:(ti + 2) * 128, :], w=[xt[(ti + 1) % 2]])
                yield
                yield
                for kc in range(8):
                    k.transpose(pt_[:, kc * 128:(kc + 1) * 128], xn_[:, kc * 128:(kc + 1) * 128], k.ident_bf[:],
                                r=[xn_, k.ident_bf], w=[pt_])
                    if kc == 3:
                        yield
                yield
                for kc in range(8):
                    o = ht[:, kc, sub * 128:(sub + 1) * 128]
                    i_ = pt_[:, kc * 128:(kc + 1) * 128]
                    if kc % 2 == 0:
                        k.ts('dve', o, i_, k.modAB[:, kc:kc + 1], k.modAB[:, 8 + kc:9 + kc], ALU.mult, ALU.add,
                             r=[pt_, k.modAB], w=[ht])
                    else:
                        k.act(o, i_, AF.Identity, r=[pt_, k.modAB], w=[ht], scale=k.modAB[:, kc:kc + 1],
                              bias=k.modAB[:, 8 + kc:9 + kc])
                yield

        def step(g):
            if g is not None:
                try:
                    next(g)
                except StopIteration:
                    return None
            return g

        for _ in norm_gen(0):
            pass
        for tb in range(NTB):
            ht = HT[tb % 2]
            g = norm_gen(tb + 1) if tb + 1 < NTB else None
            tsl = slice(tb * 512, (tb + 1) * 512)
            fm = [(c * 128, 128, 'PT', c * 128) for c in range(7)]
            fm += [(896 + c * 128, 128, 'QKT', c * 128) for c in range(13)]
            fm += [(2560, 6, 'FL', 0)]
            for (c0, m, dst, r0) in fm:
                ps = psM[cnt["pm"] % 4]; cnt["pm"] += 1
                k.mm(ps[0:m, :], [(wsb[:, kc, c0:c0 + m], ht[:, kc, :]) for kc in range(8)], r=[wsb, ht], w=[ps])
                eng = 'act' if cnt["ev"] % 2 == 0 else 'dve'
                if dst == 'QKT':
                    ev = evb[cnt["ev"] % 3]
                    dd = k.QKT[r0:r0 + m, tsl]
                else:
                    ev = evf[cnt["ev"] % 3]
                    dd = (k.PT if dst == 'PT' else k.FL)[r0:r0 + m, tsl]
                cnt["ev"] += 1
                k.copy(eng, ev[0:m, :], ps[0:m, :], r=[ps], w=[ev])
                k.dma('sp', dd, ev[0:m, :], r=[ev])
                g = step(g)
            for sub in range(4):
                ti = tb * 4 + sub
                tok = slice(ti * 128, (ti + 1) * 128)
                ps0 = psM[cnt["pm"] % 4]; cnt["pm"] += 1
                ps1 = psM[cnt["pm"] % 4]; cnt["pm"] += 1
                lhs = lambda kc: ht[:, kc, sub * 128:(sub + 1) * 128]
                k.mm(ps0[:, 0:384], [(lhs(kc), wsb[:, kc, 2566:2950]) for kc in range(8)], r=[wsb, ht], w=[ps0])
                k.mm(ps1[:, 0:274], [(lhs(kc), wsb[:, kc, 2950:3224]) for kc in range(8)], r=[wsb, ht], w=[ps1])
                et = evt[ti % 2]; eg = evg[ti % 2]
                k.copy('act', et[:, 0:384], ps0[:, 0:384], r=[ps0], w=[et])
                k.copy('dve', et[:, 384:640], ps1[:, 0:256], r=[ps1], w=[et])
                k.copy('dve', eg[:], ps1[:, 256:274], r=[ps1], w=[eg])
                k.dma('sp', k.VT[tok, :], et[:], r=[et])
                k.dma('sp', k.GT[tok, :], eg[:], r=[eg])
                g = step(g)
            while g is not None:
                g = step(g)


def prep_shared(inp):
    f = lambda a: np.ascontiguousarray(np.asarray(a, dtype=np.float32))
    sh = {}
    sh["ada_w"] = f(inp["ada_w"])
    sh["ada_b_fm"] = f(np.asarray(inp["ada_b"]).reshape(DEPTH, 48, 128).transpose(0, 2, 1))
    sh["ada_b_row"] = f(inp["ada_b"])
    sh["normg_fm"] = f(np.asarray(inp["norm_g"]).reshape(DEPTH, 4, 8, 128).transpose(0, 1, 3, 2))
    sh["normg_row"] = f(inp["norm_g"])
    sh["w_in_p"] = f(np.asarray(inp["w_in"])[:, :, w_in_perm_index()])
    sh["ident_bf"] = np.eye(128, dtype=np.float32).astype(NPBF)
    sh["ident_f"] = np.eye(128, dtype=np.float32)
    sh["w_out"] = f(inp["w_out"]); sh["ffn_up"] = f(inp["ffn_up"]); sh["ffn_down"] = f(inp["ffn_down"])
    sh["conv_w_fm"] = f(np.asarray(inp["ffn_conv_w"]).reshape(DEPTH, 3, 44, 128).transpose(0, 3, 1, 2))
    sh["conv_b_fm"] = f(np.asarray(inp["ffn_conv_b"]).reshape(DEPTH, 44, 128).transpose(0, 2, 1))
    sh.update(nsa_host_consts())
    sh["rel_bias"] = f(inp["rel_bias"])
    sh["nsa_pe_kT"] = f(np.asarray(inp["nsa_pe_k"]).transpose(0, 2, 1))
    sh["nsa_pe_vT"] = f(np.asarray(inp["nsa_pe_v"]).transpose(0, 2, 1))
    for n in ("nsa_ck_w1", "nsa_cv_w1", "nsa_ck_w2", "nsa_cv_w2"):
        sh[n] = f(inp[n])
    sh["fox_b_f"] = f(np.asarray(inp["fox_b_f"]).reshape(DEPTH, 6, 1))
    sh.update(rwkv_host(inp))
    return sh


def prep_core(inp, b):
    d = {}
    d["x"] = np.ascontiguousarray(np.asarray(inp["x"][b], dtype=np.float32))
    d["cT"] = np.ascontiguousarray(np.asarray(inp["c"][b], dtype=np.float32).reshape(8, 128).T)
    return d


def setup_fox(k):
    k.fox_bf = k.inp("fox_b_f", [DEPTH, 6, 1])
    k.CUMA = k.scratch("CUMA", [6, 3, S_LEN], BF16)


def stage_fox(k, l):
    nc = k.nc
    with Scope(k) as sc:
        nb = sc.sb("nb", [128, 32, 6], F32)
        with Scope(k) as s2:
            fl = s2.sb("fl", [6, S_LEN], F32)
            t1 = s2.sb("t1", [6, S_LEN], F32)
            ones = s2.sb("ones", [6, S_LEN], F32)
            cum = s2.sb("cum", [6, S_LEN], F32)
            parts = s2.sb("parts", [6, 3, S_LEN], BF16)
            bfv = s2.sb("bfv", [6, 2], F32)
            psn = s2.ps("psn", [128, 512])
            k.dma('sp', fl[:], k.FL, w=[fl])
            k.dma('sp', bfv[:, 0:1], k.fox_bf[l], w=[bfv])
            k.ts('dve', bfv[:, 1:2], bfv[:, 0:1], -1.0, None, ALU.mult, None, r=[bfv], w=[bfv])
            k.S.op('pool', lambda: nc.gpsimd.memset(ones[:], 1.0), [], [ones])
            k.act(t1[:], fl[:], AF.Exp, r=[fl, bfv], w=[t1], bias=bfv[:, 1:2], scale=-1.0)
            k.act(t1[:], t1[:], AF.Ln, r=[t1], w=[t1], bias=1.0, scale=1.0)
            k.ts('dve', t1[:], t1[:], -1.0, None, ALU.mult, None, r=[t1], w=[t1])
            k.S.op('dve', lambda: nc.vector.tensor_tensor_scan(out=cum[:], data0=ones[:], data1=t1[:], initial=0.0,
                                                               op0=ALU.mult, op1=ALU.add), [ones, t1], [cum])
            for t in range(32):
                k.transpose(psn[:, t * 6:(t + 1) * 6], cum[:, t * 128:(t + 1) * 128], k.ident_f[0:6, 0:6],
                            r=[cum, k.ident_f], w=[psn])
            k.ts('dve', nb[:].rearrange("p t h -> p (t h)"), psn[:, 0:192], -1.0, None, ALU.mult, None, r=[psn], w=[nb])
            k.ts('dve', t1[:], cum[:], 8.0, None, ALU.mult, None, r=[cum], w=[t1])
            k.copy('dve', parts[:, 0, :], t1[:], r=[t1], w=[parts])
            k.tt('dve', t1[:], t1[:], parts[:, 0, :], ALU.subtract, r=[t1, parts], w=[t1])
            k.copy('dve', parts[:, 1, :], t1[:], r=[t1], w=[parts])
            k.tt('dve', t1[:], t1[:], parts[:, 1, :], ALU.subtract, r=[t1, parts], w=[t1])
            k.copy('dve', parts[:, 2, :], t1[:], r=[t1], w=[parts])
            k.dma('sp', k.CUMA, parts[:], r=[parts], w=["CUMA"])
        QA = [sc.sb("QA%d" % i, [128, S_LEN], BF16) for i in range(2)]
        KA = [sc.sb("KA%d" % i, [128, S_LEN], BF16) for i in range(2)]
        VA = sc.sb("VA", [128, 32, 6, 65], BF16)
        yb = sc.sb("yb", [128, 32, 384], BF16)
        PTl = [sc.sb("PTl%d" % i, [128, 512], BF16) for i in range(6)]
        rc = [sc.sb("rc%d" % i, [128, 4], F32) for i in range(2)]
        psS = [sc.ps("psS%d" % i, [128, 512]) for i in range(4)]
        psO = [sc.ps("psO%d" % i, [128, 512]) for i in range(2)]
        k.dma('sp', yb[:], k.VT[:, 0:384].rearrange("(t p) c -> p t c", p=128), w=[yb])
        k.S.op('pool', lambda: nc.gpsimd.memset(VA[:, :, :, 64:65], 1.0), [], [VA])
        k.copy('pool', VA[:, :, :, 0:64], yb[:].rearrange("p t (h d) -> p t h d", h=6), r=[yb], w=[VA])
        for i in range(2):
            k.S.op('dve', lambda i=i: nc.vector.memset(KA[i][64:67, :], 1.0), [], [KA[i]])
        nS = 0
        nO = 0
        nP = 0
        pipe = Pipe(2)
        for h in range(6):
            qa = QA[h % 2]; ka = KA[h % 2]
            k.dma('sp', qa[0:64, :], k.QKT[h * 64:(h + 1) * 64, :], w=[qa])
            k.dma('sp', qa[64:67, :], k.CUMA[h], w=[qa])
            k.dma('sp', ka[0:64, :], k.QKT[384 + h * 64:384 + (h + 1) * 64, :], w=[ka])
            for qb in range(NTB):
                po = psO[nO % 2]; nO += 1
                nkt = 4 * qb + 4
                for kt in range(nkt):
                    j = kt - 4 * qb
                    c0 = max(j, 0) * 128
                    ps = psS[nS % len(psS)]; nS += 1
                    pt = PTl[nP % len(PTl)]; nP += 1

                    def first(ps=ps, pt=pt, kt=kt, c0=c0, j=j, qa=qa, ka=ka, qb=qb, h=h):
                        k.mm(ps[:, c0:512], [(ka[0:67, kt * 128:(kt + 1) * 128], qa[0:67, qb * 512 + c0:(qb + 1) * 512])],
                             r=[ka, qa], w=[ps])
                        k.act(pt[:, c0:512], ps[:, c0:512], AF.Exp, r=[ps, nb], w=[pt], bias=nb[:, kt, h:h + 1], scale=0.125)
                        if j >= 0:
                            k.S.op('pool', lambda: nc.gpsimd.affine_select(
                                out=pt[:, c0:c0 + 128], in_=pt[:, c0:c0 + 128], pattern=[[1, 128]], compare_op=ALU.is_ge,
                                fill=0.0, base=0, channel_multiplier=-1), [pt], [pt])

                    def second(pt=pt, kt=kt, j=j, po=po, qb=qb, h=h, last=(kt == nkt - 1)):
                        fns = []
                        for qs in range(max(j, 0), 4):
                            fns.append(lambda qs=qs: nc.tensor.matmul(
                                po[:, qs * 65:(qs + 1) * 65], lhsT=pt[:, qs * 128:(qs + 1) * 128], rhs=VA[:, kt, h, :],
                                start=(kt == 0 and qs == 0), stop=(kt == 4 * qb + qs), skip_group_check=True))
                        k.S.pe_group(fns, [pt, VA], [po])
                        if last:
                            r_ = rc[qb % 2]
                            pov = po[:, 0:260].rearrange("p (q c) -> p q c", c=65)
                            k.S.op('dve', lambda: nc.vector.reciprocal(out=r_[:], in_=pov[:, :, 64]), [po], [r_])
                            for qs in range(4):
                                k.ts('dve', yb[:, qb * 4 + qs, h * 64:(h + 1) * 64], po[:, qs * 65:qs * 65 + 64], r_[:, qs:qs + 1], None,
                                     ALU.mult, None, r=[po, r_], w=[yb])
                    pipe.push(first, second)
        pipe.flush()
        k.dma('sp', k.Y[:, 256:640].rearrange("(t p) c -> p t c", p=128), yb[:], r=[yb], w=["Y"])


LW = 1536
LC = 4608
NEG8 = -240000.0


def t5_bucket_np(n):
    n = np.maximum(n, 0)
    nf = np.maximum(n, 1).astype(np.float32)
    large = 16 + (np.log(nf / np.float32(16)) / np.float32(np.log(128 / 16)) * np.float32(16)).astype(np.int32)
    large = np.minimum(large, 31)
    return np.where(n < 16, n, large)


def nsa_host_consts():
    c = {}
    i = np.arange(LW); n = i - 511
    oh = np.zeros((33, LW), np.float32)
    ok = (n >= 0) & (n < 512)
    oh[t5_bucket_np(n)[ok], i[ok]] = 1.0
    oh[32, ~ok] = NEG8
    c["oh_w"] = oh
    i = np.arange(LC); n = i - 2063
    oh = np.zeros((33, LC), np.float32)
    ok = n >= 0
    oh[t5_bucket_np(n)[ok], i[ok]] = 1.0
    oh[32, ~ok] = NEG8
    c["oh_c"] = oh
    E = np.zeros((128, 32, 128), np.float32)
    for kt in range(32):
        E[2 * kt, kt, 0:64] = 1.0
        E[2 * kt + 1, kt, 64:128] = 1.0
    c["E_blk"] = E.astype(NPBF)
    cs = np.arange(256) * 16
    ce = cs + 31
    ss = np.arange(64) * 64
    ov = ((cs[:, None] <= ss[None, :] + 63) & (ce[:, None] >= ss[None, :])).astype(np.float32)
    ov[255] = 0.0
    c["ovl"] = np.ascontiguousarray(ov.reshape(2, 128, 64).transpose(1, 0, 2)).astype(NPBF)
    t = np.arange(S_LEN)
    cur = t // 64
    jb = np.arange(64)
    back = cur[:, None] - jb[None, :]
    valid = back >= 0
    forced = (jb[None, :] == 0) | (valid & (back < 2))
    tkm = (valid & ~forced).astype(np.float32)
    tka = np.where(valid, np.where(forced, 1e4, 0.0), -1.0).astype(np.float32)
    c["tkm"] = np.ascontiguousarray(tkm.reshape(32, 128, 64).transpose(1, 0, 2)).astype(NPBF)
    c["tka"] = np.ascontiguousarray(tka.reshape(32, 128, 64).transpose(1, 0, 2)).astype(NPBF)
    return c


def setup_nsa(k):
    nc = k.nc
    k.rel_bias = k.inp("rel_bias", [32, 6])
    k.oh_w = k.inp("oh_w", [33, LW])
    k.oh_c = k.inp("oh_c", [33, LC])
    k.E_d = k.inp("E_blk", [128, 32, 128], BF16)
    k.ovl_d = k.inp("ovl", [128, 2, 64], BF16)
    k.tkm_d = k.inp("tkm", [128, 32, 64], BF16)
    k.tka_d = k.inp("tka", [128, 32, 64], BF16)
    k.pe_kT = k.inp("nsa_pe_kT", [DEPTH, 64, 32])
    k.pe_vT = k.inp("nsa_pe_vT", [DEPTH, 64, 32])
    k.ck_w1 = k.inp("nsa_ck_w1", [DEPTH, 2048, 128])
    k.cv_w1 = k.inp("nsa_cv_w1", [DEPTH, 2048, 128])
    k.ck_w2 = k.inp("nsa_ck_w2", [DEPTH, 128, 64])
    k.cv_w2 = k.inp("nsa_cv_w2", [DEPTH, 128, 64])
    k.WVW = k.scratch("WVW", [6, 128, LW], BF16)
    k.WVC = k.scratch("WVC", [6, 128, LC], BF16)
    with Scope(k) as sc:
        rb = sc.sb("rb", [33, 6], F32)
        rb31 = sc.sb("rb31", [32, 6], F32)
        rrep = sc.sb("rrep", [33, 6, 128], F32)
        ohw = sc.sb("ohw", [33, LW], F32)
        ohc = sc.sb("ohc", [33, LC], F32)
        ps = [sc.ps("psb%d" % i, [128, 512]) for i in range(2)]
        ev = [sc.sb("evb%d" % i, [128, 512], BF16) for i in range(2)]
        k.dma('sp', rb[0:32, :], k.rel_bias, w=[rb])
        k.dma('sp', rb31[:], k.rel_bias[31:32, :].broadcast_to([32, 6]), w=[rb31])
        k.dma('sp', ohw[:], k.oh_w, w=[ohw])
        k.dma('sp', ohc[:], k.oh_c, w=[ohc])
        k.S.op('dve', lambda: nc.vector.memset(rb[32:33, :], 1.0), [], [rb])
        k.tt('dve', rb[0:32, :], rb[0:32, :], rb31[:], ALU.subtract, r=[rb, rb31], w=[rb])
        k.ts('dve', rb[0:32, :], rb[0:32, :], 8.0, None, ALU.mult, None, r=[rb], w=[rb])
        for h in range(6):
            k.copy('dve', rrep[:, h, :], rb[:, h:h + 1].to_broadcast([33, 128]), r=[rb], w=[rrep])
        n = 0
        for h in range(6):
            for (oh, L, dst) in ((ohw, LW, k.WVW), (ohc, LC, k.WVC)):
                for c0 in range(0, L, 512):
                    p_ = ps[n % 2]; e_ = ev[n % 2]; n += 1
                    k.mm(p_[:], [(rrep[:, h, :], oh[:, c0:c0 + 512])], r=[rrep, oh], w=[p_])
                    k.copy('act' if n % 2 else 'dve', e_[:], p_[:], r=[p_], w=[e_])
                    k.dma('sp', dst[h, :, c0:c0 + 512], e_[:], r=[e_])


class DbgStop(Exception):
    pass


def dbg(k, lvl):
    if getattr(k, 'dbg_stop', None) == lvl:
        raise DbgStop()


def stage_nsa(k, l):
    nc = k.nc
    with Scope(k) as sc:
        Gw = sc.sb("Gw", [128, 6, 1408], BF16)
        Gc = sc.sb("Gc", [128, 6, 2560], BF16)
        E = sc.sb("E", [128, 32, 128], BF16)
        tkm = sc.sb("tkm", [128, 32, 64], BF16)
        tka = sc.sb("tka", [128, 32, 64], BF16)
        QC = [sc.sb("QC%d" % c, [128, S_LEN], BF16) for c in range(3)]
        KS = sc.sb("KS", [128, S_LEN], BF16)
        KW = sc.sb("KW", [128, S_LEN], BF16)
        VS = sc.sb("VS", [128, 32, 2, 65], BF16)
        VW = sc.sb("VW", [128, 32, 2, 65], BF16)
        KCMP = sc.sb("KCMP", [128, 256], BF16)
        VE = sc.sb("VE", [128, 2, 2, 129], BF16)
        sg = sc.sb("sg", [128, 32, 18], F32)
        for h in range(6):
            k.dma('sp', Gw[:, h, :], bass.AP(k.WVW.tensor, h * 128 * LW + 127, [[LW - 1, 128], [1, 1408]]), w=[Gw])
            k.dma('sp', Gc[:, h, :], bass.AP(k.WVC.tensor, h * 128 * LC + 2032, [[LC - 16, 128], [1, 2560]]), w=[Gc])
        k.dma('sp', E[:], k.E_d, w=[E])
        k.dma('sp', tkm[:], k.tkm_d, w=[tkm])
        k.dma('sp', tka[:], k.tka_d, w=[tka])
        for c in range(3):
            k.dma('sp', QC[c][:], k.QKT[768 + c * 128:768 + (c + 1) * 128, :], w=[QC[c]])
        k.dma('sp', KS[:], k.QKT[1408:1536, :], w=[KS])
        k.dma('sp', KW[:], k.QKT[1536:1664, :], w=[KW])
        k.dma('sp', sg[:], k.GT.rearrange("(t p) c -> p t c", p=128), w=[sg])
        k.act(sg[:], sg[:], AF.Exp, r=[sg], w=[sg], scale=-1.0)
        k.ts('dve', sg[:], sg[:], 1.0, None, ALU.add, None, r=[sg], w=[sg])
        k.S.op('dve', lambda: nc.vector.reciprocal(out=sg[:], in_=sg[:]), [sg], [sg])
        k.dma('sp', VE[:, 0, :, 65:129], k.ovl_d, w=[VE])
        k.dma('sp', VE[:, 1, :, 65:129], k.ovl_d, w=[VE])
        k.S.op('pool', lambda: nc.gpsimd.memset(VE[:, :, :, 64:65], 1.0), [], [VE])
        k.S.op('pool', lambda: nc.gpsimd.memset(VE[:, :, :, 0:64], 0.0), [], [VE])
        k.S.op('pool', lambda: nc.gpsimd.memset(KCMP[:], 0.0), [], [KCMP])
        dbg(k, 1)
        with Scope(k) as s2:
            vst = s2.sb("vst", [128, 32, 256], BF16)
            k.dma('sp', vst[:], k.VT[:, 384:640].rearrange("(t p) c -> p t c", p=128), w=[vst])
            k.S.op('pool', lambda: nc.gpsimd.memset(VS[:, :, :, 64:65], 1.0), [], [VS])
            k.S.op('pool', lambda: nc.gpsimd.memset(VW[:, :, :, 64:65], 1.0), [], [VW])
            k.copy('pool', VS[:, :, :, 0:64], vst[:, :, 0:128].rearrange("p t (g d) -> p t g d", g=2), r=[vst], w=[VS])
            k.copy('pool', VW[:, :, :, 0:64], vst[:, :, 128:256].rearrange("p t (g d) -> p t g d", g=2), r=[vst], w=[VW])
        dbg(k, 2)
        with Scope(k) as s2:
            KC = s2.sb("KC", [128, S_LEN], BF16)
            VC = s2.sb("VC", [128, S_LEN], BF16)
            k.dma('sp', KC[:], k.QKT[1152:1280, :], w=[KC])
            k.dma('sp', VC[:], k.QKT[1280:1408, :], w=[VC])
            w1s = s2.sb("w1s", [128, 16, 128], F32)
            w1b = [s2.sb("w1b%d" % i, [128, 32, 128], BF16) for i in range(2)]
            w2s = s2.sb("w2s", [128, 2, 64], F32)
            w2b = s2.sb("w2b", [128, 2, 64], BF16)
            pes = s2.sb("pes", [128, 2, 32], F32)
            peb = s2.sb("peb", [128, 2, 32], BF16)
            hb = s2.sb("hb", [128, 2], F32)
            gx = s2.sb("gx", [128, 256], F32)
            gu = s2.sb("gu", [128, 256], F32)
            gg = s2.sb("gg", [128, 256], BF16)
            psh = s2.ps("psh", [128, 512])
            psb_ = s2.ps("pshb", [128, 512])
            pso = s2.ps("pso", [128, 512])
            for kv, (w1d, w2d, ped) in enumerate(((k.ck_w1, k.ck_w2, k.pe_kT), (k.cv_w1, k.cv_w2, k.pe_vT))):
                for lh in range(2):
                    for half in range(2):
                        k.dma('sp', w1s[half * 64:(half + 1) * 64, :, :],
                              w1d[l, lh * 1024:(lh + 1) * 1024, :].rearrange("(l d) h -> d l h", d=64), w=[w1s])
                    k.copy('pool', w1b[kv][:, lh * 16:(lh + 1) * 16, :], w1s[:], r=[w1s], w=[w1b[kv]])
                for half in range(2):
                    k.dma('sp', pes[half * 64:(half + 1) * 64, kv, :], ped[l], w=[pes])
                k.dma('sp', w2s[:, kv, :], w2d[l], w=[w2s])
            k.copy('dve', w2b[:], w2s[:], r=[w2s], w=[w2b])
            w2kd = s2.sb("w2kd", [128, 2, 64], BF16)
            for a_ in range(2):
                k.copy('dve', w2kd[:, a_, :], w2s[:, 0, :], r=[w2s], w=[w2kd])
            k.copy('dve', peb[:], pes[:], r=[pes], w=[peb])
            for kv in range(2):
                src = KC if kv == 0 else VC
                k.mm(psb_[:, kv:kv + 1], [(w1b[kv][0:64, li, :], peb[0:64, kv, li:li + 1]) for li in range(32)],
                     r=[w1b[kv], peb], w=[psb_], start=True)
                k.copy('dve', hb[:, kv:kv + 1], psb_[:, kv:kv + 1], r=[psb_], w=[hb])
                for g in range(2):
                    pr = slice(g * 64, (g + 1) * 64)
                    k.mm(psh[:, 0:255], [(w1b[kv][pr, li, :], src[pr, li:li + 16 * 254 + 1:16]) for li in range(32)],
                         r=[w1b[kv], src], w=[psh])
                    k.ts('dve', gx[:, 0:255], psh[:, 0:255], hb[:, kv:kv + 1], None, ALU.add, None, r=[psh, hb], w=[gx])
                    k.tt('dve', gu[:, 0:255], gx[:, 0:255], gx[:, 0:255], ALU.mult, r=[gx], w=[gu])
                    k.ts('dve', gu[:, 0:255], gu[:, 0:255], 0.044715, 1.0, ALU.mult, ALU.add, r=[gu], w=[gu])
                    k.tt('dve', gu[:, 0:255], gu[:, 0:255], gx[:, 0:255], ALU.mult, r=[gu, gx], w=[gu])
                    k.act(gu[:, 0:255], gu[:, 0:255], AF.Exp, r=[gu], w=[gu], scale=-2.0 * 0.7978845608028654)
                    k.ts('dve', gu[:, 0:255], gu[:, 0:255], 1.0, None, ALU.add, None, r=[gu], w=[gu])
                    k.S.op('dve', lambda: nc.vector.reciprocal(out=gu[:, 0:255], in_=gu[:, 0:255]), [gu], [gu])
                    k.S.op('dve', lambda: nc.vector.memset(gg[:, 255:256], 0.0), [], [gg])
                    k.tt('dve', gg[:, 0:255], gu[:, 0:255], gx[:, 0:255], ALU.mult, r=[gu, gx], w=[gg])
                    if kv == 0:
                        k.mm(pso[:, 0:256], [(w2kd[:].rearrange("p a d -> p (a d)"), gg[:, 0:256])], r=[w2kd, gg], w=[pso])
                        k.copy('dve', KCMP[pr, :], pso[pr, 0:256], r=[pso], w=[KCMP])
                    else:
                        for ct in range(2):
                            k.mm(pso[:, ct * 64:(ct + 1) * 64], [(gg[:, ct * 128:(ct + 1) * 128], w2b[:, 1, :])],
                                 r=[w2b, gg], w=[pso], start=(ct == 0))
                        k.copy('dve', VE[:, g, :, 0:64], pso[:, 0:128].rearrange("p (c d) -> p c d", c=2), r=[pso], w=[VE])
        dbg(k, 3)
        NM = [sc.sb("NM%d" % g, [128, 512], BF16) for g in range(2)]
        for g in range(2):
            k.S.op('pool', lambda g=g: nc.gpsimd.memset(NM[g][:], 0.0), [], [NM[g]])
        PTl = [sc.sb("PTn%d" % i, [128, 512], BF16) for i in range(6)]
        yacc = [sc.sb("yacc%d" % i, [128, 4, 384], F32) for i in range(2)]
        ybf = [sc.sb("ybf%d" % i, [128, 4, 384], BF16) for i in range(2)]
        impt = [sc.sb("impt%d" % g, [128, 4, 64], F32) for g in range(2)]
        scr = sc.sb("scr", [128, 4, 64], F32)
        wk = sc.sb("wk", [128, 4, 64], F32)
        m8 = sc.sb("m8", [128, 4, 16], F32)
        nmq = sc.sb("nmq", [128, 4, 64], BF16)
        rcs = [sc.sb("rcs%d" % i, [128, 8], F32) for i in range(3)]
        psS = [sc.ps("psS%d" % i, [128, 512]) for i in range(4)]
        psO = [sc.ps("psO%d" % i, [128, 512]) for i in range(3)]
        psT = sc.ps("psTn", [128, 1024], BF16)
        st = {"S": 0, "O": 0, "P": 0, "R": 0}

        def q_ap(h, c0, c1):
            g, hp = h // 3, h % 3
            return QC[hp][g * 64:(g + 1) * 64, c0:c1]

        def evac(views, h, branch, qb, ya, first):
            r_ = rcs[st["R"] % 3]; st["R"] += 1
            for qs, (po, cb) in enumerate(views):
                if branch == 0:
                    k.ts('dve', r_[:, qs:qs + 1], po[:, cb + 64:cb + 65], 1e-30, None, ALU.max, None, r=[po], w=[r_])
                    k.S.op('dve', lambda r_=r_, qs=qs: nc.vector.reciprocal(out=r_[:, qs:qs + 1], in_=r_[:, qs:qs + 1]), [r_], [r_])
                else:
                    k.S.op('dve', lambda r_=r_, po=po, cb=cb, qs=qs: nc.vector.reciprocal(out=r_[:, qs:qs + 1], in_=po[:, cb + 64:cb + 65]), [po], [r_])
            k.tt('dve', r_[:, 4:8], r_[:, 0:4], sg[:, qb * 4:(qb + 1) * 4, h * 3 + branch], ALU.mult, r=[r_, sg], w=[r_])
            for qs, (po, cb) in enumerate(views):
                o = ya[:, qs, h * 64:(h + 1) * 64]
                if first:
                    k.ts('dve', o, po[:, cb:cb + 64], r_[:, 4 + qs:5 + qs], None, ALU.mult, None, r=[po, r_], w=[ya])
                else:
                    k.stt('dve', o, po[:, cb:cb + 64], r_[:, 4 + qs:5 + qs], o, ALU.mult, ALU.add, r=[po, r_, ya], w=[ya])
            return r_

        pipe = Pipe(2)

        def attend(h, qb, tiles, kmat, vmat, po, g, branch, ya):
            hp = h % 3
            nt = len(tiles)
            state = {"first": True}
            for idx, (kt, c0, c1, extra) in enumerate(tiles):
                ps = psS[st["S"] % len(psS)]; st["S"] += 1
                pt = PTl[st["P"] % len(PTl)]; st["P"] += 1

                def first(ps=ps, pt=pt, kt=kt, c0=c0, c1=c1, extra=extra):
                    fns = [lambda: nc.tensor.matmul(ps[:, c0:c1], lhsT=kmat[g * 64:(g + 1) * 64, kt * 128:(kt + 1) * 128],
                                                    rhs=QC[hp][g * 64:(g + 1) * 64, qb * 512 + c0:qb * 512 + c1],
                                                    start=True, stop=(len(extra) == 0), skip_group_check=True)]
                    rd = [kmat, QC[hp]]
                    for ei, (lt, rt, lap, rap) in enumerate(extra):
                        w_ = rap.shape[-1]
                        fns.append(lambda lap=lap, rap=rap, w_=w_, ei=ei: nc.tensor.matmul(
                            ps[:, c0:c0 + w_], lhsT=lap, rhs=rap, start=False, stop=(ei == len(extra) - 1), skip_group_check=True))
                        rd += [lt, rt]
                    k.S.pe_group(fns, rd, [ps])
                    k.act(pt[:, c0:c1], ps[:, c0:c1], AF.Exp, r=[ps], w=[pt], scale=0.125)

                def second(pt=pt, kt=kt, c0=c0, c1=c1, idx=idx):
                    fns = []
                    for qs in range(c0 // 128, (c1 + 127) // 128):
                        last = all(not (t2[1] <= qs * 128 < t2[2]) for t2 in tiles[idx + 1:])
                        fo = state["first"]
                        state["first"] = False
                        fns.append(lambda qs=qs, fo=fo, last=last: nc.tensor.matmul(
                            po[:, qs * 65:(qs + 1) * 65], lhsT=pt[:, qs * 128:(qs + 1) * 128], rhs=vmat[:, kt, g, :],
                            start=fo, stop=last, skip_group_check=True))
                    k.S.pe_group(fns, [pt, vmat], [po])
                    if idx == nt - 1:
                        evac([(po, qs * 65) for qs in range(4)], h, branch, qb, ya, False)
                pipe.push(first, second)

        for qb in getattr(k, 'dbg_qbs', range(NTB)):
            ya = yacc[qb % 2]
            for h in range(6):
                g = h // 3
                poA = psO[st["O"] % 3]; st["O"] += 1
                poB = psO[st["O"] % 3]; st["O"] += 1
                cts = [0] + ([1] if qb >= 4 else [])
                state = {"A": True, "B": True}
                for ct in cts:
                    delta = 512 * qb - 2048 * ct
                    ps = psS[st["S"] % len(psS)]; st["S"] += 1
                    pt = PTl[st["P"] % len(PTl)]; st["P"] += 1

                    def first(ps=ps, pt=pt, ct=ct, delta=delta, g=g, h=h):
                        pairs = [(KCMP[g * 64:(g + 1) * 64, ct * 128:(ct + 1) * 128], q_ap(h, qb * 512, (qb + 1) * 512))]
                        rd = [KCMP, QC[h % 3]]
                        if delta < 2560:
                            pairs.append((k.ident_bf[:], Gc[:, h, delta:delta + 512])); rd += [k.ident_bf, Gc]
                        k.mm(ps[:], pairs, r=rd, w=[ps])
                        k.act(pt[:], ps[:], AF.Exp, r=[ps], w=[pt], scale=0.125)

                    def second(pt=pt, ct=ct, g=g, h=h, poA=poA, poB=poB, state=state, lastct=(ct == cts[-1])):
                        fns = []
                        for qs in range(4):
                            po, cb = (poA, qs * 129) if qs < 3 else (poB, 0)
                            key = "A" if qs < 3 else "B"
                            stt_ = state[key]
                            state[key] = False
                            fns.append(lambda qs=qs, po=po, cb=cb, stt_=stt_: nc.tensor.matmul(
                                po[:, cb:cb + 129], lhsT=pt[:, qs * 128:(qs + 1) * 128], rhs=VE[:, g, ct, :],
                                start=stt_, stop=lastct, skip_group_check=True))
                        k.S.pe_group(fns, [pt, VE], [poA, poB])
                        if lastct:
                            views = [(poA, 0), (poA, 129), (poA, 258), (poB, 0)]
                            r_ = evac(views, h, 0, qb, ya, True)
                            for qs, (po, cb) in enumerate(views):
                                o = impt[g][:, qs, :]
                                if h % 3 == 0:
                                    k.ts('dve', o, po[:, cb + 65:cb + 129], r_[:, qs:qs + 1], None, ALU.mult, None, r=[po, r_], w=[impt[g]])
                                else:
                                    k.stt('dve', o, po[:, cb + 65:cb + 129], r_[:, qs:qs + 1], o, ALU.mult, ALU.add, r=[po, r_, impt[g]], w=[impt[g]])
                    pipe.push(first, second)
            pipe.flush()
            dbg(k, 4)
            for g in range(2):
                k.tt('dve', scr[:], impt[g][:], tkm[:, qb * 4:(qb + 1) * 4, :], ALU.mult, r=[impt[g], tkm], w=[scr])
                k.tt('dve', scr[:], scr[:], tka[:, qb * 4:(qb + 1) * 4, :], ALU.add, r=[scr, tka], w=[scr])
                for qs in range(4):
                    k.S.op('dve', lambda qs=qs: nc.vector.max(out=m8[:, qs, 0:8], in_=scr[:, qs, :]), [scr], [m8])
                    k.S.op('dve', lambda qs=qs: nc.vector.match_replace(out=wk[:, qs, :], in_to_replace=m8[:, qs, 0:8],
                                                                        in_values=scr[:, qs, :], imm_value=-1e9), [scr, m8], [wk])
                    k.S.op('dve', lambda qs=qs: nc.vector.max(out=m8[:, qs, 8:16], in_=wk[:, qs, :]), [wk], [m8])
                    k.ts('dve', wk[:, qs, :], scr[:, qs, :], m8[:, qs, 15:16], 1.0, ALU.is_ge, ALU.subtract, r=[scr, m8, wk], w=[wk])
                k.ts('dve', nmq[:], wk[:], -NEG8, None, ALU.mult, None, r=[wk], w=[nmq])
                for qs in range(4):
                    k.transpose(psT[0:64, qs * 128:(qs + 1) * 128], nmq[:, qs, :], k.ident_bf[:], r=[nmq, k.ident_bf], w=[psT])
                k.copy('dve', NM[g][0:64, :], psT[0:64, 0:512], r=[psT], w=[NM[g]])
            for h in range(6):
                g = h // 3
                po = psO[st["O"] % 3]; st["O"] += 1
                tiles = []
                for kt in range(max(0, 4 * qb - 4), 4 * qb + 4):
                    delta = 512 * qb - 128 * kt
                    c0 = max(-delta, 0)
                    c1 = min(512, 640 - delta) if delta > 0 else 512
                    tiles.append((kt, c0, c1, [(k.ident_bf, Gw, k.ident_bf[:], Gw[:, h, delta + 384 + c0:delta + 384 + c1])]))
                attend(h, qb, tiles, KW, VW, po, g, 2, ya)
            for h in range(6):
                g = h // 3
                po = psO[st["O"] % 3]; st["O"] += 1
                tiles = []
                for kt in range(0, 4 * qb + 4):
                    delta = 512 * qb - 128 * kt
                    c0 = max(-delta, 0)
                    ex = [(E, NM[g], E[:, kt, :], NM[g][:, c0:512])]
                    if delta <= 128:
                        c1b = 256 if delta == 128 else 512
                        ex.append((k.ident_bf, Gw, k.ident_bf[:], Gw[:, h, delta + 384 + c0:delta + 384 + c1b]))
                    tiles.append((kt, c0, 512, ex))
                attend(h, qb, tiles, KS, VS, po, g, 1, ya)
            pipe.flush()
            yb_ = ybf[qb % 2]
            k.copy('pool', yb_[:], ya[:], r=[ya], w=[yb_])
            k.dma('sp', k.Y[qb * 512:(qb + 1) * 512, 640:1024].rearrange("(q p) c -> p q c", p=128), yb_[:], r=[yb_])
            dbg(k, 100 + qb)


def setup_ffn(k):
    k.w_out = k.inp("w_out", [DEPTH, D, D])
    k.ffn_up = k.inp("ffn_up", [DEPTH, D, 2 * D_FF])
    k.ffn_down = k.inp("ffn_down", [DEPTH, D_FF, D])
    k.conv_w = k.inp("conv_w_fm", [DEPTH, 128, 3, 44])
    k.conv_b = k.inp("conv_b_fm", [DEPTH, 128, 44])


def load_cast(k, sc, dst, src_rows, ncols, nchunks, name, col_split=1):
    w = ncols // col_split
    stg = [sc.sb("%s_stg%d" % (name, i), [128, w], F32) for i in range(4)]
    n = 0
    for c in range(nchunks):
        for cs in range(col_split):
            s = stg[n % 4]
            k.dma('sp' if n % 2 == 0 else 'act', s[:], src_rows(c)[:, cs * w:(cs + 1) * w], w=[s])
            k.copy('pool' if n % 2 == 0 else 'dve', dst[:, c, cs * w:(cs + 1) * w], s[:], r=[s], w=[dst])
            n += 1


def rms_scale(k, ss, st):
    k.act(st[:, 0:1], ss, AF.Ln, r=[st], w=[st], bias=RMS_EPS)
    k.act(st[:, 1:2], st[:, 0:1], AF.Exp, r=[st], w=[st], scale=-0.5)


def stage_out(k, l, xsrc, xdst):
    nc = k.nc
    with Scope(k) as sc:
        wo = sc.sb("wo", [128, 8, D], BF16)
        with Scope(k) as s2:
            load_cast(k, s2, wo, lambda c: k.w_out[l, c * 128:(c + 1) * 128, :], D, 8, "wo")
        yt = [sc.sb("yt%d" % i, [128, D], BF16) for i in range(2)]
        yT = [sc.sb("yT%d" % i, [128, 8, 128], BF16) for i in range(2)]
        xt = [sc.sb("xo%d" % i, [128, D], F32) for i in range(2)]
        tt_ = [sc.sb("to%d" % i, [128, D], F32) for i in range(2)]
        junk = sc.sb("junko", [128, 512], BF16)
        st = [sc.sb("sto%d" % i, [128, 4], F32) for i in range(2)]
        psT = [sc.ps("psTo%d" % i, [128, D], BF16) for i in range(2)]
        psY = [sc.ps("psYo%d" % i, [128, 512]) for i in range(4)]
        def T(ti):
            tok = slice(ti * 128, (ti + 1) * 128)
            y_ = yt[ti % 2]; yT_ = yT[ti % 2]; x_ = xt[ti % 2]; pT = psT[ti % 2]
            k.dma('act', y_[:], k.Y[tok, :], w=[y_])
            k.dma('act', x_[:], xsrc[tok, :], w=[x_])
            for kc in range(8):
                k.transpose(pT[:, kc * 128:(kc + 1) * 128], y_[:, kc * 128:(kc + 1) * 128], k.ident_bf[:], r=[y_, k.ident_bf], w=[pT])
            k.copy('act' if ti % 2 else 'dve', yT_[:].rearrange("p a b -> p (a b)"), pT[:], r=[pT], w=[yT_])

        def M(ti):
            tok = slice(ti * 128, (ti + 1) * 128)
            yT_ = yT[ti % 2]; x_ = xt[ti % 2]; t_ = tt_[ti % 2]; st_ = st[ti % 2]
            p0 = psY[(ti % 2) * 2]; p1 = psY[(ti % 2) * 2 + 1]
            for half, ps in enumerate((p0, p1)):
                k.mm(ps[:], [(yT_[:, kc, :], wo[:, kc, half * 512:(half + 1) * 512]) for kc in range(8)], r=[yT_, wo], w=[ps])
                k.act(junk[:], ps[:], AF.Square, r=[ps], w=[junk, st_], scale=1.0 / 32.0, accum=st_[:, 2 + half:3 + half])
            k.tt('dve', st_[:, 2:3], st_[:, 2:3], st_[:, 3:4], ALU.add, r=[st_], w=[st_])
            rms_scale(k, st_[:, 2:3], st_)
            for half, ps in enumerate((p0, p1)):
                cs = slice(half * 512, (half + 1) * 512)
                k.stt('dve', t_[:, cs], ps[:], st_[:, 1:2], k.gm_row[:, cs], ALU.mult, ALU.mult, r=[ps, st_, k.gm_row], w=[t_])
            k.tt('pool', t_[:], t_[:], x_[:], ALU.add, r=[t_, x_], w=[t_])
            k.dma('sp', xdst[tok, :], t_[:], r=[t_])

        T(0)
        for ti in range(32):
            if ti + 1 < 32:
                T(ti + 1)
            M(ti)


def stage_ffn(k, l, xsrc, xdst):
    nc = k.nc
    NCH = 22
    with Scope(k) as sc:
        wu = sc.sb("wu", [128, 8, 2 * D_FF], BF16)
        wd = sc.sb("wd", [128, NCH, D], BF16)
        with Scope(k) as s2:
            load_cast(k, s2, wu, lambda c: k.ffn_up[l, c * 128:(c + 1) * 128, :], 2 * D_FF, 8, "wu", col_split=2)
            load_cast(k, s2, wd, lambda c: k.ffn_down[l, c * 128:(c + 1) * 128, :], D, NCH, "wd")
        cw = sc.sb("cw", [128, 3, 44], F32)
        cb = sc.sb("cb", [128, 44], F32)
        hal = [sc.sb("hal%d" % i, [128, 44, 2], F32) for i in range(2)]
        k.dma('sp', cw[:], k.conv_w[l], w=[cw])
        k.dma('sp', cb[:], k.conv_b[l], w=[cb])
        k.S.op('pool', lambda: nc.gpsimd.memset(hal[1][:], 0.0), [], [hal[1]])
        actT = sc.sb("actT", [128, NCH, 512], BF16)
        HTs = [sc.sb("H2T%d" % i, [128, 8, 512], BF16) for i in range(2)]
        xt = [sc.sb("xf%d" % i, [128, D], F32) for i in range(2)]
        xn = sc.sb("xnf", [128, D], BF16)
        junk = sc.sb("junkf", [128, 512], BF16)
        st = [sc.sb("stf%d" % i, [128, 4], F32) for i in range(2)]
        Tg = [sc.sb("Tg%d" % i, [128, 512], F32) for i in range(2)]
        Tv = [sc.sb("Tv%d" % i, [128, 512], F32) for i in range(2)]
        psT = sc.ps("psTf", [128, D], BF16)
        psU = [sc.ps("psU%d" % i, [128, 512]) for i in range(4)]
        nxc = {"n": 0}

        def norm_gen(tb):
            HT = HTs[tb % 2]
            t0_ = tb * 4
            k.dma('act', xt[t0_ % 2][:], xsrc[t0_ * 128:(t0_ + 1) * 128, :], w=[xt[t0_ % 2]])
            yield
            for sub in range(4):
                ti = tb * 4 + sub
                x_ = xt[ti % 2]; st_ = st[ti % 2]
                k.act(xn[:], x_[:], AF.Square, r=[x_], w=[xn, st_], scale=1.0 / 32.0, accum=st_[:, 2:3])
                rms_scale(k, st_[:, 2:3], st_)
                k.ts('dve', xn[:], x_[:], st_[:, 1:2], None, ALU.mult, None, r=[x_, st_], w=[xn])
                if sub < 3:
                    k.dma('act', xt[(ti + 1) % 2][:], xsrc[(ti + 1) * 128:(ti + 2) * 128, :], w=[xt[(ti + 1) % 2]])
                yield
                yield
                for kc in range(8):
                    k.transpose(psT[:, kc * 128:(kc + 1) * 128], xn[:, kc * 128:(kc + 1) * 128], k.ident_bf[:], r=[xn, k.ident_bf], w=[psT])
                    if kc == 3:
                        yield
                yield
                for kc in range(8):
                    o = HT[:, kc, sub * 128:(sub + 1) * 128]
                    i_ = psT[:, kc * 128:(kc + 1) * 128]
                    if kc % 2 == 0:
                        k.ts('dve', o, i_, k.modAB[:, 16 + kc:17 + kc], k.modAB[:, 24 + kc:25 + kc], ALU.mult, ALU.add, r=[psT, k.modAB], w=[HT])
                    else:
                        k.act(o, i_, AF.Identity, r=[psT, k.modAB], w=[HT], scale=k.modAB[:, 16 + kc:17 + kc], bias=k.modAB[:, 24 + kc:25 + kc])
                yield

        def step(g):
            if g is not None:
                try:
                    next(g)
                except StopIteration:
                    return None
            return g

        xe = [sc.sb("xe%d" % i, [128, D], F32) for i in range(2)]
        ste = [sc.sb("ste%d" % i, [128, 4], F32) for i in range(2)]
        for _ in norm_gen(0):
            pass
        for tb in range(NTB):
            hin = hal[(tb + 1) % 2]; hout = hal[tb % 2]
            HT = HTs[tb % 2]
            g = norm_gen(tb + 1) if tb + 1 < NTB else None
            for cp in range(NCH):
                tg = Tg[cp % 2]; tv = Tv[cp % 2]
                for which, (T_, c_) in enumerate(((tg, cp), (tv, NCH + cp))):
                    ps = psU[(cp * 2 + which) % 4]
                    k.mm(ps[:], [(wu[:, kc, c_ * 128:(c_ + 1) * 128], HT[:, kc, :]) for kc in range(8)], r=[wu, HT], w=[ps])
                    k.act(T_[:], ps[:], AF.Identity, r=[ps, cw, cb], w=[T_], scale=cw[:, 2, c_:c_ + 1], bias=cb[:, c_:c_ + 1])
                    k.stt('dve', T_[:, 1:512], ps[:, 0:511], cw[:, 1, c_:c_ + 1], T_[:, 1:512], ALU.mult, ALU.add, r=[ps, cw, T_], w=[T_])
                    k.stt('dve', T_[:, 2:512], ps[:, 0:510], cw[:, 0, c_:c_ + 1], T_[:, 2:512], ALU.mult, ALU.add, r=[ps, cw, T_], w=[T_])
                    k.copy('act', hout[:, c_, :], ps[:, 510:512], r=[ps], w=[hout])
                    k.stt('dve', T_[:, 0:1], hin[:, c_, 1:2], cw[:, 1, c_:c_ + 1], T_[:, 0:1], ALU.mult, ALU.add, r=[hin, cw, T_], w=[T_])
                    k.stt('dve', T_[:, 0:2], hin[:, c_, 0:2], cw[:, 0, c_:c_ + 1], T_[:, 0:2], ALU.mult, ALU.add, r=[hin, cw, T_], w=[T_])
                k.act(tg[:], tg[:], AF.Silu, r=[tg], w=[tg])
                k.tt('pool', actT[:, cp, :], tg[:], tv[:], ALU.mult, r=[tg, tv], w=[actT])
                if cp >= 1:
                    g = step(g)
            for sub in range(4):
                ti = tb * 4 + sub
                tok = slice(ti * 128, (ti + 1) * 128)
                x_ = xe[sub % 2]; st_ = ste[sub % 2]
                k.dma('act', x_[:], xsrc[tok, :], w=[x_])
                psF = [psU[(2 * sub) % 4], psU[(2 * sub + 1) % 4]]
                for half in range(2):
                    ps = psF[half]
                    k.mm(ps[:], [(actT[:, cp, sub * 128:(sub + 1) * 128], wd[:, cp, half * 512:(half + 1) * 512]) for cp in range(NCH)],
                         r=[actT, wd], w=[ps])
                    k.act(junk[:, 0:512], ps[:], AF.Square, r=[ps], w=[junk, st_], scale=1.0 / 32.0, accum=st_[:, 2 + half:3 + half])
                k.tt('dve', st_[:, 2:3], st_[:, 2:3], st_[:, 3:4], ALU.add, r=[st_], w=[st_])
                rms_scale(k, st_[:, 2:3], st_)
                t_ = Tg[sub % 2] if False else None
                for half in range(2):
                    cs = slice(half * 512, (half + 1) * 512)
                    T_ = (Tg if half == 0 else Tv)[sub % 2]
                    k.stt('dve', T_[:], psF[half][:], st_[:, 1:2], k.gf_row[:, cs], ALU.mult, ALU.mult, r=[psF[half], st_, k.gf_row], w=[T_])
                    k.tt('pool', x_[:, cs], x_[:, cs], T_[:], ALU.add, r=[x_, T_], w=[x_])
                k.dma('sp', xdst[tok, :], x_[:], r=[x_])
                g = step(g)
            while g is not None:
                g = step(g)


def rwkv_host(inp):
    f = lambda a: np.ascontiguousarray(np.asarray(a, dtype=np.float32))
    mu = np.asarray(inp["rwkv_mu"])
    hd = lambda v: np.asarray(v).reshape(DEPTH, 4, 64).transpose(0, 2, 1)
    pp = np.stack([hd(mu[:, 0:256]), hd(mu[:, 256:512]), hd(mu[:, 512:768]), hd(inp["rwkv_w0"]), hd(inp["rwkv_a0"]),
                   hd(inp["rwkv_k_k"]), hd(inp["rwkv_k_a"]), hd(np.asarray(inp["rwkv_r_k"]).reshape(DEPTH, 256))], axis=2)
    lr = np.zeros((DEPTH, 64, 3), np.float32)
    lr[:, 0:32, 0] = mu[:, 768:800]; lr[:, 0:32, 1] = mu[:, 800:832]; lr[:, :, 2] = mu[:, 832:896]
    i = np.arange(64)
    mk = np.stack([(i[:, None] < i[None, :]), (i[:, None] > i[None, :]), (i[:, None] <= i[None, :]), np.eye(64, dtype=bool)]).astype(np.float32)
    cm = np.ones((64, 512), np.float32); cm[:, ::64] = 0.0
    return {"rwkv_pp": f(pp), "rwkv_lr": f(lr), "rwkv_w_up": f(inp["rwkv_w_up"]), "rwkv_a_up": f(inp["rwkv_a_up"]),
            "rwkv_g_up": f(inp["rwkv_g_up"]), "rwkv_ln": f(np.stack([np.asarray(inp["rwkv_ln_w"]), np.asarray(inp["rwkv_ln_b"])], axis=1)),
            "rwkv_masks": f(mk.transpose(1, 0, 2)), "rwkv_cmask": cm}


def setup_rwkv(k):
    k.rw_pp = k.inp("rwkv_pp", [DEPTH, 64, 8, 4])
    k.rw_lr = k.inp("rwkv_lr", [DEPTH, 64, 3])
    k.rw_wup = k.inp("rwkv_w_up", [DEPTH, 32, 256])
    k.rw_aup = k.inp("rwkv_a_up", [DEPTH, 32, 256])
    k.rw_gup = k.inp("rwkv_g_up", [DEPTH, 64, 256])
    k.rw_ln = k.inp("rwkv_ln", [DEPTH, 2, 256])
    k.rw_masks = k.inp("rwkv_masks", [64, 4, 64])
    k.rw_cmask = k.inp("rwkv_cmask", [64, 512])


def stage_rwkv(k, l):
    nc = k.nc
    BL = 256
    NB = S_LEN // BL
    CPB = BL // 64
    H4 = [64, 4, BL]
    bc = lambda ap, shape: ap.to_broadcast(shape)
    with Scope(k) as sc:
        pp = sc.sb("pp", [64, 8, 4], F32)
        lr = sc.sb("lr", [64, 3], F32)
        wup = sc.sb("wup", [32, 256], F32); aup = sc.sb("aup", [32, 256], F32); gup = sc.sb("gup", [64, 256], F32)
        lnr = sc.sb("lnr", [64, 2, 256], F32)
        mk = sc.sb("mk", [64, 4, 64], F32)
        cmask = sc.sb("cmask", [64, BL], F32)
        ones = sc.sb("ones64", [64, 64], F32)
        prm = sc.sb("prm", [64, 4, 4], F32)
        k.dma('sp', pp[:], k.rw_pp[l], w=[pp]); k.dma('sp', lr[:], k.rw_lr[l], w=[lr])
        k.dma('sp', wup[:], k.rw_wup[l], w=[wup]); k.dma('sp', aup[:], k.rw_aup[l], w=[aup]); k.dma('sp', gup[:], k.rw_gup[l], w=[gup])
        for i in range(2):
            k.dma('sp', lnr[:, i, :], k.rw_ln[l, i:i + 1, :].broadcast_to([64, 256]), w=[lnr])
        k.dma('sp', mk[:], k.rw_masks, w=[mk]); k.dma('sp', cmask[:], k.rw_cmask[:, 0:BL], w=[cmask])
        k.S.op('pool', lambda: nc.gpsimd.memset(ones[:], 1.0), [], [ones])
        k.ts('dve', prm[:, 0, :], pp[:, 3, :], -1.0, None, ALU.mult, None, r=[pp], w=[prm])
        k.ts('dve', prm[:, 1, :], pp[:, 6, :], -1.0, 1.0, ALU.mult, ALU.add, r=[pp], w=[prm])
        P3 = sc.sb("P3", [64, 3, 4, BL], F32)
        halo = sc.sb("halo", [64, 3, 4], F32)
        LR = sc.sb("LR", [64, 3, BL], F32)
        halo2 = sc.sb("halo2", [64, 3], F32)
        ELW = sc.sb("ELW", H4, F32); SC_ = sc.sb("SCAN", H4, F32); AA = sc.sb("AA", H4, F32); KKN = sc.sb("KKN", H4, F32)
        T1 = sc.sb("T1", H4, F32); T2 = sc.sb("T2", H4, F32); CM4 = sc.sb("CM4", H4, F32)
        OUT = [{nm: sc.sb("%s%d" % (nm, i), H4, F32 if nm == "GAM" else BF16) for nm in ("AT", "BT", "KT", "RT", "RK", "GAM", "V")} for i in range(2)]
        SGs = [sc.sb("SG%d" % i, [64, BL], BF16) for i in range(2)]
        gupb = sc.sb("gupb", [64, 256], BF16)
        ppb = sc.sb("ppb", [64, 4], BF16)
        identb64 = k.ident_bf
        XY = [[sc.sb("XY%d_%d" % (i, j), [64, 2, 4, 64], BF16) for j in range(2)] for i in range(2)]
        PP = [[sc.sb("PPi%d_%d" % (i, j), [64, 4, 64], BF16) for j in range(2)] for i in range(2)]
        AKRK = [sc.sb("AKRK%d" % i, [64, 2, 4, 64], BF16) for i in range(2)]
        RBT = [sc.sb("RBT%d" % i, [64, 4, 64], BF16) for i in range(2)]
        TOK = [sc.sb("TOK%d" % i, [64, 3, 4, 64], BF16) for i in range(2)]
        Wsb = sc.sb("Wsb", [64, 4, 64], BF16); Usb = sc.sb("Usb", [64, 4, 64], BF16)
        Hs = [sc.sb("Hs%d" % i, [64, 4, 64], F32) for i in range(2)]
        Hb = [sc.sb("Hb%d" % i, [64, 4, 64], BF16) for i in range(2)]
        yc = sc.sb("yc", [64, 4, 64], F32); ysq = sc.sb("ysq", [64, 4, 64], F32)
        sm = sc.sb("sm", [64, 6, 4], F32)
        yab = [sc.sb("yab%d" % i, [64, CPB, 256], BF16) for i in range(2)]
        psA1 = sc.ps("psA1", [64, 512]); psA2 = sc.ps("psA2", [64, 512]); psA3 = sc.ps("psA3", [64, 512]); psA4 = sc.ps("psA4", [64, 512])
        psH = sc.ps("psHr", [64, 512]); psY = sc.ps("psYr", [64, 512]); psC = sc.ps("psCr", [64, 512]); psQ = sc.ps("psQr", [64, 512])
        k.S.op('pool', lambda: nc.gpsimd.memset(Hs[1][:], 0.0), [], [Hs[1]])
        k.S.op('pool', lambda: nc.gpsimd.memset(Hb[1][:], 0.0), [], [Hb[1]])
        k.copy('dve', gupb[:], gup[:], r=[gup], w=[gupb])
        k.copy('dve', ppb[:], pp[:, 7, :], r=[pp], w=[ppb])
        k.S.op('pool', lambda: nc.gpsimd.memset(halo[:], 0.0), [], [halo])
        k.S.op('pool', lambda: nc.gpsimd.memset(halo2[:], 0.0), [], [halo2])
        k.copy('dve', CM4[:], bc(cmask[:].unsqueeze(1), H4), r=[cmask], w=[CM4])
        E_ = BL - 1

        def prep(tb):
            O = OUT[tb % 2]; SG = SGs[tb % 2]
            AT, BT, KT, RT, RK, GAM, V_ = O["AT"], O["BT"], O["KT"], O["RT"], O["RK"], O["GAM"], O["V"]
            t0 = tb * BL
            for q in range(3):
                k.dma('act', P3[:, q, :, :], k.PT[q * 256:(q + 1) * 256, t0:t0 + BL].rearrange("(h d) t -> d h t", d=64), w=[P3])
            k.dma('act', LR[0:32, 0, :], k.PT[768:800, t0:t0 + BL], w=[LR])
            k.dma('act', LR[0:32, 1, :], k.PT[800:832, t0:t0 + BL], w=[LR])
            k.dma('act', LR[:, 2, :], k.PT[832:896, t0:t0 + BL], w=[LR])
            yield
            for q in range(3):
                p_ = P3[:, q, :, :]
                k.tt('dve', T1[:, :, 1:BL], p_[:, :, 0:E_], p_[:, :, 1:BL], ALU.subtract, r=[P3], w=[T1])
                k.tt('dve', T1[:, :, 0:1], halo[:, q, :].unsqueeze(2), p_[:, :, 0:1], ALU.subtract, r=[P3, halo], w=[T1])
                k.copy('pool', halo[:, q, :].unsqueeze(2), p_[:, :, E_:BL], r=[P3, T1], w=[halo])
                k.tt('pool', T1[:], T1[:], bc(pp[:, q, :].unsqueeze(2), H4), ALU.mult, r=[T1, pp], w=[T1])
                if q < 2:
                    k.tt('pool', p_, p_, T1[:], ALU.add, r=[P3, T1, halo], w=[P3])
                else:
                    k.tt('pool', V_[:], p_, T1[:], ALU.add, r=[P3, T1, halo], w=[V_])
                yield
            for q, rows in ((0, 32), (1, 32), (2, 64)):
                x_ = LR[0:rows, q, :]
                t_ = T2[0:rows, 0, :]
                k.tt('dve', t_[:, 1:BL], x_[:, 0:E_], x_[:, 1:BL], ALU.subtract, r=[LR], w=[T2])
                k.tt('dve', t_[:, 0:1], halo2[0:rows, q:q + 1], x_[:, 0:1], ALU.subtract, r=[LR, halo2], w=[T2])
                k.copy('dve', halo2[0:rows, q:q + 1], x_[:, E_:BL], r=[LR, T2], w=[halo2])
                k.stt('dve', x_, t_, lr[0:rows, q:q + 1], x_, ALU.mult, ALU.add, r=[T2, lr, LR, halo2], w=[LR])
            yield
            R_ = P3[:, 0, :, :]; Kp = P3[:, 1, :, :]
            k.act(LR[0:32, 0, :], LR[0:32, 0, :], AF.Tanh, r=[LR], w=[LR])
            k.act(SG[:], LR[:, 2, :], AF.Sigmoid, r=[LR], w=[SG])
            for h in range(4):
                k.mm(psQ[:, 0:BL], [(wup[:, h * 64:(h + 1) * 64], LR[0:32, 0, :])], r=[wup, LR], w=[psQ])
                k.act(T1[:, h, :], psQ[:, 0:BL], AF.Exp, r=[psQ, prm], w=[T1], scale=-1.0, bias=prm[:, 0, h:h + 1])
                k.mm(psQ[:, BL:2 * BL], [(aup[:, h * 64:(h + 1) * 64], LR[0:32, 1, :])], r=[aup, LR], w=[psQ], start=False)
                k.act(AA[:, h, :], psQ[:, BL:2 * BL], AF.Sigmoid, r=[psQ, pp], w=[AA], bias=pp[:, 4, h:h + 1])
                yield
            k.act(T1[:], T1[:], AF.Ln, r=[T1], w=[T1], bias=1.0)
            k.act(ELW[:], T1[:], AF.Exp, r=[T1], w=[ELW], scale=-1.0, bias=-0.5)
            k.S.op('dve', lambda: nc.vector.tensor_tensor_scan(
                out=SC_[:].rearrange("p h t -> p (h t)"), data0=CM4[:].rearrange("p h t -> p (h t)"),
                data1=ELW[:].rearrange("p h t -> p (h t)"), initial=0.0, op0=ALU.mult, op1=ALU.add), [CM4, ELW], [SC_])
            yield
            k.tt('pool', KKN[:], Kp, bc(pp[:, 5, :].unsqueeze(2), H4), ALU.mult, r=[P3, pp], w=[KKN])
            k.tt('pool', T1[:], KKN[:], KKN[:], ALU.mult, r=[KKN], w=[T1])
            for h in range(4):
                k.mm(psQ[:, 0:BL], [(ones[:], T1[:, h, :])], r=[ones, T1], w=[psQ])
                k.act(T2[:, h, :], psQ[:, 0:BL], AF.Ln, r=[psQ], w=[T2], bias=1e-24)
                yield
            k.act(T2[:], T2[:], AF.Exp, r=[T2], w=[T2], scale=-0.5)
            k.tt('dve', KKN[:], KKN[:], T2[:], ALU.mult, r=[KKN, T2], w=[KKN])
            yield
            k.tt('pool', T1[:], SC_[:], ELW[:], ALU.subtract, r=[SC_, ELW], w=[T1])
            k.act(T1[:], T1[:], AF.Exp, r=[T1], w=[T1], scale=-1.0)
            k.stt('dve', AT[:], KKN[:], -1.0, T1[:], ALU.mult, ALU.mult, r=[KKN, T1], w=[AT])
            yield
            k.act(T2[:], SC_[:], AF.Exp, r=[SC_], w=[T2])
            k.tt('pool', T1[:], KKN[:], AA[:], ALU.mult, r=[KKN, AA], w=[T1])
            k.tt('dve', BT[:], T1[:], T2[:], ALU.mult, r=[T1, T2], w=[BT])
            yield
            k.tt('pool', T1[:], AA[:], bc(pp[:, 6, :].unsqueeze(2), H4), ALU.mult, r=[AA, pp], w=[T1])
            k.tt('pool', T1[:], T1[:], bc(prm[:, 1, :].unsqueeze(2), H4), ALU.add, r=[T1, prm], w=[T1])
            k.tt('dve', Kp, Kp, T1[:], ALU.mult, r=[P3, T1, KKN], w=[P3])
            yield
            k.tt('dve', KT[:], Kp, T2[:], ALU.mult, r=[P3, T2], w=[KT])
            k.tt('pool', RK[:], R_, Kp, ALU.mult, r=[P3], w=[RK])
            k.act(GAM[:], SC_[:], AF.Exp, r=[SC_], w=[GAM], scale=-1.0)
            k.tt('dve', RT[:], R_, GAM[:], ALU.mult, r=[P3, GAM], w=[RT])
            yield

        def phaseA(nch):
            tb, n = divmod(nch, CPB)
            O = OUT[tb % 2]
            AT, BT, KT, RT, V_ = O["AT"], O["BT"], O["KT"], O["RT"], O["V"]
            c_ = slice(n * 64, (n + 1) * 64)
            par = nch % 2
            xy = XY[par][0]; akrk = AKRK[par]; rbt = RBT[par]; tok = TOK[par]
            fns = []
            for h in range(4):
                fns.append(lambda h=h: nc.tensor.matmul(psA1[:, h * 64:(h + 1) * 64], lhsT=BT[:, h, c_], rhs=AT[:, h, c_], start=True, stop=True, skip_group_check=True))
                fns.append(lambda h=h: nc.tensor.matmul(psA1[:, 256 + h * 64:256 + (h + 1) * 64], lhsT=AT[:, h, c_], rhs=BT[:, h, c_], start=True, stop=True, skip_group_check=True))
            k.S.pe_group(fns, [AT, BT], [psA1])
            fns = []
            for h in range(4):
                fns.append(lambda h=h: nc.tensor.matmul(psA2[:, h * 64:(h + 1) * 64], lhsT=KT[:, h, c_], rhs=AT[:, h, c_], start=True, stop=True, skip_group_check=True))
                fns.append(lambda h=h: nc.tensor.matmul(psA2[:, 256 + h * 64:256 + (h + 1) * 64], lhsT=KT[:, h, c_], rhs=RT[:, h, c_], start=True, stop=True, skip_group_check=True))
            k.S.pe_group(fns, [AT, KT, RT], [psA2])
            k.S.pe_group([lambda h=h: nc.tensor.matmul(psA3[:, h * 64:(h + 1) * 64], lhsT=BT[:, h, c_], rhs=RT[:, h, c_], start=True, stop=True, skip_group_check=True)
                          for h in range(4)], [BT, RT], [psA3])
            yield
            v4 = lambda ps, a: ps[:, a * 256:(a + 1) * 256].rearrange("p (h f) -> p h f", h=4)
            mb = lambda i: bc(mk[:, i, :].unsqueeze(1), [64, 4, 64])
            k.tt('dve', xy[:, 0, :, :], v4(psA1, 0), mb(0), ALU.mult, r=[psA1, mk], w=[xy])
            k.tt('dve', xy[:, 1, :, :], v4(psA1, 1), mb(1), ALU.mult, r=[psA1, mk], w=[xy])
            k.tt('dve', akrk[:, 0, :, :], v4(psA2, 0), mb(0), ALU.mult, r=[psA2, mk], w=[akrk])
            k.tt('dve', akrk[:, 1, :, :], v4(psA2, 1), mb(2), ALU.mult, r=[psA2, mk], w=[akrk])
            k.tt('dve', rbt[:], v4(psA3, 0), mb(2), ALU.mult, r=[psA3, mk], w=[rbt])
            yield
            fns = []
            psA1b = psA1[:, :].bitcast(BF16)
            for qi, src_ in enumerate((V_, BT, KT)):
                for h in range(4):
                    dst = psA1b[:, qi * 256 + h * 64:qi * 256 + (h + 1) * 64]
                    fns.append(lambda dst=dst, s_=src_[:, h, c_]: nc.tensor.transpose(out=dst, in_=s_, identity=k.ident_bf[0:64, 0:64]))
            k.S.pe_group(fns, [V_, BT, KT, k.ident_bf], [psA1])
            yield
            k.copy('act', tok[:].rearrange("p a h f -> p (a h f)"), psA1b[:, 0:768], r=[psA1], w=[tok])
            P_ = PP[par][0]
            k.tt('dve', P_[:], xy[:, 0, :, :], mb(3), ALU.add, r=[xy, mk], w=[P_])
            yield
            Pm = None
            for lev in range(1, 7):
                xyn = XY[par][lev % 2]
                fns = []
                rd = [xy]
                wr = []
                if lev <= 5:
                    for h in range(4):
                        fns.append(lambda h=h, xy=xy: nc.tensor.matmul(psA4[:, 256 + h * 64:256 + (h + 1) * 64], lhsT=xy[:, 0, h, :], rhs=xy[:, 1, h, :], start=True, stop=True, skip_group_check=True))
                        if lev <= 4:
                            fns.append(lambda h=h, xy=xy: nc.tensor.matmul(psA4[:, h * 64:(h + 1) * 64], lhsT=xy[:, 1, h, :], rhs=xy[:, 0, h, :], start=True, stop=True, skip_group_check=True))
                    wr.append(psA4)
                if lev >= 2:
                    for h in range(4):
                        fns.append(lambda h=h, xy=xy, Pm=Pm: nc.tensor.matmul(psA3[:, 256 + h * 64:256 + (h + 1) * 64], lhsT=xy[:, 1, h, :], rhs=Pm[:, h, :], start=True, stop=True, skip_group_check=True))
                    rd.append(Pm); wr.append(psA3)
                k.S.pe_group(fns, rd, wr)
                yield
                if lev <= 4:
                    k.copy('act', xyn[:].rearrange("p a h f -> p (a h f)"), psA4[:, :], r=[psA4], w=[xyn])
                elif lev == 5:
                    k.copy('act', xyn[:, 1, :, :].rearrange("p h f -> p (h f)"), psA4[:, 256:512], r=[psA4], w=[xyn])
                if lev >= 2:
                    Pn = PP[par][(lev - 1) % 2]
                    k.tt('dve', Pn[:], Pm[:], v4(psA3, 1), ALU.add, r=[Pm, psA3], w=[Pn])
                    Pm = Pn
                else:
                    Pm = P_
                yield
                xy = xyn

        def phaseB(nch):
            tb, n = divmod(nch, CPB)
            O = OUT[tb % 2]; SG = SGs[tb % 2]
            AT, RT, RK, GAM = O["AT"], O["RT"], O["RK"], O["GAM"]
            c_ = slice(n * 64, (n + 1) * 64)
            par = nch % 2
            akrk = AKRK[par]; rbt = RBT[par]; tok = TOK[par]; TT = PP[par][1]
            Hold = Hs[(nch + 1) % 2]; Hnew = Hs[nch % 2]
            Hbo = Hb[(nch + 1) % 2]; Hbn = Hb[nch % 2]
            yab_ = yab[tb % 2]
            fns = []
            for h in range(4):
                fns.append(lambda h=h: nc.tensor.matmul(psH[:, h * 64:(h + 1) * 64], lhsT=AT[:, h, c_], rhs=Hbo[:, h, :], start=(h == 0), stop=False, skip_group_check=True))
                fns.append(lambda h=h: nc.tensor.matmul(psH[:, h * 64:(h + 1) * 64], lhsT=akrk[:, 0, h, :], rhs=tok[:, 0, h, :], start=False, stop=True, skip_group_check=True))
            k.S.pe_group(fns, [AT, Hbo, akrk, tok], [psH])
            yield
            k.copy('act', Wsb[:].rearrange("p h f -> p (h f)"), psH[:, 0:256], r=[psH], w=[Wsb])
            yield
            k.S.pe_group([lambda h=h: nc.tensor.matmul(psH[:, 256 + h * 64:256 + (h + 1) * 64], lhsT=TT[:, h, :], rhs=Wsb[:, h, :], start=False, stop=True, skip_group_check=True)
                          for h in range(4)], [TT, Wsb], [psH])
            yield
            k.copy('act', Usb[:].rearrange("p h f -> p (h f)"), psH[:, 256:512], r=[psH], w=[Usb])
            yield
            fns = []
            for h in range(4):
                fns.append(lambda h=h: nc.tensor.matmul(psC[:, h * 64:(h + 1) * 64], lhsT=tok[:, 1, h, :], rhs=Usb[:, h, :], start=(h == 0), stop=False, skip_group_check=True))
                fns.append(lambda h=h: nc.tensor.matmul(psC[:, h * 64:(h + 1) * 64], lhsT=tok[:, 2, h, :], rhs=tok[:, 0, h, :], start=False, stop=True, skip_group_check=True))
                fns.append(lambda h=h: nc.tensor.matmul(psC[:, 256 + h:256 + h + 1], lhsT=RK[:, h, c_], rhs=ppb[:, h:h + 1], start=False, stop=True, skip_group_check=True))
            k.S.pe_group(fns, [tok, Usb, RK, ppb], [psC])
            fns = []
            for h in range(4):
                fns.append(lambda h=h: nc.tensor.matmul(psY[:, h * 64:(h + 1) * 64], lhsT=RT[:, h, c_], rhs=Hbo[:, h, :], start=(h == 0), stop=False, skip_group_check=True))
                fns.append(lambda h=h: nc.tensor.matmul(psY[:, h * 64:(h + 1) * 64], lhsT=rbt[:, h, :], rhs=Usb[:, h, :], start=False, stop=False, skip_group_check=True))
                fns.append(lambda h=h: nc.tensor.matmul(psY[:, h * 64:(h + 1) * 64], lhsT=akrk[:, 1, h, :], rhs=tok[:, 0, h, :], start=False, stop=True, skip_group_check=True))
            fns.append(lambda: nc.tensor.matmul(psY[:, 256:512], lhsT=SG[:, c_], rhs=gupb[:, :], start=False, stop=True, skip_group_check=True))
            k.S.pe_group(fns, [RT, Hbo, rbt, Usb, akrk, tok, SG, gupb], [psY])
            yield
            k.tt('dve', Hnew[:], psC[:, 0:256].rearrange("p (h f) -> p h f", h=4), Hold[:], ALU.add, r=[psC, Hold], w=[Hnew])
            k.copy('dve', sm[:, 5, :], psC[:, 256:260], r=[psC], w=[sm])
            k.tt('dve', Hbn[:], Hnew[:], bc(GAM[:, :, n * 64 + 63:n * 64 + 64], [64, 4, 64]), ALU.mult, r=[Hnew, GAM], w=[Hbn])
            k.tt('pool', Hnew[:], Hnew[:], bc(GAM[:, :, n * 64 + 63:n * 64 + 64], [64, 4, 64]), ALU.mult, r=[Hnew, GAM], w=[Hnew])
            yield
            y3 = psY[:, 0:256].rearrange("p (h f) -> p h f", h=4)
            k.S.op('dve', lambda: nc.vector.reduce_sum(out=sm[:, 0, :], in_=y3, axis=AX.X), [psY], [sm])
            k.ts('dve', sm[:, 1, :], sm[:, 0, :], 1.0 / 64.0, None, ALU.mult, None, r=[sm], w=[sm])
            k.tt('dve', yc[:], y3, bc(sm[:, 1, :].unsqueeze(2), [64, 4, 64]), ALU.subtract, r=[psY, sm], w=[yc])
            yield
            k.tt('pool', ysq[:], yc[:], yc[:], ALU.mult, r=[yc], w=[ysq])
            k.S.op('dve', lambda: nc.vector.reduce_sum(out=sm[:, 2, :], in_=ysq[:], axis=AX.X), [ysq], [sm])
            k.act(sm[:, 3, :], sm[:, 2, :], AF.Ln, r=[sm], w=[sm], scale=1.0 / 64.0, bias=GN_EPS)
            k.act(sm[:, 4, :], sm[:, 3, :], AF.Exp, r=[sm], w=[sm], scale=-0.5)
            yield
            k.tt('dve', yc[:], yc[:], bc(sm[:, 4, :].unsqueeze(2), [64, 4, 64]), ALU.mult, r=[yc, sm], w=[yc])
            k.tt('pool', yc[:], yc[:], lnr[:, 0, :].rearrange("p (h f) -> p h f", h=4), ALU.mult, r=[yc, lnr], w=[yc])
            k.tt('pool', yc[:], yc[:], lnr[:, 1, :].rearrange("p (h f) -> p h f", h=4), ALU.add, r=[yc, lnr], w=[yc])
            k.tt('dve', ysq[:], tok[:, 0, :, :], bc(sm[:, 5, :].unsqueeze(2), [64, 4, 64]), ALU.mult, r=[tok, sm], w=[ysq])
            yield
            k.tt('pool', yc[:], yc[:], ysq[:], ALU.add, r=[yc, ysq], w=[yc])
            k.tt('dve', yab_[:, n, :], yc[:].rearrange("p h f -> p (h f)"), psY[:, 256:512], ALU.mult, r=[yc, psY], w=[yab_])
            if n == CPB - 1:
                k.dma('sp', k.Y[tb * BL:(tb + 1) * BL, 0:256].rearrange("(n p) c -> p n c", p=64), yab_[:], r=[yab_])
            yield

        def run_all(*gens):
            gens = [g for g in gens if g is not None]
            while gens:
                for g in list(gens):
                    try:
                        next(g)
                    except StopIteration:
                        gens.remove(g)

        NCH = S_LEN // 64
        run_all(prep(0))
        run_all(phaseA(0), prep(1) if NB > 1 else None)
        gp = None
        for nch in range(NCH):
            tb, n = divmod(nch, CPB)
            if n == 0 and tb >= 1 and tb + 1 < NB:
                gp = prep(tb + 1)
            gens = [phaseB(nch)]
            if nch + 1 < NCH:
                gens.append(phaseA(nch + 1))
            rnd = 0
            while gens:
                for g in list(gens):
                    try:
                        next(g)
                    except StopIteration:
                        gens.remove(g)
                rnd += 1
                if gp is not None and rnd % 2 == 0:
                    try:
                        next(gp)
                    except StopIteration:
                        gp = None
            if n == CPB - 2 and gp is not None:
                for _ in gp:
                    pass
                gp = None


def build(nlayers=DEPTH, taps=()):
    k = K(nlayers, taps=taps)
    setup_globals(k)
    setup_fox(k)
    setup_rwkv(k)
    setup_ffn(k)
    setup_nsa(k)
    for l in range(nlayers):
        xin = k.x_in if l == 0 else k.XR
        xout = k.OUT if l == nlayers - 1 else k.XR
        stage_mod(k, l)
        stage_proj(k, l, xin)
        stage_rwkv(k, l)
        stage_fox(k, l)
        stage_nsa(k, l)
        stage_out(k, l, xin, k.XR1)
        stage_ffn(k, l, k.XR1, xout)
    k.S.barrier()
    return k


_CACHE = {}


def kernel(**inputs):
    if "k" not in _CACHE:
        _CACHE["k"] = build(DEPTH)
    k = _CACHE["k"]
    sh = prep_shared(inputs)
    in_maps = []
    for b in range(8):
        d = dict(sh)
        d.update(prep_core(inputs, b))
        in_maps.append({n: v for n, v in d.items() if n in k.ins})
    res = run_bass_kernel_spmd(k.nc, in_maps, core_ids=list(range(8)))
    out = np.stack([np.asarray(res.results[b]["out"], dtype=np.float32) for b in range(8)], axis=0)
    return out
```

```python
import numpy as np
import ml_dtypes
from contextlib import ExitStack
import concourse.bass as bass
import concourse.mybir as mybir
from concourse.bass_utils import run_bass_kernel_spmd

F32 = mybir.dt.float32
BF16 = mybir.dt.bfloat16
AF = mybir.ActivationFunctionType
ALU = mybir.AluOpType
AX = mybir.AxisListType
NPBF = ml_dtypes.bfloat16

S_LEN = 4096
D = 1024
DEPTH = 4
NTB = 8
N_IN = 3224
D_FF = 2816
NEG = -30000.0
RMS_EPS = 1e-6
GN_EPS = 64e-5


class Sched:
    ENG = ('pe', 'act', 'dve', 'pool')
    LIMIT = 30000

    def __init__(self, nc):
        self.nc = nc
        self.e = {'pe': nc.tensor, 'act': nc.scalar, 'dve': nc.vector, 'pool': nc.gpsimd, 'sp': nc.sync}
        self.epoch = {k: 0 for k in self.ENG}
        self.sem = {k: nc.alloc_semaphore("c_%s_0" % k) for k in self.ENG}
        self.cnt = {k: 0 for k in self.ENG}
        self.seen = {k: {} for k in self.e}
        self.lastw = {}
        self.reads = {}
        self.dma_sems = {'hw': [[nc.alloc_semaphore("d%d" % i), 0, "dma%d" % i] for i in range(24)],
                         'sw': [[nc.alloc_semaphore("ds%d" % i), 0, "dmas%d" % i] for i in range(8)]}
        self.ndma = {'hw': 0, 'sw': 0}
        self.n_inst = 0
        self.n_wait = 0
        self.per = {}

    def _wait(self, eng, tok):
        key, sem, val = tok
        if self.seen[eng].get(key, 0) >= val:
            return
        self.e[eng].wait_ge(sem, val)
        self.n_wait += 1
        self.per[eng] = self.per.get(eng, 0) + 1
        self.seen[eng][key] = val

    def _deps(self, eng, reads, writes):
        for b in reads:
            t = self.lastw.get(b)
            if t is not None:
                self._wait(eng, t)
        for b in writes:
            t = self.lastw.get(b)
            if t is not None:
                self._wait(eng, t)
            for t in self.reads.get(b, ()):
                self._wait(eng, t)

    def _commit(self, tok, reads, writes):
        for b in reads:
            self.reads.setdefault(b, []).append(tok)
        for b in writes:
            self.lastw[b] = tok
            self.reads[b] = []

    def _bump(self, eng, ins):
        if self.cnt[eng] >= self.LIMIT:
            self.epoch[eng] += 1
            self.sem[eng] = self.nc.alloc_semaphore("c_%s_%d" % (eng, self.epoch[eng]))
            self.cnt[eng] = 0
        self.cnt[eng] += 1
        ins.then_inc(self.sem[eng], 1)
        return ("%s_%d" % (eng, self.epoch[eng]), self.sem[eng], self.cnt[eng])

    @staticmethod
    def _norm(reads, writes):
        rd = [getattr(b, 'n', b) for b in reads]
        wr = [getattr(b, 'n', b) for b in writes]
        ps = [b for b in rd if b.startswith("ps")]
        rd = [b for b in rd if not b.startswith("ps")]
        return rd, wr + [b for b in ps if b not in wr]

    def op(self, eng, inst_fn, reads=(), writes=()):
        reads, writes = self._norm(reads, writes)
        self._deps(eng, reads, writes)
        ins = inst_fn()
        self.per[eng] = self.per.get(eng, 0) + 1
        tok = self._bump(eng, ins)
        self._commit(tok, reads, writes)
        self.n_inst += 1
        return tok

    def pe_group(self, fns, reads=(), writes=()):
        reads, writes = self._norm(reads, writes)
        self._deps('pe', reads, writes)
        ins = None
        for f in fns:
            ins = f()
            self.n_inst += 1
            self.per['pe'] = self.per.get('pe', 0) + 1
        tok = self._bump('pe', ins)
        self._commit(tok, reads, writes)
        return tok

    def dma(self, q, out, in_, reads=(), writes=(), **kw):
        reads, writes = self._norm(reads, writes)
        self._deps(q, reads, writes)
        cls = 'sw' if q == 'pool' else 'hw'
        pool_ = self.dma_sems[cls]
        slot = pool_[self.ndma[cls] % len(pool_)]
        self.ndma[cls] += 1
        if slot[1] > 0:
            self._wait(q, (slot[2], slot[0], slot[1]))
        if slot[1] >= self.LIMIT:
            slot[0] = self.nc.alloc_semaphore("%s_e%d" % (slot[2], self.ndma[cls]))
            slot[1] = 0
            slot[2] = slot[2] + "x"
        slot[1] += 16
        ins = self.e[q].dma_start(out=out, in_=in_, **kw)
        self.per[q] = self.per.get(q, 0) + 1
        ins.then_inc(slot[0], 16)
        tok = (slot[2], slot[0], slot[1])
        self._commit(tok, reads, writes)
        self.n_inst += 1
        return tok

    def barrier(self, engines=('pe', 'act', 'dve', 'pool', 'sp')):
        toks = [("%s_%d" % (k, self.epoch[k]), self.sem[k], self.cnt[k]) for k in self.ENG if self.cnt[k] > 0]
        toks += [(s[2], s[0], s[1]) for p_ in self.dma_sems.values() for s in p_ if s[1] > 0]
        for e in engines:
            for t in toks:
                self._wait(e, t)
        self.lastw = {}
        self.reads = {}


class Pipe:
    def __init__(self, lag=2):
        self.q = []
        self.lag = lag

    def push(self, first, second):
        first()
        self.q.append(second)
        while len(self.q) > self.lag:
            self.q.pop(0)()

    def flush(self):
        while self.q:
            self.q.pop(0)()


class Tile:
    def __init__(self, h, name):
        self.h = h
        self.n = name

    def __getitem__(self, idx):
        return self.h[idx]


class Scope:
    cnt = 0

    def __init__(self, k):
        self.k = k
        self.es = ExitStack()

    def __enter__(self):
        self.es.__enter__()
        Scope.cnt += 1
        self.id = Scope.cnt
        return self

    def sb(self, name, shape, dt):
        nm = "%s_%d" % (name, self.id)
        h = self.es.enter_context(self.k.nc.sbuf_tensor(nm, list(shape), dt))
        return Tile(h, nm)

    def ps(self, name, shape, dt=F32):
        nm = "%s_%d" % (name, self.id)
        h = self.es.enter_context(self.k.nc.psum_tensor(nm, list(shape), dt))
        return Tile(h, nm)

    def __exit__(self, *a):
        self.k.S.barrier()
        return self.es.__exit__(*a)


class K:
    def __init__(self, nlayers, taps=()):
        self.nc = bass.Bass("TRN2", target_bir_lowering=False)
        self.S = Sched(self.nc)
        self.nl = nlayers
        self.taps = set(taps)
        self.ins = {}
        self.dr = {}

    def inp(self, name, shape, dt=F32):
        t = self.nc.dram_tensor(name, list(shape), dt, kind="ExternalInput").ap()
        self.ins[name] = t
        return t

    def scratch(self, name, shape, dt=F32, out=False):
        kind = "ExternalOutput" if (out or name in self.taps) else "Internal"
        t = self.nc.dram_tensor(name, list(shape), dt, kind=kind).ap()
        self.dr[name] = t
        return t

    def act(self, out, in_, func, r, w, bias=0.0, scale=1.0, accum=None):
        nc = self.nc
        if accum is None:
            return self.S.op('act', lambda: nc.scalar.activation(out=out, in_=in_, func=func, bias=bias, scale=scale), r, w)
        return self.S.op('act', lambda: nc.scalar.activation(out=out, in_=in_, func=func, bias=bias, scale=scale, accum_out=accum), r, w)

    def ts(self, eng, out, in0, s1, s2, op0, op1, r, w):
        e = self.S.e[eng]
        if op1 is None:
            return self.S.op(eng, lambda: e.tensor_scalar(out=out, in0=in0, scalar1=s1, scalar2=None, op0=op0), r, w)
        return self.S.op(eng, lambda: e.tensor_scalar(out=out, in0=in0, scalar1=s1, scalar2=s2, op0=op0, op1=op1), r, w)

    def tt(self, eng, out, in0, in1, op, r, w):
        e = self.S.e[eng]
        return self.S.op(eng, lambda: e.tensor_tensor(out=out, in0=in0, in1=in1, op=op), r, w)

    def stt(self, eng, out, in0, scalar, in1, op0, op1, r, w):
        e = self.S.e[eng]
        return self.S.op(eng, lambda: e.scalar_tensor_tensor(out=out, in0=in0, scalar=scalar, in1=in1, op0=op0, op1=op1), r, w)

    def copy(self, eng, out, in_, r, w):
        if eng == 'act':
            return self.S.op('act', lambda: self.nc.scalar.copy(out=out, in_=in_), r, w)
        e = self.S.e[eng]
        return self.S.op(eng, lambda: e.tensor_copy(out=out, in_=in_), r, w)

    def mm(self, out, pairs, r, w, start=True, stop=True, sgc=False):
        nc = self.nc
        n = len(pairs)
        fns = []
        for i, (l, rh) in enumerate(pairs):
            fns.append(lambda l=l, rh=rh, i=i: nc.tensor.matmul(out, lhsT=l, rhs=rh, start=(start and i == 0), stop=(stop and i == n - 1),
                                                               skip_group_check=(sgc or not start)))
        return self.S.pe_group(fns, r, w)

    def transpose(self, out, in_, ident, r, w):
        nc = self.nc
        return self.S.op('pe', lambda: nc.tensor.transpose(out=out, in_=in_, identity=ident), r, w)

    def dma(self, q, out, in_, r=(), w=(), **kw):
        return self.S.dma(q, out, in_, r, w, **kw)


def w_in_perm_index():
    idx = list(range(0, 896))
    idx += list(range(896, 1664))
    for c in range(3):
        idx += list(range(2054 + c * 64, 2054 + c * 64 + 64))
        idx += list(range(2054 + (c + 3) * 64, 2054 + (c + 3) * 64 + 64))
    idx += list(range(2438, 2566))
    idx += list(range(2566, 2694))
    idx += list(range(2694, 2822))
    idx += list(range(2950, 3078))
    idx += list(range(2048, 2054))
    idx += list(range(1664, 2048))
    idx += list(range(2822, 2950))
    idx += list(range(3078, 3206))
    idx += list(range(3206, 3224))
    assert len(idx) == N_IN and len(set(idx)) == N_IN
    return np.array(idx)


QKT_ROWS = 1664


def setup_globals(k):
    nc = k.nc
    k.x_in = k.inp("x", [S_LEN, D])
    k.cT = k.inp("cT", [128, 8])
    k.ada_w = k.inp("ada_w", [DEPTH, D, 6 * D])
    k.ada_b_fm = k.inp("ada_b_fm", [DEPTH, 128, 48])
    k.ada_b_row = k.inp("ada_b_row", [DEPTH, 6 * D])
    k.normg_fm = k.inp("normg_fm", [DEPTH, 4, 128, 8])
    k.normg_row = k.inp("normg_row", [DEPTH, 4, D])
    k.w_in = k.inp("w_in_p", [DEPTH, D, N_IN])
    k.ident_bf_d = k.inp("ident_bf", [128, 128], BF16)
    k.ident_f_d = k.inp("ident_f", [128, 128], F32)

    k.PT = k.scratch("PT", [896, S_LEN], F32)
    k.QKT = k.scratch("QKT", [QKT_ROWS, S_LEN], BF16)
    k.FL = k.scratch("FL", [6, S_LEN], F32)
    k.VT = k.scratch("VT", [S_LEN, 640], BF16)
    k.GT = k.scratch("GT", [S_LEN, 18], F32)
    k.Y = k.scratch("Y", [S_LEN, D], BF16)
    k.XR = k.scratch("XR", [S_LEN, D], F32)
    k.XR1 = k.scratch("XR1", [S_LEN, D], F32)
    k.OUT = k.scratch("out", [S_LEN, D], F32, out=True)

    def pers(name, shape, dt):
        return Tile(nc.alloc_sbuf_tensor(name, list(shape), dt), name)
    k.ident_bf = pers("ident_bf_sb", [128, 128], BF16)
    k.ident_f = pers("ident_f_sb", [128, 128], F32)
    k.sc = pers("sc", [128, 8], F32)
    k.modAB = pers("modAB", [128, 32], F32)
    k.gm_row = pers("gm_row", [128, D], F32)
    k.gf_row = pers("gf_row", [128, D], F32)
    k.dma('sp', k.ident_bf[:], k.ident_bf_d, w=[k.ident_bf])
    k.dma('sp', k.ident_f[:], k.ident_f_d, w=[k.ident_f])
    k.dma('sp', k.sc[:], k.cT, w=[k.sc])
    k.act(k.sc[:], k.sc[:], AF.Silu, r=[k.sc], w=[k.sc])


def stage_mod(k, l):
    with Scope(k) as sc:
        slab = [sc.sb("adaslab%d" % i, [128, 6 * D], F32) for i in range(4)]
        psA = sc.ps("psA", [128, 32])
        psR = [sc.ps("psR%d" % i, [128, 512]) for i in range(4)]
        bfm = sc.sb("bfm", [128, 48], F32)
        gfm = sc.sb("gfm", [128, 4, 8], F32)
        brow = sc.sb("brow", [128, 2, D], F32)
        grow = sc.sb("grow", [128, 2, D], F32)
        mfm = sc.sb("mfm", [128, 32], F32)
        sc_rep = sc.sb("sc_rep", [128, 8, 128], F32)
        for kc in range(8):
            k.copy('dve', sc_rep[:, kc, :], k.sc[:, kc:kc + 1].to_broadcast([128, 128]), r=[k.sc], w=[sc_rep])
        k.dma('sp', bfm[:], k.ada_b_fm[l], w=[bfm])
        k.dma('sp', gfm[:], k.normg_fm[l].rearrange("g p c -> p g c"), w=[gfm])
        k.dma('sp', brow[:, 0, :], k.ada_b_row[l:l + 1, 2 * D:3 * D].broadcast_to([128, D]), w=[brow])
        k.dma('sp', brow[:, 1, :], k.ada_b_row[l:l + 1, 5 * D:6 * D].broadcast_to([128, D]), w=[brow])
        k.dma('sp', grow[:, 0, :], k.normg_row[l, 1:2, :].broadcast_to([128, D]), w=[grow])
        k.dma('sp', grow[:, 1, :], k.normg_row[l, 3:4, :].broadcast_to([128, D]), w=[grow])
        fm_chunks = list(range(0, 16)) + list(range(24, 40))
        row_cols = [2 * D, 2 * D + 512, 5 * D, 5 * D + 512]
        for kc in range(8):
            sl = slab[kc % 4]
            k.dma('sp' if kc % 2 == 0 else 'act', sl[:], k.ada_w[l, kc * 128:(kc + 1) * 128, :], w=[sl])
            for i, j in enumerate(fm_chunks):
                k.mm(psA[:, i:i + 1], [(sl[:, j * 128:(j + 1) * 128], k.sc[:, kc:kc + 1])], r=[sl, k.sc], w=[psA],
                     start=(kc == 0 and i == 0), stop=(kc == 7), sgc=True)
            for i, c0 in enumerate(row_cols):
                k.mm(psR[i][:], [(sc_rep[:, kc, :], sl[:, c0:c0 + 512])], r=[sl, sc_rep], w=[psR[i]],
                     start=(kc == 0), stop=(kc == 7), sgc=True)
        k.tt('dve', mfm[:, 0:16], psA[:, 0:16], bfm[:, 0:16], ALU.add, r=[psA, bfm], w=[mfm])
        k.tt('dve', mfm[:, 16:32], psA[:, 16:32], bfm[:, 24:40], ALU.add, r=[psA, bfm], w=[mfm])
        k.stt('dve', k.modAB[:, 0:8], mfm[:, 8:16], 1.0, gfm[:, 0, :], ALU.add, ALU.mult, r=[mfm, gfm], w=[k.modAB])
        k.copy('dve', k.modAB[:, 8:16], mfm[:, 0:8], r=[mfm], w=[k.modAB])
        k.stt('dve', k.modAB[:, 16:24], mfm[:, 24:32], 1.0, gfm[:, 2, :], ALU.add, ALU.mult, r=[mfm, gfm], w=[k.modAB])
        k.copy('dve', k.modAB[:, 24:32], mfm[:, 16:24], r=[mfm], w=[k.modAB])
        for i in range(4):
            dst = (k.gm_row if i < 2 else k.gf_row)
            cs = slice((i % 2) * 512, (i % 2) * 512 + 512)
            k.tt('dve', dst[:, cs], psR[i][:], brow[:, i // 2, cs], ALU.add, r=[psR[i], brow], w=[dst])
            k.tt('pool', dst[:, cs], dst[:, cs], grow[:, i // 2, cs], ALU.mult, r=[dst, grow], w=[dst])


def stage_proj(k, l, xsrc):
    nc = k.nc
    with Scope(k) as sc:
        wsb = sc.sb("wsb", [128, 8, N_IN], BF16)
        wst = [sc.sb("wst%d" % i, [128, N_IN], F32) for i in range(4)]
        xt = [sc.sb("xt%d" % i, [128, D], F32) for i in range(2)]
        junk = sc.sb("junk", [128, D], BF16)
        xn = [sc.sb("xn%d" % i, [128, D], BF16) for i in range(2)]
        st = [sc.sb("st%d" % i, [128, 4], F32) for i in range(2)]
        HT = [sc.sb("HT%d" % i, [128, 8, 512], BF16) for i in range(2)]
        psT = [sc.ps("psT%d" % i, [128, D], BF16) for i in range(2)]
        psM = [sc.ps("psM%d" % i, [128, 512]) for i in range(4)]
        evf = [sc.sb("evf%d" % i, [128, 512], F32) for i in range(3)]
        evb = [sc.sb("evb%d" % i, [128, 512], BF16) for i in range(3)]
        evt = [sc.sb("evt%d" % i, [128, 640], BF16) for i in range(2)]
        evg = [sc.sb("evg%d" % i, [128, 18], F32) for i in range(2)]
        for kc in range(8):
            s = wst[kc % 4]
            k.dma('sp' if kc % 2 == 0 else 'act', s[:], k.w_in[l, kc * 128:(kc + 1) * 128, :], w=[s])
            k.copy('pool' if kc % 2 == 0 else 'dve', wsb[:, kc, :], s[:], r=[s], w=[wsb])
        cnt = {"ev": 0, "pm": 0}

        def norm_gen(tb):
            ht = HT[tb % 2]
            t0_ = tb * 4
            k.dma('act', xt[t0_ % 2][:], xsrc[t0_ * 128:(t0_ + 1) * 128, :], w=[xt[t0_ % 2]])
            yield
            for sub in range(4):
                ti = tb * 4 + sub
                x_ = xt[ti % 2]; xn_ = xn[ti % 2]; st_ = st[ti % 2]; pt_ = psT[ti % 2]
                k.act(junk[:], x_[:], AF.Square, r=[x_], w=[junk, st_], scale=1.0 / 32.0, accum=st_[:, 0:1])
                k.act(st_[:, 1:2], st_[:, 0:1], AF.Ln, r=[st_], w=[st_], bias=RMS_EPS)
                k.act(st_[:, 2:3], st_[:, 1:2], AF.Exp, r=[st_], w=[st_], scale=-0.5)
                k.ts('dve', xn_[:], x_[:], st_[:, 2:3], None, ALU.mult, None, r=[x_, st_], w=[xn_])
                if sub < 3:
                    k.dma('act', xt[(ti + 1) % 2][:], xsrc[(ti + 1) * 128:(ti + 2) * 128, :], w=[xt[(ti + 1) % 2]])
                yield
                yield
                for kc in range(8):
                    k.transpose(pt_[:, kc * 128:(kc + 1) * 128], xn_[:, kc * 128:(kc + 1) * 128], k.ident_bf[:],
                                r=[xn_, k.ident_bf], w=[pt_])
                    if kc == 3:
                        yield
                yield
                for kc in range(8):
                    o = ht[:, kc, sub * 128:(sub + 1) * 128]
                    i_ = pt_[:, kc * 128:(kc + 1) * 128]
                    if kc % 2 == 0:
                        k.ts('dve', o, i_, k.modAB[:, kc:kc + 1], k.modAB[:, 8 + kc:9 + kc], ALU.mult, ALU.add,
                             r=[pt_, k.modAB], w=[ht])
                    else:
                        k.act(o, i_, AF.Identity, r=[pt_, k.modAB], w=[ht], scale=k.modAB[:, kc:kc + 1],
                              bias=k.modAB[:, 8 + kc:9 + kc])
                yield

        def step(g):
            if g is not None:
                try:
                    next(g)
                except StopIteration:
                    return None
            return g

        for _ in norm_gen(0):
            pass
        for tb in range(NTB):
            ht = HT[tb % 2]
            g = norm_gen(tb + 1) if tb + 1 < NTB else None
            tsl = slice(tb * 512, (tb + 1) * 512)
            fm = [(c * 128, 128, 'PT', c * 128) for c in range(7)]
            fm += [(896 + c * 128, 128, 'QKT', c * 128) for c in range(13)]
            fm += [(2560, 6, 'FL', 0)]
            for (c0, m, dst, r0) in fm:
                ps = psM[cnt["pm"] % 4]; cnt["pm"] += 1
                k.mm(ps[0:m, :], [(wsb[:, kc, c0:c0 + m], ht[:, kc, :]) for kc in range(8)], r=[wsb, ht], w=[ps])
                eng = 'act' if cnt["ev"] % 2 == 0 else 'dve'
                if dst == 'QKT':
                    ev = evb[cnt["ev"] % 3]
                    dd = k.QKT[r0:r0 + m, tsl]
                else:
                    ev = evf[cnt["ev"] % 3]
                    dd = (k.PT if dst == 'PT' else k.FL)[r0:r0 + m, tsl]
                cnt["ev"] += 1
                k.copy(eng, ev[0:m, :], ps[0:m, :], r=[ps], w=[ev])
                k.dma('sp', dd, ev[0:m, :], r=[ev])
                g = step(g)
            for sub in range(4):
                ti = tb * 4 + sub
                tok = slice(ti * 128, (ti + 1) * 128)
                ps0 = psM[cnt["pm"] % 4]; cnt["pm"] += 1
                ps1 = psM[cnt["pm"] % 4]; cnt["pm"] += 1
                lhs = lambda kc: ht[:, kc, sub * 128:(sub + 1) * 128]
                k.mm(ps0[:, 0:384], [(lhs(kc), wsb[:, kc, 2566:2950]) for kc in range(8)], r=[wsb, ht], w=[ps0])
                k.mm(ps1[:, 0:274], [(lhs(kc), wsb[:, kc, 2950:3224]) for kc in range(8)], r=[wsb, ht], w=[ps1])
                et = evt[ti % 2]; eg = evg[ti % 2]
                k.copy('act', et[:, 0:384], ps0[:, 0:384], r=[ps0], w=[et])
                k.copy('dve', et[:, 384:640], ps1[:, 0:256], r=[ps1], w=[et])
                k.copy('dve', eg[:], ps1[:, 256:274], r=[ps1], w=[eg])
                k.dma('sp', k.VT[tok, :], et[:], r=[et])
                k.dma('sp', k.GT[tok, :], eg[:], r=[eg])
                g = step(g)
            while g is not None:
                g = step(g)


def prep_shared(inp):
    f = lambda a: np.ascontiguousarray(np.asarray(a, dtype=np.float32))
    sh = {}
    sh["ada_w"] = f(inp["ada_w"])
    sh["ada_b_fm"] = f(np.asarray(inp["ada_b"]).reshape(DEPTH, 48, 128).transpose(0, 2, 1))
    sh["ada_b_row"] = f(inp["ada_b"])
    sh["normg_fm"] = f(np.asarray(inp["norm_g"]).reshape(DEPTH, 4, 8, 128).transpose(0, 1, 3, 2))
    sh["normg_row"] = f(inp["norm_g"])
    sh["w_in_p"] = f(np.asarray(inp["w_in"])[:, :, w_in_perm_index()])
    sh["ident_bf"] = np.eye(128, dtype=np.float32).astype(NPBF)
    sh["ident_f"] = np.eye(128, dtype=np.float32)
    sh["w_out"] = f(inp["w_out"]); sh["ffn_up"] = f(inp["ffn_up"]); sh["ffn_down"] = f(inp["ffn_down"])
    sh["conv_w_fm"] = f(np.asarray(inp["ffn_conv_w"]).reshape(DEPTH, 3, 44, 128).transpose(0, 3, 1, 2))
    sh["conv_b_fm"] = f(np.asarray(inp["ffn_conv_b"]).reshape(DEPTH, 44, 128).transpose(0, 2, 1))
    sh.update(nsa_host_consts())
    sh["rel_bias"] = f(inp["rel_bias"])
    sh["nsa_pe_kT"] = f(np.asarray(inp["nsa_pe_k"]).transpose(0, 2, 1))
    sh["nsa_pe_vT"] = f(np.asarray(inp["nsa_pe_v"]).transpose(0, 2, 1))
    for n in ("nsa_ck_w1", "nsa_cv_w1", "nsa_ck_w2", "nsa_cv_w2"):
        sh[n] = f(inp[n])
    sh["fox_b_f"] = f(np.asarray(inp["fox_b_f"]).reshape(DEPTH, 6, 1))
    sh.update(rwkv_host(inp))
    return sh


def prep_core(inp, b):
    d = {}
    d["x"] = np.ascontiguousarray(np.asarray(inp["x"][b], dtype=np.float32))
    d["cT"] = np.ascontiguousarray(np.asarray(inp["c"][b], dtype=np.float32).reshape(8, 128).T)
    return d


def setup_fox(k):
    k.fox_bf = k.inp("fox_b_f", [DEPTH, 6, 1])
    k.CUMA = k.scratch("CUMA", [6, 3, S_LEN], BF16)


def stage_fox(k, l):
    nc = k.nc
    with Scope(k) as sc:
        nb = sc.sb("nb", [128, 32, 6], F32)
        with Scope(k) as s2:
            fl = s2.sb("fl", [6, S_LEN], F32)
            t1 = s2.sb("t1", [6, S_LEN], F32)
            ones = s2.sb("ones", [6, S_LEN], F32)
            cum = s2.sb("cum", [6, S_LEN], F32)
            parts = s2.sb("parts", [6, 3, S_LEN], BF16)
            bfv = s2.sb("bfv", [6, 2], F32)
            psn = s2.ps("psn", [128, 512])
            k.dma('sp', fl[:], k.FL, w=[fl])
            k.dma('sp', bfv[:, 0:1], k.fox_bf[l], w=[bfv])
            k.ts('dve', bfv[:, 1:2], bfv[:, 0:1], -1.0, None, ALU.mult, None, r=[bfv], w=[bfv])
            k.S.op('pool', lambda: nc.gpsimd.memset(ones[:], 1.0), [], [ones])
            k.act(t1[:], fl[:], AF.Exp, r=[fl, bfv], w=[t1], bias=bfv[:, 1:2], scale=-1.0)
            k.act(t1[:], t1[:], AF.Ln, r=[t1], w=[t1], bias=1.0, scale=1.0)
            k.ts('dve', t1[:], t1[:], -1.0, None, ALU.mult, None, r=[t1], w=[t1])
            k.S.op('dve', lambda: nc.vector.tensor_tensor_scan(out=cum[:], data0=ones[:], data1=t1[:], initial=0.0,
                                                               op0=ALU.mult, op1=ALU.add), [ones, t1], [cum])
            for t in range(32):
                k.transpose(psn[:, t * 6:(t + 1) * 6], cum[:, t * 128:(t + 1) * 128], k.ident_f[0:6, 0:6],
                            r=[cum, k.ident_f], w=[psn])
            k.ts('dve', nb[:].rearrange("p t h -> p (t h)"), psn[:, 0:192], -1.0, None, ALU.mult, None, r=[psn], w=[nb])
            k.ts('dve', t1[:], cum[:], 8.0, None, ALU.mult, None, r=[cum], w=[t1])
            k.copy('dve', parts[:, 0, :], t1[:], r=[t1], w=[parts])
            k.tt('dve', t1[:], t1[:], parts[:, 0, :], ALU.subtract, r=[t1, parts], w=[t1])
            k.copy('dve', parts[:, 1, :], t1[:], r=[t1], w=[parts])
            k.tt('dve', t1[:], t1[:], parts[:, 1, :], ALU.subtract, r=[t1, parts], w=[t1])
            k.copy('dve', parts[:, 2, :], t1[:], r=[t1], w=[parts])
            k.dma('sp', k.CUMA, parts[:], r=[parts], w=["CUMA"])
        QA = [sc.sb("QA%d" % i, [128, S_LEN], BF16) for i in range(2)]
        KA = [sc.sb("KA%d" % i, [128, S_LEN], BF16) for i in range(2)]
        VA = sc.sb("VA", [128, 32, 6, 65], BF16)
        yb = sc.sb("yb", [128, 32, 384], BF16)
        PTl = [sc.sb("PTl%d" % i, [128, 512], BF16) for i in range(6)]
        rc = [sc.sb("rc%d" % i, [128, 4], F32) for i in range(2)]
        psS = [sc.ps("psS%d" % i, [128, 512]) for i in range(4)]
        psO = [sc.ps("psO%d" % i, [128, 512]) for i in range(2)]
        k.dma('sp', yb[:], k.VT[:, 0:384].rearrange("(t p) c -> p t c", p=128), w=[yb])
        k.S.op('pool', lambda: nc.gpsimd.memset(VA[:, :, :, 64:65], 1.0), [], [VA])
        k.copy('pool', VA[:, :, :, 0:64], yb[:].rearrange("p t (h d) -> p t h d", h=6), r=[yb], w=[VA])
        for i in range(2):
            k.S.op('dve', lambda i=i: nc.vector.memset(KA[i][64:67, :], 1.0), [], [KA[i]])
        nS = 0
        nO = 0
        nP = 0
        pipe = Pipe(3)
        for h in range(6):
            qa = QA[h % 2]; ka = KA[h % 2]
            k.dma('sp', qa[0:64, :], k.QKT[h * 64:(h + 1) * 64, :], w=[qa])
            k.dma('sp', qa[64:67, :], k.CUMA[h], w=[qa])
            k.dma('sp', ka[0:64, :], k.QKT[384 + h * 64:384 + (h + 1) * 64, :], w=[ka])
            for qb in range(NTB):
                po = psO[nO % 2]; nO += 1
                nkt = 4 * qb + 4
                for kt in range(nkt):
                    j = kt - 4 * qb
                    c0 = max(j, 0) * 128
                    ps = psS[nS % len(psS)]; nS += 1
                    pt = PTl[nP % len(PTl)]; nP += 1

                    def first(ps=ps, pt=pt, kt=kt, c0=c0, j=j, qa=qa, ka=ka, qb=qb, h=h):
                        k.mm(ps[:, c0:512], [(ka[0:67, kt * 128:(kt + 1) * 128], qa[0:67, qb * 512 + c0:(qb + 1) * 512])],
                             r=[ka, qa], w=[ps])
                        k.act(pt[:, c0:512], ps[:, c0:512], AF.Exp, r=[ps, nb], w=[pt], bias=nb[:, kt, h:h + 1], scale=0.125)
                        if j >= 0:
                            k.S.op('pool', lambda: nc.gpsimd.affine_select(
                                out=pt[:, c0:c0 + 128], in_=pt[:, c0:c0 + 128], pattern=[[1, 128]], compare_op=ALU.is_ge,
                                fill=0.0, base=0, channel_multiplier=-1), [pt], [pt])

                    def second(pt=pt, kt=kt, j=j, po=po, qb=qb, h=h, last=(kt == nkt - 1)):
                        fns = []
                        for qs in range(max(j, 0), 4):
                            fns.append(lambda qs=qs: nc.tensor.matmul(
                                po[:, qs * 65:(qs + 1) * 65], lhsT=pt[:, qs * 128:(qs + 1) * 128], rhs=VA[:, kt, h, :],
                                start=(kt == 0 and qs == 0), stop=(kt == 4 * qb + qs), skip_group_check=True))
                        k.S.pe_group(fns, [pt, VA], [po])
                        if last:
                            r_ = rc[qb % 2]
                            pov = po[:, 0:260].rearrange("p (q c) -> p q c", c=65)
                            k.S.op('dve', lambda: nc.vector.reciprocal(out=r_[:], in_=pov[:, :, 64]), [po], [r_])
                            for qs in range(4):
                                k.ts('dve', yb[:, qb * 4 + qs, h * 64:(h + 1) * 64], po[:, qs * 65:qs * 65 + 64], r_[:, qs:qs + 1], None,
                                     ALU.mult, None, r=[po, r_], w=[yb])
                    pipe.push(first, second)
        pipe.flush()
        k.dma('sp', k.Y[:, 256:640].rearrange("(t p) c -> p t c", p=128), yb[:], r=[yb], w=["Y"])


LW = 1536
LC = 4608
NEG8 = -240000.0


def t5_bucket_np(n):
    n = np.maximum(n, 0)
    nf = np.maximum(n, 1).astype(np.float32)
    large = 16 + (np.log(nf / np.float32(16)) / np.float32(np.log(128 / 16)) * np.float32(16)).astype(np.int32)
    large = np.minimum(large, 31)
    return np.where(n < 16, n, large)


def nsa_host_consts():
    c = {}
    i = np.arange(LW); n = i - 511
    oh = np.zeros((33, LW), np.float32)
    ok = (n >= 0) & (n < 512)
    oh[t5_bucket_np(n)[ok], i[ok]] = 1.0
    oh[32, ~ok] = NEG8
    c["oh_w"] = oh
    i = np.arange(LC); n = i - 2063
    oh = np.zeros((33, LC), np.float32)
    ok = n >= 0
    oh[t5_bucket_np(n)[ok], i[ok]] = 1.0
    oh[32, ~ok] = NEG8
    c["oh_c"] = oh
    E = np.zeros((128, 32, 128), np.float32)
    for kt in range(32):
        E[2 * kt, kt, 0:64] = 1.0
        E[2 * kt + 1, kt, 64:128] = 1.0
    c["E_blk"] = E.astype(NPBF)
    cs = np.arange(256) * 16
    ce = cs + 31
    ss = np.arange(64) * 64
    ov = ((cs[:, None] <= ss[None, :] + 63) & (ce[:, None] >= ss[None, :])).astype(np.float32)
    ov[255] = 0.0
    c["ovl"] = np.ascontiguousarray(ov.reshape(2, 128, 64).transpose(1, 0, 2)).astype(NPBF)
    t = np.arange(S_LEN)
    cur = t // 64
    jb = np.arange(64)
    back = cur[:, None] - jb[None, :]
    valid = back >= 0
    forced = (jb[None, :] == 0) | (valid & (back < 2))
    tkm = (valid & ~forced).astype(np.float32)
    tka = np.where(valid, np.where(forced, 1e4, 0.0), -1.0).astype(np.float32)
    c["tkm"] = np.ascontiguousarray(tkm.reshape(32, 128, 64).transpose(1, 0, 2)).astype(NPBF)
    c["tka"] = np.ascontiguousarray(tka.reshape(32, 128, 64).transpose(1, 0, 2)).astype(NPBF)
    return c


def setup_nsa(k):
    nc = k.nc
    k.rel_bias = k.inp("rel_bias", [32, 6])
    k.oh_w = k.inp("oh_w", [33, LW])
    k.oh_c = k.inp("oh_c", [33, LC])
    k.E_d = k.inp("E_blk", [128, 32, 128], BF16)
    k.ovl_d = k.inp("ovl", [128, 2, 64], BF16)
    k.tkm_d = k.inp("tkm", [128, 32, 64], BF16)
    k.tka_d = k.inp("tka", [128, 32, 64], BF16)
    k.pe_kT = k.inp("nsa_pe_kT", [DEPTH, 64, 32])
    k.pe_vT = k.inp("nsa_pe_vT", [DEPTH, 64, 32])
    k.ck_w1 = k.inp("nsa_ck_w1", [DEPTH, 2048, 128])
    k.cv_w1 = k.inp("nsa_cv_w1", [DEPTH, 2048, 128])
    k.ck_w2 = k.inp("nsa_ck_w2", [DEPTH, 128, 64])
    k.cv_w2 = k.inp("nsa_cv_w2", [DEPTH, 128, 64])
    k.WVW = k.scratch("WVW", [6, 128, LW], BF16)
    k.WVC = k.scratch("WVC", [6, 128, LC], BF16)
    with Scope(k) as sc:
        rb = sc.sb("rb", [33, 6], F32)
        rb31 = sc.sb("rb31", [32, 6], F32)
        rrep = sc.sb("rrep", [33, 6, 128], F32)
        ohw = sc.sb("ohw", [33, LW], F32)
        ohc = sc.sb("ohc", [33, LC], F32)
        ps = [sc.ps("psb%d" % i, [128, 512]) for i in range(2)]
        ev = [sc.sb("evb%d" % i, [128, 512], BF16) for i in range(2)]
        k.dma('sp', rb[0:32, :], k.rel_bias, w=[rb])
        k.dma('sp', rb31[:], k.rel_bias[31:32, :].broadcast_to([32, 6]), w=[rb31])
        k.dma('sp', ohw[:], k.oh_w, w=[ohw])
        k.dma('sp', ohc[:], k.oh_c, w=[ohc])
        k.S.op('dve', lambda: nc.vector.memset(rb[32:33, :], 1.0), [], [rb])
        k.tt('dve', rb[0:32, :], rb[0:32, :], rb31[:], ALU.subtract, r=[rb, rb31], w=[rb])
        k.ts('dve', rb[0:32, :], rb[0:32, :], 8.0, None, ALU.mult, None, r=[rb], w=[rb])
        for h in range(6):
            k.copy('dve', rrep[:, h, :], rb[:, h:h + 1].to_broadcast([33, 128]), r=[rb], w=[rrep])
        n = 0
        for h in range(6):
            for (oh, L, dst) in ((ohw, LW, k.WVW), (ohc, LC, k.WVC)):
                for c0 in range(0, L, 512):
                    p_ = ps[n % 2]; e_ = ev[n % 2]; n += 1
                    k.mm(p_[:], [(rrep[:, h, :], oh[:, c0:c0 + 512])], r=[rrep, oh], w=[p_])
                    k.copy('act' if n % 2 else 'dve', e_[:], p_[:], r=[p_], w=[e_])
                    k.dma('sp', dst[h, :, c0:c0 + 512], e_[:], r=[e_])


class DbgStop(Exception):
    pass


def dbg(k, lvl):
    if getattr(k, 'dbg_stop', None) == lvl:
        raise DbgStop()


def stage_nsa(k, l):
    nc = k.nc
    with Scope(k) as sc:
        Gw = sc.sb("Gw", [128, 6, 1408], BF16)
        Gc = sc.sb("Gc", [128, 6, 2560], BF16)
        E = sc.sb("E", [128, 32, 128], BF16)
        tkm = sc.sb("tkm", [128, 32, 64], BF16)
        tka = sc.sb("tka", [128, 32, 64], BF16)
        QC = [sc.sb("QC%d" % c, [128, S_LEN], BF16) for c in range(3)]
        KS = sc.sb("KS", [128, S_LEN], BF16)
        KW = sc.sb("KW", [128, S_LEN], BF16)
        VS = sc.sb("VS", [128, 32, 2, 65], BF16)
        VW = sc.sb("VW", [128, 32, 2, 65], BF16)
        KCMP = sc.sb("KCMP", [128, 256], BF16)
        VE = sc.sb("VE", [128, 2, 2, 129], BF16)
        sg = sc.sb("sg", [128, 32, 18], F32)
        for h in range(6):
            k.dma('sp', Gw[:, h, :], bass.AP(k.WVW.tensor, h * 128 * LW + 127, [[LW - 1, 128], [1, 1408]]), w=[Gw])
            k.dma('sp', Gc[:, h, :], bass.AP(k.WVC.tensor, h * 128 * LC + 2032, [[LC - 16, 128], [1, 2560]]), w=[Gc])
        k.dma('sp', E[:], k.E_d, w=[E])
        k.dma('sp', tkm[:], k.tkm_d, w=[tkm])
        k.dma('sp', tka[:], k.tka_d, w=[tka])
        for c in range(3):
            k.dma('sp', QC[c][:], k.QKT[768 + c * 128:768 + (c + 1) * 128, :], w=[QC[c]])
        k.dma('sp', KS[:], k.QKT[1408:1536, :], w=[KS])
        k.dma('sp', KW[:], k.QKT[1536:1664, :], w=[KW])
        k.dma('sp', sg[:], k.GT.rearrange("(t p) c -> p t c", p=128), w=[sg])
        k.act(sg[:], sg[:], AF.Exp, r=[sg], w=[sg], scale=-1.0)
        k.ts('dve', sg[:], sg[:], 1.0, None, ALU.add, None, r=[sg], w=[sg])
        k.S.op('dve', lambda: nc.vector.reciprocal(out=sg[:], in_=sg[:]), [sg], [sg])
        k.dma('sp', VE[:, 0, :, 65:129], k.ovl_d, w=[VE])
        k.dma('sp', VE[:, 1, :, 65:129], k.ovl_d, w=[VE])
        k.S.op('pool', lambda: nc.gpsimd.memset(VE[:, :, :, 64:65], 1.0), [], [VE])
        k.S.op('pool', lambda: nc.gpsimd.memset(VE[:, :, :, 0:64], 0.0), [], [VE])
        k.S.op('pool', lambda: nc.gpsimd.memset(KCMP[:], 0.0), [], [KCMP])
        dbg(k, 1)
        with Scope(k) as s2:
            vst = s2.sb("vst", [128, 32, 256], BF16)
            k.dma('sp', vst[:], k.VT[:, 384:640].rearrange("(t p) c -> p t c", p=128), w=[vst])
            k.S.op('pool', lambda: nc.gpsimd.memset(VS[:, :, :, 64:65], 1.0), [], [VS])
            k.S.op('pool', lambda: nc.gpsimd.memset(VW[:, :, :, 64:65], 1.0), [], [VW])
            k.copy('pool', VS[:, :, :, 0:64], vst[:, :, 0:128].rearrange("p t (g d) -> p t g d", g=2), r=[vst], w=[VS])
            k.copy('pool', VW[:, :, :, 0:64], vst[:, :, 128:256].rearrange("p t (g d) -> p t g d", g=2), r=[vst], w=[VW])
        dbg(k, 2)
        with Scope(k) as s2:
            KC = s2.sb("KC", [128, S_LEN], BF16)
            VC = s2.sb("VC", [128, S_LEN], BF16)
            k.dma('sp', KC[:], k.QKT[1152:1280, :], w=[KC])
            k.dma('sp', VC[:], k.QKT[1280:1408, :], w=[VC])
            w1s = s2.sb("w1s", [128, 16, 128], F32)
            w1b = [s2.sb("w1b%d" % i, [128, 32, 128], BF16) for i in range(2)]
            w2s = s2.sb("w2s", [128, 2, 64], F32)
            w2b = s2.sb("w2b", [128, 2, 64], BF16)
            pes = s2.sb("pes", [128, 2, 32], F32)
            peb = s2.sb("peb", [128, 2, 32], BF16)
            hb = s2.sb("hb", [128, 2], F32)
            gx = s2.sb("gx", [128, 256], F32)
            gu = s2.sb("gu", [128, 256], F32)
            gg = s2.sb("gg", [128, 256], BF16)
            psh = s2.ps("psh", [128, 512])
            psb_ = s2.ps("pshb", [128, 512])
            pso = s2.ps("pso", [128, 512])
            for kv, (w1d, w2d, ped) in enumerate(((k.ck_w1, k.ck_w2, k.pe_kT), (k.cv_w1, k.cv_w2, k.pe_vT))):
                for lh in range(2):
                    for half in range(2):
                        k.dma('sp', w1s[half * 64:(half + 1) * 64, :, :],
                              w1d[l, lh * 1024:(lh + 1) * 1024, :].rearrange("(l d) h -> d l h", d=64), w=[w1s])
                    k.copy('pool', w1b[kv][:, lh * 16:(lh + 1) * 16, :], w1s[:], r=[w1s], w=[w1b[kv]])
                for half in range(2):
                    k.dma('sp', pes[half * 64:(half + 1) * 64, kv, :], ped[l], w=[pes])
                k.dma('sp', w2s[:, kv, :], w2d[l], w=[w2s])
            k.copy('dve', w2b[:], w2s[:], r=[w2s], w=[w2b])
            w2kd = s2.sb("w2kd", [128, 2, 64], BF16)
            for a_ in range(2):
                k.copy('dve', w2kd[:, a_, :], w2s[:, 0, :], r=[w2s], w=[w2kd])
            k.copy('dve', peb[:], pes[:], r=[pes], w=[peb])
            for kv in range(2):
                src = KC if kv == 0 else VC
                k.mm(psb_[:, kv:kv + 1], [(w1b[kv][0:64, li, :], peb[0:64, kv, li:li + 1]) for li in range(32)],
                     r=[w1b[kv], peb], w=[psb_], start=True)
                k.copy('dve', hb[:, kv:kv + 1], psb_[:, kv:kv + 1], r=[psb_], w=[hb])
                for g in range(2):
                    pr = slice(g * 64, (g + 1) * 64)
                    k.mm(psh[:, 0:255], [(w1b[kv][pr, li, :], src[pr, li:li + 16 * 254 + 1:16]) for li in range(32)],
                         r=[w1b[kv], src], w=[psh])
                    k.ts('dve', gx[:, 0:255], psh[:, 0:255], hb[:, kv:kv + 1], None, ALU.add, None, r=[psh, hb], w=[gx])
                    k.tt('dve', gu[:, 0:255], gx[:, 0:255], gx[:, 0:255], ALU.mult, r=[gx], w=[gu])
                    k.ts('dve', gu[:, 0:255], gu[:, 0:255], 0.044715, 1.0, ALU.mult, ALU.add, r=[gu], w=[gu])
                    k.tt('dve', gu[:, 0:255], gu[:, 0:255], gx[:, 0:255], ALU.mult, r=[gu, gx], w=[gu])
                    k.act(gu[:, 0:255], gu[:, 0:255], AF.Exp, r=[gu], w=[gu], scale=-2.0 * 0.7978845608028654)
                    k.ts('dve', gu[:, 0:255], gu[:, 0:255], 1.0, None, ALU.add, None, r=[gu], w=[gu])
                    k.S.op('dve', lambda: nc.vector.reciprocal(out=gu[:, 0:255], in_=gu[:, 0:255]), [gu], [gu])
                    k.S.op('dve', lambda: nc.vector.memset(gg[:, 255:256], 0.0), [], [gg])
                    k.tt('dve', gg[:, 0:255], gu[:, 0:255], gx[:, 0:255], ALU.mult, r=[gu, gx], w=[gg])
                    if kv == 0:
                        k.mm(pso[:, 0:256], [(w2kd[:].rearrange("p a d -> p (a d)"), gg[:, 0:256])], r=[w2kd, gg], w=[pso])
                        k.copy('dve', KCMP[pr, :], pso[pr, 0:256], r=[pso], w=[KCMP])
                    else:
                        for ct in range(2):
                            k.mm(pso[:, ct * 64:(ct + 1) * 64], [(gg[:, ct * 128:(ct + 1) * 128], w2b[:, 1, :])],
                                 r=[w2b, gg], w=[pso], start=(ct == 0))
                        k.copy('dve', VE[:, g, :, 0:64], pso[:, 0:128].rearrange("p (c d) -> p c d", c=2), r=[pso], w=[VE])
        dbg(k, 3)
        NM = [sc.sb("NM%d" % g, [128, 512], BF16) for g in range(2)]
        for g in range(2):
            k.S.op('pool', lambda g=g: nc.gpsimd.memset(NM[g][:], 0.0), [], [NM[g]])
        PTl = [sc.sb("PTn%d" % i, [128, 512], BF16) for i in range(6)]
        yacc = [sc.sb("yacc%d" % i, [128, 4, 384], F32) for i in range(2)]
        ybf = [sc.sb("ybf%d" % i, [128, 4, 384], BF16) for i in range(2)]
        impt = [sc.sb("impt%d" % g, [128, 4, 64], F32) for g in range(2)]
        scr = sc.sb("scr", [128, 4, 64], F32)
        wk = sc.sb("wk", [128, 4, 64], F32)
        m8 = sc.sb("m8", [128, 4, 16], F32)
        nmq = sc.sb("nmq", [128, 4, 64], BF16)
        rcs = [sc.sb("rcs%d" % i, [128, 8], F32) for i in range(3)]
        psS = [sc.ps("psS%d" % i, [128, 512]) for i in range(4)]
        psO = [sc.ps("psO%d" % i, [128, 512]) for i in range(3)]
        psT = sc.ps("psTn", [128, 1024], BF16)
        st = {"S": 0, "O": 0, "P": 0, "R": 0}

        def q_ap(h, c0, c1):
            g, hp = h // 3, h % 3
            return QC[hp][g * 64:(g + 1) * 64, c0:c1]

        def evac(views, h, branch, qb, ya, first):
            r_ = rcs[st["R"] % 3]; st["R"] += 1
            for qs, (po, cb) in enumerate(views):
                if branch == 0:
                    k.ts('dve', r_[:, qs:qs + 1], po[:, cb + 64:cb + 65], 1e-30, None, ALU.max, None, r=[po], w=[r_])
                    k.S.op('dve', lambda r_=r_, qs=qs: nc.vector.reciprocal(out=r_[:, qs:qs + 1], in_=r_[:, qs:qs + 1]), [r_], [r_])
                else:
                    k.S.op('dve', lambda r_=r_, po=po, cb=cb, qs=qs: nc.vector.reciprocal(out=r_[:, qs:qs + 1], in_=po[:, cb + 64:cb + 65]), [po], [r_])
            k.tt('dve', r_[:, 4:8], r_[:, 0:4], sg[:, qb * 4:(qb + 1) * 4, h * 3 + branch], ALU.mult, r=[r_, sg], w=[r_])
            for qs, (po, cb) in enumerate(views):
                o = ya[:, qs, h * 64:(h + 1) * 64]
                if first:
                    k.ts('dve', o, po[:, cb:cb + 64], r_[:, 4 + qs:5 + qs], None, ALU.mult, None, r=[po, r_], w=[ya])
                else:
                    k.stt('dve', o, po[:, cb:cb + 64], r_[:, 4 + qs:5 + qs], o, ALU.mult, ALU.add, r=[po, r_, ya], w=[ya])
            return r_

        pipe = Pipe(3)

        def attend(h, qb, tiles, kmat, vmat, po, g, branch, ya):
            hp = h % 3
            nt = len(tiles)
            state = {"first": True}
            for idx, (kt, c0, c1, extra) in enumerate(tiles):
                ps = psS[st["S"] % len(psS)]; st["S"] += 1
                pt = PTl[st["P"] % len(PTl)]; st["P"] += 1

                def first(ps=ps, pt=pt, kt=kt, c0=c0, c1=c1, extra=extra):
                    fns = [lambda: nc.tensor.matmul(ps[:, c0:c1], lhsT=kmat[g * 64:(g + 1) * 64, kt * 128:(kt + 1) * 128],
                                                    rhs=QC[hp][g * 64:(g + 1) * 64, qb * 512 + c0:qb * 512 + c1],
                                                    start=True, stop=(len(extra) == 0), skip_group_check=True)]
                    rd = [kmat, QC[hp]]
                    for ei, (lt, rt, lap, rap) in enumerate(extra):
                        w_ = rap.shape[-1]
                        fns.append(lambda lap=lap, rap=rap, w_=w_, ei=ei: nc.tensor.matmul(
                            ps[:, c0:c0 + w_], lhsT=lap, rhs=rap, start=False, stop=(ei == len(extra) - 1), skip_group_check=True))
                        rd += [lt, rt]
                    k.S.pe_group(fns, rd, [ps])
                    k.act(pt[:, c0:c1], ps[:, c0:c1], AF.Exp, r=[ps], w=[pt], scale=0.125)

                def second(pt=pt, kt=kt, c0=c0, c1=c1, idx=idx):
                    fns = []
                    for qs in range(c0 // 128, (c1 + 127) // 128):
                        last = all(not (t2[1] <= qs * 128 < t2[2]) for t2 in tiles[idx + 1:])
                        fo = state["first"]
                        state["first"] = False
                        fns.append(lambda qs=qs, fo=fo, last=last: nc.tensor.matmul(
                            po[:, qs * 65:(qs + 1) * 65], lhsT=pt[:, qs * 128:(qs + 1) * 128], rhs=vmat[:, kt, g, :],
                            start=fo, stop=last, skip_group_check=True))
                    k.S.pe_group(fns, [pt, vmat], [po])
                    if idx == nt - 1:
                        evac([(po, qs * 65) for qs in range(4)], h, branch, qb, ya, False)
                pipe.push(first, second)

        for qb in getattr(k, 'dbg_qbs', range(NTB)):
            ya = yacc[qb % 2]
            for h in range(6):
                g = h // 3
                poA = psO[st["O"] % 3]; st["O"] += 1
                poB = psO[st["O"] % 3]; st["O"] += 1
                cts = [0] + ([1] if qb >= 4 else [])
                state = {"A": True, "B": True}
                for ct in cts:
                    delta = 512 * qb - 2048 * ct
                    ps = psS[st["S"] % len(psS)]; st["S"] += 1
                    pt = PTl[st["P"] % len(PTl)]; st["P"] += 1

                    def first(ps=ps, pt=pt, ct=ct, delta=delta, g=g, h=h):
                        pairs = [(KCMP[g * 64:(g + 1) * 64, ct * 128:(ct + 1) * 128], q_ap(h, qb * 512, (qb + 1) * 512))]
                        rd = [KCMP, QC[h % 3]]
                        if delta < 2560:
                            pairs.append((k.ident_bf[:], Gc[:, h, delta:delta + 512])); rd += [k.ident_bf, Gc]
                        k.mm(ps[:], pairs, r=rd, w=[ps])
                        k.act(pt[:], ps[:], AF.Exp, r=[ps], w=[pt], scale=0.125)

                    def second(pt=pt, ct=ct, g=g, h=h, poA=poA, poB=poB, state=state, lastct=(ct == cts[-1])):
                        fns = []
                        for qs in range(4):
                            po, cb = (poA, qs * 129) if qs < 3 else (poB, 0)
                            key = "A" if qs < 3 else "B"
                            stt_ = state[key]
                            state[key] = False
                            fns.append(lambda qs=qs, po=po, cb=cb, stt_=stt_: nc.tensor.matmul(
                                po[:, cb:cb + 129], lhsT=pt[:, qs * 128:(qs + 1) * 128], rhs=VE[:, g, ct, :],
                                start=stt_, stop=lastct, skip_group_check=True))
                        k.S.pe_group(fns, [pt, VE], [poA, poB])
                        if lastct:
                            views = [(poA, 0), (poA, 129), (poA, 258), (poB, 0)]
                            r_ = evac(views, h, 0, qb, ya, True)
                            for qs, (po, cb) in enumerate(views):
                                o = impt[g][:, qs, :]
                                if h % 3 == 0:
                                    k.ts('dve', o, po[:, cb + 65:cb + 129], r_[:, qs:qs + 1], None, ALU.mult, None, r=[po, r_], w=[impt[g]])
                                else:
                                    k.stt('dve', o, po[:, cb + 65:cb + 129], r_[:, qs:qs + 1], o, ALU.mult, ALU.add, r=[po, r_, impt[g]], w=[impt[g]])
                    pipe.push(first, second)
            pipe.flush()
            dbg(k, 4)
            for g in range(2):
                k.tt('dve', scr[:], impt[g][:], tkm[:, qb * 4:(qb + 1) * 4, :], ALU.mult, r=[impt[g], tkm], w=[scr])
                k.tt('dve', scr[:], scr[:], tka[:, qb * 4:(qb + 1) * 4, :], ALU.add, r=[scr, tka], w=[scr])
                for qs in range(4):
                    k.S.op('dve', lambda qs=qs: nc.vector.max(out=m8[:, qs, 0:8], in_=scr[:, qs, :]), [scr], [m8])
                    k.S.op('dve', lambda qs=qs: nc.vector.match_replace(out=wk[:, qs, :], in_to_replace=m8[:, qs, 0:8],
                                                                        in_values=scr[:, qs, :], imm_value=-1e9), [scr, m8], [wk])
                    k.S.op('dve', lambda qs=qs: nc.vector.max(out=m8[:, qs, 8:16], in_=wk[:, qs, :]), [wk], [m8])
                    k.ts('dve', wk[:, qs, :], scr[:, qs, :], m8[:, qs, 15:16], 1.0, ALU.is_ge, ALU.subtract, r=[scr, m8, wk], w=[wk])
                k.ts('dve', nmq[:], wk[:], -NEG8, None, ALU.mult, None, r=[wk], w=[nmq])
                for qs in range(4):
                    k.transpose(psT[0:64, qs * 128:(qs + 1) * 128], nmq[:, qs, :], k.ident_bf[:], r=[nmq, k.ident_bf], w=[psT])
                k.copy('dve', NM[g][0:64, :], psT[0:64, 0:512], r=[psT], w=[NM[g]])
            for h in range(6):
                g = h // 3
                po = psO[st["O"] % 3]; st["O"] += 1
                tiles = []
                for kt in range(max(0, 4 * qb - 4), 4 * qb + 4):
                    delta = 512 * qb - 128 * kt
                    c0 = max(-delta, 0)
                    c1 = min(512, 640 - delta) if delta > 0 else 512
                    tiles.append((kt, c0, c1, [(k.ident_bf, Gw, k.ident_bf[:], Gw[:, h, delta + 384 + c0:delta + 384 + c1])]))
                attend(h, qb, tiles, KW, VW, po, g, 2, ya)
            for h in range(6):
                g = h // 3
                po = psO[st["O"] % 3]; st["O"] += 1
                tiles = []
                for kt in range(0, 4 * qb + 4):
                    delta = 512 * qb - 128 * kt
                    c0 = max(-delta, 0)
                    ex = [(E, NM[g], E[:, kt, :], NM[g][:, c0:512])]
                    if delta <= 128:
                        c1b = 256 if delta == 128 else 512
                        ex.append((k.ident_bf, Gw, k.ident_bf[:], Gw[:, h, delta + 384 + c0:delta + 384 + c1b]))
                    tiles.append((kt, c0, 512, ex))
                attend(h, qb, tiles, KS, VS, po, g, 1, ya)
            pipe.flush()
            yb_ = ybf[qb % 2]
            k.copy('pool', yb_[:], ya[:], r=[ya], w=[yb_])
            k.dma('sp', k.Y[qb * 512:(qb + 1) * 512, 640:1024].rearrange("(q p) c -> p q c", p=128), yb_[:], r=[yb_])
            dbg(k, 100 + qb)


def setup_ffn(k):
    k.w_out = k.inp("w_out", [DEPTH, D, D])
    k.ffn_up = k.inp("ffn_up", [DEPTH, D, 2 * D_FF])
    k.ffn_down = k.inp("ffn_down", [DEPTH, D_FF, D])
    k.conv_w = k.inp("conv_w_fm", [DEPTH, 128, 3, 44])
    k.conv_b = k.inp("conv_b_fm", [DEPTH, 128, 44])


def load_cast_gen(k, stg, dst, src_rows, ncols, nchunks, col_split=1):
    w = ncols // col_split
    n = 0
    for c in range(nchunks):
        for cs in range(col_split):
            s = stg[n % len(stg)]
            k.dma('sp' if n % 2 == 0 else 'act', s[:, 0:w], src_rows(c)[:, cs * w:(cs + 1) * w], w=[s])
            k.copy('pool' if n % 2 == 0 else 'dve', dst[:, c, cs * w:(cs + 1) * w], s[:, 0:w], r=[s], w=[dst])
            n += 1
            yield


def load_cast(k, sc, dst, src_rows, ncols, nchunks, name, col_split=1):
    w = ncols // col_split
    stg = [sc.sb("%s_stg%d" % (name, i), [128, w], F32) for i in range(4)]
    for _ in load_cast_gen(k, stg, dst, src_rows, ncols, nchunks, col_split):
        pass


def rms_scale(k, ss, st):
    k.act(st[:, 0:1], ss, AF.Ln, r=[st], w=[st], bias=RMS_EPS)
    k.act(st[:, 1:2], st[:, 0:1], AF.Exp, r=[st], w=[st], scale=-0.5)


def stage_out(k, l, xsrc, xdst, bg=None, bg_steps=2):
    nc = k.nc
    with Scope(k) as sc:
        wo = sc.sb("wo", [128, 8, D], BF16)
        with Scope(k) as s2:
            load_cast(k, s2, wo, lambda c: k.w_out[l, c * 128:(c + 1) * 128, :], D, 8, "wo")
        yt = [sc.sb("yt%d" % i, [128, D], BF16) for i in range(2)]
        yT = [sc.sb("yT%d" % i, [128, 8, 128], BF16) for i in range(2)]
        xt = [sc.sb("xo%d" % i, [128, D], F32) for i in range(2)]
        tt_ = [sc.sb("to%d" % i, [128, D], F32) for i in range(2)]
        junk = sc.sb("junko", [128, 512], BF16)
        st = [sc.sb("sto%d" % i, [128, 4], F32) for i in range(2)]
        psT = [sc.ps("psTo%d" % i, [128, D], BF16) for i in range(2)]
        psY = [sc.ps("psYo%d" % i, [128, 512]) for i in range(4)]
        def T(ti):
            tok = slice(ti * 128, (ti + 1) * 128)
            y_ = yt[ti % 2]; yT_ = yT[ti % 2]; x_ = xt[ti % 2]; pT = psT[ti % 2]
            k.dma('act', y_[:], k.Y[tok, :], w=[y_])
            k.dma('act', x_[:], xsrc[tok, :], w=[x_])
            for kc in range(8):
                k.transpose(pT[:, kc * 128:(kc + 1) * 128], y_[:, kc * 128:(kc + 1) * 128], k.ident_bf[:], r=[y_, k.ident_bf], w=[pT])
            k.copy('act' if ti % 2 else 'dve', yT_[:].rearrange("p a b -> p (a b)"), pT[:], r=[pT], w=[yT_])

        def M(ti):
            tok = slice(ti * 128, (ti + 1) * 128)
            yT_ = yT[ti % 2]; x_ = xt[ti % 2]; t_ = tt_[ti % 2]; st_ = st[ti % 2]
            p0 = psY[(ti % 2) * 2]; p1 = psY[(ti % 2) * 2 + 1]
            for half, ps in enumerate((p0, p1)):
                k.mm(ps[:], [(yT_[:, kc, :], wo[:, kc, half * 512:(half + 1) * 512]) for kc in range(8)], r=[yT_, wo], w=[ps])
                k.act(junk[:], ps[:], AF.Square, r=[ps], w=[junk, st_], scale=1.0 / 32.0, accum=st_[:, 2 + half:3 + half])
            k.tt('dve', st_[:, 2:3], st_[:, 2:3], st_[:, 3:4], ALU.add, r=[st_], w=[st_])
            rms_scale(k, st_[:, 2:3], st_)
            for half, ps in enumerate((p0, p1)):
                cs = slice(half * 512, (half + 1) * 512)
                k.stt('dve', t_[:, cs], ps[:], st_[:, 1:2], k.gm_row[:, cs], ALU.mult, ALU.mult, r=[ps, st_, k.gm_row], w=[t_])
            k.tt('pool', t_[:], t_[:], x_[:], ALU.add, r=[t_, x_], w=[t_])
            k.dma('sp', xdst[tok, :], t_[:], r=[t_])

        T(0)
        for ti in range(32):
            if ti + 1 < 32:
                T(ti + 1)
            M(ti)
            for _ in range(bg_steps):
                if bg is not None:
                    try:
                        next(bg)
                    except StopIteration:
                        bg = None
        if bg is not None:
            for _ in bg:
                pass


def stage_out_ffn(k, l, xin, xmid, xdst):
    NCH = 22
    with Scope(k) as sc:
        wu = sc.sb("wu", [128, 8, 2 * D_FF], BF16)
        wd = sc.sb("wd", [128, NCH, D], BF16)
        with Scope(k) as s2:
            stg = [s2.sb("wstg%d" % i, [128, 1408], F32) for i in range(4)]

            def bg():
                yield from load_cast_gen(k, stg, wu, lambda c: k.ffn_up[l, c * 128:(c + 1) * 128, :], 2 * D_FF, 8, col_split=4)
                yield from load_cast_gen(k, stg, wd, lambda c: k.ffn_down[l, c * 128:(c + 1) * 128, :], D, NCH)
            stage_out(k, l, xin, xmid, bg=bg(), bg_steps=2)
        stage_ffn(k, l, xmid, xdst, pre=(sc, wu, wd))


def stage_ffn(k, l, xsrc, xdst, pre=None):
    nc = k.nc
    NCH = 22
    with ExitStack() as es_:
        if pre is None:
            sc = es_.enter_context(Scope(k))
            wu = sc.sb("wu", [128, 8, 2 * D_FF], BF16)
            wd = sc.sb("wd", [128, NCH, D], BF16)
            with Scope(k) as s2:
                load_cast(k, s2, wu, lambda c: k.ffn_up[l, c * 128:(c + 1) * 128, :], 2 * D_FF, 8, "wu", col_split=2)
                load_cast(k, s2, wd, lambda c: k.ffn_down[l, c * 128:(c + 1) * 128, :], D, NCH, "wd")
        else:
            sc, wu, wd = pre
        cw = sc.sb("cw", [128, 3, 44], F32)
        cb = sc.sb("cb", [128, 44], F32)
        hal = [sc.sb("hal%d" % i, [128, 44, 2], F32) for i in range(2)]
        k.dma('sp', cw[:], k.conv_w[l], w=[cw])
        k.dma('sp', cb[:], k.conv_b[l], w=[cb])
        k.S.op('pool', lambda: nc.gpsimd.memset(hal[1][:], 0.0), [], [hal[1]])
        actT = sc.sb("actT", [128, NCH, 512], BF16)
        HTs = [sc.sb("H2T%d" % i, [128, 8, 512], BF16) for i in range(2)]
        xt = [sc.sb("xf%d" % i, [128, D], F32) for i in range(2)]
        xn = sc.sb("xnf", [128, D], BF16)
        junk = sc.sb("junkf", [128, 512], BF16)
        st = [sc.sb("stf%d" % i, [128, 4], F32) for i in range(2)]
        Tg = [sc.sb("Tg%d" % i, [128, 512], F32) for i in range(2)]
        Tv = [sc.sb("Tv%d" % i, [128, 512], F32) for i in range(2)]
        psT = sc.ps("psTf", [128, D], BF16)
        psU = [sc.ps("psU%d" % i, [128, 512]) for i in range(4)]
        nxc = {"n": 0}

        def norm_gen(tb):
            HT = HTs[tb % 2]
            t0_ = tb * 4
            k.dma('act', xt[t0_ % 2][:], xsrc[t0_ * 128:(t0_ + 1) * 128, :], w=[xt[t0_ % 2]])
            yield
            for sub in range(4):
                ti = tb * 4 + sub
                x_ = xt[ti % 2]; st_ = st[ti % 2]
                k.act(xn[:], x_[:], AF.Square, r=[x_], w=[xn, st_], scale=1.0 / 32.0, accum=st_[:, 2:3])
                rms_scale(k, st_[:, 2:3], st_)
                k.ts('dve', xn[:], x_[:], st_[:, 1:2], None, ALU.mult, None, r=[x_, st_], w=[xn])
                if sub < 3:
                    k.dma('act', xt[(ti + 1) % 2][:], xsrc[(ti + 1) * 128:(ti + 2) * 128, :], w=[xt[(ti + 1) % 2]])
                yield
                yield
                for kc in range(8):
                    k.transpose(psT[:, kc * 128:(kc + 1) * 128], xn[:, kc * 128:(kc + 1) * 128], k.ident_bf[:], r=[xn, k.ident_bf], w=[psT])
                    if kc == 3:
                        yield
                yield
                for kc in range(8):
                    o = HT[:, kc, sub * 128:(sub + 1) * 128]
                    i_ = psT[:, kc * 128:(kc + 1) * 128]
                    if kc % 2 == 0:
                        k.ts('dve', o, i_, k.modAB[:, 16 + kc:17 + kc], k.modAB[:, 24 + kc:25 + kc], ALU.mult, ALU.add, r=[psT, k.modAB], w=[HT])
                    else:
                        k.act(o, i_, AF.Identity, r=[psT, k.modAB], w=[HT], scale=k.modAB[:, 16 + kc:17 + kc], bias=k.modAB[:, 24 + kc:25 + kc])
                yield

        def step(g):
            if g is not None:
                try:
                    next(g)
                except StopIteration:
                    return None
            return g

        xe = [sc.sb("xe%d" % i, [128, D], F32) for i in range(2)]
        ste = [sc.sb("ste%d" % i, [128, 4], F32) for i in range(2)]
        for _ in norm_gen(0):
            pass
        for tb in range(NTB):
            hin = hal[(tb + 1) % 2]; hout = hal[tb % 2]
            HT = HTs[tb % 2]
            g = norm_gen(tb + 1) if tb + 1 < NTB else None
            for cp in range(NCH):
                tg = Tg[cp % 2]; tv = Tv[cp % 2]
                for which, (T_, c_) in enumerate(((tg, cp), (tv, NCH + cp))):
                    ps = psU[(cp * 2 + which) % 4]
                    k.mm(ps[:], [(wu[:, kc, c_ * 128:(c_ + 1) * 128], HT[:, kc, :]) for kc in range(8)], r=[wu, HT], w=[ps])
                    k.act(T_[:], ps[:], AF.Identity, r=[ps, cw, cb], w=[T_], scale=cw[:, 2, c_:c_ + 1], bias=cb[:, c_:c_ + 1])
                    k.stt('dve', T_[:, 1:512], ps[:, 0:511], cw[:, 1, c_:c_ + 1], T_[:, 1:512], ALU.mult, ALU.add, r=[ps, cw, T_], w=[T_])
                    k.stt('dve', T_[:, 2:512], ps[:, 0:510], cw[:, 0, c_:c_ + 1], T_[:, 2:512], ALU.mult, ALU.add, r=[ps, cw, T_], w=[T_])
                    k.copy('act', hout[:, c_, :], ps[:, 510:512], r=[ps], w=[hout])
                    k.stt('dve', T_[:, 0:1], hin[:, c_, 1:2], cw[:, 1, c_:c_ + 1], T_[:, 0:1], ALU.mult, ALU.add, r=[hin, cw, T_], w=[T_])
                    k.stt('dve', T_[:, 0:2], hin[:, c_, 0:2], cw[:, 0, c_:c_ + 1], T_[:, 0:2], ALU.mult, ALU.add, r=[hin, cw, T_], w=[T_])
                k.act(tg[:], tg[:], AF.Silu, r=[tg], w=[tg])
                k.tt('pool', actT[:, cp, :], tg[:], tv[:], ALU.mult, r=[tg, tv], w=[actT])
                if cp >= 1:
                    g = step(g)
            for sub in range(4):
                ti = tb * 4 + sub
                tok = slice(ti * 128, (ti + 1) * 128)
                x_ = xe[sub % 2]; st_ = ste[sub % 2]
                k.dma('act', x_[:], xsrc[tok, :], w=[x_])
                psF = [psU[(2 * sub) % 4], psU[(2 * sub + 1) % 4]]
                for half in range(2):
                    ps = psF[half]
                    k.mm(ps[:], [(actT[:, cp, sub * 128:(sub + 1) * 128], wd[:, cp, half * 512:(half + 1) * 512]) for cp in range(NCH)],
                         r=[actT, wd], w=[ps])
                    k.act(junk[:, 0:512], ps[:], AF.Square, r=[ps], w=[junk, st_], scale=1.0 / 32.0, accum=st_[:, 2 + half:3 + half])
                k.tt('dve', st_[:, 2:3], st_[:, 2:3], st_[:, 3:4], ALU.add, r=[st_], w=[st_])
                rms_scale(k, st_[:, 2:3], st_)
                t_ = Tg[sub % 2] if False else None
                for half in range(2):
                    cs = slice(half * 512, (half + 1) * 512)
                    T_ = (Tg if half == 0 else Tv)[sub % 2]
                    k.stt('dve', T_[:], psF[half][:], st_[:, 1:2], k.gf_row[:, cs], ALU.mult, ALU.mult, r=[psF[half], st_, k.gf_row], w=[T_])
                    k.tt('pool', x_[:, cs], x_[:, cs], T_[:], ALU.add, r=[x_, T_], w=[x_])
                k.dma('sp', xdst[tok, :], x_[:], r=[x_])
                g = step(g)
            while g is not None:
                g = step(g)


def rwkv_host(inp):
    f = lambda a: np.ascontiguousarray(np.asarray(a, dtype=np.float32))
    mu = np.asarray(inp["rwkv_mu"])
    hd = lambda v: np.asarray(v).reshape(DEPTH, 4, 64).transpose(0, 2, 1)
    pp = np.stack([hd(mu[:, 0:256]), hd(mu[:, 256:512]), hd(mu[:, 512:768]), hd(inp["rwkv_w0"]), hd(inp["rwkv_a0"]),
                   hd(inp["rwkv_k_k"]), hd(inp["rwkv_k_a"]), hd(np.asarray(inp["rwkv_r_k"]).reshape(DEPTH, 256))], axis=2)
    lr = np.zeros((DEPTH, 64, 3), np.float32)
    lr[:, 0:32, 0] = mu[:, 768:800]; lr[:, 0:32, 1] = mu[:, 800:832]; lr[:, :, 2] = mu[:, 832:896]
    i = np.arange(64)
    mk = np.stack([(i[:, None] < i[None, :]), (i[:, None] > i[None, :]), (i[:, None] <= i[None, :]), np.eye(64, dtype=bool)]).astype(np.float32)
    cm = np.ones((64, 512), np.float32); cm[:, ::64] = 0.0
    return {"rwkv_pp": f(pp), "rwkv_lr": f(lr), "rwkv_w_up": f(inp["rwkv_w_up"]), "rwkv_a_up": f(inp["rwkv_a_up"]),
            "rwkv_g_up": f(inp["rwkv_g_up"]), "rwkv_ln": f(np.stack([np.asarray(inp["rwkv_ln_w"]), np.asarray(inp["rwkv_ln_b"])], axis=1)),
            "rwkv_masks": f(mk.transpose(1, 0, 2)), "rwkv_cmask": cm}


def setup_rwkv(k):
    k.rw_pp = k.inp("rwkv_pp", [DEPTH, 64, 8, 4])
    k.rw_lr = k.inp("rwkv_lr", [DEPTH, 64, 3])
    k.rw_wup = k.inp("rwkv_w_up", [DEPTH, 32, 256])
    k.rw_aup = k.inp("rwkv_a_up", [DEPTH, 32, 256])
    k.rw_gup = k.inp("rwkv_g_up", [DEPTH, 64, 256])
    k.rw_ln = k.inp("rwkv_ln", [DEPTH, 2, 256])
    k.rw_masks = k.inp("rwkv_masks", [64, 4, 64])
    k.rw_cmask = k.inp("rwkv_cmask", [64, 512])


def stage_rwkv(k, l):
    nc = k.nc
    BL = 256
    NB = S_LEN // BL
    CPB = BL // 64
    H4 = [64, 4, BL]
    bc = lambda ap, shape: ap.to_broadcast(shape)
    with Scope(k) as sc:
        pp = sc.sb("pp", [64, 8, 4], F32)
        lr = sc.sb("lr", [64, 3], F32)
        wup = sc.sb("wup", [32, 256], F32); aup = sc.sb("aup", [32, 256], F32); gup = sc.sb("gup", [64, 256], F32)
        lnr = sc.sb("lnr", [64, 2, 256], F32)
        mk = sc.sb("mk", [64, 4, 64], F32)
        cmask = sc.sb("cmask", [64, BL], F32)
        ones = sc.sb("ones64", [64, 64], F32)
        prm = sc.sb("prm", [64, 4, 4], F32)
        k.dma('sp', pp[:], k.rw_pp[l], w=[pp]); k.dma('sp', lr[:], k.rw_lr[l], w=[lr])
        k.dma('sp', wup[:], k.rw_wup[l], w=[wup]); k.dma('sp', aup[:], k.rw_aup[l], w=[aup]); k.dma('sp', gup[:], k.rw_gup[l], w=[gup])
        for i in range(2):
            k.dma('sp', lnr[:, i, :], k.rw_ln[l, i:i + 1, :].broadcast_to([64, 256]), w=[lnr])
        k.dma('sp', mk[:], k.rw_masks, w=[mk]); k.dma('sp', cmask[:], k.rw_cmask[:, 0:BL], w=[cmask])
        k.S.op('pool', lambda: nc.gpsimd.memset(ones[:], 1.0), [], [ones])
        k.ts('dve', prm[:, 0, :], pp[:, 3, :], -1.0, None, ALU.mult, None, r=[pp], w=[prm])
        k.ts('dve', prm[:, 1, :], pp[:, 6, :], -1.0, 1.0, ALU.mult, ALU.add, r=[pp], w=[prm])
        P3 = sc.sb("P3", [64, 3, 4, BL], F32)
        halo = sc.sb("halo", [64, 3, 4], F32)
        LR = sc.sb("LR", [64, 3, BL], F32)
        halo2 = sc.sb("halo2", [64, 3], F32)
        ELW = sc.sb("ELW", H4, F32); SC_ = sc.sb("SCAN", H4, F32); AA = sc.sb("AA", H4, F32); KKN = sc.sb("KKN", H4, F32)
        T1 = sc.sb("T1", H4, F32); T2 = sc.sb("T2", H4, F32); CM4 = sc.sb("CM4", H4, F32)
        OUT = [{nm: sc.sb("%s%d" % (nm, i), H4, F32 if nm == "GAM" else BF16) for nm in ("AT", "BT", "KT", "RT", "RK", "GAM", "V")} for i in range(2)]
        SGs = [sc.sb("SG%d" % i, [64, BL], BF16) for i in range(2)]
        gupb = sc.sb("gupb", [64, 256], BF16)
        ppb = sc.sb("ppb", [64, 4], BF16)
        identb64 = k.ident_bf
        XY = [[sc.sb("XY%d_%d" % (i, j), [64, 2, 4, 64], BF16) for j in range(2)] for i in range(2)]
        PP = [[sc.sb("PPi%d_%d" % (i, j), [64, 4, 64], BF16) for j in range(2)] for i in range(2)]
        AKRK = [sc.sb("AKRK%d" % i, [64, 2, 4, 64], BF16) for i in range(2)]
        RBT = [sc.sb("RBT%d" % i, [64, 4, 64], BF16) for i in range(2)]
        TOK = [sc.sb("TOK%d" % i, [64, 3, 4, 64], BF16) for i in range(2)]
        Wsb = sc.sb("Wsb", [64, 4, 64], BF16); Usb = sc.sb("Usb", [64, 4, 64], BF16)
        Hs = [sc.sb("Hs%d" % i, [64, 4, 64], F32) for i in range(2)]
        Hb = [sc.sb("Hb%d" % i, [64, 4, 64], BF16) for i in range(2)]
        yc = sc.sb("yc", [64, 4, 64], F32); ysq = sc.sb("ysq", [64, 4, 64], F32)
        sm = sc.sb("sm", [64, 6, 4], F32)
        yab = [sc.sb("yab%d" % i, [64, CPB, 256], BF16) for i in range(2)]
        psA1 = sc.ps("psA1", [64, 512]); psA2 = sc.ps("psA2", [64, 512]); psA3 = sc.ps("psA3", [64, 512]); psA4 = sc.ps("psA4", [64, 512])
        psH = sc.ps("psHr", [64, 512]); psY = sc.ps("psYr", [64, 512]); psC = sc.ps("psCr", [64, 512]); psQ = sc.ps("psQr", [64, 512])
        k.S.op('pool', lambda: nc.gpsimd.memset(Hs[1][:], 0.0), [], [Hs[1]])
        k.S.op('pool', lambda: nc.gpsimd.memset(Hb[1][:], 0.0), [], [Hb[1]])
        k.copy('dve', gupb[:], gup[:], r=[gup], w=[gupb])
        k.copy('dve', ppb[:], pp[:, 7, :], r=[pp], w=[ppb])
        k.S.op('pool', lambda: nc.gpsimd.memset(halo[:], 0.0), [], [halo])
        k.S.op('pool', lambda: nc.gpsimd.memset(halo2[:], 0.0), [], [halo2])
        k.copy('dve', CM4[:], bc(cmask[:].unsqueeze(1), H4), r=[cmask], w=[CM4])
        E_ = BL - 1

        def prep(tb):
            O = OUT[tb % 2]; SG = SGs[tb % 2]
            AT, BT, KT, RT, RK, GAM, V_ = O["AT"], O["BT"], O["KT"], O["RT"], O["RK"], O["GAM"], O["V"]
            t0 = tb * BL
            for q in range(3):
                k.dma('act', P3[:, q, :, :], k.PT[q * 256:(q + 1) * 256, t0:t0 + BL].rearrange("(h d) t -> d h t", d=64), w=[P3])
            k.dma('act', LR[0:32, 0, :], k.PT[768:800, t0:t0 + BL], w=[LR])
            k.dma('act', LR[0:32, 1, :], k.PT[800:832, t0:t0 + BL], w=[LR])
            k.dma('act', LR[:, 2, :], k.PT[832:896, t0:t0 + BL], w=[LR])
            yield
            for q in range(3):
                p_ = P3[:, q, :, :]
                k.tt('dve', T1[:, :, 1:BL], p_[:, :, 0:E_], p_[:, :, 1:BL], ALU.subtract, r=[P3], w=[T1])
                k.tt('dve', T1[:, :, 0:1], halo[:, q, :].unsqueeze(2), p_[:, :, 0:1], ALU.subtract, r=[P3, halo], w=[T1])
                k.copy('pool', halo[:, q, :].unsqueeze(2), p_[:, :, E_:BL], r=[P3, T1], w=[halo])
                k.tt('pool', T1[:], T1[:], bc(pp[:, q, :].unsqueeze(2), H4), ALU.mult, r=[T1, pp], w=[T1])
                if q < 2:
                    k.tt('pool', p_, p_, T1[:], ALU.add, r=[P3, T1, halo], w=[P3])
                else:
                    k.tt('pool', V_[:], p_, T1[:], ALU.add, r=[P3, T1, halo], w=[V_])
                yield
            for q, rows in ((0, 32), (1, 32), (2, 64)):
                x_ = LR[0:rows, q, :]
                t_ = T2[0:rows, 0, :]
                k.tt('dve', t_[:, 1:BL], x_[:, 0:E_], x_[:, 1:BL], ALU.subtract, r=[LR], w=[T2])
                k.tt('dve', t_[:, 0:1], halo2[0:rows, q:q + 1], x_[:, 0:1], ALU.subtract, r=[LR, halo2], w=[T2])
                k.copy('dve', halo2[0:rows, q:q + 1], x_[:, E_:BL], r=[LR, T2], w=[halo2])
                k.stt('dve', x_, t_, lr[0:rows, q:q + 1], x_, ALU.mult, ALU.add, r=[T2, lr, LR, halo2], w=[LR])
            yield
            R_ = P3[:, 0, :, :]; Kp = P3[:, 1, :, :]
            k.act(LR[0:32, 0, :], LR[0:32, 0, :], AF.Tanh, r=[LR], w=[LR])
            k.act(SG[:], LR[:, 2, :], AF.Sigmoid, r=[LR], w=[SG])
            for h in range(4):
                k.mm(psQ[:, 0:BL], [(wup[:, h * 64:(h + 1) * 64], LR[0:32, 0, :])], r=[wup, LR], w=[psQ])
                k.act(T1[:, h, :], psQ[:, 0:BL], AF.Exp, r=[psQ, prm], w=[T1], scale=-1.0, bias=prm[:, 0, h:h + 1])
                k.mm(psQ[:, BL:2 * BL], [(aup[:, h * 64:(h + 1) * 64], LR[0:32, 1, :])], r=[aup, LR], w=[psQ], start=False)
                k.act(AA[:, h, :], psQ[:, BL:2 * BL], AF.Sigmoid, r=[psQ, pp], w=[AA], bias=pp[:, 4, h:h + 1])
                yield
            k.act(T1[:], T1[:], AF.Ln, r=[T1], w=[T1], bias=1.0)
            k.act(ELW[:], T1[:], AF.Exp, r=[T1], w=[ELW], scale=-1.0, bias=-0.5)
            k.S.op('dve', lambda: nc.vector.tensor_tensor_scan(
                out=SC_[:].rearrange("p h t -> p (h t)"), data0=CM4[:].rearrange("p h t -> p (h t)"),
                data1=ELW[:].rearrange("p h t -> p (h t)"), initial=0.0, op0=ALU.mult, op1=ALU.add), [CM4, ELW], [SC_])
            yield
            k.tt('pool', KKN[:], Kp, bc(pp[:, 5, :].unsqueeze(2), H4), ALU.mult, r=[P3, pp], w=[KKN])
            k.tt('pool', T1[:], KKN[:], KKN[:], ALU.mult, r=[KKN], w=[T1])
            for h in range(4):
                k.mm(psQ[:, 0:BL], [(ones[:], T1[:, h, :])], r=[ones, T1], w=[psQ])
                k.act(T2[:, h, :], psQ[:, 0:BL], AF.Ln, r=[psQ], w=[T2], bias=1e-24)
                yield
            k.act(T2[:], T2[:], AF.Exp, r=[T2], w=[T2], scale=-0.5)
            k.tt('dve', KKN[:], KKN[:], T2[:], ALU.mult, r=[KKN, T2], w=[KKN])
            yield
            k.tt('pool', T1[:], SC_[:], ELW[:], ALU.subtract, r=[SC_, ELW], w=[T1])
            k.act(T1[:], T1[:], AF.Exp, r=[T1], w=[T1], scale=-1.0)
            k.stt('dve', AT[:], KKN[:], -1.0, T1[:], ALU.mult, ALU.mult, r=[KKN, T1], w=[AT])
            yield
            k.act(T2[:], SC_[:], AF.Exp, r=[SC_], w=[T2])
            k.tt('pool', T1[:], KKN[:], AA[:], ALU.mult, r=[KKN, AA], w=[T1])
            k.tt('dve', BT[:], T1[:], T2[:], ALU.mult, r=[T1, T2], w=[BT])
            yield
            k.tt('pool', T1[:], AA[:], bc(pp[:, 6, :].unsqueeze(2), H4), ALU.mult, r=[AA, pp], w=[T1])
            k.tt('pool', T1[:], T1[:], bc(prm[:, 1, :].unsqueeze(2), H4), ALU.add, r=[T1, prm], w=[T1])
            k.tt('dve', Kp, Kp, T1[:], ALU.mult, r=[P3, T1, KKN], w=[P3])
            yield
            k.tt('dve', KT[:], Kp, T2[:], ALU.mult, r=[P3, T2], w=[KT])
            k.tt('pool', RK[:], R_, Kp, ALU.mult, r=[P3], w=[RK])
            k.act(GAM[:], SC_[:], AF.Exp, r=[SC_], w=[GAM], scale=-1.0)
            k.tt('dve', RT[:], R_, GAM[:], ALU.mult, r=[P3, GAM], w=[RT])
            yield

        def phaseA(nch):
            tb, n = divmod(nch, CPB)
            O = OUT[tb % 2]
            AT, BT, KT, RT, V_ = O["AT"], O["BT"], O["KT"], O["RT"], O["V"]
            c_ = slice(n * 64, (n + 1) * 64)
            par = nch % 2
            xy = XY[par][0]; akrk = AKRK[par]; rbt = RBT[par]; tok = TOK[par]
            fns = []
            for h in range(4):
                fns.append(lambda h=h: nc.tensor.matmul(psA1[:, h * 64:(h + 1) * 64], lhsT=BT[:, h, c_], rhs=AT[:, h, c_], start=True, stop=True, skip_group_check=True))
                fns.append(lambda h=h: nc.tensor.matmul(psA1[:, 256 + h * 64:256 + (h + 1) * 64], lhsT=AT[:, h, c_], rhs=BT[:, h, c_], start=True, stop=True, skip_group_check=True))
            k.S.pe_group(fns, [AT, BT], [psA1])
            fns = []
            for h in range(4):
                fns.append(lambda h=h: nc.tensor.matmul(psA2[:, h * 64:(h + 1) * 64], lhsT=KT[:, h, c_], rhs=AT[:, h, c_], start=True, stop=True, skip_group_check=True))
                fns.append(lambda h=h: nc.tensor.matmul(psA2[:, 256 + h * 64:256 + (h + 1) * 64], lhsT=KT[:, h, c_], rhs=RT[:, h, c_], start=True, stop=True, skip_group_check=True))
            k.S.pe_group(fns, [AT, KT, RT], [psA2])
            k.S.pe_group([lambda h=h: nc.tensor.matmul(psA3[:, h * 64:(h + 1) * 64], lhsT=BT[:, h, c_], rhs=RT[:, h, c_], start=True, stop=True, skip_group_check=True)
                          for h in range(4)], [BT, RT], [psA3])
            yield
            v4 = lambda ps, a: ps[:, a * 256:(a + 1) * 256].rearrange("p (h f) -> p h f", h=4)
            mb = lambda i: bc(mk[:, i, :].unsqueeze(1), [64, 4, 64])
            k.tt('dve', xy[:, 0, :, :], v4(psA1, 0), mb(0), ALU.mult, r=[psA1, mk], w=[xy])
            k.tt('dve', xy[:, 1, :, :], v4(psA1, 1), mb(1), ALU.mult, r=[psA1, mk], w=[xy])
            k.tt('dve', akrk[:, 0, :, :], v4(psA2, 0), mb(0), ALU.mult, r=[psA2, mk], w=[akrk])
            k.tt('dve', akrk[:, 1, :, :], v4(psA2, 1), mb(2), ALU.mult, r=[psA2, mk], w=[akrk])
            k.tt('dve', rbt[:], v4(psA3, 0), mb(2), ALU.mult, r=[psA3, mk], w=[rbt])
            yield
            fns = []
            psA1b = psA1[:, :].bitcast(BF16)
            for qi, src_ in enumerate((V_, BT, KT)):
                for h in range(4):
                    dst = psA1b[:, qi * 256 + h * 64:qi * 256 + (h + 1) * 64]
                    fns.append(lambda dst=dst, s_=src_[:, h, c_]: nc.tensor.transpose(out=dst, in_=s_, identity=k.ident_bf[0:64, 0:64]))
            k.S.pe_group(fns, [V_, BT, KT, k.ident_bf], [psA1])
            yield
            k.copy('act', tok[:].rearrange("p a h f -> p (a h f)"), psA1b[:, 0:768], r=[psA1], w=[tok])
            P_ = PP[par][0]
            k.tt('dve', P_[:], xy[:, 0, :, :], mb(3), ALU.add, r=[xy, mk], w=[P_])
            yield
            Pm = None
            for lev in range(1, 7):
                xyn = XY[par][lev % 2]
                fns = []
                rd = [xy]
                wr = []
                if lev <= 5:
                    for h in range(4):
                        fns.append(lambda h=h, xy=xy: nc.tensor.matmul(psA4[:, 256 + h * 64:256 + (h + 1) * 64], lhsT=xy[:, 0, h, :], rhs=xy[:, 1, h, :], start=True, stop=True, skip_group_check=True))
                        if lev <= 4:
                            fns.append(lambda h=h, xy=xy: nc.tensor.matmul(psA4[:, h * 64:(h + 1) * 64], lhsT=xy[:, 1, h, :], rhs=xy[:, 0, h, :], start=True, stop=True, skip_group_check=True))
                    wr.append(psA4)
                if lev >= 2:
                    for h in range(4):
                        fns.append(lambda h=h, xy=xy, Pm=Pm: nc.tensor.matmul(psA3[:, 256 + h * 64:256 + (h + 1) * 64], lhsT=xy[:, 1, h, :], rhs=Pm[:, h, :], start=True, stop=True, skip_group_check=True))
                    rd.append(Pm); wr.append(psA3)
                k.S.pe_group(fns, rd, wr)
                yield
                if lev <= 4:
                    k.copy('act', xyn[:].rearrange("p a h f -> p (a h f)"), psA4[:, :], r=[psA4], w=[xyn])
                elif lev == 5:
                    k.copy('act', xyn[:, 1, :, :].rearrange("p h f -> p (h f)"), psA4[:, 256:512], r=[psA4], w=[xyn])
                if lev >= 2:
                    Pn = PP[par][(lev - 1) % 2]
                    k.tt('dve', Pn[:], Pm[:], v4(psA3, 1), ALU.add, r=[Pm, psA3], w=[Pn])
                    Pm = Pn
                else:
                    Pm = P_
                yield
                xy = xyn

        def phaseB(nch):
            tb, n = divmod(nch, CPB)
            O = OUT[tb % 2]; SG = SGs[tb % 2]
            AT, RT, RK, GAM = O["AT"], O["RT"], O["RK"], O["GAM"]
            c_ = slice(n * 64, (n + 1) * 64)
            par = nch % 2
            akrk = AKRK[par]; rbt = RBT[par]; tok = TOK[par]; TT = PP[par][1]
            Hold = Hs[(nch + 1) % 2]; Hnew = Hs[nch % 2]
            Hbo = Hb[(nch + 1) % 2]; Hbn = Hb[nch % 2]
            yab_ = yab[tb % 2]
            fns = []
            for h in range(4):
                fns.append(lambda h=h: nc.tensor.matmul(psH[:, h * 64:(h + 1) * 64], lhsT=AT[:, h, c_], rhs=Hbo[:, h, :], start=(h == 0), stop=False, skip_group_check=True))
                fns.append(lambda h=h: nc.tensor.matmul(psH[:, h * 64:(h + 1) * 64], lhsT=akrk[:, 0, h, :], rhs=tok[:, 0, h, :], start=False, stop=True, skip_group_check=True))
            k.S.pe_group(fns, [AT, Hbo, akrk, tok], [psH])
            yield
            k.copy('act', Wsb[:].rearrange("p h f -> p (h f)"), psH[:, 0:256], r=[psH], w=[Wsb])
            yield
            k.S.pe_group([lambda h=h: nc.tensor.matmul(psH[:, 256 + h * 64:256 + (h + 1) * 64], lhsT=TT[:, h, :], rhs=Wsb[:, h, :], start=False, stop=True, skip_group_check=True)
                          for h in range(4)], [TT, Wsb], [psH])
            yield
            k.copy('act', Usb[:].rearrange("p h f -> p (h f)"), psH[:, 256:512], r=[psH], w=[Usb])
            yield
            fns = []
            for h in range(4):
                fns.append(lambda h=h: nc.tensor.matmul(psC[:, h * 64:(h + 1) * 64], lhsT=tok[:, 1, h, :], rhs=Usb[:, h, :], start=(h == 0), stop=False, skip_group_check=True))
                fns.append(lambda h=h: nc.tensor.matmul(psC[:, h * 64:(h + 1) * 64], lhsT=tok[:, 2, h, :], rhs=tok[:, 0, h, :], start=False, stop=True, skip_group_check=True))
                fns.append(lambda h=h: nc.tensor.matmul(psC[:, 256 + h:256 + h + 1], lhsT=RK[:, h, c_], rhs=ppb[:, h:h + 1], start=False, stop=True, skip_group_check=True))
            k.S.pe_group(fns, [tok, Usb, RK, ppb], [psC])
            fns = []
            for h in range(4):
                fns.append(lambda h=h: nc.tensor.matmul(psY[:, h * 64:(h + 1) * 64], lhsT=RT[:, h, c_], rhs=Hbo[:, h, :], start=(h == 0), stop=False, skip_group_check=True))
                fns.append(lambda h=h: nc.tensor.matmul(psY[:, h * 64:(h + 1) * 64], lhsT=rbt[:, h, :], rhs=Usb[:, h, :], start=False, stop=False, skip_group_check=True))
                fns.append(lambda h=h: nc.tensor.matmul(psY[:, h * 64:(h + 1) * 64], lhsT=akrk[:, 1, h, :], rhs=tok[:, 0, h, :], start=False, stop=True, skip_group_check=True))
            fns.append(lambda: nc.tensor.matmul(psY[:, 256:512], lhsT=SG[:, c_], rhs=gupb[:, :], start=False, stop=True, skip_group_check=True))
            k.S.pe_group(fns, [RT, Hbo, rbt, Usb, akrk, tok, SG, gupb], [psY])
            yield
            k.tt('dve', Hnew[:], psC[:, 0:256].rearrange("p (h f) -> p h f", h=4), Hold[:], ALU.add, r=[psC, Hold], w=[Hnew])
            k.copy('dve', sm[:, 5, :], psC[:, 256:260], r=[psC], w=[sm])
            k.tt('dve', Hbn[:], Hnew[:], bc(GAM[:, :, n * 64 + 63:n * 64 + 64], [64, 4, 64]), ALU.mult, r=[Hnew, GAM], w=[Hbn])
            k.tt('pool', Hnew[:], Hnew[:], bc(GAM[:, :, n * 64 + 63:n * 64 + 64], [64, 4, 64]), ALU.mult, r=[Hnew, GAM], w=[Hnew])
            yield
            y3 = psY[:, 0:256].rearrange("p (h f) -> p h f", h=4)
            k.S.op('dve', lambda: nc.vector.reduce_sum(out=sm[:, 0, :], in_=y3, axis=AX.X), [psY], [sm])
            k.ts('dve', sm[:, 1, :], sm[:, 0, :], 1.0 / 64.0, None, ALU.mult, None, r=[sm], w=[sm])
            k.tt('dve', yc[:], y3, bc(sm[:, 1, :].unsqueeze(2), [64, 4, 64]), ALU.subtract, r=[psY, sm], w=[yc])
            yield
            k.tt('pool', ysq[:], yc[:], yc[:], ALU.mult, r=[yc], w=[ysq])
            k.S.op('dve', lambda: nc.vector.reduce_sum(out=sm[:, 2, :], in_=ysq[:], axis=AX.X), [ysq], [sm])
            k.act(sm[:, 3, :], sm[:, 2, :], AF.Ln, r=[sm], w=[sm], scale=1.0 / 64.0, bias=GN_EPS)
            k.act(sm[:, 4, :], sm[:, 3, :], AF.Exp, r=[sm], w=[sm], scale=-0.5)
            yield
            k.tt('dve', yc[:], yc[:], bc(sm[:, 4, :].unsqueeze(2), [64, 4, 64]), ALU.mult, r=[yc, sm], w=[yc])
            k.tt('pool', yc[:], yc[:], lnr[:, 0, :].rearrange("p (h f) -> p h f", h=4), ALU.mult, r=[yc, lnr], w=[yc])
            k.tt('pool', yc[:], yc[:], lnr[:, 1, :].rearrange("p (h f) -> p h f", h=4), ALU.add, r=[yc, lnr], w=[yc])
            k.tt('dve', ysq[:], tok[:, 0, :, :], bc(sm[:, 5, :].unsqueeze(2), [64, 4, 64]), ALU.mult, r=[tok, sm], w=[ysq])
            yield
            k.tt('pool', yc[:], yc[:], ysq[:], ALU.add, r=[yc, ysq], w=[yc])
            k.tt('dve', yab_[:, n, :], yc[:].rearrange("p h f -> p (h f)"), psY[:, 256:512], ALU.mult, r=[yc, psY], w=[yab_])
            if n == CPB - 1:
                k.dma('sp', k.Y[tb * BL:(tb + 1) * BL, 0:256].rearrange("(n p) c -> p n c", p=64), yab_[:], r=[yab_])
            yield

        def run_all(*gens):
            gens = [g for g in gens if g is not None]
            while gens:
                for g in list(gens):
                    try:
                        next(g)
                    except StopIteration:
                        gens.remove(g)

        NCH = S_LEN // 64
        run_all(prep(0))
        run_all(phaseA(0), prep(1) if NB > 1 else None)
        gp = None
        for nch in range(NCH):
            tb, n = divmod(nch, CPB)
            if n == 0 and tb >= 1 and tb + 1 < NB:
                gp = prep(tb + 1)
            gens = [phaseB(nch)]
            if nch + 1 < NCH:
                gens.append(phaseA(nch + 1))
            rnd = 0
            while gens:
                for g in list(gens):
                    try:
                        next(g)
                    except StopIteration:
                        gens.remove(g)
                rnd += 1
                if gp is not None and rnd % 2 == 0:
                    try:
                        next(gp)
                    except StopIteration:
                        gp = None
            if n == CPB - 2 and gp is not None:
                for _ in gp:
                    pass
                gp = None


def build(nlayers=DEPTH, taps=()):
    k = K(nlayers, taps=taps)
    setup_globals(k)
    setup_fox(k)
    setup_rwkv(k)
    setup_ffn(k)
    setup_nsa(k)
    for l in range(nlayers):
        xin = k.x_in if l == 0 else k.XR
        xout = k.OUT if l == nlayers - 1 else k.XR
        stage_mod(k, l)
        stage_proj(k, l, xin)
        stage_rwkv(k, l)
        stage_fox(k, l)
        stage_nsa(k, l)
        stage_out_ffn(k, l, xin, k.XR1, xout)
    k.S.barrier()
    return k


_CACHE = {}


def kernel(**inputs):
    if "k" not in _CACHE:
        _CACHE["k"] = build(DEPTH)
    k = _CACHE["k"]
    sh = prep_shared(inputs)
    in_maps = []
    for b in range(8):
        d = dict(sh)
        d.update(prep_core(inputs, b))
        in_maps.append({n: v for n, v in d.items() if n in k.ins})
    res = run_bass_kernel_spmd(k.nc, in_maps, core_ids=list(range(8)))
    out = np.stack([np.asarray(res.results[b]["out"], dtype=np.float32) for b in range(8)], axis=0)
    return out
```

```python
import numpy as np
import ml_dtypes
from contextlib import ExitStack
import concourse.bass as bass
import concourse.mybir as mybir
from concourse.bass_utils import run_bass_kernel_spmd

F32 = mybir.dt.float32
BF16 = mybir.dt.bfloat16
AF = mybir.ActivationFunctionType
ALU = mybir.AluOpType
AX = mybir.AxisListType
NPBF = ml_dtypes.bfloat16

S_LEN = 4096
D = 1024
DEPTH = 4
NTB = 8
N_IN = 3224
D_FF = 2816
NEG = -30000.0
RMS_EPS = 1e-6
GN_EPS = 64e-5


class Sched:
    ENG = ('pe', 'act', 'dve', 'pool')
    LIMIT = 30000

    def __init__(self, nc):
        self.nc = nc
        self.e = {'pe': nc.tensor, 'act': nc.scalar, 'dve': nc.vector, 'pool': nc.gpsimd, 'sp': nc.sync}
        self.epoch = {k: 0 for k in self.ENG}
        self.sem = {k: nc.alloc_semaphore("c_%s_0" % k) for k in self.ENG}
        self.cnt = {k: 0 for k in self.ENG}
        self.seen = {k: {} for k in self.e}
        self.lastw = {}
        self.reads = {}
        self.dma_sems = {'hw': [[nc.alloc_semaphore("d%d" % i), 0, "dma%d" % i] for i in range(24)],
                         'sw': [[nc.alloc_semaphore("ds%d" % i), 0, "dmas%d" % i] for i in range(8)]}
        self.ndma = {'hw': 0, 'sw': 0}
        self.n_inst = 0
        self.n_wait = 0
        self.per = {}

    def _wait(self, eng, tok):
        key, sem, val = tok
        if self.seen[eng].get(key, 0) >= val:
            return
        self.e[eng].wait_ge(sem, val)
        self.n_wait += 1
        self.per[eng] = self.per.get(eng, 0) + 1
        self.seen[eng][key] = val

    def _deps(self, eng, reads, writes):
        for b in reads:
            t = self.lastw.get(b)
            if t is not None:
                self._wait(eng, t)
        for b in writes:
            t = self.lastw.get(b)
            if t is not None:
                self._wait(eng, t)
            for t in self.reads.get(b, ()):
                self._wait(eng, t)

    def _commit(self, tok, reads, writes):
        for b in reads:
            self.reads.setdefault(b, []).append(tok)
        for b in writes:
            self.lastw[b] = tok
            self.reads[b] = []

    def _bump(self, eng, ins):
        if self.cnt[eng] >= self.LIMIT:
            self.epoch[eng] += 1
            self.sem[eng] = self.nc.alloc_semaphore("c_%s_%d" % (eng, self.epoch[eng]))
            self.cnt[eng] = 0
        self.cnt[eng] += 1
        ins.then_inc(self.sem[eng], 1)
        return ("%s_%d" % (eng, self.epoch[eng]), self.sem[eng], self.cnt[eng])

    @staticmethod
    def _norm(reads, writes):
        rd = [getattr(b, 'n', b) for b in reads]
        wr = [getattr(b, 'n', b) for b in writes]
        ps = [b for b in rd if b.startswith("ps")]
        rd = [b for b in rd if not b.startswith("ps")]
        return rd, wr + [b for b in ps if b not in wr]

    def op(self, eng, inst_fn, reads=(), writes=()):
        reads, writes = self._norm(reads, writes)
        self._deps(eng, reads, writes)
        ins = inst_fn()
        self.per[eng] = self.per.get(eng, 0) + 1
        tok = self._bump(eng, ins)
        self._commit(tok, reads, writes)
        self.n_inst += 1
        return tok

    def pe_group(self, fns, reads=(), writes=()):
        reads, writes = self._norm(reads, writes)
        self._deps('pe', reads, writes)
        ins = None
        for f in fns:
            ins = f()
            self.n_inst += 1
            self.per['pe'] = self.per.get('pe', 0) + 1
        tok = self._bump('pe', ins)
        self._commit(tok, reads, writes)
        return tok

    def dma(self, q, out, in_, reads=(), writes=(), **kw):
        reads, writes = self._norm(reads, writes)
        self._deps(q, reads, writes)
        cls = 'sw' if q == 'pool' else 'hw'
        pool_ = self.dma_sems[cls]
        slot = pool_[self.ndma[cls] % len(pool_)]
        self.ndma[cls] += 1
        if slot[1] > 0:
            self._wait(q, (slot[2], slot[0], slot[1]))
        if slot[1] >= self.LIMIT:
            slot[0] = self.nc.alloc_semaphore("%s_e%d" % (slot[2], self.ndma[cls]))
            slot[1] = 0
            slot[2] = slot[2] + "x"
        slot[1] += 16
        ins = self.e[q].dma_start(out=out, in_=in_, **kw)
        self.per[q] = self.per.get(q, 0) + 1
        ins.then_inc(slot[0], 16)
        tok = (slot[2], slot[0], slot[1])
        self._commit(tok, reads, writes)
        self.n_inst += 1
        return tok

    def barrier(self, engines=('pe', 'act', 'dve', 'pool', 'sp')):
        toks = [("%s_%d" % (k, self.epoch[k]), self.sem[k], self.cnt[k]) for k in self.ENG if self.cnt[k] > 0]
        toks += [(s[2], s[0], s[1]) for p_ in self.dma_sems.values() for s in p_ if s[1] > 0]
        for e in engines:
            for t in toks:
                self._wait(e, t)
        self.lastw = {}
        self.reads = {}


class Pipe:
    def __init__(self, lag=2):
        self.q = []
        self.lag = lag

    def push(self, first, second):
        first()
        self.q.append(second)
        while len(self.q) > self.lag:
            self.q.pop(0)()

    def flush(self):
        while self.q:
            self.q.pop(0)()


class Tile:
    def __init__(self, h, name):
        self.h = h
        self.n = name

    def __getitem__(self, idx):
        return self.h[idx]


class Scope:
    cnt = 0

    def __init__(self, k):
        self.k = k
        self.es = ExitStack()

    def __enter__(self):
        self.es.__enter__()
        Scope.cnt += 1
        self.id = Scope.cnt
        return self

    def sb(self, name, shape, dt):
        nm = "%s_%d" % (name, self.id)
        h = self.es.enter_context(self.k.nc.sbuf_tensor(nm, list(shape), dt))
        return Tile(h, nm)

    def ps(self, name, shape, dt=F32):
        nm = "%s_%d" % (name, self.id)
        h = self.es.enter_context(self.k.nc.psum_tensor(nm, list(shape), dt))
        return Tile(h, nm)

    def __exit__(self, *a):
        self.k.S.barrier()
        return self.es.__exit__(*a)


class K:
    def __init__(self, nlayers, taps=()):
        self.nc = bass.Bass("TRN2", target_bir_lowering=False)
        self.S = Sched(self.nc)
        self.nl = nlayers
        self.taps = set(taps)
        self.ins = {}
        self.dr = {}

    def inp(self, name, shape, dt=F32):
        t = self.nc.dram_tensor(name, list(shape), dt, kind="ExternalInput").ap()
        self.ins[name] = t
        return t

    def scratch(self, name, shape, dt=F32, out=False):
        kind = "ExternalOutput" if (out or name in self.taps) else "Internal"
        t = self.nc.dram_tensor(name, list(shape), dt, kind=kind).ap()
        self.dr[name] = t
        return t

    def act(self, out, in_, func, r, w, bias=0.0, scale=1.0, accum=None):
        nc = self.nc
        if accum is None:
            return self.S.op('act', lambda: nc.scalar.activation(out=out, in_=in_, func=func, bias=bias, scale=scale), r, w)
        return self.S.op('act', lambda: nc.scalar.activation(out=out, in_=in_, func=func, bias=bias, scale=scale, accum_out=accum), r, w)

    def ts(self, eng, out, in0, s1, s2, op0, op1, r, w):
        e = self.S.e[eng]
        if op1 is None:
            return self.S.op(eng, lambda: e.tensor_scalar(out=out, in0=in0, scalar1=s1, scalar2=None, op0=op0), r, w)
        return self.S.op(eng, lambda: e.tensor_scalar(out=out, in0=in0, scalar1=s1, scalar2=s2, op0=op0, op1=op1), r, w)

    def tt(self, eng, out, in0, in1, op, r, w):
        e = self.S.e[eng]
        return self.S.op(eng, lambda: e.tensor_tensor(out=out, in0=in0, in1=in1, op=op), r, w)

    def stt(self, eng, out, in0, scalar, in1, op0, op1, r, w):
        e = self.S.e[eng]
        return self.S.op(eng, lambda: e.scalar_tensor_tensor(out=out, in0=in0, scalar=scalar, in1=in1, op0=op0, op1=op1), r, w)

    def copy(self, eng, out, in_, r, w):
        if eng == 'act':
            return self.S.op('act', lambda: self.nc.scalar.copy(out=out, in_=in_), r, w)
        e = self.S.e[eng]
        return self.S.op(eng, lambda: e.tensor_copy(out=out, in_=in_), r, w)

    def mm(self, out, pairs, r, w, start=True, stop=True, sgc=False):
        nc = self.nc
        n = len(pairs)
        fns = []
        for i, (l, rh) in enumerate(pairs):
            fns.append(lambda l=l, rh=rh, i=i: nc.tensor.matmul(out, lhsT=l, rhs=rh, start=(start and i == 0), stop=(stop and i == n - 1),
                                                               skip_group_check=(sgc or not start)))
        return self.S.pe_group(fns, r, w)

    def transpose(self, out, in_, ident, r, w):
        nc = self.nc
        return self.S.op('pe', lambda: nc.tensor.transpose(out=out, in_=in_, identity=ident), r, w)

    def dma(self, q, out, in_, r=(), w=(), **kw):
        return self.S.dma(q, out, in_, r, w, **kw)


def w_in_perm_index():
    idx = list(range(0, 896))
    idx += list(range(896, 1664))
    for c in range(3):
        idx += list(range(2054 + c * 64, 2054 + c * 64 + 64))
        idx += list(range(2054 + (c + 3) * 64, 2054 + (c + 3) * 64 + 64))
    idx += list(range(2438, 2566))
    idx += list(range(2566, 2694))
    idx += list(range(2694, 2822))
    idx += list(range(2950, 3078))
    idx += list(range(2048, 2054))
    idx += list(range(1664, 2048))
    idx += list(range(2822, 2950))
    idx += list(range(3078, 3206))
    idx += list(range(3206, 3224))
    assert len(idx) == N_IN and len(set(idx)) == N_IN
    return np.array(idx)


QKT_ROWS = 1664


def setup_globals(k):
    nc = k.nc
    k.x_in = k.inp("x", [S_LEN, D])
    k.cT = k.inp("cT", [128, 8])
    k.ada_w = k.inp("ada_w", [DEPTH, D, 6 * D])
    k.ada_b_fm = k.inp("ada_b_fm", [DEPTH, 128, 48])
    k.ada_b_row = k.inp("ada_b_row", [DEPTH, 6 * D])
    k.normg_fm = k.inp("normg_fm", [DEPTH, 4, 128, 8])
    k.normg_row = k.inp("normg_row", [DEPTH, 4, D])
    k.w_in = k.inp("w_in_p", [DEPTH, D, N_IN])
    k.ident_bf_d = k.inp("ident_bf", [128, 128], BF16)
    k.ident_f_d = k.inp("ident_f", [128, 128], F32)

    k.PT = k.scratch("PT", [896, S_LEN], F32)
    k.QKT = k.scratch("QKT", [QKT_ROWS, S_LEN], BF16)
    k.FL = k.scratch("FL", [6, S_LEN], F32)
    k.VT = k.scratch("VT", [S_LEN, 640], BF16)
    k.GT = k.scratch("GT", [S_LEN, 18], F32)
    k.Y = k.scratch("Y", [S_LEN, D], BF16)
    k.XR = k.scratch("XR", [S_LEN, D], F32)
    k.XR1 = k.scratch("XR1", [S_LEN, D], F32)
    k.OUT = k.scratch("out", [S_LEN, D], F32, out=True)

    def pers(name, shape, dt):
        return Tile(nc.alloc_sbuf_tensor(name, list(shape), dt), name)
    k.ident_bf = pers("ident_bf_sb", [128, 128], BF16)
    k.ident_f = pers("ident_f_sb", [128, 128], F32)
    k.sc = pers("sc", [128, 8], F32)
    k.modAB = pers("modAB", [128, 32], F32)
    k.gm_row = pers("gm_row", [128, D], F32)
    k.gf_row = pers("gf_row", [128, D], F32)
    k.dma('sp', k.ident_bf[:], k.ident_bf_d, w=[k.ident_bf])
    k.dma('sp', k.ident_f[:], k.ident_f_d, w=[k.ident_f])
    k.dma('sp', k.sc[:], k.cT, w=[k.sc])
    k.act(k.sc[:], k.sc[:], AF.Silu, r=[k.sc], w=[k.sc])


def stage_mod(k, l):
    with Scope(k) as sc:
        slab = [sc.sb("adaslab%d" % i, [128, 6 * D], F32) for i in range(4)]
        psA = sc.ps("psA", [128, 32])
        psR = [sc.ps("psR%d" % i, [128, 512]) for i in range(4)]
        bfm = sc.sb("bfm", [128, 48], F32)
        gfm = sc.sb("gfm", [128, 4, 8], F32)
        brow = sc.sb("brow", [128, 2, D], F32)
        grow = sc.sb("grow", [128, 2, D], F32)
        mfm = sc.sb("mfm", [128, 32], F32)
        sc_rep = sc.sb("sc_rep", [128, 8, 128], F32)
        for kc in range(8):
            k.copy('dve', sc_rep[:, kc, :], k.sc[:, kc:kc + 1].to_broadcast([128, 128]), r=[k.sc], w=[sc_rep])
        k.dma('sp', bfm[:], k.ada_b_fm[l], w=[bfm])
        k.dma('sp', gfm[:], k.normg_fm[l].rearrange("g p c -> p g c"), w=[gfm])
        k.dma('sp', brow[:, 0, :], k.ada_b_row[l:l + 1, 2 * D:3 * D].broadcast_to([128, D]), w=[brow])
        k.dma('sp', brow[:, 1, :], k.ada_b_row[l:l + 1, 5 * D:6 * D].broadcast_to([128, D]), w=[brow])
        k.dma('sp', grow[:, 0, :], k.normg_row[l, 1:2, :].broadcast_to([128, D]), w=[grow])
        k.dma('sp', grow[:, 1, :], k.normg_row[l, 3:4, :].broadcast_to([128, D]), w=[grow])
        fm_chunks = list(range(0, 16)) + list(range(24, 40))
        row_cols = [2 * D, 2 * D + 512, 5 * D, 5 * D + 512]
        for kc in range(8):
            sl = slab[kc % 4]
            k.dma('sp' if kc % 2 == 0 else 'act', sl[:], k.ada_w[l, kc * 128:(kc + 1) * 128, :], w=[sl])
            for i, j in enumerate(fm_chunks):
                k.mm(psA[:, i:i + 1], [(sl[:, j * 128:(j + 1) * 128], k.sc[:, kc:kc + 1])], r=[sl, k.sc], w=[psA],
                     start=(kc == 0 and i == 0), stop=(kc == 7), sgc=True)
            for i, c0 in enumerate(row_cols):
                k.mm(psR[i][:], [(sc_rep[:, kc, :], sl[:, c0:c0 + 512])], r=[sl, sc_rep], w=[psR[i]],
                     start=(kc == 0), stop=(kc == 7), sgc=True)
        k.tt('dve', mfm[:, 0:16], psA[:, 0:16], bfm[:, 0:16], ALU.add, r=[psA, bfm], w=[mfm])
        k.tt('dve', mfm[:, 16:32], psA[:, 16:32], bfm[:, 24:40], ALU.add, r=[psA, bfm], w=[mfm])
        k.stt('dve', k.modAB[:, 0:8], mfm[:, 8:16], 1.0, gfm[:, 0, :], ALU.add, ALU.mult, r=[mfm, gfm], w=[k.modAB])
        k.copy('dve', k.modAB[:, 8:16], mfm[:, 0:8], r=[mfm], w=[k.modAB])
        k.stt('dve', k.modAB[:, 16:24], mfm[:, 24:32], 1.0, gfm[:, 2, :], ALU.add, ALU.mult, r=[mfm, gfm], w=[k.modAB])
        k.copy('dve', k.modAB[:, 24:32], mfm[:, 16:24], r=[mfm], w=[k.modAB])
        for i in range(4):
            dst = (k.gm_row if i < 2 else k.gf_row)
            cs = slice((i % 2) * 512, (i % 2) * 512 + 512)
            k.tt('dve', dst[:, cs], psR[i][:], brow[:, i // 2, cs], ALU.add, r=[psR[i], brow], w=[dst])
            k.tt('pool', dst[:, cs], dst[:, cs], grow[:, i // 2, cs], ALU.mult, r=[dst, grow], w=[dst])


def stage_proj(k, l, xsrc):
    nc = k.nc
    with Scope(k) as sc:
        wsb = sc.sb("wsb", [128, 8, N_IN], BF16)
        wst = [sc.sb("wst%d" % i, [128, N_IN], F32) for i in range(4)]
        xt = [sc.sb("xt%d" % i, [128, D], F32) for i in range(2)]
        junk = sc.sb("junk", [128, D], BF16)
        xn = [sc.sb("xn%d" % i, [128, D], BF16) for i in range(2)]
        st = [sc.sb("st%d" % i, [128, 4], F32) for i in range(2)]
        HT = [sc.sb("HT%d" % i, [128, 8, 512], BF16) for i in range(2)]
        psT = [sc.ps("psT%d" % i, [128, D], BF16) for i in range(2)]
        psM = [sc.ps("psM%d" % i, [128, 512]) for i in range(4)]
        evf = [sc.sb("evf%d" % i, [128, 512], F32) for i in range(3)]
        evb = [sc.sb("evb%d" % i, [128, 512], BF16) for i in range(3)]
        evt = [sc.sb("evt%d" % i, [128, 640], BF16) for i in range(2)]
        evg = [sc.sb("evg%d" % i, [128, 18], F32) for i in range(2)]
        for kc in range(8):
            s = wst[kc % 4]
            k.dma('sp' if kc % 2 == 0 else 'act', s[:], k.w_in[l, kc * 128:(kc + 1) * 128, :], w=[s])
            k.copy('pool' if kc % 2 == 0 else 'dve', wsb[:, kc, :], s[:], r=[s], w=[wsb])
        cnt = {"ev": 0, "pm": 0}

        def norm_gen(tb):
            ht = HT[tb % 2]
            t0_ = tb * 4
            k.dma('act', xt[t0_ % 2][:], xsrc[t0_ * 128:(t0_ + 1) * 128, :], w=[xt[t0_ % 2]])
            yield
            for sub in range(4):
                ti = tb * 4 + sub
                x_ = xt[ti % 2]; xn_ = xn[ti % 2]; st_ = st[ti % 2]; pt_ = psT[ti % 2]
                k.act(junk[:], x_[:], AF.Square, r=[x_], w=[junk, st_], scale=1.0 / 32.0, accum=st_[:, 0:1])
                k.act(st_[:, 1:2], st_[:, 0:1], AF.Ln, r=[st_], w=[st_], bias=RMS_EPS)
                k.act(st_[:, 2:3], st_[:, 1:2], AF.Exp, r=[st_], w=[st_], scale=-0.5)
                k.ts('dve', xn_[:], x_[:], st_[:, 2:3], None, ALU.mult, None, r=[x_, st_], w=[xn_])
                if sub < 3:
                    k.dma('act', xt[(ti + 1) % 2][:], xsrc[(ti + 1) * 128:(ti + 2) * 128, :], w=[xt[(ti + 1) % 2]])
                yield
                yield
                for kc in range(8):
                    k.transpose(pt_[:, kc * 128:(kc + 1) * 128], xn_[:, kc * 128:(kc + 1) * 128], k.ident_bf[:],
                                r=[xn_, k.ident_bf], w=[pt_])
                    if kc == 3:
                        yield
                yield
                for kc in range(8):
                    o = ht[:, kc, sub * 128:(sub + 1) * 128]
                    i_ = pt_[:, kc * 128:(kc + 1) * 128]
                    if kc % 2 == 0:
                        k.ts('dve', o, i_, k.modAB[:, kc:kc + 1], k.modAB[:, 8 + kc:9 + kc], ALU.mult, ALU.add,
                             r=[pt_, k.modAB], w=[ht])
                    else:
                        k.act(o, i_, AF.Identity, r=[pt_, k.modAB], w=[ht], scale=k.modAB[:, kc:kc + 1],
                              bias=k.modAB[:, 8 + kc:9 + kc])
                yield

        def step(g):
            if g is not None:
                try:
                    next(g)
                except StopIteration:
                    return None
            return g

        for _ in norm_gen(0):
            pass
        for tb in range(NTB):
            ht = HT[tb % 2]
            g = norm_gen(tb + 1) if tb + 1 < NTB else None
            tsl = slice(tb * 512, (tb + 1) * 512)
            fm = [(c * 128, 128, 'PT', c * 128) for c in range(7)]
            fm += [(896 + c * 128, 128, 'QKT', c * 128) for c in range(13)]
            fm += [(2560, 6, 'FL', 0)]
            for (c0, m, dst, r0) in fm:
                ps = psM[cnt["pm"] % 4]; cnt["pm"] += 1
                k.mm(ps[0:m, :], [(wsb[:, kc, c0:c0 + m], ht[:, kc, :]) for kc in range(8)], r=[wsb, ht], w=[ps])
                eng = 'act' if cnt["ev"] % 2 == 0 else 'dve'
                if dst == 'QKT':
                    ev = evb[cnt["ev"] % 3]
                    dd = k.QKT[r0:r0 + m, tsl]
                else:
                    ev = evf[cnt["ev"] % 3]
                    dd = (k.PT if dst == 'PT' else k.FL)[r0:r0 + m, tsl]
                cnt["ev"] += 1
                k.copy(eng, ev[0:m, :], ps[0:m, :], r=[ps], w=[ev])
                k.dma('sp', dd, ev[0:m, :], r=[ev])
                g = step(g)
            for sub in range(4):
                ti = tb * 4 + sub
                tok = slice(ti * 128, (ti + 1) * 128)
                ps0 = psM[cnt["pm"] % 4]; cnt["pm"] += 1
                ps1 = psM[cnt["pm"] % 4]; cnt["pm"] += 1
                lhs = lambda kc: ht[:, kc, sub * 128:(sub + 1) * 128]
                k.mm(ps0[:, 0:384], [(lhs(kc), wsb[:, kc, 2566:2950]) for kc in range(8)], r=[wsb, ht], w=[ps0])
                k.mm(ps1[:, 0:274], [(lhs(kc), wsb[:, kc, 2950:3224]) for kc in range(8)], r=[wsb, ht], w=[ps1])
                et = evt[ti % 2]; eg = evg[ti % 2]
                k.copy('act', et[:, 0:384], ps0[:, 0:384], r=[ps0], w=[et])
                k.copy('dve', et[:, 384:640], ps1[:, 0:256], r=[ps1], w=[et])
                k.copy('dve', eg[:], ps1[:, 256:274], r=[ps1], w=[eg])
                k.dma('sp', k.VT[tok, :], et[:], r=[et])
                k.dma('sp', k.GT[tok, :], eg[:], r=[eg])
                g = step(g)
            while g is not None:
                g = step(g)


def prep_shared(inp):
    f = lambda a: np.ascontiguousarray(np.asarray(a, dtype=np.float32))
    sh = {}
    sh["ada_w"] = f(inp["ada_w"])
    sh["ada_b_fm"] = f(np.asarray(inp["ada_b"]).reshape(DEPTH, 48, 128).transpose(0, 2, 1))
    sh["ada_b_row"] = f(inp["ada_b"])
    sh["normg_fm"] = f(np.asarray(inp["norm_g"]).reshape(DEPTH, 4, 8, 128).transpose(0, 1, 3, 2))
    sh["normg_row"] = f(inp["norm_g"])
    sh["w_in_p"] = f(np.asarray(inp["w_in"])[:, :, w_in_perm_index()])
    sh["ident_bf"] = np.eye(128, dtype=np.float32).astype(NPBF)
    sh["ident_f"] = np.eye(128, dtype=np.float32)
    sh["w_out"] = f(inp["w_out"]); sh["ffn_up"] = f(inp["ffn_up"]); sh["ffn_down"] = f(inp["ffn_down"])
    sh["conv_w_fm"] = f(np.asarray(inp["ffn_conv_w"]).reshape(DEPTH, 3, 44, 128).transpose(0, 3, 1, 2))
    sh["conv_b_fm"] = f(np.asarray(inp["ffn_conv_b"]).reshape(DEPTH, 44, 128).transpose(0, 2, 1))
    sh.update(nsa_host_consts())
    sh["rel_bias"] = f(inp["rel_bias"])
    sh["nsa_pe_kT"] = f(np.asarray(inp["nsa_pe_k"]).transpose(0, 2, 1))
    sh["nsa_pe_vT"] = f(np.asarray(inp["nsa_pe_v"]).transpose(0, 2, 1))
    for n in ("nsa_ck_w1", "nsa_cv_w1", "nsa_ck_w2", "nsa_cv_w2"):
        sh[n] = f(inp[n])
    sh["fox_b_f"] = f(np.asarray(inp["fox_b_f"]).reshape(DEPTH, 6, 1))
    sh.update(rwkv_host(inp))
    return sh


def prep_core(inp, b):
    d = {}
    d["x"] = np.ascontiguousarray(np.asarray(inp["x"][b], dtype=np.float32))
    d["cT"] = np.ascontiguousarray(np.asarray(inp["c"][b], dtype=np.float32).reshape(8, 128).T)
    return d


def setup_fox(k):
    k.fox_bf = k.inp("fox_b_f", [DEPTH, 6, 1])
    k.CUMA = k.scratch("CUMA", [6, 3, S_LEN], BF16)


def stage_fox(k, l):
    nc = k.nc
    with Scope(k) as sc:
        nb = sc.sb("nb", [128, 32, 6], F32)
        with Scope(k) as s2:
            fl = s2.sb("fl", [6, S_LEN], F32)
            t1 = s2.sb("t1", [6, S_LEN], F32)
            ones = s2.sb("ones", [6, S_LEN], F32)
            cum = s2.sb("cum", [6, S_LEN], F32)
            parts = s2.sb("parts", [6, 3, S_LEN], BF16)
            bfv = s2.sb("bfv", [6, 2], F32)
            psn = s2.ps("psn", [128, 512])
            k.dma('sp', fl[:], k.FL, w=[fl])
            k.dma('sp', bfv[:, 0:1], k.fox_bf[l], w=[bfv])
            k.ts('dve', bfv[:, 1:2], bfv[:, 0:1], -1.0, None, ALU.mult, None, r=[bfv], w=[bfv])
            k.S.op('pool', lambda: nc.gpsimd.memset(ones[:], 1.0), [], [ones])
            k.act(t1[:], fl[:], AF.Exp, r=[fl, bfv], w=[t1], bias=bfv[:, 1:2], scale=-1.0)
            k.act(t1[:], t1[:], AF.Ln, r=[t1], w=[t1], bias=1.0, scale=1.0)
            k.ts('dve', t1[:], t1[:], -1.0, None, ALU.mult, None, r=[t1], w=[t1])
            k.S.op('dve', lambda: nc.vector.tensor_tensor_scan(out=cum[:], data0=ones[:], data1=t1[:], initial=0.0,
                                                               op0=ALU.mult, op1=ALU.add), [ones, t1], [cum])
            for t in range(32):
                k.transpose(psn[:, t * 6:(t + 1) * 6], cum[:, t * 128:(t + 1) * 128], k.ident_f[0:6, 0:6],
                            r=[cum, k.ident_f], w=[psn])
            k.ts('dve', nb[:].rearrange("p t h -> p (t h)"), psn[:, 0:192], -1.0, None, ALU.mult, None, r=[psn], w=[nb])
            k.ts('dve', t1[:], cum[:], 8.0, None, ALU.mult, None, r=[cum], w=[t1])
            k.copy('dve', parts[:, 0, :], t1[:], r=[t1], w=[parts])
            k.tt('dve', t1[:], t1[:], parts[:, 0, :], ALU.subtract, r=[t1, parts], w=[t1])
            k.copy('dve', parts[:, 1, :], t1[:], r=[t1], w=[parts])
            k.tt('dve', t1[:], t1[:], parts[:, 1, :], ALU.subtract, r=[t1, parts], w=[t1])
            k.copy('dve', parts[:, 2, :], t1[:], r=[t1], w=[parts])
            k.dma('sp', k.CUMA, parts[:], r=[parts], w=["CUMA"])
        QA = [sc.sb("QA%d" % i, [128, S_LEN], BF16) for i in range(2)]
        KA = [sc.sb("KA%d" % i, [128, S_LEN], BF16) for i in range(2)]
        VA = sc.sb("VA", [128, 32, 6, 65], BF16)
        yb = sc.sb("yb", [128, 32, 384], BF16)
        PTl = [sc.sb("PTl%d" % i, [128, 512], BF16) for i in range(6)]
        rc = [sc.sb("rc%d" % i, [128, 4], F32) for i in range(2)]
        psS = [sc.ps("psS%d" % i, [128, 512]) for i in range(4)]
        psO = [sc.ps("psO%d" % i, [128, 512]) for i in range(2)]
        k.dma('sp', yb[:], k.VT[:, 0:384].rearrange("(t p) c -> p t c", p=128), w=[yb])
        k.S.op('pool', lambda: nc.gpsimd.memset(VA[:, :, :, 64:65], 1.0), [], [VA])
        k.copy('pool', VA[:, :, :, 0:64], yb[:].rearrange("p t (h d) -> p t h d", h=6), r=[yb], w=[VA])
        for i in range(2):
            k.S.op('dve', lambda i=i: nc.vector.memset(KA[i][64:67, :], 1.0), [], [KA[i]])
        nS = 0
        nO = 0
        nP = 0
        pipe = Pipe(3)
        for h in range(6):
            qa = QA[h % 2]; ka = KA[h % 2]
            k.dma('sp', qa[0:64, :], k.QKT[h * 64:(h + 1) * 64, :], w=[qa])
            k.dma('sp', qa[64:67, :], k.CUMA[h], w=[qa])
            k.dma('sp', ka[0:64, :], k.QKT[384 + h * 64:384 + (h + 1) * 64, :], w=[ka])
            for qb in range(NTB):
                po = psO[nO % 2]; nO += 1
                nkt = 4 * qb + 4
                for kt in range(nkt):
                    j = kt - 4 * qb
                    c0 = max(j, 0) * 128
                    ps = psS[nS % len(psS)]; nS += 1
                    pt = PTl[nP % len(PTl)]; nP += 1

                    def first(ps=ps, pt=pt, kt=kt, c0=c0, j=j, qa=qa, ka=ka, qb=qb, h=h):
                        k.mm(ps[:, c0:512], [(ka[0:67, kt * 128:(kt + 1) * 128], qa[0:67, qb * 512 + c0:(qb + 1) * 512])],
                             r=[ka, qa], w=[ps])
                        k.act(pt[:, c0:512], ps[:, c0:512], AF.Exp, r=[ps, nb], w=[pt], bias=nb[:, kt, h:h + 1], scale=0.125)
                        if j >= 0:
                            k.S.op('pool', lambda: nc.gpsimd.affine_select(
                                out=pt[:, c0:c0 + 128], in_=pt[:, c0:c0 + 128], pattern=[[1, 128]], compare_op=ALU.is_ge,
                                fill=0.0, base=0, channel_multiplier=-1), [pt], [pt])

                    def second(pt=pt, kt=kt, j=j, po=po, qb=qb, h=h, last=(kt == nkt - 1)):
                        fns = []
                        for qs in range(max(j, 0), 4):
                            fns.append(lambda qs=qs: nc.tensor.matmul(
                                po[:, qs * 65:(qs + 1) * 65], lhsT=pt[:, qs * 128:(qs + 1) * 128], rhs=VA[:, kt, h, :],
                                start=(kt == 0 and qs == 0), stop=(kt == 4 * qb + qs), skip_group_check=True))
                        k.S.pe_group(fns, [pt, VA], [po])
                        if last:
                            r_ = rc[qb % 2]
                            pov = po[:, 0:260].rearrange("p (q c) -> p q c", c=65)
                            k.S.op('dve', lambda: nc.vector.reciprocal(out=r_[:], in_=pov[:, :, 64]), [po], [r_])
                            for qs in range(4):
                                k.ts('dve', yb[:, qb * 4 + qs, h * 64:(h + 1) * 64], po[:, qs * 65:qs * 65 + 64], r_[:, qs:qs + 1], None,
                                     ALU.mult, None, r=[po, r_], w=[yb])
                    pipe.push(first, second)
        pipe.flush()
        k.dma('sp', k.Y[:, 256:640].rearrange("(t p) c -> p t c", p=128), yb[:], r=[yb], w=["Y"])


LW = 1536
LC = 4608
NEG8 = -240000.0


def t5_bucket_np(n):
    n = np.maximum(n, 0)
    nf = np.maximum(n, 1).astype(np.float32)
    large = 16 + (np.log(nf / np.float32(16)) / np.float32(np.log(128 / 16)) * np.float32(16)).astype(np.int32)
    large = np.minimum(large, 31)
    return np.where(n < 16, n, large)


def nsa_host_consts():
    c = {}
    i = np.arange(LW); n = i - 511
    oh = np.zeros((33, LW), np.float32)
    ok = (n >= 0) & (n < 512)
    oh[t5_bucket_np(n)[ok], i[ok]] = 1.0
    oh[32, ~ok] = NEG8
    c["oh_w"] = oh
    i = np.arange(LC); n = i - 2063
    oh = np.zeros((33, LC), np.float32)
    ok = n >= 0
    oh[t5_bucket_np(n)[ok], i[ok]] = 1.0
    oh[32, ~ok] = NEG8
    c["oh_c"] = oh
    s_ = np.arange(S_LEN)
    c["E_all"] = (np.arange(64)[:, None] == (s_[None, :] // 64)).astype(np.float32).astype(NPBF)
    cs = np.arange(256) * 16
    ce = cs + 31
    ss = np.arange(64) * 64
    ov = ((cs[:, None] <= ss[None, :] + 63) & (ce[:, None] >= ss[None, :])).astype(np.float32)
    ov[255] = 0.0
    c["ovl"] = np.ascontiguousarray(ov.reshape(2, 128, 64).transpose(1, 0, 2)).astype(NPBF)
    t = np.arange(S_LEN)
    cur = t // 64
    jb = np.arange(64)
    back = cur[:, None] - jb[None, :]
    valid = back >= 0
    forced = (jb[None, :] == 0) | (valid & (back < 2))
    tkm = (valid & ~forced).astype(np.float32)
    tka = np.where(valid, np.where(forced, 1e4, 0.0), -1.0).astype(np.float32)
    c["tkm"] = np.ascontiguousarray(tkm.reshape(32, 128, 64).transpose(1, 0, 2)).astype(NPBF)
    c["tka"] = np.ascontiguousarray(tka.reshape(32, 128, 64).transpose(1, 0, 2)).astype(NPBF)
    return c


def setup_nsa(k):
    nc = k.nc
    k.rel_bias = k.inp("rel_bias", [32, 6])
    k.oh_w = k.inp("oh_w", [33, LW])
    k.oh_c = k.inp("oh_c", [33, LC])
    k.E_d = k.inp("E_all", [64, S_LEN], BF16)
    k.ovl_d = k.inp("ovl", [128, 2, 64], BF16)
    k.tkm_d = k.inp("tkm", [128, 32, 64], BF16)
    k.tka_d = k.inp("tka", [128, 32, 64], BF16)
    k.pe_kT = k.inp("nsa_pe_kT", [DEPTH, 64, 32])
    k.pe_vT = k.inp("nsa_pe_vT", [DEPTH, 64, 32])
    k.ck_w1 = k.inp("nsa_ck_w1", [DEPTH, 2048, 128])
    k.cv_w1 = k.inp("nsa_cv_w1", [DEPTH, 2048, 128])
    k.ck_w2 = k.inp("nsa_ck_w2", [DEPTH, 128, 64])
    k.cv_w2 = k.inp("nsa_cv_w2", [DEPTH, 128, 64])
    k.WVW = k.scratch("WVW", [6, 128, LW], BF16)
    k.WVC = k.scratch("WVC", [6, 128, LC], BF16)
    with Scope(k) as sc:
        rb = sc.sb("rb", [33, 6], F32)
        rb31 = sc.sb("rb31", [32, 6], F32)
        rrep = sc.sb("rrep", [33, 6, 128], F32)
        ohw = sc.sb("ohw", [33, LW], F32)
        ohc = sc.sb("ohc", [33, LC], F32)
        ps = [sc.ps("psb%d" % i, [128, 512]) for i in range(2)]
        ev = [sc.sb("evb%d" % i, [128, 512], BF16) for i in range(2)]
        k.dma('sp', rb[0:32, :], k.rel_bias, w=[rb])
        k.dma('sp', rb31[:], k.rel_bias[31:32, :].broadcast_to([32, 6]), w=[rb31])
        k.dma('sp', ohw[:], k.oh_w, w=[ohw])
        k.dma('sp', ohc[:], k.oh_c, w=[ohc])
        k.S.op('dve', lambda: nc.vector.memset(rb[32:33, :], 1.0), [], [rb])
        k.tt('dve', rb[0:32, :], rb[0:32, :], rb31[:], ALU.subtract, r=[rb, rb31], w=[rb])
        k.ts('dve', rb[0:32, :], rb[0:32, :], 8.0, None, ALU.mult, None, r=[rb], w=[rb])
        for h in range(6):
            k.copy('dve', rrep[:, h, :], rb[:, h:h + 1].to_broadcast([33, 128]), r=[rb], w=[rrep])
        n = 0
        for h in range(6):
            for (oh, L, dst) in ((ohw, LW, k.WVW), (ohc, LC, k.WVC)):
                for c0 in range(0, L, 512):
                    p_ = ps[n % 2]; e_ = ev[n % 2]; n += 1
                    k.mm(p_[:], [(rrep[:, h, :], oh[:, c0:c0 + 512])], r=[rrep, oh], w=[p_])
                    k.copy('act' if n % 2 else 'dve', e_[:], p_[:], r=[p_], w=[e_])
                    k.dma('sp', dst[h, :, c0:c0 + 512], e_[:], r=[e_])


class DbgStop(Exception):
    pass


def dbg(k, lvl):
    if getattr(k, 'dbg_stop', None) == lvl:
        raise DbgStop()


def stage_nsa(k, l):
    nc = k.nc
    with Scope(k) as sc:
        Gw = sc.sb("Gw", [128, 6, 1408], BF16)
        Gc = sc.sb("Gc", [128, 6, 2560], BF16)
        tkm = sc.sb("tkm", [128, 32, 64], BF16)
        tka = sc.sb("tka", [128, 32, 64], BF16)
        QN = [sc.sb("QN%d" % h, [128, S_LEN], BF16) for h in range(6)]
        KE = [sc.sb("KE%d" % g, [128, S_LEN], BF16) for g in range(2)]
        KW = sc.sb("KW", [128, S_LEN], BF16)
        VS = sc.sb("VS", [128, 32, 2, 65], BF16)
        VW = sc.sb("VW", [128, 32, 2, 65], BF16)
        KCMP = sc.sb("KCMP", [128, 256], BF16)
        VE = sc.sb("VE", [128, 2, 2, 129], BF16)
        sg = sc.sb("sg", [128, 32, 18], F32)
        for h in range(6):
            k.dma('sp', Gw[:, h, :], bass.AP(k.WVW.tensor, h * 128 * LW + 127, [[LW - 1, 128], [1, 1408]]), w=[Gw])
            k.dma('sp', Gc[:, h, :], bass.AP(k.WVC.tensor, h * 128 * LC + 2032, [[LC - 16, 128], [1, 2560]]), w=[Gc])
        k.dma('sp', tkm[:], k.tkm_d, w=[tkm])
        k.dma('sp', tka[:], k.tka_d, w=[tka])
        for h in range(6):
            g_, hp_ = h // 3, h % 3
            k.dma('sp', QN[h][g_ * 64:(g_ + 1) * 64, :], k.QKT[768 + hp_ * 128 + g_ * 64:768 + hp_ * 128 + (g_ + 1) * 64, :], w=[QN[h]])
            k.S.op('pool', lambda h=h, g_=g_: nc.gpsimd.memset(QN[h][(1 - g_) * 64:(2 - g_) * 64, :], 0.0), [], [QN[h]])
        for g_ in range(2):
            k.dma('sp', KE[g_][g_ * 64:(g_ + 1) * 64, :], k.QKT[1408 + g_ * 64:1408 + (g_ + 1) * 64, :], w=[KE[g_]])
            k.dma('sp', KE[g_][(1 - g_) * 64:(2 - g_) * 64, :], k.E_d, w=[KE[g_]])
        k.dma('sp', KW[:], k.QKT[1536:1664, :], w=[KW])
        k.dma('sp', sg[:], k.GT.rearrange("(t p) c -> p t c", p=128), w=[sg])
        k.act(sg[:], sg[:], AF.Exp, r=[sg], w=[sg], scale=-1.0)
        k.ts('dve', sg[:], sg[:], 1.0, None, ALU.add, None, r=[sg], w=[sg])
        k.S.op('dve', lambda: nc.vector.reciprocal(out=sg[:], in_=sg[:]), [sg], [sg])
        k.dma('sp', VE[:, 0, :, 65:129], k.ovl_d, w=[VE])
        k.dma('sp', VE[:, 1, :, 65:129], k.ovl_d, w=[VE])
        k.S.op('pool', lambda: nc.gpsimd.memset(VE[:, :, :, 64:65], 1.0), [], [VE])
        k.S.op('pool', lambda: nc.gpsimd.memset(VE[:, :, :, 0:64], 0.0), [], [VE])
        k.S.op('pool', lambda: nc.gpsimd.memset(KCMP[:], 0.0), [], [KCMP])
        dbg(k, 1)
        with Scope(k) as s2:
            vst = s2.sb("vst", [128, 32, 256], BF16)
            k.dma('sp', vst[:], k.VT[:, 384:640].rearrange("(t p) c -> p t c", p=128), w=[vst])
            k.S.op('pool', lambda: nc.gpsimd.memset(VS[:, :, :, 64:65], 1.0), [], [VS])
            k.S.op('pool', lambda: nc.gpsimd.memset(VW[:, :, :, 64:65], 1.0), [], [VW])
            k.copy('pool', VS[:, :, :, 0:64], vst[:, :, 0:128].rearrange("p t (g d) -> p t g d", g=2), r=[vst], w=[VS])
            k.copy('pool', VW[:, :, :, 0:64], vst[:, :, 128:256].rearrange("p t (g d) -> p t g d", g=2), r=[vst], w=[VW])
        dbg(k, 2)
        with Scope(k) as s2:
            KC = s2.sb("KC", [128, S_LEN], BF16)
            VC = s2.sb("VC", [128, S_LEN], BF16)
            k.dma('sp', KC[:], k.QKT[1152:1280, :], w=[KC])
            k.dma('sp', VC[:], k.QKT[1280:1408, :], w=[VC])
            w1s = s2.sb("w1s", [128, 16, 128], F32)
            w1b = [s2.sb("w1b%d" % i, [128, 32, 128], BF16) for i in range(2)]
            w2s = s2.sb("w2s", [128, 2, 64], F32)
            w2b = s2.sb("w2b", [128, 2, 64], BF16)
            pes = s2.sb("pes", [128, 2, 32], F32)
            peb = s2.sb("peb", [128, 2, 32], BF16)
            hb = s2.sb("hb", [128, 2], F32)
            gx = s2.sb("gx", [128, 256], F32)
            gu = s2.sb("gu", [128, 256], F32)
            gg = s2.sb("gg", [128, 256], BF16)
            psh = s2.ps("psh", [128, 512])
            psb_ = s2.ps("pshb", [128, 512])
            pso = s2.ps("pso", [128, 512])
            for kv, (w1d, w2d, ped) in enumerate(((k.ck_w1, k.ck_w2, k.pe_kT), (k.cv_w1, k.cv_w2, k.pe_vT))):
                for lh in range(2):
                    for half in range(2):
                        k.dma('sp', w1s[half * 64:(half + 1) * 64, :, :],
                              w1d[l, lh * 1024:(lh + 1) * 1024, :].rearrange("(l d) h -> d l h", d=64), w=[w1s])
                    k.copy('pool', w1b[kv][:, lh * 16:(lh + 1) * 16, :], w1s[:], r=[w1s], w=[w1b[kv]])
                for half in range(2):
                    k.dma('sp', pes[half * 64:(half + 1) * 64, kv, :], ped[l], w=[pes])
                k.dma('sp', w2s[:, kv, :], w2d[l], w=[w2s])
            k.copy('dve', w2b[:], w2s[:], r=[w2s], w=[w2b])
            w2kd = s2.sb("w2kd", [128, 2, 64], BF16)
            for a_ in range(2):
                k.copy('dve', w2kd[:, a_, :], w2s[:, 0, :], r=[w2s], w=[w2kd])
            k.copy('dve', peb[:], pes[:], r=[pes], w=[peb])
            for kv in range(2):
                src = KC if kv == 0 else VC
                k.mm(psb_[:, kv:kv + 1], [(w1b[kv][0:64, li, :], peb[0:64, kv, li:li + 1]) for li in range(32)],
                     r=[w1b[kv], peb], w=[psb_], start=True)
                k.copy('dve', hb[:, kv:kv + 1], psb_[:, kv:kv + 1], r=[psb_], w=[hb])
                for g in range(2):
                    pr = slice(g * 64, (g + 1) * 64)
                    k.mm(psh[:, 0:255], [(w1b[kv][pr, li, :], src[pr, li:li + 16 * 254 + 1:16]) for li in range(32)],
                         r=[w1b[kv], src], w=[psh])
                    k.ts('dve', gx[:, 0:255], psh[:, 0:255], hb[:, kv:kv + 1], None, ALU.add, None, r=[psh, hb], w=[gx])
                    k.tt('dve', gu[:, 0:255], gx[:, 0:255], gx[:, 0:255], ALU.mult, r=[gx], w=[gu])
                    k.ts('dve', gu[:, 0:255], gu[:, 0:255], 0.044715, 1.0, ALU.mult, ALU.add, r=[gu], w=[gu])
                    k.tt('dve', gu[:, 0:255], gu[:, 0:255], gx[:, 0:255], ALU.mult, r=[gu, gx], w=[gu])
                    k.act(gu[:, 0:255], gu[:, 0:255], AF.Exp, r=[gu], w=[gu], scale=-2.0 * 0.7978845608028654)
                    k.ts('dve', gu[:, 0:255], gu[:, 0:255], 1.0, None, ALU.add, None, r=[gu], w=[gu])
                    k.S.op('dve', lambda: nc.vector.reciprocal(out=gu[:, 0:255], in_=gu[:, 0:255]), [gu], [gu])
                    k.S.op('dve', lambda: nc.vector.memset(gg[:, 255:256], 0.0), [], [gg])
                    k.tt('dve', gg[:, 0:255], gu[:, 0:255], gx[:, 0:255], ALU.mult, r=[gu, gx], w=[gg])
                    if kv == 0:
                        k.mm(pso[:, 0:256], [(w2kd[:].rearrange("p a d -> p (a d)"), gg[:, 0:256])], r=[w2kd, gg], w=[pso])
                        k.copy('dve', KCMP[pr, :], pso[pr, 0:256], r=[pso], w=[KCMP])
                    else:
                        for ct in range(2):
                            k.mm(pso[:, ct * 64:(ct + 1) * 64], [(gg[:, ct * 128:(ct + 1) * 128], w2b[:, 1, :])],
                                 r=[w2b, gg], w=[pso], start=(ct == 0))
                        k.copy('dve', VE[:, g, :, 0:64], pso[:, 0:128].rearrange("p (c d) -> p c d", c=2), r=[pso], w=[VE])
        dbg(k, 3)
        PTl = [sc.sb("PTn%d" % i, [128, 512], BF16) for i in range(6)]
        yacc = [sc.sb("yacc%d" % i, [128, 4, 384], F32) for i in range(2)]
        ybf = [sc.sb("ybf%d" % i, [128, 4, 384], BF16) for i in range(2)]
        impt2 = [[sc.sb("impt%d_%d" % (i, g), [128, 4, 64], F32) for g in range(2)] for i in range(2)]
        scr = sc.sb("scr", [128, 4, 64], F32)
        wk = sc.sb("wk", [128, 4, 64], F32)
        m8 = sc.sb("m8", [128, 4, 16], F32)
        nmq = sc.sb("nmq", [128, 4, 128], BF16)
        rcs = [sc.sb("rcs%d" % i, [128, 8], F32) for i in range(3)]
        psS = [sc.ps("psS%d" % i, [128, 512]) for i in range(4)]
        psO = [sc.ps("psO%d" % i, [128, 512]) for i in range(3)]
        psT = sc.ps("psTn", [128, 1024], BF16)
        st = {"S": 0, "O": 0, "P": 0, "R": 0}

        def q_ap(h, c0, c1):
            g, hp = h // 3, h % 3
            return QN[h][g * 64:(g + 1) * 64, c0:c1]

        def evac(views, h, branch, qb, ya, first):
            r_ = rcs[st["R"] % 3]; st["R"] += 1
            for qs, (po, cb) in enumerate(views):
                if branch == 0:
                    k.ts('dve', r_[:, qs:qs + 1], po[:, cb + 64:cb + 65], 1e-30, None, ALU.max, None, r=[po], w=[r_])
                    k.S.op('dve', lambda r_=r_, qs=qs: nc.vector.reciprocal(out=r_[:, qs:qs + 1], in_=r_[:, qs:qs + 1]), [r_], [r_])
                else:
                    k.S.op('dve', lambda r_=r_, po=po, cb=cb, qs=qs: nc.vector.reciprocal(out=r_[:, qs:qs + 1], in_=po[:, cb + 64:cb + 65]), [po], [r_])
            k.tt('dve', r_[:, 4:8], r_[:, 0:4], sg[:, qb * 4:(qb + 1) * 4, h * 3 + branch], ALU.mult, r=[r_, sg], w=[r_])
            for qs, (po, cb) in enumerate(views):
                o = ya[:, qs, h * 64:(h + 1) * 64]
                if first:
                    k.ts('dve', o, po[:, cb:cb + 64], r_[:, 4 + qs:5 + qs], None, ALU.mult, None, r=[po, r_], w=[ya])
                else:
                    k.stt('dve', o, po[:, cb:cb + 64], r_[:, 4 + qs:5 + qs], o, ALU.mult, ALU.add, r=[po, r_, ya], w=[ya])
            return r_

        pipe = Pipe(3)

        def attend(h, qb, tiles, kmat, vmat, po, g, branch, ya, merged=False):
            hp = h % 3
            nt = len(tiles)
            state = {"first": True}
            for idx, (kt, c0, c1, extra) in enumerate(tiles):
                ps = psS[st["S"] % len(psS)]; st["S"] += 1
                pt = PTl[st["P"] % len(PTl)]; st["P"] += 1

                def first(ps=ps, pt=pt, kt=kt, c0=c0, c1=c1, extra=extra):
                    if merged:
                        fns = [lambda: nc.tensor.matmul(ps[:, c0:c1], lhsT=kmat[:, kt * 128:(kt + 1) * 128],
                                                        rhs=QN[h][:, qb * 512 + c0:qb * 512 + c1],
                                                        start=True, stop=(len(extra) == 0), skip_group_check=True)]
                    else:
                        fns = [lambda: nc.tensor.matmul(ps[:, c0:c1], lhsT=kmat[g * 64:(g + 1) * 64, kt * 128:(kt + 1) * 128],
                                                        rhs=QN[h][g * 64:(g + 1) * 64, qb * 512 + c0:qb * 512 + c1],
                                                        start=True, stop=(len(extra) == 0), skip_group_check=True)]
                    rd = [kmat, QN[h]]
                    for ei, (lt, rt, lap, rap) in enumerate(extra):
                        w_ = rap.shape[-1]
                        fns.append(lambda lap=lap, rap=rap, w_=w_, ei=ei: nc.tensor.matmul(
                            ps[:, c0:c0 + w_], lhsT=lap, rhs=rap, start=False, stop=(ei == len(extra) - 1), skip_group_check=True))
                        rd += [lt, rt]
                    k.S.pe_group(fns, rd, [ps])
                    k.act(pt[:, c0:c1], ps[:, c0:c1], AF.Exp, r=[ps], w=[pt], scale=0.125)

                def second(pt=pt, kt=kt, c0=c0, c1=c1, idx=idx):
                    fns = []
                    for qs in range(c0 // 128, (c1 + 127) // 128):
                        last = all(not (t2[1] <= qs * 128 < t2[2]) for t2 in tiles[idx + 1:])
                        fo = state["first"]
                        state["first"] = False
                        fns.append(lambda qs=qs, fo=fo, last=last: nc.tensor.matmul(
                            po[:, qs * 65:(qs + 1) * 65], lhsT=pt[:, qs * 128:(qs + 1) * 128], rhs=vmat[:, kt, g, :],
                            start=fo, stop=last, skip_group_check=True))
                    k.S.pe_group(fns, [pt, vmat], [po])
                    if idx == nt - 1:
                        evac([(po, qs * 65) for qs in range(4)], h, branch, qb, ya, False)
                pipe.push(first, second)

        def do_cmp(qb):
            ya = yacc[qb % 2]
            impt = impt2[qb % 2]
            for h in range(6):
                g = h // 3
                poA = psO[st["O"] % 3]; st["O"] += 1
                poB = psO[st["O"] % 3]; st["O"] += 1
                cts = [0] + ([1] if qb >= 4 else [])
                state = {"A": True, "B": True}
                for ct in cts:
                    delta = 512 * qb - 2048 * ct
                    ps = psS[st["S"] % len(psS)]; st["S"] += 1
                    pt = PTl[st["P"] % len(PTl)]; st["P"] += 1

                    def first(ps=ps, pt=pt, ct=ct, delta=delta, g=g, h=h):
                        pairs = [(KCMP[g * 64:(g + 1) * 64, ct * 128:(ct + 1) * 128], q_ap(h, qb * 512, (qb + 1) * 512))]
                        rd = [KCMP, QN[h]]
                        if delta < 2560:
                            pairs.append((k.ident_bf[:], Gc[:, h, delta:delta + 512])); rd += [k.ident_bf, Gc]
                        k.mm(ps[:], pairs, r=rd, w=[ps])
                        k.act(pt[:], ps[:], AF.Exp, r=[ps], w=[pt], scale=0.125)

                    def second(pt=pt, ct=ct, g=g, h=h, poA=poA, poB=poB, state=state, lastct=(ct == cts[-1])):
                        fns = []
                        for qs in range(4):
                            po, cb = (poA, qs * 129) if qs < 3 else (poB, 0)
                            key = "A" if qs < 3 else "B"
                            stt_ = state[key]
                            state[key] = False
                            fns.append(lambda qs=qs, po=po, cb=cb, stt_=stt_: nc.tensor.matmul(
                                po[:, cb:cb + 129], lhsT=pt[:, qs * 128:(qs + 1) * 128], rhs=VE[:, g, ct, :],
                                start=stt_, stop=lastct, skip_group_check=True))
                        k.S.pe_group(fns, [pt, VE], [poA, poB])
                        if lastct:
                            views = [(poA, 0), (poA, 129), (poA, 258), (poB, 0)]
                            r_ = evac(views, h, 0, qb, ya, True)
                            for qs, (po, cb) in enumerate(views):
                                o = impt[g][:, qs, :]
                                if h % 3 == 0:
                                    k.ts('dve', o, po[:, cb + 65:cb + 129], r_[:, qs:qs + 1], None, ALU.mult, None, r=[po, r_], w=[impt[g]])
                                else:
                                    k.stt('dve', o, po[:, cb + 65:cb + 129], r_[:, qs:qs + 1], o, ALU.mult, ALU.add, r=[po, r_, impt[g]], w=[impt[g]])
                    pipe.push(first, second)

        def do_topk(qb):
            impt = impt2[qb % 2]
            for g in range(2):
                k.tt('dve', scr[:], impt[g][:], tkm[:, qb * 4:(qb + 1) * 4, :], ALU.mult, r=[impt[g], tkm], w=[scr])
                k.tt('dve', scr[:], scr[:], tka[:, qb * 4:(qb + 1) * 4, :], ALU.add, r=[scr, tka], w=[scr])
                for qs in range(4):
                    k.S.op('dve', lambda qs=qs: nc.vector.max(out=m8[:, qs, 0:8], in_=scr[:, qs, :]), [scr], [m8])
                    k.S.op('dve', lambda qs=qs: nc.vector.match_replace(out=wk[:, qs, :], in_to_replace=m8[:, qs, 0:8],
                                                                        in_values=scr[:, qs, :], imm_value=-1e9), [scr, m8], [wk])
                    k.S.op('dve', lambda qs=qs: nc.vector.max(out=m8[:, qs, 8:16], in_=wk[:, qs, :]), [wk], [m8])
                    k.ts('dve', wk[:, qs, :], scr[:, qs, :], m8[:, qs, 15:16], 1.0, ALU.is_ge, ALU.subtract, r=[scr, m8, wk], w=[wk])
                k.ts('dve', nmq[:, :, 0:64], wk[:], -NEG8, None, ALU.mult, None, r=[wk], w=[nmq])
                k.ts('pool', nmq[:, :, 64:128], wk[:], -NEG8, None, ALU.mult, None, r=[wk], w=[nmq])
                for qs in range(4):
                    k.transpose(psT[:, qs * 128:(qs + 1) * 128], nmq[:, qs, :], k.ident_bf[:], r=[nmq, k.ident_bf], w=[psT])
                oh = (1 - g) * 64
                for hh in range(3 * g, 3 * g + 3):
                    k.copy('act' if hh % 2 else 'dve', QN[hh][oh:oh + 64, qb * 512:(qb + 1) * 512], psT[oh:oh + 64, 0:512], r=[psT], w=[QN[hh]])

        def do_win(qb):
            ya = yacc[qb % 2]
            for h in range(6):
                g = h // 3
                po = psO[st["O"] % 3]; st["O"] += 1
                tiles = []
                for kt in range(max(0, 4 * qb - 4), 4 * qb + 4):
                    delta = 512 * qb - 128 * kt
                    c0 = max(-delta, 0)
                    c1 = min(512, 640 - delta) if delta > 0 else 512
                    tiles.append((kt, c0, c1, [(k.ident_bf, Gw, k.ident_bf[:], Gw[:, h, delta + 384 + c0:delta + 384 + c1])]))
                attend(h, qb, tiles, KW, VW, po, g, 2, ya)

        def do_slc(qb):
            ya = yacc[qb % 2]
            for h in range(6):
                g = h // 3
                po = psO[st["O"] % 3]; st["O"] += 1
                tiles = []
                for kt in range(0, 4 * qb + 4):
                    delta = 512 * qb - 128 * kt
                    c0 = max(-delta, 0)
                    ex = []
                    if delta <= 128:
                        c1b = 256 if delta == 128 else 512
                        ex.append((k.ident_bf, Gw, k.ident_bf[:], Gw[:, h, delta + 384 + c0:delta + 384 + c1b]))
                    tiles.append((kt, c0, 512, ex))
                attend(h, qb, tiles, KE[g], VS, po, g, 1, ya, merged=True)

        qbs = list(getattr(k, 'dbg_qbs', range(NTB)))
        do_cmp(qbs[0])
        pipe.flush()
        do_topk(qbs[0])
        for i, qb in enumerate(qbs):
            do_win(qb)
            if i + 1 < len(qbs):
                do_cmp(qbs[i + 1])
                pipe.flush()
                do_topk(qbs[i + 1])
            do_slc(qb)
            pipe.flush()
            ya = yacc[qb % 2]
            yb_ = ybf[qb % 2]
            k.copy('pool', yb_[:], ya[:], r=[ya], w=[yb_])
            k.dma('sp', k.Y[qb * 512:(qb + 1) * 512, 640:1024].rearrange("(q p) c -> p q c", p=128), yb_[:], r=[yb_])


def setup_ffn(k):
    k.w_out = k.inp("w_out", [DEPTH, D, D])
    k.ffn_up = k.inp("ffn_up", [DEPTH, D, 2 * D_FF])
    k.ffn_down = k.inp("ffn_down", [DEPTH, D_FF, D])
    k.conv_w = k.inp("conv_w_fm", [DEPTH, 128, 3, 44])
    k.conv_b = k.inp("conv_b_fm", [DEPTH, 128, 44])


def load_cast_gen(k, stg, dst, src_rows, ncols, nchunks, col_split=1):
    w = ncols // col_split
    n = 0
    for c in range(nchunks):
        for cs in range(col_split):
            s = stg[n % len(stg)]
            k.dma('sp' if n % 2 == 0 else 'act', s[:, 0:w], src_rows(c)[:, cs * w:(cs + 1) * w], w=[s])
            k.copy('pool' if n % 2 == 0 else 'dve', dst[:, c, cs * w:(cs + 1) * w], s[:, 0:w], r=[s], w=[dst])
            n += 1
            yield


def load_cast(k, sc, dst, src_rows, ncols, nchunks, name, col_split=1):
    w = ncols // col_split
    stg = [sc.sb("%s_stg%d" % (name, i), [128, w], F32) for i in range(4)]
    for _ in load_cast_gen(k, stg, dst, src_rows, ncols, nchunks, col_split):
        pass


def rms_scale(k, ss, st):
    k.act(st[:, 0:1], ss, AF.Ln, r=[st], w=[st], bias=RMS_EPS)
    k.act(st[:, 1:2], st[:, 0:1], AF.Exp, r=[st], w=[st], scale=-0.5)


def stage_out(k, l, xsrc, xdst, bg=None, bg_steps=2):
    nc = k.nc
    with Scope(k) as sc:
        wo = sc.sb("wo", [128, 8, D], BF16)
        with Scope(k) as s2:
            load_cast(k, s2, wo, lambda c: k.w_out[l, c * 128:(c + 1) * 128, :], D, 8, "wo")
        yt = [sc.sb("yt%d" % i, [128, D], BF16) for i in range(2)]
        yT = [sc.sb("yT%d" % i, [128, 8, 128], BF16) for i in range(2)]
        xt = [sc.sb("xo%d" % i, [128, D], F32) for i in range(2)]
        tt_ = [sc.sb("to%d" % i, [128, D], F32) for i in range(2)]
        junk = sc.sb("junko", [128, 512], BF16)
        st = [sc.sb("sto%d" % i, [128, 4], F32) for i in range(2)]
        psT = [sc.ps("psTo%d" % i, [128, D], BF16) for i in range(2)]
        psY = [sc.ps("psYo%d" % i, [128, 512]) for i in range(4)]
        def T(ti):
            tok = slice(ti * 128, (ti + 1) * 128)
            y_ = yt[ti % 2]; yT_ = yT[ti % 2]; x_ = xt[ti % 2]; pT = psT[ti % 2]
            k.dma('act', y_[:], k.Y[tok, :], w=[y_])
            k.dma('act', x_[:], xsrc[tok, :], w=[x_])
            for kc in range(8):
                k.transpose(pT[:, kc * 128:(kc + 1) * 128], y_[:, kc * 128:(kc + 1) * 128], k.ident_bf[:], r=[y_, k.ident_bf], w=[pT])
            k.copy('act' if ti % 2 else 'dve', yT_[:].rearrange("p a b -> p (a b)"), pT[:], r=[pT], w=[yT_])

        def M(ti):
            tok = slice(ti * 128, (ti + 1) * 128)
            yT_ = yT[ti % 2]; x_ = xt[ti % 2]; t_ = tt_[ti % 2]; st_ = st[ti % 2]
            p0 = psY[(ti % 2) * 2]; p1 = psY[(ti % 2) * 2 + 1]
            for half, ps in enumerate((p0, p1)):
                k.mm(ps[:], [(yT_[:, kc, :], wo[:, kc, half * 512:(half + 1) * 512]) for kc in range(8)], r=[yT_, wo], w=[ps])
                k.act(junk[:], ps[:], AF.Square, r=[ps], w=[junk, st_], scale=1.0 / 32.0, accum=st_[:, 2 + half:3 + half])
            k.tt('dve', st_[:, 2:3], st_[:, 2:3], st_[:, 3:4], ALU.add, r=[st_], w=[st_])
            rms_scale(k, st_[:, 2:3], st_)
            for half, ps in enumerate((p0, p1)):
                cs = slice(half * 512, (half + 1) * 512)
                k.stt('dve', t_[:, cs], ps[:], st_[:, 1:2], k.gm_row[:, cs], ALU.mult, ALU.mult, r=[ps, st_, k.gm_row], w=[t_])
            k.tt('pool', t_[:], t_[:], x_[:], ALU.add, r=[t_, x_], w=[t_])
            k.dma('sp', xdst[tok, :], t_[:], r=[t_])

        T(0)
        for ti in range(32):
            if ti + 1 < 32:
                T(ti + 1)
            M(ti)
            for _ in range(bg_steps):
                if bg is not None:
                    try:
                        next(bg)
                    except StopIteration:
                        bg = None
        if bg is not None:
            for _ in bg:
                pass


def stage_out_ffn(k, l, xin, xmid, xdst):
    NCH = 22
    with Scope(k) as sc:
        wu = sc.sb("wu", [128, 8, 2 * D_FF], BF16)
        wd = sc.sb("wd", [128, NCH, D], BF16)
        with Scope(k) as s2:
            stg = [s2.sb("wstg%d" % i, [128, 1408], F32) for i in range(4)]

            def bg():
                yield from load_cast_gen(k, stg, wu, lambda c: k.ffn_up[l, c * 128:(c + 1) * 128, :], 2 * D_FF, 8, col_split=4)
                yield from load_cast_gen(k, stg, wd, lambda c: k.ffn_down[l, c * 128:(c + 1) * 128, :], D, NCH)
            stage_out(k, l, xin, xmid, bg=bg(), bg_steps=2)
        stage_ffn(k, l, xmid, xdst, pre=(sc, wu, wd))


def stage_ffn(k, l, xsrc, xdst, pre=None):
    nc = k.nc
    NCH = 22
    with ExitStack() as es_:
        if pre is None:
            sc = es_.enter_context(Scope(k))
            wu = sc.sb("wu", [128, 8, 2 * D_FF], BF16)
            wd = sc.sb("wd", [128, NCH, D], BF16)
            with Scope(k) as s2:
                load_cast(k, s2, wu, lambda c: k.ffn_up[l, c * 128:(c + 1) * 128, :], 2 * D_FF, 8, "wu", col_split=2)
                load_cast(k, s2, wd, lambda c: k.ffn_down[l, c * 128:(c + 1) * 128, :], D, NCH, "wd")
        else:
            sc, wu, wd = pre
        cw = sc.sb("cw", [128, 3, 44], F32)
        cb = sc.sb("cb", [128, 44], F32)
        hal = [sc.sb("hal%d" % i, [128, 44, 2], F32) for i in range(2)]
        k.dma('sp', cw[:], k.conv_w[l], w=[cw])
        k.dma('sp', cb[:], k.conv_b[l], w=[cb])
        k.S.op('pool', lambda: nc.gpsimd.memset(hal[1][:], 0.0), [], [hal[1]])
        actT = sc.sb("actT", [128, NCH, 512], BF16)
        HTs = [sc.sb("H2T%d" % i, [128, 8, 512], BF16) for i in range(2)]
        xt = [sc.sb("xf%d" % i, [128, D], F32) for i in range(2)]
        xn = sc.sb("xnf", [128, D], BF16)
        junk = sc.sb("junkf", [128, 512], BF16)
        st = [sc.sb("stf%d" % i, [128, 4], F32) for i in range(2)]
        Tg = [sc.sb("Tg%d" % i, [128, 512], F32) for i in range(2)]
        Tv = [sc.sb("Tv%d" % i, [128, 512], F32) for i in range(2)]
        psT = sc.ps("psTf", [128, D], BF16)
        psU = [sc.ps("psU%d" % i, [128, 512]) for i in range(4)]
        nxc = {"n": 0}

        def norm_gen(tb):
            HT = HTs[tb % 2]
            t0_ = tb * 4
            k.dma('act', xt[t0_ % 2][:], xsrc[t0_ * 128:(t0_ + 1) * 128, :], w=[xt[t0_ % 2]])
            yield
            for sub in range(4):
                ti = tb * 4 + sub
                x_ = xt[ti % 2]; st_ = st[ti % 2]
                k.act(xn[:], x_[:], AF.Square, r=[x_], w=[xn, st_], scale=1.0 / 32.0, accum=st_[:, 2:3])
                rms_scale(k, st_[:, 2:3], st_)
                k.ts('dve', xn[:], x_[:], st_[:, 1:2], None, ALU.mult, None, r=[x_, st_], w=[xn])
                if sub < 3:
                    k.dma('act', xt[(ti + 1) % 2][:], xsrc[(ti + 1) * 128:(ti + 2) * 128, :], w=[xt[(ti + 1) % 2]])
                yield
                yield
                for kc in range(8):
                    k.transpose(psT[:, kc * 128:(kc + 1) * 128], xn[:, kc * 128:(kc + 1) * 128], k.ident_bf[:], r=[xn, k.ident_bf], w=[psT])
                    if kc == 3:
                        yield
                yield
                for kc in range(8):
                    o = HT[:, kc, sub * 128:(sub + 1) * 128]
                    i_ = psT[:, kc * 128:(kc + 1) * 128]
                    if kc % 2 == 0:
                        k.ts('dve', o, i_, k.modAB[:, 16 + kc:17 + kc], k.modAB[:, 24 + kc:25 + kc], ALU.mult, ALU.add, r=[psT, k.modAB], w=[HT])
                    else:
                        k.act(o, i_, AF.Identity, r=[psT, k.modAB], w=[HT], scale=k.modAB[:, 16 + kc:17 + kc], bias=k.modAB[:, 24 + kc:25 + kc])
                yield

        def step(g):
            if g is not None:
                try:
                    next(g)
                except StopIteration:
                    return None
            return g

        xe = [sc.sb("xe%d" % i, [128, D], F32) for i in range(2)]
        ste = [sc.sb("ste%d" % i, [128, 4], F32) for i in range(2)]
        for _ in norm_gen(0):
            pass
        for tb in range(NTB):
            hin = hal[(tb + 1) % 2]; hout = hal[tb % 2]
            HT = HTs[tb % 2]
            g = norm_gen(tb + 1) if tb + 1 < NTB else None
            for cp in range(NCH):
                tg = Tg[cp % 2]; tv = Tv[cp % 2]
                for which, (T_, c_) in enumerate(((tg, cp), (tv, NCH + cp))):
                    ps = psU[(cp * 2 + which) % 4]
                    k.mm(ps[:], [(wu[:, kc, c_ * 128:(c_ + 1) * 128], HT[:, kc, :]) for kc in range(8)], r=[wu, HT], w=[ps])
                    k.act(T_[:], ps[:], AF.Identity, r=[ps, cw, cb], w=[T_], scale=cw[:, 2, c_:c_ + 1], bias=cb[:, c_:c_ + 1])
                    k.stt('dve', T_[:, 1:512], ps[:, 0:511], cw[:, 1, c_:c_ + 1], T_[:, 1:512], ALU.mult, ALU.add, r=[ps, cw, T_], w=[T_])
                    k.stt('dve', T_[:, 2:512], ps[:, 0:510], cw[:, 0, c_:c_ + 1], T_[:, 2:512], ALU.mult, ALU.add, r=[ps, cw, T_], w=[T_])
                    k.copy('act', hout[:, c_, :], ps[:, 510:512], r=[ps], w=[hout])
                    k.stt('dve', T_[:, 0:1], hin[:, c_, 1:2], cw[:, 1, c_:c_ + 1], T_[:, 0:1], ALU.mult, ALU.add, r=[hin, cw, T_], w=[T_])
                    k.stt('dve', T_[:, 0:2], hin[:, c_, 0:2], cw[:, 0, c_:c_ + 1], T_[:, 0:2], ALU.mult, ALU.add, r=[hin, cw, T_], w=[T_])
                k.act(tg[:], tg[:], AF.Silu, r=[tg], w=[tg])
                k.tt('pool', actT[:, cp, :], tg[:], tv[:], ALU.mult, r=[tg, tv], w=[actT])
                if cp >= 1:
                    g = step(g)
            for sub in range(4):
                ti = tb * 4 + sub
                tok = slice(ti * 128, (ti + 1) * 128)
                x_ = xe[sub % 2]; st_ = ste[sub % 2]
                k.dma('act', x_[:], xsrc[tok, :], w=[x_])
                psF = [psU[(2 * sub) % 4], psU[(2 * sub + 1) % 4]]
                for half in range(2):
                    ps = psF[half]
                    k.mm(ps[:], [(actT[:, cp, sub * 128:(sub + 1) * 128], wd[:, cp, half * 512:(half + 1) * 512]) for cp in range(NCH)],
                         r=[actT, wd], w=[ps])
                    k.act(junk[:, 0:512], ps[:], AF.Square, r=[ps], w=[junk, st_], scale=1.0 / 32.0, accum=st_[:, 2 + half:3 + half])
                k.tt('dve', st_[:, 2:3], st_[:, 2:3], st_[:, 3:4], ALU.add, r=[st_], w=[st_])
                rms_scale(k, st_[:, 2:3], st_)
                t_ = Tg[sub % 2] if False else None
                for half in range(2):
                    cs = slice(half * 512, (half + 1) * 512)
                    T_ = (Tg if half == 0 else Tv)[sub % 2]
                    k.stt('dve', T_[:], psF[half][:], st_[:, 1:2], k.gf_row[:, cs], ALU.mult, ALU.mult, r=[psF[half], st_, k.gf_row], w=[T_])
                    k.tt('pool', x_[:, cs], x_[:, cs], T_[:], ALU.add, r=[x_, T_], w=[x_])
                k.dma('sp', xdst[tok, :], x_[:], r=[x_])
                g = step(g)
            while g is not None:
                g = step(g)


def rwkv_host(inp):
    f = lambda a: np.ascontiguousarray(np.asarray(a, dtype=np.float32))
    mu = np.asarray(inp["rwkv_mu"])
    hd = lambda v: np.asarray(v).reshape(DEPTH, 4, 64).transpose(0, 2, 1)
    pp = np.stack([hd(mu[:, 0:256]), hd(mu[:, 256:512]), hd(mu[:, 512:768]), hd(inp["rwkv_w0"]), hd(inp["rwkv_a0"]),
                   hd(inp["rwkv_k_k"]), hd(inp["rwkv_k_a"]), hd(np.asarray(inp["rwkv_r_k"]).reshape(DEPTH, 256))], axis=2)
    lr = np.zeros((DEPTH, 64, 3), np.float32)
    lr[:, 0:32, 0] = mu[:, 768:800]; lr[:, 0:32, 1] = mu[:, 800:832]; lr[:, :, 2] = mu[:, 832:896]
    i = np.arange(64)
    mk = np.stack([(i[:, None] < i[None, :]), (i[:, None] > i[None, :]), (i[:, None] <= i[None, :]), np.eye(64, dtype=bool)]).astype(np.float32)
    cm = np.ones((64, 512), np.float32); cm[:, ::64] = 0.0
    return {"rwkv_pp": f(pp), "rwkv_lr": f(lr), "rwkv_w_up": f(inp["rwkv_w_up"]), "rwkv_a_up": f(inp["rwkv_a_up"]),
            "rwkv_g_up": f(inp["rwkv_g_up"]), "rwkv_ln": f(np.stack([np.asarray(inp["rwkv_ln_w"]), np.asarray(inp["rwkv_ln_b"])], axis=1)),
            "rwkv_masks": f(mk.transpose(1, 0, 2)), "rwkv_cmask": cm}


def setup_rwkv(k):
    k.rw_pp = k.inp("rwkv_pp", [DEPTH, 64, 8, 4])
    k.rw_lr = k.inp("rwkv_lr", [DEPTH, 64, 3])
    k.rw_wup = k.inp("rwkv_w_up", [DEPTH, 32, 256])
    k.rw_aup = k.inp("rwkv_a_up", [DEPTH, 32, 256])
    k.rw_gup = k.inp("rwkv_g_up", [DEPTH, 64, 256])
    k.rw_ln = k.inp("rwkv_ln", [DEPTH, 2, 256])
    k.rw_masks = k.inp("rwkv_masks", [64, 4, 64])
    k.rw_cmask = k.inp("rwkv_cmask", [64, 512])


def stage_rwkv(k, l):
    nc = k.nc
    BL = 256
    NB = S_LEN // BL
    CPB = BL // 64
    H4 = [64, 4, BL]
    bc = lambda ap, shape: ap.to_broadcast(shape)
    with Scope(k) as sc:
        pp = sc.sb("pp", [64, 8, 4], F32)
        lr = sc.sb("lr", [64, 3], F32)
        wup = sc.sb("wup", [32, 256], F32); aup = sc.sb("aup", [32, 256], F32); gup = sc.sb("gup", [64, 256], F32)
        lnr = sc.sb("lnr", [64, 2, 256], F32)
        mk = sc.sb("mk", [64, 4, 64], F32)
        cmask = sc.sb("cmask", [64, BL], F32)
        ones = sc.sb("ones64", [64, 64], F32)
        prm = sc.sb("prm", [64, 4, 4], F32)
        k.dma('sp', pp[:], k.rw_pp[l], w=[pp]); k.dma('sp', lr[:], k.rw_lr[l], w=[lr])
        k.dma('sp', wup[:], k.rw_wup[l], w=[wup]); k.dma('sp', aup[:], k.rw_aup[l], w=[aup]); k.dma('sp', gup[:], k.rw_gup[l], w=[gup])
        for i in range(2):
            k.dma('sp', lnr[:, i, :], k.rw_ln[l, i:i + 1, :].broadcast_to([64, 256]), w=[lnr])
        k.dma('sp', mk[:], k.rw_masks, w=[mk]); k.dma('sp', cmask[:], k.rw_cmask[:, 0:BL], w=[cmask])
        k.S.op('pool', lambda: nc.gpsimd.memset(ones[:], 1.0), [], [ones])
        k.ts('dve', prm[:, 0, :], pp[:, 3, :], -1.0, None, ALU.mult, None, r=[pp], w=[prm])
        k.ts('dve', prm[:, 1, :], pp[:, 6, :], -1.0, 1.0, ALU.mult, ALU.add, r=[pp], w=[prm])
        P3 = sc.sb("P3", [64, 3, 4, BL], F32)
        halo = sc.sb("halo", [64, 3, 4], F32)
        LR = sc.sb("LR", [64, 3, BL], F32)
        halo2 = sc.sb("halo2", [64, 3], F32)
        ELW = sc.sb("ELW", H4, F32); SC_ = sc.sb("SCAN", H4, F32); AA = sc.sb("AA", H4, F32); KKN = sc.sb("KKN", H4, F32)
        T1 = sc.sb("T1", H4, F32); T2 = sc.sb("T2", H4, F32); CM4 = sc.sb("CM4", H4, F32)
        OUT = [{nm: sc.sb("%s%d" % (nm, i), H4, F32 if nm == "GAM" else BF16) for nm in ("AT", "BT", "KT", "RT", "RK", "GAM", "V")} for i in range(2)]
        SGs = [sc.sb("SG%d" % i, [64, BL], BF16) for i in range(2)]
        gupb = sc.sb("gupb", [64, 256], BF16)
        ppb = sc.sb("ppb", [64, 4], BF16)
        identb64 = k.ident_bf
        XY = [[sc.sb("XY%d_%d" % (i, j), [64, 2, 4, 64], BF16) for j in range(2)] for i in range(2)]
        PP = [[sc.sb("PPi%d_%d" % (i, j), [64, 4, 64], BF16) for j in range(2)] for i in range(2)]
        AKRK = [sc.sb("AKRK%d" % i, [64, 2, 4, 64], BF16) for i in range(2)]
        RBT = [sc.sb("RBT%d" % i, [64, 4, 64], BF16) for i in range(2)]
        TOK = [sc.sb("TOK%d" % i, [64, 3, 4, 64], BF16) for i in range(2)]
        Wsb = sc.sb("Wsb", [64, 4, 64], BF16); Usb = sc.sb("Usb", [64, 4, 64], BF16)
        Hs = [sc.sb("Hs%d" % i, [64, 4, 64], F32) for i in range(2)]
        Hb = [sc.sb("Hb%d" % i, [64, 4, 64], BF16) for i in range(2)]
        yc = sc.sb("yc", [64, 4, 64], F32); ysq = sc.sb("ysq", [64, 4, 64], F32)
        sm = sc.sb("sm", [64, 6, 4], F32)
        yab = [sc.sb("yab%d" % i, [64, CPB, 256], BF16) for i in range(2)]
        psA1 = sc.ps("psA1", [64, 512]); psA2 = sc.ps("psA2", [64, 512]); psA3 = sc.ps("psA3", [64, 512]); psA4 = sc.ps("psA4", [64, 512])
        psH = sc.ps("psHr", [64, 512]); psY = sc.ps("psYr", [64, 512]); psC = sc.ps("psCr", [64, 512]); psQ = sc.ps("psQr", [64, 512])
        k.S.op('pool', lambda: nc.gpsimd.memset(Hs[1][:], 0.0), [], [Hs[1]])
        k.S.op('pool', lambda: nc.gpsimd.memset(Hb[1][:], 0.0), [], [Hb[1]])
        k.copy('dve', gupb[:], gup[:], r=[gup], w=[gupb])
        k.copy('dve', ppb[:], pp[:, 7, :], r=[pp], w=[ppb])
        k.S.op('pool', lambda: nc.gpsimd.memset(halo[:], 0.0), [], [halo])
        k.S.op('pool', lambda: nc.gpsimd.memset(halo2[:], 0.0), [], [halo2])
        k.copy('dve', CM4[:], bc(cmask[:].unsqueeze(1), H4), r=[cmask], w=[CM4])
        E_ = BL - 1

        def prep(tb):
            O = OUT[tb % 2]; SG = SGs[tb % 2]
            AT, BT, KT, RT, RK, GAM, V_ = O["AT"], O["BT"], O["KT"], O["RT"], O["RK"], O["GAM"], O["V"]
            t0 = tb * BL
            for q in range(3):
                k.dma('act', P3[:, q, :, :], k.PT[q * 256:(q + 1) * 256, t0:t0 + BL].rearrange("(h d) t -> d h t", d=64), w=[P3])
            k.dma('act', LR[0:32, 0, :], k.PT[768:800, t0:t0 + BL], w=[LR])
            k.dma('act', LR[0:32, 1, :], k.PT[800:832, t0:t0 + BL], w=[LR])
            k.dma('act', LR[:, 2, :], k.PT[832:896, t0:t0 + BL], w=[LR])
            yield
            for q in range(3):
                p_ = P3[:, q, :, :]
                k.tt('dve', T1[:, :, 1:BL], p_[:, :, 0:E_], p_[:, :, 1:BL], ALU.subtract, r=[P3], w=[T1])
                k.tt('dve', T1[:, :, 0:1], halo[:, q, :].unsqueeze(2), p_[:, :, 0:1], ALU.subtract, r=[P3, halo], w=[T1])
                k.copy('pool', halo[:, q, :].unsqueeze(2), p_[:, :, E_:BL], r=[P3, T1], w=[halo])
                k.tt('pool', T1[:], T1[:], bc(pp[:, q, :].unsqueeze(2), H4), ALU.mult, r=[T1, pp], w=[T1])
                if q < 2:
                    k.tt('pool', p_, p_, T1[:], ALU.add, r=[P3, T1, halo], w=[P3])
                else:
                    k.tt('pool', V_[:], p_, T1[:], ALU.add, r=[P3, T1, halo], w=[V_])
                yield
            for q, rows in ((0, 32), (1, 32), (2, 64)):
                x_ = LR[0:rows, q, :]
                t_ = T2[0:rows, 0, :]
                k.tt('dve', t_[:, 1:BL], x_[:, 0:E_], x_[:, 1:BL], ALU.subtract, r=[LR], w=[T2])
                k.tt('dve', t_[:, 0:1], halo2[0:rows, q:q + 1], x_[:, 0:1], ALU.subtract, r=[LR, halo2], w=[T2])
                k.copy('dve', halo2[0:rows, q:q + 1], x_[:, E_:BL], r=[LR, T2], w=[halo2])
                k.stt('dve', x_, t_, lr[0:rows, q:q + 1], x_, ALU.mult, ALU.add, r=[T2, lr, LR, halo2], w=[LR])
            yield
            R_ = P3[:, 0, :, :]; Kp = P3[:, 1, :, :]
            k.act(LR[0:32, 0, :], LR[0:32, 0, :], AF.Tanh, r=[LR], w=[LR])
            k.act(SG[:], LR[:, 2, :], AF.Sigmoid, r=[LR], w=[SG])
            for h in range(4):
                k.mm(psQ[:, 0:BL], [(wup[:, h * 64:(h + 1) * 64], LR[0:32, 0, :])], r=[wup, LR], w=[psQ])
                k.act(T1[:, h, :], psQ[:, 0:BL], AF.Exp, r=[psQ, prm], w=[T1], scale=-1.0, bias=prm[:, 0, h:h + 1])
                k.mm(psQ[:, BL:2 * BL], [(aup[:, h * 64:(h + 1) * 64], LR[0:32, 1, :])], r=[aup, LR], w=[psQ], start=False)
                k.act(AA[:, h, :], psQ[:, BL:2 * BL], AF.Sigmoid, r=[psQ, pp], w=[AA], bias=pp[:, 4, h:h + 1])
                yield
            k.act(T1[:], T1[:], AF.Ln, r=[T1], w=[T1], bias=1.0)
            k.act(ELW[:], T1[:], AF.Exp, r=[T1], w=[ELW], scale=-1.0, bias=-0.5)
            k.S.op('dve', lambda: nc.vector.tensor_tensor_scan(
                out=SC_[:].rearrange("p h t -> p (h t)"), data0=CM4[:].rearrange("p h t -> p (h t)"),
                data1=ELW[:].rearrange("p h t -> p (h t)"), initial=0.0, op0=ALU.mult, op1=ALU.add), [CM4, ELW], [SC_])
            yield
            k.tt('pool', KKN[:], Kp, bc(pp[:, 5, :].unsqueeze(2), H4), ALU.mult, r=[P3, pp], w=[KKN])
            k.tt('pool', T1[:], KKN[:], KKN[:], ALU.mult, r=[KKN], w=[T1])
            for h in range(4):
                k.mm(psQ[:, 0:BL], [(ones[:], T1[:, h, :])], r=[ones, T1], w=[psQ])
                k.act(T2[:, h, :], psQ[:, 0:BL], AF.Ln, r=[psQ], w=[T2], bias=1e-24)
                yield
            k.act(T2[:], T2[:], AF.Exp, r=[T2], w=[T2], scale=-0.5)
            k.tt('dve', KKN[:], KKN[:], T2[:], ALU.mult, r=[KKN, T2], w=[KKN])
            yield
            k.tt('pool', T1[:], SC_[:], ELW[:], ALU.subtract, r=[SC_, ELW], w=[T1])
            k.act(T1[:], T1[:], AF.Exp, r=[T1], w=[T1], scale=-1.0)
            k.stt('dve', AT[:], KKN[:], -1.0, T1[:], ALU.mult, ALU.mult, r=[KKN, T1], w=[AT])
            yield
            k.act(T2[:], SC_[:], AF.Exp, r=[SC_], w=[T2])
            k.tt('pool', T1[:], KKN[:], AA[:], ALU.mult, r=[KKN, AA], w=[T1])
            k.tt('dve', BT[:], T1[:], T2[:], ALU.mult, r=[T1, T2], w=[BT])
            yield
            k.tt('pool', T1[:], AA[:], bc(pp[:, 6, :].unsqueeze(2), H4), ALU.mult, r=[AA, pp], w=[T1])
            k.tt('pool', T1[:], T1[:], bc(prm[:, 1, :].unsqueeze(2), H4), ALU.add, r=[T1, prm], w=[T1])
            k.tt('dve', Kp, Kp, T1[:], ALU.mult, r=[P3, T1, KKN], w=[P3])
            yield
            k.tt('dve', KT[:], Kp, T2[:], ALU.mult, r=[P3, T2], w=[KT])
            k.tt('pool', RK[:], R_, Kp, ALU.mult, r=[P3], w=[RK])
            k.act(GAM[:], SC_[:], AF.Exp, r=[SC_], w=[GAM], scale=-1.0)
            k.tt('dve', RT[:], R_, GAM[:], ALU.mult, r=[P3, GAM], w=[RT])
            yield

        def phaseA(nch):
            tb, n = divmod(nch, CPB)
            O = OUT[tb % 2]
            AT, BT, KT, RT, V_ = O["AT"], O["BT"], O["KT"], O["RT"], O["V"]
            c_ = slice(n * 64, (n + 1) * 64)
            par = nch % 2
            xy = XY[par][0]; akrk = AKRK[par]; rbt = RBT[par]; tok = TOK[par]
            fns = []
            for h in range(4):
                fns.append(lambda h=h: nc.tensor.matmul(psA1[:, h * 64:(h + 1) * 64], lhsT=BT[:, h, c_], rhs=AT[:, h, c_], start=True, stop=True, skip_group_check=True))
                fns.append(lambda h=h: nc.tensor.matmul(psA1[:, 256 + h * 64:256 + (h + 1) * 64], lhsT=AT[:, h, c_], rhs=BT[:, h, c_], start=True, stop=True, skip_group_check=True))
            k.S.pe_group(fns, [AT, BT], [psA1])
            fns = []
            for h in range(4):
                fns.append(lambda h=h: nc.tensor.matmul(psA2[:, h * 64:(h + 1) * 64], lhsT=KT[:, h, c_], rhs=AT[:, h, c_], start=True, stop=True, skip_group_check=True))
                fns.append(lambda h=h: nc.tensor.matmul(psA2[:, 256 + h * 64:256 + (h + 1) * 64], lhsT=KT[:, h, c_], rhs=RT[:, h, c_], start=True, stop=True, skip_group_check=True))
            k.S.pe_group(fns, [AT, KT, RT], [psA2])
            k.S.pe_group([lambda h=h: nc.tensor.matmul(psA3[:, h * 64:(h + 1) * 64], lhsT=BT[:, h, c_], rhs=RT[:, h, c_], start=True, stop=True, skip_group_check=True)
                          for h in range(4)], [BT, RT], [psA3])
            yield
            v4 = lambda ps, a: ps[:, a * 256:(a + 1) * 256].rearrange("p (h f) -> p h f", h=4)
            mb = lambda i: bc(mk[:, i, :].unsqueeze(1), [64, 4, 64])
            k.tt('dve', xy[:, 0, :, :], v4(psA1, 0), mb(0), ALU.mult, r=[psA1, mk], w=[xy])
            k.tt('dve', xy[:, 1, :, :], v4(psA1, 1), mb(1), ALU.mult, r=[psA1, mk], w=[xy])
            k.tt('dve', akrk[:, 0, :, :], v4(psA2, 0), mb(0), ALU.mult, r=[psA2, mk], w=[akrk])
            k.tt('dve', akrk[:, 1, :, :], v4(psA2, 1), mb(2), ALU.mult, r=[psA2, mk], w=[akrk])
            k.tt('dve', rbt[:], v4(psA3, 0), mb(2), ALU.mult, r=[psA3, mk], w=[rbt])
            yield
            fns = []
            psA1b = psA1[:, :].bitcast(BF16)
            for qi, src_ in enumerate((V_, BT, KT)):
                for h in range(4):
                    dst = psA1b[:, qi * 256 + h * 64:qi * 256 + (h + 1) * 64]
                    fns.append(lambda dst=dst, s_=src_[:, h, c_]: nc.tensor.transpose(out=dst, in_=s_, identity=k.ident_bf[0:64, 0:64]))
            k.S.pe_group(fns, [V_, BT, KT, k.ident_bf], [psA1])
            yield
            k.copy('act', tok[:].rearrange("p a h f -> p (a h f)"), psA1b[:, 0:768], r=[psA1], w=[tok])
            P_ = PP[par][0]
            k.tt('dve', P_[:], xy[:, 0, :, :], mb(3), ALU.add, r=[xy, mk], w=[P_])
            yield
            Pm = None
            for lev in range(1, 7):
                xyn = XY[par][lev % 2]
                fns = []
                rd = [xy]
                wr = []
                if lev <= 5:
                    for h in range(4):
                        fns.append(lambda h=h, xy=xy: nc.tensor.matmul(psA4[:, 256 + h * 64:256 + (h + 1) * 64], lhsT=xy[:, 0, h, :], rhs=xy[:, 1, h, :], start=True, stop=True, skip_group_check=True))
                        if lev <= 4:
                            fns.append(lambda h=h, xy=xy: nc.tensor.matmul(psA4[:, h * 64:(h + 1) * 64], lhsT=xy[:, 1, h, :], rhs=xy[:, 0, h, :], start=True, stop=True, skip_group_check=True))
                    wr.append(psA4)
                if lev >= 2:
                    for h in range(4):
                        fns.append(lambda h=h, xy=xy, Pm=Pm: nc.tensor.matmul(psA3[:, 256 + h * 64:256 + (h + 1) * 64], lhsT=xy[:, 1, h, :], rhs=Pm[:, h, :], start=True, stop=True, skip_group_check=True))
                    rd.append(Pm); wr.append(psA3)
                k.S.pe_group(fns, rd, wr)
                yield
                if lev <= 4:
                    k.copy('act', xyn[:].rearrange("p a h f -> p (a h f)"), psA4[:, :], r=[psA4], w=[xyn])
                elif lev == 5:
                    k.copy('act', xyn[:, 1, :, :].rearrange("p h f -> p (h f)"), psA4[:, 256:512], r=[psA4], w=[xyn])
                if lev >= 2:
                    Pn = PP[par][(lev - 1) % 2]
                    k.tt('dve', Pn[:], Pm[:], v4(psA3, 1), ALU.add, r=[Pm, psA3], w=[Pn])
                    Pm = Pn
                else:
                    Pm = P_
                yield
                xy = xyn

        def phaseB(nch):
            tb, n = divmod(nch, CPB)
            O = OUT[tb % 2]; SG = SGs[tb % 2]
            AT, RT, RK, GAM = O["AT"], O["RT"], O["RK"], O["GAM"]
            c_ = slice(n * 64, (n + 1) * 64)
            par = nch % 2
            akrk = AKRK[par]; rbt = RBT[par]; tok = TOK[par]; TT = PP[par][1]
            Hold = Hs[(nch + 1) % 2]; Hnew = Hs[nch % 2]
            Hbo = Hb[(nch + 1) % 2]; Hbn = Hb[nch % 2]
            yab_ = yab[tb % 2]
            fns = []
            for h in range(4):
                fns.append(lambda h=h: nc.tensor.matmul(psH[:, h * 64:(h + 1) * 64], lhsT=AT[:, h, c_], rhs=Hbo[:, h, :], start=(h == 0), stop=False, skip_group_check=True))
                fns.append(lambda h=h: nc.tensor.matmul(psH[:, h * 64:(h + 1) * 64], lhsT=akrk[:, 0, h, :], rhs=tok[:, 0, h, :], start=False, stop=True, skip_group_check=True))
            k.S.pe_group(fns, [AT, Hbo, akrk, tok], [psH])
            yield
            k.copy('act', Wsb[:].rearrange("p h f -> p (h f)"), psH[:, 0:256], r=[psH], w=[Wsb])
            yield
            k.S.pe_group([lambda h=h: nc.tensor.matmul(psH[:, 256 + h * 64:256 + (h + 1) * 64], lhsT=TT[:, h, :], rhs=Wsb[:, h, :], start=False, stop=True, skip_group_check=True)
                          for h in range(4)], [TT, Wsb], [psH])
            yield
            k.copy('act', Usb[:].rearrange("p h f -> p (h f)"), psH[:, 256:512], r=[psH], w=[Usb])
            yield
            fns = []
            for h in range(4):
                fns.append(lambda h=h: nc.tensor.matmul(psC[:, h * 64:(h + 1) * 64], lhsT=tok[:, 1, h, :], rhs=Usb[:, h, :], start=(h == 0), stop=False, skip_group_check=True))
                fns.append(lambda h=h: nc.tensor.matmul(psC[:, h * 64:(h + 1) * 64], lhsT=tok[:, 2, h, :], rhs=tok[:, 0, h, :], start=False, stop=True, skip_group_check=True))
                fns.append(lambda h=h: nc.tensor.matmul(psC[:, 256 + h:256 + h + 1], lhsT=RK[:, h, c_], rhs=ppb[:, h:h + 1], start=False, stop=True, skip_group_check=True))
            k.S.pe_group(fns, [tok, Usb, RK, ppb], [psC])
            fns = []
            for h in range(4):
                fns.append(lambda h=h: nc.tensor.matmul(psY[:, h * 64:(h + 1) * 64], lhsT=RT[:, h, c_], rhs=Hbo[:, h, :], start=(h == 0), stop=False, skip_group_check=True))
                fns.append(lambda h=h: nc.tensor.matmul(psY[:, h * 64:(h + 1) * 64], lhsT=rbt[:, h, :], rhs=Usb[:, h, :], start=False, stop=False, skip_group_check=True))
                fns.append(lambda h=h: nc.tensor.matmul(psY[:, h * 64:(h + 1) * 64], lhsT=akrk[:, 1, h, :], rhs=tok[:, 0, h, :], start=False, stop=True, skip_group_check=True))
            fns.append(lambda: nc.tensor.matmul(psY[:, 256:512], lhsT=SG[:, c_], rhs=gupb[:, :], start=False, stop=True, skip_group_check=True))
            k.S.pe_group(fns, [RT, Hbo, rbt, Usb, akrk, tok, SG, gupb], [psY])
            yield
            k.tt('dve', Hnew[:], psC[:, 0:256].rearrange("p (h f) -> p h f", h=4), Hold[:], ALU.add, r=[psC, Hold], w=[Hnew])
            k.copy('dve', sm[:, 5, :], psC[:, 256:260], r=[psC], w=[sm])
            k.tt('dve', Hbn[:], Hnew[:], bc(GAM[:, :, n * 64 + 63:n * 64 + 64], [64, 4, 64]), ALU.mult, r=[Hnew, GAM], w=[Hbn])
            k.tt('pool', Hnew[:], Hnew[:], bc(GAM[:, :, n * 64 + 63:n * 64 + 64], [64, 4, 64]), ALU.mult, r=[Hnew, GAM], w=[Hnew])
            yield
            y3 = psY[:, 0:256].rearrange("p (h f) -> p h f", h=4)
            k.S.op('dve', lambda: nc.vector.reduce_sum(out=sm[:, 0, :], in_=y3, axis=AX.X), [psY], [sm])
            k.ts('dve', sm[:, 1, :], sm[:, 0, :], 1.0 / 64.0, None, ALU.mult, None, r=[sm], w=[sm])
            k.tt('dve', yc[:], y3, bc(sm[:, 1, :].unsqueeze(2), [64, 4, 64]), ALU.subtract, r=[psY, sm], w=[yc])
            yield
            k.tt('pool', ysq[:], yc[:], yc[:], ALU.mult, r=[yc], w=[ysq])
            k.S.op('dve', lambda: nc.vector.reduce_sum(out=sm[:, 2, :], in_=ysq[:], axis=AX.X), [ysq], [sm])
            k.act(sm[:, 3, :], sm[:, 2, :], AF.Ln, r=[sm], w=[sm], scale=1.0 / 64.0, bias=GN_EPS)
            k.act(sm[:, 4, :], sm[:, 3, :], AF.Exp, r=[sm], w=[sm], scale=-0.5)
            yield
            k.tt('dve', yc[:], yc[:], bc(sm[:, 4, :].unsqueeze(2), [64, 4, 64]), ALU.mult, r=[yc, sm], w=[yc])
            k.tt('pool', yc[:], yc[:], lnr[:, 0, :].rearrange("p (h f) -> p h f", h=4), ALU.mult, r=[yc, lnr], w=[yc])
            k.tt('pool', yc[:], yc[:], lnr[:, 1, :].rearrange("p (h f) -> p h f", h=4), ALU.add, r=[yc, lnr], w=[yc])
            k.tt('dve', ysq[:], tok[:, 0, :, :], bc(sm[:, 5, :].unsqueeze(2), [64, 4, 64]), ALU.mult, r=[tok, sm], w=[ysq])
            yield
            k.tt('pool', yc[:], yc[:], ysq[:], ALU.add, r=[yc, ysq], w=[yc])
            k.tt('dve', yab_[:, n, :], yc[:].rearrange("p h f -> p (h f)"), psY[:, 256:512], ALU.mult, r=[yc, psY], w=[yab_])
            if n == CPB - 1:
                k.dma('sp', k.Y[tb * BL:(tb + 1) * BL, 0:256].rearrange("(n p) c -> p n c", p=64), yab_[:], r=[yab_])
            yield

        def run_all(*gens):
            gens = [g for g in gens if g is not None]
            while gens:
                for g in list(gens):
                    try:
                        next(g)
                    except StopIteration:
                        gens.remove(g)

        NCH = S_LEN // 64
        run_all(prep(0))
        run_all(phaseA(0), prep(1) if NB > 1 else None)
        gp = None
        for nch in range(NCH):
            tb, n = divmod(nch, CPB)
            if n == 0 and tb >= 1 and tb + 1 < NB:
                gp = prep(tb + 1)
            gens = [phaseB(nch)]
            if nch + 1 < NCH:
                gens.append(phaseA(nch + 1))
            rnd = 0
            while gens:
                for g in list(gens):
                    try:
                        next(g)
                    except StopIteration:
                        gens.remove(g)
                rnd += 1
                if gp is not None and rnd % 2 == 0:
                    try:
                        next(gp)
                    except StopIteration:
                        gp = None
            if n == CPB - 2 and gp is not None:
                for _ in gp:
                    pass
                gp = None


def build(nlayers=DEPTH, taps=()):
    k = K(nlayers, taps=taps)
    setup_globals(k)
    setup_fox(k)
    setup_rwkv(k)
    setup_ffn(k)
    setup_nsa(k)
    for l in range(nlayers):
        xin = k.x_in if l == 0 else k.XR
        xout = k.OUT if l == nlayers - 1 else k.XR
        stage_mod(k, l)
        stage_proj(k, l, xin)
        stage_rwkv(k, l)
        stage_fox(k, l)
        stage_nsa(k, l)
        stage_out_ffn(k, l, xin, k.XR1, xout)
    k.S.barrier()
    return k


_CACHE = {}


def kernel(**inputs):
    if "k" not in _CACHE:
        _CACHE["k"] = build(DEPTH)
    k = _CACHE["k"]
    sh = prep_shared(inputs)
    in_maps = []
    for b in range(8):
        d = dict(sh)
        d.update(prep_core(inputs, b))
        in_maps.append({n: v for n, v in d.items() if n in k.ins})
    res = run_bass_kernel_spmd(k.nc, in_maps, core_ids=list(range(8)))
    out = np.stack([np.asarray(res.results[b]["out"], dtype=np.float32) for b in range(8)], axis=0)
    return out
```

```python
import numpy as np
import ml_dtypes
from contextlib import ExitStack
import concourse.bass as bass
import concourse.mybir as mybir
from concourse.bass_utils import run_bass_kernel_spmd

F32 = mybir.dt.float32
BF16 = mybir.dt.bfloat16
AF = mybir.ActivationFunctionType
ALU = mybir.AluOpType
AX = mybir.AxisListType
NPBF = ml_dtypes.bfloat16

S_LEN = 4096
D = 1024
DEPTH = 4
NTB = 8
N_IN = 3224
D_FF = 2816
NEG = -30000.0
RMS_EPS = 1e-6
GN_EPS = 64e-5


class Sched:
    ENG = ('pe', 'act', 'dve', 'pool')
    LIMIT = 30000

    def __init__(self, nc):
        self.nc = nc
        self.e = {'pe': nc.tensor, 'act': nc.scalar, 'dve': nc.vector, 'pool': nc.gpsimd, 'sp': nc.sync}
        self.epoch = {k: 0 for k in self.ENG}
        self.sem = {k: nc.alloc_semaphore("c_%s_0" % k) for k in self.ENG}
        self.cnt = {k: 0 for k in self.ENG}
        self.seen = {k: {} for k in self.e}
        self.lastw = {}
        self.reads = {}
        self.dma_sems = {'hw': [[nc.alloc_semaphore("d%d" % i), 0, "dma%d" % i] for i in range(24)],
                         'sw': [[nc.alloc_semaphore("ds%d" % i), 0, "dmas%d" % i] for i in range(8)]}
        self.ndma = {'hw': 0, 'sw': 0}
        self.n_inst = 0
        self.n_wait = 0
        self.per = {}

    def _wait(self, eng, tok):
        key, sem, val = tok
        if self.seen[eng].get(key, 0) >= val:
            return
        self.e[eng].wait_ge(sem, val)
        self.n_wait += 1
        self.per[eng] = self.per.get(eng, 0) + 1
        self.seen[eng][key] = val

    def _deps(self, eng, reads, writes):
        for b in reads:
            t = self.lastw.get(b)
            if t is not None:
                self._wait(eng, t)
        for b in writes:
            t = self.lastw.get(b)
            if t is not None:
                self._wait(eng, t)
            for t in self.reads.get(b, ()):
                self._wait(eng, t)

    def _commit(self, tok, reads, writes):
        for b in reads:
            self.reads.setdefault(b, []).append(tok)
        for b in writes:
            self.lastw[b] = tok
            self.reads[b] = []

    def _bump(self, eng, ins):
        if self.cnt[eng] >= self.LIMIT:
            self.epoch[eng] += 1
            self.sem[eng] = self.nc.alloc_semaphore("c_%s_%d" % (eng, self.epoch[eng]))
            self.cnt[eng] = 0
        self.cnt[eng] += 1
        ins.then_inc(self.sem[eng], 1)
        return ("%s_%d" % (eng, self.epoch[eng]), self.sem[eng], self.cnt[eng])

    @staticmethod
    def _norm(reads, writes):
        rd = [getattr(b, 'n', b) for b in reads]
        wr = [getattr(b, 'n', b) for b in writes]
        ps = [b for b in rd if b.startswith("ps")]
        rd = [b for b in rd if not b.startswith("ps")]
        return rd, wr + [b for b in ps if b not in wr]

    def op(self, eng, inst_fn, reads=(), writes=()):
        reads, writes = self._norm(reads, writes)
        self._deps(eng, reads, writes)
        ins = inst_fn()
        self.per[eng] = self.per.get(eng, 0) + 1
        tok = self._bump(eng, ins)
        self._commit(tok, reads, writes)
        self.n_inst += 1
        return tok

    def pe_group(self, fns, reads=(), writes=()):
        reads, writes = self._norm(reads, writes)
        self._deps('pe', reads, writes)
        ins = None
        for f in fns:
            ins = f()
            self.n_inst += 1
            self.per['pe'] = self.per.get('pe', 0) + 1
        tok = self._bump('pe', ins)
        self._commit(tok, reads, writes)
        return tok

    def dma(self, q, out, in_, reads=(), writes=(), **kw):
        reads, writes = self._norm(reads, writes)
        self._deps(q, reads, writes)
        cls = 'sw' if q == 'pool' else 'hw'
        pool_ = self.dma_sems[cls]
        slot = pool_[self.ndma[cls] % len(pool_)]
        self.ndma[cls] += 1
        if slot[1] > 0:
            self._wait(q, (slot[2], slot[0], slot[1]))
        if slot[1] >= self.LIMIT:
            slot[0] = self.nc.alloc_semaphore("%s_e%d" % (slot[2], self.ndma[cls]))
            slot[1] = 0
            slot[2] = slot[2] + "x"
        slot[1] += 16
        ins = self.e[q].dma_start(out=out, in_=in_, **kw)
        self.per[q] = self.per.get(q, 0) + 1
        ins.then_inc(slot[0], 16)
        tok = (slot[2], slot[0], slot[1])
        self._commit(tok, reads, writes)
        self.n_inst += 1
        return tok

    def barrier(self, engines=('pe', 'act', 'dve', 'pool', 'sp')):
        toks = [("%s_%d" % (k, self.epoch[k]), self.sem[k], self.cnt[k]) for k in self.ENG if self.cnt[k] > 0]
        toks += [(s[2], s[0], s[1]) for p_ in self.dma_sems.values() for s in p_ if s[1] > 0]
        for e in engines:
            for t in toks:
                self._wait(e, t)
        self.lastw = {}
        self.reads = {}


class Pipe:
    def __init__(self, lag=2):
        self.q = []
        self.lag = lag

    def push(self, first, second):
        first()
        self.q.append(second)
        while len(self.q) > self.lag:
            self.q.pop(0)()

    def flush(self):
        while self.q:
            self.q.pop(0)()


class Tile:
    def __init__(self, h, name):
        self.h = h
        self.n = name

    def __getitem__(self, idx):
        return self.h[idx]


class Scope:
    cnt = 0

    def __init__(self, k):
        self.k = k
        self.es = ExitStack()

    def __enter__(self):
        self.es.__enter__()
        Scope.cnt += 1
        self.id = Scope.cnt
        return self

    def sb(self, name, shape, dt):
        nm = "%s_%d" % (name, self.id)
        h = self.es.enter_context(self.k.nc.sbuf_tensor(nm, list(shape), dt))
        return Tile(h, nm)

    def ps(self, name, shape, dt=F32):
        nm = "%s_%d" % (name, self.id)
        h = self.es.enter_context(self.k.nc.psum_tensor(nm, list(shape), dt))
        return Tile(h, nm)

    def __exit__(self, *a):
        self.k.S.barrier()
        return self.es.__exit__(*a)


class K:
    def __init__(self, nlayers, taps=()):
        self.nc = bass.Bass("TRN2", target_bir_lowering=False)
        self.S = Sched(self.nc)
        self.nl = nlayers
        self.taps = set(taps)
        self.ins = {}
        self.dr = {}

    def inp(self, name, shape, dt=F32):
        t = self.nc.dram_tensor(name, list(shape), dt, kind="ExternalInput").ap()
        self.ins[name] = t
        return t

    def scratch(self, name, shape, dt=F32, out=False):
        kind = "ExternalOutput" if (out or name in self.taps) else "Internal"
        t = self.nc.dram_tensor(name, list(shape), dt, kind=kind).ap()
        self.dr[name] = t
        return t

    def act(self, out, in_, func, r, w, bias=0.0, scale=1.0, accum=None):
        nc = self.nc
        if accum is None:
            return self.S.op('act', lambda: nc.scalar.activation(out=out, in_=in_, func=func, bias=bias, scale=scale), r, w)
        return self.S.op('act', lambda: nc.scalar.activation(out=out, in_=in_, func=func, bias=bias, scale=scale, accum_out=accum), r, w)

    def ts(self, eng, out, in0, s1, s2, op0, op1, r, w):
        e = self.S.e[eng]
        if op1 is None:
            return self.S.op(eng, lambda: e.tensor_scalar(out=out, in0=in0, scalar1=s1, scalar2=None, op0=op0), r, w)
        return self.S.op(eng, lambda: e.tensor_scalar(out=out, in0=in0, scalar1=s1, scalar2=s2, op0=op0, op1=op1), r, w)

    def tt(self, eng, out, in0, in1, op, r, w):
        e = self.S.e[eng]
        return self.S.op(eng, lambda: e.tensor_tensor(out=out, in0=in0, in1=in1, op=op), r, w)

    def stt(self, eng, out, in0, scalar, in1, op0, op1, r, w):
        e = self.S.e[eng]
        return self.S.op(eng, lambda: e.scalar_tensor_tensor(out=out, in0=in0, scalar=scalar, in1=in1, op0=op0, op1=op1), r, w)

    def copy(self, eng, out, in_, r, w):
        if eng == 'act':
            return self.S.op('act', lambda: self.nc.scalar.copy(out=out, in_=in_), r, w)
        e = self.S.e[eng]
        return self.S.op(eng, lambda: e.tensor_copy(out=out, in_=in_), r, w)

    def mm(self, out, pairs, r, w, start=True, stop=True, sgc=False):
        nc = self.nc
        n = len(pairs)
        fns = []
        for i, (l, rh) in enumerate(pairs):
            fns.append(lambda l=l, rh=rh, i=i: nc.tensor.matmul(out, lhsT=l, rhs=rh, start=(start and i == 0), stop=(stop and i == n - 1),
                                                               skip_group_check=(sgc or not start)))
        return self.S.pe_group(fns, r, w)

    def transpose(self, out, in_, ident, r, w):
        nc = self.nc
        return self.S.op('pe', lambda: nc.tensor.transpose(out=out, in_=in_, identity=ident), r, w)

    def dma(self, q, out, in_, r=(), w=(), **kw):
        return self.S.dma(q, out, in_, r, w, **kw)


def w_in_perm_index():
    idx = list(range(0, 896))
    idx += list(range(896, 1664))
    for c in range(3):
        idx += list(range(2054 + c * 64, 2054 + c * 64 + 64))
        idx += list(range(2054 + (c + 3) * 64, 2054 + (c + 3) * 64 + 64))
    idx += list(range(2438, 2566))
    idx += list(range(2566, 2694))
    idx += list(range(2694, 2822))
    idx += list(range(2950, 3078))
    idx += list(range(2048, 2054))
    idx += list(range(1664, 2048))
    idx += list(range(2822, 2950))
    idx += list(range(3078, 3206))
    idx += list(range(3206, 3224))
    assert len(idx) == N_IN and len(set(idx)) == N_IN
    return np.array(idx)


QKT_ROWS = 1664


def setup_globals(k):
    nc = k.nc
    k.x_in = k.inp("x", [S_LEN, D])
    k.cT = k.inp("cT", [128, 8])
    k.ada_w = k.inp("ada_w", [DEPTH, D, 6 * D])
    k.ada_b_fm = k.inp("ada_b_fm", [DEPTH, 128, 48])
    k.ada_b_row = k.inp("ada_b_row", [DEPTH, 6 * D])
    k.normg_fm = k.inp("normg_fm", [DEPTH, 4, 128, 8])
    k.normg_row = k.inp("normg_row", [DEPTH, 4, D])
    k.w_in = k.inp("w_in_p", [DEPTH, D, N_IN])
    k.ident_bf_d = k.inp("ident_bf", [128, 128], BF16)
    k.ident_f_d = k.inp("ident_f", [128, 128], F32)

    k.PT = k.scratch("PT", [896, S_LEN], F32)
    k.QKT = k.scratch("QKT", [QKT_ROWS, S_LEN], BF16)
    k.FL = k.scratch("FL", [6, S_LEN], F32)
    k.VT = k.scratch("VT", [S_LEN, 640], BF16)
    k.GT = k.scratch("GT", [S_LEN, 18], F32)
    k.Y = k.scratch("Y", [S_LEN, D], BF16)
    k.XR = k.scratch("XR", [S_LEN, D], F32)
    k.XR1 = k.scratch("XR1", [S_LEN, D], F32)
    k.OUT = k.scratch("out", [S_LEN, D], F32, out=True)

    def pers(name, shape, dt):
        return Tile(nc.alloc_sbuf_tensor(name, list(shape), dt), name)
    k.ident_bf = pers("ident_bf_sb", [128, 128], BF16)
    k.ident_f = pers("ident_f_sb", [128, 128], F32)
    k.sc = pers("sc", [128, 8], F32)
    k.modAB = pers("modAB", [128, 32], F32)
    k.gm_row = pers("gm_row", [128, D], F32)
    k.gf_row = pers("gf_row", [128, D], F32)
    k.dma('sp', k.ident_bf[:], k.ident_bf_d, w=[k.ident_bf])
    k.dma('sp', k.ident_f[:], k.ident_f_d, w=[k.ident_f])
    k.dma('sp', k.sc[:], k.cT, w=[k.sc])
    k.act(k.sc[:], k.sc[:], AF.Silu, r=[k.sc], w=[k.sc])


def stage_mod(k, l, bg=None):
    with Scope(k) as sc:
        slab = [sc.sb("adaslab%d" % i, [128, 6 * D], F32) for i in range(4)]
        psA = sc.ps("psA", [128, 32])
        psR = [sc.ps("psR%d" % i, [128, 512]) for i in range(4)]
        bfm = sc.sb("bfm", [128, 48], F32)
        gfm = sc.sb("gfm", [128, 4, 8], F32)
        brow = sc.sb("brow", [128, 2, D], F32)
        grow = sc.sb("grow", [128, 2, D], F32)
        mfm = sc.sb("mfm", [128, 32], F32)
        sc_rep = sc.sb("sc_rep", [128, 8, 128], F32)
        for kc in range(8):
            k.copy('dve', sc_rep[:, kc, :], k.sc[:, kc:kc + 1].to_broadcast([128, 128]), r=[k.sc], w=[sc_rep])
        k.dma('sp', bfm[:], k.ada_b_fm[l], w=[bfm])
        k.dma('sp', gfm[:], k.normg_fm[l].rearrange("g p c -> p g c"), w=[gfm])
        k.dma('sp', brow[:, 0, :], k.ada_b_row[l:l + 1, 2 * D:3 * D].broadcast_to([128, D]), w=[brow])
        k.dma('sp', brow[:, 1, :], k.ada_b_row[l:l + 1, 5 * D:6 * D].broadcast_to([128, D]), w=[brow])
        k.dma('sp', grow[:, 0, :], k.normg_row[l, 1:2, :].broadcast_to([128, D]), w=[grow])
        k.dma('sp', grow[:, 1, :], k.normg_row[l, 3:4, :].broadcast_to([128, D]), w=[grow])
        fm_chunks = list(range(0, 16)) + list(range(24, 40))
        row_cols = [2 * D, 2 * D + 512, 5 * D, 5 * D + 512]
        for kc in range(8):
            sl = slab[kc % 4]
            k.dma('sp' if kc % 2 == 0 else 'act', sl[:], k.ada_w[l, kc * 128:(kc + 1) * 128, :], w=[sl])
            for i, j in enumerate(fm_chunks):
                k.mm(psA[:, i:i + 1], [(sl[:, j * 128:(j + 1) * 128], k.sc[:, kc:kc + 1])], r=[sl, k.sc], w=[psA],
                     start=(kc == 0 and i == 0), stop=(kc == 7), sgc=True)
            for i, c0 in enumerate(row_cols):
                k.mm(psR[i][:], [(sc_rep[:, kc, :], sl[:, c0:c0 + 512])], r=[sl, sc_rep], w=[psR[i]],
                     start=(kc == 0), stop=(kc == 7), sgc=True)
            if bg is not None:
                try:
                    next(bg)
                except StopIteration:
                    bg = None
        if bg is not None:
            for _ in bg:
                pass
        k.tt('dve', mfm[:, 0:16], psA[:, 0:16], bfm[:, 0:16], ALU.add, r=[psA, bfm], w=[mfm])
        k.tt('dve', mfm[:, 16:32], psA[:, 16:32], bfm[:, 24:40], ALU.add, r=[psA, bfm], w=[mfm])
        k.stt('dve', k.modAB[:, 0:8], mfm[:, 8:16], 1.0, gfm[:, 0, :], ALU.add, ALU.mult, r=[mfm, gfm], w=[k.modAB])
        k.copy('dve', k.modAB[:, 8:16], mfm[:, 0:8], r=[mfm], w=[k.modAB])
        k.stt('dve', k.modAB[:, 16:24], mfm[:, 24:32], 1.0, gfm[:, 2, :], ALU.add, ALU.mult, r=[mfm, gfm], w=[k.modAB])
        k.copy('dve', k.modAB[:, 24:32], mfm[:, 16:24], r=[mfm], w=[k.modAB])
        for i in range(4):
            dst = (k.gm_row if i < 2 else k.gf_row)
            cs = slice((i % 2) * 512, (i % 2) * 512 + 512)
            k.tt('dve', dst[:, cs], psR[i][:], brow[:, i // 2, cs], ALU.add, r=[psR[i], brow], w=[dst])
            k.tt('pool', dst[:, cs], dst[:, cs], grow[:, i // 2, cs], ALU.mult, r=[dst, grow], w=[dst])


def stage_mod_proj(k, l, xsrc):
    with Scope(k) as sc:
        wsb = sc.sb("wsb", [128, 8, N_IN], BF16)
        with Scope(k) as s2:
            wst = [s2.sb("wst%d" % i, [128, N_IN], F32) for i in range(2)]

            def bg():
                for kc in range(8):
                    s = wst[kc % 2]
                    k.dma('act' if kc % 2 == 0 else 'sp', s[:], k.w_in[l, kc * 128:(kc + 1) * 128, :], w=[s])
                    k.copy('pool' if kc % 2 == 0 else 'dve', wsb[:, kc, :], s[:], r=[s], w=[wsb])
                    yield
            stage_mod(k, l, bg=bg())
        stage_proj(k, l, xsrc, pre=(sc, wsb))


def stage_proj(k, l, xsrc, pre=None):
    nc = k.nc
    with ExitStack() as es_:
        if pre is None:
            sc = es_.enter_context(Scope(k))
            wsb = sc.sb("wsb", [128, 8, N_IN], BF16)
            wst = [sc.sb("wst%d" % i, [128, N_IN], F32) for i in range(4)]
        else:
            sc, wsb = pre
            wst = None
        xt = [sc.sb("xt%d" % i, [128, D], F32) for i in range(2)]
        junk = sc.sb("junk", [128, D], BF16)
        xn = [sc.sb("xn%d" % i, [128, D], BF16) for i in range(2)]
        st = [sc.sb("st%d" % i, [128, 4], F32) for i in range(2)]
        HT = [sc.sb("HT%d" % i, [128, 8, 512], BF16) for i in range(2)]
        psT = [sc.ps("psT%d" % i, [128, D], BF16) for i in range(2)]
        psM = [sc.ps("psM%d" % i, [128, 512]) for i in range(4)]
        evf = [sc.sb("evf%d" % i, [128, 512], F32) for i in range(3)]
        evb = [sc.sb("evb%d" % i, [128, 512], BF16) for i in range(3)]
        evt = [sc.sb("evt%d" % i, [128, 640], BF16) for i in range(2)]
        evg = [sc.sb("evg%d" % i, [128, 18], F32) for i in range(2)]
        for kc in (range(8) if pre is None else []):
            s = wst[kc % 4]
            k.dma('sp' if kc % 2 == 0 else 'act', s[:], k.w_in[l, kc * 128:(kc + 1) * 128, :], w=[s])
            k.copy('pool' if kc % 2 == 0 else 'dve', wsb[:, kc, :], s[:], r=[s], w=[wsb])
        cnt = {"ev": 0, "pm": 0}

        def norm_gen(tb):
            ht = HT[tb % 2]
            t0_ = tb * 4
            k.dma('act', xt[t0_ % 2][:], xsrc[t0_ * 128:(t0_ + 1) * 128, :], w=[xt[t0_ % 2]])
            yield
            for sub in range(4):
                ti = tb * 4 + sub
                x_ = xt[ti % 2]; xn_ = xn[ti % 2]; st_ = st[ti % 2]; pt_ = psT[ti % 2]
                k.act(junk[:], x_[:], AF.Square, r=[x_], w=[junk, st_], scale=1.0 / 32.0, accum=st_[:, 0:1])
                k.act(st_[:, 1:2], st_[:, 0:1], AF.Ln, r=[st_], w=[st_], bias=RMS_EPS)
                k.act(st_[:, 2:3], st_[:, 1:2], AF.Exp, r=[st_], w=[st_], scale=-0.5)
                k.ts('dve', xn_[:], x_[:], st_[:, 2:3], None, ALU.mult, None, r=[x_, st_], w=[xn_])
                if sub < 3:
                    k.dma('act', xt[(ti + 1) % 2][:], xsrc[(ti + 1) * 128:(ti + 2) * 128, :], w=[xt[(ti + 1) % 2]])
                yield
                yield
                for kc in range(8):
                    k.transpose(pt_[:, kc * 128:(kc + 1) * 128], xn_[:, kc * 128:(kc + 1) * 128], k.ident_bf[:],
                                r=[xn_, k.ident_bf], w=[pt_])
                    if kc == 3:
                        yield
                yield
                for kc in range(8):
                    o = ht[:, kc, sub * 128:(sub + 1) * 128]
                    i_ = pt_[:, kc * 128:(kc + 1) * 128]
                    if kc % 2 == 0:
                        k.ts('dve', o, i_, k.modAB[:, kc:kc + 1], k.modAB[:, 8 + kc:9 + kc], ALU.mult, ALU.add,
                             r=[pt_, k.modAB], w=[ht])
                    else:
                        k.act(o, i_, AF.Identity, r=[pt_, k.modAB], w=[ht], scale=k.modAB[:, kc:kc + 1],
                              bias=k.modAB[:, 8 + kc:9 + kc])
                yield

        def step(g):
            if g is not None:
                try:
                    next(g)
                except StopIteration:
                    return None
            return g

        for _ in norm_gen(0):
            pass
        for tb in range(NTB):
            ht = HT[tb % 2]
            g = norm_gen(tb + 1) if tb + 1 < NTB else None
            tsl = slice(tb * 512, (tb + 1) * 512)
            fm = [(c * 128, 128, 'PT', c * 128) for c in range(7)]
            fm += [(896 + c * 128, 128, 'QKT', c * 128) for c in range(13)]
            fm += [(2560, 6, 'FL', 0)]
            for (c0, m, dst, r0) in fm:
                ps = psM[cnt["pm"] % 4]; cnt["pm"] += 1
                k.mm(ps[0:m, :], [(wsb[:, kc, c0:c0 + m], ht[:, kc, :]) for kc in range(8)], r=[wsb, ht], w=[ps])
                eng = 'act' if cnt["ev"] % 2 == 0 else 'dve'
                if dst == 'QKT':
                    ev = evb[cnt["ev"] % 3]
                    dd = k.QKT[r0:r0 + m, tsl]
                else:
                    ev = evf[cnt["ev"] % 3]
                    dd = (k.PT if dst == 'PT' else k.FL)[r0:r0 + m, tsl]
                cnt["ev"] += 1
                k.copy(eng, ev[0:m, :], ps[0:m, :], r=[ps], w=[ev])
                k.dma('sp', dd, ev[0:m, :], r=[ev])
                g = step(g)
            for sub in range(4):
                ti = tb * 4 + sub
                tok = slice(ti * 128, (ti + 1) * 128)
                ps0 = psM[cnt["pm"] % 4]; cnt["pm"] += 1
                ps1 = psM[cnt["pm"] % 4]; cnt["pm"] += 1
                lhs = lambda kc: ht[:, kc, sub * 128:(sub + 1) * 128]
                k.mm(ps0[:, 0:384], [(lhs(kc), wsb[:, kc, 2566:2950]) for kc in range(8)], r=[wsb, ht], w=[ps0])
                k.mm(ps1[:, 0:274], [(lhs(kc), wsb[:, kc, 2950:3224]) for kc in range(8)], r=[wsb, ht], w=[ps1])
                et = evt[ti % 2]; eg = evg[ti % 2]
                k.copy('act', et[:, 0:384], ps0[:, 0:384], r=[ps0], w=[et])
                k.copy('dve', et[:, 384:640], ps1[:, 0:256], r=[ps1], w=[et])
                k.copy('dve', eg[:], ps1[:, 256:274], r=[ps1], w=[eg])
                k.dma('sp', k.VT[tok, :], et[:], r=[et])
                k.dma('sp', k.GT[tok, :], eg[:], r=[eg])
                g = step(g)
            while g is not None:
                g = step(g)


def prep_shared(inp):
    f = lambda a: np.ascontiguousarray(np.asarray(a, dtype=np.float32))
    sh = {}
    sh["ada_w"] = f(inp["ada_w"])
    sh["ada_b_fm"] = f(np.asarray(inp["ada_b"]).reshape(DEPTH, 48, 128).transpose(0, 2, 1))
    sh["ada_b_row"] = f(inp["ada_b"])
    sh["normg_fm"] = f(np.asarray(inp["norm_g"]).reshape(DEPTH, 4, 8, 128).transpose(0, 1, 3, 2))
    sh["normg_row"] = f(inp["norm_g"])
    sh["w_in_p"] = f(np.asarray(inp["w_in"])[:, :, w_in_perm_index()])
    sh["ident_bf"] = np.eye(128, dtype=np.float32).astype(NPBF)
    sh["ident_f"] = np.eye(128, dtype=np.float32)
    sh["w_out"] = f(inp["w_out"]); sh["ffn_up"] = f(inp["ffn_up"]); sh["ffn_down"] = f(inp["ffn_down"])
    sh["conv_w_fm"] = f(np.asarray(inp["ffn_conv_w"]).reshape(DEPTH, 3, 44, 128).transpose(0, 3, 1, 2))
    sh["conv_b_fm"] = f(np.asarray(inp["ffn_conv_b"]).reshape(DEPTH, 44, 128).transpose(0, 2, 1))
    sh.update(nsa_host_consts())
    sh["rel_bias"] = f(inp["rel_bias"])
    sh["nsa_pe_kT"] = f(np.asarray(inp["nsa_pe_k"]).transpose(0, 2, 1))
    sh["nsa_pe_vT"] = f(np.asarray(inp["nsa_pe_v"]).transpose(0, 2, 1))
    for n in ("nsa_ck_w1", "nsa_cv_w1", "nsa_ck_w2", "nsa_cv_w2"):
        sh[n] = f(inp[n])
    sh["fox_b_f"] = f(np.asarray(inp["fox_b_f"]).reshape(DEPTH, 6, 1))
    sh.update(rwkv_host(inp))
    return sh


def prep_core(inp, b):
    d = {}
    d["x"] = np.ascontiguousarray(np.asarray(inp["x"][b], dtype=np.float32))
    d["cT"] = np.ascontiguousarray(np.asarray(inp["c"][b], dtype=np.float32).reshape(8, 128).T)
    return d


def setup_fox(k):
    k.fox_bf = k.inp("fox_b_f", [DEPTH, 6, 1])
    k.CUMA = k.scratch("CUMA", [6, 3, S_LEN], BF16)


def stage_fox(k, l):
    nc = k.nc
    with Scope(k) as sc:
        nb = sc.sb("nb", [128, 32, 6], F32)
        with Scope(k) as s2:
            fl = s2.sb("fl", [6, S_LEN], F32)
            t1 = s2.sb("t1", [6, S_LEN], F32)
            ones = s2.sb("ones", [6, S_LEN], F32)
            cum = s2.sb("cum", [6, S_LEN], F32)
            parts = s2.sb("parts", [6, 3, S_LEN], BF16)
            bfv = s2.sb("bfv", [6, 2], F32)
            psn = s2.ps("psn", [128, 512])
            k.dma('sp', fl[:], k.FL, w=[fl])
            k.dma('sp', bfv[:, 0:1], k.fox_bf[l], w=[bfv])
            k.ts('dve', bfv[:, 1:2], bfv[:, 0:1], -1.0, None, ALU.mult, None, r=[bfv], w=[bfv])
            k.S.op('pool', lambda: nc.gpsimd.memset(ones[:], 1.0), [], [ones])
            k.act(t1[:], fl[:], AF.Exp, r=[fl, bfv], w=[t1], bias=bfv[:, 1:2], scale=-1.0)
            k.act(t1[:], t1[:], AF.Ln, r=[t1], w=[t1], bias=1.0, scale=1.0)
            k.ts('dve', t1[:], t1[:], -1.0, None, ALU.mult, None, r=[t1], w=[t1])
            k.S.op('dve', lambda: nc.vector.tensor_tensor_scan(out=cum[:], data0=ones[:], data1=t1[:], initial=0.0,
                                                               op0=ALU.mult, op1=ALU.add), [ones, t1], [cum])
            for t in range(32):
                k.transpose(psn[:, t * 6:(t + 1) * 6], cum[:, t * 128:(t + 1) * 128], k.ident_f[0:6, 0:6],
                            r=[cum, k.ident_f], w=[psn])
            k.ts('dve', nb[:].rearrange("p t h -> p (t h)"), psn[:, 0:192], -1.0, None, ALU.mult, None, r=[psn], w=[nb])
            k.ts('dve', t1[:], cum[:], 8.0, None, ALU.mult, None, r=[cum], w=[t1])
            k.copy('dve', parts[:, 0, :], t1[:], r=[t1], w=[parts])
            k.tt('dve', t1[:], t1[:], parts[:, 0, :], ALU.subtract, r=[t1, parts], w=[t1])
            k.copy('dve', parts[:, 1, :], t1[:], r=[t1], w=[parts])
            k.tt('dve', t1[:], t1[:], parts[:, 1, :], ALU.subtract, r=[t1, parts], w=[t1])
            k.copy('dve', parts[:, 2, :], t1[:], r=[t1], w=[parts])
            k.dma('sp', k.CUMA, parts[:], r=[parts], w=["CUMA"])
        QA = [sc.sb("QA%d" % i, [128, S_LEN], BF16) for i in range(2)]
        KA = [sc.sb("KA%d" % i, [128, S_LEN], BF16) for i in range(2)]
        VA = sc.sb("VA", [128, 32, 6, 65], BF16)
        yb = sc.sb("yb", [128, 32, 384], BF16)
        PTl = [sc.sb("PTl%d" % i, [128, 512], BF16) for i in range(6)]
        rc = [sc.sb("rc%d" % i, [128, 4], F32) for i in range(2)]
        psS = [sc.ps("psS%d" % i, [128, 512]) for i in range(4)]
        psO = [sc.ps("psO%d" % i, [128, 512]) for i in range(2)]
        k.dma('sp', yb[:], k.VT[:, 0:384].rearrange("(t p) c -> p t c", p=128), w=[yb])
        k.S.op('pool', lambda: nc.gpsimd.memset(VA[:, :, :, 64:65], 1.0), [], [VA])
        k.copy('pool', VA[:, :, :, 0:64], yb[:].rearrange("p t (h d) -> p t h d", h=6), r=[yb], w=[VA])
        for i in range(2):
            k.S.op('dve', lambda i=i: nc.vector.memset(KA[i][64:67, :], 1.0), [], [KA[i]])
        nS = 0
        nO = 0
        nP = 0
        pipe = Pipe(3)
        for h in range(6):
            qa = QA[h % 2]; ka = KA[h % 2]
            k.dma('sp', qa[0:64, :], k.QKT[h * 64:(h + 1) * 64, :], w=[qa])
            k.dma('sp', qa[64:67, :], k.CUMA[h], w=[qa])
            k.dma('sp', ka[0:64, :], k.QKT[384 + h * 64:384 + (h + 1) * 64, :], w=[ka])
            for qb in range(NTB):
                po = psO[nO % 2]; nO += 1
                nkt = 4 * qb + 4
                for kt in range(nkt):
                    j = kt - 4 * qb
                    c0 = max(j, 0) * 128
                    ps = psS[nS % len(psS)]; nS += 1
                    pt = PTl[nP % len(PTl)]; nP += 1

                    def first(ps=ps, pt=pt, kt=kt, c0=c0, j=j, qa=qa, ka=ka, qb=qb, h=h):
                        k.mm(ps[:, c0:512], [(ka[0:67, kt * 128:(kt + 1) * 128], qa[0:67, qb * 512 + c0:(qb + 1) * 512])],
                             r=[ka, qa], w=[ps])
                        k.act(pt[:, c0:512], ps[:, c0:512], AF.Exp, r=[ps, nb], w=[pt], bias=nb[:, kt, h:h + 1], scale=0.125)
                        if j >= 0:
                            k.S.op('pool', lambda: nc.gpsimd.affine_select(
                                out=pt[:, c0:c0 + 128], in_=pt[:, c0:c0 + 128], pattern=[[1, 128]], compare_op=ALU.is_ge,
                                fill=0.0, base=0, channel_multiplier=-1), [pt], [pt])

                    def second(pt=pt, kt=kt, j=j, po=po, qb=qb, h=h, last=(kt == nkt - 1)):
                        fns = []
                        for qs in range(max(j, 0), 4):
                            fns.append(lambda qs=qs: nc.tensor.matmul(
                                po[:, qs * 65:(qs + 1) * 65], lhsT=pt[:, qs * 128:(qs + 1) * 128], rhs=VA[:, kt, h, :],
                                start=(kt == 0 and qs == 0), stop=(kt == 4 * qb + qs), skip_group_check=True))
                        k.S.pe_group(fns, [pt, VA], [po])
                        if last:
                            r_ = rc[qb % 2]
                            pov = po[:, 0:260].rearrange("p (q c) -> p q c", c=65)
                            k.S.op('dve', lambda: nc.vector.reciprocal(out=r_[:], in_=pov[:, :, 64]), [po], [r_])
                            for qs in range(4):
                                k.ts('dve', yb[:, qb * 4 + qs, h * 64:(h + 1) * 64], po[:, qs * 65:qs * 65 + 64], r_[:, qs:qs + 1], None,
                                     ALU.mult, None, r=[po, r_], w=[yb])
                    pipe.push(first, second)
        pipe.flush()
        k.dma('sp', k.Y[:, 256:640].rearrange("(t p) c -> p t c", p=128), yb[:], r=[yb], w=["Y"])


LW = 1536
LC = 4608
NEG8 = -240000.0


def t5_bucket_np(n):
    n = np.maximum(n, 0)
    nf = np.maximum(n, 1).astype(np.float32)
    large = 16 + (np.log(nf / np.float32(16)) / np.float32(np.log(128 / 16)) * np.float32(16)).astype(np.int32)
    large = np.minimum(large, 31)
    return np.where(n < 16, n, large)


def nsa_host_consts():
    c = {}
    i = np.arange(LW); n = i - 511
    oh = np.zeros((33, LW), np.float32)
    ok = (n >= 0) & (n < 512)
    oh[t5_bucket_np(n)[ok], i[ok]] = 1.0
    oh[32, ~ok] = NEG8
    c["oh_w"] = oh
    i = np.arange(LC); n = i - 2063
    oh = np.zeros((33, LC), np.float32)
    ok = n >= 0
    oh[t5_bucket_np(n)[ok], i[ok]] = 1.0
    oh[32, ~ok] = NEG8
    c["oh_c"] = oh
    s_ = np.arange(S_LEN)
    c["E_all"] = (np.arange(64)[:, None] == (s_[None, :] // 64)).astype(np.float32).astype(NPBF)
    cs = np.arange(256) * 16
    ce = cs + 31
    ss = np.arange(64) * 64
    ov = ((cs[:, None] <= ss[None, :] + 63) & (ce[:, None] >= ss[None, :])).astype(np.float32)
    ov[255] = 0.0
    c["ovl"] = np.ascontiguousarray(ov.reshape(2, 128, 64).transpose(1, 0, 2)).astype(NPBF)
    t = np.arange(S_LEN)
    cur = t // 64
    jb = np.arange(64)
    back = cur[:, None] - jb[None, :]
    valid = back >= 0
    forced = (jb[None, :] == 0) | (valid & (back < 2))
    tkm = (valid & ~forced).astype(np.float32)
    tka = np.where(valid, np.where(forced, 1e4, 0.0), -1.0).astype(np.float32)
    c["tkm"] = np.ascontiguousarray(tkm.reshape(32, 128, 64).transpose(1, 0, 2)).astype(NPBF)
    c["tka"] = np.ascontiguousarray(tka.reshape(32, 128, 64).transpose(1, 0, 2)).astype(NPBF)
    return c


def setup_nsa(k):
    nc = k.nc
    k.rel_bias = k.inp("rel_bias", [32, 6])
    k.oh_w = k.inp("oh_w", [33, LW])
    k.oh_c = k.inp("oh_c", [33, LC])
    k.E_d = k.inp("E_all", [64, S_LEN], BF16)
    k.ovl_d = k.inp("ovl", [128, 2, 64], BF16)
    k.tkm_d = k.inp("tkm", [128, 32, 64], BF16)
    k.tka_d = k.inp("tka", [128, 32, 64], BF16)
    k.pe_kT = k.inp("nsa_pe_kT", [DEPTH, 64, 32])
    k.pe_vT = k.inp("nsa_pe_vT", [DEPTH, 64, 32])
    k.ck_w1 = k.inp("nsa_ck_w1", [DEPTH, 2048, 128])
    k.cv_w1 = k.inp("nsa_cv_w1", [DEPTH, 2048, 128])
    k.ck_w2 = k.inp("nsa_ck_w2", [DEPTH, 128, 64])
    k.cv_w2 = k.inp("nsa_cv_w2", [DEPTH, 128, 64])
    k.WVW = k.scratch("WVW", [6, 128, LW], BF16)
    k.WVC = k.scratch("WVC", [6, 128, LC], BF16)
    with Scope(k) as sc:
        rb = sc.sb("rb", [33, 6], F32)
        rb31 = sc.sb("rb31", [32, 6], F32)
        rrep = sc.sb("rrep", [33, 6, 128], F32)
        ohw = sc.sb("ohw", [33, LW], F32)
        ohc = sc.sb("ohc", [33, LC], F32)
        ps = [sc.ps("psb%d" % i, [128, 512]) for i in range(2)]
        ev = [sc.sb("evb%d" % i, [128, 512], BF16) for i in range(2)]
        k.dma('sp', rb[0:32, :], k.rel_bias, w=[rb])
        k.dma('sp', rb31[:], k.rel_bias[31:32, :].broadcast_to([32, 6]), w=[rb31])
        k.dma('sp', ohw[:], k.oh_w, w=[ohw])
        k.dma('sp', ohc[:], k.oh_c, w=[ohc])
        k.S.op('dve', lambda: nc.vector.memset(rb[32:33, :], 1.0), [], [rb])
        k.tt('dve', rb[0:32, :], rb[0:32, :], rb31[:], ALU.subtract, r=[rb, rb31], w=[rb])
        k.ts('dve', rb[0:32, :], rb[0:32, :], 8.0, None, ALU.mult, None, r=[rb], w=[rb])
        for h in range(6):
            k.copy('dve', rrep[:, h, :], rb[:, h:h + 1].to_broadcast([33, 128]), r=[rb], w=[rrep])
        n = 0
        for h in range(6):
            for (oh, L, dst) in ((ohw, LW, k.WVW), (ohc, LC, k.WVC)):
                for c0 in range(0, L, 512):
                    p_ = ps[n % 2]; e_ = ev[n % 2]; n += 1
                    k.mm(p_[:], [(rrep[:, h, :], oh[:, c0:c0 + 512])], r=[rrep, oh], w=[p_])
                    k.copy('act' if n % 2 else 'dve', e_[:], p_[:], r=[p_], w=[e_])
                    k.dma('sp', dst[h, :, c0:c0 + 512], e_[:], r=[e_])


class DbgStop(Exception):
    pass


def dbg(k, lvl):
    if getattr(k, 'dbg_stop', None) == lvl:
        raise DbgStop()


def stage_nsa(k, l):
    nc = k.nc
    with Scope(k) as sc:
        Gw = sc.sb("Gw", [128, 6, 1408], BF16)
        Gc = sc.sb("Gc", [128, 6, 2560], BF16)
        tkm = sc.sb("tkm", [128, 32, 64], BF16)
        tka = sc.sb("tka", [128, 32, 64], BF16)
        QN = [sc.sb("QN%d" % h, [128, S_LEN], BF16) for h in range(6)]
        KE = [sc.sb("KE%d" % g, [128, S_LEN], BF16) for g in range(2)]
        KW = sc.sb("KW", [128, S_LEN], BF16)
        VS = sc.sb("VS", [128, 32, 2, 65], BF16)
        VW = sc.sb("VW", [128, 32, 2, 65], BF16)
        KCMP = sc.sb("KCMP", [128, 256], BF16)
        VE = sc.sb("VE", [128, 2, 2, 129], BF16)
        sg = sc.sb("sg", [128, 32, 18], F32)
        for h in range(6):
            k.dma('sp', Gw[:, h, :], bass.AP(k.WVW.tensor, h * 128 * LW + 127, [[LW - 1, 128], [1, 1408]]), w=[Gw])
            k.dma('sp', Gc[:, h, :], bass.AP(k.WVC.tensor, h * 128 * LC + 2032, [[LC - 16, 128], [1, 2560]]), w=[Gc])
        k.dma('sp', tkm[:], k.tkm_d, w=[tkm])
        k.dma('sp', tka[:], k.tka_d, w=[tka])
        for h in range(6):
            g_, hp_ = h // 3, h % 3
            k.dma('sp', QN[h][g_ * 64:(g_ + 1) * 64, :], k.QKT[768 + hp_ * 128 + g_ * 64:768 + hp_ * 128 + (g_ + 1) * 64, :], w=[QN[h]])
            k.S.op('pool', lambda h=h, g_=g_: nc.gpsimd.memset(QN[h][(1 - g_) * 64:(2 - g_) * 64, :], 0.0), [], [QN[h]])
        for g_ in range(2):
            k.dma('sp', KE[g_][g_ * 64:(g_ + 1) * 64, :], k.QKT[1408 + g_ * 64:1408 + (g_ + 1) * 64, :], w=[KE[g_]])
            k.dma('sp', KE[g_][(1 - g_) * 64:(2 - g_) * 64, :], k.E_d, w=[KE[g_]])
        k.dma('sp', KW[:], k.QKT[1536:1664, :], w=[KW])
        k.dma('sp', sg[:], k.GT.rearrange("(t p) c -> p t c", p=128), w=[sg])
        k.act(sg[:], sg[:], AF.Exp, r=[sg], w=[sg], scale=-1.0)
        k.ts('dve', sg[:], sg[:], 1.0, None, ALU.add, None, r=[sg], w=[sg])
        k.S.op('dve', lambda: nc.vector.reciprocal(out=sg[:], in_=sg[:]), [sg], [sg])
        k.dma('sp', VE[:, 0, :, 65:129], k.ovl_d, w=[VE])
        k.dma('sp', VE[:, 1, :, 65:129], k.ovl_d, w=[VE])
        k.S.op('pool', lambda: nc.gpsimd.memset(VE[:, :, :, 64:65], 1.0), [], [VE])
        k.S.op('pool', lambda: nc.gpsimd.memset(VE[:, :, :, 0:64], 0.0), [], [VE])
        k.S.op('pool', lambda: nc.gpsimd.memset(KCMP[:], 0.0), [], [KCMP])
        dbg(k, 1)
        with Scope(k) as s2:
            vst = s2.sb("vst", [128, 32, 256], BF16)
            k.dma('sp', vst[:], k.VT[:, 384:640].rearrange("(t p) c -> p t c", p=128), w=[vst])
            k.S.op('pool', lambda: nc.gpsimd.memset(VS[:, :, :, 64:65], 1.0), [], [VS])
            k.S.op('pool', lambda: nc.gpsimd.memset(VW[:, :, :, 64:65], 1.0), [], [VW])
            k.copy('pool', VS[:, :, :, 0:64], vst[:, :, 0:128].rearrange("p t (g d) -> p t g d", g=2), r=[vst], w=[VS])
            k.copy('pool', VW[:, :, :, 0:64], vst[:, :, 128:256].rearrange("p t (g d) -> p t g d", g=2), r=[vst], w=[VW])
        dbg(k, 2)
        with Scope(k) as s2:
            KC = s2.sb("KC", [128, S_LEN], BF16)
            VC = s2.sb("VC", [128, S_LEN], BF16)
            k.dma('sp', KC[:], k.QKT[1152:1280, :], w=[KC])
            k.dma('sp', VC[:], k.QKT[1280:1408, :], w=[VC])
            w1s = s2.sb("w1s", [128, 16, 128], F32)
            w1b = [s2.sb("w1b%d" % i, [128, 32, 128], BF16) for i in range(2)]
            w2s = s2.sb("w2s", [128, 2, 64], F32)
            w2b = s2.sb("w2b", [128, 2, 64], BF16)
            pes = s2.sb("pes", [128, 2, 32], F32)
            peb = s2.sb("peb", [128, 2, 32], BF16)
            hb = s2.sb("hb", [128, 2], F32)
            gx = s2.sb("gx", [128, 256], F32)
            gu = s2.sb("gu", [128, 256], F32)
            gg = s2.sb("gg", [128, 256], BF16)
            psh = s2.ps("psh", [128, 512])
            psb_ = s2.ps("pshb", [128, 512])
            pso = s2.ps("pso", [128, 512])
            for kv, (w1d, w2d, ped) in enumerate(((k.ck_w1, k.ck_w2, k.pe_kT), (k.cv_w1, k.cv_w2, k.pe_vT))):
                for lh in range(2):
                    for half in range(2):
                        k.dma('sp', w1s[half * 64:(half + 1) * 64, :, :],
                              w1d[l, lh * 1024:(lh + 1) * 1024, :].rearrange("(l d) h -> d l h", d=64), w=[w1s])
                    k.copy('pool', w1b[kv][:, lh * 16:(lh + 1) * 16, :], w1s[:], r=[w1s], w=[w1b[kv]])
                for half in range(2):
                    k.dma('sp', pes[half * 64:(half + 1) * 64, kv, :], ped[l], w=[pes])
                k.dma('sp', w2s[:, kv, :], w2d[l], w=[w2s])
            k.copy('dve', w2b[:], w2s[:], r=[w2s], w=[w2b])
            w2kd = s2.sb("w2kd", [128, 2, 64], BF16)
            for a_ in range(2):
                k.copy('dve', w2kd[:, a_, :], w2s[:, 0, :], r=[w2s], w=[w2kd])
            k.copy('dve', peb[:], pes[:], r=[pes], w=[peb])
            for kv in range(2):
                src = KC if kv == 0 else VC
                k.mm(psb_[:, kv:kv + 1], [(w1b[kv][0:64, li, :], peb[0:64, kv, li:li + 1]) for li in range(32)],
                     r=[w1b[kv], peb], w=[psb_], start=True)
                k.copy('dve', hb[:, kv:kv + 1], psb_[:, kv:kv + 1], r=[psb_], w=[hb])
                for g in range(2):
                    pr = slice(g * 64, (g + 1) * 64)
                    k.mm(psh[:, 0:255], [(w1b[kv][pr, li, :], src[pr, li:li + 16 * 254 + 1:16]) for li in range(32)],
                         r=[w1b[kv], src], w=[psh])
                    k.ts('dve', gx[:, 0:255], psh[:, 0:255], hb[:, kv:kv + 1], None, ALU.add, None, r=[psh, hb], w=[gx])
                    k.tt('dve', gu[:, 0:255], gx[:, 0:255], gx[:, 0:255], ALU.mult, r=[gx], w=[gu])
                    k.ts('dve', gu[:, 0:255], gu[:, 0:255], 0.044715, 1.0, ALU.mult, ALU.add, r=[gu], w=[gu])
                    k.tt('dve', gu[:, 0:255], gu[:, 0:255], gx[:, 0:255], ALU.mult, r=[gu, gx], w=[gu])
                    k.act(gu[:, 0:255], gu[:, 0:255], AF.Exp, r=[gu], w=[gu], scale=-2.0 * 0.7978845608028654)
                    k.ts('dve', gu[:, 0:255], gu[:, 0:255], 1.0, None, ALU.add, None, r=[gu], w=[gu])
                    k.S.op('dve', lambda: nc.vector.reciprocal(out=gu[:, 0:255], in_=gu[:, 0:255]), [gu], [gu])
                    k.S.op('dve', lambda: nc.vector.memset(gg[:, 255:256], 0.0), [], [gg])
                    k.tt('dve', gg[:, 0:255], gu[:, 0:255], gx[:, 0:255], ALU.mult, r=[gu, gx], w=[gg])
                    if kv == 0:
                        k.mm(pso[:, 0:256], [(w2kd[:].rearrange("p a d -> p (a d)"), gg[:, 0:256])], r=[w2kd, gg], w=[pso])
                        k.copy('dve', KCMP[pr, :], pso[pr, 0:256], r=[pso], w=[KCMP])
                    else:
                        for ct in range(2):
                            k.mm(pso[:, ct * 64:(ct + 1) * 64], [(gg[:, ct * 128:(ct + 1) * 128], w2b[:, 1, :])],
                                 r=[w2b, gg], w=[pso], start=(ct == 0))
                        k.copy('dve', VE[:, g, :, 0:64], pso[:, 0:128].rearrange("p (c d) -> p c d", c=2), r=[pso], w=[VE])
        dbg(k, 3)
        PTl = [sc.sb("PTn%d" % i, [128, 512], BF16) for i in range(6)]
        yacc = [sc.sb("yacc%d" % i, [128, 4, 384], F32) for i in range(2)]
        ybf = [sc.sb("ybf%d" % i, [128, 4, 384], BF16) for i in range(2)]
        impt2 = [[sc.sb("impt%d_%d" % (i, g), [128, 4, 64], F32) for g in range(2)] for i in range(2)]
        scr = sc.sb("scr", [128, 4, 64], F32)
        wk = sc.sb("wk", [128, 4, 64], F32)
        m8 = sc.sb("m8", [128, 4, 16], F32)
        nmq = sc.sb("nmq", [128, 4, 128], BF16)
        rcs = [sc.sb("rcs%d" % i, [128, 8], F32) for i in range(3)]
        psS = [sc.ps("psS%d" % i, [128, 512]) for i in range(4)]
        psO = [sc.ps("psO%d" % i, [128, 512]) for i in range(3)]
        psT = sc.ps("psTn", [128, 1024], BF16)
        st = {"S": 0, "O": 0, "P": 0, "R": 0}

        def q_ap(h, c0, c1):
            g, hp = h // 3, h % 3
            return QN[h][g * 64:(g + 1) * 64, c0:c1]

        def evac(views, h, branch, qb, ya, first):
            r_ = rcs[st["R"] % 3]; st["R"] += 1
            for qs, (po, cb) in enumerate(views):
                if branch == 0:
                    k.ts('dve', r_[:, qs:qs + 1], po[:, cb + 64:cb + 65], 1e-30, None, ALU.max, None, r=[po], w=[r_])
                    k.S.op('dve', lambda r_=r_, qs=qs: nc.vector.reciprocal(out=r_[:, qs:qs + 1], in_=r_[:, qs:qs + 1]), [r_], [r_])
                else:
                    k.S.op('dve', lambda r_=r_, po=po, cb=cb, qs=qs: nc.vector.reciprocal(out=r_[:, qs:qs + 1], in_=po[:, cb + 64:cb + 65]), [po], [r_])
            k.tt('dve', r_[:, 4:8], r_[:, 0:4], sg[:, qb * 4:(qb + 1) * 4, h * 3 + branch], ALU.mult, r=[r_, sg], w=[r_])
            for qs, (po, cb) in enumerate(views):
                o = ya[:, qs, h * 64:(h + 1) * 64]
                if first:
                    k.ts('dve', o, po[:, cb:cb + 64], r_[:, 4 + qs:5 + qs], None, ALU.mult, None, r=[po, r_], w=[ya])
                else:
                    k.stt('dve', o, po[:, cb:cb + 64], r_[:, 4 + qs:5 + qs], o, ALU.mult, ALU.add, r=[po, r_, ya], w=[ya])
            return r_

        pipe = Pipe(3)

        def attend(h, qb, tiles, kmat, vmat, po, g, branch, ya, merged=False):
            hp = h % 3
            nt = len(tiles)
            state = {"first": True}
            for idx, (kt, c0, c1, extra) in enumerate(tiles):
                ps = psS[st["S"] % len(psS)]; st["S"] += 1
                pt = PTl[st["P"] % len(PTl)]; st["P"] += 1

                def first(ps=ps, pt=pt, kt=kt, c0=c0, c1=c1, extra=extra):
                    if merged:
                        fns = [lambda: nc.tensor.matmul(ps[:, c0:c1], lhsT=kmat[:, kt * 128:(kt + 1) * 128],
                                                        rhs=QN[h][:, qb * 512 + c0:qb * 512 + c1],
                                                        start=True, stop=(len(extra) == 0), skip_group_check=True)]
                    else:
                        fns = [lambda: nc.tensor.matmul(ps[:, c0:c1], lhsT=kmat[g * 64:(g + 1) * 64, kt * 128:(kt + 1) * 128],
                                                        rhs=QN[h][g * 64:(g + 1) * 64, qb * 512 + c0:qb * 512 + c1],
                                                        start=True, stop=(len(extra) == 0), skip_group_check=True)]
                    rd = [kmat, QN[h]]
                    for ei, (lt, rt, lap, rap) in enumerate(extra):
                        w_ = rap.shape[-1]
                        fns.append(lambda lap=lap, rap=rap, w_=w_, ei=ei: nc.tensor.matmul(
                            ps[:, c0:c0 + w_], lhsT=lap, rhs=rap, start=False, stop=(ei == len(extra) - 1), skip_group_check=True))
                        rd += [lt, rt]
                    k.S.pe_group(fns, rd, [ps])
                    k.act(pt[:, c0:c1], ps[:, c0:c1], AF.Exp, r=[ps], w=[pt], scale=0.125)

                def second(pt=pt, kt=kt, c0=c0, c1=c1, idx=idx):
                    fns = []
                    for qs in range(c0 // 128, (c1 + 127) // 128):
                        last = all(not (t2[1] <= qs * 128 < t2[2]) for t2 in tiles[idx + 1:])
                        fo = state["first"]
                        state["first"] = False
                        fns.append(lambda qs=qs, fo=fo, last=last: nc.tensor.matmul(
                            po[:, qs * 65:(qs + 1) * 65], lhsT=pt[:, qs * 128:(qs + 1) * 128], rhs=vmat[:, kt, g, :],
                            start=fo, stop=last, skip_group_check=True))
                    k.S.pe_group(fns, [pt, vmat], [po])
                    if idx == nt - 1:
                        evac([(po, qs * 65) for qs in range(4)], h, branch, qb, ya, False)
                pipe.push(first, second)

        def do_cmp(qb):
            ya = yacc[qb % 2]
            impt = impt2[qb % 2]
            for h in range(6):
                g = h // 3
                poA = psO[st["O"] % 3]; st["O"] += 1
                poB = psO[st["O"] % 3]; st["O"] += 1
                cts = [0] + ([1] if qb >= 4 else [])
                state = {"A": True, "B": True}
                for ct in cts:
                    delta = 512 * qb - 2048 * ct
                    ps = psS[st["S"] % len(psS)]; st["S"] += 1
                    pt = PTl[st["P"] % len(PTl)]; st["P"] += 1

                    def first(ps=ps, pt=pt, ct=ct, delta=delta, g=g, h=h):
                        pairs = [(KCMP[g * 64:(g + 1) * 64, ct * 128:(ct + 1) * 128], q_ap(h, qb * 512, (qb + 1) * 512))]
                        rd = [KCMP, QN[h]]
                        if delta < 2560:
                            pairs.append((k.ident_bf[:], Gc[:, h, delta:delta + 512])); rd += [k.ident_bf, Gc]
                        k.mm(ps[:], pairs, r=rd, w=[ps])
                        k.act(pt[:], ps[:], AF.Exp, r=[ps], w=[pt], scale=0.125)

                    def second(pt=pt, ct=ct, g=g, h=h, poA=poA, poB=poB, state=state, lastct=(ct == cts[-1])):
                        fns = []
                        for qs in range(4):
                            po, cb = (poA, qs * 129) if qs < 3 else (poB, 0)
                            key = "A" if qs < 3 else "B"
                            stt_ = state[key]
                            state[key] = False
                            fns.append(lambda qs=qs, po=po, cb=cb, stt_=stt_: nc.tensor.matmul(
                                po[:, cb:cb + 129], lhsT=pt[:, qs * 128:(qs + 1) * 128], rhs=VE[:, g, ct, :],
                                start=stt_, stop=lastct, skip_group_check=True))
                        k.S.pe_group(fns, [pt, VE], [poA, poB])
                        if lastct:
                            views = [(poA, 0), (poA, 129), (poA, 258), (poB, 0)]
                            r_ = evac(views, h, 0, qb, ya, True)
                            for qs, (po, cb) in enumerate(views):
                                o = impt[g][:, qs, :]
                                if h % 3 == 0:
                                    k.ts('dve', o, po[:, cb + 65:cb + 129], r_[:, qs:qs + 1], None, ALU.mult, None, r=[po, r_], w=[impt[g]])
                                else:
                                    k.stt('dve', o, po[:, cb + 65:cb + 129], r_[:, qs:qs + 1], o, ALU.mult, ALU.add, r=[po, r_, impt[g]], w=[impt[g]])
                    pipe.push(first, second)

        def do_topk(qb):
            impt = impt2[qb % 2]
            for g in range(2):
                k.tt('dve', scr[:], impt[g][:], tkm[:, qb * 4:(qb + 1) * 4, :], ALU.mult, r=[impt[g], tkm], w=[scr])
                k.tt('dve', scr[:], scr[:], tka[:, qb * 4:(qb + 1) * 4, :], ALU.add, r=[scr, tka], w=[scr])
                for qs in range(4):
                    k.S.op('dve', lambda qs=qs: nc.vector.max(out=m8[:, qs, 0:8], in_=scr[:, qs, :]), [scr], [m8])
                    k.S.op('dve', lambda qs=qs: nc.vector.match_replace(out=wk[:, qs, :], in_to_replace=m8[:, qs, 0:8],
                                                                        in_values=scr[:, qs, :], imm_value=-1e9), [scr, m8], [wk])
                    k.S.op('dve', lambda qs=qs: nc.vector.max(out=m8[:, qs, 8:16], in_=wk[:, qs, :]), [wk], [m8])
                    k.ts('dve', wk[:, qs, :], scr[:, qs, :], m8[:, qs, 15:16], 1.0, ALU.is_ge, ALU.subtract, r=[scr, m8, wk], w=[wk])
                k.ts('dve', nmq[:, :, 0:64], wk[:], -NEG8, None, ALU.mult, None, r=[wk], w=[nmq])
                k.ts('pool', nmq[:, :, 64:128], wk[:], -NEG8, None, ALU.mult, None, r=[wk], w=[nmq])
                for qs in range(4):
                    k.transpose(psT[:, qs * 128:(qs + 1) * 128], nmq[:, qs, :], k.ident_bf[:], r=[nmq, k.ident_bf], w=[psT])
                oh = (1 - g) * 64
                for hh in range(3 * g, 3 * g + 3):
                    k.copy('act' if hh % 2 else 'dve', QN[hh][oh:oh + 64, qb * 512:(qb + 1) * 512], psT[oh:oh + 64, 0:512], r=[psT], w=[QN[hh]])

        def do_win(qb):
            ya = yacc[qb % 2]
            for h in range(6):
                g = h // 3
                po = psO[st["O"] % 3]; st["O"] += 1
                tiles = []
                for kt in range(max(0, 4 * qb - 4), 4 * qb + 4):
                    delta = 512 * qb - 128 * kt
                    c0 = max(-delta, 0)
                    c1 = min(512, 640 - delta) if delta > 0 else 512
                    tiles.append((kt, c0, c1, [(k.ident_bf, Gw, k.ident_bf[:], Gw[:, h, delta + 384 + c0:delta + 384 + c1])]))
                attend(h, qb, tiles, KW, VW, po, g, 2, ya)

        def do_slc(qb):
            ya = yacc[qb % 2]
            for h in range(6):
                g = h // 3
                po = psO[st["O"] % 3]; st["O"] += 1
                tiles = []
                for kt in range(0, 4 * qb + 4):
                    delta = 512 * qb - 128 * kt
                    c0 = max(-delta, 0)
                    ex = []
                    if delta <= 128:
                        c1b = 256 if delta == 128 else 512
                        ex.append((k.ident_bf, Gw, k.ident_bf[:], Gw[:, h, delta + 384 + c0:delta + 384 + c1b]))
                    tiles.append((kt, c0, 512, ex))
                attend(h, qb, tiles, KE[g], VS, po, g, 1, ya, merged=True)

        qbs = list(getattr(k, 'dbg_qbs', range(NTB)))
        do_cmp(qbs[0])
        pipe.flush()
        do_topk(qbs[0])
        for i, qb in enumerate(qbs):
            do_win(qb)
            if i + 1 < len(qbs):
                do_cmp(qbs[i + 1])
                pipe.flush()
                do_topk(qbs[i + 1])
            do_slc(qb)
            pipe.flush()
            ya = yacc[qb % 2]
            yb_ = ybf[qb % 2]
            k.copy('pool', yb_[:], ya[:], r=[ya], w=[yb_])
            k.dma('sp', k.Y[qb * 512:(qb + 1) * 512, 640:1024].rearrange("(q p) c -> p q c", p=128), yb_[:], r=[yb_])


def setup_ffn(k):
    k.w_out = k.inp("w_out", [DEPTH, D, D])
    k.ffn_up = k.inp("ffn_up", [DEPTH, D, 2 * D_FF])
    k.ffn_down = k.inp("ffn_down", [DEPTH, D_FF, D])
    k.conv_w = k.inp("conv_w_fm", [DEPTH, 128, 3, 44])
    k.conv_b = k.inp("conv_b_fm", [DEPTH, 128, 44])


def load_cast_gen(k, stg, dst, src_rows, ncols, nchunks, col_split=1):
    w = ncols // col_split
    n = 0
    for c in range(nchunks):
        for cs in range(col_split):
            s = stg[n % len(stg)]
            k.dma('sp' if n % 2 == 0 else 'act', s[:, 0:w], src_rows(c)[:, cs * w:(cs + 1) * w], w=[s])
            k.copy('pool' if n % 2 == 0 else 'dve', dst[:, c, cs * w:(cs + 1) * w], s[:, 0:w], r=[s], w=[dst])
            n += 1
            yield


def load_cast(k, sc, dst, src_rows, ncols, nchunks, name, col_split=1):
    w = ncols // col_split
    stg = [sc.sb("%s_stg%d" % (name, i), [128, w], F32) for i in range(4)]
    for _ in load_cast_gen(k, stg, dst, src_rows, ncols, nchunks, col_split):
        pass


def rms_scale(k, ss, st):
    k.act(st[:, 0:1], ss, AF.Ln, r=[st], w=[st], bias=RMS_EPS)
    k.act(st[:, 1:2], st[:, 0:1], AF.Exp, r=[st], w=[st], scale=-0.5)


def stage_out(k, l, xsrc, xdst, bg=None, bg_steps=2):
    nc = k.nc
    with Scope(k) as sc:
        wo = sc.sb("wo", [128, 8, D], BF16)
        with Scope(k) as s2:
            load_cast(k, s2, wo, lambda c: k.w_out[l, c * 128:(c + 1) * 128, :], D, 8, "wo")
        yt = [sc.sb("yt%d" % i, [128, D], BF16) for i in range(2)]
        yT = [sc.sb("yT%d" % i, [128, 8, 128], BF16) for i in range(2)]
        xt = [sc.sb("xo%d" % i, [128, D], F32) for i in range(2)]
        tt_ = [sc.sb("to%d" % i, [128, D], F32) for i in range(2)]
        junk = sc.sb("junko", [128, 512], BF16)
        st = [sc.sb("sto%d" % i, [128, 4], F32) for i in range(2)]
        psT = [sc.ps("psTo%d" % i, [128, D], BF16) for i in range(2)]
        psY = [sc.ps("psYo%d" % i, [128, 512]) for i in range(4)]
        def T(ti):
            tok = slice(ti * 128, (ti + 1) * 128)
            y_ = yt[ti % 2]; yT_ = yT[ti % 2]; x_ = xt[ti % 2]; pT = psT[ti % 2]
            k.dma('act', y_[:], k.Y[tok, :], w=[y_])
            k.dma('act', x_[:], xsrc[tok, :], w=[x_])
            for kc in range(8):
                k.transpose(pT[:, kc * 128:(kc + 1) * 128], y_[:, kc * 128:(kc + 1) * 128], k.ident_bf[:], r=[y_, k.ident_bf], w=[pT])
            k.copy('act' if ti % 2 else 'dve', yT_[:].rearrange("p a b -> p (a b)"), pT[:], r=[pT], w=[yT_])

        def M(ti):
            tok = slice(ti * 128, (ti + 1) * 128)
            yT_ = yT[ti % 2]; x_ = xt[ti % 2]; t_ = tt_[ti % 2]; st_ = st[ti % 2]
            p0 = psY[(ti % 2) * 2]; p1 = psY[(ti % 2) * 2 + 1]
            for half, ps in enumerate((p0, p1)):
                k.mm(ps[:], [(yT_[:, kc, :], wo[:, kc, half * 512:(half + 1) * 512]) for kc in range(8)], r=[yT_, wo], w=[ps])
                k.act(junk[:], ps[:], AF.Square, r=[ps], w=[junk, st_], scale=1.0 / 32.0, accum=st_[:, 2 + half:3 + half])
            k.tt('dve', st_[:, 2:3], st_[:, 2:3], st_[:, 3:4], ALU.add, r=[st_], w=[st_])
            rms_scale(k, st_[:, 2:3], st_)
            for half, ps in enumerate((p0, p1)):
                cs = slice(half * 512, (half + 1) * 512)
                k.stt('dve', t_[:, cs], ps[:], st_[:, 1:2], k.gm_row[:, cs], ALU.mult, ALU.mult, r=[ps, st_, k.gm_row], w=[t_])
            k.tt('pool', t_[:], t_[:], x_[:], ALU.add, r=[t_, x_], w=[t_])
            k.dma('sp', xdst[tok, :], t_[:], r=[t_])

        T(0)
        for ti in range(32):
            if ti + 1 < 32:
                T(ti + 1)
            M(ti)
            for _ in range(bg_steps):
                if bg is not None:
                    try:
                        next(bg)
                    except StopIteration:
                        bg = None
        if bg is not None:
            for _ in bg:
                pass


def stage_out_ffn(k, l, xin, xmid, xdst):
    NCH = 22
    with Scope(k) as sc:
        wu = sc.sb("wu", [128, 8, 2 * D_FF], BF16)
        wd = sc.sb("wd", [128, NCH, D], BF16)
        with Scope(k) as s2:
            stg = [s2.sb("wstg%d" % i, [128, 1408], F32) for i in range(4)]

            def bg():
                yield from load_cast_gen(k, stg, wu, lambda c: k.ffn_up[l, c * 128:(c + 1) * 128, :], 2 * D_FF, 8, col_split=4)
                yield from load_cast_gen(k, stg, wd, lambda c: k.ffn_down[l, c * 128:(c + 1) * 128, :], D, NCH)
            stage_out(k, l, xin, xmid, bg=bg(), bg_steps=2)
        stage_ffn(k, l, xmid, xdst, pre=(sc, wu, wd))


def stage_ffn(k, l, xsrc, xdst, pre=None):
    nc = k.nc
    NCH = 22
    with ExitStack() as es_:
        if pre is None:
            sc = es_.enter_context(Scope(k))
            wu = sc.sb("wu", [128, 8, 2 * D_FF], BF16)
            wd = sc.sb("wd", [128, NCH, D], BF16)
            with Scope(k) as s2:
                load_cast(k, s2, wu, lambda c: k.ffn_up[l, c * 128:(c + 1) * 128, :], 2 * D_FF, 8, "wu", col_split=2)
                load_cast(k, s2, wd, lambda c: k.ffn_down[l, c * 128:(c + 1) * 128, :], D, NCH, "wd")
        else:
            sc, wu, wd = pre
        cw = sc.sb("cw", [128, 3, 44], F32)
        cb = sc.sb("cb", [128, 44], F32)
        hal = [sc.sb("hal%d" % i, [128, 44, 2], F32) for i in range(2)]
        k.dma('sp', cw[:], k.conv_w[l], w=[cw])
        k.dma('sp', cb[:], k.conv_b[l], w=[cb])
        k.S.op('pool', lambda: nc.gpsimd.memset(hal[1][:], 0.0), [], [hal[1]])
        actT = sc.sb("actT", [128, NCH, 512], BF16)
        HTs = [sc.sb("H2T%d" % i, [128, 8, 512], BF16) for i in range(2)]
        xt = [sc.sb("xf%d" % i, [128, D], F32) for i in range(2)]
        xn = sc.sb("xnf", [128, D], BF16)
        junk = sc.sb("junkf", [128, 512], BF16)
        st = [sc.sb("stf%d" % i, [128, 4], F32) for i in range(2)]
        Tg = [sc.sb("Tg%d" % i, [128, 512], F32) for i in range(2)]
        Tv = [sc.sb("Tv%d" % i, [128, 512], F32) for i in range(2)]
        psT = sc.ps("psTf", [128, D], BF16)
        psU = [sc.ps("psU%d" % i, [128, 512]) for i in range(4)]
        nxc = {"n": 0}

        def norm_gen(tb):
            HT = HTs[tb % 2]
            t0_ = tb * 4
            k.dma('act', xt[t0_ % 2][:], xsrc[t0_ * 128:(t0_ + 1) * 128, :], w=[xt[t0_ % 2]])
            yield
            for sub in range(4):
                ti = tb * 4 + sub
                x_ = xt[ti % 2]; st_ = st[ti % 2]
                k.act(xn[:], x_[:], AF.Square, r=[x_], w=[xn, st_], scale=1.0 / 32.0, accum=st_[:, 2:3])
                rms_scale(k, st_[:, 2:3], st_)
                k.ts('dve', xn[:], x_[:], st_[:, 1:2], None, ALU.mult, None, r=[x_, st_], w=[xn])
                if sub < 3:
                    k.dma('act', xt[(ti + 1) % 2][:], xsrc[(ti + 1) * 128:(ti + 2) * 128, :], w=[xt[(ti + 1) % 2]])
                yield
                yield
                for kc in range(8):
                    k.transpose(psT[:, kc * 128:(kc + 1) * 128], xn[:, kc * 128:(kc + 1) * 128], k.ident_bf[:], r=[xn, k.ident_bf], w=[psT])
                    if kc == 3:
                        yield
                yield
                for kc in range(8):
                    o = HT[:, kc, sub * 128:(sub + 1) * 128]
                    i_ = psT[:, kc * 128:(kc + 1) * 128]
                    if kc % 2 == 0:
                        k.ts('dve', o, i_, k.modAB[:, 16 + kc:17 + kc], k.modAB[:, 24 + kc:25 + kc], ALU.mult, ALU.add, r=[psT, k.modAB], w=[HT])
                    else:
                        k.act(o, i_, AF.Identity, r=[psT, k.modAB], w=[HT], scale=k.modAB[:, 16 + kc:17 + kc], bias=k.modAB[:, 24 + kc:25 + kc])
                yield

        def step(g):
            if g is not None:
                try:
                    next(g)
                except StopIteration:
                    return None
            return g

        xe = [sc.sb("xe%d" % i, [128, D], F32) for i in range(2)]
        ste = [sc.sb("ste%d" % i, [128, 4], F32) for i in range(2)]
        for _ in norm_gen(0):
            pass
        for tb in range(NTB):
            hin = hal[(tb + 1) % 2]; hout = hal[tb % 2]
            HT = HTs[tb % 2]
            g = norm_gen(tb + 1) if tb + 1 < NTB else None
            for cp in range(NCH):
                tg = Tg[cp % 2]; tv = Tv[cp % 2]
                for which, (T_, c_) in enumerate(((tg, cp), (tv, NCH + cp))):
                    ps = psU[(cp * 2 + which) % 4]
                    k.mm(ps[:], [(wu[:, kc, c_ * 128:(c_ + 1) * 128], HT[:, kc, :]) for kc in range(8)], r=[wu, HT], w=[ps])
                    k.act(T_[:], ps[:], AF.Identity, r=[ps, cw, cb], w=[T_], scale=cw[:, 2, c_:c_ + 1], bias=cb[:, c_:c_ + 1])
                    k.stt('dve', T_[:, 1:512], ps[:, 0:511], cw[:, 1, c_:c_ + 1], T_[:, 1:512], ALU.mult, ALU.add, r=[ps, cw, T_], w=[T_])
                    k.stt('dve', T_[:, 2:512], ps[:, 0:510], cw[:, 0, c_:c_ + 1], T_[:, 2:512], ALU.mult, ALU.add, r=[ps, cw, T_], w=[T_])
                    k.copy('act', hout[:, c_, :], ps[:, 510:512], r=[ps], w=[hout])
                    k.stt('dve', T_[:, 0:1], hin[:, c_, 1:2], cw[:, 1, c_:c_ + 1], T_[:, 0:1], ALU.mult, ALU.add, r=[hin, cw, T_], w=[T_])
                    k.stt('dve', T_[:, 0:2], hin[:, c_, 0:2], cw[:, 0, c_:c_ + 1], T_[:, 0:2], ALU.mult, ALU.add, r=[hin, cw, T_], w=[T_])
                k.act(tg[:], tg[:], AF.Silu, r=[tg], w=[tg])
                k.tt('pool', actT[:, cp, :], tg[:], tv[:], ALU.mult, r=[tg, tv], w=[actT])
                if cp >= 1:
                    g = step(g)
            for sub in range(4):
                ti = tb * 4 + sub
                tok = slice(ti * 128, (ti + 1) * 128)
                x_ = xe[sub % 2]; st_ = ste[sub % 2]
                k.dma('act', x_[:], xsrc[tok, :], w=[x_])
                psF = [psU[(2 * sub) % 4], psU[(2 * sub + 1) % 4]]
                for half in range(2):
                    ps = psF[half]
                    k.mm(ps[:], [(actT[:, cp, sub * 128:(sub + 1) * 128], wd[:, cp, half * 512:(half + 1) * 512]) for cp in range(NCH)],
                         r=[actT, wd], w=[ps])
                    k.act(junk[:, 0:512], ps[:], AF.Square, r=[ps], w=[junk, st_], scale=1.0 / 32.0, accum=st_[:, 2 + half:3 + half])
                k.tt('dve', st_[:, 2:3], st_[:, 2:3], st_[:, 3:4], ALU.add, r=[st_], w=[st_])
                rms_scale(k, st_[:, 2:3], st_)
                t_ = Tg[sub % 2] if False else None
                for half in range(2):
                    cs = slice(half * 512, (half + 1) * 512)
                    T_ = (Tg if half == 0 else Tv)[sub % 2]
                    k.stt('dve', T_[:], psF[half][:], st_[:, 1:2], k.gf_row[:, cs], ALU.mult, ALU.mult, r=[psF[half], st_, k.gf_row], w=[T_])
                    k.tt('pool', x_[:, cs], x_[:, cs], T_[:], ALU.add, r=[x_, T_], w=[x_])
                k.dma('sp', xdst[tok, :], x_[:], r=[x_])
                g = step(g)
            while g is not None:
                g = step(g)


def rwkv_host(inp):
    f = lambda a: np.ascontiguousarray(np.asarray(a, dtype=np.float32))
    mu = np.asarray(inp["rwkv_mu"])
    hd = lambda v: np.asarray(v).reshape(DEPTH, 4, 64).transpose(0, 2, 1)
    pp = np.stack([hd(mu[:, 0:256]), hd(mu[:, 256:512]), hd(mu[:, 512:768]), hd(inp["rwkv_w0"]), hd(inp["rwkv_a0"]),
                   hd(inp["rwkv_k_k"]), hd(inp["rwkv_k_a"]), hd(np.asarray(inp["rwkv_r_k"]).reshape(DEPTH, 256))], axis=2)
    lr = np.zeros((DEPTH, 64, 3), np.float32)
    lr[:, 0:32, 0] = mu[:, 768:800]; lr[:, 0:32, 1] = mu[:, 800:832]; lr[:, :, 2] = mu[:, 832:896]
    i = np.arange(64)
    mk = np.stack([(i[:, None] < i[None, :]), (i[:, None] > i[None, :]), (i[:, None] <= i[None, :]), np.eye(64, dtype=bool)]).astype(np.float32)
    cm = np.ones((64, 512), np.float32); cm[:, ::64] = 0.0
    return {"rwkv_pp": f(pp), "rwkv_lr": f(lr), "rwkv_w_up": f(inp["rwkv_w_up"]), "rwkv_a_up": f(inp["rwkv_a_up"]),
            "rwkv_g_up": f(inp["rwkv_g_up"]), "rwkv_ln": f(np.stack([np.asarray(inp["rwkv_ln_w"]), np.asarray(inp["rwkv_ln_b"])], axis=1)),
            "rwkv_masks": f(mk.transpose(1, 0, 2)), "rwkv_cmask": cm}


def setup_rwkv(k):
    k.rw_pp = k.inp("rwkv_pp", [DEPTH, 64, 8, 4])
    k.rw_lr = k.inp("rwkv_lr", [DEPTH, 64, 3])
    k.rw_wup = k.inp("rwkv_w_up", [DEPTH, 32, 256])
    k.rw_aup = k.inp("rwkv_a_up", [DEPTH, 32, 256])
    k.rw_gup = k.inp("rwkv_g_up", [DEPTH, 64, 256])
    k.rw_ln = k.inp("rwkv_ln", [DEPTH, 2, 256])
    k.rw_masks = k.inp("rwkv_masks", [64, 4, 64])
    k.rw_cmask = k.inp("rwkv_cmask", [64, 512])


def stage_rwkv(k, l):
    nc = k.nc
    BL = 256
    NB = S_LEN // BL
    CPB = BL // 64
    H4 = [64, 4, BL]
    bc = lambda ap, shape: ap.to_broadcast(shape)
    with Scope(k) as sc:
        pp = sc.sb("pp", [64, 8, 4], F32)
        lr = sc.sb("lr", [64, 3], F32)
        wup = sc.sb("wup", [32, 256], F32); aup = sc.sb("aup", [32, 256], F32); gup = sc.sb("gup", [64, 256], F32)
        lnr = sc.sb("lnr", [64, 2, 256], F32)
        mk = sc.sb("mk", [64, 4, 64], F32)
        cmask = sc.sb("cmask", [64, BL], F32)
        ones = sc.sb("ones64", [64, 64], F32)
        prm = sc.sb("prm", [64, 4, 4], F32)
        k.dma('sp', pp[:], k.rw_pp[l], w=[pp]); k.dma('sp', lr[:], k.rw_lr[l], w=[lr])
        k.dma('sp', wup[:], k.rw_wup[l], w=[wup]); k.dma('sp', aup[:], k.rw_aup[l], w=[aup]); k.dma('sp', gup[:], k.rw_gup[l], w=[gup])
        for i in range(2):
            k.dma('sp', lnr[:, i, :], k.rw_ln[l, i:i + 1, :].broadcast_to([64, 256]), w=[lnr])
        k.dma('sp', mk[:], k.rw_masks, w=[mk]); k.dma('sp', cmask[:], k.rw_cmask[:, 0:BL], w=[cmask])
        k.S.op('pool', lambda: nc.gpsimd.memset(ones[:], 1.0), [], [ones])
        k.ts('dve', prm[:, 0, :], pp[:, 3, :], -1.0, None, ALU.mult, None, r=[pp], w=[prm])
        k.ts('dve', prm[:, 1, :], pp[:, 6, :], -1.0, 1.0, ALU.mult, ALU.add, r=[pp], w=[prm])
        P3 = sc.sb("P3", [64, 3, 4, BL], F32)
        halo = sc.sb("halo", [64, 3, 4], F32)
        LR = sc.sb("LR", [64, 3, BL], F32)
        halo2 = sc.sb("halo2", [64, 3], F32)
        ELW = sc.sb("ELW", H4, F32); SC_ = sc.sb("SCAN", H4, F32); AA = sc.sb("AA", H4, F32); KKN = sc.sb("KKN", H4, F32)
        T1 = sc.sb("T1", H4, F32); T2 = sc.sb("T2", H4, F32); CM4 = sc.sb("CM4", H4, F32)
        OUT = [{nm: sc.sb("%s%d" % (nm, i), H4, F32 if nm == "GAM" else BF16) for nm in ("AT", "BT", "KT", "RT", "RK", "GAM", "V")} for i in range(2)]
        SGs = [sc.sb("SG%d" % i, [64, BL], BF16) for i in range(2)]
        gupb = sc.sb("gupb", [64, 256], BF16)
        ppb = sc.sb("ppb", [64, 4], BF16)
        identb64 = k.ident_bf
        XY = [[sc.sb("XY%d_%d" % (i, j), [64, 2, 4, 64], BF16) for j in range(2)] for i in range(2)]
        PP = [[sc.sb("PPi%d_%d" % (i, j), [64, 4, 64], BF16) for j in range(2)] for i in range(2)]
        AKRK = [sc.sb("AKRK%d" % i, [64, 2, 4, 64], BF16) for i in range(2)]
        RBT = [sc.sb("RBT%d" % i, [64, 4, 64], BF16) for i in range(2)]
        TOK = [sc.sb("TOK%d" % i, [64, 3, 4, 64], BF16) for i in range(2)]
        Wsb = sc.sb("Wsb", [64, 4, 64], BF16); Usb = sc.sb("Usb", [64, 4, 64], BF16)
        Hs = [sc.sb("Hs%d" % i, [64, 4, 64], F32) for i in range(2)]
        Hb = [sc.sb("Hb%d" % i, [64, 4, 64], BF16) for i in range(2)]
        yc = sc.sb("yc", [64, 4, 64], F32); ysq = sc.sb("ysq", [64, 4, 64], F32)
        sm = sc.sb("sm", [64, 6, 4], F32)
        yab = [sc.sb("yab%d" % i, [64, CPB, 256], BF16) for i in range(2)]
        psA1 = sc.ps("psA1", [64, 512]); psA2 = sc.ps("psA2", [64, 512]); psA3 = sc.ps("psA3", [64, 512]); psA4 = sc.ps("psA4", [64, 512])
        psH = sc.ps("psHr", [64, 512]); psY = sc.ps("psYr", [64, 512]); psC = sc.ps("psCr", [64, 512]); psQ = sc.ps("psQr", [64, 512])
        k.S.op('pool', lambda: nc.gpsimd.memset(Hs[1][:], 0.0), [], [Hs[1]])
        k.S.op('pool', lambda: nc.gpsimd.memset(Hb[1][:], 0.0), [], [Hb[1]])
        k.copy('dve', gupb[:], gup[:], r=[gup], w=[gupb])
        k.copy('dve', ppb[:], pp[:, 7, :], r=[pp], w=[ppb])
        k.S.op('pool', lambda: nc.gpsimd.memset(halo[:], 0.0), [], [halo])
        k.S.op('pool', lambda: nc.gpsimd.memset(halo2[:], 0.0), [], [halo2])
        k.copy('dve', CM4[:], bc(cmask[:].unsqueeze(1), H4), r=[cmask], w=[CM4])
        E_ = BL - 1

        def prep(tb):
            O = OUT[tb % 2]; SG = SGs[tb % 2]
            AT, BT, KT, RT, RK, GAM, V_ = O["AT"], O["BT"], O["KT"], O["RT"], O["RK"], O["GAM"], O["V"]
            t0 = tb * BL
            for q in range(3):
                k.dma('act', P3[:, q, :, :], k.PT[q * 256:(q + 1) * 256, t0:t0 + BL].rearrange("(h d) t -> d h t", d=64), w=[P3])
            k.dma('act', LR[0:32, 0, :], k.PT[768:800, t0:t0 + BL], w=[LR])
            k.dma('act', LR[0:32, 1, :], k.PT[800:832, t0:t0 + BL], w=[LR])
            k.dma('act', LR[:, 2, :], k.PT[832:896, t0:t0 + BL], w=[LR])
            yield
            for q in range(3):
                p_ = P3[:, q, :, :]
                k.tt('dve', T1[:, :, 1:BL], p_[:, :, 0:E_], p_[:, :, 1:BL], ALU.subtract, r=[P3], w=[T1])
                k.tt('dve', T1[:, :, 0:1], halo[:, q, :].unsqueeze(2), p_[:, :, 0:1], ALU.subtract, r=[P3, halo], w=[T1])
                k.copy('pool', halo[:, q, :].unsqueeze(2), p_[:, :, E_:BL], r=[P3, T1], w=[halo])
                k.tt('pool', T1[:], T1[:], bc(pp[:, q, :].unsqueeze(2), H4), ALU.mult, r=[T1, pp], w=[T1])
                if q < 2:
                    k.tt('pool', p_, p_, T1[:], ALU.add, r=[P3, T1, halo], w=[P3])
                else:
                    k.tt('pool', V_[:], p_, T1[:], ALU.add, r=[P3, T1, halo], w=[V_])
                yield
            for q, rows in ((0, 32), (1, 32), (2, 64)):
                x_ = LR[0:rows, q, :]
                t_ = T2[0:rows, 0, :]
                k.tt('dve', t_[:, 1:BL], x_[:, 0:E_], x_[:, 1:BL], ALU.subtract, r=[LR], w=[T2])
                k.tt('dve', t_[:, 0:1], halo2[0:rows, q:q + 1], x_[:, 0:1], ALU.subtract, r=[LR, halo2], w=[T2])
                k.copy('dve', halo2[0:rows, q:q + 1], x_[:, E_:BL], r=[LR, T2], w=[halo2])
                k.stt('dve', x_, t_, lr[0:rows, q:q + 1], x_, ALU.mult, ALU.add, r=[T2, lr, LR, halo2], w=[LR])
            yield
            R_ = P3[:, 0, :, :]; Kp = P3[:, 1, :, :]
            k.act(LR[0:32, 0, :], LR[0:32, 0, :], AF.Tanh, r=[LR], w=[LR])
            k.act(SG[:], LR[:, 2, :], AF.Sigmoid, r=[LR], w=[SG])
            for h in range(4):
                k.mm(psQ[:, 0:BL], [(wup[:, h * 64:(h + 1) * 64], LR[0:32, 0, :])], r=[wup, LR], w=[psQ])
                k.act(T1[:, h, :], psQ[:, 0:BL], AF.Exp, r=[psQ, prm], w=[T1], scale=-1.0, bias=prm[:, 0, h:h + 1])
                k.mm(psQ[:, BL:2 * BL], [(aup[:, h * 64:(h + 1) * 64], LR[0:32, 1, :])], r=[aup, LR], w=[psQ], start=False)
                k.act(AA[:, h, :], psQ[:, BL:2 * BL], AF.Sigmoid, r=[psQ, pp], w=[AA], bias=pp[:, 4, h:h + 1])
                yield
            k.act(T1[:], T1[:], AF.Ln, r=[T1], w=[T1], bias=1.0)
            k.act(ELW[:], T1[:], AF.Exp, r=[T1], w=[ELW], scale=-1.0, bias=-0.5)
            k.S.op('dve', lambda: nc.vector.tensor_tensor_scan(
                out=SC_[:].rearrange("p h t -> p (h t)"), data0=CM4[:].rearrange("p h t -> p (h t)"),
                data1=ELW[:].rearrange("p h t -> p (h t)"), initial=0.0, op0=ALU.mult, op1=ALU.add), [CM4, ELW], [SC_])
            yield
            k.tt('pool', KKN[:], Kp, bc(pp[:, 5, :].unsqueeze(2), H4), ALU.mult, r=[P3, pp], w=[KKN])
            k.tt('pool', T1[:], KKN[:], KKN[:], ALU.mult, r=[KKN], w=[T1])
            for h in range(4):
                k.mm(psQ[:, 0:BL], [(ones[:], T1[:, h, :])], r=[ones, T1], w=[psQ])
                k.act(T2[:, h, :], psQ[:, 0:BL], AF.Ln, r=[psQ], w=[T2], bias=1e-24)
                yield
            k.act(T2[:], T2[:], AF.Exp, r=[T2], w=[T2], scale=-0.5)
            k.tt('dve', KKN[:], KKN[:], T2[:], ALU.mult, r=[KKN, T2], w=[KKN])
            yield
            k.tt('pool', T1[:], SC_[:], ELW[:], ALU.subtract, r=[SC_, ELW], w=[T1])
            k.act(T1[:], T1[:], AF.Exp, r=[T1], w=[T1], scale=-1.0)
            k.stt('dve', AT[:], KKN[:], -1.0, T1[:], ALU.mult, ALU.mult, r=[KKN, T1], w=[AT])
            yield
            k.act(T2[:], SC_[:], AF.Exp, r=[SC_], w=[T2])
            k.tt('pool', T1[:], KKN[:], AA[:], ALU.mult, r=[KKN, AA], w=[T1])
            k.tt('dve', BT[:], T1[:], T2[:], ALU.mult, r=[T1, T2], w=[BT])
            yield
            k.tt('pool', T1[:], AA[:], bc(pp[:, 6, :].unsqueeze(2), H4), ALU.mult, r=[AA, pp], w=[T1])
            k.tt('pool', T1[:], T1[:], bc(prm[:, 1, :].unsqueeze(2), H4), ALU.add, r=[T1, prm], w=[T1])
            k.tt('dve', Kp, Kp, T1[:], ALU.mult, r=[P3, T1, KKN], w=[P3])
            yield
            k.tt('dve', KT[:], Kp, T2[:], ALU.mult, r=[P3, T2], w=[KT])
            k.tt('pool', RK[:], R_, Kp, ALU.mult, r=[P3], w=[RK])
            k.act(GAM[:], SC_[:], AF.Exp, r=[SC_], w=[GAM], scale=-1.0)
            k.tt('dve', RT[:], R_, GAM[:], ALU.mult, r=[P3, GAM], w=[RT])
            yield

        def phaseA(nch):
            tb, n = divmod(nch, CPB)
            O = OUT[tb % 2]
            AT, BT, KT, RT, V_ = O["AT"], O["BT"], O["KT"], O["RT"], O["V"]
            c_ = slice(n * 64, (n + 1) * 64)
            par = nch % 2
            xy = XY[par][0]; akrk = AKRK[par]; rbt = RBT[par]; tok = TOK[par]
            fns = []
            for h in range(4):
                fns.append(lambda h=h: nc.tensor.matmul(psA1[:, h * 64:(h + 1) * 64], lhsT=BT[:, h, c_], rhs=AT[:, h, c_], start=True, stop=True, skip_group_check=True))
                fns.append(lambda h=h: nc.tensor.matmul(psA1[:, 256 + h * 64:256 + (h + 1) * 64], lhsT=AT[:, h, c_], rhs=BT[:, h, c_], start=True, stop=True, skip_group_check=True))
            k.S.pe_group(fns, [AT, BT], [psA1])
            fns = []
            for h in range(4):
                fns.append(lambda h=h: nc.tensor.matmul(psA2[:, h * 64:(h + 1) * 64], lhsT=KT[:, h, c_], rhs=AT[:, h, c_], start=True, stop=True, skip_group_check=True))
                fns.append(lambda h=h: nc.tensor.matmul(psA2[:, 256 + h * 64:256 + (h + 1) * 64], lhsT=KT[:, h, c_], rhs=RT[:, h, c_], start=True, stop=True, skip_group_check=True))
            k.S.pe_group(fns, [AT, KT, RT], [psA2])
            k.S.pe_group([lambda h=h: nc.tensor.matmul(psA3[:, h * 64:(h + 1) * 64], lhsT=BT[:, h, c_], rhs=RT[:, h, c_], start=True, stop=True, skip_group_check=True)
                          for h in range(4)], [BT, RT], [psA3])
            yield
            v4 = lambda ps, a: ps[:, a * 256:(a + 1) * 256].rearrange("p (h f) -> p h f", h=4)
            mb = lambda i: bc(mk[:, i, :].unsqueeze(1), [64, 4, 64])
            k.tt('dve', xy[:, 0, :, :], v4(psA1, 0), mb(0), ALU.mult, r=[psA1, mk], w=[xy])
            k.tt('dve', xy[:, 1, :, :], v4(psA1, 1), mb(1), ALU.mult, r=[psA1, mk], w=[xy])
            k.tt('dve', akrk[:, 0, :, :], v4(psA2, 0), mb(0), ALU.mult, r=[psA2, mk], w=[akrk])
            k.tt('dve', akrk[:, 1, :, :], v4(psA2, 1), mb(2), ALU.mult, r=[psA2, mk], w=[akrk])
            k.tt('dve', rbt[:], v4(psA3, 0), mb(2), ALU.mult, r=[psA3, mk], w=[rbt])
            yield
            fns = []
            psA1b = psA1[:, :].bitcast(BF16)
            for qi, src_ in enumerate((V_, BT, KT)):
                for h in range(4):
                    dst = psA1b[:, qi * 256 + h * 64:qi * 256 + (h + 1) * 64]
                    fns.append(lambda dst=dst, s_=src_[:, h, c_]: nc.tensor.transpose(out=dst, in_=s_, identity=k.ident_bf[0:64, 0:64]))
            k.S.pe_group(fns, [V_, BT, KT, k.ident_bf], [psA1])
            yield
            k.copy('act', tok[:].rearrange("p a h f -> p (a h f)"), psA1b[:, 0:768], r=[psA1], w=[tok])
            P_ = PP[par][0]
            k.tt('dve', P_[:], xy[:, 0, :, :], mb(3), ALU.add, r=[xy, mk], w=[P_])
            yield
            Pm = None
            for lev in range(1, 7):
                xyn = XY[par][lev % 2]
                fns = []
                rd = [xy]
                wr = []
                if lev <= 5:
                    for h in range(4):
                        fns.append(lambda h=h, xy=xy: nc.tensor.matmul(psA4[:, 256 + h * 64:256 + (h + 1) * 64], lhsT=xy[:, 0, h, :], rhs=xy[:, 1, h, :], start=True, stop=True, skip_group_check=True))
                        if lev <= 4:
                            fns.append(lambda h=h, xy=xy: nc.tensor.matmul(psA4[:, h * 64:(h + 1) * 64], lhsT=xy[:, 1, h, :], rhs=xy[:, 0, h, :], start=True, stop=True, skip_group_check=True))
                    wr.append(psA4)
                if lev >= 2:
                    for h in range(4):
                        fns.append(lambda h=h, xy=xy, Pm=Pm: nc.tensor.matmul(psA3[:, 256 + h * 64:256 + (h + 1) * 64], lhsT=xy[:, 1, h, :], rhs=Pm[:, h, :], start=True, stop=True, skip_group_check=True))
                    rd.append(Pm); wr.append(psA3)
                k.S.pe_group(fns, rd, wr)
                yield
                if lev <= 4:
                    k.copy('act', xyn[:].rearrange("p a h f -> p (a h f)"), psA4[:, :], r=[psA4], w=[xyn])
                elif lev == 5:
                    k.copy('act', xyn[:, 1, :, :].rearrange("p h f -> p (h f)"), psA4[:, 256:512], r=[psA4], w=[xyn])
                if lev >= 2:
                    Pn = PP[par][(lev - 1) % 2]
                    k.tt('dve', Pn[:], Pm[:], v4(psA3, 1), ALU.add, r=[Pm, psA3], w=[Pn])
                    Pm = Pn
                else:
                    Pm = P_
                yield
                xy = xyn

        def phaseB(nch):
            tb, n = divmod(nch, CPB)
            O = OUT[tb % 2]; SG = SGs[tb % 2]
            AT, RT, RK, GAM = O["AT"], O["RT"], O["RK"], O["GAM"]
            c_ = slice(n * 64, (n + 1) * 64)
            par = nch % 2
            akrk = AKRK[par]; rbt = RBT[par]; tok = TOK[par]; TT = PP[par][1]
            Hold = Hs[(nch + 1) % 2]; Hnew = Hs[nch % 2]
            Hbo = Hb[(nch + 1) % 2]; Hbn = Hb[nch % 2]
            yab_ = yab[tb % 2]
            fns = []
            for h in range(4):
                fns.append(lambda h=h: nc.tensor.matmul(psH[:, h * 64:(h + 1) * 64], lhsT=AT[:, h, c_], rhs=Hbo[:, h, :], start=(h == 0), stop=False, skip_group_check=True))
                fns.append(lambda h=h: nc.tensor.matmul(psH[:, h * 64:(h + 1) * 64], lhsT=akrk[:, 0, h, :], rhs=tok[:, 0, h, :], start=False, stop=True, skip_group_check=True))
            k.S.pe_group(fns, [AT, Hbo, akrk, tok], [psH])
            yield
            k.copy('act', Wsb[:].rearrange("p h f -> p (h f)"), psH[:, 0:256], r=[psH], w=[Wsb])
            yield
            k.S.pe_group([lambda h=h: nc.tensor.matmul(psH[:, 256 + h * 64:256 + (h + 1) * 64], lhsT=TT[:, h, :], rhs=Wsb[:, h, :], start=False, stop=True, skip_group_check=True)
                          for h in range(4)], [TT, Wsb], [psH])
            yield
            k.copy('act', Usb[:].rearrange("p h f -> p (h f)"), psH[:, 256:512], r=[psH], w=[Usb])
            yield
            fns = []
            for h in range(4):
                fns.append(lambda h=h: nc.tensor.matmul(psC[:, h * 64:(h + 1) * 64], lhsT=tok[:, 1, h, :], rhs=Usb[:, h, :], start=(h == 0), stop=False, skip_group_check=True))
                fns.append(lambda h=h: nc.tensor.matmul(psC[:, h * 64:(h + 1) * 64], lhsT=tok[:, 2, h, :], rhs=tok[:, 0, h, :], start=False, stop=True, skip_group_check=True))
                fns.append(lambda h=h: nc.tensor.matmul(psC[:, 256 + h:256 + h + 1], lhsT=RK[:, h, c_], rhs=ppb[:, h:h + 1], start=False, stop=True, skip_group_check=True))
            k.S.pe_group(fns, [tok, Usb, RK, ppb], [psC])
            fns = []
            for h in range(4):
                fns.append(lambda h=h: nc.tensor.matmul(psY[:, h * 64:(h + 1) * 64], lhsT=RT[:, h, c_], rhs=Hbo[:, h, :], start=(h == 0), stop=False, skip_group_check=True))
                fns.append(lambda h=h: nc.tensor.matmul(psY[:, h * 64:(h + 1) * 64], lhsT=rbt[:, h, :], rhs=Usb[:, h, :], start=False, stop=False, skip_group_check=True))
                fns.append(lambda h=h: nc.tensor.matmul(psY[:, h * 64:(h + 1) * 64], lhsT=akrk[:, 1, h, :], rhs=tok[:, 0, h, :], start=False, stop=True, skip_group_check=True))
            fns.append(lambda: nc.tensor.matmul(psY[:, 256:512], lhsT=SG[:, c_], rhs=gupb[:, :], start=False, stop=True, skip_group_check=True))
            k.S.pe_group(fns, [RT, Hbo, rbt, Usb, akrk, tok, SG, gupb], [psY])
            yield
            k.tt('dve', Hnew[:], psC[:, 0:256].rearrange("p (h f) -> p h f", h=4), Hold[:], ALU.add, r=[psC, Hold], w=[Hnew])
            k.copy('dve', sm[:, 5, :], psC[:, 256:260], r=[psC], w=[sm])
            k.tt('dve', Hbn[:], Hnew[:], bc(GAM[:, :, n * 64 + 63:n * 64 + 64], [64, 4, 64]), ALU.mult, r=[Hnew, GAM], w=[Hbn])
            k.tt('pool', Hnew[:], Hnew[:], bc(GAM[:, :, n * 64 + 63:n * 64 + 64], [64, 4, 64]), ALU.mult, r=[Hnew, GAM], w=[Hnew])
            yield
            y3 = psY[:, 0:256].rearrange("p (h f) -> p h f", h=4)
            k.S.op('dve', lambda: nc.vector.reduce_sum(out=sm[:, 0, :], in_=y3, axis=AX.X), [psY], [sm])
            k.ts('dve', sm[:, 1, :], sm[:, 0, :], 1.0 / 64.0, None, ALU.mult, None, r=[sm], w=[sm])
            k.tt('dve', yc[:], y3, bc(sm[:, 1, :].unsqueeze(2), [64, 4, 64]), ALU.subtract, r=[psY, sm], w=[yc])
            yield
            k.tt('pool', ysq[:], yc[:], yc[:], ALU.mult, r=[yc], w=[ysq])
            k.S.op('dve', lambda: nc.vector.reduce_sum(out=sm[:, 2, :], in_=ysq[:], axis=AX.X), [ysq], [sm])
            k.act(sm[:, 3, :], sm[:, 2, :], AF.Ln, r=[sm], w=[sm], scale=1.0 / 64.0, bias=GN_EPS)
            k.act(sm[:, 4, :], sm[:, 3, :], AF.Exp, r=[sm], w=[sm], scale=-0.5)
            yield
            k.tt('dve', yc[:], yc[:], bc(sm[:, 4, :].unsqueeze(2), [64, 4, 64]), ALU.mult, r=[yc, sm], w=[yc])
            k.tt('pool', yc[:], yc[:], lnr[:, 0, :].rearrange("p (h f) -> p h f", h=4), ALU.mult, r=[yc, lnr], w=[yc])
            k.tt('pool', yc[:], yc[:], lnr[:, 1, :].rearrange("p (h f) -> p h f", h=4), ALU.add, r=[yc, lnr], w=[yc])
            k.tt('dve', ysq[:], tok[:, 0, :, :], bc(sm[:, 5, :].unsqueeze(2), [64, 4, 64]), ALU.mult, r=[tok, sm], w=[ysq])
            yield
            k.tt('pool', yc[:], yc[:], ysq[:], ALU.add, r=[yc, ysq], w=[yc])
            k.tt('dve', yab_[:, n, :], yc[:].rearrange("p h f -> p (h f)"), psY[:, 256:512], ALU.mult, r=[yc, psY], w=[yab_])
            if n == CPB - 1:
                k.dma('sp', k.Y[tb * BL:(tb + 1) * BL, 0:256].rearrange("(n p) c -> p n c", p=64), yab_[:], r=[yab_])
            yield

        def run_all(*gens):
            gens = [g for g in gens if g is not None]
            while gens:
                for g in list(gens):
                    try:
                        next(g)
                    except StopIteration:
                        gens.remove(g)

        NCH = S_LEN // 64
        run_all(prep(0))
        run_all(phaseA(0), prep(1) if NB > 1 else None)
        gp = None
        for nch in range(NCH):
            tb, n = divmod(nch, CPB)
            if n == 0 and tb >= 1 and tb + 1 < NB:
                gp = prep(tb + 1)
            gens = [phaseB(nch)]
            if nch + 1 < NCH:
                gens.append(phaseA(nch + 1))
            rnd = 0
            while gens:
                for g in list(gens):
                    try:
                        next(g)
                    except StopIteration:
                        gens.remove(g)
                rnd += 1
                if gp is not None and rnd % 2 == 0:
                    try:
                        next(gp)
                    except StopIteration:
                        gp = None
            if n == CPB - 2 and gp is not None:
                for _ in gp:
                    pass
                gp = None


def build(nlayers=DEPTH, taps=()):
    k = K(nlayers, taps=taps)
    setup_globals(k)
    setup_fox(k)
    setup_rwkv(k)
    setup_ffn(k)
    setup_nsa(k)
    for l in range(nlayers):
        xin = k.x_in if l == 0 else k.XR
        xout = k.OUT if l == nlayers - 1 else k.XR
        stage_mod_proj(k, l, xin)
        stage_rwkv(k, l)
        stage_fox(k, l)
        stage_nsa(k, l)
        stage_out_ffn(k, l, xin, k.XR1, xout)
    k.S.barrier()
    return k


_CACHE = {}


def kernel(**inputs):
    if "k" not in _CACHE:
        _CACHE["k"] = build(DEPTH)
    k = _CACHE["k"]
    sh = prep_shared(inputs)
    in_maps = []
    for b in range(8):
        d = dict(sh)
        d.update(prep_core(inputs, b))
        in_maps.append({n: v for n, v in d.items() if n in k.ins})
    res = run_bass_kernel_spmd(k.nc, in_maps, core_ids=list(range(8)))
    out = np.stack([np.asarray(res.results[b]["out"], dtype=np.float32) for b in range(8)], axis=0)
    return out
```

```python
import numpy as np
import ml_dtypes
from contextlib import ExitStack
import concourse.bass as bass
import concourse.mybir as mybir
from concourse.bass_utils import run_bass_kernel_spmd

F32 = mybir.dt.float32
BF16 = mybir.dt.bfloat16
AF = mybir.ActivationFunctionType
ALU = mybir.AluOpType
AX = mybir.AxisListType
NPBF = ml_dtypes.bfloat16

S_LEN = 4096
D = 1024
DEPTH = 4
NTB = 8
N_IN = 3224
D_FF = 2816
NEG = -30000.0
RMS_EPS = 1e-6
GN_EPS = 64e-5


class Sched:
    ENG = ('pe', 'act', 'dve', 'pool')
    LIMIT = 30000

    def __init__(self, nc):
        self.nc = nc
        self.e = {'pe': nc.tensor, 'act': nc.scalar, 'dve': nc.vector, 'pool': nc.gpsimd, 'sp': nc.sync}
        self.epoch = {k: 0 for k in self.ENG}
        self.sem = {k: nc.alloc_semaphore("c_%s_0" % k) for k in self.ENG}
        self.cnt = {k: 0 for k in self.ENG}
        self.seen = {k: {} for k in self.e}
        self.lastw = {}
        self.reads = {}
        self.dma_sems = {'hw': [[nc.alloc_semaphore("d%d" % i), 0, "dma%d" % i] for i in range(24)],
                         'sw': [[nc.alloc_semaphore("ds%d" % i), 0, "dmas%d" % i] for i in range(8)]}
        self.ndma = {'hw': 0, 'sw': 0}
        self.n_inst = 0
        self.n_wait = 0
        self.per = {}

    def _wait(self, eng, tok):
        key, sem, val = tok
        if self.seen[eng].get(key, 0) >= val:
            return
        self.e[eng].wait_ge(sem, val)
        self.n_wait += 1
        self.per[eng] = self.per.get(eng, 0) + 1
        self.seen[eng][key] = val

    def _deps(self, eng, reads, writes):
        for b in reads:
            t = self.lastw.get(b)
            if t is not None:
                self._wait(eng, t)
        for b in writes:
            t = self.lastw.get(b)
            if t is not None:
                self._wait(eng, t)
            for t in self.reads.get(b, ()):
                self._wait(eng, t)

    def _commit(self, tok, reads, writes):
        for b in reads:
            self.reads.setdefault(b, []).append(tok)
        for b in writes:
            self.lastw[b] = tok
            self.reads[b] = []

    def _bump(self, eng, ins):
        if self.cnt[eng] >= self.LIMIT:
            self.epoch[eng] += 1
            self.sem[eng] = self.nc.alloc_semaphore("c_%s_%d" % (eng, self.epoch[eng]))
            self.cnt[eng] = 0
        self.cnt[eng] += 1
        ins.then_inc(self.sem[eng], 1)
        return ("%s_%d" % (eng, self.epoch[eng]), self.sem[eng], self.cnt[eng])

    @staticmethod
    def _norm(reads, writes):
        rd = [getattr(b, 'n', b) for b in reads]
        wr = [getattr(b, 'n', b) for b in writes]
        ps = [b for b in rd if b.startswith("ps")]
        rd = [b for b in rd if not b.startswith("ps")]
        return rd, wr + [b for b in ps if b not in wr]

    def op(self, eng, inst_fn, reads=(), writes=()):
        reads, writes = self._norm(reads, writes)
        self._deps(eng, reads, writes)
        ins = inst_fn()
        self.per[eng] = self.per.get(eng, 0) + 1
        tok = self._bump(eng, ins)
        self._commit(tok, reads, writes)
        self.n_inst += 1
        return tok

    def pe_group(self, fns, reads=(), writes=()):
        reads, writes = self._norm(reads, writes)
        self._deps('pe', reads, writes)
        ins = None
        for f in fns:
            ins = f()
            self.n_inst += 1
            self.per['pe'] = self.per.get('pe', 0) + 1
        tok = self._bump('pe', ins)
        self._commit(tok, reads, writes)
        return tok

    def dma(self, q, out, in_, reads=(), writes=(), **kw):
        reads, writes = self._norm(reads, writes)
        self._deps(q, reads, writes)
        cls = 'sw' if q == 'pool' else 'hw'
        pool_ = self.dma_sems[cls]
        slot = pool_[self.ndma[cls] % len(pool_)]
        self.ndma[cls] += 1
        if slot[1] > 0:
            self._wait(q, (slot[2], slot[0], slot[1]))
        if slot[1] >= self.LIMIT:
            slot[0] = self.nc.alloc_semaphore("%s_e%d" % (slot[2], self.ndma[cls]))
            slot[1] = 0
            slot[2] = slot[2] + "x"
        slot[1] += 16
        ins = self.e[q].dma_start(out=out, in_=in_, **kw)
        self.per[q] = self.per.get(q, 0) + 1
        ins.then_inc(slot[0], 16)
        tok = (slot[2], slot[0], slot[1])
        self._commit(tok, reads, writes)
        self.n_inst += 1
        return tok

    def barrier(self, engines=('pe', 'act', 'dve', 'pool', 'sp')):
        toks = [("%s_%d" % (k, self.epoch[k]), self.sem[k], self.cnt[k]) for k in self.ENG if self.cnt[k] > 0]
        toks += [(s[2], s[0], s[1]) for p_ in self.dma_sems.values() for s in p_ if s[1] > 0]
        for e in engines:
            for t in toks:
                self._wait(e, t)
        self.lastw = {}
        self.reads = {}


class Pipe:
    def __init__(self, lag=2):
        self.q = []
        self.lag = lag

    def push(self, first, second):
        first()
        self.q.append(second)
        while len(self.q) > self.lag:
            self.q.pop(0)()

    def flush(self):
        while self.q:
            self.q.pop(0)()


class Tile:
    def __init__(self, h, name):
        self.h = h
        self.n = name

    def __getitem__(self, idx):
        return self.h[idx]


class Scope:
    cnt = 0

    def __init__(self, k):
        self.k = k
        self.es = ExitStack()

    def __enter__(self):
        self.es.__enter__()
        Scope.cnt += 1
        self.id = Scope.cnt
        return self

    def sb(self, name, shape, dt):
        nm = "%s_%d" % (name, self.id)
        h = self.es.enter_context(self.k.nc.sbuf_tensor(nm, list(shape), dt))
        return Tile(h, nm)

    def ps(self, name, shape, dt=F32):
        nm = "%s_%d" % (name, self.id)
        h = self.es.enter_context(self.k.nc.psum_tensor(nm, list(shape), dt))
        return Tile(h, nm)

    def __exit__(self, *a):
        self.k.S.barrier()
        return self.es.__exit__(*a)


class K:
    def __init__(self, nlayers, taps=()):
        self.nc = bass.Bass("TRN2", target_bir_lowering=False)
        self.S = Sched(self.nc)
        self.nl = nlayers
        self.taps = set(taps)
        self.ins = {}
        self.dr = {}

    def inp(self, name, shape, dt=F32):
        t = self.nc.dram_tensor(name, list(shape), dt, kind="ExternalInput").ap()
        self.ins[name] = t
        return t

    def scratch(self, name, shape, dt=F32, out=False):
        kind = "ExternalOutput" if (out or name in self.taps) else "Internal"
        t = self.nc.dram_tensor(name, list(shape), dt, kind=kind).ap()
        self.dr[name] = t
        return t

    def act(self, out, in_, func, r, w, bias=0.0, scale=1.0, accum=None):
        nc = self.nc
        if accum is None:
            return self.S.op('act', lambda: nc.scalar.activation(out=out, in_=in_, func=func, bias=bias, scale=scale), r, w)
        return self.S.op('act', lambda: nc.scalar.activation(out=out, in_=in_, func=func, bias=bias, scale=scale, accum_out=accum), r, w)

    def ts(self, eng, out, in0, s1, s2, op0, op1, r, w):
        e = self.S.e[eng]
        if op1 is None:
            return self.S.op(eng, lambda: e.tensor_scalar(out=out, in0=in0, scalar1=s1, scalar2=None, op0=op0), r, w)
        return self.S.op(eng, lambda: e.tensor_scalar(out=out, in0=in0, scalar1=s1, scalar2=s2, op0=op0, op1=op1), r, w)

    def tt(self, eng, out, in0, in1, op, r, w):
        e = self.S.e[eng]
        return self.S.op(eng, lambda: e.tensor_tensor(out=out, in0=in0, in1=in1, op=op), r, w)

    def stt(self, eng, out, in0, scalar, in1, op0, op1, r, w):
        e = self.S.e[eng]
        return self.S.op(eng, lambda: e.scalar_tensor_tensor(out=out, in0=in0, scalar=scalar, in1=in1, op0=op0, op1=op1), r, w)

    def copy(self, eng, out, in_, r, w):
        if eng == 'act':
            return self.S.op('act', lambda: self.nc.scalar.copy(out=out, in_=in_), r, w)
        e = self.S.e[eng]
        return self.S.op(eng, lambda: e.tensor_copy(out=out, in_=in_), r, w)

    def mm(self, out, pairs, r, w, start=True, stop=True, sgc=False):
        nc = self.nc
        n = len(pairs)
        fns = []
        for i, (l, rh) in enumerate(pairs):
            fns.append(lambda l=l, rh=rh, i=i: nc.tensor.matmul(out, lhsT=l, rhs=rh, start=(start and i == 0), stop=(stop and i == n - 1),
                                                               skip_group_check=(sgc or not start)))
        return self.S.pe_group(fns, r, w)

    def transpose(self, out, in_, ident, r, w):
        nc = self.nc
        return self.S.op('pe', lambda: nc.tensor.transpose(out=out, in_=in_, identity=ident), r, w)

    def dma(self, q, out, in_, r=(), w=(), **kw):
        return self.S.dma(q, out, in_, r, w, **kw)


def w_in_perm_index():
    idx = list(range(0, 896))
    idx += list(range(896, 1664))
    for c in range(3):
        idx += list(range(2054 + c * 64, 2054 + c * 64 + 64))
        idx += list(range(2054 + (c + 3) * 64, 2054 + (c + 3) * 64 + 64))
    idx += list(range(2438, 2566))
    idx += list(range(2566, 2694))
    idx += list(range(2694, 2822))
    idx += list(range(2950, 3078))
    idx += list(range(2048, 2054))
    idx += list(range(1664, 2048))
    idx += list(range(2822, 2950))
    idx += list(range(3078, 3206))
    idx += list(range(3206, 3224))
    assert len(idx) == N_IN and len(set(idx)) == N_IN
    return np.array(idx)


QKT_ROWS = 1664


def setup_globals(k):
    nc = k.nc
    k.x_in = k.inp("x", [S_LEN, D])
    k.cT = k.inp("cT", [128, 8])
    k.ada_w = k.inp("ada_w", [DEPTH, D, 6 * D])
    k.ada_b_fm = k.inp("ada_b_fm", [DEPTH, 128, 48])
    k.ada_b_row = k.inp("ada_b_row", [DEPTH, 6 * D])
    k.normg_fm = k.inp("normg_fm", [DEPTH, 4, 128, 8])
    k.normg_row = k.inp("normg_row", [DEPTH, 4, D])
    k.w_in = k.inp("w_in_p", [DEPTH, D, N_IN])
    k.ident_bf_d = k.inp("ident_bf", [128, 128], BF16)
    k.ident_f_d = k.inp("ident_f", [128, 128], F32)

    k.PT = k.scratch("PT", [896, S_LEN], F32)
    k.QKT = k.scratch("QKT", [QKT_ROWS, S_LEN], BF16)
    k.FL = k.scratch("FL", [6, S_LEN], F32)
    k.VT = k.scratch("VT", [S_LEN, 640], BF16)
    k.GT = k.scratch("GT", [S_LEN, 18], F32)
    k.Y = k.scratch("Y", [S_LEN, D], BF16)
    k.XR = k.scratch("XR", [S_LEN, D], F32)
    k.XR1 = k.scratch("XR1", [S_LEN, D], F32)
    k.OUT = k.scratch("out", [S_LEN, D], F32, out=True)

    def pers(name, shape, dt):
        return Tile(nc.alloc_sbuf_tensor(name, list(shape), dt), name)
    k.ident_bf = pers("ident_bf_sb", [128, 128], BF16)
    k.ident_f = pers("ident_f_sb", [128, 128], F32)
    k.sc = pers("sc", [128, 8], F32)
    k.modAB = pers("modAB", [128, 32], F32)
    k.gm_row = pers("gm_row", [128, D], F32)
    k.gf_row = pers("gf_row", [128, D], F32)
    k.dma('sp', k.ident_bf[:], k.ident_bf_d, w=[k.ident_bf])
    k.dma('sp', k.ident_f[:], k.ident_f_d, w=[k.ident_f])
    k.dma('sp', k.sc[:], k.cT, w=[k.sc])
    k.act(k.sc[:], k.sc[:], AF.Silu, r=[k.sc], w=[k.sc])


def stage_mod(k, l, bg=None):
    with Scope(k) as sc:
        slab = [sc.sb("adaslab%d" % i, [128, 6 * D], F32) for i in range(4)]
        psA = sc.ps("psA", [128, 32])
        psR = [sc.ps("psR%d" % i, [128, 512]) for i in range(4)]
        bfm = sc.sb("bfm", [128, 48], F32)
        gfm = sc.sb("gfm", [128, 4, 8], F32)
        brow = sc.sb("brow", [128, 2, D], F32)
        grow = sc.sb("grow", [128, 2, D], F32)
        mfm = sc.sb("mfm", [128, 32], F32)
        sc_rep = sc.sb("sc_rep", [128, 8, 128], F32)
        for kc in range(8):
            k.copy('dve', sc_rep[:, kc, :], k.sc[:, kc:kc + 1].to_broadcast([128, 128]), r=[k.sc], w=[sc_rep])
        k.dma('sp', bfm[:], k.ada_b_fm[l], w=[bfm])
        k.dma('sp', gfm[:], k.normg_fm[l].rearrange("g p c -> p g c"), w=[gfm])
        k.dma('sp', brow[:, 0, :], k.ada_b_row[l:l + 1, 2 * D:3 * D].broadcast_to([128, D]), w=[brow])
        k.dma('sp', brow[:, 1, :], k.ada_b_row[l:l + 1, 5 * D:6 * D].broadcast_to([128, D]), w=[brow])
        k.dma('sp', grow[:, 0, :], k.normg_row[l, 1:2, :].broadcast_to([128, D]), w=[grow])
        k.dma('sp', grow[:, 1, :], k.normg_row[l, 3:4, :].broadcast_to([128, D]), w=[grow])
        fm_chunks = list(range(0, 16)) + list(range(24, 40))
        row_cols = [2 * D, 2 * D + 512, 5 * D, 5 * D + 512]
        for kc in range(8):
            sl = slab[kc % 4]
            k.dma('sp' if kc % 2 == 0 else 'act', sl[:], k.ada_w[l, kc * 128:(kc + 1) * 128, :], w=[sl])
            for i, j in enumerate(fm_chunks):
                k.mm(psA[:, i:i + 1], [(sl[:, j * 128:(j + 1) * 128], k.sc[:, kc:kc + 1])], r=[sl, k.sc], w=[psA],
                     start=(kc == 0 and i == 0), stop=(kc == 7), sgc=True)
            for i, c0 in enumerate(row_cols):
                k.mm(psR[i][:], [(sc_rep[:, kc, :], sl[:, c0:c0 + 512])], r=[sl, sc_rep], w=[psR[i]],
                     start=(kc == 0), stop=(kc == 7), sgc=True)
            if bg is not None:
                try:
                    next(bg)
                except StopIteration:
                    bg = None
        if bg is not None:
            for _ in bg:
                pass
        k.tt('dve', mfm[:, 0:16], psA[:, 0:16], bfm[:, 0:16], ALU.add, r=[psA, bfm], w=[mfm])
        k.tt('dve', mfm[:, 16:32], psA[:, 16:32], bfm[:, 24:40], ALU.add, r=[psA, bfm], w=[mfm])
        k.stt('dve', k.modAB[:, 0:8], mfm[:, 8:16], 1.0, gfm[:, 0, :], ALU.add, ALU.mult, r=[mfm, gfm], w=[k.modAB])
        k.copy('dve', k.modAB[:, 8:16], mfm[:, 0:8], r=[mfm], w=[k.modAB])
        k.stt('dve', k.modAB[:, 16:24], mfm[:, 24:32], 1.0, gfm[:, 2, :], ALU.add, ALU.mult, r=[mfm, gfm], w=[k.modAB])
        k.copy('dve', k.modAB[:, 24:32], mfm[:, 16:24], r=[mfm], w=[k.modAB])
        for i in range(4):
            dst = (k.gm_row if i < 2 else k.gf_row)
            cs = slice((i % 2) * 512, (i % 2) * 512 + 512)
            k.tt('dve', dst[:, cs], psR[i][:], brow[:, i // 2, cs], ALU.add, r=[psR[i], brow], w=[dst])
            k.tt('pool', dst[:, cs], dst[:, cs], grow[:, i // 2, cs], ALU.mult, r=[dst, grow], w=[dst])


def stage_mod_proj(k, l, xsrc):
    with Scope(k) as sc:
        wsb = sc.sb("wsb", [128, 8, N_IN], BF16)
        with Scope(k) as s2:
            wst = [s2.sb("wst%d" % i, [128, N_IN], F32) for i in range(2)]

            def bg():
                for kc in range(8):
                    s = wst[kc % 2]
                    k.dma('act' if kc % 2 == 0 else 'sp', s[:], k.w_in[l, kc * 128:(kc + 1) * 128, :], w=[s])
                    k.copy('pool' if kc % 2 == 0 else 'dve', wsb[:, kc, :], s[:], r=[s], w=[wsb])
                    yield
            stage_mod(k, l, bg=bg())
        stage_proj(k, l, xsrc, pre=(sc, wsb))


def stage_proj(k, l, xsrc, pre=None):
    nc = k.nc
    with ExitStack() as es_:
        if pre is None:
            sc = es_.enter_context(Scope(k))
            wsb = sc.sb("wsb", [128, 8, N_IN], BF16)
            wst = [sc.sb("wst%d" % i, [128, N_IN], F32) for i in range(4)]
        else:
            sc, wsb = pre
            wst = None
        xt = [sc.sb("xt%d" % i, [128, D], F32) for i in range(2)]
        junk = sc.sb("junk", [128, D], BF16)
        xn = [sc.sb("xn%d" % i, [128, D], BF16) for i in range(2)]
        st = [sc.sb("st%d" % i, [128, 4], F32) for i in range(2)]
        HT = [sc.sb("HT%d" % i, [128, 8, 512], BF16) for i in range(2)]
        psT = [sc.ps("psT%d" % i, [128, D], BF16) for i in range(2)]
        psM = [sc.ps("psM%d" % i, [128, 512]) for i in range(4)]
        evf = [sc.sb("evf%d" % i, [128, 512], F32) for i in range(3)]
        evb = [sc.sb("evb%d" % i, [128, 512], BF16) for i in range(3)]
        evt = [sc.sb("evt%d" % i, [128, 640], BF16) for i in range(2)]
        evg = [sc.sb("evg%d" % i, [128, 18], F32) for i in range(2)]
        for kc in (range(8) if pre is None else []):
            s = wst[kc % 4]
            k.dma('sp' if kc % 2 == 0 else 'act', s[:], k.w_in[l, kc * 128:(kc + 1) * 128, :], w=[s])
            k.copy('pool' if kc % 2 == 0 else 'dve', wsb[:, kc, :], s[:], r=[s], w=[wsb])
        cnt = {"ev": 0, "pm": 0}

        def norm_gen(tb):
            ht = HT[tb % 2]
            t0_ = tb * 4
            k.dma('act', xt[t0_ % 2][:], xsrc[t0_ * 128:(t0_ + 1) * 128, :], w=[xt[t0_ % 2]])
            yield
            for sub in range(4):
                ti = tb * 4 + sub
                x_ = xt[ti % 2]; xn_ = xn[ti % 2]; st_ = st[ti % 2]; pt_ = psT[ti % 2]
                k.act(junk[:], x_[:], AF.Square, r=[x_], w=[junk, st_], scale=1.0 / 32.0, accum=st_[:, 0:1])
                k.act(st_[:, 1:2], st_[:, 0:1], AF.Ln, r=[st_], w=[st_], bias=RMS_EPS)
                k.act(st_[:, 2:3], st_[:, 1:2], AF.Exp, r=[st_], w=[st_], scale=-0.5)
                k.ts('dve', xn_[:], x_[:], st_[:, 2:3], None, ALU.mult, None, r=[x_, st_], w=[xn_])
                if sub < 3:
                    k.dma('act', xt[(ti + 1) % 2][:], xsrc[(ti + 1) * 128:(ti + 2) * 128, :], w=[xt[(ti + 1) % 2]])
                yield
                yield
                for kc in range(8):
                    k.transpose(pt_[:, kc * 128:(kc + 1) * 128], xn_[:, kc * 128:(kc + 1) * 128], k.ident_bf[:],
                                r=[xn_, k.ident_bf], w=[pt_])
                    if kc == 3:
                        yield
                yield
                for kc in range(8):
                    o = ht[:, kc, sub * 128:(sub + 1) * 128]
                    i_ = pt_[:, kc * 128:(kc + 1) * 128]
                    if kc % 2 == 0:
                        k.ts('dve', o, i_, k.modAB[:, kc:kc + 1], k.modAB[:, 8 + kc:9 + kc], ALU.mult, ALU.add,
                             r=[pt_, k.modAB], w=[ht])
                    else:
                        k.act(o, i_, AF.Identity, r=[pt_, k.modAB], w=[ht], scale=k.modAB[:, kc:kc + 1],
                              bias=k.modAB[:, 8 + kc:9 + kc])
                yield

        def step(g):
            if g is not None:
                try:
                    next(g)
                except StopIteration:
                    return None
            return g

        for _ in norm_gen(0):
            pass
        for tb in range(NTB):
            ht = HT[tb % 2]
            g = norm_gen(tb + 1) if tb + 1 < NTB else None
            tsl = slice(tb * 512, (tb + 1) * 512)
            fm = [(c * 128, 128, 'PT', c * 128) for c in range(7)]
            fm += [(896 + c * 128, 128, 'QKT', c * 128) for c in range(13)]
            fm += [(2560, 6, 'FL', 0)]
            for (c0, m, dst, r0) in fm:
                ps = psM[cnt["pm"] % 4]; cnt["pm"] += 1
                k.mm(ps[0:m, :], [(wsb[:, kc, c0:c0 + m], ht[:, kc, :]) for kc in range(8)], r=[wsb, ht], w=[ps])
                eng = 'act' if cnt["ev"] % 2 == 0 else 'dve'
                if dst == 'QKT':
                    ev = evb[cnt["ev"] % 3]
                    dd = k.QKT[r0:r0 + m, tsl]
                else:
                    ev = evf[cnt["ev"] % 3]
                    dd = (k.PT if dst == 'PT' else k.FL)[r0:r0 + m, tsl]
                cnt["ev"] += 1
                k.copy(eng, ev[0:m, :], ps[0:m, :], r=[ps], w=[ev])
                k.dma('sp', dd, ev[0:m, :], r=[ev])
                g = step(g)
            for sub in range(4):
                ti = tb * 4 + sub
                tok = slice(ti * 128, (ti + 1) * 128)
                ps0 = psM[cnt["pm"] % 4]; cnt["pm"] += 1
                ps1 = psM[cnt["pm"] % 4]; cnt["pm"] += 1
                lhs = lambda kc: ht[:, kc, sub * 128:(sub + 1) * 128]
                k.mm(ps0[:, 0:384], [(lhs(kc), wsb[:, kc, 2566:2950]) for kc in range(8)], r=[wsb, ht], w=[ps0])
                k.mm(ps1[:, 0:274], [(lhs(kc), wsb[:, kc, 2950:3224]) for kc in range(8)], r=[wsb, ht], w=[ps1])
                et = evt[ti % 2]; eg = evg[ti % 2]
                k.copy('act', et[:, 0:384], ps0[:, 0:384], r=[ps0], w=[et])
                k.copy('dve', et[:, 384:640], ps1[:, 0:256], r=[ps1], w=[et])
                k.copy('dve', eg[:], ps1[:, 256:274], r=[ps1], w=[eg])
                k.dma('sp', k.VT[tok, :], et[:], r=[et])
                k.dma('sp', k.GT[tok, :], eg[:], r=[eg])
                g = step(g)
            while g is not None:
                g = step(g)


def prep_shared(inp):
    f = lambda a: np.ascontiguousarray(np.asarray(a, dtype=np.float32))
    sh = {}
    sh["ada_w"] = f(inp["ada_w"])
    sh["ada_b_fm"] = f(np.asarray(inp["ada_b"]).reshape(DEPTH, 48, 128).transpose(0, 2, 1))
    sh["ada_b_row"] = f(inp["ada_b"])
    sh["normg_fm"] = f(np.asarray(inp["norm_g"]).reshape(DEPTH, 4, 8, 128).transpose(0, 1, 3, 2))
    sh["normg_row"] = f(inp["norm_g"])
    sh["w_in_p"] = f(np.asarray(inp["w_in"])[:, :, w_in_perm_index()])
    sh["ident_bf"] = np.eye(128, dtype=np.float32).astype(NPBF)
    sh["ident_f"] = np.eye(128, dtype=np.float32)
    sh["w_out"] = f(inp["w_out"]); sh["ffn_up"] = f(inp["ffn_up"]); sh["ffn_down"] = f(inp["ffn_down"])
    sh["conv_w_fm"] = f(np.asarray(inp["ffn_conv_w"]).reshape(DEPTH, 3, 44, 128).transpose(0, 3, 1, 2))
    sh["conv_b_fm"] = f(np.asarray(inp["ffn_conv_b"]).reshape(DEPTH, 44, 128).transpose(0, 2, 1))
    sh.update(nsa_host_consts())
    sh["rel_bias"] = f(inp["rel_bias"])
    sh["nsa_pe_kT"] = f(np.asarray(inp["nsa_pe_k"]).transpose(0, 2, 1))
    sh["nsa_pe_vT"] = f(np.asarray(inp["nsa_pe_v"]).transpose(0, 2, 1))
    for n in ("nsa_ck_w1", "nsa_cv_w1", "nsa_ck_w2", "nsa_cv_w2"):
        sh[n] = f(inp[n])
    sh["fox_b_f"] = f(np.asarray(inp["fox_b_f"]).reshape(DEPTH, 6, 1))
    sh.update(rwkv_host(inp))
    return sh


def prep_core(inp, b):
    d = {}
    d["x"] = np.ascontiguousarray(np.asarray(inp["x"][b], dtype=np.float32))
    d["cT"] = np.ascontiguousarray(np.asarray(inp["c"][b], dtype=np.float32).reshape(8, 128).T)
    return d


def setup_fox(k):
    k.fox_bf = k.inp("fox_b_f", [DEPTH, 6, 1])
    k.CUMA = k.scratch("CUMA", [6, 3, S_LEN], BF16)


def stage_fox(k, l):
    nc = k.nc
    with Scope(k) as sc:
        nb = sc.sb("nb", [128, 32, 6], F32)
        with Scope(k) as s2:
            fl = s2.sb("fl", [6, S_LEN], F32)
            t1 = s2.sb("t1", [6, S_LEN], F32)
            ones = s2.sb("ones", [6, S_LEN], F32)
            cum = s2.sb("cum", [6, S_LEN], F32)
            parts = s2.sb("parts", [6, 3, S_LEN], BF16)
            bfv = s2.sb("bfv", [6, 2], F32)
            psn = s2.ps("psn", [128, 512])
            k.dma('sp', fl[:], k.FL, w=[fl])
            k.dma('sp', bfv[:, 0:1], k.fox_bf[l], w=[bfv])
            k.ts('dve', bfv[:, 1:2], bfv[:, 0:1], -1.0, None, ALU.mult, None, r=[bfv], w=[bfv])
            k.S.op('pool', lambda: nc.gpsimd.memset(ones[:], 1.0), [], [ones])
            k.act(t1[:], fl[:], AF.Exp, r=[fl, bfv], w=[t1], bias=bfv[:, 1:2], scale=-1.0)
            k.act(t1[:], t1[:], AF.Ln, r=[t1], w=[t1], bias=1.0, scale=1.0)
            k.ts('dve', t1[:], t1[:], -1.0, None, ALU.mult, None, r=[t1], w=[t1])
            k.S.op('dve', lambda: nc.vector.tensor_tensor_scan(out=cum[:], data0=ones[:], data1=t1[:], initial=0.0,
                                                               op0=ALU.mult, op1=ALU.add), [ones, t1], [cum])
            for t in range(32):
                k.transpose(psn[:, t * 6:(t + 1) * 6], cum[:, t * 128:(t + 1) * 128], k.ident_f[0:6, 0:6],
                            r=[cum, k.ident_f], w=[psn])
            k.ts('dve', nb[:].rearrange("p t h -> p (t h)"), psn[:, 0:192], -1.0, None, ALU.mult, None, r=[psn], w=[nb])
            k.ts('dve', t1[:], cum[:], 8.0, None, ALU.mult, None, r=[cum], w=[t1])
            k.copy('dve', parts[:, 0, :], t1[:], r=[t1], w=[parts])
            k.tt('dve', t1[:], t1[:], parts[:, 0, :], ALU.subtract, r=[t1, parts], w=[t1])
            k.copy('dve', parts[:, 1, :], t1[:], r=[t1], w=[parts])
            k.tt('dve', t1[:], t1[:], parts[:, 1, :], ALU.subtract, r=[t1, parts], w=[t1])
            k.copy('dve', parts[:, 2, :], t1[:], r=[t1], w=[parts])
            k.dma('sp', k.CUMA, parts[:], r=[parts], w=["CUMA"])
        QA = [sc.sb("QA%d" % i, [128, S_LEN], BF16) for i in range(2)]
        KA = [sc.sb("KA%d" % i, [128, S_LEN], BF16) for i in range(2)]
        VA = sc.sb("VA", [128, 32, 6, 65], BF16)
        yb = sc.sb("yb", [128, 32, 384], BF16)
        PTl = [sc.sb("PTl%d" % i, [128, 512], BF16) for i in range(6)]
        rc = [sc.sb("rc%d" % i, [128, 4], F32) for i in range(2)]
        psS = [sc.ps("psS%d" % i, [128, 512]) for i in range(4)]
        psO = [sc.ps("psO%d" % i, [128, 512]) for i in range(2)]
        k.dma('sp', yb[:], k.VT[:, 0:384].rearrange("(t p) c -> p t c", p=128), w=[yb])
        k.S.op('pool', lambda: nc.gpsimd.memset(VA[:, :, :, 64:65], 1.0), [], [VA])
        k.copy('pool', VA[:, :, :, 0:64], yb[:].rearrange("p t (h d) -> p t h d", h=6), r=[yb], w=[VA])
        for i in range(2):
            k.S.op('dve', lambda i=i: nc.vector.memset(KA[i][64:67, :], 1.0), [], [KA[i]])
        nS = 0
        nO = 0
        nP = 0
        pipe = Pipe(3)
        for h in range(6):
            qa = QA[h % 2]; ka = KA[h % 2]
            k.dma('sp', qa[0:64, :], k.QKT[h * 64:(h + 1) * 64, :], w=[qa])
            k.dma('sp', qa[64:67, :], k.CUMA[h], w=[qa])
            k.dma('sp', ka[0:64, :], k.QKT[384 + h * 64:384 + (h + 1) * 64, :], w=[ka])
            for qb in range(NTB):
                po = psO[nO % 2]; nO += 1
                nkt = 4 * qb + 4
                for kt in range(nkt):
                    j = kt - 4 * qb
                    c0 = max(j, 0) * 128
                    ps = psS[nS % len(psS)]; nS += 1
                    pt = PTl[nP % len(PTl)]; nP += 1

                    def first(ps=ps, pt=pt, kt=kt, c0=c0, j=j, qa=qa, ka=ka, qb=qb, h=h):
                        k.mm(ps[:, c0:512], [(ka[0:67, kt * 128:(kt + 1) * 128], qa[0:67, qb * 512 + c0:(qb + 1) * 512])],
                             r=[ka, qa], w=[ps])
                        k.act(pt[:, c0:512], ps[:, c0:512], AF.Exp, r=[ps, nb], w=[pt], bias=nb[:, kt, h:h + 1], scale=0.125)
                        if j >= 0:
                            k.S.op('pool', lambda: nc.gpsimd.affine_select(
                                out=pt[:, c0:c0 + 128], in_=pt[:, c0:c0 + 128], pattern=[[1, 128]], compare_op=ALU.is_ge,
                                fill=0.0, base=0, channel_multiplier=-1), [pt], [pt])

                    def second(pt=pt, kt=kt, j=j, po=po, qb=qb, h=h, last=(kt == nkt - 1)):
                        fns = []
                        for qs in range(max(j, 0), 4):
                            fns.append(lambda qs=qs: nc.tensor.matmul(
                                po[:, qs * 65:(qs + 1) * 65], lhsT=pt[:, qs * 128:(qs + 1) * 128], rhs=VA[:, kt, h, :],
                                start=(kt == 0 and qs == 0), stop=(kt == 4 * qb + qs), skip_group_check=True))
                        k.S.pe_group(fns, [pt, VA], [po])
                        if last:
                            r_ = rc[qb % 2]
                            pov = po[:, 0:260].rearrange("p (q c) -> p q c", c=65)
                            k.S.op('dve', lambda: nc.vector.reciprocal(out=r_[:], in_=pov[:, :, 64]), [po], [r_])
                            for qs in range(4):
                                k.ts('dve', yb[:, qb * 4 + qs, h * 64:(h + 1) * 64], po[:, qs * 65:qs * 65 + 64], r_[:, qs:qs + 1], None,
                                     ALU.mult, None, r=[po, r_], w=[yb])
                    pipe.push(first, second)
        pipe.flush()
        k.dma('sp', k.Y[:, 256:640].rearrange("(t p) c -> p t c", p=128), yb[:], r=[yb], w=["Y"])


RW_BPRIO = True
LW = 1536
LC = 4608
NEG8 = -240000.0


def t5_bucket_np(n):
    n = np.maximum(n, 0)
    nf = np.maximum(n, 1).astype(np.float32)
    large = 16 + (np.log(nf / np.float32(16)) / np.float32(np.log(128 / 16)) * np.float32(16)).astype(np.int32)
    large = np.minimum(large, 31)
    return np.where(n < 16, n, large)


def nsa_host_consts():
    c = {}
    i = np.arange(LW); n = i - 511
    oh = np.zeros((33, LW), np.float32)
    ok = (n >= 0) & (n < 512)
    oh[t5_bucket_np(n)[ok], i[ok]] = 1.0
    oh[32, ~ok] = NEG8
    c["oh_w"] = oh
    i = np.arange(LC); n = i - 2063
    oh = np.zeros((33, LC), np.float32)
    ok = n >= 0
    oh[t5_bucket_np(n)[ok], i[ok]] = 1.0
    oh[32, ~ok] = NEG8
    c["oh_c"] = oh
    s_ = np.arange(S_LEN)
    c["E_all"] = (np.arange(64)[:, None] == (s_[None, :] // 64)).astype(np.float32).astype(NPBF)
    cs = np.arange(256) * 16
    ce = cs + 31
    ss = np.arange(64) * 64
    ov = ((cs[:, None] <= ss[None, :] + 63) & (ce[:, None] >= ss[None, :])).astype(np.float32)
    ov[255] = 0.0
    c["ovl"] = np.ascontiguousarray(ov.reshape(2, 128, 64).transpose(1, 0, 2)).astype(NPBF)
    t = np.arange(S_LEN)
    cur = t // 64
    jb = np.arange(64)
    back = cur[:, None] - jb[None, :]
    valid = back >= 0
    forced = (jb[None, :] == 0) | (valid & (back < 2))
    tkm = (valid & ~forced).astype(np.float32)
    tka = np.where(valid, np.where(forced, 1e4, 0.0), -1.0).astype(np.float32)
    c["tkm"] = np.ascontiguousarray(tkm.reshape(32, 128, 64).transpose(1, 0, 2)).astype(NPBF)
    c["tka"] = np.ascontiguousarray(tka.reshape(32, 128, 64).transpose(1, 0, 2)).astype(NPBF)
    return c


def setup_nsa(k):
    nc = k.nc
    k.rel_bias = k.inp("rel_bias", [32, 6])
    k.oh_w = k.inp("oh_w", [33, LW])
    k.oh_c = k.inp("oh_c", [33, LC])
    k.E_d = k.inp("E_all", [64, S_LEN], BF16)
    k.ovl_d = k.inp("ovl", [128, 2, 64], BF16)
    k.tkm_d = k.inp("tkm", [128, 32, 64], BF16)
    k.tka_d = k.inp("tka", [128, 32, 64], BF16)
    k.pe_kT = k.inp("nsa_pe_kT", [DEPTH, 64, 32])
    k.pe_vT = k.inp("nsa_pe_vT", [DEPTH, 64, 32])
    k.ck_w1 = k.inp("nsa_ck_w1", [DEPTH, 2048, 128])
    k.cv_w1 = k.inp("nsa_cv_w1", [DEPTH, 2048, 128])
    k.ck_w2 = k.inp("nsa_ck_w2", [DEPTH, 128, 64])
    k.cv_w2 = k.inp("nsa_cv_w2", [DEPTH, 128, 64])
    k.WVW = k.scratch("WVW", [6, 128, LW], BF16)
    k.WVC = k.scratch("WVC", [6, 128, LC], BF16)
    with Scope(k) as sc:
        rb = sc.sb("rb", [33, 6], F32)
        rb31 = sc.sb("rb31", [32, 6], F32)
        rrep = sc.sb("rrep", [33, 6, 128], F32)
        ohw = sc.sb("ohw", [33, LW], F32)
        ohc = sc.sb("ohc", [33, LC], F32)
        ps = [sc.ps("psb%d" % i, [128, 512]) for i in range(2)]
        ev = [sc.sb("evb%d" % i, [128, 512], BF16) for i in range(2)]
        k.dma('sp', rb[0:32, :], k.rel_bias, w=[rb])
        k.dma('sp', rb31[:], k.rel_bias[31:32, :].broadcast_to([32, 6]), w=[rb31])
        k.dma('sp', ohw[:], k.oh_w, w=[ohw])
        k.dma('sp', ohc[:], k.oh_c, w=[ohc])
        k.S.op('dve', lambda: nc.vector.memset(rb[32:33, :], 1.0), [], [rb])
        k.tt('dve', rb[0:32, :], rb[0:32, :], rb31[:], ALU.subtract, r=[rb, rb31], w=[rb])
        k.ts('dve', rb[0:32, :], rb[0:32, :], 8.0, None, ALU.mult, None, r=[rb], w=[rb])
        for h in range(6):
            k.copy('dve', rrep[:, h, :], rb[:, h:h + 1].to_broadcast([33, 128]), r=[rb], w=[rrep])
        n = 0
        for h in range(6):
            for (oh, L, dst) in ((ohw, LW, k.WVW), (ohc, LC, k.WVC)):
                for c0 in range(0, L, 512):
                    p_ = ps[n % 2]; e_ = ev[n % 2]; n += 1
                    k.mm(p_[:], [(rrep[:, h, :], oh[:, c0:c0 + 512])], r=[rrep, oh], w=[p_])
                    k.copy('act' if n % 2 else 'dve', e_[:], p_[:], r=[p_], w=[e_])
                    k.dma('sp', dst[h, :, c0:c0 + 512], e_[:], r=[e_])


class DbgStop(Exception):
    pass


def dbg(k, lvl):
    if getattr(k, 'dbg_stop', None) == lvl:
        raise DbgStop()


def stage_nsa(k, l):
    nc = k.nc
    with Scope(k) as sc:
        Gw = sc.sb("Gw", [128, 6, 1408], BF16)
        Gc = sc.sb("Gc", [128, 6, 2560], BF16)
        tkm = sc.sb("tkm", [128, 32, 64], BF16)
        tka = sc.sb("tka", [128, 32, 64], BF16)
        QN = [sc.sb("QN%d" % h, [128, S_LEN], BF16) for h in range(6)]
        KE = [sc.sb("KE%d" % g, [128, S_LEN], BF16) for g in range(2)]
        KW = sc.sb("KW", [128, S_LEN], BF16)
        VS = sc.sb("VS", [128, 32, 2, 65], BF16)
        VW = sc.sb("VW", [128, 32, 2, 65], BF16)
        KCMP = sc.sb("KCMP", [128, 256], BF16)
        VE = sc.sb("VE", [128, 2, 2, 129], BF16)
        sg = sc.sb("sg", [128, 32, 18], F32)
        for h in range(6):
            k.dma('sp', Gw[:, h, :], bass.AP(k.WVW.tensor, h * 128 * LW + 127, [[LW - 1, 128], [1, 1408]]), w=[Gw])
            k.dma('sp', Gc[:, h, :], bass.AP(k.WVC.tensor, h * 128 * LC + 2032, [[LC - 16, 128], [1, 2560]]), w=[Gc])
        k.dma('sp', tkm[:], k.tkm_d, w=[tkm])
        k.dma('sp', tka[:], k.tka_d, w=[tka])
        for h in range(6):
            g_, hp_ = h // 3, h % 3
            k.dma('sp', QN[h][g_ * 64:(g_ + 1) * 64, :], k.QKT[768 + hp_ * 128 + g_ * 64:768 + hp_ * 128 + (g_ + 1) * 64, :], w=[QN[h]])
            k.S.op('pool', lambda h=h, g_=g_: nc.gpsimd.memset(QN[h][(1 - g_) * 64:(2 - g_) * 64, :], 0.0), [], [QN[h]])
        for g_ in range(2):
            k.dma('sp', KE[g_][g_ * 64:(g_ + 1) * 64, :], k.QKT[1408 + g_ * 64:1408 + (g_ + 1) * 64, :], w=[KE[g_]])
            k.dma('sp', KE[g_][(1 - g_) * 64:(2 - g_) * 64, :], k.E_d, w=[KE[g_]])
        k.dma('sp', KW[:], k.QKT[1536:1664, :], w=[KW])
        k.dma('sp', sg[:], k.GT.rearrange("(t p) c -> p t c", p=128), w=[sg])
        k.act(sg[:], sg[:], AF.Exp, r=[sg], w=[sg], scale=-1.0)
        k.ts('dve', sg[:], sg[:], 1.0, None, ALU.add, None, r=[sg], w=[sg])
        k.S.op('dve', lambda: nc.vector.reciprocal(out=sg[:], in_=sg[:]), [sg], [sg])
        k.dma('sp', VE[:, 0, :, 65:129], k.ovl_d, w=[VE])
        k.dma('sp', VE[:, 1, :, 65:129], k.ovl_d, w=[VE])
        k.S.op('pool', lambda: nc.gpsimd.memset(VE[:, :, :, 64:65], 1.0), [], [VE])
        k.S.op('pool', lambda: nc.gpsimd.memset(VE[:, :, :, 0:64], 0.0), [], [VE])
        k.S.op('pool', lambda: nc.gpsimd.memset(KCMP[:], 0.0), [], [KCMP])
        dbg(k, 1)
        with Scope(k) as s2:
            vst = s2.sb("vst", [128, 32, 256], BF16)
            k.dma('sp', vst[:], k.VT[:, 384:640].rearrange("(t p) c -> p t c", p=128), w=[vst])
            k.S.op('pool', lambda: nc.gpsimd.memset(VS[:, :, :, 64:65], 1.0), [], [VS])
            k.S.op('pool', lambda: nc.gpsimd.memset(VW[:, :, :, 64:65], 1.0), [], [VW])
            k.copy('pool', VS[:, :, :, 0:64], vst[:, :, 0:128].rearrange("p t (g d) -> p t g d", g=2), r=[vst], w=[VS])
            k.copy('pool', VW[:, :, :, 0:64], vst[:, :, 128:256].rearrange("p t (g d) -> p t g d", g=2), r=[vst], w=[VW])
        dbg(k, 2)
        with Scope(k) as s2:
            KC = s2.sb("KC", [128, S_LEN], BF16)
            VC = s2.sb("VC", [128, S_LEN], BF16)
            k.dma('sp', KC[:], k.QKT[1152:1280, :], w=[KC])
            k.dma('sp', VC[:], k.QKT[1280:1408, :], w=[VC])
            w1s = s2.sb("w1s", [128, 16, 128], F32)
            w1b = [s2.sb("w1b%d" % i, [128, 32, 128], BF16) for i in range(2)]
            w2s = s2.sb("w2s", [128, 2, 64], F32)
            w2b = s2.sb("w2b", [128, 2, 64], BF16)
            pes = s2.sb("pes", [128, 2, 32], F32)
            peb = s2.sb("peb", [128, 2, 32], BF16)
            hb = s2.sb("hb", [128, 2], F32)
            gx = s2.sb("gx", [128, 256], F32)
            gu = s2.sb("gu", [128, 256], F32)
            gg = s2.sb("gg", [128, 256], BF16)
            psh = s2.ps("psh", [128, 512])
            psb_ = s2.ps("pshb", [128, 512])
            pso = s2.ps("pso", [128, 512])
            for kv, (w1d, w2d, ped) in enumerate(((k.ck_w1, k.ck_w2, k.pe_kT), (k.cv_w1, k.cv_w2, k.pe_vT))):
                for lh in range(2):
                    for half in range(2):
                        k.dma('sp', w1s[half * 64:(half + 1) * 64, :, :],
                              w1d[l, lh * 1024:(lh + 1) * 1024, :].rearrange("(l d) h -> d l h", d=64), w=[w1s])
                    k.copy('pool', w1b[kv][:, lh * 16:(lh + 1) * 16, :], w1s[:], r=[w1s], w=[w1b[kv]])
                for half in range(2):
                    k.dma('sp', pes[half * 64:(half + 1) * 64, kv, :], ped[l], w=[pes])
                k.dma('sp', w2s[:, kv, :], w2d[l], w=[w2s])
            k.copy('dve', w2b[:], w2s[:], r=[w2s], w=[w2b])
            w2kd = s2.sb("w2kd", [128, 2, 64], BF16)
            for a_ in range(2):
                k.copy('dve', w2kd[:, a_, :], w2s[:, 0, :], r=[w2s], w=[w2kd])
            k.copy('dve', peb[:], pes[:], r=[pes], w=[peb])
            for kv in range(2):
                src = KC if kv == 0 else VC
                k.mm(psb_[:, kv:kv + 1], [(w1b[kv][0:64, li, :], peb[0:64, kv, li:li + 1]) for li in range(32)],
                     r=[w1b[kv], peb], w=[psb_], start=True)
                k.copy('dve', hb[:, kv:kv + 1], psb_[:, kv:kv + 1], r=[psb_], w=[hb])
                for g in range(2):
                    pr = slice(g * 64, (g + 1) * 64)
                    k.mm(psh[:, 0:255], [(w1b[kv][pr, li, :], src[pr, li:li + 16 * 254 + 1:16]) for li in range(32)],
                         r=[w1b[kv], src], w=[psh])
                    k.ts('dve', gx[:, 0:255], psh[:, 0:255], hb[:, kv:kv + 1], None, ALU.add, None, r=[psh, hb], w=[gx])
                    k.tt('dve', gu[:, 0:255], gx[:, 0:255], gx[:, 0:255], ALU.mult, r=[gx], w=[gu])
                    k.ts('dve', gu[:, 0:255], gu[:, 0:255], 0.044715, 1.0, ALU.mult, ALU.add, r=[gu], w=[gu])
                    k.tt('dve', gu[:, 0:255], gu[:, 0:255], gx[:, 0:255], ALU.mult, r=[gu, gx], w=[gu])
                    k.act(gu[:, 0:255], gu[:, 0:255], AF.Exp, r=[gu], w=[gu], scale=-2.0 * 0.7978845608028654)
                    k.ts('dve', gu[:, 0:255], gu[:, 0:255], 1.0, None, ALU.add, None, r=[gu], w=[gu])
                    k.S.op('dve', lambda: nc.vector.reciprocal(out=gu[:, 0:255], in_=gu[:, 0:255]), [gu], [gu])
                    k.S.op('dve', lambda: nc.vector.memset(gg[:, 255:256], 0.0), [], [gg])
                    k.tt('dve', gg[:, 0:255], gu[:, 0:255], gx[:, 0:255], ALU.mult, r=[gu, gx], w=[gg])
                    if kv == 0:
                        k.mm(pso[:, 0:256], [(w2kd[:].rearrange("p a d -> p (a d)"), gg[:, 0:256])], r=[w2kd, gg], w=[pso])
                        k.copy('dve', KCMP[pr, :], pso[pr, 0:256], r=[pso], w=[KCMP])
                    else:
                        for ct in range(2):
                            k.mm(pso[:, ct * 64:(ct + 1) * 64], [(gg[:, ct * 128:(ct + 1) * 128], w2b[:, 1, :])],
                                 r=[w2b, gg], w=[pso], start=(ct == 0))
                        k.copy('dve', VE[:, g, :, 0:64], pso[:, 0:128].rearrange("p (c d) -> p c d", c=2), r=[pso], w=[VE])
        dbg(k, 3)
        PTl = [sc.sb("PTn%d" % i, [128, 512], BF16) for i in range(6)]
        yacc = [sc.sb("yacc%d" % i, [128, 4, 384], F32) for i in range(2)]
        ybf = [sc.sb("ybf%d" % i, [128, 4, 384], BF16) for i in range(2)]
        impt2 = [[sc.sb("impt%d_%d" % (i, g), [128, 4, 64], F32) for g in range(2)] for i in range(2)]
        scr = sc.sb("scr", [128, 4, 64], F32)
        wk = sc.sb("wk", [128, 4, 64], F32)
        m8 = sc.sb("m8", [128, 4, 16], F32)
        nmq = sc.sb("nmq", [128, 4, 128], BF16)
        rcs = [sc.sb("rcs%d" % i, [128, 8], F32) for i in range(3)]
        psS = [sc.ps("psS%d" % i, [128, 512]) for i in range(4)]
        psO = [sc.ps("psO%d" % i, [128, 512]) for i in range(3)]
        psT = sc.ps("psTn", [128, 1024], BF16)
        st = {"S": 0, "O": 0, "P": 0, "R": 0}

        def q_ap(h, c0, c1):
            g, hp = h // 3, h % 3
            return QN[h][g * 64:(g + 1) * 64, c0:c1]

        def evac(views, h, branch, qb, ya, first):
            r_ = rcs[st["R"] % 3]; st["R"] += 1
            for qs, (po, cb) in enumerate(views):
                if branch == 0:
                    k.ts('dve', r_[:, qs:qs + 1], po[:, cb + 64:cb + 65], 1e-30, None, ALU.max, None, r=[po], w=[r_])
                    k.S.op('dve', lambda r_=r_, qs=qs: nc.vector.reciprocal(out=r_[:, qs:qs + 1], in_=r_[:, qs:qs + 1]), [r_], [r_])
                else:
                    k.S.op('dve', lambda r_=r_, po=po, cb=cb, qs=qs: nc.vector.reciprocal(out=r_[:, qs:qs + 1], in_=po[:, cb + 64:cb + 65]), [po], [r_])
            k.tt('dve', r_[:, 4:8], r_[:, 0:4], sg[:, qb * 4:(qb + 1) * 4, h * 3 + branch], ALU.mult, r=[r_, sg], w=[r_])
            for qs, (po, cb) in enumerate(views):
                o = ya[:, qs, h * 64:(h + 1) * 64]
                if first:
                    k.ts('dve', o, po[:, cb:cb + 64], r_[:, 4 + qs:5 + qs], None, ALU.mult, None, r=[po, r_], w=[ya])
                else:
                    k.stt('dve', o, po[:, cb:cb + 64], r_[:, 4 + qs:5 + qs], o, ALU.mult, ALU.add, r=[po, r_, ya], w=[ya])
            return r_

        pipe = Pipe(3)

        def attend(h, qb, tiles, kmat, vmat, po, g, branch, ya, merged=False):
            hp = h % 3
            nt = len(tiles)
            state = {"first": True}
            for idx, (kt, c0, c1, extra) in enumerate(tiles):
                ps = psS[st["S"] % len(psS)]; st["S"] += 1
                pt = PTl[st["P"] % len(PTl)]; st["P"] += 1

                def first(ps=ps, pt=pt, kt=kt, c0=c0, c1=c1, extra=extra):
                    if merged:
                        fns = [lambda: nc.tensor.matmul(ps[:, c0:c1], lhsT=kmat[:, kt * 128:(kt + 1) * 128],
                                                        rhs=QN[h][:, qb * 512 + c0:qb * 512 + c1],
                                                        start=True, stop=(len(extra) == 0), skip_group_check=True)]
                    else:
                        fns = [lambda: nc.tensor.matmul(ps[:, c0:c1], lhsT=kmat[g * 64:(g + 1) * 64, kt * 128:(kt + 1) * 128],
                                                        rhs=QN[h][g * 64:(g + 1) * 64, qb * 512 + c0:qb * 512 + c1],
                                                        start=True, stop=(len(extra) == 0), skip_group_check=True)]
                    rd = [kmat, QN[h]]
                    for ei, (lt, rt, lap, rap) in enumerate(extra):
                        w_ = rap.shape[-1]
                        fns.append(lambda lap=lap, rap=rap, w_=w_, ei=ei: nc.tensor.matmul(
                            ps[:, c0:c0 + w_], lhsT=lap, rhs=rap, start=False, stop=(ei == len(extra) - 1), skip_group_check=True))
                        rd += [lt, rt]
                    k.S.pe_group(fns, rd, [ps])
                    k.act(pt[:, c0:c1], ps[:, c0:c1], AF.Exp, r=[ps], w=[pt], scale=0.125)

                def second(pt=pt, kt=kt, c0=c0, c1=c1, idx=idx):
                    fns = []
                    for qs in range(c0 // 128, (c1 + 127) // 128):
                        last = all(not (t2[1] <= qs * 128 < t2[2]) for t2 in tiles[idx + 1:])
                        fo = state["first"]
                        state["first"] = False
                        fns.append(lambda qs=qs, fo=fo, last=last: nc.tensor.matmul(
                            po[:, qs * 65:(qs + 1) * 65], lhsT=pt[:, qs * 128:(qs + 1) * 128], rhs=vmat[:, kt, g, :],
                            start=fo, stop=last, skip_group_check=True))
                    k.S.pe_group(fns, [pt, vmat], [po])
                    if idx == nt - 1:
                        evac([(po, qs * 65) for qs in range(4)], h, branch, qb, ya, False)
                pipe.push(first, second)

        def do_cmp(qb):
            ya = yacc[qb % 2]
            impt = impt2[qb % 2]
            for h in range(6):
                g = h // 3
                poA = psO[st["O"] % 3]; st["O"] += 1
                poB = psO[st["O"] % 3]; st["O"] += 1
                cts = [0] + ([1] if qb >= 4 else [])
                state = {"A": True, "B": True}
                for ct in cts:
                    delta = 512 * qb - 2048 * ct
                    ps = psS[st["S"] % len(psS)]; st["S"] += 1
                    pt = PTl[st["P"] % len(PTl)]; st["P"] += 1

                    def first(ps=ps, pt=pt, ct=ct, delta=delta, g=g, h=h):
                        pairs = [(KCMP[g * 64:(g + 1) * 64, ct * 128:(ct + 1) * 128], q_ap(h, qb * 512, (qb + 1) * 512))]
                        rd = [KCMP, QN[h]]
                        if delta < 2560:
                            pairs.append((k.ident_bf[:], Gc[:, h, delta:delta + 512])); rd += [k.ident_bf, Gc]
                        k.mm(ps[:], pairs, r=rd, w=[ps])
                        k.act(pt[:], ps[:], AF.Exp, r=[ps], w=[pt], scale=0.125)

                    def second(pt=pt, ct=ct, g=g, h=h, poA=poA, poB=poB, state=state, lastct=(ct == cts[-1])):
                        fns = []
                        for qs in range(4):
                            po, cb = (poA, qs * 129) if qs < 3 else (poB, 0)
                            key = "A" if qs < 3 else "B"
                            stt_ = state[key]
                            state[key] = False
                            fns.append(lambda qs=qs, po=po, cb=cb, stt_=stt_: nc.tensor.matmul(
                                po[:, cb:cb + 129], lhsT=pt[:, qs * 128:(qs + 1) * 128], rhs=VE[:, g, ct, :],
                                start=stt_, stop=lastct, skip_group_check=True))
                        k.S.pe_group(fns, [pt, VE], [poA, poB])
                        if lastct:
                            views = [(poA, 0), (poA, 129), (poA, 258), (poB, 0)]
                            r_ = evac(views, h, 0, qb, ya, True)
                            for qs, (po, cb) in enumerate(views):
                                o = impt[g][:, qs, :]
                                if h % 3 == 0:
                                    k.ts('dve', o, po[:, cb + 65:cb + 129], r_[:, qs:qs + 1], None, ALU.mult, None, r=[po, r_], w=[impt[g]])
                                else:
                                    k.stt('dve', o, po[:, cb + 65:cb + 129], r_[:, qs:qs + 1], o, ALU.mult, ALU.add, r=[po, r_, impt[g]], w=[impt[g]])
                    pipe.push(first, second)

        def do_topk(qb):
            impt = impt2[qb % 2]
            for g in range(2):
                k.tt('dve', scr[:], impt[g][:], tkm[:, qb * 4:(qb + 1) * 4, :], ALU.mult, r=[impt[g], tkm], w=[scr])
                k.tt('dve', scr[:], scr[:], tka[:, qb * 4:(qb + 1) * 4, :], ALU.add, r=[scr, tka], w=[scr])
                for qs in range(4):
                    k.S.op('dve', lambda qs=qs: nc.vector.max(out=m8[:, qs, 0:8], in_=scr[:, qs, :]), [scr], [m8])
                    k.S.op('dve', lambda qs=qs: nc.vector.match_replace(out=wk[:, qs, :], in_to_replace=m8[:, qs, 0:8],
                                                                        in_values=scr[:, qs, :], imm_value=-1e9), [scr, m8], [wk])
                    k.S.op('dve', lambda qs=qs: nc.vector.max(out=m8[:, qs, 8:16], in_=wk[:, qs, :]), [wk], [m8])
                    k.ts('dve', wk[:, qs, :], scr[:, qs, :], m8[:, qs, 15:16], 1.0, ALU.is_ge, ALU.subtract, r=[scr, m8, wk], w=[wk])
                k.ts('dve', nmq[:, :, 0:64], wk[:], -NEG8, None, ALU.mult, None, r=[wk], w=[nmq])
                k.ts('pool', nmq[:, :, 64:128], wk[:], -NEG8, None, ALU.mult, None, r=[wk], w=[nmq])
                for qs in range(4):
                    k.transpose(psT[:, qs * 128:(qs + 1) * 128], nmq[:, qs, :], k.ident_bf[:], r=[nmq, k.ident_bf], w=[psT])
                oh = (1 - g) * 64
                for hh in range(3 * g, 3 * g + 3):
                    k.copy('act' if hh % 2 else 'dve', QN[hh][oh:oh + 64, qb * 512:(qb + 1) * 512], psT[oh:oh + 64, 0:512], r=[psT], w=[QN[hh]])

        def do_win(qb):
            ya = yacc[qb % 2]
            for h in range(6):
                g = h // 3
                po = psO[st["O"] % 3]; st["O"] += 1
                tiles = []
                for kt in range(max(0, 4 * qb - 4), 4 * qb + 4):
                    delta = 512 * qb - 128 * kt
                    c0 = max(-delta, 0)
                    c1 = min(512, 640 - delta) if delta > 0 else 512
                    tiles.append((kt, c0, c1, [(k.ident_bf, Gw, k.ident_bf[:], Gw[:, h, delta + 384 + c0:delta + 384 + c1])]))
                attend(h, qb, tiles, KW, VW, po, g, 2, ya)

        def do_slc(qb):
            ya = yacc[qb % 2]
            for h in range(6):
                g = h // 3
                po = psO[st["O"] % 3]; st["O"] += 1
                tiles = []
                for kt in range(0, 4 * qb + 4):
                    delta = 512 * qb - 128 * kt
                    c0 = max(-delta, 0)
                    ex = []
                    if delta <= 128:
                        c1b = 256 if delta == 128 else 512
                        ex.append((k.ident_bf, Gw, k.ident_bf[:], Gw[:, h, delta + 384 + c0:delta + 384 + c1b]))
                    tiles.append((kt, c0, 512, ex))
                attend(h, qb, tiles, KE[g], VS, po, g, 1, ya, merged=True)

        qbs = list(getattr(k, 'dbg_qbs', range(NTB)))
        do_cmp(qbs[0])
        pipe.flush()
        do_topk(qbs[0])
        for i, qb in enumerate(qbs):
            do_win(qb)
            if i + 1 < len(qbs):
                do_cmp(qbs[i + 1])
                pipe.flush()
                do_topk(qbs[i + 1])
            do_slc(qb)
            pipe.flush()
            ya = yacc[qb % 2]
            yb_ = ybf[qb % 2]
            k.copy('pool', yb_[:], ya[:], r=[ya], w=[yb_])
            k.dma('sp', k.Y[qb * 512:(qb + 1) * 512, 640:1024].rearrange("(q p) c -> p q c", p=128), yb_[:], r=[yb_])


def setup_ffn(k):
    k.w_out = k.inp("w_out", [DEPTH, D, D])
    k.ffn_up = k.inp("ffn_up", [DEPTH, D, 2 * D_FF])
    k.ffn_down = k.inp("ffn_down", [DEPTH, D_FF, D])
    k.conv_w = k.inp("conv_w_fm", [DEPTH, 128, 3, 44])
    k.conv_b = k.inp("conv_b_fm", [DEPTH, 128, 44])


def load_cast_gen(k, stg, dst, src_rows, ncols, nchunks, col_split=1):
    w = ncols // col_split
    n = 0
    for c in range(nchunks):
        for cs in range(col_split):
            s = stg[n % len(stg)]
            k.dma('sp' if n % 2 == 0 else 'act', s[:, 0:w], src_rows(c)[:, cs * w:(cs + 1) * w], w=[s])
            k.copy('pool' if n % 2 == 0 else 'dve', dst[:, c, cs * w:(cs + 1) * w], s[:, 0:w], r=[s], w=[dst])
            n += 1
            yield


def load_cast(k, sc, dst, src_rows, ncols, nchunks, name, col_split=1):
    w = ncols // col_split
    stg = [sc.sb("%s_stg%d" % (name, i), [128, w], F32) for i in range(4)]
    for _ in load_cast_gen(k, stg, dst, src_rows, ncols, nchunks, col_split):
        pass


def rms_scale(k, ss, st):
    k.act(st[:, 0:1], ss, AF.Ln, r=[st], w=[st], bias=RMS_EPS)
    k.act(st[:, 1:2], st[:, 0:1], AF.Exp, r=[st], w=[st], scale=-0.5)


def stage_out(k, l, xsrc, xdst, bg=None, bg_steps=2):
    nc = k.nc
    with Scope(k) as sc:
        wo = sc.sb("wo", [128, 8, D], BF16)
        with Scope(k) as s2:
            load_cast(k, s2, wo, lambda c: k.w_out[l, c * 128:(c + 1) * 128, :], D, 8, "wo")
        yt = [sc.sb("yt%d" % i, [128, D], BF16) for i in range(2)]
        yT = [sc.sb("yT%d" % i, [128, 8, 128], BF16) for i in range(2)]
        xt = [sc.sb("xo%d" % i, [128, D], F32) for i in range(2)]
        tt_ = [sc.sb("to%d" % i, [128, D], F32) for i in range(2)]
        junk = sc.sb("junko", [128, 512], BF16)
        st = [sc.sb("sto%d" % i, [128, 4], F32) for i in range(2)]
        psT = [sc.ps("psTo%d" % i, [128, D], BF16) for i in range(2)]
        psY = [sc.ps("psYo%d" % i, [128, 512]) for i in range(4)]
        def T(ti):
            tok = slice(ti * 128, (ti + 1) * 128)
            y_ = yt[ti % 2]; yT_ = yT[ti % 2]; x_ = xt[ti % 2]; pT = psT[ti % 2]
            k.dma('act', y_[:], k.Y[tok, :], w=[y_])
            k.dma('act', x_[:], xsrc[tok, :], w=[x_])
            for kc in range(8):
                k.transpose(pT[:, kc * 128:(kc + 1) * 128], y_[:, kc * 128:(kc + 1) * 128], k.ident_bf[:], r=[y_, k.ident_bf], w=[pT])
            k.copy('act' if ti % 2 else 'dve', yT_[:].rearrange("p a b -> p (a b)"), pT[:], r=[pT], w=[yT_])

        def M(ti):
            tok = slice(ti * 128, (ti + 1) * 128)
            yT_ = yT[ti % 2]; x_ = xt[ti % 2]; t_ = tt_[ti % 2]; st_ = st[ti % 2]
            p0 = psY[(ti % 2) * 2]; p1 = psY[(ti % 2) * 2 + 1]
            for half, ps in enumerate((p0, p1)):
                k.mm(ps[:], [(yT_[:, kc, :], wo[:, kc, half * 512:(half + 1) * 512]) for kc in range(8)], r=[yT_, wo], w=[ps])
                k.act(junk[:], ps[:], AF.Square, r=[ps], w=[junk, st_], scale=1.0 / 32.0, accum=st_[:, 2 + half:3 + half])
            k.tt('dve', st_[:, 2:3], st_[:, 2:3], st_[:, 3:4], ALU.add, r=[st_], w=[st_])
            rms_scale(k, st_[:, 2:3], st_)
            for half, ps in enumerate((p0, p1)):
                cs = slice(half * 512, (half + 1) * 512)
                k.stt('dve', t_[:, cs], ps[:], st_[:, 1:2], k.gm_row[:, cs], ALU.mult, ALU.mult, r=[ps, st_, k.gm_row], w=[t_])
            k.tt('pool', t_[:], t_[:], x_[:], ALU.add, r=[t_, x_], w=[t_])
            k.dma('sp', xdst[tok, :], t_[:], r=[t_])

        T(0)
        for ti in range(32):
            if ti + 1 < 32:
                T(ti + 1)
            M(ti)
            for _ in range(bg_steps):
                if bg is not None:
                    try:
                        next(bg)
                    except StopIteration:
                        bg = None
        if bg is not None:
            for _ in bg:
                pass


def stage_out_ffn(k, l, xin, xmid, xdst):
    NCH = 22
    with Scope(k) as sc:
        wu = sc.sb("wu", [128, 8, 2 * D_FF], BF16)
        wd = sc.sb("wd", [128, NCH, D], BF16)
        with Scope(k) as s2:
            stg = [s2.sb("wstg%d" % i, [128, 1408], F32) for i in range(4)]

            def bg():
                yield from load_cast_gen(k, stg, wu, lambda c: k.ffn_up[l, c * 128:(c + 1) * 128, :], 2 * D_FF, 8, col_split=4)
                yield from load_cast_gen(k, stg, wd, lambda c: k.ffn_down[l, c * 128:(c + 1) * 128, :], D, NCH)
            stage_out(k, l, xin, xmid, bg=bg(), bg_steps=2)
        stage_ffn(k, l, xmid, xdst, pre=(sc, wu, wd))


def stage_ffn(k, l, xsrc, xdst, pre=None):
    nc = k.nc
    NCH = 22
    with ExitStack() as es_:
        if pre is None:
            sc = es_.enter_context(Scope(k))
            wu = sc.sb("wu", [128, 8, 2 * D_FF], BF16)
            wd = sc.sb("wd", [128, NCH, D], BF16)
            with Scope(k) as s2:
                load_cast(k, s2, wu, lambda c: k.ffn_up[l, c * 128:(c + 1) * 128, :], 2 * D_FF, 8, "wu", col_split=2)
                load_cast(k, s2, wd, lambda c: k.ffn_down[l, c * 128:(c + 1) * 128, :], D, NCH, "wd")
        else:
            sc, wu, wd = pre
        cw = sc.sb("cw", [128, 3, 44], F32)
        cb = sc.sb("cb", [128, 44], F32)
        hal = [sc.sb("hal%d" % i, [128, 44, 2], F32) for i in range(2)]
        k.dma('sp', cw[:], k.conv_w[l], w=[cw])
        k.dma('sp', cb[:], k.conv_b[l], w=[cb])
        k.S.op('pool', lambda: nc.gpsimd.memset(hal[1][:], 0.0), [], [hal[1]])
        actT = sc.sb("actT", [128, NCH, 512], BF16)
        HTs = [sc.sb("H2T%d" % i, [128, 8, 512], BF16) for i in range(2)]
        xt = [sc.sb("xf%d" % i, [128, D], F32) for i in range(2)]
        xn = sc.sb("xnf", [128, D], BF16)
        junk = sc.sb("junkf", [128, 512], BF16)
        st = [sc.sb("stf%d" % i, [128, 4], F32) for i in range(2)]
        Tg = [sc.sb("Tg%d" % i, [128, 512], F32) for i in range(2)]
        Tv = [sc.sb("Tv%d" % i, [128, 512], F32) for i in range(2)]
        psT = sc.ps("psTf", [128, D], BF16)
        psU = [sc.ps("psU%d" % i, [128, 512]) for i in range(4)]
        nxc = {"n": 0}

        def norm_gen(tb):
            HT = HTs[tb % 2]
            t0_ = tb * 4
            k.dma('act', xt[t0_ % 2][:], xsrc[t0_ * 128:(t0_ + 1) * 128, :], w=[xt[t0_ % 2]])
            yield
            for sub in range(4):
                ti = tb * 4 + sub
                x_ = xt[ti % 2]; st_ = st[ti % 2]
                k.act(xn[:], x_[:], AF.Square, r=[x_], w=[xn, st_], scale=1.0 / 32.0, accum=st_[:, 2:3])
                rms_scale(k, st_[:, 2:3], st_)
                k.ts('dve', xn[:], x_[:], st_[:, 1:2], None, ALU.mult, None, r=[x_, st_], w=[xn])
                if sub < 3:
                    k.dma('act', xt[(ti + 1) % 2][:], xsrc[(ti + 1) * 128:(ti + 2) * 128, :], w=[xt[(ti + 1) % 2]])
                yield
                yield
                for kc in range(8):
                    k.transpose(psT[:, kc * 128:(kc + 1) * 128], xn[:, kc * 128:(kc + 1) * 128], k.ident_bf[:], r=[xn, k.ident_bf], w=[psT])
                    if kc == 3:
                        yield
                yield
                for kc in range(8):
                    o = HT[:, kc, sub * 128:(sub + 1) * 128]
                    i_ = psT[:, kc * 128:(kc + 1) * 128]
                    if kc % 2 == 0:
                        k.ts('dve', o, i_, k.modAB[:, 16 + kc:17 + kc], k.modAB[:, 24 + kc:25 + kc], ALU.mult, ALU.add, r=[psT, k.modAB], w=[HT])
                    else:
                        k.act(o, i_, AF.Identity, r=[psT, k.modAB], w=[HT], scale=k.modAB[:, 16 + kc:17 + kc], bias=k.modAB[:, 24 + kc:25 + kc])
                yield

        def step(g):
            if g is not None:
                try:
                    next(g)
                except StopIteration:
                    return None
            return g

        xe = [sc.sb("xe%d" % i, [128, D], F32) for i in range(2)]
        ste = [sc.sb("ste%d" % i, [128, 4], F32) for i in range(2)]
        for _ in norm_gen(0):
            pass
        for tb in range(NTB):
            hin = hal[(tb + 1) % 2]; hout = hal[tb % 2]
            HT = HTs[tb % 2]
            g = norm_gen(tb + 1) if tb + 1 < NTB else None
            for cp in range(NCH):
                tg = Tg[cp % 2]; tv = Tv[cp % 2]
                for which, (T_, c_) in enumerate(((tg, cp), (tv, NCH + cp))):
                    ps = psU[(cp * 2 + which) % 4]
                    k.mm(ps[:], [(wu[:, kc, c_ * 128:(c_ + 1) * 128], HT[:, kc, :]) for kc in range(8)], r=[wu, HT], w=[ps])
                    k.act(T_[:], ps[:], AF.Identity, r=[ps, cw, cb], w=[T_], scale=cw[:, 2, c_:c_ + 1], bias=cb[:, c_:c_ + 1])
                    k.stt('dve', T_[:, 1:512], ps[:, 0:511], cw[:, 1, c_:c_ + 1], T_[:, 1:512], ALU.mult, ALU.add, r=[ps, cw, T_], w=[T_])
                    k.stt('dve', T_[:, 2:512], ps[:, 0:510], cw[:, 0, c_:c_ + 1], T_[:, 2:512], ALU.mult, ALU.add, r=[ps, cw, T_], w=[T_])
                    k.copy('act', hout[:, c_, :], ps[:, 510:512], r=[ps], w=[hout])
                    k.stt('dve', T_[:, 0:1], hin[:, c_, 1:2], cw[:, 1, c_:c_ + 1], T_[:, 0:1], ALU.mult, ALU.add, r=[hin, cw, T_], w=[T_])
                    k.stt('dve', T_[:, 0:2], hin[:, c_, 0:2], cw[:, 0, c_:c_ + 1], T_[:, 0:2], ALU.mult, ALU.add, r=[hin, cw, T_], w=[T_])
                k.act(tg[:], tg[:], AF.Silu, r=[tg], w=[tg])
                k.tt('pool', actT[:, cp, :], tg[:], tv[:], ALU.mult, r=[tg, tv], w=[actT])
                if cp >= 1:
                    g = step(g)
            for sub in range(4):
                ti = tb * 4 + sub
                tok = slice(ti * 128, (ti + 1) * 128)
                x_ = xe[sub % 2]; st_ = ste[sub % 2]
                k.dma('act', x_[:], xsrc[tok, :], w=[x_])
                psF = [psU[(2 * sub) % 4], psU[(2 * sub + 1) % 4]]
                for half in range(2):
                    ps = psF[half]
                    k.mm(ps[:], [(actT[:, cp, sub * 128:(sub + 1) * 128], wd[:, cp, half * 512:(half + 1) * 512]) for cp in range(NCH)],
                         r=[actT, wd], w=[ps])
                    k.act(junk[:, 0:512], ps[:], AF.Square, r=[ps], w=[junk, st_], scale=1.0 / 32.0, accum=st_[:, 2 + half:3 + half])
                k.tt('dve', st_[:, 2:3], st_[:, 2:3], st_[:, 3:4], ALU.add, r=[st_], w=[st_])
                rms_scale(k, st_[:, 2:3], st_)
                t_ = Tg[sub % 2] if False else None
                for half in range(2):
                    cs = slice(half * 512, (half + 1) * 512)
                    T_ = (Tg if half == 0 else Tv)[sub % 2]
                    k.stt('dve', T_[:], psF[half][:], st_[:, 1:2], k.gf_row[:, cs], ALU.mult, ALU.mult, r=[psF[half], st_, k.gf_row], w=[T_])
                    k.tt('pool', x_[:, cs], x_[:, cs], T_[:], ALU.add, r=[x_, T_], w=[x_])
                k.dma('sp', xdst[tok, :], x_[:], r=[x_])
                g = step(g)
            while g is not None:
                g = step(g)


def rwkv_host(inp):
    f = lambda a: np.ascontiguousarray(np.asarray(a, dtype=np.float32))
    mu = np.asarray(inp["rwkv_mu"])
    hd = lambda v: np.asarray(v).reshape(DEPTH, 4, 64).transpose(0, 2, 1)
    pp = np.stack([hd(mu[:, 0:256]), hd(mu[:, 256:512]), hd(mu[:, 512:768]), hd(inp["rwkv_w0"]), hd(inp["rwkv_a0"]),
                   hd(inp["rwkv_k_k"]), hd(inp["rwkv_k_a"]), hd(np.asarray(inp["rwkv_r_k"]).reshape(DEPTH, 256))], axis=2)
    lr = np.zeros((DEPTH, 64, 3), np.float32)
    lr[:, 0:32, 0] = mu[:, 768:800]; lr[:, 0:32, 1] = mu[:, 800:832]; lr[:, :, 2] = mu[:, 832:896]
    i = np.arange(64)
    mk = np.stack([(i[:, None] < i[None, :]), (i[:, None] > i[None, :]), (i[:, None] <= i[None, :]), np.eye(64, dtype=bool)]).astype(np.float32)
    cm = np.ones((64, 512), np.float32); cm[:, ::64] = 0.0
    return {"rwkv_pp": f(pp), "rwkv_lr": f(lr), "rwkv_w_up": f(inp["rwkv_w_up"]), "rwkv_a_up": f(inp["rwkv_a_up"]),
            "rwkv_g_up": f(inp["rwkv_g_up"]), "rwkv_ln": f(np.stack([np.asarray(inp["rwkv_ln_w"]), np.asarray(inp["rwkv_ln_b"])], axis=1)),
            "rwkv_masks": f(mk.transpose(1, 0, 2)), "rwkv_cmask": cm}


def setup_rwkv(k):
    k.rw_pp = k.inp("rwkv_pp", [DEPTH, 64, 8, 4])
    k.rw_lr = k.inp("rwkv_lr", [DEPTH, 64, 3])
    k.rw_wup = k.inp("rwkv_w_up", [DEPTH, 32, 256])
    k.rw_aup = k.inp("rwkv_a_up", [DEPTH, 32, 256])
    k.rw_gup = k.inp("rwkv_g_up", [DEPTH, 64, 256])
    k.rw_ln = k.inp("rwkv_ln", [DEPTH, 2, 256])
    k.rw_masks = k.inp("rwkv_masks", [64, 4, 64])
    k.rw_cmask = k.inp("rwkv_cmask", [64, 512])


def stage_rwkv(k, l):
    nc = k.nc
    BL = 256
    NB = S_LEN // BL
    CPB = BL // 64
    H4 = [64, 4, BL]
    bc = lambda ap, shape: ap.to_broadcast(shape)
    with Scope(k) as sc:
        pp = sc.sb("pp", [64, 8, 4], F32)
        lr = sc.sb("lr", [64, 3], F32)
        wup = sc.sb("wup", [32, 256], F32); aup = sc.sb("aup", [32, 256], F32); gup = sc.sb("gup", [64, 256], F32)
        lnr = sc.sb("lnr", [64, 2, 256], F32)
        mk = sc.sb("mk", [64, 4, 64], F32)
        cmask = sc.sb("cmask", [64, BL], F32)
        ones = sc.sb("ones64", [64, 64], F32)
        prm = sc.sb("prm", [64, 4, 4], F32)
        k.dma('sp', pp[:], k.rw_pp[l], w=[pp]); k.dma('sp', lr[:], k.rw_lr[l], w=[lr])
        k.dma('sp', wup[:], k.rw_wup[l], w=[wup]); k.dma('sp', aup[:], k.rw_aup[l], w=[aup]); k.dma('sp', gup[:], k.rw_gup[l], w=[gup])
        for i in range(2):
            k.dma('sp', lnr[:, i, :], k.rw_ln[l, i:i + 1, :].broadcast_to([64, 256]), w=[lnr])
        k.dma('sp', mk[:], k.rw_masks, w=[mk]); k.dma('sp', cmask[:], k.rw_cmask[:, 0:BL], w=[cmask])
        k.S.op('pool', lambda: nc.gpsimd.memset(ones[:], 1.0), [], [ones])
        k.ts('dve', prm[:, 0, :], pp[:, 3, :], -1.0, None, ALU.mult, None, r=[pp], w=[prm])
        k.ts('dve', prm[:, 1, :], pp[:, 6, :], -1.0, 1.0, ALU.mult, ALU.add, r=[pp], w=[prm])
        P3 = sc.sb("P3", [64, 3, 4, BL], F32)
        halo = sc.sb("halo", [64, 3, 4], F32)
        LR = sc.sb("LR", [64, 3, BL], F32)
        halo2 = sc.sb("halo2", [64, 3], F32)
        ELW = sc.sb("ELW", H4, F32); SC_ = sc.sb("SCAN", H4, F32); AA = sc.sb("AA", H4, F32); KKN = sc.sb("KKN", H4, F32)
        T1 = sc.sb("T1", H4, F32); T2 = sc.sb("T2", H4, F32); CM4 = sc.sb("CM4", H4, F32)
        OUT = [{nm: sc.sb("%s%d" % (nm, i), H4, F32 if nm == "GAM" else BF16) for nm in ("AT", "BT", "KT", "RT", "RK", "GAM", "V")} for i in range(2)]
        SGs = [sc.sb("SG%d" % i, [64, BL], BF16) for i in range(2)]
        gupb = sc.sb("gupb", [64, 256], BF16)
        ppb = sc.sb("ppb", [64, 4], BF16)
        identb64 = k.ident_bf
        XY = [[sc.sb("XY%d_%d" % (i, j), [64, 2, 4, 64], BF16) for j in range(2)] for i in range(2)]
        PP = [[sc.sb("PPi%d_%d" % (i, j), [64, 4, 64], BF16) for j in range(2)] for i in range(2)]
        AKRK = [sc.sb("AKRK%d" % i, [64, 2, 4, 64], BF16) for i in range(2)]
        RBT = [sc.sb("RBT%d" % i, [64, 4, 64], BF16) for i in range(2)]
        TOK = [sc.sb("TOK%d" % i, [64, 3, 4, 64], BF16) for i in range(2)]
        Wsb = sc.sb("Wsb", [64, 4, 64], BF16); Usb = sc.sb("Usb", [64, 4, 64], BF16)
        Hs = [sc.sb("Hs%d" % i, [64, 4, 64], F32) for i in range(2)]
        Hb = [sc.sb("Hb%d" % i, [64, 4, 64], BF16) for i in range(2)]
        yc = sc.sb("yc", [64, 4, 64], F32); ysq = sc.sb("ysq", [64, 4, 64], F32)
        sm = sc.sb("sm", [64, 6, 4], F32)
        yab = [sc.sb("yab%d" % i, [64, CPB, 256], BF16) for i in range(2)]
        psA1 = sc.ps("psA1", [64, 512]); psA2 = sc.ps("psA2", [64, 512]); psA3 = sc.ps("psA3", [64, 512]); psA4 = sc.ps("psA4", [64, 512])
        psH = sc.ps("psHr", [64, 512]); psY = sc.ps("psYr", [64, 512]); psC = sc.ps("psCr", [64, 512]); psQ = sc.ps("psQr", [64, 512])
        k.S.op('pool', lambda: nc.gpsimd.memset(Hs[1][:], 0.0), [], [Hs[1]])
        k.S.op('pool', lambda: nc.gpsimd.memset(Hb[1][:], 0.0), [], [Hb[1]])
        k.copy('dve', gupb[:], gup[:], r=[gup], w=[gupb])
        k.copy('dve', ppb[:], pp[:, 7, :], r=[pp], w=[ppb])
        k.S.op('pool', lambda: nc.gpsimd.memset(halo[:], 0.0), [], [halo])
        k.S.op('pool', lambda: nc.gpsimd.memset(halo2[:], 0.0), [], [halo2])
        k.copy('dve', CM4[:], bc(cmask[:].unsqueeze(1), H4), r=[cmask], w=[CM4])
        E_ = BL - 1

        def prep(tb):
            O = OUT[tb % 2]; SG = SGs[tb % 2]
            AT, BT, KT, RT, RK, GAM, V_ = O["AT"], O["BT"], O["KT"], O["RT"], O["RK"], O["GAM"], O["V"]
            t0 = tb * BL
            for q in range(3):
                k.dma('act', P3[:, q, :, :], k.PT[q * 256:(q + 1) * 256, t0:t0 + BL].rearrange("(h d) t -> d h t", d=64), w=[P3])
            k.dma('act', LR[0:32, 0, :], k.PT[768:800, t0:t0 + BL], w=[LR])
            k.dma('act', LR[0:32, 1, :], k.PT[800:832, t0:t0 + BL], w=[LR])
            k.dma('act', LR[:, 2, :], k.PT[832:896, t0:t0 + BL], w=[LR])
            yield
            for q in range(3):
                p_ = P3[:, q, :, :]
                k.tt('dve', T1[:, :, 1:BL], p_[:, :, 0:E_], p_[:, :, 1:BL], ALU.subtract, r=[P3], w=[T1])
                k.tt('dve', T1[:, :, 0:1], halo[:, q, :].unsqueeze(2), p_[:, :, 0:1], ALU.subtract, r=[P3, halo], w=[T1])
                k.copy('pool', halo[:, q, :].unsqueeze(2), p_[:, :, E_:BL], r=[P3, T1], w=[halo])
                k.tt('pool', T1[:], T1[:], bc(pp[:, q, :].unsqueeze(2), H4), ALU.mult, r=[T1, pp], w=[T1])
                if q < 2:
                    k.tt('pool', p_, p_, T1[:], ALU.add, r=[P3, T1, halo], w=[P3])
                else:
                    k.tt('pool', V_[:], p_, T1[:], ALU.add, r=[P3, T1, halo], w=[V_])
                yield
            for q, rows in ((0, 32), (1, 32), (2, 64)):
                x_ = LR[0:rows, q, :]
                t_ = T2[0:rows, 0, :]
                k.tt('dve', t_[:, 1:BL], x_[:, 0:E_], x_[:, 1:BL], ALU.subtract, r=[LR], w=[T2])
                k.tt('dve', t_[:, 0:1], halo2[0:rows, q:q + 1], x_[:, 0:1], ALU.subtract, r=[LR, halo2], w=[T2])
                k.copy('dve', halo2[0:rows, q:q + 1], x_[:, E_:BL], r=[LR, T2], w=[halo2])
                k.stt('dve', x_, t_, lr[0:rows, q:q + 1], x_, ALU.mult, ALU.add, r=[T2, lr, LR, halo2], w=[LR])
            yield
            R_ = P3[:, 0, :, :]; Kp = P3[:, 1, :, :]
            k.act(LR[0:32, 0, :], LR[0:32, 0, :], AF.Tanh, r=[LR], w=[LR])
            k.act(SG[:], LR[:, 2, :], AF.Sigmoid, r=[LR], w=[SG])
            for h in range(4):
                k.mm(psQ[:, 0:BL], [(wup[:, h * 64:(h + 1) * 64], LR[0:32, 0, :])], r=[wup, LR], w=[psQ])
                k.act(T1[:, h, :], psQ[:, 0:BL], AF.Exp, r=[psQ, prm], w=[T1], scale=-1.0, bias=prm[:, 0, h:h + 1])
                k.mm(psQ[:, BL:2 * BL], [(aup[:, h * 64:(h + 1) * 64], LR[0:32, 1, :])], r=[aup, LR], w=[psQ], start=False)
                k.act(AA[:, h, :], psQ[:, BL:2 * BL], AF.Sigmoid, r=[psQ, pp], w=[AA], bias=pp[:, 4, h:h + 1])
                yield
            k.act(T1[:], T1[:], AF.Ln, r=[T1], w=[T1], bias=1.0)
            k.act(ELW[:], T1[:], AF.Exp, r=[T1], w=[ELW], scale=-1.0, bias=-0.5)
            k.S.op('dve', lambda: nc.vector.tensor_tensor_scan(
                out=SC_[:].rearrange("p h t -> p (h t)"), data0=CM4[:].rearrange("p h t -> p (h t)"),
                data1=ELW[:].rearrange("p h t -> p (h t)"), initial=0.0, op0=ALU.mult, op1=ALU.add), [CM4, ELW], [SC_])
            yield
            k.tt('pool', KKN[:], Kp, bc(pp[:, 5, :].unsqueeze(2), H4), ALU.mult, r=[P3, pp], w=[KKN])
            k.tt('pool', T1[:], KKN[:], KKN[:], ALU.mult, r=[KKN], w=[T1])
            for h in range(4):
                k.mm(psQ[:, 0:BL], [(ones[:], T1[:, h, :])], r=[ones, T1], w=[psQ])
                k.act(T2[:, h, :], psQ[:, 0:BL], AF.Ln, r=[psQ], w=[T2], bias=1e-24)
                yield
            k.act(T2[:], T2[:], AF.Exp, r=[T2], w=[T2], scale=-0.5)
            k.tt('dve', KKN[:], KKN[:], T2[:], ALU.mult, r=[KKN, T2], w=[KKN])
            yield
            k.tt('pool', T1[:], SC_[:], ELW[:], ALU.subtract, r=[SC_, ELW], w=[T1])
            k.act(T1[:], T1[:], AF.Exp, r=[T1], w=[T1], scale=-1.0)
            k.stt('dve', AT[:], KKN[:], -1.0, T1[:], ALU.mult, ALU.mult, r=[KKN, T1], w=[AT])
            yield
            k.act(T2[:], SC_[:], AF.Exp, r=[SC_], w=[T2])
            k.tt('pool', T1[:], KKN[:], AA[:], ALU.mult, r=[KKN, AA], w=[T1])
            k.tt('dve', BT[:], T1[:], T2[:], ALU.mult, r=[T1, T2], w=[BT])
            yield
            k.tt('pool', T1[:], AA[:], bc(pp[:, 6, :].unsqueeze(2), H4), ALU.mult, r=[AA, pp], w=[T1])
            k.tt('pool', T1[:], T1[:], bc(prm[:, 1, :].unsqueeze(2), H4), ALU.add, r=[T1, prm], w=[T1])
            k.tt('dve', Kp, Kp, T1[:], ALU.mult, r=[P3, T1, KKN], w=[P3])
            yield
            k.tt('dve', KT[:], Kp, T2[:], ALU.mult, r=[P3, T2], w=[KT])
            k.tt('pool', RK[:], R_, Kp, ALU.mult, r=[P3], w=[RK])
            k.act(GAM[:], SC_[:], AF.Exp, r=[SC_], w=[GAM], scale=-1.0)
            k.tt('dve', RT[:], R_, GAM[:], ALU.mult, r=[P3, GAM], w=[RT])
            yield

        def phaseA(nch):
            tb, n = divmod(nch, CPB)
            O = OUT[tb % 2]
            AT, BT, KT, RT, V_ = O["AT"], O["BT"], O["KT"], O["RT"], O["V"]
            c_ = slice(n * 64, (n + 1) * 64)
            par = nch % 2
            xy = XY[par][0]; akrk = AKRK[par]; rbt = RBT[par]; tok = TOK[par]
            fns = []
            for h in range(4):
                fns.append(lambda h=h: nc.tensor.matmul(psA1[:, h * 64:(h + 1) * 64], lhsT=BT[:, h, c_], rhs=AT[:, h, c_], start=True, stop=True, skip_group_check=True))
                fns.append(lambda h=h: nc.tensor.matmul(psA1[:, 256 + h * 64:256 + (h + 1) * 64], lhsT=AT[:, h, c_], rhs=BT[:, h, c_], start=True, stop=True, skip_group_check=True))
            k.S.pe_group(fns, [AT, BT], [psA1])
            fns = []
            for h in range(4):
                fns.append(lambda h=h: nc.tensor.matmul(psA2[:, h * 64:(h + 1) * 64], lhsT=KT[:, h, c_], rhs=AT[:, h, c_], start=True, stop=True, skip_group_check=True))
                fns.append(lambda h=h: nc.tensor.matmul(psA2[:, 256 + h * 64:256 + (h + 1) * 64], lhsT=KT[:, h, c_], rhs=RT[:, h, c_], start=True, stop=True, skip_group_check=True))
            k.S.pe_group(fns, [AT, KT, RT], [psA2])
            k.S.pe_group([lambda h=h: nc.tensor.matmul(psA3[:, h * 64:(h + 1) * 64], lhsT=BT[:, h, c_], rhs=RT[:, h, c_], start=True, stop=True, skip_group_check=True)
                          for h in range(4)], [BT, RT], [psA3])
            yield
            v4 = lambda ps, a: ps[:, a * 256:(a + 1) * 256].rearrange("p (h f) -> p h f", h=4)
            mb = lambda i: bc(mk[:, i, :].unsqueeze(1), [64, 4, 64])
            k.tt('dve', xy[:, 0, :, :], v4(psA1, 0), mb(0), ALU.mult, r=[psA1, mk], w=[xy])
            k.tt('dve', xy[:, 1, :, :], v4(psA1, 1), mb(1), ALU.mult, r=[psA1, mk], w=[xy])
            k.tt('dve', akrk[:, 0, :, :], v4(psA2, 0), mb(0), ALU.mult, r=[psA2, mk], w=[akrk])
            k.tt('dve', akrk[:, 1, :, :], v4(psA2, 1), mb(2), ALU.mult, r=[psA2, mk], w=[akrk])
            k.tt('dve', rbt[:], v4(psA3, 0), mb(2), ALU.mult, r=[psA3, mk], w=[rbt])
            yield
            fns = []
            psA1b = psA1[:, :].bitcast(BF16)
            for qi, src_ in enumerate((V_, BT, KT)):
                for h in range(4):
                    dst = psA1b[:, qi * 256 + h * 64:qi * 256 + (h + 1) * 64]
                    fns.append(lambda dst=dst, s_=src_[:, h, c_]: nc.tensor.transpose(out=dst, in_=s_, identity=k.ident_bf[0:64, 0:64]))
            k.S.pe_group(fns, [V_, BT, KT, k.ident_bf], [psA1])
            yield
            k.copy('act', tok[:].rearrange("p a h f -> p (a h f)"), psA1b[:, 0:768], r=[psA1], w=[tok])
            P_ = PP[par][0]
            k.tt('dve', P_[:], xy[:, 0, :, :], mb(3), ALU.add, r=[xy, mk], w=[P_])
            yield
            Pm = None
            for lev in range(1, 7):
                xyn = XY[par][lev % 2]
                fns = []
                rd = [xy]
                wr = []
                if lev <= 5:
                    for h in range(4):
                        fns.append(lambda h=h, xy=xy: nc.tensor.matmul(psA4[:, 256 + h * 64:256 + (h + 1) * 64], lhsT=xy[:, 0, h, :], rhs=xy[:, 1, h, :], start=True, stop=True, skip_group_check=True))
                        if lev <= 4:
                            fns.append(lambda h=h, xy=xy: nc.tensor.matmul(psA4[:, h * 64:(h + 1) * 64], lhsT=xy[:, 1, h, :], rhs=xy[:, 0, h, :], start=True, stop=True, skip_group_check=True))
                    wr.append(psA4)
                if lev >= 2:
                    for h in range(4):
                        fns.append(lambda h=h, xy=xy, Pm=Pm: nc.tensor.matmul(psA3[:, 256 + h * 64:256 + (h + 1) * 64], lhsT=xy[:, 1, h, :], rhs=Pm[:, h, :], start=True, stop=True, skip_group_check=True))
                    rd.append(Pm); wr.append(psA3)
                k.S.pe_group(fns, rd, wr)
                yield
                if lev <= 4:
                    k.copy('act', xyn[:].rearrange("p a h f -> p (a h f)"), psA4[:, :], r=[psA4], w=[xyn])
                elif lev == 5:
                    k.copy('act', xyn[:, 1, :, :].rearrange("p h f -> p (h f)"), psA4[:, 256:512], r=[psA4], w=[xyn])
                if lev >= 2:
                    Pn = PP[par][(lev - 1) % 2]
                    k.tt('dve', Pn[:], Pm[:], v4(psA3, 1), ALU.add, r=[Pm, psA3], w=[Pn])
                    Pm = Pn
                else:
                    Pm = P_
                yield
                xy = xyn

        def phaseB(nch):
            tb, n = divmod(nch, CPB)
            O = OUT[tb % 2]; SG = SGs[tb % 2]
            AT, RT, RK, GAM = O["AT"], O["RT"], O["RK"], O["GAM"]
            c_ = slice(n * 64, (n + 1) * 64)
            par = nch % 2
            akrk = AKRK[par]; rbt = RBT[par]; tok = TOK[par]; TT = PP[par][1]
            Hold = Hs[(nch + 1) % 2]; Hnew = Hs[nch % 2]
            Hbo = Hb[(nch + 1) % 2]; Hbn = Hb[nch % 2]
            yab_ = yab[tb % 2]
            fns = []
            for h in range(4):
                fns.append(lambda h=h: nc.tensor.matmul(psH[:, h * 64:(h + 1) * 64], lhsT=AT[:, h, c_], rhs=Hbo[:, h, :], start=(h == 0), stop=False, skip_group_check=True))
                fns.append(lambda h=h: nc.tensor.matmul(psH[:, h * 64:(h + 1) * 64], lhsT=akrk[:, 0, h, :], rhs=tok[:, 0, h, :], start=False, stop=True, skip_group_check=True))
            k.S.pe_group(fns, [AT, Hbo, akrk, tok], [psH])
            yield
            k.copy('act', Wsb[:].rearrange("p h f -> p (h f)"), psH[:, 0:256], r=[psH], w=[Wsb])
            yield
            k.S.pe_group([lambda h=h: nc.tensor.matmul(psH[:, 256 + h * 64:256 + (h + 1) * 64], lhsT=TT[:, h, :], rhs=Wsb[:, h, :], start=False, stop=True, skip_group_check=True)
                          for h in range(4)], [TT, Wsb], [psH])
            yield
            k.copy('act', Usb[:].rearrange("p h f -> p (h f)"), psH[:, 256:512], r=[psH], w=[Usb])
            yield
            fns = []
            for h in range(4):
                fns.append(lambda h=h: nc.tensor.matmul(psC[:, h * 64:(h + 1) * 64], lhsT=tok[:, 1, h, :], rhs=Usb[:, h, :], start=(h == 0), stop=False, skip_group_check=True))
                fns.append(lambda h=h: nc.tensor.matmul(psC[:, h * 64:(h + 1) * 64], lhsT=tok[:, 2, h, :], rhs=tok[:, 0, h, :], start=False, stop=True, skip_group_check=True))
                fns.append(lambda h=h: nc.tensor.matmul(psC[:, 256 + h:256 + h + 1], lhsT=RK[:, h, c_], rhs=ppb[:, h:h + 1], start=False, stop=True, skip_group_check=True))
            k.S.pe_group(fns, [tok, Usb, RK, ppb], [psC])
            fns = []
            for h in range(4):
                fns.append(lambda h=h: nc.tensor.matmul(psY[:, h * 64:(h + 1) * 64], lhsT=RT[:, h, c_], rhs=Hbo[:, h, :], start=(h == 0), stop=False, skip_group_check=True))
                fns.append(lambda h=h: nc.tensor.matmul(psY[:, h * 64:(h + 1) * 64], lhsT=rbt[:, h, :], rhs=Usb[:, h, :], start=False, stop=False, skip_group_check=True))
                fns.append(lambda h=h: nc.tensor.matmul(psY[:, h * 64:(h + 1) * 64], lhsT=akrk[:, 1, h, :], rhs=tok[:, 0, h, :], start=False, stop=True, skip_group_check=True))
            fns.append(lambda: nc.tensor.matmul(psY[:, 256:512], lhsT=SG[:, c_], rhs=gupb[:, :], start=False, stop=True, skip_group_check=True))
            k.S.pe_group(fns, [RT, Hbo, rbt, Usb, akrk, tok, SG, gupb], [psY])
            yield
            k.tt('dve', Hnew[:], psC[:, 0:256].rearrange("p (h f) -> p h f", h=4), Hold[:], ALU.add, r=[psC, Hold], w=[Hnew])
            k.copy('dve', sm[:, 5, :], psC[:, 256:260], r=[psC], w=[sm])
            k.tt('dve', Hbn[:], Hnew[:], bc(GAM[:, :, n * 64 + 63:n * 64 + 64], [64, 4, 64]), ALU.mult, r=[Hnew, GAM], w=[Hbn])
            k.tt('pool', Hnew[:], Hnew[:], bc(GAM[:, :, n * 64 + 63:n * 64 + 64], [64, 4, 64]), ALU.mult, r=[Hnew, GAM], w=[Hnew])
            yield
            y3 = psY[:, 0:256].rearrange("p (h f) -> p h f", h=4)
            k.S.op('dve', lambda: nc.vector.reduce_sum(out=sm[:, 0, :], in_=y3, axis=AX.X), [psY], [sm])
            k.ts('dve', sm[:, 1, :], sm[:, 0, :], 1.0 / 64.0, None, ALU.mult, None, r=[sm], w=[sm])
            k.tt('dve', yc[:], y3, bc(sm[:, 1, :].unsqueeze(2), [64, 4, 64]), ALU.subtract, r=[psY, sm], w=[yc])
            yield
            k.tt('pool', ysq[:], yc[:], yc[:], ALU.mult, r=[yc], w=[ysq])
            k.S.op('dve', lambda: nc.vector.reduce_sum(out=sm[:, 2, :], in_=ysq[:], axis=AX.X), [ysq], [sm])
            k.act(sm[:, 3, :], sm[:, 2, :], AF.Ln, r=[sm], w=[sm], scale=1.0 / 64.0, bias=GN_EPS)
            k.act(sm[:, 4, :], sm[:, 3, :], AF.Exp, r=[sm], w=[sm], scale=-0.5)
            yield
            k.tt('dve', yc[:], yc[:], bc(sm[:, 4, :].unsqueeze(2), [64, 4, 64]), ALU.mult, r=[yc, sm], w=[yc])
            k.tt('pool', yc[:], yc[:], lnr[:, 0, :].rearrange("p (h f) -> p h f", h=4), ALU.mult, r=[yc, lnr], w=[yc])
            k.tt('pool', yc[:], yc[:], lnr[:, 1, :].rearrange("p (h f) -> p h f", h=4), ALU.add, r=[yc, lnr], w=[yc])
            k.tt('dve', ysq[:], tok[:, 0, :, :], bc(sm[:, 5, :].unsqueeze(2), [64, 4, 64]), ALU.mult, r=[tok, sm], w=[ysq])
            yield
            k.tt('pool', yc[:], yc[:], ysq[:], ALU.add, r=[yc, ysq], w=[yc])
            k.tt('dve', yab_[:, n, :], yc[:].rearrange("p h f -> p (h f)"), psY[:, 256:512], ALU.mult, r=[yc, psY], w=[yab_])
            if n == CPB - 1:
                k.dma('sp', k.Y[tb * BL:(tb + 1) * BL, 0:256].rearrange("(n p) c -> p n c", p=64), yab_[:], r=[yab_])
            yield

        def run_all(*gens):
            gens = [g for g in gens if g is not None]
            while gens:
                for g in list(gens):
                    try:
                        next(g)
                    except StopIteration:
                        gens.remove(g)

        NCH = S_LEN // 64
        run_all(prep(0))
        run_all(phaseA(0), prep(1) if NB > 1 else None)
        gp = None
        for nch in range(NCH):
            tb, n = divmod(nch, CPB)
            if n == 0 and tb >= 1 and tb + 1 < NB:
                gp = prep(tb + 1)
            gens = [phaseB(nch)]
            if nch + 1 < NCH:
                gens.append(phaseA(nch + 1))
            rnd = 0
            while gens:
                for gi, g in enumerate(list(gens)):
                    for _rep in range(2 if (gi == 0 and RW_BPRIO) else 1):
                        try:
                            next(g)
                        except StopIteration:
                            if g in gens:
                                gens.remove(g)
                            break
                rnd += 1
                if gp is not None and rnd % 2 == 0:
                    try:
                        next(gp)
                    except StopIteration:
                        gp = None
            if n == CPB - 2 and gp is not None:
                for _ in gp:
                    pass
                gp = None


def build(nlayers=DEPTH, taps=()):
    k = K(nlayers, taps=taps)
    setup_globals(k)
    setup_fox(k)
    setup_rwkv(k)
    setup_ffn(k)
    setup_nsa(k)
    for l in range(nlayers):
        xin = k.x_in if l == 0 else k.XR
        xout = k.OUT if l == nlayers - 1 else k.XR
        stage_mod_proj(k, l, xin)
        stage_rwkv(k, l)
        stage_fox(k, l)
        stage_nsa(k, l)
        stage_out_ffn(k, l, xin, k.XR1, xout)
    k.S.barrier()
    return k


_CACHE = {}


def kernel(**inputs):
    if "k" not in _CACHE:
        _CACHE["k"] = build(DEPTH)
    k = _CACHE["k"]
    sh = prep_shared(inputs)
    in_maps = []
    for b in range(8):
        d = dict(sh)
        d.update(prep_core(inputs, b))
        in_maps.append({n: v for n, v in d.items() if n in k.ins})
    res = run_bass_kernel_spmd(k.nc, in_maps, core_ids=list(range(8)))
    out = np.stack([np.asarray(res.results[b]["out"], dtype=np.float32) for b in range(8)], axis=0)
    return out
```

```python
import numpy as np
import ml_dtypes
from contextlib import ExitStack
import concourse.bass as bass
import concourse.mybir as mybir
from concourse.bass_utils import run_bass_kernel_spmd

F32 = mybir.dt.float32
BF16 = mybir.dt.bfloat16
AF = mybir.ActivationFunctionType
ALU = mybir.AluOpType
AX = mybir.AxisListType
NPBF = ml_dtypes.bfloat16

S_LEN = 4096
D = 1024
DEPTH = 4
NTB = 8
N_IN = 3224
D_FF = 2816
NEG = -30000.0
RMS_EPS = 1e-6
GN_EPS = 64e-5


class Sched:
    ENG = ('pe', 'act', 'dve', 'pool')
    LIMIT = 30000

    def __init__(self, nc):
        self.nc = nc
        self.e = {'pe': nc.tensor, 'act': nc.scalar, 'dve': nc.vector, 'pool': nc.gpsimd, 'sp': nc.sync}
        self.epoch = {k: 0 for k in self.ENG}
        self.sem = {k: nc.alloc_semaphore("c_%s_0" % k) for k in self.ENG}
        self.cnt = {k: 0 for k in self.ENG}
        self.seen = {k: {} for k in self.e}
        self.lastw = {}
        self.reads = {}
        self.dma_sems = {'hw': [[nc.alloc_semaphore("d%d" % i), 0, "dma%d" % i] for i in range(24)],
                         'sw': [[nc.alloc_semaphore("ds%d" % i), 0, "dmas%d" % i] for i in range(8)]}
        self.ndma = {'hw': 0, 'sw': 0}
        self.n_inst = 0
        self.n_wait = 0
        self.per = {}

    def _wait(self, eng, tok):
        key, sem, val = tok
        if self.seen[eng].get(key, 0) >= val:
            return
        self.e[eng].wait_ge(sem, val)
        self.n_wait += 1
        self.per[eng] = self.per.get(eng, 0) + 1
        self.seen[eng][key] = val

    def _deps(self, eng, reads, writes):
        for b in reads:
            t = self.lastw.get(b)
            if t is not None:
                self._wait(eng, t)
        for b in writes:
            t = self.lastw.get(b)
            if t is not None:
                self._wait(eng, t)
            for t in self.reads.get(b, ()):
                self._wait(eng, t)

    def _commit(self, tok, reads, writes):
        for b in reads:
            self.reads.setdefault(b, []).append(tok)
        for b in writes:
            self.lastw[b] = tok
            self.reads[b] = []

    def _bump(self, eng, ins):
        if self.cnt[eng] >= self.LIMIT:
            self.epoch[eng] += 1
            self.sem[eng] = self.nc.alloc_semaphore("c_%s_%d" % (eng, self.epoch[eng]))
            self.cnt[eng] = 0
        self.cnt[eng] += 1
        ins.then_inc(self.sem[eng], 1)
        return ("%s_%d" % (eng, self.epoch[eng]), self.sem[eng], self.cnt[eng])

    @staticmethod
    def _norm(reads, writes):
        rd = [getattr(b, 'n', b) for b in reads]
        wr = [getattr(b, 'n', b) for b in writes]
        ps = [b for b in rd if b.startswith("ps")]
        rd = [b for b in rd if not b.startswith("ps")]
        return rd, wr + [b for b in ps if b not in wr]

    def op(self, eng, inst_fn, reads=(), writes=()):
        reads, writes = self._norm(reads, writes)
        self._deps(eng, reads, writes)
        ins = inst_fn()
        self.per[eng] = self.per.get(eng, 0) + 1
        tok = self._bump(eng, ins)
        self._commit(tok, reads, writes)
        self.n_inst += 1
        return tok

    def pe_group(self, fns, reads=(), writes=()):
        reads, writes = self._norm(reads, writes)
        self._deps('pe', reads, writes)
        ins = None
        for f in fns:
            ins = f()
            self.n_inst += 1
            self.per['pe'] = self.per.get('pe', 0) + 1
        tok = self._bump('pe', ins)
        self._commit(tok, reads, writes)
        return tok

    def dma(self, q, out, in_, reads=(), writes=(), **kw):
        reads, writes = self._norm(reads, writes)
        self._deps(q, reads, writes)
        cls = 'sw' if q == 'pool' else 'hw'
        pool_ = self.dma_sems[cls]
        slot = pool_[self.ndma[cls] % len(pool_)]
        self.ndma[cls] += 1
        if slot[1] > 0:
            self._wait(q, (slot[2], slot[0], slot[1]))
        if slot[1] >= self.LIMIT:
            slot[0] = self.nc.alloc_semaphore("%s_e%d" % (slot[2], self.ndma[cls]))
            slot[1] = 0
            slot[2] = slot[2] + "x"
        slot[1] += 16
        ins = self.e[q].dma_start(out=out, in_=in_, **kw)
        self.per[q] = self.per.get(q, 0) + 1
        ins.then_inc(slot[0], 16)
        tok = (slot[2], slot[0], slot[1])
        self._commit(tok, reads, writes)
        self.n_inst += 1
        return tok

    def barrier(self, engines=('pe', 'act', 'dve', 'pool', 'sp')):
        toks = [("%s_%d" % (k, self.epoch[k]), self.sem[k], self.cnt[k]) for k in self.ENG if self.cnt[k] > 0]
        toks += [(s[2], s[0], s[1]) for p_ in self.dma_sems.values() for s in p_ if s[1] > 0]
        for e in engines:
            for t in toks:
                self._wait(e, t)
        self.lastw = {}
        self.reads = {}


class Pipe:
    def __init__(self, lag=2):
        self.q = []
        self.lag = lag

    def push(self, first, second):
        first()
        self.q.append(second)
        while len(self.q) > self.lag:
            self.q.pop(0)()

    def flush(self):
        while self.q:
            self.q.pop(0)()


class Tile:
    def __init__(self, h, name):
        self.h = h
        self.n = name

    def __getitem__(self, idx):
        return self.h[idx]


class Scope:
    cnt = 0

    def __init__(self, k):
        self.k = k
        self.es = ExitStack()

    def __enter__(self):
        self.es.__enter__()
        Scope.cnt += 1
        self.id = Scope.cnt
        return self

    def sb(self, name, shape, dt):
        nm = "%s_%d" % (name, self.id)
        h = self.es.enter_context(self.k.nc.sbuf_tensor(nm, list(shape), dt))
        return Tile(h, nm)

    def ps(self, name, shape, dt=F32):
        nm = "%s_%d" % (name, self.id)
        h = self.es.enter_context(self.k.nc.psum_tensor(nm, list(shape), dt))
        return Tile(h, nm)

    def __exit__(self, *a):
        self.k.S.barrier()
        return self.es.__exit__(*a)


class K:
    def __init__(self, nlayers, taps=()):
        self.nc = bass.Bass("TRN2", target_bir_lowering=False)
        self.S = Sched(self.nc)
        self.nl = nlayers
        self.taps = set(taps)
        self.ins = {}
        self.dr = {}

    def inp(self, name, shape, dt=F32):
        t = self.nc.dram_tensor(name, list(shape), dt, kind="ExternalInput").ap()
        self.ins[name] = t
        return t

    def scratch(self, name, shape, dt=F32, out=False):
        kind = "ExternalOutput" if (out or name in self.taps) else "Internal"
        t = self.nc.dram_tensor(name, list(shape), dt, kind=kind).ap()
        self.dr[name] = t
        return t

    def act(self, out, in_, func, r, w, bias=0.0, scale=1.0, accum=None):
        nc = self.nc
        if accum is None:
            return self.S.op('act', lambda: nc.scalar.activation(out=out, in_=in_, func=func, bias=bias, scale=scale), r, w)
        return self.S.op('act', lambda: nc.scalar.activation(out=out, in_=in_, func=func, bias=bias, scale=scale, accum_out=accum), r, w)

    def ts(self, eng, out, in0, s1, s2, op0, op1, r, w):
        e = self.S.e[eng]
        if op1 is None:
            return self.S.op(eng, lambda: e.tensor_scalar(out=out, in0=in0, scalar1=s1, scalar2=None, op0=op0), r, w)
        return self.S.op(eng, lambda: e.tensor_scalar(out=out, in0=in0, scalar1=s1, scalar2=s2, op0=op0, op1=op1), r, w)

    def tt(self, eng, out, in0, in1, op, r, w):
        e = self.S.e[eng]
        return self.S.op(eng, lambda: e.tensor_tensor(out=out, in0=in0, in1=in1, op=op), r, w)

    def stt(self, eng, out, in0, scalar, in1, op0, op1, r, w):
        e = self.S.e[eng]
        return self.S.op(eng, lambda: e.scalar_tensor_tensor(out=out, in0=in0, scalar=scalar, in1=in1, op0=op0, op1=op1), r, w)

    def copy(self, eng, out, in_, r, w):
        if eng == 'act':
            return self.S.op('act', lambda: self.nc.scalar.copy(out=out, in_=in_), r, w)
        e = self.S.e[eng]
        return self.S.op(eng, lambda: e.tensor_copy(out=out, in_=in_), r, w)

    def mm(self, out, pairs, r, w, start=True, stop=True, sgc=False):
        nc = self.nc
        n = len(pairs)
        fns = []
        for i, (l, rh) in enumerate(pairs):
            fns.append(lambda l=l, rh=rh, i=i: nc.tensor.matmul(out, lhsT=l, rhs=rh, start=(start and i == 0), stop=(stop and i == n - 1),
                                                               skip_group_check=(sgc or not start)))
        return self.S.pe_group(fns, r, w)

    def transpose(self, out, in_, ident, r, w):
        nc = self.nc
        return self.S.op('pe', lambda: nc.tensor.transpose(out=out, in_=in_, identity=ident), r, w)

    def dma(self, q, out, in_, r=(), w=(), **kw):
        return self.S.dma(q, out, in_, r, w, **kw)


def w_in_perm_index():
    idx = list(range(0, 896))
    idx += list(range(896, 1664))
    for c in range(3):
        idx += list(range(2054 + c * 64, 2054 + c * 64 + 64))
        idx += list(range(2054 + (c + 3) * 64, 2054 + (c + 3) * 64 + 64))
    idx += list(range(2438, 2566))
    idx += list(range(2566, 2694))
    idx += list(range(2694, 2822))
    idx += list(range(2950, 3078))
    idx += list(range(2048, 2054))
    idx += list(range(1664, 2048))
    idx += list(range(2822, 2950))
    idx += list(range(3078, 3206))
    idx += list(range(3206, 3224))
    assert len(idx) == N_IN and len(set(idx)) == N_IN
    return np.array(idx)


QKT_ROWS = 1664


def setup_globals(k):
    nc = k.nc
    k.x_in = k.inp("x", [S_LEN, D])
    k.cT = k.inp("cT", [128, 8])
    k.ada_w = k.inp("ada_w", [DEPTH, D, 6 * D])
    k.ada_b_fm = k.inp("ada_b_fm", [DEPTH, 128, 48])
    k.ada_b_row = k.inp("ada_b_row", [DEPTH, 6 * D])
    k.normg_fm = k.inp("normg_fm", [DEPTH, 4, 128, 8])
    k.normg_row = k.inp("normg_row", [DEPTH, 4, D])
    k.w_in = k.inp("w_in_p", [DEPTH, D, N_IN])
    k.ident_bf_d = k.inp("ident_bf", [128, 128], BF16)
    k.ident_f_d = k.inp("ident_f", [128, 128], F32)

    k.PT = k.scratch("PT", [896, S_LEN], F32)
    k.QKT = k.scratch("QKT", [QKT_ROWS, S_LEN], BF16)
    k.FL = k.scratch("FL", [6, S_LEN], F32)
    k.VT = k.scratch("VT", [S_LEN, 640], BF16)
    k.GT = k.scratch("GT", [S_LEN, 18], F32)
    k.Y = k.scratch("Y", [S_LEN, D], BF16)
    k.XR = k.scratch("XR", [S_LEN, D], F32)
    k.XR1 = k.scratch("XR1", [S_LEN, D], F32)
    k.OUT = k.scratch("out", [S_LEN, D], F32, out=True)

    def pers(name, shape, dt):
        return Tile(nc.alloc_sbuf_tensor(name, list(shape), dt), name)
    k.ident_bf = pers("ident_bf_sb", [128, 128], BF16)
    k.ident_f = pers("ident_f_sb", [128, 128], F32)
    k.sc = pers("sc", [128, 8], F32)
    k.modAB = pers("modAB", [128, 32], F32)
    k.gm_row = pers("gm_row", [128, D], F32)
    k.gf_row = pers("gf_row", [128, D], F32)
    k.dma('sp', k.ident_bf[:], k.ident_bf_d, w=[k.ident_bf])
    k.dma('sp', k.ident_f[:], k.ident_f_d, w=[k.ident_f])
    k.dma('sp', k.sc[:], k.cT, w=[k.sc])
    k.act(k.sc[:], k.sc[:], AF.Silu, r=[k.sc], w=[k.sc])


def stage_mod(k, l, bg=None):
    with Scope(k) as sc:
        slab = [sc.sb("adaslab%d" % i, [128, 6 * D], F32) for i in range(4)]
        psA = sc.ps("psA", [128, 32])
        psR = [sc.ps("psR%d" % i, [128, 512]) for i in range(4)]
        bfm = sc.sb("bfm", [128, 48], F32)
        gfm = sc.sb("gfm", [128, 4, 8], F32)
        brow = sc.sb("brow", [128, 2, D], F32)
        grow = sc.sb("grow", [128, 2, D], F32)
        mfm = sc.sb("mfm", [128, 32], F32)
        sc_rep = sc.sb("sc_rep", [128, 8, 128], F32)
        for kc in range(8):
            k.copy('dve', sc_rep[:, kc, :], k.sc[:, kc:kc + 1].to_broadcast([128, 128]), r=[k.sc], w=[sc_rep])
        k.dma('sp', bfm[:], k.ada_b_fm[l], w=[bfm])
        k.dma('sp', gfm[:], k.normg_fm[l].rearrange("g p c -> p g c"), w=[gfm])
        k.dma('sp', brow[:, 0, :], k.ada_b_row[l:l + 1, 2 * D:3 * D].broadcast_to([128, D]), w=[brow])
        k.dma('sp', brow[:, 1, :], k.ada_b_row[l:l + 1, 5 * D:6 * D].broadcast_to([128, D]), w=[brow])
        k.dma('sp', grow[:, 0, :], k.normg_row[l, 1:2, :].broadcast_to([128, D]), w=[grow])
        k.dma('sp', grow[:, 1, :], k.normg_row[l, 3:4, :].broadcast_to([128, D]), w=[grow])
        fm_chunks = list(range(0, 16)) + list(range(24, 40))
        row_cols = [2 * D, 2 * D + 512, 5 * D, 5 * D + 512]
        for kc in range(8):
            sl = slab[kc % 4]
            k.dma('sp' if kc % 2 == 0 else 'act', sl[:], k.ada_w[l, kc * 128:(kc + 1) * 128, :], w=[sl])
            for i, j in enumerate(fm_chunks):
                k.mm(psA[:, i:i + 1], [(sl[:, j * 128:(j + 1) * 128], k.sc[:, kc:kc + 1])], r=[sl, k.sc], w=[psA],
                     start=(kc == 0 and i == 0), stop=(kc == 7), sgc=True)
            for i, c0 in enumerate(row_cols):
                k.mm(psR[i][:], [(sc_rep[:, kc, :], sl[:, c0:c0 + 512])], r=[sl, sc_rep], w=[psR[i]],
                     start=(kc == 0), stop=(kc == 7), sgc=True)
            if bg is not None:
                try:
                    next(bg)
                except StopIteration:
                    bg = None
        if bg is not None:
            for _ in bg:
                pass
        k.tt('dve', mfm[:, 0:16], psA[:, 0:16], bfm[:, 0:16], ALU.add, r=[psA, bfm], w=[mfm])
        k.tt('dve', mfm[:, 16:32], psA[:, 16:32], bfm[:, 24:40], ALU.add, r=[psA, bfm], w=[mfm])
        k.stt('dve', k.modAB[:, 0:8], mfm[:, 8:16], 1.0, gfm[:, 0, :], ALU.add, ALU.mult, r=[mfm, gfm], w=[k.modAB])
        k.copy('dve', k.modAB[:, 8:16], mfm[:, 0:8], r=[mfm], w=[k.modAB])
        k.stt('dve', k.modAB[:, 16:24], mfm[:, 24:32], 1.0, gfm[:, 2, :], ALU.add, ALU.mult, r=[mfm, gfm], w=[k.modAB])
        k.copy('dve', k.modAB[:, 24:32], mfm[:, 16:24], r=[mfm], w=[k.modAB])
        for i in range(4):
            dst = (k.gm_row if i < 2 else k.gf_row)
            cs = slice((i % 2) * 512, (i % 2) * 512 + 512)
            k.tt('dve', dst[:, cs], psR[i][:], brow[:, i // 2, cs], ALU.add, r=[psR[i], brow], w=[dst])
            k.tt('pool', dst[:, cs], dst[:, cs], grow[:, i // 2, cs], ALU.mult, r=[dst, grow], w=[dst])


def stage_mod_proj(k, l, xsrc):
    with Scope(k) as sc:
        wsb = sc.sb("wsb", [128, 8, N_IN], BF16)
        with Scope(k) as s2:
            wst = [s2.sb("wst%d" % i, [128, N_IN], F32) for i in range(2)]

            def bg():
                for kc in range(8):
                    s = wst[kc % 2]
                    k.dma('act' if kc % 2 == 0 else 'sp', s[:], k.w_in[l, kc * 128:(kc + 1) * 128, :], w=[s])
                    k.copy('pool' if kc % 2 == 0 else 'dve', wsb[:, kc, :], s[:], r=[s], w=[wsb])
                    yield
            stage_mod(k, l, bg=bg())
        stage_proj(k, l, xsrc, pre=(sc, wsb))


def stage_proj(k, l, xsrc, pre=None):
    nc = k.nc
    with ExitStack() as es_:
        if pre is None:
            sc = es_.enter_context(Scope(k))
            wsb = sc.sb("wsb", [128, 8, N_IN], BF16)
            wst = [sc.sb("wst%d" % i, [128, N_IN], F32) for i in range(4)]
        else:
            sc, wsb = pre
            wst = None
        xt = [sc.sb("xt%d" % i, [128, D], F32) for i in range(2)]
        junk = sc.sb("junk", [128, D], BF16)
        xn = [sc.sb("xn%d" % i, [128, D], BF16) for i in range(2)]
        st = [sc.sb("st%d" % i, [128, 4], F32) for i in range(2)]
        HT = [sc.sb("HT%d" % i, [128, 8, 512], BF16) for i in range(2)]
        psT = [sc.ps("psT%d" % i, [128, D], BF16) for i in range(2)]
        psM = [sc.ps("psM%d" % i, [128, 512]) for i in range(4)]
        evf = [sc.sb("evf%d" % i, [128, 512], F32) for i in range(3)]
        evb = [sc.sb("evb%d" % i, [128, 512], BF16) for i in range(3)]
        evt = [sc.sb("evt%d" % i, [128, 640], BF16) for i in range(2)]
        evg = [sc.sb("evg%d" % i, [128, 18], F32) for i in range(2)]
        for kc in (range(8) if pre is None else []):
            s = wst[kc % 4]
            k.dma('sp' if kc % 2 == 0 else 'act', s[:], k.w_in[l, kc * 128:(kc + 1) * 128, :], w=[s])
            k.copy('pool' if kc % 2 == 0 else 'dve', wsb[:, kc, :], s[:], r=[s], w=[wsb])
        cnt = {"ev": 0, "pm": 0}

        def norm_gen(tb):
            ht = HT[tb % 2]
            t0_ = tb * 4
            k.dma('act', xt[t0_ % 2][:], xsrc[t0_ * 128:(t0_ + 1) * 128, :], w=[xt[t0_ % 2]])
            yield
            for sub in range(4):
                ti = tb * 4 + sub
                x_ = xt[ti % 2]; xn_ = xn[ti % 2]; st_ = st[ti % 2]; pt_ = psT[ti % 2]
                k.act(junk[:], x_[:], AF.Square, r=[x_], w=[junk, st_], scale=1.0 / 32.0, accum=st_[:, 0:1])
                k.act(st_[:, 1:2], st_[:, 0:1], AF.Ln, r=[st_], w=[st_], bias=RMS_EPS)
                k.act(st_[:, 2:3], st_[:, 1:2], AF.Exp, r=[st_], w=[st_], scale=-0.5)
                k.ts('dve', xn_[:], x_[:], st_[:, 2:3], None, ALU.mult, None, r=[x_, st_], w=[xn_])
                if sub < 3:
                    k.dma('act', xt[(ti + 1) % 2][:], xsrc[(ti + 1) * 128:(ti + 2) * 128, :], w=[xt[(ti + 1) % 2]])
                yield
                yield
                for kc in range(8):
                    k.transpose(pt_[:, kc * 128:(kc + 1) * 128], xn_[:, kc * 128:(kc + 1) * 128], k.ident_bf[:],
                                r=[xn_, k.ident_bf], w=[pt_])
                    if kc == 3:
                        yield
                yield
                for kc in range(8):
                    o = ht[:, kc, sub * 128:(sub + 1) * 128]
                    i_ = pt_[:, kc * 128:(kc + 1) * 128]
                    if kc % 2 == 0:
                        k.ts('dve', o, i_, k.modAB[:, kc:kc + 1], k.modAB[:, 8 + kc:9 + kc], ALU.mult, ALU.add,
                             r=[pt_, k.modAB], w=[ht])
                    else:
                        k.act(o, i_, AF.Identity, r=[pt_, k.modAB], w=[ht], scale=k.modAB[:, kc:kc + 1],
                              bias=k.modAB[:, 8 + kc:9 + kc])
                yield

        def step(g):
            if g is not None:
                try:
                    next(g)
                except StopIteration:
                    return None
            return g

        for _ in norm_gen(0):
            pass
        for tb in range(NTB):
            ht = HT[tb % 2]
            g = norm_gen(tb + 1) if tb + 1 < NTB else None
            tsl = slice(tb * 512, (tb + 1) * 512)
            fm = [(c * 128, 128, 'PT', c * 128) for c in range(7)]
            fm += [(896 + c * 128, 128, 'QKT', c * 128) for c in range(13)]
            fm += [(2560, 6, 'FL', 0)]
            for (c0, m, dst, r0) in fm:
                ps = psM[cnt["pm"] % 4]; cnt["pm"] += 1
                k.mm(ps[0:m, :], [(wsb[:, kc, c0:c0 + m], ht[:, kc, :]) for kc in range(8)], r=[wsb, ht], w=[ps])
                eng = 'act' if cnt["ev"] % 2 == 0 else 'dve'
                if dst == 'QKT':
                    ev = evb[cnt["ev"] % 3]
                    dd = k.QKT[r0:r0 + m, tsl]
                else:
                    ev = evf[cnt["ev"] % 3]
                    dd = (k.PT if dst == 'PT' else k.FL)[r0:r0 + m, tsl]
                cnt["ev"] += 1
                k.copy(eng, ev[0:m, :], ps[0:m, :], r=[ps], w=[ev])
                k.dma('sp', dd, ev[0:m, :], r=[ev])
                g = step(g)
            for sub in range(4):
                ti = tb * 4 + sub
                tok = slice(ti * 128, (ti + 1) * 128)
                ps0 = psM[cnt["pm"] % 4]; cnt["pm"] += 1
                ps1 = psM[cnt["pm"] % 4]; cnt["pm"] += 1
                lhs = lambda kc: ht[:, kc, sub * 128:(sub + 1) * 128]
                k.mm(ps0[:, 0:384], [(lhs(kc), wsb[:, kc, 2566:2950]) for kc in range(8)], r=[wsb, ht], w=[ps0])
                k.mm(ps1[:, 0:274], [(lhs(kc), wsb[:, kc, 2950:3224]) for kc in range(8)], r=[wsb, ht], w=[ps1])
                et = evt[ti % 2]; eg = evg[ti % 2]
                k.copy('act', et[:, 0:384], ps0[:, 0:384], r=[ps0], w=[et])
                k.copy('dve', et[:, 384:640], ps1[:, 0:256], r=[ps1], w=[et])
                k.copy('dve', eg[:], ps1[:, 256:274], r=[ps1], w=[eg])
                k.dma('sp', k.VT[tok, :], et[:], r=[et])
                k.dma('sp', k.GT[tok, :], eg[:], r=[eg])
                g = step(g)
            while g is not None:
                g = step(g)


def prep_shared(inp):
    f = lambda a: np.ascontiguousarray(np.asarray(a, dtype=np.float32))
    sh = {}
    sh["ada_w"] = f(inp["ada_w"])
    sh["ada_b_fm"] = f(np.asarray(inp["ada_b"]).reshape(DEPTH, 48, 128).transpose(0, 2, 1))
    sh["ada_b_row"] = f(inp["ada_b"])
    sh["normg_fm"] = f(np.asarray(inp["norm_g"]).reshape(DEPTH, 4, 8, 128).transpose(0, 1, 3, 2))
    sh["normg_row"] = f(inp["norm_g"])
    sh["w_in_p"] = f(np.asarray(inp["w_in"])[:, :, w_in_perm_index()])
    sh["ident_bf"] = np.eye(128, dtype=np.float32).astype(NPBF)
    sh["ident_f"] = np.eye(128, dtype=np.float32)
    sh["w_out"] = f(inp["w_out"]); sh["ffn_up"] = f(inp["ffn_up"]); sh["ffn_down"] = f(inp["ffn_down"])
    sh["conv_w_fm"] = f(np.asarray(inp["ffn_conv_w"]).reshape(DEPTH, 3, 44, 128).transpose(0, 3, 1, 2))
    sh["conv_b_fm"] = f(np.asarray(inp["ffn_conv_b"]).reshape(DEPTH, 44, 128).transpose(0, 2, 1))
    sh.update(nsa_host_consts())
    sh["rel_bias"] = f(inp["rel_bias"])
    sh["nsa_pe_kT"] = f(np.asarray(inp["nsa_pe_k"]).transpose(0, 2, 1))
    sh["nsa_pe_vT"] = f(np.asarray(inp["nsa_pe_v"]).transpose(0, 2, 1))
    for n in ("nsa_ck_w1", "nsa_cv_w1", "nsa_ck_w2", "nsa_cv_w2"):
        sh[n] = f(inp[n])
    sh["fox_b_f"] = f(np.asarray(inp["fox_b_f"]).reshape(DEPTH, 6, 1))
    sh.update(rwkv_host(inp))
    return sh


def prep_core(inp, b):
    d = {}
    d["x"] = np.ascontiguousarray(np.asarray(inp["x"][b], dtype=np.float32))
    d["cT"] = np.ascontiguousarray(np.asarray(inp["c"][b], dtype=np.float32).reshape(8, 128).T)
    return d


def setup_fox(k):
    k.fox_bf = k.inp("fox_b_f", [DEPTH, 6, 1])
    k.CUMA = k.scratch("CUMA", [6, 3, S_LEN], BF16)


def stage_fox(k, l):
    nc = k.nc
    with Scope(k) as sc:
        nb = sc.sb("nb", [128, 32, 6], F32)
        with Scope(k) as s2:
            fl = s2.sb("fl", [6, S_LEN], F32)
            t1 = s2.sb("t1", [6, S_LEN], F32)
            ones = s2.sb("ones", [6, S_LEN], F32)
            cum = s2.sb("cum", [6, S_LEN], F32)
            parts = s2.sb("parts", [6, 3, S_LEN], BF16)
            bfv = s2.sb("bfv", [6, 2], F32)
            psn = s2.ps("psn", [128, 512])
            k.dma('sp', fl[:], k.FL, w=[fl])
            k.dma('sp', bfv[:, 0:1], k.fox_bf[l], w=[bfv])
            k.ts('dve', bfv[:, 1:2], bfv[:, 0:1], -1.0, None, ALU.mult, None, r=[bfv], w=[bfv])
            k.S.op('pool', lambda: nc.gpsimd.memset(ones[:], 1.0), [], [ones])
            k.act(t1[:], fl[:], AF.Exp, r=[fl, bfv], w=[t1], bias=bfv[:, 1:2], scale=-1.0)
            k.act(t1[:], t1[:], AF.Ln, r=[t1], w=[t1], bias=1.0, scale=1.0)
            k.ts('dve', t1[:], t1[:], -1.0, None, ALU.mult, None, r=[t1], w=[t1])
            k.S.op('dve', lambda: nc.vector.tensor_tensor_scan(out=cum[:], data0=ones[:], data1=t1[:], initial=0.0,
                                                               op0=ALU.mult, op1=ALU.add), [ones, t1], [cum])
            for t in range(32):
                k.transpose(psn[:, t * 6:(t + 1) * 6], cum[:, t * 128:(t + 1) * 128], k.ident_f[0:6, 0:6],
                            r=[cum, k.ident_f], w=[psn])
            k.ts('dve', nb[:].rearrange("p t h -> p (t h)"), psn[:, 0:192], -1.0, None, ALU.mult, None, r=[psn], w=[nb])
            k.ts('dve', t1[:], cum[:], 8.0, None, ALU.mult, None, r=[cum], w=[t1])
            k.copy('dve', parts[:, 0, :], t1[:], r=[t1], w=[parts])
            k.tt('dve', t1[:], t1[:], parts[:, 0, :], ALU.subtract, r=[t1, parts], w=[t1])
            k.copy('dve', parts[:, 1, :], t1[:], r=[t1], w=[parts])
            k.tt('dve', t1[:], t1[:], parts[:, 1, :], ALU.subtract, r=[t1, parts], w=[t1])
            k.copy('dve', parts[:, 2, :], t1[:], r=[t1], w=[parts])
            k.dma('sp', k.CUMA, parts[:], r=[parts], w=["CUMA"])
        QA = [sc.sb("QA%d" % i, [128, S_LEN], BF16) for i in range(2)]
        KA = [sc.sb("KA%d" % i, [128, S_LEN], BF16) for i in range(2)]
        VA = sc.sb("VA", [128, 32, 6, 65], BF16)
        yb = sc.sb("yb", [128, 32, 384], BF16)
        PTl = [sc.sb("PTl%d" % i, [128, 512], BF16) for i in range(6)]
        rc = [sc.sb("rc%d" % i, [128, 4], F32) for i in range(2)]
        psS = [sc.ps("psS%d" % i, [128, 512]) for i in range(4)]
        psO = [sc.ps("psO%d" % i, [128, 512]) for i in range(2)]
        k.dma('sp', yb[:], k.VT[:, 0:384].rearrange("(t p) c -> p t c", p=128), w=[yb])
        k.S.op('pool', lambda: nc.gpsimd.memset(VA[:, :, :, 64:65], 1.0), [], [VA])
        k.copy('dve', VA[:, :, :, 0:64], yb[:].rearrange("p t (h d) -> p t h d", h=6), r=[yb], w=[VA])
        for i in range(2):
            k.S.op('dve', lambda i=i: nc.vector.memset(KA[i][64:67, :], 1.0), [], [KA[i]])
        nS = 0
        nO = 0
        nP = 0
        pipe = Pipe(3)
        for h in range(6):
            qa = QA[h % 2]; ka = KA[h % 2]
            k.dma('sp', qa[0:64, :], k.QKT[h * 64:(h + 1) * 64, :], w=[qa])
            k.dma('sp', qa[64:67, :], k.CUMA[h], w=[qa])
            k.dma('sp', ka[0:64, :], k.QKT[384 + h * 64:384 + (h + 1) * 64, :], w=[ka])
            for qb in range(NTB):
                po = psO[nO % 2]; nO += 1
                nkt = 4 * qb + 4
                for kt in range(nkt):
                    j = kt - 4 * qb
                    c0 = max(j, 0) * 128
                    ps = psS[nS % len(psS)]; nS += 1
                    pt = PTl[nP % len(PTl)]; nP += 1

                    def first(ps=ps, pt=pt, kt=kt, c0=c0, j=j, qa=qa, ka=ka, qb=qb, h=h):
                        k.mm(ps[:, c0:512], [(ka[0:67, kt * 128:(kt + 1) * 128], qa[0:67, qb * 512 + c0:(qb + 1) * 512])],
                             r=[ka, qa], w=[ps])
                        k.act(pt[:, c0:512], ps[:, c0:512], AF.Exp, r=[ps, nb], w=[pt], bias=nb[:, kt, h:h + 1], scale=0.125)
                        if j >= 0:
                            k.S.op('pool', lambda: nc.gpsimd.affine_select(
                                out=pt[:, c0:c0 + 128], in_=pt[:, c0:c0 + 128], pattern=[[1, 128]], compare_op=ALU.is_ge,
                                fill=0.0, base=0, channel_multiplier=-1), [pt], [pt])

                    def second(pt=pt, kt=kt, j=j, po=po, qb=qb, h=h, last=(kt == nkt - 1)):
                        fns = []
                        for qs in range(max(j, 0), 4):
                            fns.append(lambda qs=qs: nc.tensor.matmul(
                                po[:, qs * 65:(qs + 1) * 65], lhsT=pt[:, qs * 128:(qs + 1) * 128], rhs=VA[:, kt, h, :],
                                start=(kt == 0 and qs == 0), stop=(kt == 4 * qb + qs), skip_group_check=True))
                        k.S.pe_group(fns, [pt, VA], [po])
                        if last:
                            r_ = rc[qb % 2]
                            pov = po[:, 0:260].rearrange("p (q c) -> p q c", c=65)
                            k.S.op('dve', lambda: nc.vector.reciprocal(out=r_[:], in_=pov[:, :, 64]), [po], [r_])
                            for qs in range(4):
                                k.ts('dve', yb[:, qb * 4 + qs, h * 64:(h + 1) * 64], po[:, qs * 65:qs * 65 + 64], r_[:, qs:qs + 1], None,
                                     ALU.mult, None, r=[po, r_], w=[yb])
                    pipe.push(first, second)
        pipe.flush()
        k.dma('sp', k.Y[:, 256:640].rearrange("(t p) c -> p t c", p=128), yb[:], r=[yb], w=["Y"])


RW_BPRIO = True
LW = 1536
LC = 4608
NEG8 = -240000.0


def t5_bucket_np(n):
    n = np.maximum(n, 0)
    nf = np.maximum(n, 1).astype(np.float32)
    large = 16 + (np.log(nf / np.float32(16)) / np.float32(np.log(128 / 16)) * np.float32(16)).astype(np.int32)
    large = np.minimum(large, 31)
    return np.where(n < 16, n, large)


def nsa_host_consts():
    c = {}
    i = np.arange(LW); n = i - 511
    oh = np.zeros((33, LW), np.float32)
    ok = (n >= 0) & (n < 512)
    oh[t5_bucket_np(n)[ok], i[ok]] = 1.0
    oh[32, ~ok] = NEG8
    c["oh_w"] = oh
    i = np.arange(LC); n = i - 2063
    oh = np.zeros((33, LC), np.float32)
    ok = n >= 0
    oh[t5_bucket_np(n)[ok], i[ok]] = 1.0
    oh[32, ~ok] = NEG8
    c["oh_c"] = oh
    s_ = np.arange(S_LEN)
    c["E_all"] = (np.arange(64)[:, None] == (s_[None, :] // 64)).astype(np.float32).astype(NPBF)
    cs = np.arange(256) * 16
    ce = cs + 31
    ss = np.arange(64) * 64
    ov = ((cs[:, None] <= ss[None, :] + 63) & (ce[:, None] >= ss[None, :])).astype(np.float32)
    ov[255] = 0.0
    c["ovl"] = np.ascontiguousarray(ov.reshape(2, 128, 64).transpose(1, 0, 2)).astype(NPBF)
    t = np.arange(S_LEN)
    cur = t // 64
    jb = np.arange(64)
    back = cur[:, None] - jb[None, :]
    valid = back >= 0
    forced = (jb[None, :] == 0) | (valid & (back < 2))
    tkm = (valid & ~forced).astype(np.float32)
    tka = np.where(valid, np.where(forced, 1e4, 0.0), -1.0).astype(np.float32)
    c["tkm"] = np.ascontiguousarray(tkm.reshape(32, 128, 64).transpose(1, 0, 2)).astype(NPBF)
    c["tka"] = np.ascontiguousarray(tka.reshape(32, 128, 64).transpose(1, 0, 2)).astype(NPBF)
    return c


def setup_nsa(k):
    nc = k.nc
    k.rel_bias = k.inp("rel_bias", [32, 6])
    k.oh_w = k.inp("oh_w", [33, LW])
    k.oh_c = k.inp("oh_c", [33, LC])
    k.E_d = k.inp("E_all", [64, S_LEN], BF16)
    k.ovl_d = k.inp("ovl", [128, 2, 64], BF16)
    k.tkm_d = k.inp("tkm", [128, 32, 64], BF16)
    k.tka_d = k.inp("tka", [128, 32, 64], BF16)
    k.pe_kT = k.inp("nsa_pe_kT", [DEPTH, 64, 32])
    k.pe_vT = k.inp("nsa_pe_vT", [DEPTH, 64, 32])
    k.ck_w1 = k.inp("nsa_ck_w1", [DEPTH, 2048, 128])
    k.cv_w1 = k.inp("nsa_cv_w1", [DEPTH, 2048, 128])
    k.ck_w2 = k.inp("nsa_ck_w2", [DEPTH, 128, 64])
    k.cv_w2 = k.inp("nsa_cv_w2", [DEPTH, 128, 64])
    k.WVW = k.scratch("WVW", [6, 128, LW], BF16)
    k.WVC = k.scratch("WVC", [6, 128, LC], BF16)
    with Scope(k) as sc:
        rb = sc.sb("rb", [33, 6], F32)
        rb31 = sc.sb("rb31", [32, 6], F32)
        rrep = sc.sb("rrep", [33, 6, 128], F32)
        ohw = sc.sb("ohw", [33, LW], F32)
        ohc = sc.sb("ohc", [33, LC], F32)
        ps = [sc.ps("psb%d" % i, [128, 512]) for i in range(2)]
        ev = [sc.sb("evb%d" % i, [128, 512], BF16) for i in range(2)]
        k.dma('sp', rb[0:32, :], k.rel_bias, w=[rb])
        k.dma('sp', rb31[:], k.rel_bias[31:32, :].broadcast_to([32, 6]), w=[rb31])
        k.dma('sp', ohw[:], k.oh_w, w=[ohw])
        k.dma('sp', ohc[:], k.oh_c, w=[ohc])
        k.S.op('dve', lambda: nc.vector.memset(rb[32:33, :], 1.0), [], [rb])
        k.tt('dve', rb[0:32, :], rb[0:32, :], rb31[:], ALU.subtract, r=[rb, rb31], w=[rb])
        k.ts('dve', rb[0:32, :], rb[0:32, :], 8.0, None, ALU.mult, None, r=[rb], w=[rb])
        for h in range(6):
            k.copy('dve', rrep[:, h, :], rb[:, h:h + 1].to_broadcast([33, 128]), r=[rb], w=[rrep])
        n = 0
        for h in range(6):
            for (oh, L, dst) in ((ohw, LW, k.WVW), (ohc, LC, k.WVC)):
                for c0 in range(0, L, 512):
                    p_ = ps[n % 2]; e_ = ev[n % 2]; n += 1
                    k.mm(p_[:], [(rrep[:, h, :], oh[:, c0:c0 + 512])], r=[rrep, oh], w=[p_])
                    k.copy('act' if n % 2 else 'dve', e_[:], p_[:], r=[p_], w=[e_])
                    k.dma('sp', dst[h, :, c0:c0 + 512], e_[:], r=[e_])


class DbgStop(Exception):
    pass


def dbg(k, lvl):
    if getattr(k, 'dbg_stop', None) == lvl:
        raise DbgStop()


def stage_nsa(k, l):
    nc = k.nc
    with Scope(k) as sc:
        Gw = sc.sb("Gw", [128, 6, 1408], BF16)
        Gc = sc.sb("Gc", [128, 6, 2560], BF16)
        tkm = sc.sb("tkm", [128, 32, 64], BF16)
        tka = sc.sb("tka", [128, 32, 64], BF16)
        QN = [sc.sb("QN%d" % h, [128, S_LEN], BF16) for h in range(6)]
        KE = [sc.sb("KE%d" % g, [128, S_LEN], BF16) for g in range(2)]
        KW = sc.sb("KW", [128, S_LEN], BF16)
        VS = sc.sb("VS", [128, 32, 2, 65], BF16)
        VW = sc.sb("VW", [128, 32, 2, 65], BF16)
        KCMP = sc.sb("KCMP", [128, 256], BF16)
        VE = sc.sb("VE", [128, 2, 2, 129], BF16)
        sg = sc.sb("sg", [128, 32, 18], F32)
        for h in range(6):
            k.dma('sp', Gw[:, h, :], bass.AP(k.WVW.tensor, h * 128 * LW + 127, [[LW - 1, 128], [1, 1408]]), w=[Gw])
            k.dma('sp', Gc[:, h, :], bass.AP(k.WVC.tensor, h * 128 * LC + 2032, [[LC - 16, 128], [1, 2560]]), w=[Gc])
        k.dma('sp', tkm[:], k.tkm_d, w=[tkm])
        k.dma('sp', tka[:], k.tka_d, w=[tka])
        for h in range(6):
            g_, hp_ = h // 3, h % 3
            k.dma('sp', QN[h][g_ * 64:(g_ + 1) * 64, :], k.QKT[768 + hp_ * 128 + g_ * 64:768 + hp_ * 128 + (g_ + 1) * 64, :], w=[QN[h]])
            k.S.op('pool', lambda h=h, g_=g_: nc.gpsimd.memset(QN[h][(1 - g_) * 64:(2 - g_) * 64, :], 0.0), [], [QN[h]])
        for g_ in range(2):
            k.dma('sp', KE[g_][g_ * 64:(g_ + 1) * 64, :], k.QKT[1408 + g_ * 64:1408 + (g_ + 1) * 64, :], w=[KE[g_]])
            k.dma('sp', KE[g_][(1 - g_) * 64:(2 - g_) * 64, :], k.E_d, w=[KE[g_]])
        k.dma('sp', KW[:], k.QKT[1536:1664, :], w=[KW])
        k.dma('sp', sg[:], k.GT.rearrange("(t p) c -> p t c", p=128), w=[sg])
        k.act(sg[:], sg[:], AF.Exp, r=[sg], w=[sg], scale=-1.0)
        k.ts('dve', sg[:], sg[:], 1.0, None, ALU.add, None, r=[sg], w=[sg])
        k.S.op('dve', lambda: nc.vector.reciprocal(out=sg[:], in_=sg[:]), [sg], [sg])
        k.dma('sp', VE[:, 0, :, 65:129], k.ovl_d, w=[VE])
        k.dma('sp', VE[:, 1, :, 65:129], k.ovl_d, w=[VE])
        k.S.op('pool', lambda: nc.gpsimd.memset(VE[:, :, :, 64:65], 1.0), [], [VE])
        k.S.op('pool', lambda: nc.gpsimd.memset(VE[:, :, :, 0:64], 0.0), [], [VE])
        k.S.op('pool', lambda: nc.gpsimd.memset(KCMP[:], 0.0), [], [KCMP])
        dbg(k, 1)
        with Scope(k) as s2:
            vst = s2.sb("vst", [128, 32, 256], BF16)
            k.dma('sp', vst[:], k.VT[:, 384:640].rearrange("(t p) c -> p t c", p=128), w=[vst])
            k.S.op('pool', lambda: nc.gpsimd.memset(VS[:, :, :, 64:65], 1.0), [], [VS])
            k.S.op('pool', lambda: nc.gpsimd.memset(VW[:, :, :, 64:65], 1.0), [], [VW])
            k.copy('dve', VS[:, :, :, 0:64], vst[:, :, 0:128].rearrange("p t (g d) -> p t g d", g=2), r=[vst], w=[VS])
            k.copy('pool', VW[:, :, :, 0:64], vst[:, :, 128:256].rearrange("p t (g d) -> p t g d", g=2), r=[vst], w=[VW])
        dbg(k, 2)
        with Scope(k) as s2:
            KC = s2.sb("KC", [128, S_LEN], BF16)
            VC = s2.sb("VC", [128, S_LEN], BF16)
            k.dma('sp', KC[:], k.QKT[1152:1280, :], w=[KC])
            k.dma('sp', VC[:], k.QKT[1280:1408, :], w=[VC])
            w1s = s2.sb("w1s", [128, 16, 128], F32)
            w1b = [s2.sb("w1b%d" % i, [128, 32, 128], BF16) for i in range(2)]
            w2s = s2.sb("w2s", [128, 2, 64], F32)
            w2b = s2.sb("w2b", [128, 2, 64], BF16)
            pes = s2.sb("pes", [128, 2, 32], F32)
            peb = s2.sb("peb", [128, 2, 32], BF16)
            hb = s2.sb("hb", [128, 2], F32)
            gx = s2.sb("gx", [128, 256], F32)
            gu = s2.sb("gu", [128, 256], F32)
            gg = s2.sb("gg", [128, 256], BF16)
            psh = s2.ps("psh", [128, 512])
            psb_ = s2.ps("pshb", [128, 512])
            pso = s2.ps("pso", [128, 512])
            for kv, (w1d, w2d, ped) in enumerate(((k.ck_w1, k.ck_w2, k.pe_kT), (k.cv_w1, k.cv_w2, k.pe_vT))):
                for lh in range(2):
                    for half in range(2):
                        k.dma('sp', w1s[half * 64:(half + 1) * 64, :, :],
                              w1d[l, lh * 1024:(lh + 1) * 1024, :].rearrange("(l d) h -> d l h", d=64), w=[w1s])
                    k.copy('dve' if lh == 0 else 'act', w1b[kv][:, lh * 16:(lh + 1) * 16, :], w1s[:], r=[w1s], w=[w1b[kv]])
                for half in range(2):
                    k.dma('sp', pes[half * 64:(half + 1) * 64, kv, :], ped[l], w=[pes])
                k.dma('sp', w2s[:, kv, :], w2d[l], w=[w2s])
            k.copy('dve', w2b[:], w2s[:], r=[w2s], w=[w2b])
            w2kd = s2.sb("w2kd", [128, 2, 64], BF16)
            for a_ in range(2):
                k.copy('dve', w2kd[:, a_, :], w2s[:, 0, :], r=[w2s], w=[w2kd])
            k.copy('dve', peb[:], pes[:], r=[pes], w=[peb])
            for kv in range(2):
                src = KC if kv == 0 else VC
                k.mm(psb_[:, kv:kv + 1], [(w1b[kv][0:64, li, :], peb[0:64, kv, li:li + 1]) for li in range(32)],
                     r=[w1b[kv], peb], w=[psb_], start=True)
                k.copy('dve', hb[:, kv:kv + 1], psb_[:, kv:kv + 1], r=[psb_], w=[hb])
                for g in range(2):
                    pr = slice(g * 64, (g + 1) * 64)
                    k.mm(psh[:, 0:255], [(w1b[kv][pr, li, :], src[pr, li:li + 16 * 254 + 1:16]) for li in range(32)],
                         r=[w1b[kv], src], w=[psh])
                    k.ts('dve', gx[:, 0:255], psh[:, 0:255], hb[:, kv:kv + 1], None, ALU.add, None, r=[psh, hb], w=[gx])
                    k.tt('dve', gu[:, 0:255], gx[:, 0:255], gx[:, 0:255], ALU.mult, r=[gx], w=[gu])
                    k.ts('dve', gu[:, 0:255], gu[:, 0:255], 0.044715, 1.0, ALU.mult, ALU.add, r=[gu], w=[gu])
                    k.tt('dve', gu[:, 0:255], gu[:, 0:255], gx[:, 0:255], ALU.mult, r=[gu, gx], w=[gu])
                    k.act(gu[:, 0:255], gu[:, 0:255], AF.Exp, r=[gu], w=[gu], scale=-2.0 * 0.7978845608028654)
                    k.ts('dve', gu[:, 0:255], gu[:, 0:255], 1.0, None, ALU.add, None, r=[gu], w=[gu])
                    k.S.op('dve', lambda: nc.vector.reciprocal(out=gu[:, 0:255], in_=gu[:, 0:255]), [gu], [gu])
                    k.S.op('dve', lambda: nc.vector.memset(gg[:, 255:256], 0.0), [], [gg])
                    k.tt('dve', gg[:, 0:255], gu[:, 0:255], gx[:, 0:255], ALU.mult, r=[gu, gx], w=[gg])
                    if kv == 0:
                        k.mm(pso[:, 0:256], [(w2kd[:].rearrange("p a d -> p (a d)"), gg[:, 0:256])], r=[w2kd, gg], w=[pso])
                        k.copy('dve', KCMP[pr, :], pso[pr, 0:256], r=[pso], w=[KCMP])
                    else:
                        for ct in range(2):
                            k.mm(pso[:, ct * 64:(ct + 1) * 64], [(gg[:, ct * 128:(ct + 1) * 128], w2b[:, 1, :])],
                                 r=[w2b, gg], w=[pso], start=(ct == 0))
                        k.copy('dve', VE[:, g, :, 0:64], pso[:, 0:128].rearrange("p (c d) -> p c d", c=2), r=[pso], w=[VE])
        dbg(k, 3)
        PTl = [sc.sb("PTn%d" % i, [128, 512], BF16) for i in range(6)]
        yacc = [sc.sb("yacc%d" % i, [128, 4, 384], F32) for i in range(2)]
        ybf = [sc.sb("ybf%d" % i, [128, 4, 384], BF16) for i in range(2)]
        impt2 = [[sc.sb("impt%d_%d" % (i, g), [128, 4, 64], F32) for g in range(2)] for i in range(2)]
        scr = sc.sb("scr", [128, 4, 64], F32)
        wk = sc.sb("wk", [128, 4, 64], F32)
        m8 = sc.sb("m8", [128, 4, 16], F32)
        nmq = sc.sb("nmq", [128, 4, 128], BF16)
        rcs = [sc.sb("rcs%d" % i, [128, 8], F32) for i in range(3)]
        psS = [sc.ps("psS%d" % i, [128, 512]) for i in range(4)]
        psO = [sc.ps("psO%d" % i, [128, 512]) for i in range(3)]
        psT = sc.ps("psTn", [128, 1024], BF16)
        st = {"S": 0, "O": 0, "P": 0, "R": 0}

        def q_ap(h, c0, c1):
            g, hp = h // 3, h % 3
            return QN[h][g * 64:(g + 1) * 64, c0:c1]

        def evac(views, h, branch, qb, ya, first):
            r_ = rcs[st["R"] % 3]; st["R"] += 1
            for qs, (po, cb) in enumerate(views):
                if branch == 0:
                    k.ts('dve', r_[:, qs:qs + 1], po[:, cb + 64:cb + 65], 1e-30, None, ALU.max, None, r=[po], w=[r_])
                    k.S.op('dve', lambda r_=r_, qs=qs: nc.vector.reciprocal(out=r_[:, qs:qs + 1], in_=r_[:, qs:qs + 1]), [r_], [r_])
                else:
                    k.S.op('dve', lambda r_=r_, po=po, cb=cb, qs=qs: nc.vector.reciprocal(out=r_[:, qs:qs + 1], in_=po[:, cb + 64:cb + 65]), [po], [r_])
            k.tt('dve', r_[:, 4:8], r_[:, 0:4], sg[:, qb * 4:(qb + 1) * 4, h * 3 + branch], ALU.mult, r=[r_, sg], w=[r_])
            for qs, (po, cb) in enumerate(views):
                o = ya[:, qs, h * 64:(h + 1) * 64]
                if first:
                    k.ts('dve', o, po[:, cb:cb + 64], r_[:, 4 + qs:5 + qs], None, ALU.mult, None, r=[po, r_], w=[ya])
                else:
                    k.stt('dve', o, po[:, cb:cb + 64], r_[:, 4 + qs:5 + qs], o, ALU.mult, ALU.add, r=[po, r_, ya], w=[ya])
            return r_

        pipe = Pipe(3)

        def attend(h, qb, tiles, kmat, vmat, po, g, branch, ya, merged=False):
            hp = h % 3
            nt = len(tiles)
            state = {"first": True}
            for idx, (kt, c0, c1, extra) in enumerate(tiles):
                ps = psS[st["S"] % len(psS)]; st["S"] += 1
                pt = PTl[st["P"] % len(PTl)]; st["P"] += 1

                def first(ps=ps, pt=pt, kt=kt, c0=c0, c1=c1, extra=extra):
                    if merged:
                        fns = [lambda: nc.tensor.matmul(ps[:, c0:c1], lhsT=kmat[:, kt * 128:(kt + 1) * 128],
                                                        rhs=QN[h][:, qb * 512 + c0:qb * 512 + c1],
                                                        start=True, stop=(len(extra) == 0), skip_group_check=True)]
                    else:
                        fns = [lambda: nc.tensor.matmul(ps[:, c0:c1], lhsT=kmat[g * 64:(g + 1) * 64, kt * 128:(kt + 1) * 128],
                                                        rhs=QN[h][g * 64:(g + 1) * 64, qb * 512 + c0:qb * 512 + c1],
                                                        start=True, stop=(len(extra) == 0), skip_group_check=True)]
                    rd = [kmat, QN[h]]
                    for ei, (lt, rt, lap, rap) in enumerate(extra):
                        w_ = rap.shape[-1]
                        fns.append(lambda lap=lap, rap=rap, w_=w_, ei=ei: nc.tensor.matmul(
                            ps[:, c0:c0 + w_], lhsT=lap, rhs=rap, start=False, stop=(ei == len(extra) - 1), skip_group_check=True))
                        rd += [lt, rt]
                    k.S.pe_group(fns, rd, [ps])
                    k.act(pt[:, c0:c1], ps[:, c0:c1], AF.Exp, r=[ps], w=[pt], scale=0.125)

                def second(pt=pt, kt=kt, c0=c0, c1=c1, idx=idx):
                    fns = []
                    for qs in range(c0 // 128, (c1 + 127) // 128):
                        last = all(not (t2[1] <= qs * 128 < t2[2]) for t2 in tiles[idx + 1:])
                        fo = state["first"]
                        state["first"] = False
                        fns.append(lambda qs=qs, fo=fo, last=last: nc.tensor.matmul(
                            po[:, qs * 65:(qs + 1) * 65], lhsT=pt[:, qs * 128:(qs + 1) * 128], rhs=vmat[:, kt, g, :],
                            start=fo, stop=last, skip_group_check=True))
                    k.S.pe_group(fns, [pt, vmat], [po])
                    if idx == nt - 1:
                        evac([(po, qs * 65) for qs in range(4)], h, branch, qb, ya, False)
                pipe.push(first, second)

        def do_cmp(qb):
            ya = yacc[qb % 2]
            impt = impt2[qb % 2]
            for h in range(6):
                g = h // 3
                poA = psO[st["O"] % 3]; st["O"] += 1
                poB = psO[st["O"] % 3]; st["O"] += 1
                cts = [0] + ([1] if qb >= 4 else [])
                state = {"A": True, "B": True}
                for ct in cts:
                    delta = 512 * qb - 2048 * ct
                    ps = psS[st["S"] % len(psS)]; st["S"] += 1
                    pt = PTl[st["P"] % len(PTl)]; st["P"] += 1

                    def first(ps=ps, pt=pt, ct=ct, delta=delta, g=g, h=h):
                        pairs = [(KCMP[g * 64:(g + 1) * 64, ct * 128:(ct + 1) * 128], q_ap(h, qb * 512, (qb + 1) * 512))]
                        rd = [KCMP, QN[h]]
                        if delta < 2560:
                            pairs.append((k.ident_bf[:], Gc[:, h, delta:delta + 512])); rd += [k.ident_bf, Gc]
                        k.mm(ps[:], pairs, r=rd, w=[ps])
                        k.act(pt[:], ps[:], AF.Exp, r=[ps], w=[pt], scale=0.125)

                    def second(pt=pt, ct=ct, g=g, h=h, poA=poA, poB=poB, state=state, lastct=(ct == cts[-1])):
                        fns = []
                        for qs in range(4):
                            po, cb = (poA, qs * 129) if qs < 3 else (poB, 0)
                            key = "A" if qs < 3 else "B"
                            stt_ = state[key]
                            state[key] = False
                            fns.append(lambda qs=qs, po=po, cb=cb, stt_=stt_: nc.tensor.matmul(
                                po[:, cb:cb + 129], lhsT=pt[:, qs * 128:(qs + 1) * 128], rhs=VE[:, g, ct, :],
                                start=stt_, stop=lastct, skip_group_check=True))
                        k.S.pe_group(fns, [pt, VE], [poA, poB])
                        if lastct:
                            views = [(poA, 0), (poA, 129), (poA, 258), (poB, 0)]
                            r_ = evac(views, h, 0, qb, ya, True)
                            for qs, (po, cb) in enumerate(views):
                                o = impt[g][:, qs, :]
                                if h % 3 == 0:
                                    k.ts('dve', o, po[:, cb + 65:cb + 129], r_[:, qs:qs + 1], None, ALU.mult, None, r=[po, r_], w=[impt[g]])
                                else:
                                    k.stt('dve', o, po[:, cb + 65:cb + 129], r_[:, qs:qs + 1], o, ALU.mult, ALU.add, r=[po, r_, impt[g]], w=[impt[g]])
                    pipe.push(first, second)

        def do_topk(qb):
            impt = impt2[qb % 2]
            for g in range(2):
                k.tt('dve', scr[:], impt[g][:], tkm[:, qb * 4:(qb + 1) * 4, :], ALU.mult, r=[impt[g], tkm], w=[scr])
                k.tt('dve', scr[:], scr[:], tka[:, qb * 4:(qb + 1) * 4, :], ALU.add, r=[scr, tka], w=[scr])
                for qs in range(4):
                    k.S.op('dve', lambda qs=qs: nc.vector.max(out=m8[:, qs, 0:8], in_=scr[:, qs, :]), [scr], [m8])
                    k.S.op('dve', lambda qs=qs: nc.vector.match_replace(out=wk[:, qs, :], in_to_replace=m8[:, qs, 0:8],
                                                                        in_values=scr[:, qs, :], imm_value=-1e9), [scr, m8], [wk])
                    k.S.op('dve', lambda qs=qs: nc.vector.max(out=m8[:, qs, 8:16], in_=wk[:, qs, :]), [wk], [m8])
                    k.ts('dve', wk[:, qs, :], scr[:, qs, :], m8[:, qs, 15:16], 1.0, ALU.is_ge, ALU.subtract, r=[scr, m8, wk], w=[wk])
                k.ts('dve', nmq[:, :, 0:64], wk[:], -NEG8, None, ALU.mult, None, r=[wk], w=[nmq])
                k.ts('pool', nmq[:, :, 64:128], wk[:], -NEG8, None, ALU.mult, None, r=[wk], w=[nmq])
                for qs in range(4):
                    k.transpose(psT[:, qs * 128:(qs + 1) * 128], nmq[:, qs, :], k.ident_bf[:], r=[nmq, k.ident_bf], w=[psT])
                oh = (1 - g) * 64
                for hh in range(3 * g, 3 * g + 3):
                    k.copy('act' if hh % 2 else 'dve', QN[hh][oh:oh + 64, qb * 512:(qb + 1) * 512], psT[oh:oh + 64, 0:512], r=[psT], w=[QN[hh]])

        def do_win(qb):
            ya = yacc[qb % 2]
            for h in range(6):
                g = h // 3
                po = psO[st["O"] % 3]; st["O"] += 1
                tiles = []
                for kt in range(max(0, 4 * qb - 4), 4 * qb + 4):
                    delta = 512 * qb - 128 * kt
                    c0 = max(-delta, 0)
                    c1 = min(512, 640 - delta) if delta > 0 else 512
                    tiles.append((kt, c0, c1, [(k.ident_bf, Gw, k.ident_bf[:], Gw[:, h, delta + 384 + c0:delta + 384 + c1])]))
                attend(h, qb, tiles, KW, VW, po, g, 2, ya)

        def do_slc(qb):
            ya = yacc[qb % 2]
            for h in range(6):
                g = h // 3
                po = psO[st["O"] % 3]; st["O"] += 1
                tiles = []
                for kt in range(0, 4 * qb + 4):
                    delta = 512 * qb - 128 * kt
                    c0 = max(-delta, 0)
                    ex = []
                    if delta <= 128:
                        c1b = 256 if delta == 128 else 512
                        ex.append((k.ident_bf, Gw, k.ident_bf[:], Gw[:, h, delta + 384 + c0:delta + 384 + c1b]))
                    tiles.append((kt, c0, 512, ex))
                attend(h, qb, tiles, KE[g], VS, po, g, 1, ya, merged=True)

        qbs = list(getattr(k, 'dbg_qbs', range(NTB)))
        do_cmp(qbs[0])
        pipe.flush()
        do_topk(qbs[0])
        for i, qb in enumerate(qbs):
            do_win(qb)
            if i + 1 < len(qbs):
                do_cmp(qbs[i + 1])
                pipe.flush()
                do_topk(qbs[i + 1])
            do_slc(qb)
            pipe.flush()
            ya = yacc[qb % 2]
            yb_ = ybf[qb % 2]
            k.copy('pool', yb_[:], ya[:], r=[ya], w=[yb_])
            k.dma('sp', k.Y[qb * 512:(qb + 1) * 512, 640:1024].rearrange("(q p) c -> p q c", p=128), yb_[:], r=[yb_])


def setup_ffn(k):
    k.w_out = k.inp("w_out", [DEPTH, D, D])
    k.ffn_up = k.inp("ffn_up", [DEPTH, D, 2 * D_FF])
    k.ffn_down = k.inp("ffn_down", [DEPTH, D_FF, D])
    k.conv_w = k.inp("conv_w_fm", [DEPTH, 128, 3, 44])
    k.conv_b = k.inp("conv_b_fm", [DEPTH, 128, 44])


def load_cast_gen(k, stg, dst, src_rows, ncols, nchunks, col_split=1):
    w = ncols // col_split
    n = 0
    for c in range(nchunks):
        for cs in range(col_split):
            s = stg[n % len(stg)]
            k.dma('sp' if n % 2 == 0 else 'act', s[:, 0:w], src_rows(c)[:, cs * w:(cs + 1) * w], w=[s])
            k.copy('pool' if n % 2 == 0 else 'dve', dst[:, c, cs * w:(cs + 1) * w], s[:, 0:w], r=[s], w=[dst])
            n += 1
            yield


def load_cast(k, sc, dst, src_rows, ncols, nchunks, name, col_split=1):
    w = ncols // col_split
    stg = [sc.sb("%s_stg%d" % (name, i), [128, w], F32) for i in range(4)]
    for _ in load_cast_gen(k, stg, dst, src_rows, ncols, nchunks, col_split):
        pass


def rms_scale(k, ss, st):
    k.act(st[:, 0:1], ss, AF.Ln, r=[st], w=[st], bias=RMS_EPS)
    k.act(st[:, 1:2], st[:, 0:1], AF.Exp, r=[st], w=[st], scale=-0.5)


def stage_out(k, l, xsrc, xdst, bg=None, bg_steps=2):
    nc = k.nc
    with Scope(k) as sc:
        wo = sc.sb("wo", [128, 8, D], BF16)
        with Scope(k) as s2:
            load_cast(k, s2, wo, lambda c: k.w_out[l, c * 128:(c + 1) * 128, :], D, 8, "wo")
        yt = [sc.sb("yt%d" % i, [128, D], BF16) for i in range(2)]
        yT = [sc.sb("yT%d" % i, [128, 8, 128], BF16) for i in range(2)]
        xt = [sc.sb("xo%d" % i, [128, D], F32) for i in range(2)]
        tt_ = [sc.sb("to%d" % i, [128, D], F32) for i in range(2)]
        junk = sc.sb("junko", [128, 512], BF16)
        st = [sc.sb("sto%d" % i, [128, 4], F32) for i in range(2)]
        psT = [sc.ps("psTo%d" % i, [128, D], BF16) for i in range(2)]
        psY = [sc.ps("psYo%d" % i, [128, 512]) for i in range(4)]
        def T(ti):
            tok = slice(ti * 128, (ti + 1) * 128)
            y_ = yt[ti % 2]; yT_ = yT[ti % 2]; x_ = xt[ti % 2]; pT = psT[ti % 2]
            k.dma('act', y_[:], k.Y[tok, :], w=[y_])
            k.dma('act', x_[:], xsrc[tok, :], w=[x_])
            for kc in range(8):
                k.transpose(pT[:, kc * 128:(kc + 1) * 128], y_[:, kc * 128:(kc + 1) * 128], k.ident_bf[:], r=[y_, k.ident_bf], w=[pT])
            k.copy('act' if ti % 2 else 'dve', yT_[:].rearrange("p a b -> p (a b)"), pT[:], r=[pT], w=[yT_])

        def M(ti):
            tok = slice(ti * 128, (ti + 1) * 128)
            yT_ = yT[ti % 2]; x_ = xt[ti % 2]; t_ = tt_[ti % 2]; st_ = st[ti % 2]
            p0 = psY[(ti % 2) * 2]; p1 = psY[(ti % 2) * 2 + 1]
            for half, ps in enumerate((p0, p1)):
                k.mm(ps[:], [(yT_[:, kc, :], wo[:, kc, half * 512:(half + 1) * 512]) for kc in range(8)], r=[yT_, wo], w=[ps])
                k.act(junk[:], ps[:], AF.Square, r=[ps], w=[junk, st_], scale=1.0 / 32.0, accum=st_[:, 2 + half:3 + half])
            k.tt('dve', st_[:, 2:3], st_[:, 2:3], st_[:, 3:4], ALU.add, r=[st_], w=[st_])
            rms_scale(k, st_[:, 2:3], st_)
            for half, ps in enumerate((p0, p1)):
                cs = slice(half * 512, (half + 1) * 512)
                k.stt('dve', t_[:, cs], ps[:], st_[:, 1:2], k.gm_row[:, cs], ALU.mult, ALU.mult, r=[ps, st_, k.gm_row], w=[t_])
            k.tt('dve', t_[:], t_[:], x_[:], ALU.add, r=[t_, x_], w=[t_])
            k.dma('sp', xdst[tok, :], t_[:], r=[t_])

        T(0)
        for ti in range(32):
            if ti + 1 < 32:
                T(ti + 1)
            M(ti)
            for _ in range(bg_steps):
                if bg is not None:
                    try:
                        next(bg)
                    except StopIteration:
                        bg = None
        if bg is not None:
            for _ in bg:
                pass


def stage_out_ffn(k, l, xin, xmid, xdst):
    NCH = 22
    with Scope(k) as sc:
        wu = sc.sb("wu", [128, 8, 2 * D_FF], BF16)
        wd = sc.sb("wd", [128, NCH, D], BF16)
        with Scope(k) as s2:
            stg = [s2.sb("wstg%d" % i, [128, 1408], F32) for i in range(4)]

            def bg():
                yield from load_cast_gen(k, stg, wu, lambda c: k.ffn_up[l, c * 128:(c + 1) * 128, :], 2 * D_FF, 8, col_split=4)
                yield from load_cast_gen(k, stg, wd, lambda c: k.ffn_down[l, c * 128:(c + 1) * 128, :], D, NCH)
            stage_out(k, l, xin, xmid, bg=bg(), bg_steps=2)
        stage_ffn(k, l, xmid, xdst, pre=(sc, wu, wd))


def stage_ffn(k, l, xsrc, xdst, pre=None):
    nc = k.nc
    NCH = 22
    with ExitStack() as es_:
        if pre is None:
            sc = es_.enter_context(Scope(k))
            wu = sc.sb("wu", [128, 8, 2 * D_FF], BF16)
            wd = sc.sb("wd", [128, NCH, D], BF16)
            with Scope(k) as s2:
                load_cast(k, s2, wu, lambda c: k.ffn_up[l, c * 128:(c + 1) * 128, :], 2 * D_FF, 8, "wu", col_split=2)
                load_cast(k, s2, wd, lambda c: k.ffn_down[l, c * 128:(c + 1) * 128, :], D, NCH, "wd")
        else:
            sc, wu, wd = pre
        cw = sc.sb("cw", [128, 3, 44], F32)
        cb = sc.sb("cb", [128, 44], F32)
        hal = [sc.sb("hal%d" % i, [128, 44, 2], F32) for i in range(2)]
        k.dma('sp', cw[:], k.conv_w[l], w=[cw])
        k.dma('sp', cb[:], k.conv_b[l], w=[cb])
        k.S.op('pool', lambda: nc.gpsimd.memset(hal[1][:], 0.0), [], [hal[1]])
        actT = sc.sb("actT", [128, NCH, 512], BF16)
        HTs = [sc.sb("H2T%d" % i, [128, 8, 512], BF16) for i in range(2)]
        xt = [sc.sb("xf%d" % i, [128, D], F32) for i in range(2)]
        xn = sc.sb("xnf", [128, D], BF16)
        junk = sc.sb("junkf", [128, 512], BF16)
        st = [sc.sb("stf%d" % i, [128, 4], F32) for i in range(2)]
        Tg = [sc.sb("Tg%d" % i, [128, 512], F32) for i in range(2)]
        Tv = [sc.sb("Tv%d" % i, [128, 512], F32) for i in range(2)]
        psT = sc.ps("psTf", [128, D], BF16)
        psU = [sc.ps("psU%d" % i, [128, 512]) for i in range(4)]
        nxc = {"n": 0}

        def norm_gen(tb):
            HT = HTs[tb % 2]
            t0_ = tb * 4
            k.dma('act', xt[t0_ % 2][:], xsrc[t0_ * 128:(t0_ + 1) * 128, :], w=[xt[t0_ % 2]])
            yield
            for sub in range(4):
                ti = tb * 4 + sub
                x_ = xt[ti % 2]; st_ = st[ti % 2]
                k.act(xn[:], x_[:], AF.Square, r=[x_], w=[xn, st_], scale=1.0 / 32.0, accum=st_[:, 2:3])
                rms_scale(k, st_[:, 2:3], st_)
                k.ts('dve', xn[:], x_[:], st_[:, 1:2], None, ALU.mult, None, r=[x_, st_], w=[xn])
                if sub < 3:
                    k.dma('act', xt[(ti + 1) % 2][:], xsrc[(ti + 1) * 128:(ti + 2) * 128, :], w=[xt[(ti + 1) % 2]])
                yield
                yield
                for kc in range(8):
                    k.transpose(psT[:, kc * 128:(kc + 1) * 128], xn[:, kc * 128:(kc + 1) * 128], k.ident_bf[:], r=[xn, k.ident_bf], w=[psT])
                    if kc == 3:
                        yield
                yield
                for kc in range(8):
                    o = HT[:, kc, sub * 128:(sub + 1) * 128]
                    i_ = psT[:, kc * 128:(kc + 1) * 128]
                    if kc % 2 == 0:
                        k.ts('dve', o, i_, k.modAB[:, 16 + kc:17 + kc], k.modAB[:, 24 + kc:25 + kc], ALU.mult, ALU.add, r=[psT, k.modAB], w=[HT])
                    else:
                        k.act(o, i_, AF.Identity, r=[psT, k.modAB], w=[HT], scale=k.modAB[:, 16 + kc:17 + kc], bias=k.modAB[:, 24 + kc:25 + kc])
                yield

        def step(g):
            if g is not None:
                try:
                    next(g)
                except StopIteration:
                    return None
            return g

        xe = [sc.sb("xe%d" % i, [128, D], F32) for i in range(2)]
        ste = [sc.sb("ste%d" % i, [128, 4], F32) for i in range(2)]
        for _ in norm_gen(0):
            pass
        for tb in range(NTB):
            hin = hal[(tb + 1) % 2]; hout = hal[tb % 2]
            HT = HTs[tb % 2]
            g = norm_gen(tb + 1) if tb + 1 < NTB else None
            for cp in range(NCH):
                tg = Tg[cp % 2]; tv = Tv[cp % 2]
                for which, (T_, c_) in enumerate(((tg, cp), (tv, NCH + cp))):
                    ps = psU[(cp * 2 + which) % 4]
                    k.mm(ps[:], [(wu[:, kc, c_ * 128:(c_ + 1) * 128], HT[:, kc, :]) for kc in range(8)], r=[wu, HT], w=[ps])
                    k.act(T_[:], ps[:], AF.Identity, r=[ps, cw, cb], w=[T_], scale=cw[:, 2, c_:c_ + 1], bias=cb[:, c_:c_ + 1])
                    k.stt('dve', T_[:, 1:512], ps[:, 0:511], cw[:, 1, c_:c_ + 1], T_[:, 1:512], ALU.mult, ALU.add, r=[ps, cw, T_], w=[T_])
                    k.stt('dve', T_[:, 2:512], ps[:, 0:510], cw[:, 0, c_:c_ + 1], T_[:, 2:512], ALU.mult, ALU.add, r=[ps, cw, T_], w=[T_])
                    k.copy('act', hout[:, c_, :], ps[:, 510:512], r=[ps], w=[hout])
                    k.stt('dve', T_[:, 0:1], hin[:, c_, 1:2], cw[:, 1, c_:c_ + 1], T_[:, 0:1], ALU.mult, ALU.add, r=[hin, cw, T_], w=[T_])
                    k.stt('dve', T_[:, 0:2], hin[:, c_, 0:2], cw[:, 0, c_:c_ + 1], T_[:, 0:2], ALU.mult, ALU.add, r=[hin, cw, T_], w=[T_])
                k.act(tg[:], tg[:], AF.Silu, r=[tg], w=[tg])
                k.tt('pool', actT[:, cp, :], tg[:], tv[:], ALU.mult, r=[tg, tv], w=[actT])
                if cp >= 1:
                    g = step(g)
            for sub in range(4):
                ti = tb * 4 + sub
                tok = slice(ti * 128, (ti + 1) * 128)
                x_ = xe[sub % 2]; st_ = ste[sub % 2]
                k.dma('act', x_[:], xsrc[tok, :], w=[x_])
                psF = [psU[(2 * sub) % 4], psU[(2 * sub + 1) % 4]]
                for half in range(2):
                    ps = psF[half]
                    k.mm(ps[:], [(actT[:, cp, sub * 128:(sub + 1) * 128], wd[:, cp, half * 512:(half + 1) * 512]) for cp in range(NCH)],
                         r=[actT, wd], w=[ps])
                    k.act(junk[:, 0:512], ps[:], AF.Square, r=[ps], w=[junk, st_], scale=1.0 / 32.0, accum=st_[:, 2 + half:3 + half])
                k.tt('dve', st_[:, 2:3], st_[:, 2:3], st_[:, 3:4], ALU.add, r=[st_], w=[st_])
                rms_scale(k, st_[:, 2:3], st_)
                t_ = Tg[sub % 2] if False else None
                for half in range(2):
                    cs = slice(half * 512, (half + 1) * 512)
                    T_ = (Tg if half == 0 else Tv)[sub % 2]
                    k.stt('dve', T_[:], psF[half][:], st_[:, 1:2], k.gf_row[:, cs], ALU.mult, ALU.mult, r=[psF[half], st_, k.gf_row], w=[T_])
                    k.tt('pool', x_[:, cs], x_[:, cs], T_[:], ALU.add, r=[x_, T_], w=[x_])
                k.dma('sp', xdst[tok, :], x_[:], r=[x_])
                g = step(g)
            while g is not None:
                g = step(g)


def rwkv_host(inp):
    f = lambda a: np.ascontiguousarray(np.asarray(a, dtype=np.float32))
    mu = np.asarray(inp["rwkv_mu"])
    hd = lambda v: np.asarray(v).reshape(DEPTH, 4, 64).transpose(0, 2, 1)
    pp = np.stack([hd(mu[:, 0:256]), hd(mu[:, 256:512]), hd(mu[:, 512:768]), hd(inp["rwkv_w0"]), hd(inp["rwkv_a0"]),
                   hd(inp["rwkv_k_k"]), hd(inp["rwkv_k_a"]), hd(np.asarray(inp["rwkv_r_k"]).reshape(DEPTH, 256))], axis=2)
    lr = np.zeros((DEPTH, 64, 3), np.float32)
    lr[:, 0:32, 0] = mu[:, 768:800]; lr[:, 0:32, 1] = mu[:, 800:832]; lr[:, :, 2] = mu[:, 832:896]
    i = np.arange(64)
    mk = np.stack([(i[:, None] < i[None, :]), (i[:, None] > i[None, :]), (i[:, None] <= i[None, :]), np.eye(64, dtype=bool)]).astype(np.float32)
    cm = np.ones((64, 512), np.float32); cm[:, ::64] = 0.0
    return {"rwkv_pp": f(pp), "rwkv_lr": f(lr), "rwkv_w_up": f(inp["rwkv_w_up"]), "rwkv_a_up": f(inp["rwkv_a_up"]),
            "rwkv_g_up": f(inp["rwkv_g_up"]), "rwkv_ln": f(np.stack([np.asarray(inp["rwkv_ln_w"]), np.asarray(inp["rwkv_ln_b"])], axis=1)),
            "rwkv_masks": f(mk.transpose(1, 0, 2)), "rwkv_cmask": cm}


def setup_rwkv(k):
    k.rw_pp = k.inp("rwkv_pp", [DEPTH, 64, 8, 4])
    k.rw_lr = k.inp("rwkv_lr", [DEPTH, 64, 3])
    k.rw_wup = k.inp("rwkv_w_up", [DEPTH, 32, 256])
    k.rw_aup = k.inp("rwkv_a_up", [DEPTH, 32, 256])
    k.rw_gup = k.inp("rwkv_g_up", [DEPTH, 64, 256])
    k.rw_ln = k.inp("rwkv_ln", [DEPTH, 2, 256])
    k.rw_masks = k.inp("rwkv_masks", [64, 4, 64])
    k.rw_cmask = k.inp("rwkv_cmask", [64, 512])


def stage_rwkv(k, l):
    nc = k.nc
    BL = 256
    NB = S_LEN // BL
    CPB = BL // 64
    H4 = [64, 4, BL]
    bc = lambda ap, shape: ap.to_broadcast(shape)
    with Scope(k) as sc:
        pp = sc.sb("pp", [64, 8, 4], F32)
        lr = sc.sb("lr", [64, 3], F32)
        wup = sc.sb("wup", [32, 256], F32); aup = sc.sb("aup", [32, 256], F32); gup = sc.sb("gup", [64, 256], F32)
        lnr = sc.sb("lnr", [64, 2, 256], F32)
        mk = sc.sb("mk", [64, 4, 64], F32)
        cmask = sc.sb("cmask", [64, BL], F32)
        ones = sc.sb("ones64", [64, 64], F32)
        prm = sc.sb("prm", [64, 4, 4], F32)
        k.dma('sp', pp[:], k.rw_pp[l], w=[pp]); k.dma('sp', lr[:], k.rw_lr[l], w=[lr])
        k.dma('sp', wup[:], k.rw_wup[l], w=[wup]); k.dma('sp', aup[:], k.rw_aup[l], w=[aup]); k.dma('sp', gup[:], k.rw_gup[l], w=[gup])
        for i in range(2):
            k.dma('sp', lnr[:, i, :], k.rw_ln[l, i:i + 1, :].broadcast_to([64, 256]), w=[lnr])
        k.dma('sp', mk[:], k.rw_masks, w=[mk]); k.dma('sp', cmask[:], k.rw_cmask[:, 0:BL], w=[cmask])
        k.S.op('pool', lambda: nc.gpsimd.memset(ones[:], 1.0), [], [ones])
        k.ts('dve', prm[:, 0, :], pp[:, 3, :], -1.0, None, ALU.mult, None, r=[pp], w=[prm])
        k.ts('dve', prm[:, 1, :], pp[:, 6, :], -1.0, 1.0, ALU.mult, ALU.add, r=[pp], w=[prm])
        P3 = sc.sb("P3", [64, 3, 4, BL], F32)
        halo = sc.sb("halo", [64, 3, 4], F32)
        LR = sc.sb("LR", [64, 3, BL], F32)
        halo2 = sc.sb("halo2", [64, 3], F32)
        ELW = sc.sb("ELW", H4, F32); SC_ = sc.sb("SCAN", H4, F32); AA = sc.sb("AA", H4, F32); KKN = sc.sb("KKN", H4, F32)
        T1 = sc.sb("T1", H4, F32); T2 = sc.sb("T2", H4, F32); CM4 = sc.sb("CM4", H4, F32)
        OUT = [{nm: sc.sb("%s%d" % (nm, i), H4, F32 if nm == "GAM" else BF16) for nm in ("AT", "BT", "KT", "RT", "RK", "GAM", "V")} for i in range(2)]
        SGs = [sc.sb("SG%d" % i, [64, BL], BF16) for i in range(2)]
        gupb = sc.sb("gupb", [64, 256], BF16)
        ppb = sc.sb("ppb", [64, 4], BF16)
        identb64 = k.ident_bf
        XY = [[sc.sb("XY%d_%d" % (i, j), [64, 2, 4, 64], BF16) for j in range(2)] for i in range(2)]
        PP = [[sc.sb("PPi%d_%d" % (i, j), [64, 4, 64], BF16) for j in range(2)] for i in range(2)]
        AKRK = [sc.sb("AKRK%d" % i, [64, 2, 4, 64], BF16) for i in range(2)]
        RBT = [sc.sb("RBT%d" % i, [64, 4, 64], BF16) for i in range(2)]
        TOK = [sc.sb("TOK%d" % i, [64, 3, 4, 64], BF16) for i in range(2)]
        Wsb = sc.sb("Wsb", [64, 4, 64], BF16); Usb = sc.sb("Usb", [64, 4, 64], BF16)
        Hs = [sc.sb("Hs%d" % i, [64, 4, 64], F32) for i in range(2)]
        Hb = [sc.sb("Hb%d" % i, [64, 4, 64], BF16) for i in range(2)]
        yc = sc.sb("yc", [64, 4, 64], F32); ysq = sc.sb("ysq", [64, 4, 64], F32)
        sm = sc.sb("sm", [64, 6, 4], F32)
        yab = [sc.sb("yab%d" % i, [64, CPB, 256], BF16) for i in range(2)]
        psA1 = sc.ps("psA1", [64, 512]); psA2 = sc.ps("psA2", [64, 512]); psA3 = sc.ps("psA3", [64, 512]); psA4 = sc.ps("psA4", [64, 512])
        psH = sc.ps("psHr", [64, 512]); psY = sc.ps("psYr", [64, 512]); psC = sc.ps("psCr", [64, 512]); psQ = sc.ps("psQr", [64, 512])
        k.S.op('pool', lambda: nc.gpsimd.memset(Hs[1][:], 0.0), [], [Hs[1]])
        k.S.op('pool', lambda: nc.gpsimd.memset(Hb[1][:], 0.0), [], [Hb[1]])
        k.copy('dve', gupb[:], gup[:], r=[gup], w=[gupb])
        k.copy('dve', ppb[:], pp[:, 7, :], r=[pp], w=[ppb])
        k.S.op('pool', lambda: nc.gpsimd.memset(halo[:], 0.0), [], [halo])
        k.S.op('pool', lambda: nc.gpsimd.memset(halo2[:], 0.0), [], [halo2])
        k.copy('dve', CM4[:], bc(cmask[:].unsqueeze(1), H4), r=[cmask], w=[CM4])
        E_ = BL - 1

        def prep(tb):
            O = OUT[tb % 2]; SG = SGs[tb % 2]
            AT, BT, KT, RT, RK, GAM, V_ = O["AT"], O["BT"], O["KT"], O["RT"], O["RK"], O["GAM"], O["V"]
            t0 = tb * BL
            for q in range(3):
                k.dma('act', P3[:, q, :, :], k.PT[q * 256:(q + 1) * 256, t0:t0 + BL].rearrange("(h d) t -> d h t", d=64), w=[P3])
            k.dma('act', LR[0:32, 0, :], k.PT[768:800, t0:t0 + BL], w=[LR])
            k.dma('act', LR[0:32, 1, :], k.PT[800:832, t0:t0 + BL], w=[LR])
            k.dma('act', LR[:, 2, :], k.PT[832:896, t0:t0 + BL], w=[LR])
            yield
            for q in range(3):
                p_ = P3[:, q, :, :]
                k.tt('dve', T1[:, :, 1:BL], p_[:, :, 0:E_], p_[:, :, 1:BL], ALU.subtract, r=[P3], w=[T1])
                k.tt('dve', T1[:, :, 0:1], halo[:, q, :].unsqueeze(2), p_[:, :, 0:1], ALU.subtract, r=[P3, halo], w=[T1])
                k.copy('pool', halo[:, q, :].unsqueeze(2), p_[:, :, E_:BL], r=[P3, T1], w=[halo])
                k.tt('pool', T1[:], T1[:], bc(pp[:, q, :].unsqueeze(2), H4), ALU.mult, r=[T1, pp], w=[T1])
                if q < 2:
                    k.tt('pool', p_, p_, T1[:], ALU.add, r=[P3, T1, halo], w=[P3])
                else:
                    k.tt('pool', V_[:], p_, T1[:], ALU.add, r=[P3, T1, halo], w=[V_])
                yield
            for q, rows in ((0, 32), (1, 32), (2, 64)):
                x_ = LR[0:rows, q, :]
                t_ = T2[0:rows, 0, :]
                k.tt('dve', t_[:, 1:BL], x_[:, 0:E_], x_[:, 1:BL], ALU.subtract, r=[LR], w=[T2])
                k.tt('dve', t_[:, 0:1], halo2[0:rows, q:q + 1], x_[:, 0:1], ALU.subtract, r=[LR, halo2], w=[T2])
                k.copy('dve', halo2[0:rows, q:q + 1], x_[:, E_:BL], r=[LR, T2], w=[halo2])
                k.stt('dve', x_, t_, lr[0:rows, q:q + 1], x_, ALU.mult, ALU.add, r=[T2, lr, LR, halo2], w=[LR])
            yield
            R_ = P3[:, 0, :, :]; Kp = P3[:, 1, :, :]
            k.act(LR[0:32, 0, :], LR[0:32, 0, :], AF.Tanh, r=[LR], w=[LR])
            k.act(SG[:], LR[:, 2, :], AF.Sigmoid, r=[LR], w=[SG])
            for h in range(4):
                k.mm(psQ[:, 0:BL], [(wup[:, h * 64:(h + 1) * 64], LR[0:32, 0, :])], r=[wup, LR], w=[psQ])
                k.act(T1[:, h, :], psQ[:, 0:BL], AF.Exp, r=[psQ, prm], w=[T1], scale=-1.0, bias=prm[:, 0, h:h + 1])
                k.mm(psQ[:, BL:2 * BL], [(aup[:, h * 64:(h + 1) * 64], LR[0:32, 1, :])], r=[aup, LR], w=[psQ], start=False)
                k.act(AA[:, h, :], psQ[:, BL:2 * BL], AF.Sigmoid, r=[psQ, pp], w=[AA], bias=pp[:, 4, h:h + 1])
                yield
            k.act(T1[:], T1[:], AF.Ln, r=[T1], w=[T1], bias=1.0)
            k.act(ELW[:], T1[:], AF.Exp, r=[T1], w=[ELW], scale=-1.0, bias=-0.5)
            k.S.op('dve', lambda: nc.vector.tensor_tensor_scan(
                out=SC_[:].rearrange("p h t -> p (h t)"), data0=CM4[:].rearrange("p h t -> p (h t)"),
                data1=ELW[:].rearrange("p h t -> p (h t)"), initial=0.0, op0=ALU.mult, op1=ALU.add), [CM4, ELW], [SC_])
            yield
            k.tt('pool', KKN[:], Kp, bc(pp[:, 5, :].unsqueeze(2), H4), ALU.mult, r=[P3, pp], w=[KKN])
            k.tt('pool', T1[:], KKN[:], KKN[:], ALU.mult, r=[KKN], w=[T1])
            for h in range(4):
                k.mm(psQ[:, 0:BL], [(ones[:], T1[:, h, :])], r=[ones, T1], w=[psQ])
                k.act(T2[:, h, :], psQ[:, 0:BL], AF.Ln, r=[psQ], w=[T2], bias=1e-24)
                yield
            k.act(T2[:], T2[:], AF.Exp, r=[T2], w=[T2], scale=-0.5)
            k.tt('dve', KKN[:], KKN[:], T2[:], ALU.mult, r=[KKN, T2], w=[KKN])
            yield
            k.tt('pool', T1[:], SC_[:], ELW[:], ALU.subtract, r=[SC_, ELW], w=[T1])
            k.act(T1[:], T1[:], AF.Exp, r=[T1], w=[T1], scale=-1.0)
            k.stt('dve', AT[:], KKN[:], -1.0, T1[:], ALU.mult, ALU.mult, r=[KKN, T1], w=[AT])
            yield
            k.act(T2[:], SC_[:], AF.Exp, r=[SC_], w=[T2])
            k.tt('pool', T1[:], KKN[:], AA[:], ALU.mult, r=[KKN, AA], w=[T1])
            k.tt('dve', BT[:], T1[:], T2[:], ALU.mult, r=[T1, T2], w=[BT])
            yield
            k.tt('pool', T1[:], AA[:], bc(pp[:, 6, :].unsqueeze(2), H4), ALU.mult, r=[AA, pp], w=[T1])
            k.tt('pool', T1[:], T1[:], bc(prm[:, 1, :].unsqueeze(2), H4), ALU.add, r=[T1, prm], w=[T1])
            k.tt('dve', Kp, Kp, T1[:], ALU.mult, r=[P3, T1, KKN], w=[P3])
            yield
            k.tt('dve', KT[:], Kp, T2[:], ALU.mult, r=[P3, T2], w=[KT])
            k.tt('pool', RK[:], R_, Kp, ALU.mult, r=[P3], w=[RK])
            k.act(GAM[:], SC_[:], AF.Exp, r=[SC_], w=[GAM], scale=-1.0)
            k.tt('dve', RT[:], R_, GAM[:], ALU.mult, r=[P3, GAM], w=[RT])
            yield

        def phaseA(nch):
            tb, n = divmod(nch, CPB)
            O = OUT[tb % 2]
            AT, BT, KT, RT, V_ = O["AT"], O["BT"], O["KT"], O["RT"], O["V"]
            c_ = slice(n * 64, (n + 1) * 64)
            par = nch % 2
            xy = XY[par][0]; akrk = AKRK[par]; rbt = RBT[par]; tok = TOK[par]
            fns = []
            for h in range(4):
                fns.append(lambda h=h: nc.tensor.matmul(psA1[:, h * 64:(h + 1) * 64], lhsT=BT[:, h, c_], rhs=AT[:, h, c_], start=True, stop=True, skip_group_check=True))
                fns.append(lambda h=h: nc.tensor.matmul(psA1[:, 256 + h * 64:256 + (h + 1) * 64], lhsT=AT[:, h, c_], rhs=BT[:, h, c_], start=True, stop=True, skip_group_check=True))
            k.S.pe_group(fns, [AT, BT], [psA1])
            fns = []
            for h in range(4):
                fns.append(lambda h=h: nc.tensor.matmul(psA2[:, h * 64:(h + 1) * 64], lhsT=KT[:, h, c_], rhs=AT[:, h, c_], start=True, stop=True, skip_group_check=True))
                fns.append(lambda h=h: nc.tensor.matmul(psA2[:, 256 + h * 64:256 + (h + 1) * 64], lhsT=KT[:, h, c_], rhs=RT[:, h, c_], start=True, stop=True, skip_group_check=True))
            k.S.pe_group(fns, [AT, KT, RT], [psA2])
            k.S.pe_group([lambda h=h: nc.tensor.matmul(psA3[:, h * 64:(h + 1) * 64], lhsT=BT[:, h, c_], rhs=RT[:, h, c_], start=True, stop=True, skip_group_check=True)
                          for h in range(4)], [BT, RT], [psA3])
            fns = []
            psQb = psQ[:, :].bitcast(BF16)
            for qi, src_ in enumerate((V_, BT, KT)):
                for h in range(4):
                    dst = psQb[:, qi * 256 + h * 64:qi * 256 + (h + 1) * 64]
                    fns.append(lambda dst=dst, s_=src_[:, h, c_]: nc.tensor.transpose(out=dst, in_=s_, identity=k.ident_bf[0:64, 0:64]))
            k.S.pe_group(fns, [V_, BT, KT, k.ident_bf], [psQ])
            k.copy('act', tok[:].rearrange("p a h f -> p (a h f)"), psQb[:, 0:768], r=[psQ], w=[tok])
            yield
            v4 = lambda ps, a: ps[:, a * 256:(a + 1) * 256].rearrange("p (h f) -> p h f", h=4)
            mb = lambda i: bc(mk[:, i, :].unsqueeze(1), [64, 4, 64])
            k.tt('dve', xy[:, 0, :, :], v4(psA1, 0), mb(0), ALU.mult, r=[psA1, mk], w=[xy])
            k.tt('dve', xy[:, 1, :, :], v4(psA1, 1), mb(1), ALU.mult, r=[psA1, mk], w=[xy])
            k.tt('dve', akrk[:, 0, :, :], v4(psA2, 0), mb(0), ALU.mult, r=[psA2, mk], w=[akrk])
            k.tt('dve', akrk[:, 1, :, :], v4(psA2, 1), mb(2), ALU.mult, r=[psA2, mk], w=[akrk])
            k.tt('dve', rbt[:], v4(psA3, 0), mb(2), ALU.mult, r=[psA3, mk], w=[rbt])
            P_ = PP[par][0]
            k.tt('dve', P_[:], xy[:, 0, :, :], mb(3), ALU.add, r=[xy, mk], w=[P_])
            yield
            Pm = None
            for lev in range(1, 7):
                xyn = XY[par][lev % 2]
                fns = []
                rd = [xy]
                wr = []
                if lev <= 5:
                    for h in range(4):
                        fns.append(lambda h=h, xy=xy: nc.tensor.matmul(psA4[:, 256 + h * 64:256 + (h + 1) * 64], lhsT=xy[:, 0, h, :], rhs=xy[:, 1, h, :], start=True, stop=True, skip_group_check=True))
                        if lev <= 4:
                            fns.append(lambda h=h, xy=xy: nc.tensor.matmul(psA4[:, h * 64:(h + 1) * 64], lhsT=xy[:, 1, h, :], rhs=xy[:, 0, h, :], start=True, stop=True, skip_group_check=True))
                    wr.append(psA4)
                if lev >= 2:
                    for h in range(4):
                        fns.append(lambda h=h, xy=xy, Pm=Pm: nc.tensor.matmul(psA3[:, 256 + h * 64:256 + (h + 1) * 64], lhsT=xy[:, 1, h, :], rhs=Pm[:, h, :], start=True, stop=True, skip_group_check=True))
                    rd.append(Pm); wr.append(psA3)
                k.S.pe_group(fns, rd, wr)
                yield
                if lev <= 4:
                    k.copy('act', xyn[:].rearrange("p a h f -> p (a h f)"), psA4[:, :], r=[psA4], w=[xyn])
                elif lev == 5:
                    k.copy('act', xyn[:, 1, :, :].rearrange("p h f -> p (h f)"), psA4[:, 256:512], r=[psA4], w=[xyn])
                if lev >= 2:
                    Pn = PP[par][(lev - 1) % 2]
                    k.tt('dve', Pn[:], Pm[:], v4(psA3, 1), ALU.add, r=[Pm, psA3], w=[Pn])
                    Pm = Pn
                else:
                    Pm = P_
                yield
                xy = xyn

        def phaseB(nch):
            tb, n = divmod(nch, CPB)
            O = OUT[tb % 2]; SG = SGs[tb % 2]
            AT, RT, RK, GAM = O["AT"], O["RT"], O["RK"], O["GAM"]
            c_ = slice(n * 64, (n + 1) * 64)
            par = nch % 2
            akrk = AKRK[par]; rbt = RBT[par]; tok = TOK[par]; TT = PP[par][1]
            Hold = Hs[(nch + 1) % 2]; Hnew = Hs[nch % 2]
            Hbo = Hb[(nch + 1) % 2]; Hbn = Hb[nch % 2]
            yab_ = yab[tb % 2]
            fns = []
            for h in range(4):
                fns.append(lambda h=h: nc.tensor.matmul(psH[:, h * 64:(h + 1) * 64], lhsT=AT[:, h, c_], rhs=Hbo[:, h, :], start=(h == 0), stop=False, skip_group_check=True))
                fns.append(lambda h=h: nc.tensor.matmul(psH[:, h * 64:(h + 1) * 64], lhsT=akrk[:, 0, h, :], rhs=tok[:, 0, h, :], start=False, stop=True, skip_group_check=True))
            k.S.pe_group(fns, [AT, Hbo, akrk, tok], [psH])
            yield
            k.copy('act', Wsb[:].rearrange("p h f -> p (h f)"), psH[:, 0:256], r=[psH], w=[Wsb])
            yield
            k.S.pe_group([lambda h=h: nc.tensor.matmul(psH[:, 256 + h * 64:256 + (h + 1) * 64], lhsT=TT[:, h, :], rhs=Wsb[:, h, :], start=False, stop=True, skip_group_check=True)
                          for h in range(4)], [TT, Wsb], [psH])
            yield
            k.copy('act', Usb[:].rearrange("p h f -> p (h f)"), psH[:, 256:512], r=[psH], w=[Usb])
            yield
            fns = []
            for h in range(4):
                fns.append(lambda h=h: nc.tensor.matmul(psC[:, h * 64:(h + 1) * 64], lhsT=tok[:, 1, h, :], rhs=Usb[:, h, :], start=(h == 0), stop=False, skip_group_check=True))
                fns.append(lambda h=h: nc.tensor.matmul(psC[:, h * 64:(h + 1) * 64], lhsT=tok[:, 2, h, :], rhs=tok[:, 0, h, :], start=False, stop=True, skip_group_check=True))
                fns.append(lambda h=h: nc.tensor.matmul(psC[:, 256 + h:256 + h + 1], lhsT=RK[:, h, c_], rhs=ppb[:, h:h + 1], start=False, stop=True, skip_group_check=True))
            k.S.pe_group(fns, [tok, Usb, RK, ppb], [psC])
            fns = []
            for h in range(4):
                fns.append(lambda h=h: nc.tensor.matmul(psY[:, h * 64:(h + 1) * 64], lhsT=RT[:, h, c_], rhs=Hbo[:, h, :], start=(h == 0), stop=False, skip_group_check=True))
                fns.append(lambda h=h: nc.tensor.matmul(psY[:, h * 64:(h + 1) * 64], lhsT=rbt[:, h, :], rhs=Usb[:, h, :], start=False, stop=False, skip_group_check=True))
                fns.append(lambda h=h: nc.tensor.matmul(psY[:, h * 64:(h + 1) * 64], lhsT=akrk[:, 1, h, :], rhs=tok[:, 0, h, :], start=False, stop=True, skip_group_check=True))
            fns.append(lambda: nc.tensor.matmul(psY[:, 256:512], lhsT=SG[:, c_], rhs=gupb[:, :], start=False, stop=True, skip_group_check=True))
            k.S.pe_group(fns, [RT, Hbo, rbt, Usb, akrk, tok, SG, gupb], [psY])
            yield
            k.tt('dve', Hnew[:], psC[:, 0:256].rearrange("p (h f) -> p h f", h=4), Hold[:], ALU.add, r=[psC, Hold], w=[Hnew])
            k.copy('dve', sm[:, 5, :], psC[:, 256:260], r=[psC], w=[sm])
            k.tt('dve', Hbn[:], Hnew[:], bc(GAM[:, :, n * 64 + 63:n * 64 + 64], [64, 4, 64]), ALU.mult, r=[Hnew, GAM], w=[Hbn])
            k.tt('pool', Hnew[:], Hnew[:], bc(GAM[:, :, n * 64 + 63:n * 64 + 64], [64, 4, 64]), ALU.mult, r=[Hnew, GAM], w=[Hnew])
            yield
            y3 = psY[:, 0:256].rearrange("p (h f) -> p h f", h=4)
            k.S.op('dve', lambda: nc.vector.reduce_sum(out=sm[:, 0, :], in_=y3, axis=AX.X), [psY], [sm])
            k.ts('dve', sm[:, 1, :], sm[:, 0, :], 1.0 / 64.0, None, ALU.mult, None, r=[sm], w=[sm])
            k.tt('dve', yc[:], y3, bc(sm[:, 1, :].unsqueeze(2), [64, 4, 64]), ALU.subtract, r=[psY, sm], w=[yc])
            yield
            k.tt('pool', ysq[:], yc[:], yc[:], ALU.mult, r=[yc], w=[ysq])
            k.S.op('dve', lambda: nc.vector.reduce_sum(out=sm[:, 2, :], in_=ysq[:], axis=AX.X), [ysq], [sm])
            k.act(sm[:, 3, :], sm[:, 2, :], AF.Ln, r=[sm], w=[sm], scale=1.0 / 64.0, bias=GN_EPS)
            k.act(sm[:, 4, :], sm[:, 3, :], AF.Exp, r=[sm], w=[sm], scale=-0.5)
            yield
            k.tt('dve', yc[:], yc[:], bc(sm[:, 4, :].unsqueeze(2), [64, 4, 64]), ALU.mult, r=[yc, sm], w=[yc])
            k.tt('pool', yc[:], yc[:], lnr[:, 0, :].rearrange("p (h f) -> p h f", h=4), ALU.mult, r=[yc, lnr], w=[yc])
            k.tt('pool', yc[:], yc[:], lnr[:, 1, :].rearrange("p (h f) -> p h f", h=4), ALU.add, r=[yc, lnr], w=[yc])
            k.tt('dve', ysq[:], tok[:, 0, :, :], bc(sm[:, 5, :].unsqueeze(2), [64, 4, 64]), ALU.mult, r=[tok, sm], w=[ysq])
            yield
            k.tt('pool', yc[:], yc[:], ysq[:], ALU.add, r=[yc, ysq], w=[yc])
            k.tt('dve', yab_[:, n, :], yc[:].rearrange("p h f -> p (h f)"), psY[:, 256:512], ALU.mult, r=[yc, psY], w=[yab_])
            if n == CPB - 1:
                k.dma('sp', k.Y[tb * BL:(tb + 1) * BL, 0:256].rearrange("(n p) c -> p n c", p=64), yab_[:], r=[yab_])
            yield

        def run_all(*gens):
            gens = [g for g in gens if g is not None]
            while gens:
                for g in list(gens):
                    try:
                        next(g)
                    except StopIteration:
                        gens.remove(g)

        NCH = S_LEN // 64
        run_all(prep(0))
        run_all(phaseA(0), prep(1) if NB > 1 else None)
        gp = None
        for nch in range(NCH):
            tb, n = divmod(nch, CPB)
            if n == 0 and tb >= 1 and tb + 1 < NB:
                gp = prep(tb + 1)
            gens = [phaseB(nch)]
            if nch + 1 < NCH:
                gens.append(phaseA(nch + 1))
            rnd = 0
            while gens:
                for gi, g in enumerate(list(gens)):
                    for _rep in range(2 if (gi == 0 and RW_BPRIO) else 1):
                        try:
                            next(g)
                        except StopIteration:
                            if g in gens:
                                gens.remove(g)
                            break
                rnd += 1
                if gp is not None and rnd % 2 == 0:
                    try:
                        next(gp)
                    except StopIteration:
                        gp = None
            if n == CPB - 2 and gp is not None:
                for _ in gp:
                    pass
                gp = None


def build(nlayers=DEPTH, taps=()):
    k = K(nlayers, taps=taps)
    setup_globals(k)
    setup_fox(k)
    setup_rwkv(k)
    setup_ffn(k)
    setup_nsa(k)
    for l in range(nlayers):
        xin = k.x_in if l == 0 else k.XR
        xout = k.OUT if l == nlayers - 1 else k.XR
        stage_mod_proj(k, l, xin)
        stage_rwkv(k, l)
        stage_fox(k, l)
        stage_nsa(k, l)
        stage_out_ffn(k, l, xin, k.XR1, xout)
    k.S.barrier()
    return k


_CACHE = {}


def kernel(**inputs):
    if "k" not in _CACHE:
        _CACHE["k"] = build(DEPTH)
    k = _CACHE["k"]
    sh = prep_shared(inputs)
    in_maps = []
    for b in range(8):
        d = dict(sh)
        d.update(prep_core(inputs, b))
        in_maps.append({n: v for n, v in d.items() if n in k.ins})
    res = run_bass_kernel_spmd(k.nc, in_maps, core_ids=list(range(8)))
    out = np.stack([np.asarray(res.results[b]["out"], dtype=np.float32) for b in range(8)], axis=0)
    return out
```

```python
import numpy as np
import ml_dtypes
from contextlib import ExitStack
import concourse.bass as bass
import concourse.mybir as mybir
from concourse.bass_utils import run_bass_kernel_spmd

F32 = mybir.dt.float32
BF16 = mybir.dt.bfloat16
AF = mybir.ActivationFunctionType
ALU = mybir.AluOpType
AX = mybir.AxisListType
NPBF = ml_dtypes.bfloat16

S_LEN = 4096
D = 1024
DEPTH = 4
NTB = 8
N_IN = 3224
D_FF = 2816
NEG = -30000.0
RMS_EPS = 1e-6
GN_EPS = 64e-5


class Sched:
    ENG = ('pe', 'act', 'dve', 'pool')
    LIMIT = 30000

    def __init__(self, nc):
        self.nc = nc
        self.e = {'pe': nc.tensor, 'act': nc.scalar, 'dve': nc.vector, 'pool': nc.gpsimd, 'sp': nc.sync}
        self.epoch = {k: 0 for k in self.ENG}
        self.sem = {k: nc.alloc_semaphore("c_%s_0" % k) for k in self.ENG}
        self.cnt = {k: 0 for k in self.ENG}
        self.seen = {k: {} for k in self.e}
        self.lastw = {}
        self.reads = {}
        self.dma_sems = {'hw': [[nc.alloc_semaphore("d%d" % i), 0, "dma%d" % i] for i in range(24)],
                         'sw': [[nc.alloc_semaphore("ds%d" % i), 0, "dmas%d" % i] for i in range(8)]}
        self.ndma = {'hw': 0, 'sw': 0}
        self.n_inst = 0
        self.n_wait = 0
        self.per = {}

    def _wait(self, eng, tok):
        key, sem, val = tok
        if self.seen[eng].get(key, 0) >= val:
            return
        self.e[eng].wait_ge(sem, val)
        self.n_wait += 1
        self.per[eng] = self.per.get(eng, 0) + 1
        self.seen[eng][key] = val

    def _deps(self, eng, reads, writes):
        for b in reads:
            t = self.lastw.get(b)
            if t is not None:
                self._wait(eng, t)
        for b in writes:
            t = self.lastw.get(b)
            if t is not None:
                self._wait(eng, t)
            for t in self.reads.get(b, ()):
                self._wait(eng, t)

    def _commit(self, tok, reads, writes):
        for b in reads:
            self.reads.setdefault(b, []).append(tok)
        for b in writes:
            self.lastw[b] = tok
            self.reads[b] = []

    def _bump(self, eng, ins):
        if self.cnt[eng] >= self.LIMIT:
            self.epoch[eng] += 1
            self.sem[eng] = self.nc.alloc_semaphore("c_%s_%d" % (eng, self.epoch[eng]))
            self.cnt[eng] = 0
        self.cnt[eng] += 1
        ins.then_inc(self.sem[eng], 1)
        return ("%s_%d" % (eng, self.epoch[eng]), self.sem[eng], self.cnt[eng])

    @staticmethod
    def _norm(reads, writes):
        rd = [getattr(b, 'n', b) for b in reads]
        wr = [getattr(b, 'n', b) for b in writes]
        ps = [b for b in rd if b.startswith("ps")]
        rd = [b for b in rd if not b.startswith("ps")]
        return rd, wr + [b for b in ps if b not in wr]

    def op(self, eng, inst_fn, reads=(), writes=()):
        reads, writes = self._norm(reads, writes)
        self._deps(eng, reads, writes)
        ins = inst_fn()
        self.per[eng] = self.per.get(eng, 0) + 1
        tok = self._bump(eng, ins)
        self._commit(tok, reads, writes)
        self.n_inst += 1
        return tok

    def pe_group(self, fns, reads=(), writes=()):
        reads, writes = self._norm(reads, writes)
        self._deps('pe', reads, writes)
        ins = None
        for f in fns:
            ins = f()
            self.n_inst += 1
            self.per['pe'] = self.per.get('pe', 0) + 1
        tok = self._bump('pe', ins)
        self._commit(tok, reads, writes)
        return tok

    def dma(self, q, out, in_, reads=(), writes=(), **kw):
        reads, writes = self._norm(reads, writes)
        self._deps(q, reads, writes)
        cls = 'sw' if q == 'pool' else 'hw'
        pool_ = self.dma_sems[cls]
        slot = pool_[self.ndma[cls] % len(pool_)]
        self.ndma[cls] += 1
        if slot[1] > 0:
            self._wait(q, (slot[2], slot[0], slot[1]))
        if slot[1] >= self.LIMIT:
            slot[0] = self.nc.alloc_semaphore("%s_e%d" % (slot[2], self.ndma[cls]))
            slot[1] = 0
            slot[2] = slot[2] + "x"
        slot[1] += 16
        ins = self.e[q].dma_start(out=out, in_=in_, **kw)
        self.per[q] = self.per.get(q, 0) + 1
        ins.then_inc(slot[0], 16)
        tok = (slot[2], slot[0], slot[1])
        self._commit(tok, reads, writes)
        self.n_inst += 1
        return tok

    def barrier(self, engines=('pe', 'act', 'dve', 'pool', 'sp')):
        toks = [("%s_%d" % (k, self.epoch[k]), self.sem[k], self.cnt[k]) for k in self.ENG if self.cnt[k] > 0]
        toks += [(s[2], s[0], s[1]) for p_ in self.dma_sems.values() for s in p_ if s[1] > 0]
        for e in engines:
            for t in toks:
                self._wait(e, t)
        self.lastw = {}
        self.reads = {}


class Pipe:
    def __init__(self, lag=2):
        self.q = []
        self.lag = lag

    def push(self, first, second):
        first()
        self.q.append(second)
        while len(self.q) > self.lag:
            self.q.pop(0)()

    def flush(self):
        while self.q:
            self.q.pop(0)()


class Tile:
    def __init__(self, h, name):
        self.h = h
        self.n = name

    def __getitem__(self, idx):
        return self.h[idx]


class Scope:
    cnt = 0

    def __init__(self, k):
        self.k = k
        self.es = ExitStack()

    def __enter__(self):
        self.es.__enter__()
        Scope.cnt += 1
        self.id = Scope.cnt
        return self

    def sb(self, name, shape, dt):
        nm = "%s_%d" % (name, self.id)
        h = self.es.enter_context(self.k.nc.sbuf_tensor(nm, list(shape), dt))
        return Tile(h, nm)

    def ps(self, name, shape, dt=F32):
        nm = "%s_%d" % (name, self.id)
        h = self.es.enter_context(self.k.nc.psum_tensor(nm, list(shape), dt))
        return Tile(h, nm)

    def __exit__(self, *a):
        self.k.S.barrier()
        return self.es.__exit__(*a)


class K:
    def __init__(self, nlayers, taps=()):
        self.nc = bass.Bass("TRN2", target_bir_lowering=False)
        self.S = Sched(self.nc)
        self.nl = nlayers
        self.taps = set(taps)
        self.ins = {}
        self.dr = {}

    def inp(self, name, shape, dt=F32):
        t = self.nc.dram_tensor(name, list(shape), dt, kind="ExternalInput").ap()
        self.ins[name] = t
        return t

    def scratch(self, name, shape, dt=F32, out=False):
        kind = "ExternalOutput" if (out or name in self.taps) else "Internal"
        t = self.nc.dram_tensor(name, list(shape), dt, kind=kind).ap()
        self.dr[name] = t
        return t

    def act(self, out, in_, func, r, w, bias=0.0, scale=1.0, accum=None):
        nc = self.nc
        if accum is None:
            return self.S.op('act', lambda: nc.scalar.activation(out=out, in_=in_, func=func, bias=bias, scale=scale), r, w)
        return self.S.op('act', lambda: nc.scalar.activation(out=out, in_=in_, func=func, bias=bias, scale=scale, accum_out=accum), r, w)

    def ts(self, eng, out, in0, s1, s2, op0, op1, r, w):
        e = self.S.e[eng]
        if op1 is None:
            return self.S.op(eng, lambda: e.tensor_scalar(out=out, in0=in0, scalar1=s1, scalar2=None, op0=op0), r, w)
        return self.S.op(eng, lambda: e.tensor_scalar(out=out, in0=in0, scalar1=s1, scalar2=s2, op0=op0, op1=op1), r, w)

    def tt(self, eng, out, in0, in1, op, r, w):
        e = self.S.e[eng]
        return self.S.op(eng, lambda: e.tensor_tensor(out=out, in0=in0, in1=in1, op=op), r, w)

    def stt(self, eng, out, in0, scalar, in1, op0, op1, r, w):
        e = self.S.e[eng]
        return self.S.op(eng, lambda: e.scalar_tensor_tensor(out=out, in0=in0, scalar=scalar, in1=in1, op0=op0, op1=op1), r, w)

    def copy(self, eng, out, in_, r, w):
        if eng == 'act':
            return self.S.op('act', lambda: self.nc.scalar.copy(out=out, in_=in_), r, w)
        e = self.S.e[eng]
        return self.S.op(eng, lambda: e.tensor_copy(out=out, in_=in_), r, w)

    def mm(self, out, pairs, r, w, start=True, stop=True, sgc=False):
        nc = self.nc
        n = len(pairs)
        fns = []
        for i, (l, rh) in enumerate(pairs):
            fns.append(lambda l=l, rh=rh, i=i: nc.tensor.matmul(out, lhsT=l, rhs=rh, start=(start and i == 0), stop=(stop and i == n - 1),
                                                               skip_group_check=(sgc or not start)))
        return self.S.pe_group(fns, r, w)

    def transpose(self, out, in_, ident, r, w):
        nc = self.nc
        return self.S.op('pe', lambda: nc.tensor.transpose(out=out, in_=in_, identity=ident), r, w)

    def dma(self, q, out, in_, r=(), w=(), **kw):
        return self.S.dma(q, out, in_, r, w, **kw)


def w_in_perm_index():
    idx = list(range(0, 896))
    idx += list(range(896, 1664))
    for c in range(3):
        idx += list(range(2054 + c * 64, 2054 + c * 64 + 64))
        idx += list(range(2054 + (c + 3) * 64, 2054 + (c + 3) * 64 + 64))
    idx += list(range(2438, 2566))
    idx += list(range(2566, 2694))
    idx += list(range(2694, 2822))
    idx += list(range(2950, 3078))
    idx += list(range(2048, 2054))
    idx += list(range(1664, 2048))
    idx += list(range(2822, 2950))
    idx += list(range(3078, 3206))
    idx += list(range(3206, 3224))
    assert len(idx) == N_IN and len(set(idx)) == N_IN
    return np.array(idx)


QKT_ROWS = 1664


def setup_globals(k):
    nc = k.nc
    k.x_in = k.inp("x", [S_LEN, D])
    k.cT = k.inp("cT", [128, 8])
    k.ada_w = k.inp("ada_w", [DEPTH, D, 6 * D])
    k.ada_b_fm = k.inp("ada_b_fm", [DEPTH, 128, 48])
    k.ada_b_row = k.inp("ada_b_row", [DEPTH, 6 * D])
    k.normg_fm = k.inp("normg_fm", [DEPTH, 4, 128, 8])
    k.normg_row = k.inp("normg_row", [DEPTH, 4, D])
    k.w_in = k.inp("w_in_p", [DEPTH, D, N_IN])
    k.ident_bf_d = k.inp("ident_bf", [128, 128], BF16)
    k.ident_f_d = k.inp("ident_f", [128, 128], F32)

    k.PT = k.scratch("PT", [896, S_LEN], F32)
    k.QKT = k.scratch("QKT", [QKT_ROWS, S_LEN], BF16)
    k.FL = k.scratch("FL", [6, S_LEN], F32)
    k.VT = k.scratch("VT", [S_LEN, 640], BF16)
    k.GT = k.scratch("GT", [S_LEN, 18], F32)
    k.Y = k.scratch("Y", [S_LEN, D], BF16)
    k.XR = k.scratch("XR", [S_LEN, D], F32)
    k.XR1 = k.scratch("XR1", [S_LEN, D], F32)
    k.OUT = k.scratch("out", [S_LEN, D], F32, out=True)

    def pers(name, shape, dt):
        return Tile(nc.alloc_sbuf_tensor(name, list(shape), dt), name)
    k.ident_bf = pers("ident_bf_sb", [128, 128], BF16)
    k.ident_f = pers("ident_f_sb", [128, 128], F32)
    k.sc = pers("sc", [128, 8], F32)
    k.modAB = pers("modAB", [128, 32], F32)
    k.gm_row = pers("gm_row", [128, D], F32)
    k.gf_row = pers("gf_row", [128, D], F32)
    k.dma('sp', k.ident_bf[:], k.ident_bf_d, w=[k.ident_bf])
    k.dma('sp', k.ident_f[:], k.ident_f_d, w=[k.ident_f])
    k.dma('sp', k.sc[:], k.cT, w=[k.sc])
    k.act(k.sc[:], k.sc[:], AF.Silu, r=[k.sc], w=[k.sc])


def stage_mod(k, l, bg=None):
    with Scope(k) as sc:
        slab = [sc.sb("adaslab%d" % i, [128, 6 * D], F32) for i in range(4)]
        psA = sc.ps("psA", [128, 32])
        psR = [sc.ps("psR%d" % i, [128, 512]) for i in range(4)]
        bfm = sc.sb("bfm", [128, 48], F32)
        gfm = sc.sb("gfm", [128, 4, 8], F32)
        brow = sc.sb("brow", [128, 2, D], F32)
        grow = sc.sb("grow", [128, 2, D], F32)
        mfm = sc.sb("mfm", [128, 32], F32)
        sc_rep = sc.sb("sc_rep", [128, 8, 128], F32)
        for kc in range(8):
            k.copy('dve', sc_rep[:, kc, :], k.sc[:, kc:kc + 1].to_broadcast([128, 128]), r=[k.sc], w=[sc_rep])
        k.dma('sp', bfm[:], k.ada_b_fm[l], w=[bfm])
        k.dma('sp', gfm[:], k.normg_fm[l].rearrange("g p c -> p g c"), w=[gfm])
        k.dma('sp', brow[:, 0, :], k.ada_b_row[l:l + 1, 2 * D:3 * D].broadcast_to([128, D]), w=[brow])
        k.dma('sp', brow[:, 1, :], k.ada_b_row[l:l + 1, 5 * D:6 * D].broadcast_to([128, D]), w=[brow])
        k.dma('sp', grow[:, 0, :], k.normg_row[l, 1:2, :].broadcast_to([128, D]), w=[grow])
        k.dma('sp', grow[:, 1, :], k.normg_row[l, 3:4, :].broadcast_to([128, D]), w=[grow])
        fm_chunks = list(range(0, 16)) + list(range(24, 40))
        row_cols = [2 * D, 2 * D + 512, 5 * D, 5 * D + 512]
        for kc in range(8):
            sl = slab[kc % 4]
            k.dma('sp' if kc % 2 == 0 else 'act', sl[:], k.ada_w[l, kc * 128:(kc + 1) * 128, :], w=[sl])
            for i, j in enumerate(fm_chunks):
                k.mm(psA[:, i:i + 1], [(sl[:, j * 128:(j + 1) * 128], k.sc[:, kc:kc + 1])], r=[sl, k.sc], w=[psA],
                     start=(kc == 0 and i == 0), stop=(kc == 7), sgc=True)
            for i, c0 in enumerate(row_cols):
                k.mm(psR[i][:], [(sc_rep[:, kc, :], sl[:, c0:c0 + 512])], r=[sl, sc_rep], w=[psR[i]],
                     start=(kc == 0), stop=(kc == 7), sgc=True)
            if bg is not None:
                try:
                    next(bg)
                except StopIteration:
                    bg = None
        if bg is not None:
            for _ in bg:
                pass
        k.tt('dve', mfm[:, 0:16], psA[:, 0:16], bfm[:, 0:16], ALU.add, r=[psA, bfm], w=[mfm])
        k.tt('dve', mfm[:, 16:32], psA[:, 16:32], bfm[:, 24:40], ALU.add, r=[psA, bfm], w=[mfm])
        k.stt('dve', k.modAB[:, 0:8], mfm[:, 8:16], 1.0, gfm[:, 0, :], ALU.add, ALU.mult, r=[mfm, gfm], w=[k.modAB])
        k.copy('dve', k.modAB[:, 8:16], mfm[:, 0:8], r=[mfm], w=[k.modAB])
        k.stt('dve', k.modAB[:, 16:24], mfm[:, 24:32], 1.0, gfm[:, 2, :], ALU.add, ALU.mult, r=[mfm, gfm], w=[k.modAB])
        k.copy('dve', k.modAB[:, 24:32], mfm[:, 16:24], r=[mfm], w=[k.modAB])
        for i in range(4):
            dst = (k.gm_row if i < 2 else k.gf_row)
            cs = slice((i % 2) * 512, (i % 2) * 512 + 512)
            k.tt('dve', dst[:, cs], psR[i][:], brow[:, i // 2, cs], ALU.add, r=[psR[i], brow], w=[dst])
            k.tt('pool', dst[:, cs], dst[:, cs], grow[:, i // 2, cs], ALU.mult, r=[dst, grow], w=[dst])


def stage_mod_proj(k, l, xsrc):
    with Scope(k) as sc:
        wsb = sc.sb("wsb", [128, 8, N_IN], BF16)
        with Scope(k) as s2:
            wst = [s2.sb("wst%d" % i, [128, N_IN], F32) for i in range(2)]

            def bg():
                for kc in range(8):
                    s = wst[kc % 2]
                    k.dma('act' if kc % 2 == 0 else 'sp', s[:], k.w_in[l, kc * 128:(kc + 1) * 128, :], w=[s])
                    k.copy('pool' if kc % 2 == 0 else 'dve', wsb[:, kc, :], s[:], r=[s], w=[wsb])
                    yield
            stage_mod(k, l, bg=bg())
        stage_proj(k, l, xsrc, pre=(sc, wsb))


def stage_proj(k, l, xsrc, pre=None):
    nc = k.nc
    with ExitStack() as es_:
        if pre is None:
            sc = es_.enter_context(Scope(k))
            wsb = sc.sb("wsb", [128, 8, N_IN], BF16)
            wst = [sc.sb("wst%d" % i, [128, N_IN], F32) for i in range(4)]
        else:
            sc, wsb = pre
            wst = None
        xt = [sc.sb("xt%d" % i, [128, D], F32) for i in range(2)]
        junk = sc.sb("junk", [128, D], BF16)
        xn = [sc.sb("xn%d" % i, [128, D], BF16) for i in range(2)]
        st = [sc.sb("st%d" % i, [128, 4], F32) for i in range(2)]
        HT = [sc.sb("HT%d" % i, [128, 8, 512], BF16) for i in range(2)]
        psT = [sc.ps("psT%d" % i, [128, D], BF16) for i in range(2)]
        psM = [sc.ps("psM%d" % i, [128, 512]) for i in range(6)]
        evf = [sc.sb("evf%d" % i, [128, 512], F32) for i in range(3)]
        evb = [sc.sb("evb%d" % i, [128, 512], BF16) for i in range(3)]
        evt = [sc.sb("evt%d" % i, [128, 640], BF16) for i in range(2)]
        evg = [sc.sb("evg%d" % i, [128, 18], F32) for i in range(2)]
        for kc in (range(8) if pre is None else []):
            s = wst[kc % 4]
            k.dma('sp' if kc % 2 == 0 else 'act', s[:], k.w_in[l, kc * 128:(kc + 1) * 128, :], w=[s])
            k.copy('pool' if kc % 2 == 0 else 'dve', wsb[:, kc, :], s[:], r=[s], w=[wsb])
        cnt = {"ev": 0, "pm": 0}

        def norm_gen(tb):
            ht = HT[tb % 2]
            t0_ = tb * 4
            k.dma('act', xt[t0_ % 2][:], xsrc[t0_ * 128:(t0_ + 1) * 128, :], w=[xt[t0_ % 2]])
            yield
            for sub in range(4):
                ti = tb * 4 + sub
                x_ = xt[ti % 2]; xn_ = xn[ti % 2]; st_ = st[ti % 2]; pt_ = psT[ti % 2]
                k.act(junk[:], x_[:], AF.Square, r=[x_], w=[junk, st_], scale=1.0 / 32.0, accum=st_[:, 0:1])
                k.act(st_[:, 1:2], st_[:, 0:1], AF.Ln, r=[st_], w=[st_], bias=RMS_EPS)
                k.act(st_[:, 2:3], st_[:, 1:2], AF.Exp, r=[st_], w=[st_], scale=-0.5)
                k.ts('dve', xn_[:], x_[:], st_[:, 2:3], None, ALU.mult, None, r=[x_, st_], w=[xn_])
                if sub < 3:
                    k.dma('act', xt[(ti + 1) % 2][:], xsrc[(ti + 1) * 128:(ti + 2) * 128, :], w=[xt[(ti + 1) % 2]])
                yield
                yield
                for kc in range(8):
                    k.transpose(pt_[:, kc * 128:(kc + 1) * 128], xn_[:, kc * 128:(kc + 1) * 128], k.ident_bf[:],
                                r=[xn_, k.ident_bf], w=[pt_])
                    if kc == 3:
                        yield
                yield
                for kc in range(8):
                    o = ht[:, kc, sub * 128:(sub + 1) * 128]
                    i_ = pt_[:, kc * 128:(kc + 1) * 128]
                    if kc % 2 == 0:
                        k.ts('dve', o, i_, k.modAB[:, kc:kc + 1], k.modAB[:, 8 + kc:9 + kc], ALU.mult, ALU.add,
                             r=[pt_, k.modAB], w=[ht])
                    else:
                        k.act(o, i_, AF.Identity, r=[pt_, k.modAB], w=[ht], scale=k.modAB[:, kc:kc + 1],
                              bias=k.modAB[:, 8 + kc:9 + kc])
                yield

        def step(g):
            if g is not None:
                try:
                    next(g)
                except StopIteration:
                    return None
            return g

        for _ in norm_gen(0):
            pass
        for tb in range(NTB):
            ht = HT[tb % 2]
            g = norm_gen(tb + 1) if tb + 1 < NTB else None
            tsl = slice(tb * 512, (tb + 1) * 512)
            fm = [(c * 128, 128, 'PT', c * 128) for c in range(7)]
            fm += [(896 + c * 128, 128, 'QKT', c * 128) for c in range(13)]
            fm += [(2560, 6, 'FL', 0)]
            for (c0, m, dst, r0) in fm:
                ps = psM[cnt["pm"] % 6]; cnt["pm"] += 1
                k.mm(ps[0:m, :], [(wsb[:, kc, c0:c0 + m], ht[:, kc, :]) for kc in range(8)], r=[wsb, ht], w=[ps])
                eng = 'act' if cnt["ev"] % 2 == 0 else 'dve'
                if dst == 'QKT':
                    ev = evb[cnt["ev"] % 3]
                    dd = k.QKT[r0:r0 + m, tsl]
                else:
                    ev = evf[cnt["ev"] % 3]
                    dd = (k.PT if dst == 'PT' else k.FL)[r0:r0 + m, tsl]
                cnt["ev"] += 1
                k.copy(eng, ev[0:m, :], ps[0:m, :], r=[ps], w=[ev])
                k.dma('sp', dd, ev[0:m, :], r=[ev])
                g = step(g)
            for sub in range(4):
                ti = tb * 4 + sub
                tok = slice(ti * 128, (ti + 1) * 128)
                ps0 = psM[cnt["pm"] % 6]; cnt["pm"] += 1
                ps1 = psM[cnt["pm"] % 6]; cnt["pm"] += 1
                lhs = lambda kc: ht[:, kc, sub * 128:(sub + 1) * 128]
                k.mm(ps0[:, 0:384], [(lhs(kc), wsb[:, kc, 2566:2950]) for kc in range(8)], r=[wsb, ht], w=[ps0])
                k.mm(ps1[:, 0:274], [(lhs(kc), wsb[:, kc, 2950:3224]) for kc in range(8)], r=[wsb, ht], w=[ps1])
                et = evt[ti % 2]; eg = evg[ti % 2]
                k.copy('act', et[:, 0:384], ps0[:, 0:384], r=[ps0], w=[et])
                k.copy('dve', et[:, 384:640], ps1[:, 0:256], r=[ps1], w=[et])
                k.copy('dve', eg[:], ps1[:, 256:274], r=[ps1], w=[eg])
                k.dma('sp', k.VT[tok, :], et[:], r=[et])
                k.dma('sp', k.GT[tok, :], eg[:], r=[eg])
                g = step(g)
            while g is not None:
                g = step(g)


def prep_shared(inp):
    f = lambda a: np.ascontiguousarray(np.asarray(a, dtype=np.float32))
    sh = {}
    sh["ada_w"] = f(inp["ada_w"])
    sh["ada_b_fm"] = f(np.asarray(inp["ada_b"]).reshape(DEPTH, 48, 128).transpose(0, 2, 1))
    sh["ada_b_row"] = f(inp["ada_b"])
    sh["normg_fm"] = f(np.asarray(inp["norm_g"]).reshape(DEPTH, 4, 8, 128).transpose(0, 1, 3, 2))
    sh["normg_row"] = f(inp["norm_g"])
    sh["w_in_p"] = f(np.asarray(inp["w_in"])[:, :, w_in_perm_index()])
    sh["ident_bf"] = np.eye(128, dtype=np.float32).astype(NPBF)
    sh["ident_f"] = np.eye(128, dtype=np.float32)
    sh["w_out"] = f(inp["w_out"]); sh["ffn_up"] = f(inp["ffn_up"]); sh["ffn_down"] = f(inp["ffn_down"])
    sh["conv_w_fm"] = f(np.asarray(inp["ffn_conv_w"]).reshape(DEPTH, 3, 44, 128).transpose(0, 3, 1, 2))
    sh["conv_b_fm"] = f(np.asarray(inp["ffn_conv_b"]).reshape(DEPTH, 44, 128).transpose(0, 2, 1))
    sh.update(nsa_host_consts())
    sh["rel_bias"] = f(inp["rel_bias"])
    sh["nsa_pe_kT"] = f(np.asarray(inp["nsa_pe_k"]).transpose(0, 2, 1))
    sh["nsa_pe_vT"] = f(np.asarray(inp["nsa_pe_v"]).transpose(0, 2, 1))
    for n in ("nsa_ck_w1", "nsa_cv_w1", "nsa_ck_w2", "nsa_cv_w2"):
        sh[n] = f(inp[n])
    sh["fox_b_f"] = f(np.asarray(inp["fox_b_f"]).reshape(DEPTH, 6, 1))
    sh.update(rwkv_host(inp))
    return sh


def prep_core(inp, b):
    d = {}
    d["x"] = np.ascontiguousarray(np.asarray(inp["x"][b], dtype=np.float32))
    d["cT"] = np.ascontiguousarray(np.asarray(inp["c"][b], dtype=np.float32).reshape(8, 128).T)
    return d


def setup_fox(k):
    k.fox_bf = k.inp("fox_b_f", [DEPTH, 6, 1])
    k.CUMA = k.scratch("CUMA", [6, 3, S_LEN], BF16)


def stage_fox(k, l):
    nc = k.nc
    with Scope(k) as sc:
        nb = sc.sb("nb", [128, 32, 6], F32)
        with Scope(k) as s2:
            fl = s2.sb("fl", [6, S_LEN], F32)
            t1 = s2.sb("t1", [6, S_LEN], F32)
            ones = s2.sb("ones", [6, S_LEN], F32)
            cum = s2.sb("cum", [6, S_LEN], F32)
            parts = s2.sb("parts", [6, 3, S_LEN], BF16)
            bfv = s2.sb("bfv", [6, 2], F32)
            psn = s2.ps("psn", [128, 512])
            k.dma('sp', fl[:], k.FL, w=[fl])
            k.dma('sp', bfv[:, 0:1], k.fox_bf[l], w=[bfv])
            k.ts('dve', bfv[:, 1:2], bfv[:, 0:1], -1.0, None, ALU.mult, None, r=[bfv], w=[bfv])
            k.S.op('pool', lambda: nc.gpsimd.memset(ones[:], 1.0), [], [ones])
            k.act(t1[:], fl[:], AF.Exp, r=[fl, bfv], w=[t1], bias=bfv[:, 1:2], scale=-1.0)
            k.act(t1[:], t1[:], AF.Ln, r=[t1], w=[t1], bias=1.0, scale=1.0)
            k.ts('dve', t1[:], t1[:], -1.0, None, ALU.mult, None, r=[t1], w=[t1])
            k.S.op('dve', lambda: nc.vector.tensor_tensor_scan(out=cum[:], data0=ones[:], data1=t1[:], initial=0.0,
                                                               op0=ALU.mult, op1=ALU.add), [ones, t1], [cum])
            for t in range(32):
                k.transpose(psn[:, t * 6:(t + 1) * 6], cum[:, t * 128:(t + 1) * 128], k.ident_f[0:6, 0:6],
                            r=[cum, k.ident_f], w=[psn])
            k.ts('dve', nb[:].rearrange("p t h -> p (t h)"), psn[:, 0:192], -1.0, None, ALU.mult, None, r=[psn], w=[nb])
            k.ts('dve', t1[:], cum[:], 8.0, None, ALU.mult, None, r=[cum], w=[t1])
            k.copy('dve', parts[:, 0, :], t1[:], r=[t1], w=[parts])
            k.tt('dve', t1[:], t1[:], parts[:, 0, :], ALU.subtract, r=[t1, parts], w=[t1])
            k.copy('dve', parts[:, 1, :], t1[:], r=[t1], w=[parts])
            k.tt('dve', t1[:], t1[:], parts[:, 1, :], ALU.subtract, r=[t1, parts], w=[t1])
            k.copy('dve', parts[:, 2, :], t1[:], r=[t1], w=[parts])
            k.dma('sp', k.CUMA, parts[:], r=[parts], w=["CUMA"])
        QA = [sc.sb("QA%d" % i, [128, S_LEN], BF16) for i in range(2)]
        KA = [sc.sb("KA%d" % i, [128, S_LEN], BF16) for i in range(2)]
        VA = sc.sb("VA", [128, 32, 6, 65], BF16)
        yb = sc.sb("yb", [128, 32, 384], BF16)
        PTl = [sc.sb("PTl%d" % i, [128, 512], BF16) for i in range(6)]
        rc = [sc.sb("rc%d" % i, [128, 4], F32) for i in range(2)]
        psS = [sc.ps("psS%d" % i, [128, 512]) for i in range(4)]
        psO = [sc.ps("psO%d" % i, [128, 512]) for i in range(2)]
        k.dma('sp', yb[:], k.VT[:, 0:384].rearrange("(t p) c -> p t c", p=128), w=[yb])
        k.S.op('pool', lambda: nc.gpsimd.memset(VA[:, :, :, 64:65], 1.0), [], [VA])
        k.copy('dve', VA[:, :, :, 0:64], yb[:].rearrange("p t (h d) -> p t h d", h=6), r=[yb], w=[VA])
        for i in range(2):
            k.S.op('dve', lambda i=i: nc.vector.memset(KA[i][64:67, :], 1.0), [], [KA[i]])
        nS = 0
        nO = 0
        nP = 0
        pipe = Pipe(3)
        for h in range(6):
            qa = QA[h % 2]; ka = KA[h % 2]
            k.dma('sp', qa[0:64, :], k.QKT[h * 64:(h + 1) * 64, :], w=[qa])
            k.dma('sp', qa[64:67, :], k.CUMA[h], w=[qa])
            k.dma('sp', ka[0:64, :], k.QKT[384 + h * 64:384 + (h + 1) * 64, :], w=[ka])
            for qb in range(NTB):
                po = psO[nO % 2]; nO += 1
                nkt = 4 * qb + 4
                for kt in range(nkt):
                    j = kt - 4 * qb
                    c0 = max(j, 0) * 128
                    ps = psS[nS % len(psS)]; nS += 1
                    pt = PTl[nP % len(PTl)]; nP += 1

                    def first(ps=ps, pt=pt, kt=kt, c0=c0, j=j, qa=qa, ka=ka, qb=qb, h=h):
                        k.mm(ps[:, c0:512], [(ka[0:67, kt * 128:(kt + 1) * 128], qa[0:67, qb * 512 + c0:(qb + 1) * 512])],
                             r=[ka, qa], w=[ps])
                        k.act(pt[:, c0:512], ps[:, c0:512], AF.Exp, r=[ps, nb], w=[pt], bias=nb[:, kt, h:h + 1], scale=0.125)
                        if j >= 0:
                            k.S.op('pool', lambda: nc.gpsimd.affine_select(
                                out=pt[:, c0:c0 + 128], in_=pt[:, c0:c0 + 128], pattern=[[1, 128]], compare_op=ALU.is_ge,
                                fill=0.0, base=0, channel_multiplier=-1), [pt], [pt])

                    def second(pt=pt, kt=kt, j=j, po=po, qb=qb, h=h, last=(kt == nkt - 1)):
                        fns = []
                        for qs in range(max(j, 0), 4):
                            fns.append(lambda qs=qs: nc.tensor.matmul(
                                po[:, qs * 65:(qs + 1) * 65], lhsT=pt[:, qs * 128:(qs + 1) * 128], rhs=VA[:, kt, h, :],
                                start=(kt == 0 and qs == 0), stop=(kt == 4 * qb + qs), skip_group_check=True))
                        k.S.pe_group(fns, [pt, VA], [po])
                        if last:
                            r_ = rc[qb % 2]
                            pov = po[:, 0:260].rearrange("p (q c) -> p q c", c=65)
                            k.S.op('dve', lambda: nc.vector.reciprocal(out=r_[:], in_=pov[:, :, 64]), [po], [r_])
                            for qs in range(4):
                                k.ts('dve', yb[:, qb * 4 + qs, h * 64:(h + 1) * 64], po[:, qs * 65:qs * 65 + 64], r_[:, qs:qs + 1], None,
                                     ALU.mult, None, r=[po, r_], w=[yb])
                    pipe.push(first, second)
        pipe.flush()
        k.dma('sp', k.Y[:, 256:640].rearrange("(t p) c -> p t c", p=128), yb[:], r=[yb], w=["Y"])


RW_BPRIO = True
LW = 1536
LC = 4608
NEG8 = -240000.0


def t5_bucket_np(n):
    n = np.maximum(n, 0)
    nf = np.maximum(n, 1).astype(np.float32)
    large = 16 + (np.log(nf / np.float32(16)) / np.float32(np.log(128 / 16)) * np.float32(16)).astype(np.int32)
    large = np.minimum(large, 31)
    return np.where(n < 16, n, large)


def nsa_host_consts():
    c = {}
    i = np.arange(LW); n = i - 511
    oh = np.zeros((33, LW), np.float32)
    ok = (n >= 0) & (n < 512)
    oh[t5_bucket_np(n)[ok], i[ok]] = 1.0
    oh[32, ~ok] = NEG8
    c["oh_w"] = oh
    i = np.arange(LC); n = i - 2063
    oh = np.zeros((33, LC), np.float32)
    ok = n >= 0
    oh[t5_bucket_np(n)[ok], i[ok]] = 1.0
    oh[32, ~ok] = NEG8
    c["oh_c"] = oh
    s_ = np.arange(S_LEN)
    c["E_all"] = (np.arange(64)[:, None] == (s_[None, :] // 64)).astype(np.float32).astype(NPBF)
    cs = np.arange(256) * 16
    ce = cs + 31
    ss = np.arange(64) * 64
    ov = ((cs[:, None] <= ss[None, :] + 63) & (ce[:, None] >= ss[None, :])).astype(np.float32)
    ov[255] = 0.0
    c["ovl"] = np.ascontiguousarray(ov.reshape(2, 128, 64).transpose(1, 0, 2)).astype(NPBF)
    t = np.arange(S_LEN)
    cur = t // 64
    jb = np.arange(64)
    back = cur[:, None] - jb[None, :]
    valid = back >= 0
    forced = (jb[None, :] == 0) | (valid & (back < 2))
    tkm = (valid & ~forced).astype(np.float32)
    tka = np.where(valid, np.where(forced, 1e4, 0.0), -1.0).astype(np.float32)
    c["tkm"] = np.ascontiguousarray(tkm.reshape(32, 128, 64).transpose(1, 0, 2)).astype(NPBF)
    c["tka"] = np.ascontiguousarray(tka.reshape(32, 128, 64).transpose(1, 0, 2)).astype(NPBF)
    return c


def setup_nsa(k):
    nc = k.nc
    k.rel_bias = k.inp("rel_bias", [32, 6])
    k.oh_w = k.inp("oh_w", [33, LW])
    k.oh_c = k.inp("oh_c", [33, LC])
    k.E_d = k.inp("E_all", [64, S_LEN], BF16)
    k.ovl_d = k.inp("ovl", [128, 2, 64], BF16)
    k.tkm_d = k.inp("tkm", [128, 32, 64], BF16)
    k.tka_d = k.inp("tka", [128, 32, 64], BF16)
    k.pe_kT = k.inp("nsa_pe_kT", [DEPTH, 64, 32])
    k.pe_vT = k.inp("nsa_pe_vT", [DEPTH, 64, 32])
    k.ck_w1 = k.inp("nsa_ck_w1", [DEPTH, 2048, 128])
    k.cv_w1 = k.inp("nsa_cv_w1", [DEPTH, 2048, 128])
    k.ck_w2 = k.inp("nsa_ck_w2", [DEPTH, 128, 64])
    k.cv_w2 = k.inp("nsa_cv_w2", [DEPTH, 128, 64])
    k.WVW = k.scratch("WVW", [6, 128, LW], BF16)
    k.WVC = k.scratch("WVC", [6, 128, LC], BF16)
    with Scope(k) as sc:
        rb = sc.sb("rb", [33, 6], F32)
        rb31 = sc.sb("rb31", [32, 6], F32)
        rrep = sc.sb("rrep", [33, 6, 128], F32)
        ohw = sc.sb("ohw", [33, LW], F32)
        ohc = sc.sb("ohc", [33, LC], F32)
        ps = [sc.ps("psb%d" % i, [128, 512]) for i in range(2)]
        ev = [sc.sb("evb%d" % i, [128, 512], BF16) for i in range(2)]
        k.dma('sp', rb[0:32, :], k.rel_bias, w=[rb])
        k.dma('sp', rb31[:], k.rel_bias[31:32, :].broadcast_to([32, 6]), w=[rb31])
        k.dma('sp', ohw[:], k.oh_w, w=[ohw])
        k.dma('sp', ohc[:], k.oh_c, w=[ohc])
        k.S.op('dve', lambda: nc.vector.memset(rb[32:33, :], 1.0), [], [rb])
        k.tt('dve', rb[0:32, :], rb[0:32, :], rb31[:], ALU.subtract, r=[rb, rb31], w=[rb])
        k.ts('dve', rb[0:32, :], rb[0:32, :], 8.0, None, ALU.mult, None, r=[rb], w=[rb])
        for h in range(6):
            k.copy('dve', rrep[:, h, :], rb[:, h:h + 1].to_broadcast([33, 128]), r=[rb], w=[rrep])
        n = 0
        for h in range(6):
            for (oh, L, dst) in ((ohw, LW, k.WVW), (ohc, LC, k.WVC)):
                for c0 in range(0, L, 512):
                    p_ = ps[n % 2]; e_ = ev[n % 2]; n += 1
                    k.mm(p_[:], [(rrep[:, h, :], oh[:, c0:c0 + 512])], r=[rrep, oh], w=[p_])
                    k.copy('act' if n % 2 else 'dve', e_[:], p_[:], r=[p_], w=[e_])
                    k.dma('sp', dst[h, :, c0:c0 + 512], e_[:], r=[e_])


class DbgStop(Exception):
    pass


def dbg(k, lvl):
    if getattr(k, 'dbg_stop', None) == lvl:
        raise DbgStop()


def stage_nsa(k, l):
    nc = k.nc
    with Scope(k) as sc:
        Gw = sc.sb("Gw", [128, 6, 1408], BF16)
        Gc = sc.sb("Gc", [128, 6, 2560], BF16)
        tkm = sc.sb("tkm", [128, 32, 64], BF16)
        tka = sc.sb("tka", [128, 32, 64], BF16)
        QN = [sc.sb("QN%d" % h, [128, S_LEN], BF16) for h in range(6)]
        KE = [sc.sb("KE%d" % g, [128, S_LEN], BF16) for g in range(2)]
        KW = sc.sb("KW", [128, S_LEN], BF16)
        VS = sc.sb("VS", [128, 32, 2, 65], BF16)
        VW = sc.sb("VW", [128, 32, 2, 65], BF16)
        KCMP = sc.sb("KCMP", [128, 256], BF16)
        VE = sc.sb("VE", [128, 2, 2, 129], BF16)
        sg = sc.sb("sg", [128, 32, 18], F32)
        for h in range(6):
            k.dma('sp', Gw[:, h, :], bass.AP(k.WVW.tensor, h * 128 * LW + 127, [[LW - 1, 128], [1, 1408]]), w=[Gw])
            k.dma('sp', Gc[:, h, :], bass.AP(k.WVC.tensor, h * 128 * LC + 2032, [[LC - 16, 128], [1, 2560]]), w=[Gc])
        k.dma('sp', tkm[:], k.tkm_d, w=[tkm])
        k.dma('sp', tka[:], k.tka_d, w=[tka])
        for h in range(6):
            g_, hp_ = h // 3, h % 3
            k.dma('sp', QN[h][g_ * 64:(g_ + 1) * 64, :], k.QKT[768 + hp_ * 128 + g_ * 64:768 + hp_ * 128 + (g_ + 1) * 64, :], w=[QN[h]])
            k.S.op('pool', lambda h=h, g_=g_: nc.gpsimd.memset(QN[h][(1 - g_) * 64:(2 - g_) * 64, :], 0.0), [], [QN[h]])
        for g_ in range(2):
            k.dma('sp', KE[g_][g_ * 64:(g_ + 1) * 64, :], k.QKT[1408 + g_ * 64:1408 + (g_ + 1) * 64, :], w=[KE[g_]])
            k.dma('sp', KE[g_][(1 - g_) * 64:(2 - g_) * 64, :], k.E_d, w=[KE[g_]])
        k.dma('sp', KW[:], k.QKT[1536:1664, :], w=[KW])
        k.dma('sp', sg[:], k.GT.rearrange("(t p) c -> p t c", p=128), w=[sg])
        k.act(sg[:], sg[:], AF.Exp, r=[sg], w=[sg], scale=-1.0)
        k.ts('dve', sg[:], sg[:], 1.0, None, ALU.add, None, r=[sg], w=[sg])
        k.S.op('dve', lambda: nc.vector.reciprocal(out=sg[:], in_=sg[:]), [sg], [sg])
        k.dma('sp', VE[:, 0, :, 65:129], k.ovl_d, w=[VE])
        k.dma('sp', VE[:, 1, :, 65:129], k.ovl_d, w=[VE])
        k.S.op('pool', lambda: nc.gpsimd.memset(VE[:, :, :, 64:65], 1.0), [], [VE])
        k.S.op('pool', lambda: nc.gpsimd.memset(VE[:, :, :, 0:64], 0.0), [], [VE])
        k.S.op('pool', lambda: nc.gpsimd.memset(KCMP[:], 0.0), [], [KCMP])
        dbg(k, 1)
        with Scope(k) as s2:
            vst = s2.sb("vst", [128, 32, 256], BF16)
            k.dma('sp', vst[:], k.VT[:, 384:640].rearrange("(t p) c -> p t c", p=128), w=[vst])
            k.S.op('pool', lambda: nc.gpsimd.memset(VS[:, :, :, 64:65], 1.0), [], [VS])
            k.S.op('pool', lambda: nc.gpsimd.memset(VW[:, :, :, 64:65], 1.0), [], [VW])
            k.copy('dve', VS[:, :, :, 0:64], vst[:, :, 0:128].rearrange("p t (g d) -> p t g d", g=2), r=[vst], w=[VS])
            k.copy('pool', VW[:, :, :, 0:64], vst[:, :, 128:256].rearrange("p t (g d) -> p t g d", g=2), r=[vst], w=[VW])
        dbg(k, 2)
        with Scope(k) as s2:
            KC = s2.sb("KC", [128, S_LEN], BF16)
            VC = s2.sb("VC", [128, S_LEN], BF16)
            k.dma('sp', KC[:], k.QKT[1152:1280, :], w=[KC])
            k.dma('sp', VC[:], k.QKT[1280:1408, :], w=[VC])
            w1s = s2.sb("w1s", [128, 16, 128], F32)
            w1b = [s2.sb("w1b%d" % i, [128, 32, 128], BF16) for i in range(2)]
            w2s = s2.sb("w2s", [128, 2, 64], F32)
            w2b = s2.sb("w2b", [128, 2, 64], BF16)
            pes = s2.sb("pes", [128, 2, 32], F32)
            peb = s2.sb("peb", [128, 2, 32], BF16)
            hb = s2.sb("hb", [128, 2], F32)
            gx = s2.sb("gx", [128, 256], F32)
            gu = s2.sb("gu", [128, 256], F32)
            gg = s2.sb("gg", [128, 256], BF16)
            psh = s2.ps("psh", [128, 512])
            psb_ = s2.ps("pshb", [128, 512])
            pso = s2.ps("pso", [128, 512])
            for kv, (w1d, w2d, ped) in enumerate(((k.ck_w1, k.ck_w2, k.pe_kT), (k.cv_w1, k.cv_w2, k.pe_vT))):
                for lh in range(2):
                    for half in range(2):
                        k.dma('sp', w1s[half * 64:(half + 1) * 64, :, :],
                              w1d[l, lh * 1024:(lh + 1) * 1024, :].rearrange("(l d) h -> d l h", d=64), w=[w1s])
                    k.copy('dve' if lh == 0 else 'act', w1b[kv][:, lh * 16:(lh + 1) * 16, :], w1s[:], r=[w1s], w=[w1b[kv]])
                for half in range(2):
                    k.dma('sp', pes[half * 64:(half + 1) * 64, kv, :], ped[l], w=[pes])
                k.dma('sp', w2s[:, kv, :], w2d[l], w=[w2s])
            k.copy('dve', w2b[:], w2s[:], r=[w2s], w=[w2b])
            w2kd = s2.sb("w2kd", [128, 2, 64], BF16)
            for a_ in range(2):
                k.copy('dve', w2kd[:, a_, :], w2s[:, 0, :], r=[w2s], w=[w2kd])
            k.copy('dve', peb[:], pes[:], r=[pes], w=[peb])
            for kv in range(2):
                src = KC if kv == 0 else VC
                k.mm(psb_[:, kv:kv + 1], [(w1b[kv][0:64, li, :], peb[0:64, kv, li:li + 1]) for li in range(32)],
                     r=[w1b[kv], peb], w=[psb_], start=True)
                k.copy('dve', hb[:, kv:kv + 1], psb_[:, kv:kv + 1], r=[psb_], w=[hb])
                for g in range(2):
                    pr = slice(g * 64, (g + 1) * 64)
                    k.mm(psh[:, 0:255], [(w1b[kv][pr, li, :], src[pr, li:li + 16 * 254 + 1:16]) for li in range(32)],
                         r=[w1b[kv], src], w=[psh])
                    k.ts('dve', gx[:, 0:255], psh[:, 0:255], hb[:, kv:kv + 1], None, ALU.add, None, r=[psh, hb], w=[gx])
                    k.tt('dve', gu[:, 0:255], gx[:, 0:255], gx[:, 0:255], ALU.mult, r=[gx], w=[gu])
                    k.ts('dve', gu[:, 0:255], gu[:, 0:255], 0.044715, 1.0, ALU.mult, ALU.add, r=[gu], w=[gu])
                    k.tt('dve', gu[:, 0:255], gu[:, 0:255], gx[:, 0:255], ALU.mult, r=[gu, gx], w=[gu])
                    k.act(gu[:, 0:255], gu[:, 0:255], AF.Exp, r=[gu], w=[gu], scale=-2.0 * 0.7978845608028654)
                    k.ts('dve', gu[:, 0:255], gu[:, 0:255], 1.0, None, ALU.add, None, r=[gu], w=[gu])
                    k.S.op('dve', lambda: nc.vector.reciprocal(out=gu[:, 0:255], in_=gu[:, 0:255]), [gu], [gu])
                    k.S.op('dve', lambda: nc.vector.memset(gg[:, 255:256], 0.0), [], [gg])
                    k.tt('dve', gg[:, 0:255], gu[:, 0:255], gx[:, 0:255], ALU.mult, r=[gu, gx], w=[gg])
                    if kv == 0:
                        k.mm(pso[:, 0:256], [(w2kd[:].rearrange("p a d -> p (a d)"), gg[:, 0:256])], r=[w2kd, gg], w=[pso])
                        k.copy('dve', KCMP[pr, :], pso[pr, 0:256], r=[pso], w=[KCMP])
                    else:
                        for ct in range(2):
                            k.mm(pso[:, ct * 64:(ct + 1) * 64], [(gg[:, ct * 128:(ct + 1) * 128], w2b[:, 1, :])],
                                 r=[w2b, gg], w=[pso], start=(ct == 0))
                        k.copy('dve', VE[:, g, :, 0:64], pso[:, 0:128].rearrange("p (c d) -> p c d", c=2), r=[pso], w=[VE])
        dbg(k, 3)
        PTl = [sc.sb("PTn%d" % i, [128, 512], BF16) for i in range(6)]
        yacc = [sc.sb("yacc%d" % i, [128, 4, 384], F32) for i in range(2)]
        ybf = [sc.sb("ybf%d" % i, [128, 4, 384], BF16) for i in range(2)]
        impt2 = [[sc.sb("impt%d_%d" % (i, g), [128, 4, 64], F32) for g in range(2)] for i in range(2)]
        scr = sc.sb("scr", [128, 4, 64], F32)
        wk = sc.sb("wk", [128, 4, 64], F32)
        m8 = sc.sb("m8", [128, 4, 16], F32)
        nmq = sc.sb("nmq", [128, 4, 128], BF16)
        rcs = [sc.sb("rcs%d" % i, [128, 8], F32) for i in range(3)]
        psS = [sc.ps("psS%d" % i, [128, 512]) for i in range(4)]
        psO = [sc.ps("psO%d" % i, [128, 512]) for i in range(3)]
        psT = sc.ps("psTn", [128, 1024], BF16)
        st = {"S": 0, "O": 0, "P": 0, "R": 0}

        def q_ap(h, c0, c1):
            g, hp = h // 3, h % 3
            return QN[h][g * 64:(g + 1) * 64, c0:c1]

        def evac(views, h, branch, qb, ya, first):
            r_ = rcs[st["R"] % 3]; st["R"] += 1
            for qs, (po, cb) in enumerate(views):
                if branch == 0:
                    k.ts('dve', r_[:, qs:qs + 1], po[:, cb + 64:cb + 65], 1e-30, None, ALU.max, None, r=[po], w=[r_])
                    k.S.op('dve', lambda r_=r_, qs=qs: nc.vector.reciprocal(out=r_[:, qs:qs + 1], in_=r_[:, qs:qs + 1]), [r_], [r_])
                else:
                    k.S.op('dve', lambda r_=r_, po=po, cb=cb, qs=qs: nc.vector.reciprocal(out=r_[:, qs:qs + 1], in_=po[:, cb + 64:cb + 65]), [po], [r_])
            k.tt('dve', r_[:, 4:8], r_[:, 0:4], sg[:, qb * 4:(qb + 1) * 4, h * 3 + branch], ALU.mult, r=[r_, sg], w=[r_])
            for qs, (po, cb) in enumerate(views):
                o = ya[:, qs, h * 64:(h + 1) * 64]
                if first:
                    k.ts('dve', o, po[:, cb:cb + 64], r_[:, 4 + qs:5 + qs], None, ALU.mult, None, r=[po, r_], w=[ya])
                else:
                    k.stt('dve', o, po[:, cb:cb + 64], r_[:, 4 + qs:5 + qs], o, ALU.mult, ALU.add, r=[po, r_, ya], w=[ya])
            return r_

        pipe = Pipe(3)

        def attend(h, qb, tiles, kmat, vmat, po, g, branch, ya, merged=False):
            hp = h % 3
            nt = len(tiles)
            state = {"first": True}
            for idx, (kt, c0, c1, extra) in enumerate(tiles):
                ps = psS[st["S"] % len(psS)]; st["S"] += 1
                pt = PTl[st["P"] % len(PTl)]; st["P"] += 1

                def first(ps=ps, pt=pt, kt=kt, c0=c0, c1=c1, extra=extra):
                    if merged:
                        fns = [lambda: nc.tensor.matmul(ps[:, c0:c1], lhsT=kmat[:, kt * 128:(kt + 1) * 128],
                                                        rhs=QN[h][:, qb * 512 + c0:qb * 512 + c1],
                                                        start=True, stop=(len(extra) == 0), skip_group_check=True)]
                    else:
                        fns = [lambda: nc.tensor.matmul(ps[:, c0:c1], lhsT=kmat[g * 64:(g + 1) * 64, kt * 128:(kt + 1) * 128],
                                                        rhs=QN[h][g * 64:(g + 1) * 64, qb * 512 + c0:qb * 512 + c1],
                                                        start=True, stop=(len(extra) == 0), skip_group_check=True)]
                    rd = [kmat, QN[h]]
                    for ei, (lt, rt, lap, rap) in enumerate(extra):
                        w_ = rap.shape[-1]
                        fns.append(lambda lap=lap, rap=rap, w_=w_, ei=ei: nc.tensor.matmul(
                            ps[:, c0:c0 + w_], lhsT=lap, rhs=rap, start=False, stop=(ei == len(extra) - 1), skip_group_check=True))
                        rd += [lt, rt]
                    k.S.pe_group(fns, rd, [ps])
                    k.act(pt[:, c0:c1], ps[:, c0:c1], AF.Exp, r=[ps], w=[pt], scale=0.125)

                def second(pt=pt, kt=kt, c0=c0, c1=c1, idx=idx):
                    fns = []
                    for qs in range(c0 // 128, (c1 + 127) // 128):
                        last = all(not (t2[1] <= qs * 128 < t2[2]) for t2 in tiles[idx + 1:])
                        fo = state["first"]
                        state["first"] = False
                        fns.append(lambda qs=qs, fo=fo, last=last: nc.tensor.matmul(
                            po[:, qs * 65:(qs + 1) * 65], lhsT=pt[:, qs * 128:(qs + 1) * 128], rhs=vmat[:, kt, g, :],
                            start=fo, stop=last, skip_group_check=True))
                    k.S.pe_group(fns, [pt, vmat], [po])
                    if idx == nt - 1:
                        evac([(po, qs * 65) for qs in range(4)], h, branch, qb, ya, False)
                pipe.push(first, second)

        def do_cmp(qb):
            ya = yacc[qb % 2]
            impt = impt2[qb % 2]
            for h in range(6):
                g = h // 3
                poA = psO[st["O"] % 3]; st["O"] += 1
                poB = psO[st["O"] % 3]; st["O"] += 1
                cts = [0] + ([1] if qb >= 4 else [])
                state = {"A": True, "B": True}
                for ct in cts:
                    delta = 512 * qb - 2048 * ct
                    ps = psS[st["S"] % len(psS)]; st["S"] += 1
                    pt = PTl[st["P"] % len(PTl)]; st["P"] += 1

                    def first(ps=ps, pt=pt, ct=ct, delta=delta, g=g, h=h):
                        pairs = [(KCMP[g * 64:(g + 1) * 64, ct * 128:(ct + 1) * 128], q_ap(h, qb * 512, (qb + 1) * 512))]
                        rd = [KCMP, QN[h]]
                        if delta < 2560:
                            pairs.append((k.ident_bf[:], Gc[:, h, delta:delta + 512])); rd += [k.ident_bf, Gc]
                        k.mm(ps[:], pairs, r=rd, w=[ps])
                        k.act(pt[:], ps[:], AF.Exp, r=[ps], w=[pt], scale=0.125)

                    def second(pt=pt, ct=ct, g=g, h=h, poA=poA, poB=poB, state=state, lastct=(ct == cts[-1])):
                        fns = []
                        for qs in range(4):
                            po, cb = (poA, qs * 129) if qs < 3 else (poB, 0)
                            key = "A" if qs < 3 else "B"
                            stt_ = state[key]
                            state[key] = False
                            fns.append(lambda qs=qs, po=po, cb=cb, stt_=stt_: nc.tensor.matmul(
                                po[:, cb:cb + 129], lhsT=pt[:, qs * 128:(qs + 1) * 128], rhs=VE[:, g, ct, :],
                                start=stt_, stop=lastct, skip_group_check=True))
                        k.S.pe_group(fns, [pt, VE], [poA, poB])
                        if lastct:
                            views = [(poA, 0), (poA, 129), (poA, 258), (poB, 0)]
                            r_ = evac(views, h, 0, qb, ya, True)
                            for qs, (po, cb) in enumerate(views):
                                o = impt[g][:, qs, :]
                                if h % 3 == 0:
                                    k.ts('dve', o, po[:, cb + 65:cb + 129], r_[:, qs:qs + 1], None, ALU.mult, None, r=[po, r_], w=[impt[g]])
                                else:
                                    k.stt('dve', o, po[:, cb + 65:cb + 129], r_[:, qs:qs + 1], o, ALU.mult, ALU.add, r=[po, r_, impt[g]], w=[impt[g]])
                    pipe.push(first, second)

        def do_topk(qb):
            impt = impt2[qb % 2]
            for g in range(2):
                k.tt('dve', scr[:], impt[g][:], tkm[:, qb * 4:(qb + 1) * 4, :], ALU.mult, r=[impt[g], tkm], w=[scr])
                k.tt('dve', scr[:], scr[:], tka[:, qb * 4:(qb + 1) * 4, :], ALU.add, r=[scr, tka], w=[scr])
                for qs in range(4):
                    k.S.op('dve', lambda qs=qs: nc.vector.max(out=m8[:, qs, 0:8], in_=scr[:, qs, :]), [scr], [m8])
                    k.S.op('dve', lambda qs=qs: nc.vector.match_replace(out=wk[:, qs, :], in_to_replace=m8[:, qs, 0:8],
                                                                        in_values=scr[:, qs, :], imm_value=-1e9), [scr, m8], [wk])
                    k.S.op('dve', lambda qs=qs: nc.vector.max(out=m8[:, qs, 8:16], in_=wk[:, qs, :]), [wk], [m8])
                    k.ts('dve', wk[:, qs, :], scr[:, qs, :], m8[:, qs, 15:16], 1.0, ALU.is_ge, ALU.subtract, r=[scr, m8, wk], w=[wk])
                k.ts('dve', nmq[:, :, 0:64], wk[:], -NEG8, None, ALU.mult, None, r=[wk], w=[nmq])
                k.ts('pool', nmq[:, :, 64:128], wk[:], -NEG8, None, ALU.mult, None, r=[wk], w=[nmq])
                for qs in range(4):
                    k.transpose(psT[:, qs * 128:(qs + 1) * 128], nmq[:, qs, :], k.ident_bf[:], r=[nmq, k.ident_bf], w=[psT])
                oh = (1 - g) * 64
                for hh in range(3 * g, 3 * g + 3):
                    k.copy('act' if hh % 2 else 'dve', QN[hh][oh:oh + 64, qb * 512:(qb + 1) * 512], psT[oh:oh + 64, 0:512], r=[psT], w=[QN[hh]])

        def do_win(qb):
            ya = yacc[qb % 2]
            for h in range(6):
                g = h // 3
                po = psO[st["O"] % 3]; st["O"] += 1
                tiles = []
                for kt in range(max(0, 4 * qb - 4), 4 * qb + 4):
                    delta = 512 * qb - 128 * kt
                    c0 = max(-delta, 0)
                    c1 = min(512, 640 - delta) if delta > 0 else 512
                    tiles.append((kt, c0, c1, [(k.ident_bf, Gw, k.ident_bf[:], Gw[:, h, delta + 384 + c0:delta + 384 + c1])]))
                attend(h, qb, tiles, KW, VW, po, g, 2, ya)

        def do_slc(qb):
            ya = yacc[qb % 2]
            for h in range(6):
                g = h // 3
                po = psO[st["O"] % 3]; st["O"] += 1
                tiles = []
                for kt in range(0, 4 * qb + 4):
                    delta = 512 * qb - 128 * kt
                    c0 = max(-delta, 0)
                    ex = []
                    if delta <= 128:
                        c1b = 256 if delta == 128 else 512
                        ex.append((k.ident_bf, Gw, k.ident_bf[:], Gw[:, h, delta + 384 + c0:delta + 384 + c1b]))
                    tiles.append((kt, c0, 512, ex))
                attend(h, qb, tiles, KE[g], VS, po, g, 1, ya, merged=True)

        qbs = list(getattr(k, 'dbg_qbs', range(NTB)))
        do_cmp(qbs[0])
        pipe.flush()
        do_topk(qbs[0])
        for i, qb in enumerate(qbs):
            do_win(qb)
            if i + 1 < len(qbs):
                do_cmp(qbs[i + 1])
                pipe.flush()
                do_topk(qbs[i + 1])
            do_slc(qb)
            pipe.flush()
            ya = yacc[qb % 2]
            yb_ = ybf[qb % 2]
            k.copy('pool', yb_[:], ya[:], r=[ya], w=[yb_])
            k.dma('sp', k.Y[qb * 512:(qb + 1) * 512, 640:1024].rearrange("(q p) c -> p q c", p=128), yb_[:], r=[yb_])


def setup_ffn(k):
    k.w_out = k.inp("w_out", [DEPTH, D, D])
    k.ffn_up = k.inp("ffn_up", [DEPTH, D, 2 * D_FF])
    k.ffn_down = k.inp("ffn_down", [DEPTH, D_FF, D])
    k.conv_w = k.inp("conv_w_fm", [DEPTH, 128, 3, 44])
    k.conv_b = k.inp("conv_b_fm", [DEPTH, 128, 44])


def load_cast_gen(k, stg, dst, src_rows, ncols, nchunks, col_split=1):
    w = ncols // col_split
    n = 0
    for c in range(nchunks):
        for cs in range(col_split):
            s = stg[n % len(stg)]
            k.dma('sp' if n % 2 == 0 else 'act', s[:, 0:w], src_rows(c)[:, cs * w:(cs + 1) * w], w=[s])
            k.copy('pool' if n % 2 == 0 else 'dve', dst[:, c, cs * w:(cs + 1) * w], s[:, 0:w], r=[s], w=[dst])
            n += 1
            yield


def load_cast(k, sc, dst, src_rows, ncols, nchunks, name, col_split=1):
    w = ncols // col_split
    stg = [sc.sb("%s_stg%d" % (name, i), [128, w], F32) for i in range(4)]
    for _ in load_cast_gen(k, stg, dst, src_rows, ncols, nchunks, col_split):
        pass


def rms_scale(k, ss, st):
    k.act(st[:, 0:1], ss, AF.Ln, r=[st], w=[st], bias=RMS_EPS)
    k.act(st[:, 1:2], st[:, 0:1], AF.Exp, r=[st], w=[st], scale=-0.5)


def stage_out(k, l, xsrc, xdst, bg=None, bg_steps=2):
    nc = k.nc
    with Scope(k) as sc:
        wo = sc.sb("wo", [128, 8, D], BF16)
        with Scope(k) as s2:
            load_cast(k, s2, wo, lambda c: k.w_out[l, c * 128:(c + 1) * 128, :], D, 8, "wo")
        yt = [sc.sb("yt%d" % i, [128, D], BF16) for i in range(2)]
        yT = [sc.sb("yT%d" % i, [128, 8, 128], BF16) for i in range(2)]
        xt = [sc.sb("xo%d" % i, [128, D], F32) for i in range(2)]
        tt_ = [sc.sb("to%d" % i, [128, D], F32) for i in range(2)]
        junk = sc.sb("junko", [128, 512], BF16)
        st = [sc.sb("sto%d" % i, [128, 4], F32) for i in range(2)]
        psT = [sc.ps("psTo%d" % i, [128, D], BF16) for i in range(2)]
        psY = [sc.ps("psYo%d" % i, [128, 512]) for i in range(4)]
        def T(ti):
            tok = slice(ti * 128, (ti + 1) * 128)
            y_ = yt[ti % 2]; yT_ = yT[ti % 2]; x_ = xt[ti % 2]; pT = psT[ti % 2]
            k.dma('act', y_[:], k.Y[tok, :], w=[y_])
            k.dma('act', x_[:], xsrc[tok, :], w=[x_])
            for kc in range(8):
                k.transpose(pT[:, kc * 128:(kc + 1) * 128], y_[:, kc * 128:(kc + 1) * 128], k.ident_bf[:], r=[y_, k.ident_bf], w=[pT])
            k.copy('act' if ti % 2 else 'dve', yT_[:].rearrange("p a b -> p (a b)"), pT[:], r=[pT], w=[yT_])

        def M(ti):
            tok = slice(ti * 128, (ti + 1) * 128)
            yT_ = yT[ti % 2]; x_ = xt[ti % 2]; t_ = tt_[ti % 2]; st_ = st[ti % 2]
            p0 = psY[(ti % 2) * 2]; p1 = psY[(ti % 2) * 2 + 1]
            for half, ps in enumerate((p0, p1)):
                k.mm(ps[:], [(yT_[:, kc, :], wo[:, kc, half * 512:(half + 1) * 512]) for kc in range(8)], r=[yT_, wo], w=[ps])
                k.act(junk[:], ps[:], AF.Square, r=[ps], w=[junk, st_], scale=1.0 / 32.0, accum=st_[:, 2 + half:3 + half])
            k.tt('dve', st_[:, 2:3], st_[:, 2:3], st_[:, 3:4], ALU.add, r=[st_], w=[st_])
            rms_scale(k, st_[:, 2:3], st_)
            for half, ps in enumerate((p0, p1)):
                cs = slice(half * 512, (half + 1) * 512)
                k.stt('dve', t_[:, cs], ps[:], st_[:, 1:2], k.gm_row[:, cs], ALU.mult, ALU.mult, r=[ps, st_, k.gm_row], w=[t_])
            k.tt('dve', t_[:], t_[:], x_[:], ALU.add, r=[t_, x_], w=[t_])
            k.dma('sp', xdst[tok, :], t_[:], r=[t_])

        T(0)
        for ti in range(32):
            if ti + 1 < 32:
                T(ti + 1)
            M(ti)
            for _ in range(bg_steps):
                if bg is not None:
                    try:
                        next(bg)
                    except StopIteration:
                        bg = None
        if bg is not None:
            for _ in bg:
                pass


def stage_out_ffn(k, l, xin, xmid, xdst):
    NCH = 22
    with Scope(k) as sc:
        wu = sc.sb("wu", [128, 8, 2 * D_FF], BF16)
        wd = sc.sb("wd", [128, NCH, D], BF16)
        with Scope(k) as s2:
            stg = [s2.sb("wstg%d" % i, [128, 1408], F32) for i in range(4)]

            def bg():
                yield from load_cast_gen(k, stg, wu, lambda c: k.ffn_up[l, c * 128:(c + 1) * 128, :], 2 * D_FF, 8, col_split=4)
                yield from load_cast_gen(k, stg, wd, lambda c: k.ffn_down[l, c * 128:(c + 1) * 128, :], D, NCH)
            stage_out(k, l, xin, xmid, bg=bg(), bg_steps=2)
        stage_ffn(k, l, xmid, xdst, pre=(sc, wu, wd))


def stage_ffn(k, l, xsrc, xdst, pre=None):
    nc = k.nc
    NCH = 22
    with ExitStack() as es_:
        if pre is None:
            sc = es_.enter_context(Scope(k))
            wu = sc.sb("wu", [128, 8, 2 * D_FF], BF16)
            wd = sc.sb("wd", [128, NCH, D], BF16)
            with Scope(k) as s2:
                load_cast(k, s2, wu, lambda c: k.ffn_up[l, c * 128:(c + 1) * 128, :], 2 * D_FF, 8, "wu", col_split=2)
                load_cast(k, s2, wd, lambda c: k.ffn_down[l, c * 128:(c + 1) * 128, :], D, NCH, "wd")
        else:
            sc, wu, wd = pre
        cw = sc.sb("cw", [128, 3, 44], F32)
        cb = sc.sb("cb", [128, 44], F32)
        hal = [sc.sb("hal%d" % i, [128, 44, 2], F32) for i in range(2)]
        k.dma('sp', cw[:], k.conv_w[l], w=[cw])
        k.dma('sp', cb[:], k.conv_b[l], w=[cb])
        k.S.op('pool', lambda: nc.gpsimd.memset(hal[1][:], 0.0), [], [hal[1]])
        actT = sc.sb("actT", [128, NCH, 512], BF16)
        HTs = [sc.sb("H2T%d" % i, [128, 8, 512], BF16) for i in range(2)]
        xt = [sc.sb("xf%d" % i, [128, D], F32) for i in range(2)]
        xn = sc.sb("xnf", [128, D], BF16)
        junk = sc.sb("junkf", [128, 512], BF16)
        st = [sc.sb("stf%d" % i, [128, 4], F32) for i in range(2)]
        Tg = [sc.sb("Tg%d" % i, [128, 512], F32) for i in range(2)]
        Tv = [sc.sb("Tv%d" % i, [128, 512], F32) for i in range(2)]
        psT = sc.ps("psTf", [128, D], BF16)
        psU = [sc.ps("psU%d" % i, [128, 512]) for i in range(4)]
        nxc = {"n": 0}

        def norm_gen(tb):
            HT = HTs[tb % 2]
            t0_ = tb * 4
            k.dma('act', xt[t0_ % 2][:], xsrc[t0_ * 128:(t0_ + 1) * 128, :], w=[xt[t0_ % 2]])
            yield
            for sub in range(4):
                ti = tb * 4 + sub
                x_ = xt[ti % 2]; st_ = st[ti % 2]
                k.act(xn[:], x_[:], AF.Square, r=[x_], w=[xn, st_], scale=1.0 / 32.0, accum=st_[:, 2:3])
                rms_scale(k, st_[:, 2:3], st_)
                k.ts('dve', xn[:], x_[:], st_[:, 1:2], None, ALU.mult, None, r=[x_, st_], w=[xn])
                if sub < 3:
                    k.dma('act', xt[(ti + 1) % 2][:], xsrc[(ti + 1) * 128:(ti + 2) * 128, :], w=[xt[(ti + 1) % 2]])
                yield
                yield
                for kc in range(8):
                    k.transpose(psT[:, kc * 128:(kc + 1) * 128], xn[:, kc * 128:(kc + 1) * 128], k.ident_bf[:], r=[xn, k.ident_bf], w=[psT])
                    if kc == 3:
                        yield
                yield
                for kc in range(8):
                    o = HT[:, kc, sub * 128:(sub + 1) * 128]
                    i_ = psT[:, kc * 128:(kc + 1) * 128]
                    if kc % 2 == 0:
                        k.ts('dve', o, i_, k.modAB[:, 16 + kc:17 + kc], k.modAB[:, 24 + kc:25 + kc], ALU.mult, ALU.add, r=[psT, k.modAB], w=[HT])
                    else:
                        k.act(o, i_, AF.Identity, r=[psT, k.modAB], w=[HT], scale=k.modAB[:, 16 + kc:17 + kc], bias=k.modAB[:, 24 + kc:25 + kc])
                yield

        def step(g):
            if g is not None:
                try:
                    next(g)
                except StopIteration:
                    return None
            return g

        xe = [sc.sb("xe%d" % i, [128, D], F32) for i in range(2)]
        ste = [sc.sb("ste%d" % i, [128, 4], F32) for i in range(2)]
        for _ in norm_gen(0):
            pass
        for tb in range(NTB):
            hin = hal[(tb + 1) % 2]; hout = hal[tb % 2]
            HT = HTs[tb % 2]
            g = norm_gen(tb + 1) if tb + 1 < NTB else None
            for cp in range(NCH):
                tg = Tg[cp % 2]; tv = Tv[cp % 2]
                for which, (T_, c_) in enumerate(((tg, cp), (tv, NCH + cp))):
                    ps = psU[(cp * 2 + which) % 4]
                    k.mm(ps[:], [(wu[:, kc, c_ * 128:(c_ + 1) * 128], HT[:, kc, :]) for kc in range(8)], r=[wu, HT], w=[ps])
                    k.act(T_[:], ps[:], AF.Identity, r=[ps, cw, cb], w=[T_], scale=cw[:, 2, c_:c_ + 1], bias=cb[:, c_:c_ + 1])
                    k.stt('dve', T_[:, 1:512], ps[:, 0:511], cw[:, 1, c_:c_ + 1], T_[:, 1:512], ALU.mult, ALU.add, r=[ps, cw, T_], w=[T_])
                    k.stt('dve', T_[:, 2:512], ps[:, 0:510], cw[:, 0, c_:c_ + 1], T_[:, 2:512], ALU.mult, ALU.add, r=[ps, cw, T_], w=[T_])
                    k.copy('act', hout[:, c_, :], ps[:, 510:512], r=[ps], w=[hout])
                    k.stt('dve', T_[:, 0:1], hin[:, c_, 1:2], cw[:, 1, c_:c_ + 1], T_[:, 0:1], ALU.mult, ALU.add, r=[hin, cw, T_], w=[T_])
                    k.stt('dve', T_[:, 0:2], hin[:, c_, 0:2], cw[:, 0, c_:c_ + 1], T_[:, 0:2], ALU.mult, ALU.add, r=[hin, cw, T_], w=[T_])
                k.act(tg[:], tg[:], AF.Silu, r=[tg], w=[tg])
                k.tt('pool', actT[:, cp, :], tg[:], tv[:], ALU.mult, r=[tg, tv], w=[actT])
                if cp >= 1:
                    g = step(g)
            for sub in range(4):
                ti = tb * 4 + sub
                tok = slice(ti * 128, (ti + 1) * 128)
                x_ = xe[sub % 2]; st_ = ste[sub % 2]
                k.dma('act', x_[:], xsrc[tok, :], w=[x_])
                psF = [psU[(2 * sub) % 4], psU[(2 * sub + 1) % 4]]
                for half in range(2):
                    ps = psF[half]
                    k.mm(ps[:], [(actT[:, cp, sub * 128:(sub + 1) * 128], wd[:, cp, half * 512:(half + 1) * 512]) for cp in range(NCH)],
                         r=[actT, wd], w=[ps])
                    k.act(junk[:, 0:512], ps[:], AF.Square, r=[ps], w=[junk, st_], scale=1.0 / 32.0, accum=st_[:, 2 + half:3 + half])
                k.tt('dve', st_[:, 2:3], st_[:, 2:3], st_[:, 3:4], ALU.add, r=[st_], w=[st_])
                rms_scale(k, st_[:, 2:3], st_)
                t_ = Tg[sub % 2] if False else None
                for half in range(2):
                    cs = slice(half * 512, (half + 1) * 512)
                    T_ = (Tg if half == 0 else Tv)[sub % 2]
                    k.stt('dve', T_[:], psF[half][:], st_[:, 1:2], k.gf_row[:, cs], ALU.mult, ALU.mult, r=[psF[half], st_, k.gf_row], w=[T_])
                    k.tt('pool', x_[:, cs], x_[:, cs], T_[:], ALU.add, r=[x_, T_], w=[x_])
                k.dma('sp', xdst[tok, :], x_[:], r=[x_])
                g = step(g)
            while g is not None:
                g = step(g)


def rwkv_host(inp):
    f = lambda a: np.ascontiguousarray(np.asarray(a, dtype=np.float32))
    mu = np.asarray(inp["rwkv_mu"])
    hd = lambda v: np.asarray(v).reshape(DEPTH, 4, 64).transpose(0, 2, 1)
    pp = np.stack([hd(mu[:, 0:256]), hd(mu[:, 256:512]), hd(mu[:, 512:768]), hd(inp["rwkv_w0"]), hd(inp["rwkv_a0"]),
                   hd(inp["rwkv_k_k"]), hd(inp["rwkv_k_a"]), hd(np.asarray(inp["rwkv_r_k"]).reshape(DEPTH, 256))], axis=2)
    lr = np.zeros((DEPTH, 64, 3), np.float32)
    lr[:, 0:32, 0] = mu[:, 768:800]; lr[:, 0:32, 1] = mu[:, 800:832]; lr[:, :, 2] = mu[:, 832:896]
    i = np.arange(64)
    mk = np.stack([(i[:, None] < i[None, :]), (i[:, None] > i[None, :]), (i[:, None] <= i[None, :]), np.eye(64, dtype=bool)]).astype(np.float32)
    cm = np.ones((64, 512), np.float32); cm[:, ::64] = 0.0
    return {"rwkv_pp": f(pp), "rwkv_lr": f(lr), "rwkv_w_up": f(inp["rwkv_w_up"]), "rwkv_a_up": f(inp["rwkv_a_up"]),
            "rwkv_g_up": f(inp["rwkv_g_up"]), "rwkv_ln": f(np.stack([np.asarray(inp["rwkv_ln_w"]), np.asarray(inp["rwkv_ln_b"])], axis=1)),
            "rwkv_masks": f(mk.transpose(1, 0, 2)), "rwkv_cmask": cm}


def setup_rwkv(k):
    k.rw_pp = k.inp("rwkv_pp", [DEPTH, 64, 8, 4])
    k.rw_lr = k.inp("rwkv_lr", [DEPTH, 64, 3])
    k.rw_wup = k.inp("rwkv_w_up", [DEPTH, 32, 256])
    k.rw_aup = k.inp("rwkv_a_up", [DEPTH, 32, 256])
    k.rw_gup = k.inp("rwkv_g_up", [DEPTH, 64, 256])
    k.rw_ln = k.inp("rwkv_ln", [DEPTH, 2, 256])
    k.rw_masks = k.inp("rwkv_masks", [64, 4, 64])
    k.rw_cmask = k.inp("rwkv_cmask", [64, 512])


def stage_rwkv(k, l):
    nc = k.nc
    BL = 256
    NB = S_LEN // BL
    CPB = BL // 64
    H4 = [64, 4, BL]
    bc = lambda ap, shape: ap.to_broadcast(shape)
    with Scope(k) as sc:
        pp = sc.sb("pp", [64, 8, 4], F32)
        lr = sc.sb("lr", [64, 3], F32)
        wup = sc.sb("wup", [32, 256], F32); aup = sc.sb("aup", [32, 256], F32); gup = sc.sb("gup", [64, 256], F32)
        lnr = sc.sb("lnr", [64, 2, 256], F32)
        mk = sc.sb("mk", [64, 4, 64], F32)
        cmask = sc.sb("cmask", [64, BL], F32)
        ones = sc.sb("ones64", [64, 64], F32)
        prm = sc.sb("prm", [64, 4, 4], F32)
        k.dma('sp', pp[:], k.rw_pp[l], w=[pp]); k.dma('sp', lr[:], k.rw_lr[l], w=[lr])
        k.dma('sp', wup[:], k.rw_wup[l], w=[wup]); k.dma('sp', aup[:], k.rw_aup[l], w=[aup]); k.dma('sp', gup[:], k.rw_gup[l], w=[gup])
        for i in range(2):
            k.dma('sp', lnr[:, i, :], k.rw_ln[l, i:i + 1, :].broadcast_to([64, 256]), w=[lnr])
        k.dma('sp', mk[:], k.rw_masks, w=[mk]); k.dma('sp', cmask[:], k.rw_cmask[:, 0:BL], w=[cmask])
        k.S.op('pool', lambda: nc.gpsimd.memset(ones[:], 1.0), [], [ones])
        k.ts('dve', prm[:, 0, :], pp[:, 3, :], -1.0, None, ALU.mult, None, r=[pp], w=[prm])
        k.ts('dve', prm[:, 1, :], pp[:, 6, :], -1.0, 1.0, ALU.mult, ALU.add, r=[pp], w=[prm])
        P3 = sc.sb("P3", [64, 3, 4, BL], F32)
        halo = sc.sb("halo", [64, 3, 4], F32)
        LR = sc.sb("LR", [64, 3, BL], F32)
        halo2 = sc.sb("halo2", [64, 3], F32)
        ELW = sc.sb("ELW", H4, F32); SC_ = sc.sb("SCAN", H4, F32); AA = sc.sb("AA", H4, F32); KKN = sc.sb("KKN", H4, F32)
        T1 = sc.sb("T1", H4, F32); T2 = sc.sb("T2", H4, F32); CM4 = sc.sb("CM4", H4, F32)
        OUT = [{nm: sc.sb("%s%d" % (nm, i), H4, F32 if nm == "GAM" else BF16) for nm in ("AT", "BT", "KT", "RT", "RK", "GAM", "V")} for i in range(2)]
        SGs = [sc.sb("SG%d" % i, [64, BL], BF16) for i in range(2)]
        gupb = sc.sb("gupb", [64, 256], BF16)
        ppb = sc.sb("ppb", [64, 4], BF16)
        identb64 = k.ident_bf
        XY = [[sc.sb("XY%d_%d" % (i, j), [64, 2, 4, 64], BF16) for j in range(2)] for i in range(2)]
        PP = [[sc.sb("PPi%d_%d" % (i, j), [64, 4, 64], BF16) for j in range(2)] for i in range(2)]
        AKRK = [sc.sb("AKRK%d" % i, [64, 2, 4, 64], BF16) for i in range(2)]
        RBT = [sc.sb("RBT%d" % i, [64, 4, 64], BF16) for i in range(2)]
        TOK = [sc.sb("TOK%d" % i, [64, 3, 4, 64], BF16) for i in range(2)]
        Wsb = sc.sb("Wsb", [64, 4, 64], BF16); Usb = sc.sb("Usb", [64, 4, 64], BF16)
        Hs = [sc.sb("Hs%d" % i, [64, 4, 64], F32) for i in range(2)]
        Hb = [sc.sb("Hb%d" % i, [64, 4, 64], BF16) for i in range(2)]
        yc = sc.sb("yc", [64, 4, 64], F32); ysq = sc.sb("ysq", [64, 4, 64], F32)
        sm = sc.sb("sm", [64, 6, 4], F32)
        yab = [sc.sb("yab%d" % i, [64, CPB, 256], BF16) for i in range(2)]
        psA1 = sc.ps("psA1", [64, 512]); psA2 = sc.ps("psA2", [64, 512]); psA3 = sc.ps("psA3", [64, 512]); psA4 = sc.ps("psA4", [64, 512])
        psH = sc.ps("psHr", [64, 512]); psY = sc.ps("psYr", [64, 512]); psC = sc.ps("psCr", [64, 512]); psQ = sc.ps("psQr", [64, 512])
        k.S.op('pool', lambda: nc.gpsimd.memset(Hs[1][:], 0.0), [], [Hs[1]])
        k.S.op('pool', lambda: nc.gpsimd.memset(Hb[1][:], 0.0), [], [Hb[1]])
        k.copy('dve', gupb[:], gup[:], r=[gup], w=[gupb])
        k.copy('dve', ppb[:], pp[:, 7, :], r=[pp], w=[ppb])
        k.S.op('pool', lambda: nc.gpsimd.memset(halo[:], 0.0), [], [halo])
        k.S.op('pool', lambda: nc.gpsimd.memset(halo2[:], 0.0), [], [halo2])
        k.copy('dve', CM4[:], bc(cmask[:].unsqueeze(1), H4), r=[cmask], w=[CM4])
        E_ = BL - 1

        def prep(tb):
            O = OUT[tb % 2]; SG = SGs[tb % 2]
            AT, BT, KT, RT, RK, GAM, V_ = O["AT"], O["BT"], O["KT"], O["RT"], O["RK"], O["GAM"], O["V"]
            t0 = tb * BL
            for q in range(3):
                k.dma('act', P3[:, q, :, :], k.PT[q * 256:(q + 1) * 256, t0:t0 + BL].rearrange("(h d) t -> d h t", d=64), w=[P3])
            k.dma('act', LR[0:32, 0, :], k.PT[768:800, t0:t0 + BL], w=[LR])
            k.dma('act', LR[0:32, 1, :], k.PT[800:832, t0:t0 + BL], w=[LR])
            k.dma('act', LR[:, 2, :], k.PT[832:896, t0:t0 + BL], w=[LR])
            yield
            for q in range(3):
                p_ = P3[:, q, :, :]
                k.tt('dve', T1[:, :, 1:BL], p_[:, :, 0:E_], p_[:, :, 1:BL], ALU.subtract, r=[P3], w=[T1])
                k.tt('dve', T1[:, :, 0:1], halo[:, q, :].unsqueeze(2), p_[:, :, 0:1], ALU.subtract, r=[P3, halo], w=[T1])
                k.copy('pool', halo[:, q, :].unsqueeze(2), p_[:, :, E_:BL], r=[P3, T1], w=[halo])
                k.tt('pool', T1[:], T1[:], bc(pp[:, q, :].unsqueeze(2), H4), ALU.mult, r=[T1, pp], w=[T1])
                if q < 2:
                    k.tt('pool', p_, p_, T1[:], ALU.add, r=[P3, T1, halo], w=[P3])
                else:
                    k.tt('pool', V_[:], p_, T1[:], ALU.add, r=[P3, T1, halo], w=[V_])
                yield
            for q, rows in ((0, 32), (1, 32), (2, 64)):
                x_ = LR[0:rows, q, :]
                t_ = T2[0:rows, 0, :]
                k.tt('dve', t_[:, 1:BL], x_[:, 0:E_], x_[:, 1:BL], ALU.subtract, r=[LR], w=[T2])
                k.tt('dve', t_[:, 0:1], halo2[0:rows, q:q + 1], x_[:, 0:1], ALU.subtract, r=[LR, halo2], w=[T2])
                k.copy('dve', halo2[0:rows, q:q + 1], x_[:, E_:BL], r=[LR, T2], w=[halo2])
                k.stt('dve', x_, t_, lr[0:rows, q:q + 1], x_, ALU.mult, ALU.add, r=[T2, lr, LR, halo2], w=[LR])
            yield
            R_ = P3[:, 0, :, :]; Kp = P3[:, 1, :, :]
            k.act(LR[0:32, 0, :], LR[0:32, 0, :], AF.Tanh, r=[LR], w=[LR])
            k.act(SG[:], LR[:, 2, :], AF.Sigmoid, r=[LR], w=[SG])
            for h in range(4):
                k.mm(psQ[:, 0:BL], [(wup[:, h * 64:(h + 1) * 64], LR[0:32, 0, :])], r=[wup, LR], w=[psQ])
                k.act(T1[:, h, :], psQ[:, 0:BL], AF.Exp, r=[psQ, prm], w=[T1], scale=-1.0, bias=prm[:, 0, h:h + 1])
                k.mm(psQ[:, BL:2 * BL], [(aup[:, h * 64:(h + 1) * 64], LR[0:32, 1, :])], r=[aup, LR], w=[psQ], start=False)
                k.act(AA[:, h, :], psQ[:, BL:2 * BL], AF.Sigmoid, r=[psQ, pp], w=[AA], bias=pp[:, 4, h:h + 1])
                yield
            k.act(T1[:], T1[:], AF.Ln, r=[T1], w=[T1], bias=1.0)
            k.act(ELW[:], T1[:], AF.Exp, r=[T1], w=[ELW], scale=-1.0, bias=-0.5)
            k.S.op('dve', lambda: nc.vector.tensor_tensor_scan(
                out=SC_[:].rearrange("p h t -> p (h t)"), data0=CM4[:].rearrange("p h t -> p (h t)"),
                data1=ELW[:].rearrange("p h t -> p (h t)"), initial=0.0, op0=ALU.mult, op1=ALU.add), [CM4, ELW], [SC_])
            yield
            k.tt('pool', KKN[:], Kp, bc(pp[:, 5, :].unsqueeze(2), H4), ALU.mult, r=[P3, pp], w=[KKN])
            k.tt('pool', T1[:], KKN[:], KKN[:], ALU.mult, r=[KKN], w=[T1])
            for h in range(4):
                k.mm(psQ[:, 0:BL], [(ones[:], T1[:, h, :])], r=[ones, T1], w=[psQ])
                k.act(T2[:, h, :], psQ[:, 0:BL], AF.Ln, r=[psQ], w=[T2], bias=1e-24)
                yield
            k.act(T2[:], T2[:], AF.Exp, r=[T2], w=[T2], scale=-0.5)
            k.tt('dve', KKN[:], KKN[:], T2[:], ALU.mult, r=[KKN, T2], w=[KKN])
            yield
            k.tt('pool', T1[:], SC_[:], ELW[:], ALU.subtract, r=[SC_, ELW], w=[T1])
            k.act(T1[:], T1[:], AF.Exp, r=[T1], w=[T1], scale=-1.0)
            k.stt('dve', AT[:], KKN[:], -1.0, T1[:], ALU.mult, ALU.mult, r=[KKN, T1], w=[AT])
            yield
            k.act(T2[:], SC_[:], AF.Exp, r=[SC_], w=[T2])
            k.tt('pool', T1[:], KKN[:], AA[:], ALU.mult, r=[KKN, AA], w=[T1])
            k.tt('dve', BT[:], T1[:], T2[:], ALU.mult, r=[T1, T2], w=[BT])
            yield
            k.tt('pool', T1[:], AA[:], bc(pp[:, 6, :].unsqueeze(2), H4), ALU.mult, r=[AA, pp], w=[T1])
            k.tt('pool', T1[:], T1[:], bc(prm[:, 1, :].unsqueeze(2), H4), ALU.add, r=[T1, prm], w=[T1])
            k.tt('dve', Kp, Kp, T1[:], ALU.mult, r=[P3, T1, KKN], w=[P3])
            yield
            k.tt('dve', KT[:], Kp, T2[:], ALU.mult, r=[P3, T2], w=[KT])
            k.tt('pool', RK[:], R_, Kp, ALU.mult, r=[P3], w=[RK])
            k.act(GAM[:], SC_[:], AF.Exp, r=[SC_], w=[GAM], scale=-1.0)
            k.tt('dve', RT[:], R_, GAM[:], ALU.mult, r=[P3, GAM], w=[RT])
            yield

        def phaseA(nch):
            tb, n = divmod(nch, CPB)
            O = OUT[tb % 2]
            AT, BT, KT, RT, V_ = O["AT"], O["BT"], O["KT"], O["RT"], O["V"]
            c_ = slice(n * 64, (n + 1) * 64)
            par = nch % 2
            xy = XY[par][0]; akrk = AKRK[par]; rbt = RBT[par]; tok = TOK[par]
            fns = []
            for h in range(4):
                fns.append(lambda h=h: nc.tensor.matmul(psA1[:, h * 64:(h + 1) * 64], lhsT=BT[:, h, c_], rhs=AT[:, h, c_], start=True, stop=True, skip_group_check=True))
                fns.append(lambda h=h: nc.tensor.matmul(psA1[:, 256 + h * 64:256 + (h + 1) * 64], lhsT=AT[:, h, c_], rhs=BT[:, h, c_], start=True, stop=True, skip_group_check=True))
            k.S.pe_group(fns, [AT, BT], [psA1])
            fns = []
            for h in range(4):
                fns.append(lambda h=h: nc.tensor.matmul(psA2[:, h * 64:(h + 1) * 64], lhsT=KT[:, h, c_], rhs=AT[:, h, c_], start=True, stop=True, skip_group_check=True))
                fns.append(lambda h=h: nc.tensor.matmul(psA2[:, 256 + h * 64:256 + (h + 1) * 64], lhsT=KT[:, h, c_], rhs=RT[:, h, c_], start=True, stop=True, skip_group_check=True))
            k.S.pe_group(fns, [AT, KT, RT], [psA2])
            k.S.pe_group([lambda h=h: nc.tensor.matmul(psA3[:, h * 64:(h + 1) * 64], lhsT=BT[:, h, c_], rhs=RT[:, h, c_], start=True, stop=True, skip_group_check=True)
                          for h in range(4)], [BT, RT], [psA3])
            fns = []
            psQb = psQ[:, :].bitcast(BF16)
            for qi, src_ in enumerate((V_, BT, KT)):
                for h in range(4):
                    dst = psQb[:, qi * 256 + h * 64:qi * 256 + (h + 1) * 64]
                    fns.append(lambda dst=dst, s_=src_[:, h, c_]: nc.tensor.transpose(out=dst, in_=s_, identity=k.ident_bf[0:64, 0:64]))
            k.S.pe_group(fns, [V_, BT, KT, k.ident_bf], [psQ])
            k.copy('act', tok[:].rearrange("p a h f -> p (a h f)"), psQb[:, 0:768], r=[psQ], w=[tok])
            yield
            v4 = lambda ps, a: ps[:, a * 256:(a + 1) * 256].rearrange("p (h f) -> p h f", h=4)
            mb = lambda i: bc(mk[:, i, :].unsqueeze(1), [64, 4, 64])
            k.tt('dve', xy[:, 0, :, :], v4(psA1, 0), mb(0), ALU.mult, r=[psA1, mk], w=[xy])
            k.tt('dve', xy[:, 1, :, :], v4(psA1, 1), mb(1), ALU.mult, r=[psA1, mk], w=[xy])
            k.tt('dve', akrk[:, 0, :, :], v4(psA2, 0), mb(0), ALU.mult, r=[psA2, mk], w=[akrk])
            k.tt('dve', akrk[:, 1, :, :], v4(psA2, 1), mb(2), ALU.mult, r=[psA2, mk], w=[akrk])
            k.tt('dve', rbt[:], v4(psA3, 0), mb(2), ALU.mult, r=[psA3, mk], w=[rbt])
            P_ = PP[par][0]
            k.tt('dve', P_[:], xy[:, 0, :, :], mb(3), ALU.add, r=[xy, mk], w=[P_])
            yield
            Pm = None
            for lev in range(1, 7):
                xyn = XY[par][lev % 2]
                fns = []
                rd = [xy]
                wr = []
                if lev <= 5:
                    for h in range(4):
                        fns.append(lambda h=h, xy=xy: nc.tensor.matmul(psA4[:, 256 + h * 64:256 + (h + 1) * 64], lhsT=xy[:, 0, h, :], rhs=xy[:, 1, h, :], start=True, stop=True, skip_group_check=True))
                        if lev <= 4:
                            fns.append(lambda h=h, xy=xy: nc.tensor.matmul(psA4[:, h * 64:(h + 1) * 64], lhsT=xy[:, 1, h, :], rhs=xy[:, 0, h, :], start=True, stop=True, skip_group_check=True))
                    wr.append(psA4)
                if lev >= 2:
                    for h in range(4):
                        fns.append(lambda h=h, xy=xy, Pm=Pm: nc.tensor.matmul(psA3[:, 256 + h * 64:256 + (h + 1) * 64], lhsT=xy[:, 1, h, :], rhs=Pm[:, h, :], start=True, stop=True, skip_group_check=True))
                    rd.append(Pm); wr.append(psA3)
                k.S.pe_group(fns, rd, wr)
                yield
                if lev <= 4:
                    k.copy('act', xyn[:].rearrange("p a h f -> p (a h f)"), psA4[:, :], r=[psA4], w=[xyn])
                elif lev == 5:
                    k.copy('act', xyn[:, 1, :, :].rearrange("p h f -> p (h f)"), psA4[:, 256:512], r=[psA4], w=[xyn])
                if lev >= 2:
                    Pn = PP[par][(lev - 1) % 2]
                    k.tt('dve', Pn[:], Pm[:], v4(psA3, 1), ALU.add, r=[Pm, psA3], w=[Pn])
                    Pm = Pn
                else:
                    Pm = P_
                yield
                xy = xyn

        def phaseB(nch):
            tb, n = divmod(nch, CPB)
            O = OUT[tb % 2]; SG = SGs[tb % 2]
            AT, RT, RK, GAM = O["AT"], O["RT"], O["RK"], O["GAM"]
            c_ = slice(n * 64, (n + 1) * 64)
            par = nch % 2
            akrk = AKRK[par]; rbt = RBT[par]; tok = TOK[par]; TT = PP[par][1]
            Hold = Hs[(nch + 1) % 2]; Hnew = Hs[nch % 2]
            Hbo = Hb[(nch + 1) % 2]; Hbn = Hb[nch % 2]
            yab_ = yab[tb % 2]
            fns = []
            for h in range(4):
                fns.append(lambda h=h: nc.tensor.matmul(psH[:, h * 64:(h + 1) * 64], lhsT=AT[:, h, c_], rhs=Hbo[:, h, :], start=(h == 0), stop=False, skip_group_check=True))
                fns.append(lambda h=h: nc.tensor.matmul(psH[:, h * 64:(h + 1) * 64], lhsT=akrk[:, 0, h, :], rhs=tok[:, 0, h, :], start=False, stop=True, skip_group_check=True))
            k.S.pe_group(fns, [AT, Hbo, akrk, tok], [psH])
            yield
            k.copy('act', Wsb[:].rearrange("p h f -> p (h f)"), psH[:, 0:256], r=[psH], w=[Wsb])
            yield
            k.S.pe_group([lambda h=h: nc.tensor.matmul(psH[:, 256 + h * 64:256 + (h + 1) * 64], lhsT=TT[:, h, :], rhs=Wsb[:, h, :], start=False, stop=True, skip_group_check=True)
                          for h in range(4)], [TT, Wsb], [psH])
            yield
            k.copy('act', Usb[:].rearrange("p h f -> p (h f)"), psH[:, 256:512], r=[psH], w=[Usb])
            yield
            fns = []
            for h in range(4):
                fns.append(lambda h=h: nc.tensor.matmul(psC[:, h * 64:(h + 1) * 64], lhsT=tok[:, 1, h, :], rhs=Usb[:, h, :], start=(h == 0), stop=False, skip_group_check=True))
                fns.append(lambda h=h: nc.tensor.matmul(psC[:, h * 64:(h + 1) * 64], lhsT=tok[:, 2, h, :], rhs=tok[:, 0, h, :], start=False, stop=True, skip_group_check=True))
                fns.append(lambda h=h: nc.tensor.matmul(psC[:, 256 + h:256 + h + 1], lhsT=RK[:, h, c_], rhs=ppb[:, h:h + 1], start=False, stop=True, skip_group_check=True))
            k.S.pe_group(fns, [tok, Usb, RK, ppb], [psC])
            fns = []
            for h in range(4):
                fns.append(lambda h=h: nc.tensor.matmul(psY[:, h * 64:(h + 1) * 64], lhsT=RT[:, h, c_], rhs=Hbo[:, h, :], start=(h == 0), stop=False, skip_group_check=True))
                fns.append(lambda h=h: nc.tensor.matmul(psY[:, h * 64:(h + 1) * 64], lhsT=rbt[:, h, :], rhs=Usb[:, h, :], start=False, stop=False, skip_group_check=True))
                fns.append(lambda h=h: nc.tensor.matmul(psY[:, h * 64:(h + 1) * 64], lhsT=akrk[:, 1, h, :], rhs=tok[:, 0, h, :], start=False, stop=True, skip_group_check=True))
            fns.append(lambda: nc.tensor.matmul(psY[:, 256:512], lhsT=SG[:, c_], rhs=gupb[:, :], start=False, stop=True, skip_group_check=True))
            k.S.pe_group(fns, [RT, Hbo, rbt, Usb, akrk, tok, SG, gupb], [psY])
            yield
            k.tt('dve', Hnew[:], psC[:, 0:256].rearrange("p (h f) -> p h f", h=4), Hold[:], ALU.add, r=[psC, Hold], w=[Hnew])
            k.copy('dve', sm[:, 5, :], psC[:, 256:260], r=[psC], w=[sm])
            k.tt('dve', Hbn[:], Hnew[:], bc(GAM[:, :, n * 64 + 63:n * 64 + 64], [64, 4, 64]), ALU.mult, r=[Hnew, GAM], w=[Hbn])
            k.tt('pool', Hnew[:], Hnew[:], bc(GAM[:, :, n * 64 + 63:n * 64 + 64], [64, 4, 64]), ALU.mult, r=[Hnew, GAM], w=[Hnew])
            yield
            y3 = psY[:, 0:256].rearrange("p (h f) -> p h f", h=4)
            k.S.op('dve', lambda: nc.vector.reduce_sum(out=sm[:, 0, :], in_=y3, axis=AX.X), [psY], [sm])
            k.ts('dve', sm[:, 1, :], sm[:, 0, :], 1.0 / 64.0, None, ALU.mult, None, r=[sm], w=[sm])
            k.tt('dve', yc[:], y3, bc(sm[:, 1, :].unsqueeze(2), [64, 4, 64]), ALU.subtract, r=[psY, sm], w=[yc])
            yield
            k.tt('pool', ysq[:], yc[:], yc[:], ALU.mult, r=[yc], w=[ysq])
            k.S.op('dve', lambda: nc.vector.reduce_sum(out=sm[:, 2, :], in_=ysq[:], axis=AX.X), [ysq], [sm])
            k.act(sm[:, 3, :], sm[:, 2, :], AF.Ln, r=[sm], w=[sm], scale=1.0 / 64.0, bias=GN_EPS)
            k.act(sm[:, 4, :], sm[:, 3, :], AF.Exp, r=[sm], w=[sm], scale=-0.5)
            yield
            k.tt('dve', yc[:], yc[:], bc(sm[:, 4, :].unsqueeze(2), [64, 4, 64]), ALU.mult, r=[yc, sm], w=[yc])
            k.tt('pool', yc[:], yc[:], lnr[:, 0, :].rearrange("p (h f) -> p h f", h=4), ALU.mult, r=[yc, lnr], w=[yc])
            k.tt('pool', yc[:], yc[:], lnr[:, 1, :].rearrange("p (h f) -> p h f", h=4), ALU.add, r=[yc, lnr], w=[yc])
            k.tt('dve', ysq[:], tok[:, 0, :, :], bc(sm[:, 5, :].unsqueeze(2), [64, 4, 64]), ALU.mult, r=[tok, sm], w=[ysq])
            yield
            k.tt('pool', yc[:], yc[:], ysq[:], ALU.add, r=[yc, ysq], w=[yc])
            k.tt('dve', yab_[:, n, :], yc[:].rearrange("p h f -> p (h f)"), psY[:, 256:512], ALU.mult, r=[yc, psY], w=[yab_])
            if n == CPB - 1:
                k.dma('sp', k.Y[tb * BL:(tb + 1) * BL, 0:256].rearrange("(n p) c -> p n c", p=64), yab_[:], r=[yab_])
            yield

        def run_all(*gens):
            gens = [g for g in gens if g is not None]
            while gens:
                for g in list(gens):
                    try:
                        next(g)
                    except StopIteration:
                        gens.remove(g)

        NCH = S_LEN // 64
        run_all(prep(0))
        run_all(phaseA(0), prep(1) if NB > 1 else None)
        gp = None
        for nch in range(NCH):
            tb, n = divmod(nch, CPB)
            if n == 0 and tb >= 1 and tb + 1 < NB:
                gp = prep(tb + 1)
            gens = [phaseB(nch)]
            if nch + 1 < NCH:
                gens.append(phaseA(nch + 1))
            rnd = 0
            while gens:
                for gi, g in enumerate(list(gens)):
                    for _rep in range(2 if (gi == 0 and RW_BPRIO) else 1):
                        try:
                            next(g)
                        except StopIteration:
                            if g in gens:
                                gens.remove(g)
                            break
                rnd += 1
                if gp is not None and rnd % 2 == 0:
                    try:
                        next(gp)
                    except StopIteration:
                        gp = None
            if n == CPB - 2 and gp is not None:
                for _ in gp:
                    pass
                gp = None


def build(nlayers=DEPTH, taps=()):
    k = K(nlayers, taps=taps)
    setup_globals(k)
    setup_fox(k)
    setup_rwkv(k)
    setup_ffn(k)
    setup_nsa(k)
    for l in range(nlayers):
        xin = k.x_in if l == 0 else k.XR
        xout = k.OUT if l == nlayers - 1 else k.XR
        stage_mod_proj(k, l, xin)
        stage_rwkv(k, l)
        stage_fox(k, l)
        stage_nsa(k, l)
        stage_out_ffn(k, l, xin, k.XR1, xout)
    k.S.barrier()
    return k


_CACHE = {}


def kernel(**inputs):
    if "k" not in _CACHE:
        _CACHE["k"] = build(DEPTH)
    k = _CACHE["k"]
    sh = prep_shared(inputs)
    in_maps = []
    for b in range(8):
        d = dict(sh)
        d.update(prep_core(inputs, b))
        in_maps.append({n: v for n, v in d.items() if n in k.ins})
    res = run_bass_kernel_spmd(k.nc, in_maps, core_ids=list(range(8)))
    out = np.stack([np.asarray(res.results[b]["out"], dtype=np.float32) for b in range(8)], axis=0)
    return out
```

```python
import numpy as np
import ml_dtypes
from contextlib import ExitStack
import concourse.bass as bass
import concourse.mybir as mybir
from concourse.bass_utils import run_bass_kernel_spmd

F32 = mybir.dt.float32
BF16 = mybir.dt.bfloat16
AF = mybir.ActivationFunctionType
ALU = mybir.AluOpType
AX = mybir.AxisListType
NPBF = ml_dtypes.bfloat16

S_LEN = 4096
D = 1024
DEPTH = 4
NTB = 8
N_IN = 3224
D_FF = 2816
NEG = -30000.0
RMS_EPS = 1e-6
GN_EPS = 64e-5


class Sched:
    ENG = ('pe', 'act', 'dve', 'pool')
    LIMIT = 30000

    def __init__(self, nc):
        self.nc = nc
        self.e = {'pe': nc.tensor, 'act': nc.scalar, 'dve': nc.vector, 'pool': nc.gpsimd, 'sp': nc.sync}
        self.epoch = {k: 0 for k in self.ENG}
        self.sem = {k: nc.alloc_semaphore("c_%s_0" % k) for k in self.ENG}
        self.cnt = {k: 0 for k in self.ENG}
        self.seen = {k: {} for k in self.e}
        self.lastw = {}
        self.reads = {}
        self.dma_sems = {'hw': [[nc.alloc_semaphore("d%d" % i), 0, "dma%d" % i] for i in range(24)],
                         'sw': [[nc.alloc_semaphore("ds%d" % i), 0, "dmas%d" % i] for i in range(8)]}
        self.ndma = {'hw': 0, 'sw': 0}
        self.n_inst = 0
        self.n_wait = 0
        self.per = {}

    def _wait(self, eng, tok):
        key, sem, val = tok
        if self.seen[eng].get(key, 0) >= val:
            return
        self.e[eng].wait_ge(sem, val)
        self.n_wait += 1
        self.per[eng] = self.per.get(eng, 0) + 1
        self.seen[eng][key] = val

    def _deps(self, eng, reads, writes):
        for b in reads:
            t = self.lastw.get(b)
            if t is not None:
                self._wait(eng, t)
        for b in writes:
            t = self.lastw.get(b)
            if t is not None:
                self._wait(eng, t)
            for t in self.reads.get(b, ()):
                self._wait(eng, t)

    def _commit(self, tok, reads, writes):
        for b in reads:
            self.reads.setdefault(b, []).append(tok)
        for b in writes:
            self.lastw[b] = tok
            self.reads[b] = []

    def _bump(self, eng, ins):
        if self.cnt[eng] >= self.LIMIT:
            self.epoch[eng] += 1
            self.sem[eng] = self.nc.alloc_semaphore("c_%s_%d" % (eng, self.epoch[eng]))
            self.cnt[eng] = 0
        self.cnt[eng] += 1
        ins.then_inc(self.sem[eng], 1)
        return ("%s_%d" % (eng, self.epoch[eng]), self.sem[eng], self.cnt[eng])

    @staticmethod
    def _norm(reads, writes):
        rd = [getattr(b, 'n', b) for b in reads]
        wr = [getattr(b, 'n', b) for b in writes]
        ps = [b for b in rd if b.startswith("ps")]
        rd = [b for b in rd if not b.startswith("ps")]
        return rd, wr + [b for b in ps if b not in wr]

    def op(self, eng, inst_fn, reads=(), writes=()):
        reads, writes = self._norm(reads, writes)
        self._deps(eng, reads, writes)
        ins = inst_fn()
        self.per[eng] = self.per.get(eng, 0) + 1
        tok = self._bump(eng, ins)
        self._commit(tok, reads, writes)
        self.n_inst += 1
        return tok

    def pe_group(self, fns, reads=(), writes=()):
        reads, writes = self._norm(reads, writes)
        self._deps('pe', reads, writes)
        ins = None
        for f in fns:
            ins = f()
            self.n_inst += 1
            self.per['pe'] = self.per.get('pe', 0) + 1
        tok = self._bump('pe', ins)
        self._commit(tok, reads, writes)
        return tok

    def dma(self, q, out, in_, reads=(), writes=(), **kw):
        reads, writes = self._norm(reads, writes)
        self._deps(q, reads, writes)
        cls = 'sw' if q == 'pool' else 'hw'
        pool_ = self.dma_sems[cls]
        slot = pool_[self.ndma[cls] % len(pool_)]
        self.ndma[cls] += 1
        if slot[1] > 0:
            self._wait(q, (slot[2], slot[0], slot[1]))
        if slot[1] >= self.LIMIT:
            slot[0] = self.nc.alloc_semaphore("%s_e%d" % (slot[2], self.ndma[cls]))
            slot[1] = 0
            slot[2] = slot[2] + "x"
        slot[1] += 16
        ins = self.e[q].dma_start(out=out, in_=in_, **kw)
        self.per[q] = self.per.get(q, 0) + 1
        ins.then_inc(slot[0], 16)
        tok = (slot[2], slot[0], slot[1])
        self._commit(tok, reads, writes)
        self.n_inst += 1
        return tok

    def barrier(self, engines=('pe', 'act', 'dve', 'pool', 'sp')):
        toks = [("%s_%d" % (k, self.epoch[k]), self.sem[k], self.cnt[k]) for k in self.ENG if self.cnt[k] > 0]
        toks += [(s[2], s[0], s[1]) for p_ in self.dma_sems.values() for s in p_ if s[1] > 0]
        for e in engines:
            for t in toks:
                self._wait(e, t)
        self.lastw = {}
        self.reads = {}


class Pipe:
    def __init__(self, lag=2):
        self.q = []
        self.lag = lag

    def push(self, first, second):
        first()
        self.q.append(second)
        while len(self.q) > self.lag:
            self.q.pop(0)()

    def flush(self):
        while self.q:
            self.q.pop(0)()


class Tile:
    def __init__(self, h, name):
        self.h = h
        self.n = name

    def __getitem__(self, idx):
        return self.h[idx]


class Scope:
    cnt = 0

    def __init__(self, k):
        self.k = k
        self.es = ExitStack()

    def __enter__(self):
        self.es.__enter__()
        Scope.cnt += 1
        self.id = Scope.cnt
        return self

    def sb(self, name, shape, dt):
        nm = "%s_%d" % (name, self.id)
        h = self.es.enter_context(self.k.nc.sbuf_tensor(nm, list(shape), dt))
        return Tile(h, nm)

    def ps(self, name, shape, dt=F32):
        nm = "%s_%d" % (name, self.id)
        h = self.es.enter_context(self.k.nc.psum_tensor(nm, list(shape), dt))
        return Tile(h, nm)

    def __exit__(self, *a):
        self.k.S.barrier()
        return self.es.__exit__(*a)


class K:
    def __init__(self, nlayers, taps=()):
        self.nc = bass.Bass("TRN2", target_bir_lowering=False)
        self.S = Sched(self.nc)
        self.nl = nlayers
        self.taps = set(taps)
        self.ins = {}
        self.dr = {}

    def inp(self, name, shape, dt=F32):
        t = self.nc.dram_tensor(name, list(shape), dt, kind="ExternalInput").ap()
        self.ins[name] = t
        return t

    def scratch(self, name, shape, dt=F32, out=False):
        kind = "ExternalOutput" if (out or name in self.taps) else "Internal"
        t = self.nc.dram_tensor(name, list(shape), dt, kind=kind).ap()
        self.dr[name] = t
        return t

    def act(self, out, in_, func, r, w, bias=0.0, scale=1.0, accum=None):
        nc = self.nc
        if accum is None:
            return self.S.op('act', lambda: nc.scalar.activation(out=out, in_=in_, func=func, bias=bias, scale=scale), r, w)
        return self.S.op('act', lambda: nc.scalar.activation(out=out, in_=in_, func=func, bias=bias, scale=scale, accum_out=accum), r, w)

    def ts(self, eng, out, in0, s1, s2, op0, op1, r, w):
        e = self.S.e[eng]
        if op1 is None:
            return self.S.op(eng, lambda: e.tensor_scalar(out=out, in0=in0, scalar1=s1, scalar2=None, op0=op0), r, w)
        return self.S.op(eng, lambda: e.tensor_scalar(out=out, in0=in0, scalar1=s1, scalar2=s2, op0=op0, op1=op1), r, w)

    def tt(self, eng, out, in0, in1, op, r, w):
        e = self.S.e[eng]
        return self.S.op(eng, lambda: e.tensor_tensor(out=out, in0=in0, in1=in1, op=op), r, w)

    def stt(self, eng, out, in0, scalar, in1, op0, op1, r, w):
        e = self.S.e[eng]
        return self.S.op(eng, lambda: e.scalar_tensor_tensor(out=out, in0=in0, scalar=scalar, in1=in1, op0=op0, op1=op1), r, w)

    def copy(self, eng, out, in_, r, w):
        if eng == 'act':
            return self.S.op('act', lambda: self.nc.scalar.copy(out=out, in_=in_), r, w)
        e = self.S.e[eng]
        return self.S.op(eng, lambda: e.tensor_copy(out=out, in_=in_), r, w)

    def mm(self, out, pairs, r, w, start=True, stop=True, sgc=False):
        nc = self.nc
        n = len(pairs)
        fns = []
        for i, (l, rh) in enumerate(pairs):
            fns.append(lambda l=l, rh=rh, i=i: nc.tensor.matmul(out, lhsT=l, rhs=rh, start=(start and i == 0), stop=(stop and i == n - 1),
                                                               skip_group_check=(sgc or not start)))
        return self.S.pe_group(fns, r, w)

    def transpose(self, out, in_, ident, r, w):
        nc = self.nc
        return self.S.op('pe', lambda: nc.tensor.transpose(out=out, in_=in_, identity=ident), r, w)

    def dma(self, q, out, in_, r=(), w=(), **kw):
        return self.S.dma(q, out, in_, r, w, **kw)


def w_in_perm_index():
    idx = list(range(0, 896))
    idx += list(range(896, 1664))
    for c in range(3):
        idx += list(range(2054 + c * 64, 2054 + c * 64 + 64))
        idx += list(range(2054 + (c + 3) * 64, 2054 + (c + 3) * 64 + 64))
    idx += list(range(2438, 2566))
    idx += list(range(2566, 2694))
    idx += list(range(2694, 2822))
    idx += list(range(2950, 3078))
    idx += list(range(2048, 2054))
    idx += list(range(1664, 2048))
    idx += list(range(2822, 2950))
    idx += list(range(3078, 3206))
    idx += list(range(3206, 3224))
    assert len(idx) == N_IN and len(set(idx)) == N_IN
    return np.array(idx)


QKT_ROWS = 1664


def setup_globals(k):
    nc = k.nc
    k.x_in = k.inp("x", [S_LEN, D])
    k.cT = k.inp("cT", [128, 8])
    k.ada_w = k.inp("ada_w", [DEPTH, D, 6 * D])
    k.ada_b_fm = k.inp("ada_b_fm", [DEPTH, 128, 48])
    k.ada_b_row = k.inp("ada_b_row", [DEPTH, 6 * D])
    k.normg_fm = k.inp("normg_fm", [DEPTH, 4, 128, 8])
    k.normg_row = k.inp("normg_row", [DEPTH, 4, D])
    k.w_in = k.inp("w_in_p", [DEPTH, D, N_IN])
    k.ident_bf_d = k.inp("ident_bf", [128, 128], BF16)
    k.ident_f_d = k.inp("ident_f", [128, 128], F32)

    k.PT = k.scratch("PT", [896, S_LEN], F32)
    k.QKT = k.scratch("QKT", [QKT_ROWS, S_LEN], BF16)
    k.FL = k.scratch("FL", [6, S_LEN], F32)
    k.VT = k.scratch("VT", [S_LEN, 640], BF16)
    k.GT = k.scratch("GT", [S_LEN, 18], F32)
    k.Y = k.scratch("Y", [S_LEN, D], BF16)
    k.XR = k.scratch("XR", [S_LEN, D], F32)
    k.XR1 = k.scratch("XR1", [S_LEN, D], F32)
    k.OUT = k.scratch("out", [S_LEN, D], F32, out=True)

    def pers(name, shape, dt):
        return Tile(nc.alloc_sbuf_tensor(name, list(shape), dt), name)
    k.ident_bf = pers("ident_bf_sb", [128, 128], BF16)
    k.ident_f = pers("ident_f_sb", [128, 128], F32)
    k.sc = pers("sc", [128, 8], F32)
    k.modAB = pers("modAB", [128, 32], F32)
    k.gm_row = pers("gm_row", [128, D], F32)
    k.gf_row = pers("gf_row", [128, D], F32)
    k.dma('sp', k.ident_bf[:], k.ident_bf_d, w=[k.ident_bf])
    k.dma('sp', k.ident_f[:], k.ident_f_d, w=[k.ident_f])
    k.dma('sp', k.sc[:], k.cT, w=[k.sc])
    k.act(k.sc[:], k.sc[:], AF.Silu, r=[k.sc], w=[k.sc])


def stage_mod(k, l, bg=None):
    with Scope(k) as sc:
        slab = [sc.sb("adaslab%d" % i, [128, 6 * D], F32) for i in range(4)]
        psA = sc.ps("psA", [128, 32])
        psR = [sc.ps("psR%d" % i, [128, 512]) for i in range(4)]
        bfm = sc.sb("bfm", [128, 48], F32)
        gfm = sc.sb("gfm", [128, 4, 8], F32)
        brow = sc.sb("brow", [128, 2, D], F32)
        grow = sc.sb("grow", [128, 2, D], F32)
        mfm = sc.sb("mfm", [128, 32], F32)
        sc_rep = sc.sb("sc_rep", [128, 8, 128], F32)
        for kc in range(8):
            k.copy('dve', sc_rep[:, kc, :], k.sc[:, kc:kc + 1].to_broadcast([128, 128]), r=[k.sc], w=[sc_rep])
        k.dma('sp', bfm[:], k.ada_b_fm[l], w=[bfm])
        k.dma('sp', gfm[:], k.normg_fm[l].rearrange("g p c -> p g c"), w=[gfm])
        k.dma('sp', brow[:, 0, :], k.ada_b_row[l:l + 1, 2 * D:3 * D].broadcast_to([128, D]), w=[brow])
        k.dma('sp', brow[:, 1, :], k.ada_b_row[l:l + 1, 5 * D:6 * D].broadcast_to([128, D]), w=[brow])
        k.dma('sp', grow[:, 0, :], k.normg_row[l, 1:2, :].broadcast_to([128, D]), w=[grow])
        k.dma('sp', grow[:, 1, :], k.normg_row[l, 3:4, :].broadcast_to([128, D]), w=[grow])
        fm_chunks = list(range(0, 16)) + list(range(24, 40))
        row_cols = [2 * D, 2 * D + 512, 5 * D, 5 * D + 512]
        for kc in range(8):
            sl = slab[kc % 4]
            k.dma('sp' if kc % 2 == 0 else 'act', sl[:], k.ada_w[l, kc * 128:(kc + 1) * 128, :], w=[sl])
            for i, j in enumerate(fm_chunks):
                k.mm(psA[:, i:i + 1], [(sl[:, j * 128:(j + 1) * 128], k.sc[:, kc:kc + 1])], r=[sl, k.sc], w=[psA],
                     start=(kc == 0 and i == 0), stop=(kc == 7), sgc=True)
            for i, c0 in enumerate(row_cols):
                k.mm(psR[i][:], [(sc_rep[:, kc, :], sl[:, c0:c0 + 512])], r=[sl, sc_rep], w=[psR[i]],
                     start=(kc == 0), stop=(kc == 7), sgc=True)
            if bg is not None:
                try:
                    next(bg)
                except StopIteration:
                    bg = None
        if bg is not None:
            for _ in bg:
                pass
        k.tt('dve', mfm[:, 0:16], psA[:, 0:16], bfm[:, 0:16], ALU.add, r=[psA, bfm], w=[mfm])
        k.tt('dve', mfm[:, 16:32], psA[:, 16:32], bfm[:, 24:40], ALU.add, r=[psA, bfm], w=[mfm])
        k.stt('dve', k.modAB[:, 0:8], mfm[:, 8:16], 1.0, gfm[:, 0, :], ALU.add, ALU.mult, r=[mfm, gfm], w=[k.modAB])
        k.copy('dve', k.modAB[:, 8:16], mfm[:, 0:8], r=[mfm], w=[k.modAB])
        k.stt('dve', k.modAB[:, 16:24], mfm[:, 24:32], 1.0, gfm[:, 2, :], ALU.add, ALU.mult, r=[mfm, gfm], w=[k.modAB])
        k.copy('dve', k.modAB[:, 24:32], mfm[:, 16:24], r=[mfm], w=[k.modAB])
        for i in range(4):
            dst = (k.gm_row if i < 2 else k.gf_row)
            cs = slice((i % 2) * 512, (i % 2) * 512 + 512)
            k.tt('dve', dst[:, cs], psR[i][:], brow[:, i // 2, cs], ALU.add, r=[psR[i], brow], w=[dst])
            k.tt('pool', dst[:, cs], dst[:, cs], grow[:, i // 2, cs], ALU.mult, r=[dst, grow], w=[dst])


def stage_mod_proj(k, l, xsrc):
    with Scope(k) as sc:
        wsb = sc.sb("wsb", [128, 8, N_IN], BF16)
        with Scope(k) as s2:
            wst = [s2.sb("wst%d" % i, [128, N_IN], F32) for i in range(2)]

            def bg():
                for kc in range(8):
                    s = wst[kc % 2]
                    k.dma('act' if kc % 2 == 0 else 'sp', s[:], k.w_in[l, kc * 128:(kc + 1) * 128, :], w=[s])
                    k.copy('pool' if kc % 2 == 0 else 'dve', wsb[:, kc, :], s[:], r=[s], w=[wsb])
                    yield
            stage_mod(k, l, bg=bg())
        stage_proj(k, l, xsrc, pre=(sc, wsb))


def stage_proj(k, l, xsrc, pre=None):
    nc = k.nc
    with ExitStack() as es_:
        if pre is None:
            sc = es_.enter_context(Scope(k))
            wsb = sc.sb("wsb", [128, 8, N_IN], BF16)
            wst = [sc.sb("wst%d" % i, [128, N_IN], F32) for i in range(4)]
        else:
            sc, wsb = pre
            wst = None
        xt = [sc.sb("xt%d" % i, [128, D], F32) for i in range(2)]
        junk = sc.sb("junk", [128, D], BF16)
        xn = [sc.sb("xn%d" % i, [128, D], BF16) for i in range(2)]
        st = [sc.sb("st%d" % i, [128, 4], F32) for i in range(2)]
        HT = [sc.sb("HT%d" % i, [128, 8, 512], BF16) for i in range(2)]
        psT = [sc.ps("psT%d" % i, [128, D], BF16) for i in range(2)]
        psM = [sc.ps("psM%d" % i, [128, 512]) for i in range(6)]
        evf = [sc.sb("evf%d" % i, [128, 512], F32) for i in range(3)]
        evb = [sc.sb("evb%d" % i, [128, 512], BF16) for i in range(3)]
        evt = [sc.sb("evt%d" % i, [128, 640], BF16) for i in range(2)]
        evg = [sc.sb("evg%d" % i, [128, 18], F32) for i in range(2)]
        for kc in (range(8) if pre is None else []):
            s = wst[kc % 4]
            k.dma('sp' if kc % 2 == 0 else 'act', s[:], k.w_in[l, kc * 128:(kc + 1) * 128, :], w=[s])
            k.copy('pool' if kc % 2 == 0 else 'dve', wsb[:, kc, :], s[:], r=[s], w=[wsb])
        cnt = {"ev": 0, "pm": 0}

        def norm_gen(tb):
            ht = HT[tb % 2]
            t0_ = tb * 4
            k.dma('act', xt[t0_ % 2][:], xsrc[t0_ * 128:(t0_ + 1) * 128, :], w=[xt[t0_ % 2]])
            yield
            for sub in range(4):
                ti = tb * 4 + sub
                x_ = xt[ti % 2]; xn_ = xn[ti % 2]; st_ = st[ti % 2]; pt_ = psT[ti % 2]
                k.act(junk[:], x_[:], AF.Square, r=[x_], w=[junk, st_], scale=1.0 / 32.0, accum=st_[:, 0:1])
                k.act(st_[:, 1:2], st_[:, 0:1], AF.Ln, r=[st_], w=[st_], bias=RMS_EPS)
                k.act(st_[:, 2:3], st_[:, 1:2], AF.Exp, r=[st_], w=[st_], scale=-0.5)
                k.ts('dve', xn_[:], x_[:], st_[:, 2:3], None, ALU.mult, None, r=[x_, st_], w=[xn_])
                if sub < 3:
                    k.dma('act', xt[(ti + 1) % 2][:], xsrc[(ti + 1) * 128:(ti + 2) * 128, :], w=[xt[(ti + 1) % 2]])
                yield
                yield
                for kc in range(8):
                    k.transpose(pt_[:, kc * 128:(kc + 1) * 128], xn_[:, kc * 128:(kc + 1) * 128], k.ident_bf[:],
                                r=[xn_, k.ident_bf], w=[pt_])
                    if kc == 3:
                        yield
                yield
                for kc in range(8):
                    o = ht[:, kc, sub * 128:(sub + 1) * 128]
                    i_ = pt_[:, kc * 128:(kc + 1) * 128]
                    if kc % 2 == 0:
                        k.ts('dve', o, i_, k.modAB[:, kc:kc + 1], k.modAB[:, 8 + kc:9 + kc], ALU.mult, ALU.add,
                             r=[pt_, k.modAB], w=[ht])
                    else:
                        k.act(o, i_, AF.Identity, r=[pt_, k.modAB], w=[ht], scale=k.modAB[:, kc:kc + 1],
                              bias=k.modAB[:, 8 + kc:9 + kc])
                yield

        def step(g):
            if g is not None:
                try:
                    next(g)
                except StopIteration:
                    return None
            return g

        for _ in norm_gen(0):
            pass
        for tb in range(NTB):
            ht = HT[tb % 2]
            g = norm_gen(tb + 1) if tb + 1 < NTB else None
            tsl = slice(tb * 512, (tb + 1) * 512)
            fm = [(c * 128, 128, 'PT', c * 128) for c in range(7)]
            fm += [(896 + c * 128, 128, 'QKT', c * 128) for c in range(13)]
            fm += [(2560, 6, 'FL', 0)]
            for (c0, m, dst, r0) in fm:
                ps = psM[cnt["pm"] % 6]; cnt["pm"] += 1
                k.mm(ps[0:m, :], [(wsb[:, kc, c0:c0 + m], ht[:, kc, :]) for kc in range(8)], r=[wsb, ht], w=[ps])
                eng = 'act' if cnt["ev"] % 2 == 0 else 'dve'
                if dst == 'QKT':
                    ev = evb[cnt["ev"] % 3]
                    dd = k.QKT[r0:r0 + m, tsl]
                else:
                    ev = evf[cnt["ev"] % 3]
                    dd = (k.PT if dst == 'PT' else k.FL)[r0:r0 + m, tsl]
                cnt["ev"] += 1
                k.copy(eng, ev[0:m, :], ps[0:m, :], r=[ps], w=[ev])
                k.dma('sp', dd, ev[0:m, :], r=[ev])
                g = step(g)
            for sub in range(4):
                ti = tb * 4 + sub
                tok = slice(ti * 128, (ti + 1) * 128)
                ps0 = psM[cnt["pm"] % 6]; cnt["pm"] += 1
                ps1 = psM[cnt["pm"] % 6]; cnt["pm"] += 1
                lhs = lambda kc: ht[:, kc, sub * 128:(sub + 1) * 128]
                k.mm(ps0[:, 0:384], [(lhs(kc), wsb[:, kc, 2566:2950]) for kc in range(8)], r=[wsb, ht], w=[ps0])
                k.mm(ps1[:, 0:274], [(lhs(kc), wsb[:, kc, 2950:3224]) for kc in range(8)], r=[wsb, ht], w=[ps1])
                et = evt[ti % 2]; eg = evg[ti % 2]
                k.copy('act', et[:, 0:384], ps0[:, 0:384], r=[ps0], w=[et])
                k.copy('dve', et[:, 384:640], ps1[:, 0:256], r=[ps1], w=[et])
                k.copy('dve', eg[:], ps1[:, 256:274], r=[ps1], w=[eg])
                k.dma('sp', k.VT[tok, :], et[:], r=[et])
                k.dma('sp', k.GT[tok, :], eg[:], r=[eg])
                g = step(g)
            while g is not None:
                g = step(g)


def prep_shared(inp):
    f = lambda a: np.ascontiguousarray(np.asarray(a, dtype=np.float32))
    sh = {}
    sh["ada_w"] = f(inp["ada_w"])
    sh["ada_b_fm"] = f(np.asarray(inp["ada_b"]).reshape(DEPTH, 48, 128).transpose(0, 2, 1))
    sh["ada_b_row"] = f(inp["ada_b"])
    sh["normg_fm"] = f(np.asarray(inp["norm_g"]).reshape(DEPTH, 4, 8, 128).transpose(0, 1, 3, 2))
    sh["normg_row"] = f(inp["norm_g"])
    sh["w_in_p"] = f(np.asarray(inp["w_in"])[:, :, w_in_perm_index()])
    sh["ident_bf"] = np.eye(128, dtype=np.float32).astype(NPBF)
    sh["ident_f"] = np.eye(128, dtype=np.float32)
    sh["w_out"] = f(inp["w_out"]); sh["ffn_up"] = f(inp["ffn_up"]); sh["ffn_down"] = f(inp["ffn_down"])
    sh["conv_w_fm"] = f(np.asarray(inp["ffn_conv_w"]).reshape(DEPTH, 3, 44, 128).transpose(0, 3, 1, 2))
    sh["conv_b_fm"] = f(np.asarray(inp["ffn_conv_b"]).reshape(DEPTH, 44, 128).transpose(0, 2, 1))
    sh.update(nsa_host_consts())
    sh["rel_bias"] = f(inp["rel_bias"])
    sh["nsa_pe_kT"] = f(np.asarray(inp["nsa_pe_k"]).transpose(0, 2, 1))
    sh["nsa_pe_vT"] = f(np.asarray(inp["nsa_pe_v"]).transpose(0, 2, 1))
    for n in ("nsa_ck_w1", "nsa_cv_w1", "nsa_ck_w2", "nsa_cv_w2"):
        sh[n] = f(inp[n])
    sh["fox_b_f"] = f(np.asarray(inp["fox_b_f"]).reshape(DEPTH, 6, 1))
    sh.update(rwkv_host(inp))
    return sh


def prep_core(inp, b):
    d = {}
    d["x"] = np.ascontiguousarray(np.asarray(inp["x"][b], dtype=np.float32))
    d["cT"] = np.ascontiguousarray(np.asarray(inp["c"][b], dtype=np.float32).reshape(8, 128).T)
    return d


def setup_fox(k):
    k.fox_bf = k.inp("fox_b_f", [DEPTH, 6, 1])
    k.CUMA = k.scratch("CUMA", [6, 3, S_LEN], BF16)


def stage_fox(k, l):
    nc = k.nc
    with Scope(k) as sc:
        nb = sc.sb("nb", [128, 32, 6], F32)
        with Scope(k) as s2:
            fl = s2.sb("fl", [6, S_LEN], F32)
            t1 = s2.sb("t1", [6, S_LEN], F32)
            ones = s2.sb("ones", [6, S_LEN], F32)
            cum = s2.sb("cum", [6, S_LEN], F32)
            parts = s2.sb("parts", [6, 3, S_LEN], BF16)
            bfv = s2.sb("bfv", [6, 2], F32)
            psn = s2.ps("psn", [128, 512])
            k.dma('sp', fl[:], k.FL, w=[fl])
            k.dma('sp', bfv[:, 0:1], k.fox_bf[l], w=[bfv])
            k.ts('dve', bfv[:, 1:2], bfv[:, 0:1], -1.0, None, ALU.mult, None, r=[bfv], w=[bfv])
            k.S.op('pool', lambda: nc.gpsimd.memset(ones[:], 1.0), [], [ones])
            k.act(t1[:], fl[:], AF.Exp, r=[fl, bfv], w=[t1], bias=bfv[:, 1:2], scale=-1.0)
            k.act(t1[:], t1[:], AF.Ln, r=[t1], w=[t1], bias=1.0, scale=1.0)
            k.ts('dve', t1[:], t1[:], -1.0, None, ALU.mult, None, r=[t1], w=[t1])
            k.S.op('dve', lambda: nc.vector.tensor_tensor_scan(out=cum[:], data0=ones[:], data1=t1[:], initial=0.0,
                                                               op0=ALU.mult, op1=ALU.add), [ones, t1], [cum])
            for t in range(32):
                k.transpose(psn[:, t * 6:(t + 1) * 6], cum[:, t * 128:(t + 1) * 128], k.ident_f[0:6, 0:6],
                            r=[cum, k.ident_f], w=[psn])
            k.ts('dve', nb[:].rearrange("p t h -> p (t h)"), psn[:, 0:192], -1.0, None, ALU.mult, None, r=[psn], w=[nb])
            k.ts('dve', t1[:], cum[:], 8.0, None, ALU.mult, None, r=[cum], w=[t1])
            k.copy('dve', parts[:, 0, :], t1[:], r=[t1], w=[parts])
            k.tt('dve', t1[:], t1[:], parts[:, 0, :], ALU.subtract, r=[t1, parts], w=[t1])
            k.copy('dve', parts[:, 1, :], t1[:], r=[t1], w=[parts])
            k.tt('dve', t1[:], t1[:], parts[:, 1, :], ALU.subtract, r=[t1, parts], w=[t1])
            k.copy('dve', parts[:, 2, :], t1[:], r=[t1], w=[parts])
            k.dma('sp', k.CUMA, parts[:], r=[parts], w=["CUMA"])
        QA = [sc.sb("QA%d" % i, [128, S_LEN], BF16) for i in range(2)]
        KA = [sc.sb("KA%d" % i, [128, S_LEN], BF16) for i in range(2)]
        VA = sc.sb("VA", [128, 32, 6, 65], BF16)
        yb = sc.sb("yb", [128, 32, 384], BF16)
        PTl = [sc.sb("PTl%d" % i, [128, 512], BF16) for i in range(8)]
        rc = [sc.sb("rc%d" % i, [128, 4], F32) for i in range(2)]
        psS = [sc.ps("psS%d" % i, [128, 512]) for i in range(5)]
        psO = [sc.ps("psO%d" % i, [128, 512]) for i in range(2)]
        k.dma('sp', yb[:], k.VT[:, 0:384].rearrange("(t p) c -> p t c", p=128), w=[yb])
        k.S.op('pool', lambda: nc.gpsimd.memset(VA[:, :, :, 64:65], 1.0), [], [VA])
        k.copy('dve', VA[:, :, :, 0:64], yb[:].rearrange("p t (h d) -> p t h d", h=6), r=[yb], w=[VA])
        for i in range(2):
            k.S.op('dve', lambda i=i: nc.vector.memset(KA[i][64:67, :], 1.0), [], [KA[i]])
        nS = 0
        nO = 0
        nP = 0
        pipe = Pipe(4)
        for h in range(6):
            qa = QA[h % 2]; ka = KA[h % 2]
            k.dma('sp', qa[0:64, :], k.QKT[h * 64:(h + 1) * 64, :], w=[qa])
            k.dma('sp', qa[64:67, :], k.CUMA[h], w=[qa])
            k.dma('sp', ka[0:64, :], k.QKT[384 + h * 64:384 + (h + 1) * 64, :], w=[ka])
            for qb in range(NTB):
                po = psO[nO % 2]; nO += 1
                nkt = 4 * qb + 4
                for kt in range(nkt):
                    j = kt - 4 * qb
                    c0 = max(j, 0) * 128
                    ps = psS[nS % len(psS)]; nS += 1
                    pt = PTl[nP % len(PTl)]; nP += 1

                    def first(ps=ps, pt=pt, kt=kt, c0=c0, j=j, qa=qa, ka=ka, qb=qb, h=h):
                        k.mm(ps[:, c0:512], [(ka[0:67, kt * 128:(kt + 1) * 128], qa[0:67, qb * 512 + c0:(qb + 1) * 512])],
                             r=[ka, qa], w=[ps])
                        k.act(pt[:, c0:512], ps[:, c0:512], AF.Exp, r=[ps, nb], w=[pt], bias=nb[:, kt, h:h + 1], scale=0.125)
                        if j >= 0:
                            k.S.op('pool', lambda: nc.gpsimd.affine_select(
                                out=pt[:, c0:c0 + 128], in_=pt[:, c0:c0 + 128], pattern=[[1, 128]], compare_op=ALU.is_ge,
                                fill=0.0, base=0, channel_multiplier=-1), [pt], [pt])

                    def second(pt=pt, kt=kt, j=j, po=po, qb=qb, h=h, last=(kt == nkt - 1)):
                        fns = []
                        for qs in range(max(j, 0), 4):
                            fns.append(lambda qs=qs: nc.tensor.matmul(
                                po[:, qs * 65:(qs + 1) * 65], lhsT=pt[:, qs * 128:(qs + 1) * 128], rhs=VA[:, kt, h, :],
                                start=(kt == 0 and qs == 0), stop=(kt == 4 * qb + qs), skip_group_check=True))
                        k.S.pe_group(fns, [pt, VA], [po])
                        if last:
                            r_ = rc[qb % 2]
                            pov = po[:, 0:260].rearrange("p (q c) -> p q c", c=65)
                            k.S.op('dve', lambda: nc.vector.reciprocal(out=r_[:], in_=pov[:, :, 64]), [po], [r_])
                            for qs in range(4):
                                k.ts('dve', yb[:, qb * 4 + qs, h * 64:(h + 1) * 64], po[:, qs * 65:qs * 65 + 64], r_[:, qs:qs + 1], None,
                                     ALU.mult, None, r=[po, r_], w=[yb])
                    pipe.push(first, second)
        pipe.flush()
        k.dma('sp', k.Y[:, 256:640].rearrange("(t p) c -> p t c", p=128), yb[:], r=[yb], w=["Y"])


RW_BPRIO = True
LW = 1536
LC = 4608
NEG8 = -240000.0


def t5_bucket_np(n):
    n = np.maximum(n, 0)
    nf = np.maximum(n, 1).astype(np.float32)
    large = 16 + (np.log(nf / np.float32(16)) / np.float32(np.log(128 / 16)) * np.float32(16)).astype(np.int32)
    large = np.minimum(large, 31)
    return np.where(n < 16, n, large)


def nsa_host_consts():
    c = {}
    i = np.arange(LW); n = i - 511
    oh = np.zeros((33, LW), np.float32)
    ok = (n >= 0) & (n < 512)
    oh[t5_bucket_np(n)[ok], i[ok]] = 1.0
    oh[32, ~ok] = NEG8
    c["oh_w"] = oh
    i = np.arange(LC); n = i - 2063
    oh = np.zeros((33, LC), np.float32)
    ok = n >= 0
    oh[t5_bucket_np(n)[ok], i[ok]] = 1.0
    oh[32, ~ok] = NEG8
    c["oh_c"] = oh
    s_ = np.arange(S_LEN)
    c["E_all"] = (np.arange(64)[:, None] == (s_[None, :] // 64)).astype(np.float32).astype(NPBF)
    cs = np.arange(256) * 16
    ce = cs + 31
    ss = np.arange(64) * 64
    ov = ((cs[:, None] <= ss[None, :] + 63) & (ce[:, None] >= ss[None, :])).astype(np.float32)
    ov[255] = 0.0
    c["ovl"] = np.ascontiguousarray(ov.reshape(2, 128, 64).transpose(1, 0, 2)).astype(NPBF)
    t = np.arange(S_LEN)
    cur = t // 64
    jb = np.arange(64)
    back = cur[:, None] - jb[None, :]
    valid = back >= 0
    forced = (jb[None, :] == 0) | (valid & (back < 2))
    tkm = (valid & ~forced).astype(np.float32)
    tka = np.where(valid, np.where(forced, 1e4, 0.0), -1.0).astype(np.float32)
    c["tkm"] = np.ascontiguousarray(tkm.reshape(32, 128, 64).transpose(1, 0, 2)).astype(NPBF)
    c["tka"] = np.ascontiguousarray(tka.reshape(32, 128, 64).transpose(1, 0, 2)).astype(NPBF)
    return c


def setup_nsa(k):
    nc = k.nc
    k.rel_bias = k.inp("rel_bias", [32, 6])
    k.oh_w = k.inp("oh_w", [33, LW])
    k.oh_c = k.inp("oh_c", [33, LC])
    k.E_d = k.inp("E_all", [64, S_LEN], BF16)
    k.ovl_d = k.inp("ovl", [128, 2, 64], BF16)
    k.tkm_d = k.inp("tkm", [128, 32, 64], BF16)
    k.tka_d = k.inp("tka", [128, 32, 64], BF16)
    k.pe_kT = k.inp("nsa_pe_kT", [DEPTH, 64, 32])
    k.pe_vT = k.inp("nsa_pe_vT", [DEPTH, 64, 32])
    k.ck_w1 = k.inp("nsa_ck_w1", [DEPTH, 2048, 128])
    k.cv_w1 = k.inp("nsa_cv_w1", [DEPTH, 2048, 128])
    k.ck_w2 = k.inp("nsa_ck_w2", [DEPTH, 128, 64])
    k.cv_w2 = k.inp("nsa_cv_w2", [DEPTH, 128, 64])
    k.WVW = k.scratch("WVW", [6, 128, LW], BF16)
    k.WVC = k.scratch("WVC", [6, 128, LC], BF16)
    with Scope(k) as sc:
        rb = sc.sb("rb", [33, 6], F32)
        rb31 = sc.sb("rb31", [32, 6], F32)
        rrep = sc.sb("rrep", [33, 6, 128], F32)
        ohw = sc.sb("ohw", [33, LW], F32)
        ohc = sc.sb("ohc", [33, LC], F32)
        ps = [sc.ps("psb%d" % i, [128, 512]) for i in range(2)]
        ev = [sc.sb("evb%d" % i, [128, 512], BF16) for i in range(2)]
        k.dma('sp', rb[0:32, :], k.rel_bias, w=[rb])
        k.dma('sp', rb31[:], k.rel_bias[31:32, :].broadcast_to([32, 6]), w=[rb31])
        k.dma('sp', ohw[:], k.oh_w, w=[ohw])
        k.dma('sp', ohc[:], k.oh_c, w=[ohc])
        k.S.op('dve', lambda: nc.vector.memset(rb[32:33, :], 1.0), [], [rb])
        k.tt('dve', rb[0:32, :], rb[0:32, :], rb31[:], ALU.subtract, r=[rb, rb31], w=[rb])
        k.ts('dve', rb[0:32, :], rb[0:32, :], 8.0, None, ALU.mult, None, r=[rb], w=[rb])
        for h in range(6):
            k.copy('dve', rrep[:, h, :], rb[:, h:h + 1].to_broadcast([33, 128]), r=[rb], w=[rrep])
        n = 0
        for h in range(6):
            for (oh, L, dst) in ((ohw, LW, k.WVW), (ohc, LC, k.WVC)):
                for c0 in range(0, L, 512):
                    p_ = ps[n % 2]; e_ = ev[n % 2]; n += 1
                    k.mm(p_[:], [(rrep[:, h, :], oh[:, c0:c0 + 512])], r=[rrep, oh], w=[p_])
                    k.copy('act' if n % 2 else 'dve', e_[:], p_[:], r=[p_], w=[e_])
                    k.dma('sp', dst[h, :, c0:c0 + 512], e_[:], r=[e_])


class DbgStop(Exception):
    pass


def dbg(k, lvl):
    if getattr(k, 'dbg_stop', None) == lvl:
        raise DbgStop()


def stage_nsa(k, l):
    nc = k.nc
    with Scope(k) as sc:
        Gw = sc.sb("Gw", [128, 6, 1408], BF16)
        Gc = sc.sb("Gc", [128, 6, 2560], BF16)
        tkm = sc.sb("tkm", [128, 32, 64], BF16)
        tka = sc.sb("tka", [128, 32, 64], BF16)
        QN = [sc.sb("QN%d" % h, [128, S_LEN], BF16) for h in range(6)]
        KE = [sc.sb("KE%d" % g, [128, S_LEN], BF16) for g in range(2)]
        KW = sc.sb("KW", [128, S_LEN], BF16)
        VS = sc.sb("VS", [128, 32, 2, 65], BF16)
        VW = sc.sb("VW", [128, 32, 2, 65], BF16)
        KCMP = sc.sb("KCMP", [128, 256], BF16)
        VE = sc.sb("VE", [128, 2, 2, 129], BF16)
        sg = sc.sb("sg", [128, 32, 18], F32)
        for h in range(6):
            k.dma('sp', Gw[:, h, :], bass.AP(k.WVW.tensor, h * 128 * LW + 127, [[LW - 1, 128], [1, 1408]]), w=[Gw])
            k.dma('sp', Gc[:, h, :], bass.AP(k.WVC.tensor, h * 128 * LC + 2032, [[LC - 16, 128], [1, 2560]]), w=[Gc])
        k.dma('sp', tkm[:], k.tkm_d, w=[tkm])
        k.dma('sp', tka[:], k.tka_d, w=[tka])
        for h in range(6):
            g_, hp_ = h // 3, h % 3
            k.dma('sp', QN[h][g_ * 64:(g_ + 1) * 64, :], k.QKT[768 + hp_ * 128 + g_ * 64:768 + hp_ * 128 + (g_ + 1) * 64, :], w=[QN[h]])
            k.S.op('pool', lambda h=h, g_=g_: nc.gpsimd.memset(QN[h][(1 - g_) * 64:(2 - g_) * 64, :], 0.0), [], [QN[h]])
        for g_ in range(2):
            k.dma('sp', KE[g_][g_ * 64:(g_ + 1) * 64, :], k.QKT[1408 + g_ * 64:1408 + (g_ + 1) * 64, :], w=[KE[g_]])
            k.dma('sp', KE[g_][(1 - g_) * 64:(2 - g_) * 64, :], k.E_d, w=[KE[g_]])
        k.dma('sp', KW[:], k.QKT[1536:1664, :], w=[KW])
        k.dma('sp', sg[:], k.GT.rearrange("(t p) c -> p t c", p=128), w=[sg])
        k.act(sg[:], sg[:], AF.Exp, r=[sg], w=[sg], scale=-1.0)
        k.ts('dve', sg[:], sg[:], 1.0, None, ALU.add, None, r=[sg], w=[sg])
        k.S.op('dve', lambda: nc.vector.reciprocal(out=sg[:], in_=sg[:]), [sg], [sg])
        k.dma('sp', VE[:, 0, :, 65:129], k.ovl_d, w=[VE])
        k.dma('sp', VE[:, 1, :, 65:129], k.ovl_d, w=[VE])
        k.S.op('pool', lambda: nc.gpsimd.memset(VE[:, :, :, 64:65], 1.0), [], [VE])
        k.S.op('pool', lambda: nc.gpsimd.memset(VE[:, :, :, 0:64], 0.0), [], [VE])
        k.S.op('pool', lambda: nc.gpsimd.memset(KCMP[:], 0.0), [], [KCMP])
        dbg(k, 1)
        with Scope(k) as s2:
            vst = s2.sb("vst", [128, 32, 256], BF16)
            k.dma('sp', vst[:], k.VT[:, 384:640].rearrange("(t p) c -> p t c", p=128), w=[vst])
            k.S.op('pool', lambda: nc.gpsimd.memset(VS[:, :, :, 64:65], 1.0), [], [VS])
            k.S.op('pool', lambda: nc.gpsimd.memset(VW[:, :, :, 64:65], 1.0), [], [VW])
            k.copy('dve', VS[:, :, :, 0:64], vst[:, :, 0:128].rearrange("p t (g d) -> p t g d", g=2), r=[vst], w=[VS])
            k.copy('pool', VW[:, :, :, 0:64], vst[:, :, 128:256].rearrange("p t (g d) -> p t g d", g=2), r=[vst], w=[VW])
        dbg(k, 2)
        with Scope(k) as s2:
            KC = s2.sb("KC", [128, S_LEN], BF16)
            VC = s2.sb("VC", [128, S_LEN], BF16)
            k.dma('sp', KC[:], k.QKT[1152:1280, :], w=[KC])
            k.dma('sp', VC[:], k.QKT[1280:1408, :], w=[VC])
            w1s = s2.sb("w1s", [128, 16, 128], F32)
            w1b = [s2.sb("w1b%d" % i, [128, 32, 128], BF16) for i in range(2)]
            w2s = s2.sb("w2s", [128, 2, 64], F32)
            w2b = s2.sb("w2b", [128, 2, 64], BF16)
            pes = s2.sb("pes", [128, 2, 32], F32)
            peb = s2.sb("peb", [128, 2, 32], BF16)
            hb = s2.sb("hb", [128, 2], F32)
            gx = s2.sb("gx", [128, 256], F32)
            gu = s2.sb("gu", [128, 256], F32)
            gg = s2.sb("gg", [128, 256], BF16)
            psh = s2.ps("psh", [128, 512])
            psb_ = s2.ps("pshb", [128, 512])
            pso = s2.ps("pso", [128, 512])
            for kv, (w1d, w2d, ped) in enumerate(((k.ck_w1, k.ck_w2, k.pe_kT), (k.cv_w1, k.cv_w2, k.pe_vT))):
                for lh in range(2):
                    for half in range(2):
                        k.dma('sp', w1s[half * 64:(half + 1) * 64, :, :],
                              w1d[l, lh * 1024:(lh + 1) * 1024, :].rearrange("(l d) h -> d l h", d=64), w=[w1s])
                    k.copy('dve' if lh == 0 else 'act', w1b[kv][:, lh * 16:(lh + 1) * 16, :], w1s[:], r=[w1s], w=[w1b[kv]])
                for half in range(2):
                    k.dma('sp', pes[half * 64:(half + 1) * 64, kv, :], ped[l], w=[pes])
                k.dma('sp', w2s[:, kv, :], w2d[l], w=[w2s])
            k.copy('dve', w2b[:], w2s[:], r=[w2s], w=[w2b])
            w2kd = s2.sb("w2kd", [128, 2, 64], BF16)
            for a_ in range(2):
                k.copy('dve', w2kd[:, a_, :], w2s[:, 0, :], r=[w2s], w=[w2kd])
            k.copy('dve', peb[:], pes[:], r=[pes], w=[peb])
            for kv in range(2):
                src = KC if kv == 0 else VC
                k.mm(psb_[:, kv:kv + 1], [(w1b[kv][0:64, li, :], peb[0:64, kv, li:li + 1]) for li in range(32)],
                     r=[w1b[kv], peb], w=[psb_], start=True)
                k.copy('dve', hb[:, kv:kv + 1], psb_[:, kv:kv + 1], r=[psb_], w=[hb])
                for g in range(2):
                    pr = slice(g * 64, (g + 1) * 64)
                    k.mm(psh[:, 0:255], [(w1b[kv][pr, li, :], src[pr, li:li + 16 * 254 + 1:16]) for li in range(32)],
                         r=[w1b[kv], src], w=[psh])
                    k.ts('dve', gx[:, 0:255], psh[:, 0:255], hb[:, kv:kv + 1], None, ALU.add, None, r=[psh, hb], w=[gx])
                    k.tt('dve', gu[:, 0:255], gx[:, 0:255], gx[:, 0:255], ALU.mult, r=[gx], w=[gu])
                    k.ts('dve', gu[:, 0:255], gu[:, 0:255], 0.044715, 1.0, ALU.mult, ALU.add, r=[gu], w=[gu])
                    k.tt('dve', gu[:, 0:255], gu[:, 0:255], gx[:, 0:255], ALU.mult, r=[gu, gx], w=[gu])
                    k.act(gu[:, 0:255], gu[:, 0:255], AF.Exp, r=[gu], w=[gu], scale=-2.0 * 0.7978845608028654)
                    k.ts('dve', gu[:, 0:255], gu[:, 0:255], 1.0, None, ALU.add, None, r=[gu], w=[gu])
                    k.S.op('dve', lambda: nc.vector.reciprocal(out=gu[:, 0:255], in_=gu[:, 0:255]), [gu], [gu])
                    k.S.op('dve', lambda: nc.vector.memset(gg[:, 255:256], 0.0), [], [gg])
                    k.tt('dve', gg[:, 0:255], gu[:, 0:255], gx[:, 0:255], ALU.mult, r=[gu, gx], w=[gg])
                    if kv == 0:
                        k.mm(pso[:, 0:256], [(w2kd[:].rearrange("p a d -> p (a d)"), gg[:, 0:256])], r=[w2kd, gg], w=[pso])
                        k.copy('dve', KCMP[pr, :], pso[pr, 0:256], r=[pso], w=[KCMP])
                    else:
                        for ct in range(2):
                            k.mm(pso[:, ct * 64:(ct + 1) * 64], [(gg[:, ct * 128:(ct + 1) * 128], w2b[:, 1, :])],
                                 r=[w2b, gg], w=[pso], start=(ct == 0))
                        k.copy('dve', VE[:, g, :, 0:64], pso[:, 0:128].rearrange("p (c d) -> p c d", c=2), r=[pso], w=[VE])
        dbg(k, 3)
        PTl = [sc.sb("PTn%d" % i, [128, 512], BF16) for i in range(6)]
        yacc = [sc.sb("yacc%d" % i, [128, 4, 384], F32) for i in range(2)]
        ybf = [sc.sb("ybf%d" % i, [128, 4, 384], BF16) for i in range(2)]
        impt2 = [[sc.sb("impt%d_%d" % (i, g), [128, 4, 64], F32) for g in range(2)] for i in range(2)]
        scr = sc.sb("scr", [128, 4, 64], F32)
        wk = sc.sb("wk", [128, 4, 64], F32)
        m8 = sc.sb("m8", [128, 4, 16], F32)
        nmq = sc.sb("nmq", [128, 4, 128], BF16)
        rcs = [sc.sb("rcs%d" % i, [128, 8], F32) for i in range(3)]
        psS = [sc.ps("psS%d" % i, [128, 512]) for i in range(4)]
        psO = [sc.ps("psO%d" % i, [128, 512]) for i in range(3)]
        psT = sc.ps("psTn", [128, 1024], BF16)
        st = {"S": 0, "O": 0, "P": 0, "R": 0}

        def q_ap(h, c0, c1):
            g, hp = h // 3, h % 3
            return QN[h][g * 64:(g + 1) * 64, c0:c1]

        def evac(views, h, branch, qb, ya, first):
            r_ = rcs[st["R"] % 3]; st["R"] += 1
            for qs, (po, cb) in enumerate(views):
                if branch == 0:
                    k.ts('dve', r_[:, qs:qs + 1], po[:, cb + 64:cb + 65], 1e-30, None, ALU.max, None, r=[po], w=[r_])
                    k.S.op('dve', lambda r_=r_, qs=qs: nc.vector.reciprocal(out=r_[:, qs:qs + 1], in_=r_[:, qs:qs + 1]), [r_], [r_])
                else:
                    k.S.op('dve', lambda r_=r_, po=po, cb=cb, qs=qs: nc.vector.reciprocal(out=r_[:, qs:qs + 1], in_=po[:, cb + 64:cb + 65]), [po], [r_])
            k.tt('dve', r_[:, 4:8], r_[:, 0:4], sg[:, qb * 4:(qb + 1) * 4, h * 3 + branch], ALU.mult, r=[r_, sg], w=[r_])
            for qs, (po, cb) in enumerate(views):
                o = ya[:, qs, h * 64:(h + 1) * 64]
                if first:
                    k.ts('dve', o, po[:, cb:cb + 64], r_[:, 4 + qs:5 + qs], None, ALU.mult, None, r=[po, r_], w=[ya])
                else:
                    k.stt('dve', o, po[:, cb:cb + 64], r_[:, 4 + qs:5 + qs], o, ALU.mult, ALU.add, r=[po, r_, ya], w=[ya])
            return r_

        pipe = Pipe(3)

        def attend(h, qb, tiles, kmat, vmat, po, g, branch, ya, merged=False):
            hp = h % 3
            nt = len(tiles)
            state = {"first": True}
            for idx, (kt, c0, c1, extra) in enumerate(tiles):
                ps = psS[st["S"] % len(psS)]; st["S"] += 1
                pt = PTl[st["P"] % len(PTl)]; st["P"] += 1

                def first(ps=ps, pt=pt, kt=kt, c0=c0, c1=c1, extra=extra):
                    if merged:
                        fns = [lambda: nc.tensor.matmul(ps[:, c0:c1], lhsT=kmat[:, kt * 128:(kt + 1) * 128],
                                                        rhs=QN[h][:, qb * 512 + c0:qb * 512 + c1],
                                                        start=True, stop=(len(extra) == 0), skip_group_check=True)]
                    else:
                        fns = [lambda: nc.tensor.matmul(ps[:, c0:c1], lhsT=kmat[g * 64:(g + 1) * 64, kt * 128:(kt + 1) * 128],
                                                        rhs=QN[h][g * 64:(g + 1) * 64, qb * 512 + c0:qb * 512 + c1],
                                                        start=True, stop=(len(extra) == 0), skip_group_check=True)]
                    rd = [kmat, QN[h]]
                    for ei, (lt, rt, lap, rap) in enumerate(extra):
                        w_ = rap.shape[-1]
                        fns.append(lambda lap=lap, rap=rap, w_=w_, ei=ei: nc.tensor.matmul(
                            ps[:, c0:c0 + w_], lhsT=lap, rhs=rap, start=False, stop=(ei == len(extra) - 1), skip_group_check=True))
                        rd += [lt, rt]
                    k.S.pe_group(fns, rd, [ps])
                    k.act(pt[:, c0:c1], ps[:, c0:c1], AF.Exp, r=[ps], w=[pt], scale=0.125)

                def second(pt=pt, kt=kt, c0=c0, c1=c1, idx=idx):
                    fns = []
                    for qs in range(c0 // 128, (c1 + 127) // 128):
                        last = all(not (t2[1] <= qs * 128 < t2[2]) for t2 in tiles[idx + 1:])
                        fo = state["first"]
                        state["first"] = False
                        fns.append(lambda qs=qs, fo=fo, last=last: nc.tensor.matmul(
                            po[:, qs * 65:(qs + 1) * 65], lhsT=pt[:, qs * 128:(qs + 1) * 128], rhs=vmat[:, kt, g, :],
                            start=fo, stop=last, skip_group_check=True))
                    k.S.pe_group(fns, [pt, vmat], [po])
                    if idx == nt - 1:
                        evac([(po, qs * 65) for qs in range(4)], h, branch, qb, ya, False)
                pipe.push(first, second)

        def do_cmp(qb):
            ya = yacc[qb % 2]
            impt = impt2[qb % 2]
            for h in range(6):
                g = h // 3
                poA = psO[st["O"] % 3]; st["O"] += 1
                poB = psO[st["O"] % 3]; st["O"] += 1
                cts = [0] + ([1] if qb >= 4 else [])
                state = {"A": True, "B": True}
                for ct in cts:
                    delta = 512 * qb - 2048 * ct
                    ps = psS[st["S"] % len(psS)]; st["S"] += 1
                    pt = PTl[st["P"] % len(PTl)]; st["P"] += 1

                    def first(ps=ps, pt=pt, ct=ct, delta=delta, g=g, h=h):
                        pairs = [(KCMP[g * 64:(g + 1) * 64, ct * 128:(ct + 1) * 128], q_ap(h, qb * 512, (qb + 1) * 512))]
                        rd = [KCMP, QN[h]]
                        if delta < 2560:
                            pairs.append((k.ident_bf[:], Gc[:, h, delta:delta + 512])); rd += [k.ident_bf, Gc]
                        k.mm(ps[:], pairs, r=rd, w=[ps])
                        k.act(pt[:], ps[:], AF.Exp, r=[ps], w=[pt], scale=0.125)

                    def second(pt=pt, ct=ct, g=g, h=h, poA=poA, poB=poB, state=state, lastct=(ct == cts[-1])):
                        fns = []
                        for qs in range(4):
                            po, cb = (poA, qs * 129) if qs < 3 else (poB, 0)
                            key = "A" if qs < 3 else "B"
                            stt_ = state[key]
                            state[key] = False
                            fns.append(lambda qs=qs, po=po, cb=cb, stt_=stt_: nc.tensor.matmul(
                                po[:, cb:cb + 129], lhsT=pt[:, qs * 128:(qs + 1) * 128], rhs=VE[:, g, ct, :],
                                start=stt_, stop=lastct, skip_group_check=True))
                        k.S.pe_group(fns, [pt, VE], [poA, poB])
                        if lastct:
                            views = [(poA, 0), (poA, 129), (poA, 258), (poB, 0)]
                            r_ = evac(views, h, 0, qb, ya, True)
                            for qs, (po, cb) in enumerate(views):
                                o = impt[g][:, qs, :]
                                if h % 3 == 0:
                                    k.ts('dve', o, po[:, cb + 65:cb + 129], r_[:, qs:qs + 1], None, ALU.mult, None, r=[po, r_], w=[impt[g]])
                                else:
                                    k.stt('dve', o, po[:, cb + 65:cb + 129], r_[:, qs:qs + 1], o, ALU.mult, ALU.add, r=[po, r_, impt[g]], w=[impt[g]])
                    pipe.push(first, second)

        def do_topk(qb):
            impt = impt2[qb % 2]
            for g in range(2):
                k.tt('dve', scr[:], impt[g][:], tkm[:, qb * 4:(qb + 1) * 4, :], ALU.mult, r=[impt[g], tkm], w=[scr])
                k.tt('dve', scr[:], scr[:], tka[:, qb * 4:(qb + 1) * 4, :], ALU.add, r=[scr, tka], w=[scr])
                for qs in range(4):
                    k.S.op('dve', lambda qs=qs: nc.vector.max(out=m8[:, qs, 0:8], in_=scr[:, qs, :]), [scr], [m8])
                    k.S.op('dve', lambda qs=qs: nc.vector.match_replace(out=wk[:, qs, :], in_to_replace=m8[:, qs, 0:8],
                                                                        in_values=scr[:, qs, :], imm_value=-1e9), [scr, m8], [wk])
                    k.S.op('dve', lambda qs=qs: nc.vector.max(out=m8[:, qs, 8:16], in_=wk[:, qs, :]), [wk], [m8])
                    k.ts('dve', wk[:, qs, :], scr[:, qs, :], m8[:, qs, 15:16], 1.0, ALU.is_ge, ALU.subtract, r=[scr, m8, wk], w=[wk])
                k.ts('dve', nmq[:, :, 0:64], wk[:], -NEG8, None, ALU.mult, None, r=[wk], w=[nmq])
                k.ts('pool', nmq[:, :, 64:128], wk[:], -NEG8, None, ALU.mult, None, r=[wk], w=[nmq])
                for qs in range(4):
                    k.transpose(psT[:, qs * 128:(qs + 1) * 128], nmq[:, qs, :], k.ident_bf[:], r=[nmq, k.ident_bf], w=[psT])
                oh = (1 - g) * 64
                for hh in range(3 * g, 3 * g + 3):
                    k.copy('act' if hh % 2 else 'dve', QN[hh][oh:oh + 64, qb * 512:(qb + 1) * 512], psT[oh:oh + 64, 0:512], r=[psT], w=[QN[hh]])

        def do_win(qb):
            ya = yacc[qb % 2]
            for h in range(6):
                g = h // 3
                po = psO[st["O"] % 3]; st["O"] += 1
                tiles = []
                for kt in range(max(0, 4 * qb - 4), 4 * qb + 4):
                    delta = 512 * qb - 128 * kt
                    c0 = max(-delta, 0)
                    c1 = min(512, 640 - delta) if delta > 0 else 512
                    tiles.append((kt, c0, c1, [(k.ident_bf, Gw, k.ident_bf[:], Gw[:, h, delta + 384 + c0:delta + 384 + c1])]))
                attend(h, qb, tiles, KW, VW, po, g, 2, ya)

        def do_slc(qb):
            ya = yacc[qb % 2]
            for h in range(6):
                g = h // 3
                po = psO[st["O"] % 3]; st["O"] += 1
                tiles = []
                for kt in range(0, 4 * qb + 4):
                    delta = 512 * qb - 128 * kt
                    c0 = max(-delta, 0)
                    ex = []
                    if delta <= 128:
                        c1b = 256 if delta == 128 else 512
                        ex.append((k.ident_bf, Gw, k.ident_bf[:], Gw[:, h, delta + 384 + c0:delta + 384 + c1b]))
                    tiles.append((kt, c0, 512, ex))
                attend(h, qb, tiles, KE[g], VS, po, g, 1, ya, merged=True)

        qbs = list(getattr(k, 'dbg_qbs', range(NTB)))
        do_cmp(qbs[0])
        pipe.flush()
        do_topk(qbs[0])
        for i, qb in enumerate(qbs):
            do_win(qb)
            if i + 1 < len(qbs):
                do_cmp(qbs[i + 1])
                pipe.flush()
                do_topk(qbs[i + 1])
            do_slc(qb)
            pipe.flush()
            ya = yacc[qb % 2]
            yb_ = ybf[qb % 2]
            k.copy('pool', yb_[:], ya[:], r=[ya], w=[yb_])
            k.dma('sp', k.Y[qb * 512:(qb + 1) * 512, 640:1024].rearrange("(q p) c -> p q c", p=128), yb_[:], r=[yb_])


def setup_ffn(k):
    k.w_out = k.inp("w_out", [DEPTH, D, D])
    k.ffn_up = k.inp("ffn_up", [DEPTH, D, 2 * D_FF])
    k.ffn_down = k.inp("ffn_down", [DEPTH, D_FF, D])
    k.conv_w = k.inp("conv_w_fm", [DEPTH, 128, 3, 44])
    k.conv_b = k.inp("conv_b_fm", [DEPTH, 128, 44])


def load_cast_gen(k, stg, dst, src_rows, ncols, nchunks, col_split=1):
    w = ncols // col_split
    n = 0
    for c in range(nchunks):
        for cs in range(col_split):
            s = stg[n % len(stg)]
            k.dma('sp' if n % 2 == 0 else 'act', s[:, 0:w], src_rows(c)[:, cs * w:(cs + 1) * w], w=[s])
            k.copy('pool' if n % 2 == 0 else 'dve', dst[:, c, cs * w:(cs + 1) * w], s[:, 0:w], r=[s], w=[dst])
            n += 1
            yield


def load_cast(k, sc, dst, src_rows, ncols, nchunks, name, col_split=1):
    w = ncols // col_split
    stg = [sc.sb("%s_stg%d" % (name, i), [128, w], F32) for i in range(4)]
    for _ in load_cast_gen(k, stg, dst, src_rows, ncols, nchunks, col_split):
        pass


def rms_scale(k, ss, st):
    k.act(st[:, 0:1], ss, AF.Ln, r=[st], w=[st], bias=RMS_EPS)
    k.act(st[:, 1:2], st[:, 0:1], AF.Exp, r=[st], w=[st], scale=-0.5)


def stage_out(k, l, xsrc, xdst, bg=None, bg_steps=2):
    nc = k.nc
    with Scope(k) as sc:
        wo = sc.sb("wo", [128, 8, D], BF16)
        with Scope(k) as s2:
            load_cast(k, s2, wo, lambda c: k.w_out[l, c * 128:(c + 1) * 128, :], D, 8, "wo")
        yt = [sc.sb("yt%d" % i, [128, D], BF16) for i in range(2)]
        yT = [sc.sb("yT%d" % i, [128, 8, 128], BF16) for i in range(2)]
        xt = [sc.sb("xo%d" % i, [128, D], F32) for i in range(2)]
        tt_ = [sc.sb("to%d" % i, [128, D], F32) for i in range(2)]
        junk = sc.sb("junko", [128, 512], BF16)
        st = [sc.sb("sto%d" % i, [128, 4], F32) for i in range(2)]
        psT = [sc.ps("psTo%d" % i, [128, D], BF16) for i in range(2)]
        psY = [sc.ps("psYo%d" % i, [128, 512]) for i in range(4)]
        def T(ti):
            tok = slice(ti * 128, (ti + 1) * 128)
            y_ = yt[ti % 2]; yT_ = yT[ti % 2]; x_ = xt[ti % 2]; pT = psT[ti % 2]
            k.dma('act', y_[:], k.Y[tok, :], w=[y_])
            k.dma('act', x_[:], xsrc[tok, :], w=[x_])
            for kc in range(8):
                k.transpose(pT[:, kc * 128:(kc + 1) * 128], y_[:, kc * 128:(kc + 1) * 128], k.ident_bf[:], r=[y_, k.ident_bf], w=[pT])
            k.copy('act' if ti % 2 else 'dve', yT_[:].rearrange("p a b -> p (a b)"), pT[:], r=[pT], w=[yT_])

        def M(ti):
            tok = slice(ti * 128, (ti + 1) * 128)
            yT_ = yT[ti % 2]; x_ = xt[ti % 2]; t_ = tt_[ti % 2]; st_ = st[ti % 2]
            p0 = psY[(ti % 2) * 2]; p1 = psY[(ti % 2) * 2 + 1]
            for half, ps in enumerate((p0, p1)):
                k.mm(ps[:], [(yT_[:, kc, :], wo[:, kc, half * 512:(half + 1) * 512]) for kc in range(8)], r=[yT_, wo], w=[ps])
                k.act(junk[:], ps[:], AF.Square, r=[ps], w=[junk, st_], scale=1.0 / 32.0, accum=st_[:, 2 + half:3 + half])
            k.tt('dve', st_[:, 2:3], st_[:, 2:3], st_[:, 3:4], ALU.add, r=[st_], w=[st_])
            rms_scale(k, st_[:, 2:3], st_)
            for half, ps in enumerate((p0, p1)):
                cs = slice(half * 512, (half + 1) * 512)
                k.stt('dve', t_[:, cs], ps[:], st_[:, 1:2], k.gm_row[:, cs], ALU.mult, ALU.mult, r=[ps, st_, k.gm_row], w=[t_])
            k.tt('dve', t_[:], t_[:], x_[:], ALU.add, r=[t_, x_], w=[t_])
            k.dma('sp', xdst[tok, :], t_[:], r=[t_])

        T(0)
        for ti in range(32):
            if ti + 1 < 32:
                T(ti + 1)
            M(ti)
            for _ in range(bg_steps):
                if bg is not None:
                    try:
                        next(bg)
                    except StopIteration:
                        bg = None
        if bg is not None:
            for _ in bg:
                pass


def stage_out_ffn(k, l, xin, xmid, xdst):
    NCH = 22
    with Scope(k) as sc:
        wu = sc.sb("wu", [128, 8, 2 * D_FF], BF16)
        wd = sc.sb("wd", [128, NCH, D], BF16)
        with Scope(k) as s2:
            stg = [s2.sb("wstg%d" % i, [128, 1408], F32) for i in range(4)]

            def bg():
                yield from load_cast_gen(k, stg, wu, lambda c: k.ffn_up[l, c * 128:(c + 1) * 128, :], 2 * D_FF, 8, col_split=4)
                yield from load_cast_gen(k, stg, wd, lambda c: k.ffn_down[l, c * 128:(c + 1) * 128, :], D, NCH)
            stage_out(k, l, xin, xmid, bg=bg(), bg_steps=2)
        stage_ffn(k, l, xmid, xdst, pre=(sc, wu, wd))


def stage_ffn(k, l, xsrc, xdst, pre=None):
    nc = k.nc
    NCH = 22
    with ExitStack() as es_:
        if pre is None:
            sc = es_.enter_context(Scope(k))
            wu = sc.sb("wu", [128, 8, 2 * D_FF], BF16)
            wd = sc.sb("wd", [128, NCH, D], BF16)
            with Scope(k) as s2:
                load_cast(k, s2, wu, lambda c: k.ffn_up[l, c * 128:(c + 1) * 128, :], 2 * D_FF, 8, "wu", col_split=2)
                load_cast(k, s2, wd, lambda c: k.ffn_down[l, c * 128:(c + 1) * 128, :], D, NCH, "wd")
        else:
            sc, wu, wd = pre
        cw = sc.sb("cw", [128, 3, 44], F32)
        cb = sc.sb("cb", [128, 44], F32)
        hal = [sc.sb("hal%d" % i, [128, 44, 2], F32) for i in range(2)]
        k.dma('sp', cw[:], k.conv_w[l], w=[cw])
        k.dma('sp', cb[:], k.conv_b[l], w=[cb])
        k.S.op('pool', lambda: nc.gpsimd.memset(hal[1][:], 0.0), [], [hal[1]])
        actT = sc.sb("actT", [128, NCH, 512], BF16)
        HTs = [sc.sb("H2T%d" % i, [128, 8, 512], BF16) for i in range(2)]
        xt = [sc.sb("xf%d" % i, [128, D], F32) for i in range(2)]
        xn = sc.sb("xnf", [128, D], BF16)
        junk = sc.sb("junkf", [128, 512], BF16)
        st = [sc.sb("stf%d" % i, [128, 4], F32) for i in range(2)]
        Tg = [sc.sb("Tg%d" % i, [128, 512], F32) for i in range(2)]
        Tv = [sc.sb("Tv%d" % i, [128, 512], F32) for i in range(2)]
        psT = sc.ps("psTf", [128, D], BF16)
        psU = [sc.ps("psU%d" % i, [128, 512]) for i in range(4)]
        nxc = {"n": 0}

        def norm_gen(tb):
            HT = HTs[tb % 2]
            t0_ = tb * 4
            k.dma('act', xt[t0_ % 2][:], xsrc[t0_ * 128:(t0_ + 1) * 128, :], w=[xt[t0_ % 2]])
            yield
            for sub in range(4):
                ti = tb * 4 + sub
                x_ = xt[ti % 2]; st_ = st[ti % 2]
                k.act(xn[:], x_[:], AF.Square, r=[x_], w=[xn, st_], scale=1.0 / 32.0, accum=st_[:, 2:3])
                rms_scale(k, st_[:, 2:3], st_)
                k.ts('dve', xn[:], x_[:], st_[:, 1:2], None, ALU.mult, None, r=[x_, st_], w=[xn])
                if sub < 3:
                    k.dma('act', xt[(ti + 1) % 2][:], xsrc[(ti + 1) * 128:(ti + 2) * 128, :], w=[xt[(ti + 1) % 2]])
                yield
                yield
                for kc in range(8):
                    k.transpose(psT[:, kc * 128:(kc + 1) * 128], xn[:, kc * 128:(kc + 1) * 128], k.ident_bf[:], r=[xn, k.ident_bf], w=[psT])
                    if kc == 3:
                        yield
                yield
                for kc in range(8):
                    o = HT[:, kc, sub * 128:(sub + 1) * 128]
                    i_ = psT[:, kc * 128:(kc + 1) * 128]
                    if kc % 2 == 0:
                        k.ts('dve', o, i_, k.modAB[:, 16 + kc:17 + kc], k.modAB[:, 24 + kc:25 + kc], ALU.mult, ALU.add, r=[psT, k.modAB], w=[HT])
                    else:
                        k.act(o, i_, AF.Identity, r=[psT, k.modAB], w=[HT], scale=k.modAB[:, 16 + kc:17 + kc], bias=k.modAB[:, 24 + kc:25 + kc])
                yield

        def step(g):
            if g is not None:
                try:
                    next(g)
                except StopIteration:
                    return None
            return g

        xe = [sc.sb("xe%d" % i, [128, D], F32) for i in range(2)]
        ste = [sc.sb("ste%d" % i, [128, 4], F32) for i in range(2)]
        for _ in norm_gen(0):
            pass
        for tb in range(NTB):
            hin = hal[(tb + 1) % 2]; hout = hal[tb % 2]
            HT = HTs[tb % 2]
            g = norm_gen(tb + 1) if tb + 1 < NTB else None
            for cp in range(NCH):
                tg = Tg[cp % 2]; tv = Tv[cp % 2]
                for which, (T_, c_) in enumerate(((tg, cp), (tv, NCH + cp))):
                    ps = psU[(cp * 2 + which) % 4]
                    k.mm(ps[:], [(wu[:, kc, c_ * 128:(c_ + 1) * 128], HT[:, kc, :]) for kc in range(8)], r=[wu, HT], w=[ps])
                    k.act(T_[:], ps[:], AF.Identity, r=[ps, cw, cb], w=[T_], scale=cw[:, 2, c_:c_ + 1], bias=cb[:, c_:c_ + 1])
                    k.stt('dve', T_[:, 1:512], ps[:, 0:511], cw[:, 1, c_:c_ + 1], T_[:, 1:512], ALU.mult, ALU.add, r=[ps, cw, T_], w=[T_])
                    k.stt('dve', T_[:, 2:512], ps[:, 0:510], cw[:, 0, c_:c_ + 1], T_[:, 2:512], ALU.mult, ALU.add, r=[ps, cw, T_], w=[T_])
                    k.copy('act', hout[:, c_, :], ps[:, 510:512], r=[ps], w=[hout])
                    k.stt('dve', T_[:, 0:1], hin[:, c_, 1:2], cw[:, 1, c_:c_ + 1], T_[:, 0:1], ALU.mult, ALU.add, r=[hin, cw, T_], w=[T_])
                    k.stt('dve', T_[:, 0:2], hin[:, c_, 0:2], cw[:, 0, c_:c_ + 1], T_[:, 0:2], ALU.mult, ALU.add, r=[hin, cw, T_], w=[T_])
                k.act(tg[:], tg[:], AF.Silu, r=[tg], w=[tg])
                k.tt('pool', actT[:, cp, :], tg[:], tv[:], ALU.mult, r=[tg, tv], w=[actT])
                if cp >= 1:
                    g = step(g)
            for sub in range(4):
                ti = tb * 4 + sub
                tok = slice(ti * 128, (ti + 1) * 128)
                x_ = xe[sub % 2]; st_ = ste[sub % 2]
                k.dma('act', x_[:], xsrc[tok, :], w=[x_])
                psF = [psU[(2 * sub) % 4], psU[(2 * sub + 1) % 4]]
                for half in range(2):
                    ps = psF[half]
                    k.mm(ps[:], [(actT[:, cp, sub * 128:(sub + 1) * 128], wd[:, cp, half * 512:(half + 1) * 512]) for cp in range(NCH)],
                         r=[actT, wd], w=[ps])
                    k.act(junk[:, 0:512], ps[:], AF.Square, r=[ps], w=[junk, st_], scale=1.0 / 32.0, accum=st_[:, 2 + half:3 + half])
                k.tt('dve', st_[:, 2:3], st_[:, 2:3], st_[:, 3:4], ALU.add, r=[st_], w=[st_])
                rms_scale(k, st_[:, 2:3], st_)
                t_ = Tg[sub % 2] if False else None
                for half in range(2):
                    cs = slice(half * 512, (half + 1) * 512)
                    T_ = (Tg if half == 0 else Tv)[sub % 2]
                    k.stt('dve', T_[:], psF[half][:], st_[:, 1:2], k.gf_row[:, cs], ALU.mult, ALU.mult, r=[psF[half], st_, k.gf_row], w=[T_])
                    k.tt('pool', x_[:, cs], x_[:, cs], T_[:], ALU.add, r=[x_, T_], w=[x_])
                k.dma('sp', xdst[tok, :], x_[:], r=[x_])
                g = step(g)
            while g is not None:
                g = step(g)


def rwkv_host(inp):
    f = lambda a: np.ascontiguousarray(np.asarray(a, dtype=np.float32))
    mu = np.asarray(inp["rwkv_mu"])
    hd = lambda v: np.asarray(v).reshape(DEPTH, 4, 64).transpose(0, 2, 1)
    pp = np.stack([hd(mu[:, 0:256]), hd(mu[:, 256:512]), hd(mu[:, 512:768]), hd(inp["rwkv_w0"]), hd(inp["rwkv_a0"]),
                   hd(inp["rwkv_k_k"]), hd(inp["rwkv_k_a"]), hd(np.asarray(inp["rwkv_r_k"]).reshape(DEPTH, 256))], axis=2)
    lr = np.zeros((DEPTH, 64, 3), np.float32)
    lr[:, 0:32, 0] = mu[:, 768:800]; lr[:, 0:32, 1] = mu[:, 800:832]; lr[:, :, 2] = mu[:, 832:896]
    i = np.arange(64)
    mk = np.stack([(i[:, None] < i[None, :]), (i[:, None] > i[None, :]), (i[:, None] <= i[None, :]), np.eye(64, dtype=bool)]).astype(np.float32)
    cm = np.ones((64, 512), np.float32); cm[:, ::64] = 0.0
    return {"rwkv_pp": f(pp), "rwkv_lr": f(lr), "rwkv_w_up": f(inp["rwkv_w_up"]), "rwkv_a_up": f(inp["rwkv_a_up"]),
            "rwkv_g_up": f(inp["rwkv_g_up"]), "rwkv_ln": f(np.stack([np.asarray(inp["rwkv_ln_w"]), np.asarray(inp["rwkv_ln_b"])], axis=1)),
            "rwkv_masks": f(mk.transpose(1, 0, 2)), "rwkv_cmask": cm}


def setup_rwkv(k):
    k.rw_pp = k.inp("rwkv_pp", [DEPTH, 64, 8, 4])
    k.rw_lr = k.inp("rwkv_lr", [DEPTH, 64, 3])
    k.rw_wup = k.inp("rwkv_w_up", [DEPTH, 32, 256])
    k.rw_aup = k.inp("rwkv_a_up", [DEPTH, 32, 256])
    k.rw_gup = k.inp("rwkv_g_up", [DEPTH, 64, 256])
    k.rw_ln = k.inp("rwkv_ln", [DEPTH, 2, 256])
    k.rw_masks = k.inp("rwkv_masks", [64, 4, 64])
    k.rw_cmask = k.inp("rwkv_cmask", [64, 512])


def stage_rwkv(k, l):
    nc = k.nc
    BL = 256
    NB = S_LEN // BL
    CPB = BL // 64
    H4 = [64, 4, BL]
    bc = lambda ap, shape: ap.to_broadcast(shape)
    with Scope(k) as sc:
        pp = sc.sb("pp", [64, 8, 4], F32)
        lr = sc.sb("lr", [64, 3], F32)
        wup = sc.sb("wup", [32, 256], F32); aup = sc.sb("aup", [32, 256], F32); gup = sc.sb("gup", [64, 256], F32)
        lnr = sc.sb("lnr", [64, 2, 256], F32)
        mk = sc.sb("mk", [64, 4, 64], F32)
        cmask = sc.sb("cmask", [64, BL], F32)
        ones = sc.sb("ones64", [64, 64], F32)
        prm = sc.sb("prm", [64, 4, 4], F32)
        k.dma('sp', pp[:], k.rw_pp[l], w=[pp]); k.dma('sp', lr[:], k.rw_lr[l], w=[lr])
        k.dma('sp', wup[:], k.rw_wup[l], w=[wup]); k.dma('sp', aup[:], k.rw_aup[l], w=[aup]); k.dma('sp', gup[:], k.rw_gup[l], w=[gup])
        for i in range(2):
            k.dma('sp', lnr[:, i, :], k.rw_ln[l, i:i + 1, :].broadcast_to([64, 256]), w=[lnr])
        k.dma('sp', mk[:], k.rw_masks, w=[mk]); k.dma('sp', cmask[:], k.rw_cmask[:, 0:BL], w=[cmask])
        k.S.op('pool', lambda: nc.gpsimd.memset(ones[:], 1.0), [], [ones])
        k.ts('dve', prm[:, 0, :], pp[:, 3, :], -1.0, None, ALU.mult, None, r=[pp], w=[prm])
        k.ts('dve', prm[:, 1, :], pp[:, 6, :], -1.0, 1.0, ALU.mult, ALU.add, r=[pp], w=[prm])
        P3 = sc.sb("P3", [64, 3, 4, BL], F32)
        halo = sc.sb("halo", [64, 3, 4], F32)
        LR = sc.sb("LR", [64, 3, BL], F32)
        halo2 = sc.sb("halo2", [64, 3], F32)
        ELW = sc.sb("ELW", H4, F32); SC_ = sc.sb("SCAN", H4, F32); AA = sc.sb("AA", H4, F32); KKN = sc.sb("KKN", H4, F32)
        T1 = sc.sb("T1", H4, F32); T2 = sc.sb("T2", H4, F32); CM4 = sc.sb("CM4", H4, F32)
        OUT = [{nm: sc.sb("%s%d" % (nm, i), H4, F32 if nm == "GAM" else BF16) for nm in ("AT", "BT", "KT", "RT", "RK", "GAM", "V")} for i in range(2)]
        SGs = [sc.sb("SG%d" % i, [64, BL], BF16) for i in range(2)]
        gupb = sc.sb("gupb", [64, 256], BF16)
        ppb = sc.sb("ppb", [64, 4], BF16)
        identb64 = k.ident_bf
        XY = [[sc.sb("XY%d_%d" % (i, j), [64, 2, 4, 64], BF16) for j in range(2)] for i in range(2)]
        PP = [[sc.sb("PPi%d_%d" % (i, j), [64, 4, 64], BF16) for j in range(2)] for i in range(2)]
        AKRK = [sc.sb("AKRK%d" % i, [64, 2, 4, 64], BF16) for i in range(2)]
        RBT = [sc.sb("RBT%d" % i, [64, 4, 64], BF16) for i in range(2)]
        TOK = [sc.sb("TOK%d" % i, [64, 3, 4, 64], BF16) for i in range(2)]
        Wsb = sc.sb("Wsb", [64, 4, 64], BF16); Usb = sc.sb("Usb", [64, 4, 64], BF16)
        Hs = [sc.sb("Hs%d" % i, [64, 4, 64], F32) for i in range(2)]
        Hb = [sc.sb("Hb%d" % i, [64, 4, 64], BF16) for i in range(2)]
        yc = sc.sb("yc", [64, 4, 64], F32); ysq = sc.sb("ysq", [64, 4, 64], F32)
        sm = sc.sb("sm", [64, 6, 4], F32)
        yab = [sc.sb("yab%d" % i, [64, CPB, 256], BF16) for i in range(2)]
        psA1 = sc.ps("psA1", [64, 512]); psA2 = sc.ps("psA2", [64, 512]); psA3 = sc.ps("psA3", [64, 512]); psA4 = sc.ps("psA4", [64, 512])
        psH = sc.ps("psHr", [64, 512]); psY = sc.ps("psYr", [64, 512]); psC = sc.ps("psCr", [64, 512]); psQ = sc.ps("psQr", [64, 512])
        k.S.op('pool', lambda: nc.gpsimd.memset(Hs[1][:], 0.0), [], [Hs[1]])
        k.S.op('pool', lambda: nc.gpsimd.memset(Hb[1][:], 0.0), [], [Hb[1]])
        k.copy('dve', gupb[:], gup[:], r=[gup], w=[gupb])
        k.copy('dve', ppb[:], pp[:, 7, :], r=[pp], w=[ppb])
        k.S.op('pool', lambda: nc.gpsimd.memset(halo[:], 0.0), [], [halo])
        k.S.op('pool', lambda: nc.gpsimd.memset(halo2[:], 0.0), [], [halo2])
        k.copy('dve', CM4[:], bc(cmask[:].unsqueeze(1), H4), r=[cmask], w=[CM4])
        E_ = BL - 1

        def prep(tb):
            O = OUT[tb % 2]; SG = SGs[tb % 2]
            AT, BT, KT, RT, RK, GAM, V_ = O["AT"], O["BT"], O["KT"], O["RT"], O["RK"], O["GAM"], O["V"]
            t0 = tb * BL
            for q in range(3):
                k.dma('act', P3[:, q, :, :], k.PT[q * 256:(q + 1) * 256, t0:t0 + BL].rearrange("(h d) t -> d h t", d=64), w=[P3])
            k.dma('act', LR[0:32, 0, :], k.PT[768:800, t0:t0 + BL], w=[LR])
            k.dma('act', LR[0:32, 1, :], k.PT[800:832, t0:t0 + BL], w=[LR])
            k.dma('act', LR[:, 2, :], k.PT[832:896, t0:t0 + BL], w=[LR])
            yield
            for q in range(3):
                p_ = P3[:, q, :, :]
                k.tt('dve', T1[:, :, 1:BL], p_[:, :, 0:E_], p_[:, :, 1:BL], ALU.subtract, r=[P3], w=[T1])
                k.tt('dve', T1[:, :, 0:1], halo[:, q, :].unsqueeze(2), p_[:, :, 0:1], ALU.subtract, r=[P3, halo], w=[T1])
                k.copy('pool', halo[:, q, :].unsqueeze(2), p_[:, :, E_:BL], r=[P3, T1], w=[halo])
                k.tt('pool', T1[:], T1[:], bc(pp[:, q, :].unsqueeze(2), H4), ALU.mult, r=[T1, pp], w=[T1])
                if q < 2:
                    k.tt('pool', p_, p_, T1[:], ALU.add, r=[P3, T1, halo], w=[P3])
                else:
                    k.tt('pool', V_[:], p_, T1[:], ALU.add, r=[P3, T1, halo], w=[V_])
                yield
            for q, rows in ((0, 32), (1, 32), (2, 64)):
                x_ = LR[0:rows, q, :]
                t_ = T2[0:rows, 0, :]
                k.tt('dve', t_[:, 1:BL], x_[:, 0:E_], x_[:, 1:BL], ALU.subtract, r=[LR], w=[T2])
                k.tt('dve', t_[:, 0:1], halo2[0:rows, q:q + 1], x_[:, 0:1], ALU.subtract, r=[LR, halo2], w=[T2])
                k.copy('dve', halo2[0:rows, q:q + 1], x_[:, E_:BL], r=[LR, T2], w=[halo2])
                k.stt('dve', x_, t_, lr[0:rows, q:q + 1], x_, ALU.mult, ALU.add, r=[T2, lr, LR, halo2], w=[LR])
            yield
            R_ = P3[:, 0, :, :]; Kp = P3[:, 1, :, :]
            k.act(LR[0:32, 0, :], LR[0:32, 0, :], AF.Tanh, r=[LR], w=[LR])
            k.act(SG[:], LR[:, 2, :], AF.Sigmoid, r=[LR], w=[SG])
            for h in range(4):
                k.mm(psQ[:, 0:BL], [(wup[:, h * 64:(h + 1) * 64], LR[0:32, 0, :])], r=[wup, LR], w=[psQ])
                k.act(T1[:, h, :], psQ[:, 0:BL], AF.Exp, r=[psQ, prm], w=[T1], scale=-1.0, bias=prm[:, 0, h:h + 1])
                k.mm(psQ[:, BL:2 * BL], [(aup[:, h * 64:(h + 1) * 64], LR[0:32, 1, :])], r=[aup, LR], w=[psQ], start=False)
                k.act(AA[:, h, :], psQ[:, BL:2 * BL], AF.Sigmoid, r=[psQ, pp], w=[AA], bias=pp[:, 4, h:h + 1])
                yield
            k.act(T1[:], T1[:], AF.Ln, r=[T1], w=[T1], bias=1.0)
            k.act(ELW[:], T1[:], AF.Exp, r=[T1], w=[ELW], scale=-1.0, bias=-0.5)
            k.S.op('dve', lambda: nc.vector.tensor_tensor_scan(
                out=SC_[:].rearrange("p h t -> p (h t)"), data0=CM4[:].rearrange("p h t -> p (h t)"),
                data1=ELW[:].rearrange("p h t -> p (h t)"), initial=0.0, op0=ALU.mult, op1=ALU.add), [CM4, ELW], [SC_])
            yield
            k.tt('pool', KKN[:], Kp, bc(pp[:, 5, :].unsqueeze(2), H4), ALU.mult, r=[P3, pp], w=[KKN])
            k.tt('pool', T1[:], KKN[:], KKN[:], ALU.mult, r=[KKN], w=[T1])
            for h in range(4):
                k.mm(psQ[:, 0:BL], [(ones[:], T1[:, h, :])], r=[ones, T1], w=[psQ])
                k.act(T2[:, h, :], psQ[:, 0:BL], AF.Ln, r=[psQ], w=[T2], bias=1e-24)
                yield
            k.act(T2[:], T2[:], AF.Exp, r=[T2], w=[T2], scale=-0.5)
            k.tt('dve', KKN[:], KKN[:], T2[:], ALU.mult, r=[KKN, T2], w=[KKN])
            yield
            k.tt('pool', T1[:], SC_[:], ELW[:], ALU.subtract, r=[SC_, ELW], w=[T1])
            k.act(T1[:], T1[:], AF.Exp, r=[T1], w=[T1], scale=-1.0)
            k.stt('dve', AT[:], KKN[:], -1.0, T1[:], ALU.mult, ALU.mult, r=[KKN, T1], w=[AT])
            yield
            k.act(T2[:], SC_[:], AF.Exp, r=[SC_], w=[T2])
            k.tt('pool', T1[:], KKN[:], AA[:], ALU.mult, r=[KKN, AA], w=[T1])
            k.tt('dve', BT[:], T1[:], T2[:], ALU.mult, r=[T1, T2], w=[BT])
            yield
            k.tt('pool', T1[:], AA[:], bc(pp[:, 6, :].unsqueeze(2), H4), ALU.mult, r=[AA, pp], w=[T1])
            k.tt('pool', T1[:], T1[:], bc(prm[:, 1, :].unsqueeze(2), H4), ALU.add, r=[T1, prm], w=[T1])
            k.tt('dve', Kp, Kp, T1[:], ALU.mult, r=[P3, T1, KKN], w=[P3])
            yield
            k.tt('dve', KT[:], Kp, T2[:], ALU.mult, r=[P3, T2], w=[KT])
            k.tt('pool', RK[:], R_, Kp, ALU.mult, r=[P3], w=[RK])
            k.act(GAM[:], SC_[:], AF.Exp, r=[SC_], w=[GAM], scale=-1.0)
            k.tt('dve', RT[:], R_, GAM[:], ALU.mult, r=[P3, GAM], w=[RT])
            yield

        def phaseA(nch):
            tb, n = divmod(nch, CPB)
            O = OUT[tb % 2]
            AT, BT, KT, RT, V_ = O["AT"], O["BT"], O["KT"], O["RT"], O["V"]
            c_ = slice(n * 64, (n + 1) * 64)
            par = nch % 2
            xy = XY[par][0]; akrk = AKRK[par]; rbt = RBT[par]; tok = TOK[par]
            fns = []
            for h in range(4):
                fns.append(lambda h=h: nc.tensor.matmul(psA1[:, h * 64:(h + 1) * 64], lhsT=BT[:, h, c_], rhs=AT[:, h, c_], start=True, stop=True, skip_group_check=True))
                fns.append(lambda h=h: nc.tensor.matmul(psA1[:, 256 + h * 64:256 + (h + 1) * 64], lhsT=AT[:, h, c_], rhs=BT[:, h, c_], start=True, stop=True, skip_group_check=True))
            k.S.pe_group(fns, [AT, BT], [psA1])
            fns = []
            for h in range(4):
                fns.append(lambda h=h: nc.tensor.matmul(psA2[:, h * 64:(h + 1) * 64], lhsT=KT[:, h, c_], rhs=AT[:, h, c_], start=True, stop=True, skip_group_check=True))
                fns.append(lambda h=h: nc.tensor.matmul(psA2[:, 256 + h * 64:256 + (h + 1) * 64], lhsT=KT[:, h, c_], rhs=RT[:, h, c_], start=True, stop=True, skip_group_check=True))
            k.S.pe_group(fns, [AT, KT, RT], [psA2])
            k.S.pe_group([lambda h=h: nc.tensor.matmul(psA3[:, h * 64:(h + 1) * 64], lhsT=BT[:, h, c_], rhs=RT[:, h, c_], start=True, stop=True, skip_group_check=True)
                          for h in range(4)], [BT, RT], [psA3])
            fns = []
            psQb = psQ[:, :].bitcast(BF16)
            for qi, src_ in enumerate((V_, BT, KT)):
                for h in range(4):
                    dst = psQb[:, qi * 256 + h * 64:qi * 256 + (h + 1) * 64]
                    fns.append(lambda dst=dst, s_=src_[:, h, c_]: nc.tensor.transpose(out=dst, in_=s_, identity=k.ident_bf[0:64, 0:64]))
            k.S.pe_group(fns, [V_, BT, KT, k.ident_bf], [psQ])
            k.copy('act', tok[:].rearrange("p a h f -> p (a h f)"), psQb[:, 0:768], r=[psQ], w=[tok])
            yield
            v4 = lambda ps, a: ps[:, a * 256:(a + 1) * 256].rearrange("p (h f) -> p h f", h=4)
            mb = lambda i: bc(mk[:, i, :].unsqueeze(1), [64, 4, 64])
            k.tt('dve', xy[:, 0, :, :], v4(psA1, 0), mb(0), ALU.mult, r=[psA1, mk], w=[xy])
            k.tt('dve', xy[:, 1, :, :], v4(psA1, 1), mb(1), ALU.mult, r=[psA1, mk], w=[xy])
            k.tt('dve', akrk[:, 0, :, :], v4(psA2, 0), mb(0), ALU.mult, r=[psA2, mk], w=[akrk])
            k.tt('dve', akrk[:, 1, :, :], v4(psA2, 1), mb(2), ALU.mult, r=[psA2, mk], w=[akrk])
            k.tt('dve', rbt[:], v4(psA3, 0), mb(2), ALU.mult, r=[psA3, mk], w=[rbt])
            P_ = PP[par][0]
            k.tt('dve', P_[:], xy[:, 0, :, :], mb(3), ALU.add, r=[xy, mk], w=[P_])
            yield
            Pm = None
            for lev in range(1, 7):
                xyn = XY[par][lev % 2]
                fns = []
                rd = [xy]
                wr = []
                if lev <= 5:
                    for h in range(4):
                        fns.append(lambda h=h, xy=xy: nc.tensor.matmul(psA4[:, 256 + h * 64:256 + (h + 1) * 64], lhsT=xy[:, 0, h, :], rhs=xy[:, 1, h, :], start=True, stop=True, skip_group_check=True))
                        if lev <= 4:
                            fns.append(lambda h=h, xy=xy: nc.tensor.matmul(psA4[:, h * 64:(h + 1) * 64], lhsT=xy[:, 1, h, :], rhs=xy[:, 0, h, :], start=True, stop=True, skip_group_check=True))
                    wr.append(psA4)
                if lev >= 2:
                    for h in range(4):
                        fns.append(lambda h=h, xy=xy, Pm=Pm: nc.tensor.matmul(psA3[:, 256 + h * 64:256 + (h + 1) * 64], lhsT=xy[:, 1, h, :], rhs=Pm[:, h, :], start=True, stop=True, skip_group_check=True))
                    rd.append(Pm); wr.append(psA3)
                k.S.pe_group(fns, rd, wr)
                yield
                if lev <= 4:
                    k.copy('act', xyn[:].rearrange("p a h f -> p (a h f)"), psA4[:, :], r=[psA4], w=[xyn])
                elif lev == 5:
                    k.copy('act', xyn[:, 1, :, :].rearrange("p h f -> p (h f)"), psA4[:, 256:512], r=[psA4], w=[xyn])
                if lev >= 2:
                    Pn = PP[par][(lev - 1) % 2]
                    k.tt('dve', Pn[:], Pm[:], v4(psA3, 1), ALU.add, r=[Pm, psA3], w=[Pn])
                    Pm = Pn
                else:
                    Pm = P_
                yield
                xy = xyn

        def phaseB(nch):
            tb, n = divmod(nch, CPB)
            O = OUT[tb % 2]; SG = SGs[tb % 2]
            AT, RT, RK, GAM = O["AT"], O["RT"], O["RK"], O["GAM"]
            c_ = slice(n * 64, (n + 1) * 64)
            par = nch % 2
            akrk = AKRK[par]; rbt = RBT[par]; tok = TOK[par]; TT = PP[par][1]
            Hold = Hs[(nch + 1) % 2]; Hnew = Hs[nch % 2]
            Hbo = Hb[(nch + 1) % 2]; Hbn = Hb[nch % 2]
            yab_ = yab[tb % 2]
            fns = []
            for h in range(4):
                fns.append(lambda h=h: nc.tensor.matmul(psH[:, h * 64:(h + 1) * 64], lhsT=AT[:, h, c_], rhs=Hbo[:, h, :], start=(h == 0), stop=False, skip_group_check=True))
                fns.append(lambda h=h: nc.tensor.matmul(psH[:, h * 64:(h + 1) * 64], lhsT=akrk[:, 0, h, :], rhs=tok[:, 0, h, :], start=False, stop=True, skip_group_check=True))
            k.S.pe_group(fns, [AT, Hbo, akrk, tok], [psH])
            yield
            k.copy('act', Wsb[:].rearrange("p h f -> p (h f)"), psH[:, 0:256], r=[psH], w=[Wsb])
            yield
            k.S.pe_group([lambda h=h: nc.tensor.matmul(psH[:, 256 + h * 64:256 + (h + 1) * 64], lhsT=TT[:, h, :], rhs=Wsb[:, h, :], start=False, stop=True, skip_group_check=True)
                          for h in range(4)], [TT, Wsb], [psH])
            yield
            k.copy('act', Usb[:].rearrange("p h f -> p (h f)"), psH[:, 256:512], r=[psH], w=[Usb])
            yield
            fns = []
            for h in range(4):
                fns.append(lambda h=h: nc.tensor.matmul(psC[:, h * 64:(h + 1) * 64], lhsT=tok[:, 1, h, :], rhs=Usb[:, h, :], start=(h == 0), stop=False, skip_group_check=True))
                fns.append(lambda h=h: nc.tensor.matmul(psC[:, h * 64:(h + 1) * 64], lhsT=tok[:, 2, h, :], rhs=tok[:, 0, h, :], start=False, stop=True, skip_group_check=True))
                fns.append(lambda h=h: nc.tensor.matmul(psC[:, 256 + h:256 + h + 1], lhsT=RK[:, h, c_], rhs=ppb[:, h:h + 1], start=False, stop=True, skip_group_check=True))
            k.S.pe_group(fns, [tok, Usb, RK, ppb], [psC])
            fns = []
            for h in range(4):
                fns.append(lambda h=h: nc.tensor.matmul(psY[:, h * 64:(h + 1) * 64], lhsT=RT[:, h, c_], rhs=Hbo[:, h, :], start=(h == 0), stop=False, skip_group_check=True))
                fns.append(lambda h=h: nc.tensor.matmul(psY[:, h * 64:(h + 1) * 64], lhsT=rbt[:, h, :], rhs=Usb[:, h, :], start=False, stop=False, skip_group_check=True))
                fns.append(lambda h=h: nc.tensor.matmul(psY[:, h * 64:(h + 1) * 64], lhsT=akrk[:, 1, h, :], rhs=tok[:, 0, h, :], start=False, stop=True, skip_group_check=True))
            fns.append(lambda: nc.tensor.matmul(psY[:, 256:512], lhsT=SG[:, c_], rhs=gupb[:, :], start=False, stop=True, skip_group_check=True))
            k.S.pe_group(fns, [RT, Hbo, rbt, Usb, akrk, tok, SG, gupb], [psY])
            yield
            k.tt('dve', Hnew[:], psC[:, 0:256].rearrange("p (h f) -> p h f", h=4), Hold[:], ALU.add, r=[psC, Hold], w=[Hnew])
            k.copy('dve', sm[:, 5, :], psC[:, 256:260], r=[psC], w=[sm])
            k.tt('dve', Hbn[:], Hnew[:], bc(GAM[:, :, n * 64 + 63:n * 64 + 64], [64, 4, 64]), ALU.mult, r=[Hnew, GAM], w=[Hbn])
            k.tt('pool', Hnew[:], Hnew[:], bc(GAM[:, :, n * 64 + 63:n * 64 + 64], [64, 4, 64]), ALU.mult, r=[Hnew, GAM], w=[Hnew])
            yield
            y3 = psY[:, 0:256].rearrange("p (h f) -> p h f", h=4)
            k.S.op('dve', lambda: nc.vector.reduce_sum(out=sm[:, 0, :], in_=y3, axis=AX.X), [psY], [sm])
            k.ts('dve', sm[:, 1, :], sm[:, 0, :], 1.0 / 64.0, None, ALU.mult, None, r=[sm], w=[sm])
            k.tt('dve', yc[:], y3, bc(sm[:, 1, :].unsqueeze(2), [64, 4, 64]), ALU.subtract, r=[psY, sm], w=[yc])
            yield
            k.tt('pool', ysq[:], yc[:], yc[:], ALU.mult, r=[yc], w=[ysq])
            k.S.op('dve', lambda: nc.vector.reduce_sum(out=sm[:, 2, :], in_=ysq[:], axis=AX.X), [ysq], [sm])
            k.act(sm[:, 3, :], sm[:, 2, :], AF.Ln, r=[sm], w=[sm], scale=1.0 / 64.0, bias=GN_EPS)
            k.act(sm[:, 4, :], sm[:, 3, :], AF.Exp, r=[sm], w=[sm], scale=-0.5)
            yield
            k.tt('dve', yc[:], yc[:], bc(sm[:, 4, :].unsqueeze(2), [64, 4, 64]), ALU.mult, r=[yc, sm], w=[yc])
            k.tt('pool', yc[:], yc[:], lnr[:, 0, :].rearrange("p (h f) -> p h f", h=4), ALU.mult, r=[yc, lnr], w=[yc])
            k.tt('pool', yc[:], yc[:], lnr[:, 1, :].rearrange("p (h f) -> p h f", h=4), ALU.add, r=[yc, lnr], w=[yc])
            k.tt('dve', ysq[:], tok[:, 0, :, :], bc(sm[:, 5, :].unsqueeze(2), [64, 4, 64]), ALU.mult, r=[tok, sm], w=[ysq])
            yield
            k.tt('pool', yc[:], yc[:], ysq[:], ALU.add, r=[yc, ysq], w=[yc])
            k.tt('dve', yab_[:, n, :], yc[:].rearrange("p h f -> p (h f)"), psY[:, 256:512], ALU.mult, r=[yc, psY], w=[yab_])
            if n == CPB - 1:
                k.dma('sp', k.Y[tb * BL:(tb + 1) * BL, 0:256].rearrange("(n p) c -> p n c", p=64), yab_[:], r=[yab_])
            yield

        def run_all(*gens):
            gens = [g for g in gens if g is not None]
            while gens:
                for g in list(gens):
                    try:
                        next(g)
                    except StopIteration:
                        gens.remove(g)

        NCH = S_LEN // 64
        run_all(prep(0))
        run_all(phaseA(0), prep(1) if NB > 1 else None)
        gp = None
        for nch in range(NCH):
            tb, n = divmod(nch, CPB)
            if n == 0 and tb >= 1 and tb + 1 < NB:
                gp = prep(tb + 1)
            gens = [phaseB(nch)]
            if nch + 1 < NCH:
                gens.append(phaseA(nch + 1))
            rnd = 0
            while gens:
                for gi, g in enumerate(list(gens)):
                    for _rep in range(2 if (gi == 0 and RW_BPRIO) else 1):
                        try:
                            next(g)
                        except StopIteration:
                            if g in gens:
                                gens.remove(g)
                            break
                rnd += 1
                if gp is not None and rnd % 2 == 0:
                    try:
                        next(gp)
                    except StopIteration:
                        gp = None
            if n == CPB - 2 and gp is not None:
                for _ in gp:
                    pass
                gp = None


def build(nlayers=DEPTH, taps=()):
    k = K(nlayers, taps=taps)
    setup_globals(k)
    setup_fox(k)
    setup_rwkv(k)
    setup_ffn(k)
    setup_nsa(k)
    for l in range(nlayers):
        xin = k.x_in if l == 0 else k.XR
        xout = k.OUT if l == nlayers - 1 else k.XR
        stage_mod_proj(k, l, xin)
        stage_rwkv(k, l)
        stage_fox(k, l)
        stage_nsa(k, l)
        stage_out_ffn(k, l, xin, k.XR1, xout)
    k.S.barrier()
    return k


_CACHE = {}


def kernel(**inputs):
    if "k" not in _CACHE:
        _CACHE["k"] = build(DEPTH)
    k = _CACHE["k"]
    sh = prep_shared(inputs)
    in_maps = []
    for b in range(8):
        d = dict(sh)
        d.update(prep_core(inputs, b))
        in_maps.append({n: v for n, v in d.items() if n in k.ins})
    res = run_bass_kernel_spmd(k.nc, in_maps, core_ids=list(range(8)))
    out = np.stack([np.asarray(res.results[b]["out"], dtype=np.float32) for b in range(8)], axis=0)
    return out
```
